# Optimizing a Trainium2 kernel written in Bass

```python
import jax
import jax.numpy as jnp
from jax import lax
import numpy as np

D_MODEL = 1024
BATCH = 4
SEQ = 4096
DEPTH = 2

CHUNK = 64
PLE_DIM = 256
HEAD_DIM = 64
D_BRANCH = D_MODEL // 2
N_HEADS = D_BRANCH // HEAD_DIM
N_BRANCH = 3
NORM_EPS = 1e-6
LORA_W = 64
LORA_A = 64
RWKV_GN_EPS = 64e-5
IDX_HEADS = 4
IDX_DIM = 64
TOPK_MAX = 256
Q_BLOCK = 128
ROPE_THETA = 500000.0
ROPE_DIM = HEAD_DIM // 4
CONV_WIDTH = 4
LRU_C = 8.0

SPLITS_A = (D_BRANCH, D_BRANCH, D_BRANCH, LORA_W, LORA_A, D_BRANCH)
SPLITS_B = (D_BRANCH, D_BRANCH, D_BRANCH, IDX_HEADS * IDX_DIM, IDX_DIM, IDX_HEADS, D_BRANCH)
SPLITS_C = (D_BRANCH, D_BRANCH)
D_A_IN = sum(SPLITS_A)
D_B_IN = sum(SPLITS_B)
D_C_IN = sum(SPLITS_C)
D_GATE_IN = N_BRANCH * D_MODEL
D_IN = D_A_IN + D_B_IN + D_C_IN + D_GATE_IN

kernel_name = 'hybrid_rwkv7_dsa_rglru_gated_encoder'

F32 = jnp.float32


def split_cols(u, sizes):
    offsets = np.cumsum(np.array(sizes))[:-1].tolist()
    return jnp.split(u, offsets, axis=-1)


def rmsnorm(x, g):
    xf = x.astype(F32)
    y = xf * lax.rsqrt(jnp.mean(xf * xf, axis=-1, keepdims=True) + NORM_EPS)
    return (y * g.astype(F32)).astype(x.dtype)


def rope_tables(positions):
    inv_freq = ROPE_THETA ** (-jnp.arange(0, ROPE_DIM, 2, dtype=F32) / ROPE_DIM)
    ang = positions.astype(F32)[..., None] * inv_freq
    return jnp.cos(ang), jnp.sin(ang)


def partial_rope(x, cos, sin):
    half = ROPE_DIM // 2
    x1 = x[..., :half].astype(F32)
    x2 = x[..., half:ROPE_DIM].astype(F32)
    rot = jnp.concatenate([x1 * cos - x2 * sin, x2 * cos + x1 * sin], axis=-1)
    return jnp.concatenate([rot.astype(x.dtype), x[..., ROPE_DIM:]], axis=-1)


def token_shift(u, mu):
    prev = jnp.pad(u, ((0, 0), (1, 0), (0, 0)))[:, :-1]
    return u + (prev - u) * mu


def rwkv7_scan(r, decay, k, v, kk, a):
    B, S, H, N = r.shape

    def step(state, inp):
        r_t, w_t, k_t, v_t, kk_t, a_t = inp
        sa = jnp.einsum('bhij,bhj->bhi', state, -kk_t)
        state = (state * w_t[:, :, None, :]
                 + sa[..., None] * (kk_t * a_t)[:, :, None, :]
                 + v_t[..., None] * k_t[:, :, None, :])
        return state, jnp.einsum('bhij,bhj->bhi', state, r_t)

    xs = tuple(jnp.moveaxis(t, 1, 0) for t in (r, decay, k, v, kk, a))
    _, out = lax.scan(step, jnp.zeros((B, H, N, N), F32), xs)
    return jnp.moveaxis(out, 0, 1)


def rwkv7_branch(u, mu, w0, w2, a0, a2, k_k, k_a, r_k, gn_g, gn_b):
    B, S, _ = u.shape
    r, k, v, wl, al, g = split_cols(token_shift(u, mu), SPLITS_A)
    heads = lambda t: t.reshape(B, S, N_HEADS, HEAD_DIM)
    w_log = -jax.nn.softplus(-(w0 + jnp.tanh(wl) @ w2).astype(F32)) - 0.5
    decay = jnp.exp(-jnp.exp(w_log))
    a = jax.nn.sigmoid((a0 + al @ a2).astype(F32))
    kk = heads(k.astype(F32) * k_k.astype(F32))
    kk = kk / jnp.maximum(jnp.sqrt(jnp.sum(kk * kk, axis=-1, keepdims=True)), 1e-12)
    k_mod = k.astype(F32) * (1.0 + (a - 1.0) * k_a.astype(F32))
    rh, kh, vh = heads(r.astype(F32)), heads(k_mod), heads(v.astype(F32))
    o = rwkv7_scan(rh, heads(decay), kh, vh, kk, heads(a))
    mean = jnp.mean(o, axis=-1, keepdims=True)
    var = jnp.mean(jnp.square(o - mean), axis=-1, keepdims=True)
    o = ((o - mean) * lax.rsqrt(var + RWKV_GN_EPS)).reshape(B, S, D_BRANCH)
    o = o * gn_g.astype(F32) + gn_b.astype(F32)
    bonus = jnp.sum(rh * kh * r_k.astype(F32), axis=-1, keepdims=True) * vh
    y = o + bonus.reshape(B, S, D_BRANCH)
    return y.astype(u.dtype) * jax.nn.silu(g)


def dsa_attention(q, k, v, qi, ki, wi, k_sel):
    B, S, H, D = q.shape
    n_blocks = S // Q_BLOCK
    chunk_of = jnp.arange(S) // CHUNK
    scale = HEAD_DIM ** -0.5

    def block(bi):
        start = bi * Q_BLOCK
        qb = lax.dynamic_slice_in_dim(q, start, Q_BLOCK, axis=1)
        qib = lax.dynamic_slice_in_dim(qi, start, Q_BLOCK, axis=1)
        wib = lax.dynamic_slice_in_dim(wi, start, Q_BLOCK, axis=1)
        q_chunk = lax.dynamic_slice_in_dim(chunk_of, start, Q_BLOCK, axis=0)
        dots = jnp.einsum('bqhd,bsd->bqhs', qib, ki).astype(F32)
        score = jnp.einsum('bqh,bqhs->bqs', wib.astype(F32), jax.nn.relu(dots))
        admissible = chunk_of[None, :] <= q_chunk[:, None]
        score = jnp.where(admissible[None], score, -jnp.inf)
        _, idx = lax.top_k(score, k_sel)
        valid = chunk_of[idx] <= q_chunk[None, :, None]
        ks = jax.vmap(lambda kb, ib: kb[ib])(k, idx)
        vs = jax.vmap(lambda vb, ib: vb[ib])(v, idx)
        logits = jnp.einsum('bqhd,bqkhd->bqhk', qb, ks).astype(F32) * scale
        logits = jnp.where(valid[:, :, None, :], logits, -jnp.inf)
        prob = jax.nn.softmax(logits, axis=-1).astype(v.dtype)
        return jnp.einsum('bqhk,bqkhd->bqhd', prob, vs)

    out = lax.map(block, jnp.arange(n_blocks))
    return jnp.moveaxis(out, 0, 1).reshape(B, S, H * D)


def dsa_branch(u, cos, sin, q_g, k_g, k_sel):
    B, S, _ = u.shape
    q, k, v, qi, ki, wi, g = split_cols(u, SPLITS_B)
    heads = lambda t: t.reshape(B, S, N_HEADS, HEAD_DIM)
    cos_h, sin_h = cos[:, :, None, :], sin[:, :, None, :]
    q = partial_rope(rmsnorm(heads(q), q_g), cos_h, sin_h)
    k = partial_rope(rmsnorm(heads(k), k_g), cos_h, sin_h)
    qi = partial_rope(qi.reshape(B, S, IDX_HEADS, IDX_DIM), cos_h, sin_h)
    ki = partial_rope(ki, cos, sin)
    wi = wi * (IDX_HEADS ** -0.5 * IDX_DIM ** -0.5)
    o = dsa_attention(q, k, heads(v), qi, ki, wi, k_sel)
    return o * jax.nn.silu(g)


def causal_depthwise_conv(x, w, b):
    y = lax.conv_general_dilated(
        x, w[:, None, :].astype(x.dtype), window_strides=(1,),
        padding=[(CONV_WIDTH - 1, 0)],
        dimension_numbers=('NWC', 'WIO', 'NWC'),
        feature_group_count=x.shape[-1])
    return y + b


def rglru_branch(u, conv_w, conv_b, w_r, b_r, w_i, b_i, lam):
    B, S, _ = u.shape
    x, g = split_cols(u, SPLITS_C)
    xc = causal_depthwise_conv(x, conv_w, conv_b)
    xh = xc.reshape(B, S, N_HEADS, HEAD_DIM)
    r = jax.nn.sigmoid((jnp.einsum('bshi,hij->bshj', xh, w_r).reshape(B, S, D_BRANCH) + b_r).astype(F32))
    i = jax.nn.sigmoid((jnp.einsum('bshi,hij->bshj', xh, w_i).reshape(B, S, D_BRANCH) + b_i).astype(F32))
    log_a = -LRU_C * r * jax.nn.softplus(-lam.astype(F32))
    a = jnp.exp(log_a)
    b = jnp.sqrt(-jnp.expm1(2.0 * log_a)) * (i * xc.astype(F32))

    def combine(c1, c2):
        a1, b1 = c1
        a2, b2 = c2
        return a1 * a2, a2 * b1 + b2

    _, h = lax.associative_scan(combine, (a, b), axis=1)
    return h.astype(u.dtype) * jax.nn.silu(g)


def setup_inputs(seed: int = 0) -> dict:
    key = jax.random.key(seed)
    ks = jax.random.split(key, 32)
    nrm = lambda k, shape, s: jax.random.normal(k, shape, F32) * s
    a8 = jax.random.uniform(ks[22], (DEPTH, D_BRANCH), F32, 0.9, 0.999)
    a_base = a8 ** (1.0 / LRU_C)
    return {
        'x': nrm(ks[0], (BATCH, SEQ, D_MODEL), 1.0),
        'p': nrm(ks[1], (DEPTH, BATCH, SEQ, PLE_DIM), 1.0),
        'positions': jax.random.randint(ks[2], (BATCH, 1), 0, 2048, jnp.int32) + jnp.arange(SEQ, dtype=jnp.int32)[None, :],
        'norm_g': 1.0 + nrm(ks[3], (DEPTH, D_MODEL), 0.02),
        'w_in': nrm(ks[4], (DEPTH, D_MODEL, D_IN), D_MODEL ** -0.5),
        'rwkv_mu': jax.random.uniform(ks[5], (DEPTH, D_A_IN), F32, 0.0, 1.0),
        'rwkv_w0': jax.random.uniform(ks[6], (DEPTH, D_BRANCH), F32, -6.0, 1.0),
        'rwkv_w2': nrm(ks[7], (DEPTH, LORA_W, D_BRANCH), 0.1 * LORA_W ** -0.5),
        'rwkv_a0': nrm(ks[8], (DEPTH, D_BRANCH), 0.1),
        'rwkv_a2': nrm(ks[9], (DEPTH, LORA_A, D_BRANCH), 0.5 * LORA_A ** -0.5),
        'rwkv_k_k': 0.85 + nrm(ks[10], (DEPTH, D_BRANCH), 0.05),
        'rwkv_k_a': 1.0 + nrm(ks[11], (DEPTH, D_BRANCH), 0.05),
        'rwkv_r_k': nrm(ks[12], (DEPTH, N_HEADS, HEAD_DIM), 0.1),
        'rwkv_gn_g': 1.0 + nrm(ks[13], (DEPTH, D_BRANCH), 0.02),
        'rwkv_gn_b': nrm(ks[14], (DEPTH, D_BRANCH), 0.01),
        'dsa_q_g': 1.0 + nrm(ks[15], (DEPTH, HEAD_DIM), 0.02),
        'dsa_k_g': 1.0 + nrm(ks[16], (DEPTH, HEAD_DIM), 0.02),
        'lru_conv_w': nrm(ks[17], (DEPTH, CONV_WIDTH, D_BRANCH), CONV_WIDTH ** -0.5),
        'lru_conv_b': nrm(ks[18], (DEPTH, D_BRANCH), 0.01),
        'lru_w_r': nrm(ks[19], (DEPTH, N_HEADS, HEAD_DIM, HEAD_DIM), HEAD_DIM ** -0.5),
        'lru_b_r': nrm(ks[20], (DEPTH, D_BRANCH), 0.01),
        'lru_w_i': nrm(ks[21], (DEPTH, N_HEADS, HEAD_DIM, HEAD_DIM), HEAD_DIM ** -0.5),
        'lru_b_i': nrm(ks[23], (DEPTH, D_BRANCH), 0.01),
        'lru_lambda': jnp.log(a_base) - jnp.log1p(-a_base),
        'w_branch': nrm(ks[24], (DEPTH, N_BRANCH, D_BRANCH, D_MODEL), D_BRANCH ** -0.5),
        'w_out': nrm(ks[25], (DEPTH, D_MODEL, D_MODEL), D_MODEL ** -0.5),
        'w_ple': nrm(ks[26], (DEPTH, PLE_DIM, D_MODEL), PLE_DIM ** -0.5),
        'w_ple_gate': nrm(ks[27], (DEPTH, D_MODEL, D_MODEL), D_MODEL ** -0.5),
    }


def reference(x, p, positions, norm_g, w_in, rwkv_mu, rwkv_w0, rwkv_w2, rwkv_a0, rwkv_a2,
              rwkv_k_k, rwkv_k_a, rwkv_r_k, rwkv_gn_g, rwkv_gn_b, dsa_q_g, dsa_k_g,
              lru_conv_w, lru_conv_b, lru_w_r, lru_b_r, lru_w_i, lru_b_i, lru_lambda,
              w_branch, w_out, w_ple, w_ple_gate):
    B, S, _ = x.shape
    k_sel = min(TOPK_MAX, S // 4)
    cos, sin = rope_tables(positions)
    h = x
    for i in range(DEPTH):
        hn = rmsnorm(h, norm_g[i])
        u = hn @ w_in[i]
        u_a, u_b, u_c, u_g = split_cols(u, (D_A_IN, D_B_IN, D_C_IN, D_GATE_IN))
        y_a = rwkv7_branch(u_a, rwkv_mu[i], rwkv_w0[i], rwkv_w2[i], rwkv_a0[i], rwkv_a2[i],
                           rwkv_k_k[i], rwkv_k_a[i], rwkv_r_k[i], rwkv_gn_g[i], rwkv_gn_b[i])
        y_b = dsa_branch(u_b, cos, sin, dsa_q_g[i], dsa_k_g[i], k_sel)
        y_c = rglru_branch(u_c, lru_conv_w[i], lru_conv_b[i], lru_w_r[i], lru_b_r[i],
                           lru_w_i[i], lru_b_i[i], lru_lambda[i])
        ys = jnp.stack([y_a, y_b, y_c], axis=2)
        y_proj = jnp.einsum('bsnc,ncd->bsnd', ys, w_branch[i])
        gates = jax.nn.sigmoid(u_g.reshape(B, S, N_BRANCH, D_MODEL))
        merged = jnp.sum(gates * y_proj, axis=2)
        h = h + merged @ w_out[i]
        h = h + jax.nn.sigmoid(h @ w_ple_gate[i]) * (p[i] @ w_ple[i])
    return h
```

```python
from contextlib import ExitStack
import numpy as np
import concourse.bass as bass
import concourse.mybir as mybir
from concourse.bass_utils import run_bass_kernel_spmd

F32 = mybir.dt.float32
BF16 = mybir.dt.bfloat16
I32 = mybir.dt.int32
AF = mybir.ActivationFunctionType
ALU = mybir.AluOpType
AX = mybir.AxisListType

ENGS = ("tensor", "vector", "scalar", "gpsimd", "sync")
N_DMA_SEMS = 24

D = 1024
DIN = 8644
OFF_A, OFF_B, OFF_C, OFF_G = 0, 2176, 4548, 5572
NORM_EPS = 1e-6
GN_EPS = 64e-5


class FW:
    def __init__(self, nc, stack, same_engine_sync=True):
        self.nc = nc
        self.stack = stack
        self.q = {e: [] for e in ENGS}
        self.cnt = {e: 0 for e in ENGS}
        self.sem = {e: stack.enter_context(nc.semaphore("s_" + e)) for e in ENGS}
        self.dsem = [stack.enter_context(nc.semaphore("d%d" % i)) for i in range(N_DMA_SEMS)]
        self.dcnt = [0] * N_DMA_SEMS
        self.dnext = 0
        self.seen = {e: {} for e in ENGS}
        self.lastw = {}
        self.readers = {}
        self.same = same_engine_sync
        self.ninst = 0
        self.rr = 0

    def sb(self, name, shape, dt):
        return self.stack.enter_context(self.nc.sbuf_tensor(name, list(shape), dt))

    def ps(self, name, shape, dt=F32):
        return self.stack.enter_context(self.nc.psum_tensor(name, list(shape), dt))

    def _deps(self, eng, reads, writes):
        ev = []
        for k in reads:
            if k in self.lastw:
                ev.append(self.lastw[k])
        for k in writes:
            if k in self.lastw:
                ev.append(self.lastw[k])
            ev.extend(self.readers.get(k, ()))
        best = {}
        for (sname, sem, val, src) in ev:
            if src == eng and (eng == "tensor" or not self.same):
                continue
            if self.seen[eng].get(sname, 0) >= val:
                continue
            if sname not in best or best[sname][1] < val:
                best[sname] = (sem, val)
        waits = []
        for sname, (sem, val) in best.items():
            self.seen[eng][sname] = val
            waits.append((sem, val))
        return waits

    def _commit(self, event, reads, writes):
        for k in writes:
            self.lastw[k] = event
            self.readers[k] = []
        for k in reads:
            if k in writes:
                continue
            self.readers.setdefault(k, []).append(event)

    def op(self, eng, fn, reads=(), writes=()):
        waits = self._deps(eng, reads, writes)
        self.cnt[eng] += 1
        idx = self.cnt[eng]
        sem = self.sem[eng]
        self.q[eng].append((waits, fn, sem, 1))
        self._commit(("s_" + eng, sem, idx, eng), reads, writes)
        self.ninst += 1

    def dma(self, out, in_, reads=(), writes=(), eng="sync", **kw):
        lo, n = (0, 16) if eng == "sync" else (16, N_DMA_SEMS - 16)
        self.dnext_q = getattr(self, "dnext_q", {})
        i = self.dnext_q.get(eng, 0)
        self.dnext_q[eng] = (i + 1) % n
        slot = lo + i
        sem = self.dsem[slot]
        sname = "d%d" % slot
        waits = self._deps(eng, reads, writes)
        prev = self.dcnt[slot] * 16
        if prev and self.seen[eng].get(sname, 0) < prev:
            waits.append((sem, prev))
            self.seen[eng][sname] = prev
        self.dcnt[slot] += 1
        val = self.dcnt[slot] * 16
        self.q[eng].append((waits, lambda e: e.dma_start(out=out, in_=in_, **kw), sem, 16))
        self._commit((sname, sem, val, "dma"), reads, writes)
        self.ninst += 1

    def finish(self, keys, eng="sync"):
        waits = self._deps(eng, keys, ())
        self.q[eng].append((waits, None, None, 0))

    def emit(self):
        nc = self.nc
        with nc.Block() as block:
            for ename in ENGS:
                items = self.q[ename]
                if not items:
                    continue

                def body(e, items=items):
                    for waits, fn, sem, inc in items:
                        for (ws, wv) in waits:
                            e.wait_ge(ws, wv)
                        if fn is not None:
                            fn(e).then_inc(sem, inc)

                getattr(block, ename)(body)

    def mm(self, out, lhsT, rhs, start=True, stop=True, reads=(), writes=()):
        self.op("tensor", lambda e: e.matmul(out, lhsT, rhs, start=start, stop=stop), reads, writes)

    def tr(self, out, in_, ident, reads=(), writes=()):
        self.op("tensor", lambda e: e.transpose(out, in_, ident), reads, writes)

    def act(self, out, in_, func, bias=0.0, scale=1.0, reads=(), writes=(), accum_out=None):
        if accum_out is None:
            self.op("scalar", lambda e: e.activation(out, in_, func, bias=bias, scale=scale), reads, writes)
        else:
            self.op("scalar", lambda e: e.activation(out, in_, func, bias=bias, scale=scale,
                                                     accum_out=accum_out), reads, writes)

    def v(self, name, *args, reads=(), writes=(), eng="vector", **kw):
        self.op(eng, lambda e: getattr(e, name)(*args, **kw), reads, writes)

    @staticmethod
    def lockstep(gens):
        gens = list(gens)
        while gens:
            for g_ in list(gens):
                try:
                    next(g_)
                except StopIteration:
                    gens.remove(g_)

    def cast_eng(self):
        self.rr += 1
        return ("vector", "gpsimd")[self.rr % 2]


PCOLS = {}
_o = 0
for _n, _w in [("norm_g", 8), ("mu_r", 4), ("mu_k", 4), ("mu_v", 4), ("mu_g", 4), ("mu_wl", 1), ("mu_al", 1),
               ("w0", 4), ("a0", 4), ("k_k", 4), ("k_a", 4), ("gn_g", 4), ("gn_b", 4), ("r_k", 4),
               ("q_g", 1), ("k_g", 1),
               ("conv_w", 16), ("conv_b", 4), ("b_r", 4), ("b_i", 4), ("lam", 4)]:
    PCOLS[_n] = (_o, _w)
    _o += _w
NPRM = _o


def _col4(v):
    return np.ascontiguousarray(np.asarray(v, np.float32).reshape(4, 128).T)


def pack_params(inp, l):
    prm = np.zeros((128, NPRM), np.float32)

    def put(name, arr):
        o, w = PCOLS[name]
        prm[:arr.shape[0], o:o + w] = arr

    put("norm_g", np.asarray(inp["norm_g"][l], np.float32).reshape(8, 128).T)
    mu = np.asarray(inp["rwkv_mu"][l], np.float32)
    put("mu_r", _col4(mu[0:512])); put("mu_k", _col4(mu[512:1024])); put("mu_v", _col4(mu[1024:1536]))
    put("mu_wl", mu[1536:1600].reshape(64, 1)); put("mu_al", mu[1600:1664].reshape(64, 1))
    put("mu_g", _col4(mu[1664:2176]))
    put("w0", _col4(inp["rwkv_w0"][l])); put("a0", _col4(inp["rwkv_a0"][l]))
    put("k_k", _col4(inp["rwkv_k_k"][l])); put("k_a", _col4(inp["rwkv_k_a"][l]))
    put("gn_g", _col4(inp["rwkv_gn_g"][l])); put("gn_b", _col4(inp["rwkv_gn_b"][l]))
    put("r_k", _col4(np.asarray(inp["rwkv_r_k"][l]).reshape(512)))
    put("q_g", np.tile(np.asarray(inp["dsa_q_g"][l], np.float32), 2).reshape(128, 1))
    put("k_g", np.tile(np.asarray(inp["dsa_k_g"][l], np.float32), 2).reshape(128, 1))
    cw = np.asarray(inp["lru_conv_w"][l], np.float32)
    put("conv_w", np.concatenate([_col4(cw[i]) for i in range(4)], axis=1))
    put("conv_b", _col4(inp["lru_conv_b"][l])); put("b_r", _col4(inp["lru_b_r"][l]))
    put("b_i", _col4(inp["lru_b_i"][l])); put("lam", _col4(inp["lru_lambda"][l]))
    return prm


def blockdiag(w):
    w = np.asarray(w, np.float32)
    out = np.zeros((4, 128, 128), np.float32)
    for ct in range(4):
        out[ct, 0:64, 0:64] = w[2 * ct]
        out[ct, 64:128, 64:128] = w[2 * ct + 1]
    return out


class Prog:
    def __init__(self, S, L, phases="NACBM", dbg=()):
        self.S, self.L, self.phases, self.dbg = S, L, phases, dbg
        self.G = S // 512
        nc = self.nc = bass.Bass("TRN2", target_bir_lowering=False)
        dt = nc.dram_tensor
        self.xT = dt("xT", [8, 128, S], F32, kind="ExternalInput").ap()
        self.pT = dt("pT", [L, 2, 128, S], F32, kind="ExternalInput").ap()
        self.pos = dt("pos", [1, S], I32, kind="ExternalInput").ap()
        self.prm = dt("prm", [L, 128, NPRM], F32, kind="ExternalInput").ap()
        self.w_in = dt("w_in", [L, D, DIN], F32, kind="ExternalInput").ap()
        self.w_kidup = dt("w_kidup", [L, D, 128], F32, kind="ExternalInput").ap()
        self.w2 = dt("w2", [L, 64, 512], F32, kind="ExternalInput").ap()
        self.a2 = dt("a2", [L, 64, 512], F32, kind="ExternalInput").ap()
        self.wr_bd = dt("wr_bd", [L, 4, 128, 128], F32, kind="ExternalInput").ap()
        self.wi_bd = dt("wi_bd", [L, 4, 128, 128], F32, kind="ExternalInput").ap()
        self.w_branch = dt("w_branch", [L, 3, 512, D], F32, kind="ExternalInput").ap()
        self.w_out = dt("w_out", [L, D, D], F32, kind="ExternalInput").ap()
        self.w_ple = dt("w_ple", [L, 256, D], F32, kind="ExternalInput").ap()
        self.w_pg = dt("w_pg", [L, D, D], F32, kind="ExternalInput").ap()
        self.cst_d = dt("cst", [128, 32], F32, kind="ExternalInput").ap()
        self.ropeR_d = dt("ropeR", [128, 128], F32, kind="ExternalInput").ap()
        self.ropeT = dt("ropeT", [2, 128, S], F32, kind="Internal").ap()
        self.outT = dt("outT", [8, 128, S], F32, kind="ExternalOutput").ap()
        okind = lambda n: "ExternalOutput" if n in dbg else "Internal"
        self.hT = dt("hT", [8, 128, S], F32, kind=okind("hT")).ap()
        self.hnT = dt("hnT", [8, 128, S], BF16, kind=okind("hnT")).ap()
        self.yT = [dt("yT%d" % n, [4, 128, S], BF16, kind=okind("yT%d" % n)).ap() for n in range(3)]

    def uname(self, n):
        self._uid = getattr(self, "_uid", 0) + 1
        return "%s_u%d" % (n, self._uid)

    def pcol(self, name, j=0, rows=128):
        o, w = PCOLS[name]
        return self.prm_sb[0:rows, o + j:o + j + 1]

    def load_w(self, dst, key, src_fn, ncols, kt, scale=None, rows=128):
        fw = self.fw
        for k in range(kt):
            for c0 in range(0, ncols, 512):
                cn = min(512, ncols - c0)
                si = self.stg_i
                self.stg_i ^= 1
                stg = self.stg[si]
                fw.dma(stg[0:rows, 0:cn], src_fn(k)[:, c0:c0 + cn], writes=["stg%d" % si])
                self.cast_rr = getattr(self, "cast_rr", 0) + 1
                eng = ("vector", "scalar", "gpsimd")[self.cast_rr % 3]
                o_ap, i_ap = dst[0:rows, k, c0:c0 + cn], stg[0:rows, 0:cn]
                rk_ = ["stg%d" % si] + (["prm"] if scale is not None else [])
                if eng == "scalar":
                    fw.act(o_ap, i_ap, AF.Copy, scale=(scale(k) if scale is not None else 1.0), reads=rk_, writes=[(key, k)])
                elif scale is not None:
                    fw.v("tensor_scalar", o_ap, i_ap, scale(k), 0.0, ALU.mult, ALU.add, reads=rk_, writes=[(key, k)], eng=eng)
                else:
                    fw.v("tensor_copy", o_ap, i_ap, reads=rk_, writes=[(key, k)], eng=eng)

    def gcol(self, k):
        return self.pcol("norm_g", k)

    def build(self):
        nc = self.nc
        with ExitStack() as st:
            fw = self.fw = FW(nc, st)
            self.st = st
            self.stg = [fw.sb("stg%d" % i, [128, 512], F32) for i in range(2)]
            self.stg_i = 0
            self.prm_sb = fw.sb("prm_sb", [128, NPRM], F32)
            self.ones_f = fw.sb("ones_f", [128, 128], F32)
            fw.v("memset", self.ones_f[:], 1.0, writes=["ones_f"])
            self.PSALL = fw.ps("psall", [128, 8, 512], F32)
            self.PS = [self.PSALL[:, i, :] for i in range(8)]
            self.tiny_col = fw.sb("tiny_col", [128, 1], F32)
            self.gneps_col = fw.sb("gneps_col", [128, 1], F32)
            fw.v("memset", self.tiny_col[:], 1e-30, writes=["tiny"])
            fw.v("memset", self.gneps_col[:], GN_EPS, writes=["tiny"])
            self.eps6_col = fw.sb("eps6_col", [128, 1], F32)
            fw.v("memset", self.eps6_col[:], NORM_EPS, writes=["tiny"])
            self.cst_sb = fw.sb("cst_sb", [128, 32], F32)
            fw.dma(self.cst_sb[:], self.cst_d, writes=["cst"])
            self.make_consts()
            if "B" in self.phases:
                self.phase_R()
            for n, ph_ in enumerate("ABC"):
                if ph_ not in self.phases:
                    zt = fw.sb("zt%d" % n, [128, 4, 512], BF16)
                    fw.v("memset", zt[:], 0.0, writes=["zt"])
                    for g in range(self.G):
                        fw.dma(self.yT[n][:, :, g * 512:(g + 1) * 512].rearrange("k p s -> p k s"), zt[:], reads=["zt"],
                               writes=[("yT%d" % n, g, ct) for ct in range(4)])
            for l in range(self.L):
                fw.dma(self.prm_sb[:], self.prm[l], writes=["prm"])
                if l == 0 and "N" in self.phases:
                    self.phase_N0()
                if "A" in self.phases:
                    self.phase_A(l)
                if "C" in self.phases:
                    self.phase_C(l)
                if "B" in self.phases:
                    self.phase_B(l)
                if "M" in self.phases:
                    self.phase_M(l)
            fw.finish([("outT", g) for g in range(self.G)])
            fw.emit()
        return nc

    def norm_group(self, hbuf, hkey, g, tmp, tmpkey):
        fw, S = self.fw, self.S
        c0 = g * 512
        ps = self.PS[7]
        for k in range(8):
            fw.act(tmp[:, k % 2, :], hbuf[:, k, :], AF.Square, reads=[hkey], writes=[(tmpkey, k % 2)])
            fw.mm(ps[:], self.ones_f[:], tmp[:, k % 2, :], start=(k == 0), stop=(k == 7),
                  reads=["ones_f", (tmpkey, k % 2)], writes=["ps7"])
        rs = self.rs_sb
        fw.act(rs[:], ps[:], AF.Sqrt, bias=self.eps_col[:, 0:1], scale=1.0 / D, reads=["ps7", "eps"], writes=["rs"])
        fw.v("reciprocal", rs[:], rs[:], reads=["rs"], writes=["rs"])
        hn = self.hn_out
        fw.v("tensor_tensor", hn[:], hbuf[:], rs[:].unsqueeze(1).to_broadcast([128, 8, 512]), ALU.mult,
             reads=[hkey, "rs"], writes=[self.hn_out_key])
        fw.dma(self.hnT[:, :, c0:c0 + 512].rearrange("k p s -> p k s"), hn[:], reads=[self.hn_out_key],
               writes=[("hnT", g)], eng="gpsimd")

    def phase_N0(self):
        fw = self.fw
        with ExitStack() as ph:
            sb = lambda n, s, d: ph.enter_context(self.nc.sbuf_tensor(self.uname(n), list(s), d))
            hb = [sb("n0_h%d" % i, [128, 8, 512], F32) for i in range(2)]
            tmp = sb("n0_tmp", [128, 2, 512], F32)
            self.rs_sb = sb("n0_rs", [128, 512], F32)
            self.hn_out = sb("n0_hn", [128, 8, 512], BF16)
            self.hn_out_key = "hn_out"
            self.eps_col = sb("n0_eps", [128, 1], F32)
            self.acquire(["n0_h0", "n0_h1", ("n0_tmp", 0), ("n0_tmp", 1), "rs", "hn_out", "eps"])
            fw.v("memset", self.eps_col[:], NORM_EPS, writes=["eps"])
            for g in range(self.G):
                c0 = g * 512
                h = hb[g % 2]
                fw.dma(h[:], self.xT[:, :, c0:c0 + 512].rearrange("k p s -> p k s"), writes=["n0_h%d" % (g % 2)])
                self.norm_group(h, "n0_h%d" % (g % 2), g, tmp, "n0_tmp")
            self.release(["n0_h0", "n0_h1", ("n0_tmp", 0), ("n0_tmp", 1), "rs", "hn_out", "eps"])

    def release(self, keys):
        fw = self.fw
        ev = []
        for k in keys:
            if k in fw.lastw:
                ev.append(fw.lastw[k])
            ev.extend(fw.readers.get(k, ()))
        best = {}
        for e in getattr(fw, "pending_release", []) + ev:
            if e[0] not in best or best[e[0]][2] < e[2]:
                best[e[0]] = e
        fw.pending_release = list(best.values())

    def acquire(self, keys):
        fw = self.fw
        ev = getattr(fw, "pending_release", [])
        for k in keys:
            fw.readers.setdefault(k, []).extend(ev)


    def make_consts(self):
        fw = self.fw
        onesb = fw.sb("k_onesb", [128, 256], BF16)
        self.ident_b = fw.sb("k_ident", [128, 128], BF16)
        self.mask_ui = fw.sb("k_mask_ui", [128, 256], BF16)
        self.mask_sl = fw.sb("k_mask_sl", [128, 128], BF16)
        self.blk1 = fw.sb("k_blk1", [128, 128], F32)
        g = "gpsimd"
        fw.v("memset", onesb[:], 1.0, writes=["k_onesb"], eng=g)
        sel = lambda out, pat, cm, op, key: fw.op(g, lambda e: e.affine_select(out, onesb[:, 0:128], pat, op, 0.0, base=0,
                                                                                channel_multiplier=cm),
                                                  reads=["k_onesb"], writes=[key])
        sel(self.ident_b[:], [[-1, 128]], 1, ALU.is_equal, "k_ident")
        sel(self.mask_ui[:, 0:128], [[1, 128]], -1, ALU.is_gt, "k_mask_ui")
        sel(self.mask_ui[:, 128:256], [[1, 128]], -1, ALU.is_ge, "k_mask_ui")
        sel(self.mask_sl[:], [[-1, 128]], 1, ALU.is_gt, "k_mask_sl")
        fw.v("memset", self.blk1[:], 0.0, writes=["k_blk1"], eng=g)
        fw.v("memset", self.blk1[0:64, 0:64], 1.0, writes=["k_blk1"], eng=g)
        fw.v("memset", self.blk1[64:128, 64:128], 1.0, writes=["k_blk1"], eng=g)

    def phase_A(self, l):
        fw, S, G = self.fw, self.S, self.G
        PS = self.PS
        CDEC = 0.6065306597126334
        with ExitStack() as ph:
            allkeys = []

            def sb(n, s, d):
                allkeys.append(n)
                return ph.enter_context(self.nc.sbuf_tensor(self.uname(n), list(s), d))

            wA = sb("wA", [128, 8, 2176], BF16)
            w2b = sb("a_w2b", [64, 1, 512], BF16)
            a2b = sb("a_a2b", [64, 1, 512], BF16)
            hn = [sb("a_hn0", [128, 8, 512], BF16)] * 2
            omu = sb("a_omu", [128, NPRM], F32)
            prevc = sb("a_prevc", [128, 18], F32)
            ubP = [[sb("a_ub%d_%d" % (p, q), [128, 513], F32) for q in range(4)] for p in range(2)]
            usP = [[sb("a_us%d_%d" % (p, q), [128, 512], F32) for q in range(4)] for p in range(2)]
            ulo = [sb("a_ulo%d" % q, [64, 513], F32) for q in range(2)]
            twl = sb("a_twl", [64, 512], BF16)
            alb = sb("a_alb", [64, 512], BF16)
            tP = [[sb("a_t%d_%d" % (p, i), [128, 512], F32) for i in range(8)] for p in range(2)]
            t_ = tP[0]
            art = [sb("a_art%d" % ct, [128, 4, 2, 128], BF16) for ct in range(4)]
            bk = [sb("a_bk%d" % ct, [128, 2, 512], BF16) for ct in range(4)]
            vb = [sb("a_vb%d" % ct, [128, 512], BF16) for ct in range(4)]
            tok = [sb("a_tok%d" % ct, [128, 4, 3, 128], BF16) for ct in range(4)]
            bonus = [sb("a_bonus%d" % ct, [128, 512], BF16) for ct in range(4)]
            sgt = [sb("a_sg%d" % ct, [128, 512], BF16) for ct in range(4)]
            PC = sb("a_PC", [128, 4, 4], F32)
            T = sb("a_T", [128, 4, 64], F32)
            Tb = sb("a_Tb", [128, 4, 64], BF16)
            LAb = [sb("a_LAb%d" % i, [128, 4, 256], BF16) for i in range(2)]
            KAb = [sb("a_KAb%d" % i, [128, 4, 256], BF16) for i in range(2)]
            Lb = [sb("a_Lb%d" % i, [128, 4, 128], BF16) for i in range(2)]
            PPb = [[sb("a_PPb%d_%d" % (i, j), [128, 4, 256], BF16) for j in range(2)] for i in range(2)]
            XT = [[sb("a_XT%d_%d" % (i, j), [128, 4, 128], BF16) for j in range(2)] for i in range(2)]
            Wb = [sb("a_Wb%d" % i, [128, 4, 64], BF16) for i in range(2)]
            Ub = [sb("a_Ub%d" % i, [128, 4, 64], BF16) for i in range(2)]
            xc = sb("a_xc", [128, 8, 64], F32)
            sq = sb("a_sq", [128, 8, 64], F32)
            st8 = sb("a_st8", [128, 4, 8], F32)
            onb = sb("a_onb", [128, 512], BF16)
            yv = [sb("a_yv%d" % i, [128, 128], F32) for i in range(2)]
            yout = sb("a_yout", [128, 4, 512], BF16)
            self.rstm = sb("k_rstm", [128, 512], F32)
            keys = allkeys + [("wA", k) for k in range(8)] + [("a_w2b", 0), ("a_a2b", 0)]
            self.acquire(keys + ["stg0", "stg1"])
            fw.v("memset", self.rstm[:], 1.0, writes=["k_rstm"], eng="gpsimd")
            for c in range(4):
                fw.v("memset", self.rstm[:, c * 128:c * 128 + 1], 0.0, writes=["k_rstm"], eng="gpsimd")

            self.load_w(wA, "wA", lambda k: self.w_in[l, k * 128:(k + 1) * 128, OFF_A:OFF_A + 2176], 2176, 8, self.gcol)
            self.load_w(w2b, "a_w2b", lambda k: self.w2[l], 512, 1, rows=64)
            self.load_w(a2b, "a_a2b", lambda k: self.a2[l], 512, 1, rows=64)
            fw.v("tensor_scalar", omu[:], self.prm_sb[:], -1.0, 1.0, ALU.mult, ALU.add, reads=["prm"], writes=["a_omu"])
            fw.v("memset", prevc[:], 0.0, writes=["a_prevc"])
            fw.v("memset", T[:], 0.0, writes=["a_T"])
            fw.v("memset", Tb[:], 0.0, writes=["a_Tb"])
            oc = lambda name, j=0, rows=128: omu[0:rows, PCOLS[name][0] + j:PCOLS[name][0] + j + 1]
            psb0 = PS[0][:].bitcast(BF16)
            psb1 = PS[1][:].bitcast(BF16)

            def shift(ps, pskey, ubt, ubkey, pcol, out, okey, mu_ap, omu_ap, rows=128):
                fw.v("tensor_copy", ubt[0:rows, 0:1], prevc[0:rows, pcol:pcol + 1], reads=["a_prevc"], writes=[ubkey], eng="gpsimd")
                fw.act(ubt[0:rows, 1:513], ps, AF.Copy, reads=[pskey], writes=[ubkey])
                fw.v("tensor_copy", prevc[0:rows, pcol:pcol + 1], ubt[0:rows, 512:513], reads=[ubkey], writes=["a_prevc"], eng="gpsimd")
                fw.v("tensor_scalar", out, ubt[0:rows, 0:512], mu_ap, None, ALU.mult, reads=[ubkey, "prm"], writes=[okey])
                fw.v("scalar_tensor_tensor", out, ubt[0:rows, 1:513], omu_ap, out, ALU.mult, ALU.add,
                     reads=[ubkey, "a_omu", okey], writes=[okey])

            for g in range(G):
                c0 = g * 512
                hk = "a_hn0"
                hg = hn[0]
                fw.dma(hg[:], self.hnT[:, :, c0:c0 + 512].rearrange("k p s -> p k s"), reads=[("hnT", g)], writes=[hk])
                for q, (coff, nm) in enumerate([(1536, "mu_wl"), (1600, "mu_al")]):
                    for k in range(8):
                        fw.mm(PS[q][0:64, :], wA[:, k, coff:coff + 64], hg[:, k, :], start=(k == 0), stop=(k == 7),
                              reads=[("wA", k), hk], writes=["ps%d" % q])
                    shift(PS[q][0:64, :], "ps%d" % q, ulo[q], "a_ulo%d" % q, 16 + q, t_[q][0:64, :], "a_t0_%d" % q,
                          self.pcol(nm, 0, 64), oc(nm, 0, 64), rows=64)
                fw.act(twl[:], t_[0][0:64, :], AF.Tanh, reads=["a_t0_0"], writes=["a_twl"])
                fw.v("tensor_copy", alb[:], t_[1][0:64, :], reads=["a_t0_1"], writes=["a_alb"])
                def abody(ct, g=g, hk=hk, hg=hg):
                    p_ = ct % 2
                    PSp = PS[4 * p_:4 * p_ + 4]
                    pk = lambda q: "ps%d" % (4 * p_ + q)
                    ub, us, t_ = ubP[p_], usP[p_], tP[p_]
                    psb0 = PSp[0][:].bitcast(BF16)
                    psb1 = PSp[1][:].bitcast(BF16)
                    cs = slice(ct * 128, (ct + 1) * 128)
                    for q, (coff, nm) in enumerate([(0, "mu_r"), (512, "mu_k"), (1024, "mu_v"), (1664, "mu_g")]):
                        for k in range(8):
                            fw.mm(PSp[q][:], wA[:, k, coff + ct * 128:coff + (ct + 1) * 128], hg[:, k, :], start=(k == 0), stop=(k == 7),
                                  reads=[("wA", k), hk], writes=[pk(q)])
                            yield
                        shift(PSp[q][:], pk(q), ub[q], "a_ub%d_%d" % (p_, q), ct * 4 + q, us[q][:], "a_us%d_%d" % (p_, q),
                              self.pcol(nm, ct), oc(nm, ct))
                        yield
                    r_s, k_s, v_s, g_s = us
                    K = lambda i: "a_t%d_%d" % (p_, i)
                    fw.mm(PSp[0][:], w2b[:, 0, cs], twl[:], reads=[("a_w2b", 0), "a_twl"], writes=[pk(0)])
                    yield
                    fw.act(t_[0][:], PSp[0][:], AF.Sigmoid, bias=self.pcol("w0", ct), reads=[pk(0), "prm"], writes=[K(0)])
                    yield
                    fw.v("tensor_scalar", t_[0][:], t_[0][:], -CDEC, 0.0, ALU.mult, ALU.add, reads=[K(0)], writes=[K(0)], eng="gpsimd")
                    yield
                    fw.mm(PSp[1][:], a2b[:, 0, cs], alb[:], reads=[("a_a2b", 0), "a_alb"], writes=[pk(1)])
                    yield
                    fw.act(t_[1][:], PSp[1][:], AF.Sigmoid, bias=self.pcol("a0", ct), reads=[pk(1), "prm"], writes=[K(1)])
                    yield
                    fw.v("tensor_scalar", t_[2][:], k_s[:], self.pcol("k_k", ct), None, ALU.mult, reads=["a_us%d_1" % p_, "prm"], writes=[K(2)])
                    yield
                    fw.v("tensor_tensor", t_[3][:], t_[2][:], t_[2][:], ALU.mult, reads=[K(2)], writes=[K(3)], eng="gpsimd")
                    yield
                    fw.mm(PSp[2][:], self.blk1[:], t_[3][:], reads=["k_blk1", K(3)], writes=[pk(2)])
                    yield
                    fw.act(t_[3][:], PSp[2][:], AF.Sqrt, bias=self.tiny_col[:, 0:1], reads=[pk(2), "tiny"], writes=[K(3)])
                    yield
                    fw.v("reciprocal", t_[3][:], t_[3][:], reads=[K(3)], writes=[K(3)])
                    yield
                    fw.v("tensor_tensor", t_[2][:], t_[2][:], t_[3][:], ALU.mult, reads=[K(2), K(3)], writes=[K(2)])
                    yield
                    fw.v("tensor_scalar", t_[3][:], t_[1][:], self.pcol("k_a", ct), oc("k_a", ct), ALU.mult, ALU.add,
                         reads=[K(1), "prm", "a_omu"], writes=[K(3)])
                    yield
                    fw.v("tensor_tensor", t_[3][:], t_[3][:], k_s[:], ALU.mult, reads=[K(3), "a_us%d_1" % p_], writes=[K(3)], eng="gpsimd")
                    yield
                    fw.v("tensor_tensor", t_[4][:], t_[2][:], t_[1][:], ALU.mult, reads=[K(2), K(1)], writes=[K(4)], eng="gpsimd")
                    yield
                    fw.v("tensor_tensor_scan", t_[5][:], self.rstm[:], t_[0][:], 0.0, ALU.mult, ALU.add,
                         reads=["k_rstm", K(0)], writes=[K(5)])
                    yield
                    fw.v("tensor_tensor", t_[6][:], t_[5][:], t_[0][:], ALU.subtract, reads=[K(5), K(0)], writes=[K(6)], eng="gpsimd")
                    yield
                    fw.act(t_[6][:], t_[6][:], AF.Exp, reads=[K(6)], writes=[K(6)])
                    yield
                    fw.act(t_[7][:], t_[5][:], AF.Exp, scale=-1.0, reads=[K(5)], writes=[K(7)])
                    yield
                    fw.act(t_[5][:], t_[5][:], AF.Exp, reads=[K(5)], writes=[K(5)])
                    yield
                    fw.v("tensor_copy", PC[:, ct, :], t_[5][:].rearrange("p (c t) -> p c t", t=128)[:, :, 127], reads=[K(5)],
                         writes=["a_PC"], eng="gpsimd")
                    yield
                    v3 = lambda ap: ap.rearrange("p (c t) -> p c t", t=128)
                    akey = "a_art%d" % ct
                    fw.v("scalar_tensor_tensor", art[ct][:, :, 0, :], v3(t_[2][:]), -1.0, v3(t_[6][:]), ALU.mult, ALU.mult,
                         reads=[K(2), K(6)], writes=[akey])
                    yield
                    fw.v("tensor_tensor", art[ct][:, :, 1, :], v3(r_s[:]), v3(t_[5][:]), ALU.mult, reads=["a_us%d_0" % p_, K(5)], writes=[akey])
                    yield
                    fw.v("tensor_tensor", bk[ct][:, 0, :], t_[4][:], t_[7][:], ALU.mult, reads=[K(4), K(7)], writes=["a_bk%d" % ct])
                    yield
                    fw.v("tensor_tensor", bk[ct][:, 1, :], t_[3][:], t_[7][:], ALU.mult, reads=[K(3), K(7)], writes=["a_bk%d" % ct], eng="gpsimd")
                    yield
                    fw.v("tensor_copy", vb[ct][:], v_s[:], reads=["a_us%d_2" % p_], writes=["a_vb%d" % ct], eng="gpsimd")
                    yield
                    fw.v("scalar_tensor_tensor", t_[4][:], r_s[:], self.pcol("r_k", ct), t_[3][:], ALU.mult, ALU.mult,
                         reads=["a_us%d_0" % p_, "prm", K(3), K(4)], writes=[K(4)])
                    yield
                    fw.mm(PSp[3][:], self.blk1[:], t_[4][:], reads=["k_blk1", K(4)], writes=[pk(3)])
                    yield
                    fw.v("tensor_tensor", bonus[ct][:], PSp[3][:], v_s[:], ALU.mult, reads=[pk(3), "a_us%d_2" % p_], writes=["a_bonus%d" % ct])
                    yield
                    fw.act(sgt[ct][:], g_s[:], AF.Silu, reads=["a_us%d_3" % p_], writes=["a_sg%d" % ct])
                    yield
                    for half in range(2):
                        psb, pkey = (psb0, pk(0)) if half == 0 else (psb1, pk(1))
                        for cc in range(2):
                            c = half * 2 + cc
                            for qi_, (src, skey) in enumerate([(bk[ct][:, 0, c * 128:(c + 1) * 128], "a_bk%d" % ct),
                                                               (bk[ct][:, 1, c * 128:(c + 1) * 128], "a_bk%d" % ct),
                                                               (vb[ct][:, c * 128:(c + 1) * 128], "a_vb%d" % ct)]):
                                o = (cc * 3 + qi_) * 128
                                fw.tr(psb[:, o:o + 128], src, self.ident_b[:], reads=[skey, "k_ident"], writes=[pkey])
                                yield
                        fw.act(tok[ct][:, half * 2:half * 2 + 2, :, :].rearrange("p a b c -> p (a b c)"), psb[:, 0:768], AF.Copy,
                               reads=[pkey], writes=["a_tok%d" % ct])
                        yield

                fw.lockstep([abody(0), abody(1)])
                fw.lockstep([abody(2), abody(3)])
                PSALL = self.PSALL
                idb4 = self.ident_b[:].unsqueeze(1).to_broadcast([128, 4, 128])
                mui2 = self.mask_ui[:].unsqueeze(1).to_broadcast([128, 2, 256])
                msl4 = self.mask_sl[:].unsqueeze(1).to_broadcast([128, 4, 128])
                for c in range(4):
                    ccols = slice(c * 128, (c + 1) * 128)

                    def hv(qd, hi):
                        h = 2 * hi + qd
                        ct, hp = hi, qd
                        pr_ = slice(hp * 64, hp * 64 + 64)
                        d = dict(h=h, ct=ct, hp=hp, pr=pr_, po=hp * 64,
                                 at=art[ct][pr_, c, 0, :], rt=art[ct][pr_, c, 1, :],
                                 ar=art[ct][pr_, c, :, :].rearrange("p a t -> p (a t)"),
                                 bt=bk[ct][pr_, 0, ccols], kt=bk[ct][pr_, 1, ccols],
                                 rk=["a_art%d" % ct, "a_bk%d" % ct], tkey="a_tok%d" % ct,
                                 vt=tok[ct][:, c, 2, hp * 64:hp * 64 + 64], btk=tok[ct][:, c, 0, hp * 64:hp * 64 + 64],
                                 ktk=tok[ct][:, c, 1, hp * 64:hp * 64 + 64], T0b=Tb[pr_, ct, :])
                        return d

                    XYk = lambda qd: ["ps%d" % (3 * qd), "ps%d" % (3 * qd + 1)]
                    Zk = lambda qd: ["ps%d" % (3 * qd + 2)]
                    XY = lambda qd: PSALL[:, 3 * qd:3 * qd + 2, :].rearrange("p b (h x) -> p (b h) x", x=256)
                    Zv = lambda qd: PSALL[:, 3 * qd + 2, :].rearrange("p (h x) -> p h x", x=128)
                    import os as _os
                    _stop = int(_os.environ.get("A_STOP", "99"))
                    if _stop <= 0:
                        continue
                    for qd in range(2):
                        for hi in range(4):
                            d = hv(qd, hi)
                            fw.mm(XY(qd)[:, hi, :], d["bt"], d["ar"], reads=d["rk"], writes=[XYk(qd)[hi // 2]])
                            fw.mm(Zv(qd)[:, hi, :], d["at"], d["bt"], reads=d["rk"], writes=Zk(qd))
                    for qd in range(2):
                        for b2 in range(2):
                            fw.v("tensor_tensor", LAb[qd][:, 2 * b2:2 * b2 + 2, :], XY(qd)[:, 2 * b2:2 * b2 + 2, :], mui2, ALU.mult,
                                 reads=[XYk(qd)[b2], "k_mask_ui"], writes=["a_LAb%d" % qd])
                        fw.v("tensor_tensor", Lb[qd][:], Zv(qd), msl4, ALU.mult, reads=Zk(qd) + ["k_mask_sl"], writes=["a_Lb%d" % qd])
                        fw.v("tensor_tensor", XT[qd][0][:], LAb[qd][:, :, 0:128], idb4, ALU.add,
                             reads=["a_LAb%d" % qd, "k_ident"], writes=["a_XT%d_0" % qd], eng="gpsimd")
                    if _stop <= 1:
                        continue
                    for qd in range(2):
                        for hi in range(4):
                            d = hv(qd, hi)
                            fw.mm(XY(qd)[:, hi, :], d["kt"], d["ar"], reads=d["rk"], writes=[XYk(qd)[hi // 2]])
                    for qd in range(2):
                        for b2 in range(2):
                            fw.v("tensor_tensor", KAb[qd][:, 2 * b2:2 * b2 + 2, :], XY(qd)[:, 2 * b2:2 * b2 + 2, :], mui2, ALU.mult,
                                 reads=[XYk(qd)[b2], "k_mask_ui"], writes=["a_KAb%d" % qd])
                    if _stop <= 2:
                        continue
                    for k in range(1, 8):
                        for qd in range(2):
                            if k == 1:
                                Pp, PTp, pkeys = (lambda hi: Lb[qd][:, hi, :]), (lambda hi: LAb[qd][:, hi, 0:128]), ["a_Lb%d" % qd, "a_LAb%d" % qd]
                            else:
                                pb_ = PPb[qd][(k - 1) % 2]
                                Pp, PTp, pkeys = (lambda hi, pb_=pb_: pb_[:, hi, 0:128]), (lambda hi, pb_=pb_: pb_[:, hi, 128:256]), ["a_PPb%d_%d" % (qd, (k - 1) % 2)]
                            for hi in range(4):
                                if k <= 6:
                                    fw.mm(XY(qd)[:, hi, 0:128], PTp(hi), Pp(hi), reads=pkeys, writes=[XYk(qd)[hi // 2]])
                                    fw.mm(XY(qd)[:, hi, 128:256], Pp(hi), PTp(hi), reads=pkeys, writes=[XYk(qd)[hi // 2]])
                                if k >= 2:
                                    xo = XT[qd][(k - 2) % 2]
                                    xok = "a_XT%d_%d" % (qd, (k - 2) % 2)
                                    fw.mm(Zv(qd)[:, hi, :], self.ident_b[:], xo[:, hi, :], start=True, stop=False, reads=["k_ident", xok], writes=Zk(qd))
                                    fw.mm(Zv(qd)[:, hi, :], Pp(hi), xo[:, hi, :], start=False, stop=True, reads=pkeys + [xok], writes=Zk(qd))
                        for qd in range(2):
                            if k <= 6:
                                for b2 in range(2):
                                    fw.act(PPb[qd][k % 2][:, 2 * b2:2 * b2 + 2, :], XY(qd)[:, 2 * b2:2 * b2 + 2, :], AF.Copy,
                                           reads=[XYk(qd)[b2]], writes=["a_PPb%d_%d" % (qd, k % 2)])
                            if k >= 2:
                                fw.v("tensor_copy", XT[qd][(k - 1) % 2][:], Zv(qd), reads=Zk(qd), writes=["a_XT%d_%d" % (qd, (k - 1) % 2)])
                    if _stop <= 3:
                        continue
                    XTf = [XT[qd][0] for qd in range(2)]
                    xfk = ["a_XT%d_0" % qd for qd in range(2)]
                    Wv = lambda qd: PSALL[:, 3 * qd + 2, 0:256].rearrange("p (h x) -> p h x", x=64)
                    Uv = lambda qd: PSALL[:, 3 * qd + 2, 256:512].rearrange("p (h x) -> p h x", x=64)
                    for qd in range(2):
                        for hi in range(4):
                            d = hv(qd, hi)
                            fw.mm(Wv(qd)[:, hi, :], d["at"], d["T0b"], start=True, stop=False, reads=["a_art%d" % d["ct"], "a_Tb"], writes=Zk(qd))
                            fw.mm(Wv(qd)[:, hi, :], KAb[qd][:, hi, 0:128], d["vt"], start=False, stop=True, reads=["a_KAb%d" % qd, d["tkey"]], writes=Zk(qd))
                        fw.v("tensor_copy", Wb[qd][:], Wv(qd), reads=Zk(qd), writes=["a_Wb%d" % qd])
                    for qd in range(2):
                        for hi in range(4):
                            fw.mm(Uv(qd)[:, hi, :], XTf[qd][:, hi, :], Wb[qd][:, hi, :], reads=[xfk[qd], "a_Wb%d" % qd], writes=Zk(qd))
                        fw.v("tensor_copy", Ub[qd][:], Uv(qd), reads=Zk(qd), writes=["a_Ub%d" % qd])
                    for qd in range(2):
                        for hi in range(4):
                            d = hv(qd, hi)
                            h, ct = d["h"], d["ct"]
                            ob, okey = (PS[6], "ps6") if qd == 0 else (PS[7], "ps7")
                            osl = slice(qd * 256 + hi * 64, qd * 256 + (hi + 1) * 64)
                            fw.mm(ob[:, osl], d["rt"], d["T0b"], start=True, stop=False, reads=["a_art%d" % ct, "a_Tb"], writes=[okey])
                            fw.mm(ob[:, osl], LAb[qd][:, hi, 128:256], Ub[qd][:, hi, :], start=False, stop=False,
                                  reads=["a_LAb%d" % qd, "a_Ub%d" % qd], writes=[okey])
                            fw.mm(ob[:, osl], KAb[qd][:, hi, 128:256], d["vt"], start=False, stop=True, reads=["a_KAb%d" % qd, d["tkey"]], writes=[okey])
                            zsl = slice(ct * 64, (ct + 1) * 64)
                            fw.mm(PS[7][d["pr"], zsl], d["btk"], Ub[qd][:, hi, :], start=True, stop=False, reads=[d["tkey"], "a_Ub%d" % qd], writes=["ps7"])
                            fw.mm(PS[7][d["pr"], zsl], d["ktk"], d["vt"], start=False, stop=True, reads=[d["tkey"]], writes=["ps7"])
                    if _stop <= 4:
                        continue
                    zall = PS[7][:, 0:256].rearrange("p (c i) -> p c i", i=64)
                    fw.v("tensor_tensor", T[:], T[:], zall, ALU.add, reads=["a_T", "ps7"], writes=["a_T"])
                    fw.v("tensor_tensor", T[:], T[:], PC[:, :, c:c + 1].to_broadcast([128, 4, 64]), ALU.mult, reads=["a_T", "a_PC"], writes=["a_T"])
                    fw.v("tensor_copy", Tb[:], T[:], reads=["a_T"], writes=["a_Tb"], eng="gpsimd")
                    ov = [PS[6][:, 0:256].rearrange("p (h i) -> p h i", i=64), PS[7][:, 256:512].rearrange("p (h i) -> p h i", i=64)]
                    okeys = ["ps6", "ps7"]
                    for qd in range(2):
                        fw.v("tensor_reduce", st8[:, 0, qd * 4:qd * 4 + 4], ov[qd], AX.X, ALU.add, reads=[okeys[qd]], writes=["a_st8"])
                    fw.v("tensor_scalar", st8[:, 0, :], st8[:, 0, :], 1.0 / 64, None, ALU.mult, reads=["a_st8"], writes=["a_st8"])
                    for qd in range(2):
                        fw.v("tensor_tensor", xc[:, qd * 4:qd * 4 + 4, :], ov[qd],
                             st8[:, 0, qd * 4:qd * 4 + 4].unsqueeze(2).to_broadcast([128, 4, 64]), ALU.subtract,
                             reads=[okeys[qd], "a_st8"], writes=["a_xc"])
                    fw.v("tensor_tensor", sq[:], xc[:], xc[:], ALU.mult, reads=["a_xc"], writes=["a_sq"], eng="gpsimd")
                    fw.v("tensor_reduce", st8[:, 1, :], sq[:], AX.X, ALU.add, reads=["a_sq"], writes=["a_st8"])
                    fw.act(st8[:, 1, :], st8[:, 1, :], AF.Sqrt, bias=self.gneps_col[:, 0:1], scale=1.0 / 64, reads=["a_st8", "tiny"], writes=["a_st8"])
                    fw.v("reciprocal", st8[:, 1, :], st8[:, 1, :], reads=["a_st8"], writes=["a_st8"])
                    fw.v("tensor_tensor", onb[:].rearrange("p (h i) -> p h i", i=64), xc[:],
                         st8[:, 1, :].unsqueeze(2).to_broadcast([128, 8, 64]), ALU.mult, reads=["a_xc", "a_st8"], writes=["a_onb"])
                    for ct in range(4):
                        for hp in range(2):
                            qo = (hp * 4 + ct) * 64
                            fw.tr(psb0[hp * 64:(hp + 1) * 64, ct * 128:(ct + 1) * 128], onb[:, qo:qo + 64], self.ident_b[:],
                                  reads=["a_onb", "k_ident"], writes=["ps0"])
                    for ct in range(4):
                        j = ct % 2
                        fw.v("tensor_scalar", yv[j][:], psb0[:, ct * 128:(ct + 1) * 128], self.pcol("gn_g", ct), self.pcol("gn_b", ct),
                             ALU.mult, ALU.add, reads=["ps0", "prm"], writes=["a_yv%d" % j])
                        fw.v("tensor_tensor", yv[j][:], yv[j][:], bonus[ct][:, ccols], ALU.add, reads=["a_yv%d" % j, "a_bonus%d" % ct],
                             writes=["a_yv%d" % j], eng="gpsimd")
                        fw.v("tensor_tensor", yout[:, ct, ccols], yv[j][:], sgt[ct][:, ccols], ALU.mult,
                             reads=["a_yv%d" % j, "a_sg%d" % ct], writes=["a_yout"], eng="gpsimd")
                fw.dma(self.yT[0][:, :, c0:c0 + 512].rearrange("k p s -> p k s"), yout[:], reads=["a_yout"],
                       writes=[("yT0", g, ct) for ct in range(4)], eng="gpsimd")
            self.release(keys)


    def phase_R(self):
        fw, S, G = self.fw, self.S, self.G
        TWO_PI = 6.283185307179586
        C1 = 6.28125
        C2 = 0.0019350051879882812
        C3 = TWO_PI - C1 - C2
        PI = 3.1415925
        with ExitStack() as ph:
            sb = lambda n, s, d: ph.enter_context(self.nc.sbuf_tensor(self.uname(n), list(s), d))
            posi = sb("r_posi", [128, 512], I32)
            a = sb("r_a", [128, 512], F32)
            k = sb("r_k", [128, 512], F32)
            r = sb("r_r", [128, 512], F32)
            r2 = sb("r_r2", [128, 512], F32)
            m = sb("r_m", [128, 512], F32)
            cs = sb("r_cs", [128, 2, 512], F32)
            keys = ["r_posi", "r_a", "r_k", "r_r", "r_r2", "r_m", "r_cs"]
            self.acquire(keys)
            for g in range(G):
                c0 = g * 512
                fw.dma(posi[:], self.pos[0:1, c0:c0 + 512].to_broadcast([128, 512]), writes=["r_posi"])
                fw.v("tensor_copy", a[:], posi[:], reads=["r_posi"], writes=["r_a"])
                fw.v("tensor_scalar", a[:], a[:], self.cst_sb[:, 0:1], None, ALU.mult, reads=["r_a", "cst"], writes=["r_a"])
                fw.v("tensor_scalar", k[:], a[:], 1.0 / TWO_PI, None, ALU.mult, reads=["r_a"], writes=["r_k"])
                fw.v("tensor_scalar", k[:], k[:], 12582912.0, None, ALU.add, reads=["r_k"], writes=["r_k"])
                fw.v("tensor_scalar", k[:], k[:], 12582912.0, None, ALU.subtract, reads=["r_k"], writes=["r_k"])
                fw.v("scalar_tensor_tensor", r[:], k[:], -C1, a[:], ALU.mult, ALU.add, reads=["r_k", "r_a"], writes=["r_r"])
                fw.v("scalar_tensor_tensor", r[:], k[:], -C2, r[:], ALU.mult, ALU.add, reads=["r_k", "r_r"], writes=["r_r"])
                fw.v("scalar_tensor_tensor", r[:], k[:], -C3, r[:], ALU.mult, ALU.add, reads=["r_k", "r_r"], writes=["r_r"])
                fw.v("tensor_scalar", r[:], r[:], PI, -PI, ALU.min, ALU.max, reads=["r_r"], writes=["r_r"])
                fw.v("tensor_scalar", r2[:], r[:], TWO_PI / 4, None, ALU.add, reads=["r_r"], writes=["r_r2"])
                fw.v("tensor_scalar", m[:], r2[:], PI, -TWO_PI, ALU.is_gt, ALU.mult, reads=["r_r2"], writes=["r_m"])
                fw.v("tensor_tensor", r2[:], r2[:], m[:], ALU.add, reads=["r_r2", "r_m"], writes=["r_r2"])
                fw.v("tensor_scalar", r2[:], r2[:], PI, -PI, ALU.min, ALU.max, reads=["r_r2"], writes=["r_r2"])
                fw.act(cs[:, 0, :], r2[:], AF.Sin, reads=["r_r2"], writes=["r_cs"])
                fw.act(cs[:, 1, :], r[:], AF.Sin, reads=["r_r"], writes=["r_cs"])
                fw.dma(self.ropeT[:, :, c0:c0 + 512].rearrange("k p s -> p k s"), cs[:], reads=["r_cs"], writes=[("ropeT", g)], eng="gpsimd")
            self.release(keys)

    def phase_B(self, l):
        fw, S, G = self.fw, self.S, self.G
        PS = self.PS
        NT = S // 128
        NIT = 20
        with ExitStack() as ph:
            allkeys = []

            def sb(n, s, d):
                allkeys.append(n)
                return ph.enter_context(self.nc.sbuf_tensor(self.uname(n), list(s), d))

            wB = sb("wB", [128, 8, 2372], BF16)
            wkd = sb("b_wkd", [128, 8, 128], BF16)
            ropeR = sb("b_ropeR", [128, 1, 128], BF16)
            KT = [sb("b_KT%d" % ct, [128, S], BF16) for ct in range(4)]
            KI = sb("b_KI", [128, S], BF16)
            V = sb("b_V", [128, NT, 8, 65], BF16)
            hn = sb("b_hn", [128, 8, 512], BF16)
            QT = [[sb("b_QT%d_%d" % (ct, i), [128, 512], BF16) for ct in range(4)] for i in range(2)]
            QI = [[sb("b_QI%d_%d" % (j, i), [128, 512], BF16) for j in range(2)] for i in range(2)]
            SG = [[sb("b_SG%d_%d" % (ct, i), [128, 512], BF16) for ct in range(4)] for i in range(2)]
            WI = [sb("b_WI%d" % i, [128, 4, 4], F32) for i in range(2)]
            yout = [sb("b_yout0", [128, 4, 512], BF16)] * 2
            score = sb("b_score", [128, S], F32)
            alias = S >= 4096
            if alias:
                xL = [score[:, 0:512], score[:, 1280:1792]]
                x2L = [score[:, 512:1024], score[:, 1792:2304]]
                xbL = [score[:, 1024:1280].bitcast(BF16), score[:, 2304:2560].bitcast(BF16)]
                cs = score[:, 2560:3584].rearrange("p (a b) -> p a b", b=512)
            else:
                cs = sb("b_cs", [128, 2, 512], F32)
                xL = [sb("b_x%d" % i, [128, 512], F32) for i in range(2)]
                x2L = [sb("b_x2%d" % i, [128, 512], F32) for i in range(2)]
                xbL = [sb("b_xb%d" % i, [128, 512], BF16) for i in range(2)]
            tkeys = ["b_cs"] + ["b_x%d" % i for i in range(2)] + ["b_x2%d" % i for i in range(2)] + ["b_xb%d" % i for i in range(2)]
            mm1 = [sb("b_mm1_0", [128, S], BF16)] * 2
            MT = [sb("b_MT%d" % i, [128, NT, 128], BF16) for i in range(2)]
            E = [sb("b_E%d" % i, [128, 512], BF16) for i in range(4)]
            rl = [sb("b_rl%d" % i, [128, 512], F32) for i in range(2)]
            PT = [sb("b_PT%d" % i, [128, 512], BF16) for i in range(4)]
            bs = sb("b_bs", [128, 8], F32)
            steps = sb("b_steps", [128, NIT], F32)
            rec = sb("b_rec", [128, 8], F32)
            otok = sb("b_otok", [128, 8, 64], BF16)
            dmask = sb("b_dmask", [128, 128], F32)
            keys = allkeys + tkeys + [("wB", k) for k in range(8)] + [("b_wkd", k) for k in range(8)] + [("b_ropeR", 0)] + \
                [("b_KT", ct, g) for ct in range(4) for g in range(G)] + [("b_KI", g) for g in range(G)] + [("b_V", g) for g in range(G)]
            self.acquire(keys + ["stg0", "stg1"])
            self.load_w(wB, "wB", lambda k: self.w_in[l, k * 128:(k + 1) * 128, OFF_B:OFF_B + 2372], 2372, 8, self.gcol)
            self.load_w(wkd, "b_wkd", lambda k: self.w_kidup[l, k * 128:(k + 1) * 128, :], 128, 8, self.gcol)
            self.load_w(ropeR, "b_ropeR", lambda k: self.ropeR_d, 128, 1)
            fw.v("memset", V[:], 1.0, writes=[("b_V", g) for g in range(G)], eng="gpsimd")
            fw.v("memset", dmask[:], 0.0, writes=["b_dmask"], eng="gpsimd")
            fw.v("memset", dmask[0:64, 64:128], -1e30, writes=["b_dmask"], eng="gpsimd")

            def lane(p, chains):
                x_, x2, xb = xL[p], x2L[p], xbL[p]
                kx, kx2, kxb = "b_x%d" % p, "b_x2%d" % p, "b_xb%d" % p
                ps, pk = PS[p], "ps%d" % p

                def proj(w, wkey, c_lo, c_hi):
                    for k in range(8):
                        fw.mm(ps[:], w[:, k, c_lo:c_hi], hn[:, k, :], start=(k == 0), stop=(k == 7), reads=[(wkey, k), "b_hn"], writes=[pk])
                        yield

                def rope(dst, dkey):
                    fw.v("tensor_copy", xb[:], x_[:], reads=[kx], writes=[kxb], eng="gpsimd")
                    yield
                    fw.mm(ps[:], ropeR[:, 0, :], xb[:], reads=[("b_ropeR", 0), kxb], writes=[pk])
                    yield
                    fw.v("tensor_tensor", x2[:], x_[:], cs[:, 0, :], ALU.mult, reads=[kx, "b_cs"], writes=[kx2], eng="gpsimd")
                    yield
                    fw.v("tensor_tensor", x_[:], ps[:], cs[:, 1, :], ALU.mult, reads=[pk, "b_cs", kx], writes=[kx])
                    yield
                    fw.v("tensor_tensor", dst, x2[:], x_[:], ALU.add, reads=[kx2, kx], writes=[dkey], eng="gpsimd")
                    yield

                for ch in chains:
                    kind = ch[0]
                    if kind in ("q", "k"):
                        _, ct, g = ch
                        gp = g % 2
                        gc = slice(g * 512, g * 512 + 512)
                        coff, gname = (0, "q_g") if kind == "q" else (512, "k_g")
                        yield from proj(wB, "wB", coff + ct * 128, coff + (ct + 1) * 128)
                        fw.act(x_[:], ps[:], AF.Copy, reads=[pk], writes=[kx])
                        yield
                        fw.v("tensor_tensor", x2[:], x_[:], x_[:], ALU.mult, reads=[kx], writes=[kx2], eng="gpsimd")
                        yield
                        fw.mm(ps[:], self.blk1[:], x2[:], reads=["k_blk1", kx2], writes=[pk])
                        yield
                        fw.act(x2[:], ps[:], AF.Sqrt, bias=self.eps6_col[:, 0:1], scale=1.0 / 64, reads=[pk, "tiny"], writes=[kx2])
                        yield
                        fw.v("reciprocal", x2[:], x2[:], reads=[kx2], writes=[kx2])
                        yield
                        fw.v("scalar_tensor_tensor", x_[:], x_[:], self.pcol(gname, 0), x2[:], ALU.mult, ALU.mult,
                             reads=[kx, "prm", kx2], writes=[kx])
                        yield
                        if kind == "q":
                            yield from rope(QT[gp][ct][:], "b_QT%d_%d" % (ct, gp))
                        else:
                            yield from rope(KT[ct][:, gc], ("b_KT", ct, g))
                    elif kind == "qi":
                        _, j, g = ch
                        gp = g % 2
                        yield from proj(wB, "wB", 1536 + j * 128, 1536 + (j + 1) * 128)
                        fw.act(x_[:], ps[:], AF.Copy, reads=[pk], writes=[kx])
                        yield
                        yield from rope(QI[gp][j][:], "b_QI%d_%d" % (j, gp))
                    elif kind == "ki":
                        _, g = ch
                        gc = slice(g * 512, g * 512 + 512)
                        yield from proj(wkd, "b_wkd", 0, 128)
                        fw.act(x_[:], ps[:], AF.Copy, reads=[pk], writes=[kx])
                        yield
                        yield from rope(KI[:, gc], ("b_KI", g))
                    elif kind == "sg":
                        _, ct, g = ch
                        gp = g % 2
                        yield from proj(wB, "wB", 1860 + ct * 128, 1860 + (ct + 1) * 128)
                        fw.act(SG[gp][ct][:], ps[:], AF.Silu, reads=[pk], writes=["b_SG%d_%d" % (ct, gp)])
                        yield
                    elif kind == "v":
                        _, tt, g = ch
                        gp = g % 2
                        tcols = slice(tt * 128, (tt + 1) * 128)
                        for k in range(8):
                            fw.mm(ps[:], hn[:, k, tcols], wB[:, k, 1024:1536], start=(k == 0), stop=(k == 7),
                                  reads=[("wB", k), "b_hn"], writes=[pk])
                            yield
                        fw.act(V[:, g * 4 + tt, :, 0:64], ps[:].rearrange("p (h i) -> p h i", i=64), AF.Copy, reads=[pk], writes=[("b_V", g)])
                        yield
                        for k in range(8):
                            fw.mm(ps[:, 0:4], hn[:, k, tcols], wB[:, k, 1856:1860], start=(k == 0), stop=(k == 7),
                                  reads=[("wB", k), "b_hn"], writes=[pk])
                            yield
                        fw.v("tensor_scalar", WI[gp][:, tt, :], ps[:, 0:4], 1.0 / 16, None, ALU.mult, reads=[pk], writes=["b_WI%d" % gp])
                        yield

            def prep_begin(g):
                gc = slice(g * 512, g * 512 + 512)
                if alias:
                    self.release(["b_score"])
                    self.acquire(tkeys)
                fw.dma(hn[:], self.hnT[:, :, gc].rearrange("k p s -> p k s"), reads=[("hnT", g)], writes=["b_hn"])
                fw.dma(cs[:], self.ropeT[:, :, gc].rearrange("k p s -> p k s"), reads=[("ropeT", g)], writes=["b_cs"])

            def prep_lanes(g):
                chains = []
                for ct in range(4):
                    chains += [("q", ct, g), ("k", ct, g)]
                chains += [("qi", 0, g), ("qi", 1, g), ("ki", g)]
                chains += [("sg", ct, g) for ct in range(4)]
                chains += [("v", tt, g) for tt in range(4)]
                return [lane(0, chains[0::2]), lane(1, chains[1::2])]

            def prep_end(g):
                if alias:
                    self.release(tkeys)
                    self.acquire(["b_score"])

            def scores(qt):
                g, tt = qt // 4, qt % 4
                gp = g % 2
                N = (qt + 1) * 128
                tq = slice(tt * 128, (tt + 1) * 128)
                for pc in range((N + 511) // 512):
                    p0 = pc * 512
                    pn = min(512, N - p0)
                    for ih in range(4):
                        po = (ih % 2) * 64
                        fw.mm(PS[ih][:, 0:pn], QI[gp][ih // 2][po:po + 64, tq], KI[po:po + 64, p0:p0 + pn],
                              reads=["b_QI%d_%d" % (ih // 2, gp), ("b_KI", pc)], writes=["ps%d" % ih])
                    for ih in range(4):
                        r_ = rl[ih % 2]
                        rkey = "b_rl%d" % (ih % 2)
                        fw.act(r_[:, 0:pn], PS[ih][:, 0:pn], AF.Relu, reads=["ps%d" % ih], writes=[rkey])
                        if ih == 0:
                            fw.v("tensor_scalar", score[:, p0:p0 + pn], r_[:, 0:pn], WI[gp][:, tt, 0:1], None, ALU.mult,
                                 reads=[rkey, "b_WI%d" % gp], writes=["b_score"])
                        else:
                            fw.v("scalar_tensor_tensor", score[:, p0:p0 + pn], r_[:, 0:pn], WI[gp][:, tt, ih:ih + 1], score[:, p0:p0 + pn],
                                 ALU.mult, ALU.add, reads=[rkey, "b_WI%d" % gp, "b_score"], writes=["b_score"])

            def bisect_mask(qt):
                NB = qt + 1
                N = NB * 128
                mk = mm1[0]
                mkey = "b_mm1_0"
                A, lo, mid, cnt, tmp = (bs[:, i:i + 1] for i in range(5))
                if NB >= 3:
                    fw.v("tensor_reduce", A, score[:, 0:N], AX.X, ALU.max, apply_absolute_value=True, reads=["b_score"], writes=["b_bs"])
                    fw.v("tensor_scalar", A, A, 1.0001, 1e-20, ALU.mult, ALU.add, reads=["b_bs"], writes=["b_bs"])
                fw.v("tensor_tensor", score[:, N - 128:N], score[:, N - 128:N], dmask[:], ALU.add, reads=["b_score", "b_dmask"],
                     writes=["b_score"])
                if NB >= 3:
                    fw.v("tensor_scalar", steps[:], self.cst_sb[:, 1:1 + NIT], A, None, ALU.mult, reads=["cst", "b_bs"], writes=["b_steps"])
                    fw.v("tensor_scalar", lo, A, -1.0, None, ALU.mult, reads=["b_bs"], writes=["b_bs"])
                    for it in range(NIT):
                        fw.v("tensor_tensor", mid, lo, steps[:, it:it + 1], ALU.add, reads=["b_bs", "b_steps"], writes=["b_bs"])
                        fw.v("tensor_scalar", mk[:, 0:N], score[:, 0:N], mid, None, ALU.is_ge, ALU.add, accum_out=cnt,
                             reads=["b_score", "b_bs", mkey], writes=[mkey, "b_bs"])
                        fw.v("tensor_scalar", tmp, cnt, 255.5, steps[:, it:it + 1], ALU.is_ge, ALU.mult, reads=["b_bs", "b_steps"], writes=["b_bs"])
                        fw.v("tensor_tensor", lo, lo, tmp, ALU.add, reads=["b_bs"], writes=["b_bs"])
                else:
                    fw.v("memset", lo, -1e29, writes=["b_bs"])
                fw.v("tensor_scalar", mk[:, 0:N], score[:, 0:N], lo, None, ALU.is_ge, reads=["b_score", "b_bs"], writes=[mkey])
                psb1 = PS[1][:].bitcast(BF16)
                mt, mtkey = MT[qt % 2], "b_MT%d" % (qt % 2)
                for kb0 in range(0, NB, 8):
                    nk = min(8, NB - kb0)
                    for j in range(nk):
                        kb = kb0 + j
                        fw.tr(psb1[:, j * 128:(j + 1) * 128], mk[:, kb * 128:(kb + 1) * 128], self.ident_b[:],
                              reads=[mkey, "k_ident"], writes=["ps1"])
                    fw.act(mt[:, kb0:kb0 + nk, :].rearrange("p a b -> p (a b)"), psb1[:, 0:nk * 128], AF.Copy, reads=["ps1"], writes=[mtkey])

            def attention_gen(qt):
                g, tt = qt // 4, qt % 4
                gp = g % 2
                NB = qt + 1
                tq = slice(tt * 128, (tt + 1) * 128)
                mt, mtkey = MT[qt % 2], "b_MT%d" % (qt % 2)
                for hpair in range(4):
                    ct = hpair
                    for gi, kb0 in enumerate(range(0, NB, 4)):
                        nk = min(4, NB - kb0)
                        bis = [2 * e + gi % 2 for e in range(2)]
                        for j in range(nk):
                            kb = kb0 + j
                            for e in range(2):
                                po = e * 64
                                pl = PS[2 + bis[e]]
                                fw.mm(pl[:, j * 128:(j + 1) * 128], KT[ct][po:po + 64, kb * 128:(kb + 1) * 128], QT[gp][ct][po:po + 64, tq],
                                      reads=[("b_KT", ct, kb // 4), "b_QT%d_%d" % (ct, gp)], writes=["ps%d" % (2 + bis[e])])
                                yield
                        for e in range(2):
                            bi = bis[e]
                            fw.act(E[bi][:, 0:nk * 128], PS[2 + bi][:, 0:nk * 128], AF.Exp, scale=0.125, reads=["ps%d" % (2 + bi)], writes=["b_E%d" % bi])
                            yield
                            fw.v("tensor_tensor", PT[bi][:, 0:nk * 128], E[bi][:, 0:nk * 128],
                                 mt[:, kb0:kb0 + nk, :].rearrange("p a b -> p (a b)"), ALU.mult,
                                 reads=["b_E%d" % bi, mtkey], writes=["b_PT%d" % bi], eng="gpsimd")
                            yield
                        for e in range(2):
                            h = 2 * hpair + e
                            bi = bis[e]
                            pob = PS[7] if e == 0 else PS[6]
                            pokey = "ps7" if e == 0 else "ps6"
                            osl = slice(hpair * 65, hpair * 65 + 65)
                            for j in range(nk):
                                kb = kb0 + j
                                fw.mm(pob[:, osl], PT[bi][:, j * 128:(j + 1) * 128], V[:, kb, h, :], start=(kb == 0), stop=(kb == NB - 1),
                                      reads=["b_PT%d" % bi, ("b_V", kb // 4)], writes=[pokey])
                                yield

            def final(qt):
                g, tt = qt // 4, qt % 4
                gp = g % 2
                tq = slice(tt * 128, (tt + 1) * 128)
                otok4 = otok[:].rearrange("p (a e) i -> p a e i", e=2)
                for hb_ in range(2):
                    pob = PS[7] if hb_ == 0 else PS[6]
                    pokey = "ps7" if hb_ == 0 else "ps6"
                    pv = pob[:, 0:260].rearrange("p (h i) -> p h i", i=65)
                    fw.v("reciprocal", rec[:, hb_ * 4:hb_ * 4 + 4], pv[:, :, 64], reads=[pokey], writes=["b_rec"])
                    fw.v("tensor_tensor", otok4[:, :, hb_, :], pv[:, :, 0:64],
                         rec[:, hb_ * 4:hb_ * 4 + 4].unsqueeze(2).to_broadcast([128, 4, 64]), ALU.mult,
                         reads=[pokey, "b_rec"], writes=["b_otok"])
                of = otok[:].rearrange("p h i -> p (h i)")
                for ct in range(4):
                    pb_ = PS[7 - ct // 2][:, 384:512].bitcast(BF16)
                    pkey = "ps%d" % (7 - ct // 2)
                    fw.tr(pb_[:, (ct % 2) * 128:(ct % 2 + 1) * 128], of[:, ct * 128:(ct + 1) * 128], self.ident_b[:],
                          reads=["b_otok", "k_ident"], writes=[pkey])
                for ct in range(4):
                    pb_ = PS[7 - ct // 2][:, 384:512].bitcast(BF16)
                    pkey = "ps%d" % (7 - ct // 2)
                    fw.v("tensor_tensor", yout[gp][:, ct, tq], pb_[:, (ct % 2) * 128:(ct % 2 + 1) * 128], SG[gp][ct][:, tq], ALU.mult,
                         reads=[pkey, "b_SG%d_%d" % (ct, gp)], writes=["b_yout0"])
                if tt == 3:
                    gc = slice(g * 512, g * 512 + 512)
                    fw.dma(self.yT[1][:, :, gc].rearrange("k p s -> p k s"), yout[gp][:], reads=["b_yout0"],
                           writes=[("yT1", g, ct) for ct in range(4)], eng="gpsimd")

            prep_begin(0)
            fw.lockstep(prep_lanes(0))
            prep_end(0)
            scores(0)
            bisect_mask(0)
            for qt in range(NT):
                nxt = qt + 1
                if nxt < NT and nxt % 4 == 0:
                    gn = nxt // 4
                    prep_begin(gn)
                    fw.lockstep([attention_gen(qt)] + prep_lanes(gn))
                    prep_end(gn)
                    scores(nxt)
                    bisect_mask(nxt)
                    final(qt)
                else:
                    if nxt < NT:
                        scores(nxt)
                    fw.lockstep([attention_gen(qt)])
                    if nxt < NT:
                        bisect_mask(nxt)
                    final(qt)
            self.release(keys)

    def phase_C(self, l):
        fw, S, G = self.fw, self.S, self.G
        PS = self.PS
        with ExitStack() as ph:
            sb = lambda n, s, d: ph.enter_context(self.nc.sbuf_tensor(self.uname(n), list(s), d))
            wC = sb("wC", [128, 8, 1024], BF16)
            wr = sb("c_wr", [128, 4, 128], BF16)
            wi = sb("c_wi", [128, 4, 128], BF16)
            hn = [sb("c_hn%d" % i, [128, 8, 512], BF16) for i in range(2)]
            xbuf = sb("c_xbuf", [128, 4, 515], F32)
            hprev = sb("c_hprev", [128, 4], F32)
            cl = sb("c_cl", [128, 4], F32)
            xc = [sb("c_xc%d" % i, [128, 512], F32) for i in range(2)]
            xcb = [sb("c_xcb%d" % i, [128, 512], BF16) for i in range(2)]
            r_ = [sb("c_r%d" % i, [128, 512], F32) for i in range(2)]
            i_ = [sb("c_i%d" % i, [128, 512], F32) for i in range(2)]
            a_ = [sb("c_a%d" % i, [128, 512], F32) for i in range(2)]
            b_ = [sb("c_b%d" % i, [128, 512], F32) for i in range(2)]
            sg = [sb("c_sg%d" % i, [128, 512], F32) for i in range(2)]
            yo = [sb("c_y%d" % i, [128, 512], BF16) for i in range(2)]
            names = ["wC", "c_wr", "c_wi", "c_hn0", "c_hn1", "c_xbuf", "c_hprev", "c_cl"] + \
                    [n + str(i) for n in ("c_xc", "c_xcb", "c_r", "c_i", "c_a", "c_b", "c_sg", "c_y") for i in range(2)]
            keys = names + [("wC", k) for k in range(8)] + [("c_wr", k) for k in range(4)] + [("c_wi", k) for k in range(4)] + \
                ["c_xbuf%d" % i for i in range(4)] + ["c_hprev%d" % i for i in range(4)]
            self.acquire(keys + ["stg0", "stg1"])
            self.load_w(wC, "wC", lambda k: self.w_in[l, k * 128:(k + 1) * 128, OFF_C:OFF_C + 1024], 1024, 8, self.gcol)
            self.load_w(wr, "c_wr", lambda k: self.wr_bd[l, k], 128, 4)
            self.load_w(wi, "c_wi", lambda k: self.wi_bd[l, k], 128, 4)
            fw.act(cl[:], self.prm_sb[:, PCOLS["lam"][0]:PCOLS["lam"][0] + 4], AF.Exp, scale=-1.0, reads=["prm"], writes=["c_cl"])
            fw.act(cl[:], cl[:], AF.Ln, bias=1.0, reads=["c_cl"], writes=["c_cl"])
            fw.v("tensor_scalar", cl[:], cl[:], -8.0, None, ALU.mult, reads=["c_cl"], writes=["c_cl"])
            fw.v("memset", xbuf[:], 0.0, writes=["c_xbuf%d" % i for i in range(4)])
            fw.v("memset", hprev[:], 0.0, writes=["c_hprev%d" % i for i in range(4)])
            for g in range(G):
                c0 = g * 512
                hk = "c_hn%d" % (g % 2)
                hg = hn[g % 2]
                fw.dma(hg[:], self.hnT[:, :, c0:c0 + 512].rearrange("k p s -> p k s"), reads=[("hnT", g)], writes=[hk])
                def cbody(ct, g=g, c0=c0, hk=hk, hg=hg):
                    j = ct % 2
                    pb = 4 * j
                    px, pg, pr, pi = PS[pb], PS[pb + 1], PS[pb + 2], PS[pb + 3]
                    kx, kg, kr, ki = ["ps%d" % (pb + t) for t in range(4)]
                    for k in range(8):
                        fw.mm(px[:], wC[:, k, ct * 128:(ct + 1) * 128], hg[:, k, :], start=(k == 0), stop=(k == 7),
                              reads=[("wC", k), hk], writes=[kx])
                        yield
                    for k in range(8):
                        fw.mm(pg[:], wC[:, k, 512 + ct * 128:512 + (ct + 1) * 128], hg[:, k, :], start=(k == 0), stop=(k == 7),
                              reads=[("wC", k), hk], writes=[kg])
                        yield
                    xb = xbuf[:, ct, :]
                    fw.act(xb[:, 3:515], px[:], AF.Copy, reads=[kx], writes=["c_xbuf%d" % ct])
                    yield
                    cw = lambda i: self.pcol("conv_w", i * 4 + ct)
                    fw.v("tensor_scalar", xc[j][:], xb[:, 3:515], cw(3), self.pcol("conv_b", ct), ALU.mult, ALU.add,
                         reads=["c_xbuf%d" % ct, "prm"], writes=["c_xc%d" % j])
                    yield
                    for i in range(3):
                        fw.v("scalar_tensor_tensor", xc[j][:], xb[:, i:i + 512], cw(i), xc[j][:], ALU.mult, ALU.add,
                             reads=["c_xbuf%d" % ct, "prm", "c_xc%d" % j], writes=["c_xc%d" % j])
                        yield
                    fw.v("tensor_copy", xb[:, 0:3], xb[:, 512:515], reads=["c_xbuf%d" % ct], writes=["c_xbuf%d" % ct], eng="gpsimd")
                    yield
                    fw.v("tensor_copy", xcb[j][:], xc[j][:], reads=["c_xc%d" % j], writes=["c_xcb%d" % j], eng="gpsimd")
                    yield
                    fw.mm(pr[:], wr[:, ct, :], xcb[j][:], reads=[("c_wr", ct), "c_xcb%d" % j], writes=[kr])
                    yield
                    fw.mm(pi[:], wi[:, ct, :], xcb[j][:], reads=[("c_wi", ct), "c_xcb%d" % j], writes=[ki])
                    yield
                    fw.act(r_[j][:], pr[:], AF.Sigmoid, bias=self.pcol("b_r", ct), reads=[kr, "prm"], writes=["c_r%d" % j])
                    yield
                    fw.act(i_[j][:], pi[:], AF.Sigmoid, bias=self.pcol("b_i", ct), reads=[ki, "prm"], writes=["c_i%d" % j])
                    yield
                    fw.act(sg[j][:], pg[:], AF.Silu, reads=[kg], writes=["c_sg%d" % j])
                    yield
                    fw.act(a_[j][:], r_[j][:], AF.Exp, scale=cl[:, ct:ct + 1], reads=["c_r%d" % j, "c_cl"], writes=["c_a%d" % j])
                    yield
                    fw.v("tensor_tensor", b_[j][:], a_[j][:], a_[j][:], ALU.mult, reads=["c_a%d" % j], writes=["c_b%d" % j])
                    yield
                    fw.v("tensor_scalar", b_[j][:], b_[j][:], -1.0, 1.0, ALU.mult, ALU.add, reads=["c_b%d" % j], writes=["c_b%d" % j])
                    yield
                    fw.act(b_[j][:], b_[j][:], AF.Sqrt, reads=["c_b%d" % j], writes=["c_b%d" % j])
                    yield
                    fw.v("tensor_tensor", i_[j][:], i_[j][:], xc[j][:], ALU.mult, reads=["c_i%d" % j, "c_xc%d" % j],
                         writes=["c_i%d" % j], eng="gpsimd")
                    yield
                    fw.v("tensor_tensor", b_[j][:], b_[j][:], i_[j][:], ALU.mult, reads=["c_b%d" % j, "c_i%d" % j], writes=["c_b%d" % j])
                    yield
                    fw.v("tensor_tensor_scan", r_[j][:], a_[j][:], b_[j][:], hprev[:, ct:ct + 1], ALU.mult, ALU.add,
                         reads=["c_a%d" % j, "c_b%d" % j, "c_hprev%d" % ct, "c_r%d" % j], writes=["c_r%d" % j])
                    yield
                    fw.v("tensor_copy", hprev[:, ct:ct + 1], r_[j][:, 511:512], reads=["c_r%d" % j], writes=["c_hprev%d" % ct])
                    yield
                    fw.v("tensor_tensor", yo[j][:], r_[j][:], sg[j][:], ALU.mult, reads=["c_r%d" % j, "c_sg%d" % j],
                         writes=["c_y%d" % j], eng="gpsimd")
                    yield
                    fw.dma(self.yT[2][ct, :, c0:c0 + 512], yo[j][:], reads=["c_y%d" % j], writes=[("yT2", g, ct)], eng="gpsimd")
                    yield
                fw.lockstep([cbody(0), cbody(1)])
                fw.lockstep([cbody(2), cbody(3)])
            self.release(keys)

    def phase_M(self, l):
        fw, S, G, L = self.fw, self.S, self.G, self.L
        PS = self.PS
        last = (l == L - 1)
        with ExitStack() as ph:
            sb = lambda n, s, d: ph.enter_context(self.nc.sbuf_tensor(self.uname(n), list(s), d))
            wG = sb("wG", [128, 8, 3072], BF16)
            wbr = sb("wbr", [128, 12, 1024], BF16)
            wo = sb("wo", [128, 8, 1024], BF16)
            wpg = sb("wpg", [128, 8, 1024], BF16)
            wple = sb("wple", [128, 2, 1024], BF16)
            hn = sb("m_hn", [128, 8, 512], BF16)
            ys = [sb("m_y%d" % n, [128, 4, 512], BF16) for n in range(3)]
            hb = sb("m_h", [128, 8, 512], F32)
            h1b = sb("m_h1b", [128, 8, 512], BF16)
            pf = sb("m_pf", [128, 2, 512], F32)
            pb_ = sb("m_pb", [128, 2, 512], BF16)
            mrg = sb("m_mrg", [128, 8, 512], BF16)
            sgs = [sb("m_sg%d" % n, [128, 512], F32) for n in range(3)]
            tmp = sb("m_tmp", [128, 2, 512], F32)
            self.rs_sb = sb("m_rs", [128, 512], F32)
            self.hn_out = h1b
            self.hn_out_key = "m_h1b"
            self.eps_col = sb("m_eps", [128, 1], F32)
            names = ["wG", "wbr", "wo", "wpg", "wple", "m_hn", "m_y0", "m_y1", "m_y2", "m_h", "m_h1b", "m_pf", "m_pb",
                     "m_mrg", "m_sg0", "m_sg1", "m_sg2", ("m_tmp", 0), ("m_tmp", 1), "rs", "hn_out", "eps"]
            keys = names + [("wG", k) for k in range(8)] + [("wbr", k) for k in range(12)] + \
                [("wo", k) for k in range(8)] + [("wpg", k) for k in range(8)] + [("wple", k) for k in range(2)]
            self.acquire(keys + ["stg0", "stg1"])
            fw.v("memset", self.eps_col[:], NORM_EPS, writes=["eps"])
            self.load_w(wG, "wG", lambda k: self.w_in[l, k * 128:(k + 1) * 128, OFF_G:OFF_G + 3072], 3072, 8, self.gcol)
            self.load_w(wbr, "wbr", lambda k: self.w_branch[l, k // 4, (k % 4) * 128:(k % 4 + 1) * 128, :], 1024, 12)
            self.load_w(wo, "wo", lambda k: self.w_out[l, k * 128:(k + 1) * 128, :], 1024, 8)
            self.load_w(wpg, "wpg", lambda k: self.w_pg[l, k * 128:(k + 1) * 128, :], 1024, 8)
            self.load_w(wple, "wple", lambda k: self.w_ple[l, k * 128:(k + 1) * 128, :], 1024, 2)
            hsrc = self.xT if l == 0 else self.hT
            hdst = self.outT if last else self.hT
            for g in range(G):
                c0 = g * 512
                fw.dma(hn[:], self.hnT[:, :, c0:c0 + 512].rearrange("k p s -> p k s"), reads=[("hnT", g)], writes=["m_hn"])
                for n in range(3):
                    fw.dma(ys[n][:], self.yT[n][:, :, c0:c0 + 512].rearrange("k p s -> p k s"),
                           reads=[("yT%d" % n, g, ct) for ct in range(4)], writes=["m_y%d" % n])
                fw.dma(hb[:], hsrc[:, :, c0:c0 + 512].rearrange("k p s -> p k s"),
                       reads=([("hT", g)] if l > 0 else []), writes=["m_h"])
                fw.dma(pf[:], self.pT[l, :, :, c0:c0 + 512].rearrange("k p s -> p k s"), writes=["m_pf"])
                fw.v("tensor_copy", pb_[:], pf[:], reads=["m_pf"], writes=["m_pb"], eng="gpsimd")
                for dmt in range(8):
                    cs = slice(dmt * 128, (dmt + 1) * 128)
                    for n in range(3):
                        for k in range(8):
                            fw.mm(PS[n][:], wG[:, k, n * 1024 + dmt * 128:n * 1024 + (dmt + 1) * 128], hn[:, k, :],
                                  start=(k == 0), stop=(k == 7), reads=[("wG", k), "m_hn"], writes=["ps%d" % n])
                        for kc in range(4):
                            fw.mm(PS[3 + n][:], wbr[:, n * 4 + kc, cs], ys[n][:, kc, :], start=(kc == 0), stop=(kc == 3),
                                  reads=[("wbr", n * 4 + kc), "m_y%d" % n], writes=["ps%d" % (3 + n)])
                    for n in range(3):
                        fw.act(sgs[n][:], PS[n][:], AF.Sigmoid, reads=["ps%d" % n], writes=["m_sg%d" % n])
                        fw.v("tensor_tensor", sgs[n][:], PS[3 + n][:], sgs[n][:], ALU.mult,
                             reads=["ps%d" % (3 + n), "m_sg%d" % n], writes=["m_sg%d" % n])
                    fw.v("tensor_tensor", sgs[0][:], sgs[0][:], sgs[1][:], ALU.add, reads=["m_sg0", "m_sg1"], writes=["m_sg0"], eng="gpsimd")
                    fw.v("tensor_tensor", mrg[:, dmt, :], sgs[0][:], sgs[2][:], ALU.add, reads=["m_sg0", "m_sg2"], writes=["m_mrg"], eng="gpsimd")
                for d2 in range(8):
                    pk = 6 + d2 % 2
                    for k in range(8):
                        fw.mm(PS[pk][:], wo[:, k, d2 * 128:(d2 + 1) * 128], mrg[:, k, :], start=(k == 0), stop=(k == 7),
                              reads=[("wo", k), "m_mrg"], writes=["ps%d" % pk])
                    fw.v("tensor_tensor", hb[:, d2, :], hb[:, d2, :], PS[pk][:], ALU.add, reads=["m_h", "ps%d" % pk], writes=["m_h"])
                fw.act(h1b[:], hb[:], AF.Copy, reads=["m_h"], writes=["m_h1b"])
                for d2 in range(8):
                    pa, pp = (0, 1) if d2 % 2 == 0 else (2, 3)
                    for k in range(8):
                        fw.mm(PS[pa][:], wpg[:, k, d2 * 128:(d2 + 1) * 128], h1b[:, k, :], start=(k == 0), stop=(k == 7),
                              reads=[("wpg", k), "m_h1b"], writes=["ps%d" % pa])
                    for k in range(2):
                        fw.mm(PS[pp][:], wple[:, k, d2 * 128:(d2 + 1) * 128], pb_[:, k, :], start=(k == 0), stop=(k == 1),
                              reads=[("wple", k), "m_pb"], writes=["ps%d" % pp])
                    sgk = d2 % 2
                    fw.act(sgs[sgk][:], PS[pa][:], AF.Sigmoid, reads=["ps%d" % pa], writes=["m_sg%d" % sgk])
                    fw.v("tensor_tensor", sgs[sgk][:], PS[pp][:], sgs[sgk][:], ALU.mult, reads=["ps%d" % pp, "m_sg%d" % sgk],
                         writes=["m_sg%d" % sgk])
                    fw.v("tensor_tensor", hb[:, d2, :], hb[:, d2, :], sgs[sgk][:], ALU.add, reads=["m_h", "m_sg%d" % sgk],
                         writes=["m_h"], eng="gpsimd")
                fw.dma(hdst[:, :, c0:c0 + 512].rearrange("k p s -> p k s"), hb[:], reads=["m_h"],
                       writes=[("outT" if last else "hT", g)], eng="gpsimd")
                if not last:
                    self.norm_group(hb, "m_h", g, tmp, "m_tmp")
            self.release(keys)


_CACHE = {}


def make_in_maps(inp, S, L, ncores):
    maps = []
    w_in = np.ascontiguousarray(np.asarray(inp["w_in"], np.float32)[:L])
    ki0 = OFF_B + 1792
    w_kidup = np.ascontiguousarray(np.concatenate([w_in[:, :, ki0:ki0 + 64], w_in[:, :, ki0:ki0 + 64]], axis=2))
    prm = np.stack([pack_params(inp, l) for l in range(L)])
    cst = np.zeros((128, 32), np.float32)
    invf = (np.float32(500000.0) ** (-(np.arange(0, 16, 2, dtype=np.float32) / np.float32(16)))).astype(np.float32)
    for p_ in range(128):
        if p_ % 64 < 16:
            cst[p_, 0] = invf[p_ % 8]
    cst[:, 1:25] = (2.0 ** (-np.arange(24, dtype=np.float64)))[None, :].astype(np.float32)
    ropeR = np.zeros((128, 128), np.float32)
    for m_ in range(128):
        if m_ % 64 < 8:
            ropeR[m_ + 8, m_] = -1.0
        elif m_ % 64 < 16:
            ropeR[m_ - 8, m_] = 1.0
    shared = {
        "cst": cst, "ropeR": ropeR,
        "prm": prm, "w_in": w_in, "w_kidup": w_kidup,
        "w2": np.ascontiguousarray(np.asarray(inp["rwkv_w2"], np.float32)[:L]),
        "a2": np.ascontiguousarray(np.asarray(inp["rwkv_a2"], np.float32)[:L]),
        "wr_bd": np.stack([blockdiag(inp["lru_w_r"][l]) for l in range(L)]),
        "wi_bd": np.stack([blockdiag(inp["lru_w_i"][l]) for l in range(L)]),
        "w_branch": np.ascontiguousarray(np.asarray(inp["w_branch"], np.float32)[:L]),
        "w_out": np.ascontiguousarray(np.asarray(inp["w_out"], np.float32)[:L]),
        "w_ple": np.ascontiguousarray(np.asarray(inp["w_ple"], np.float32)[:L]),
        "w_pg": np.ascontiguousarray(np.asarray(inp["w_ple_gate"], np.float32)[:L]),
    }
    x = np.asarray(inp["x"], np.float32)
    p = np.asarray(inp["p"], np.float32)
    pos = np.asarray(inp["positions"], np.int32)
    nb = x.shape[0]
    for c in range(ncores):
        b = (c // 2) % nb
        m = dict(shared)
        m["xT"] = np.ascontiguousarray(x[b].T.reshape(8, 128, S))
        m["pT"] = np.ascontiguousarray(np.stack([p[l, b].T.reshape(2, 128, S) for l in range(L)]))
        m["pos"] = np.ascontiguousarray(pos[b].reshape(1, S))
        maps.append(m)
    return maps


def kernel(**inputs):
    x = np.asarray(inputs["x"])
    B, S, _ = x.shape
    L = np.asarray(inputs["w_in"]).shape[0]
    key = (S, L)
    if key not in _CACHE:
        _CACHE[key] = Prog(S, L).build()
    nc = _CACHE[key]
    maps = make_in_maps(inputs, S, L, 8)
    res = run_bass_kernel_spmd(nc, maps, core_ids=list(range(8)))
    out = np.zeros((B, S, D), np.float32)
    for b in range(B):
        out[b] = res.results[2 * b]["outT"].reshape(D, S).T
    return out
```

```python
from contextlib import ExitStack
import numpy as np
import concourse.bass as bass
import concourse.mybir as mybir
from concourse.bass_utils import run_bass_kernel_spmd

F32 = mybir.dt.float32
BF16 = mybir.dt.bfloat16
I32 = mybir.dt.int32
AF = mybir.ActivationFunctionType
ALU = mybir.AluOpType
AX = mybir.AxisListType

ENGS = ("tensor", "vector", "scalar", "gpsimd", "sync")
N_DMA_SEMS = 24

D = 1024
DIN = 8644
OFF_A, OFF_B, OFF_C, OFF_G = 0, 2176, 4548, 5572
NORM_EPS = 1e-6
GN_EPS = 64e-5


class FW:
    def __init__(self, nc, stack, same_engine_sync=True):
        self.nc = nc
        self.stack = stack
        self.q = {e: [] for e in ENGS}
        self.cnt = {e: 0 for e in ENGS}
        self.sem = {e: stack.enter_context(nc.semaphore("s_" + e)) for e in ENGS}
        self.dsem = [stack.enter_context(nc.semaphore("d%d" % i)) for i in range(N_DMA_SEMS)]
        self.dcnt = [0] * N_DMA_SEMS
        self.dnext = 0
        self.seen = {e: {} for e in ENGS}
        self.lastw = {}
        self.readers = {}
        self.same = same_engine_sync
        self.ninst = 0
        self.rr = 0

    def sb(self, name, shape, dt):
        return self.stack.enter_context(self.nc.sbuf_tensor(name, list(shape), dt))

    def ps(self, name, shape, dt=F32):
        return self.stack.enter_context(self.nc.psum_tensor(name, list(shape), dt))

    def _deps(self, eng, reads, writes):
        ev = []
        for k in reads:
            if k in self.lastw:
                ev.append(self.lastw[k])
        for k in writes:
            if k in self.lastw:
                ev.append(self.lastw[k])
            ev.extend(self.readers.get(k, ()))
        best = {}
        for (sname, sem, val, src) in ev:
            if src == eng and (eng == "tensor" or not self.same):
                continue
            if self.seen[eng].get(sname, 0) >= val:
                continue
            if sname not in best or best[sname][1] < val:
                best[sname] = (sem, val)
        waits = []
        for sname, (sem, val) in best.items():
            self.seen[eng][sname] = val
            waits.append((sem, val))
        return waits

    def _commit(self, event, reads, writes):
        for k in writes:
            self.lastw[k] = event
            self.readers[k] = []
        for k in reads:
            if k in writes:
                continue
            self.readers.setdefault(k, []).append(event)

    def op(self, eng, fn, reads=(), writes=()):
        waits = self._deps(eng, reads, writes)
        self.cnt[eng] += 1
        idx = self.cnt[eng]
        sem = self.sem[eng]
        self.q[eng].append((waits, fn, sem, 1))
        self._commit(("s_" + eng, sem, idx, eng), reads, writes)
        self.ninst += 1

    def dma(self, out, in_, reads=(), writes=(), eng="sync", **kw):
        lo, n = (0, 16) if eng == "sync" else (16, N_DMA_SEMS - 16)
        self.dnext_q = getattr(self, "dnext_q", {})
        i = self.dnext_q.get(eng, 0)
        self.dnext_q[eng] = (i + 1) % n
        slot = lo + i
        sem = self.dsem[slot]
        sname = "d%d" % slot
        waits = self._deps(eng, reads, writes)
        prev = self.dcnt[slot] * 16
        if prev and self.seen[eng].get(sname, 0) < prev:
            waits.append((sem, prev))
            self.seen[eng][sname] = prev
        self.dcnt[slot] += 1
        val = self.dcnt[slot] * 16
        self.q[eng].append((waits, lambda e: e.dma_start(out=out, in_=in_, **kw), sem, 16))
        self._commit((sname, sem, val, "dma"), reads, writes)
        self.ninst += 1

    def finish(self, keys, eng="sync"):
        waits = self._deps(eng, keys, ())
        self.q[eng].append((waits, None, None, 0))

    def emit(self):
        nc = self.nc
        with nc.Block() as block:
            for ename in ENGS:
                items = self.q[ename]
                if not items:
                    continue

                def body(e, items=items):
                    for waits, fn, sem, inc in items:
                        for (ws, wv) in waits:
                            e.wait_ge(ws, wv)
                        if fn is not None:
                            fn(e).then_inc(sem, inc)

                getattr(block, ename)(body)

    def mm(self, out, lhsT, rhs, start=True, stop=True, reads=(), writes=()):
        self.op("tensor", lambda e: e.matmul(out, lhsT, rhs, start=start, stop=stop), reads, writes)

    def tr(self, out, in_, ident, reads=(), writes=()):
        self.op("tensor", lambda e: e.transpose(out, in_, ident), reads, writes)

    def act(self, out, in_, func, bias=0.0, scale=1.0, reads=(), writes=(), accum_out=None):
        if accum_out is None:
            self.op("scalar", lambda e: e.activation(out, in_, func, bias=bias, scale=scale), reads, writes)
        else:
            self.op("scalar", lambda e: e.activation(out, in_, func, bias=bias, scale=scale,
                                                     accum_out=accum_out), reads, writes)

    def v(self, name, *args, reads=(), writes=(), eng="vector", **kw):
        self.op(eng, lambda e: getattr(e, name)(*args, **kw), reads, writes)

    @staticmethod
    def lockstep(gens):
        gens = list(gens)
        while gens:
            for g_ in list(gens):
                try:
                    next(g_)
                except StopIteration:
                    gens.remove(g_)

    def cast_eng(self):
        self.rr += 1
        return ("vector", "gpsimd")[self.rr % 2]


PCOLS = {}
_o = 0
for _n, _w in [("norm_g", 8), ("mu_r", 4), ("mu_k", 4), ("mu_v", 4), ("mu_g", 4), ("mu_wl", 1), ("mu_al", 1),
               ("w0", 4), ("a0", 4), ("k_k", 4), ("k_a", 4), ("gn_g", 4), ("gn_b", 4), ("r_k", 4),
               ("q_g", 1), ("k_g", 1),
               ("conv_w", 16), ("conv_b", 4), ("b_r", 4), ("b_i", 4), ("lam", 4)]:
    PCOLS[_n] = (_o, _w)
    _o += _w
NPRM = _o


def _col4(v):
    return np.ascontiguousarray(np.asarray(v, np.float32).reshape(4, 128).T)


def pack_params(inp, l):
    prm = np.zeros((128, NPRM), np.float32)

    def put(name, arr):
        o, w = PCOLS[name]
        prm[:arr.shape[0], o:o + w] = arr

    put("norm_g", np.asarray(inp["norm_g"][l], np.float32).reshape(8, 128).T)
    mu = np.asarray(inp["rwkv_mu"][l], np.float32)
    put("mu_r", _col4(mu[0:512])); put("mu_k", _col4(mu[512:1024])); put("mu_v", _col4(mu[1024:1536]))
    put("mu_wl", mu[1536:1600].reshape(64, 1)); put("mu_al", mu[1600:1664].reshape(64, 1))
    put("mu_g", _col4(mu[1664:2176]))
    put("w0", _col4(inp["rwkv_w0"][l])); put("a0", _col4(inp["rwkv_a0"][l]))
    put("k_k", _col4(inp["rwkv_k_k"][l])); put("k_a", _col4(inp["rwkv_k_a"][l]))
    put("gn_g", _col4(inp["rwkv_gn_g"][l])); put("gn_b", _col4(inp["rwkv_gn_b"][l]))
    put("r_k", _col4(np.asarray(inp["rwkv_r_k"][l]).reshape(512)))
    put("q_g", np.tile(np.asarray(inp["dsa_q_g"][l], np.float32), 2).reshape(128, 1))
    put("k_g", np.tile(np.asarray(inp["dsa_k_g"][l], np.float32), 2).reshape(128, 1))
    cw = np.asarray(inp["lru_conv_w"][l], np.float32)
    put("conv_w", np.concatenate([_col4(cw[i]) for i in range(4)], axis=1))
    put("conv_b", _col4(inp["lru_conv_b"][l])); put("b_r", _col4(inp["lru_b_r"][l]))
    put("b_i", _col4(inp["lru_b_i"][l])); put("lam", _col4(inp["lru_lambda"][l]))
    return prm


def blockdiag(w):
    w = np.asarray(w, np.float32)
    out = np.zeros((4, 128, 128), np.float32)
    for ct in range(4):
        out[ct, 0:64, 0:64] = w[2 * ct]
        out[ct, 64:128, 64:128] = w[2 * ct + 1]
    return out


class Prog:
    def __init__(self, S, L, phases="NACBM", dbg=()):
        self.S, self.L, self.phases, self.dbg = S, L, phases, dbg
        self.G = S // 512
        nc = self.nc = bass.Bass("TRN2", target_bir_lowering=False)
        dt = nc.dram_tensor
        self.xT = dt("xT", [8, 128, S], F32, kind="ExternalInput").ap()
        self.pT = dt("pT", [L, 2, 128, S], F32, kind="ExternalInput").ap()
        self.pos = dt("pos", [1, S], I32, kind="ExternalInput").ap()
        self.prm = dt("prm", [L, 128, NPRM], F32, kind="ExternalInput").ap()
        self.w_in = dt("w_in", [L, D, DIN], F32, kind="ExternalInput").ap()
        self.w_kidup = dt("w_kidup", [L, D, 128], F32, kind="ExternalInput").ap()
        self.w2 = dt("w2", [L, 64, 512], F32, kind="ExternalInput").ap()
        self.a2 = dt("a2", [L, 64, 512], F32, kind="ExternalInput").ap()
        self.wr_bd = dt("wr_bd", [L, 4, 128, 128], F32, kind="ExternalInput").ap()
        self.wi_bd = dt("wi_bd", [L, 4, 128, 128], F32, kind="ExternalInput").ap()
        self.w_branch = dt("w_branch", [L, 3, 512, D], F32, kind="ExternalInput").ap()
        self.w_out = dt("w_out", [L, D, D], F32, kind="ExternalInput").ap()
        self.w_ple = dt("w_ple", [L, 256, D], F32, kind="ExternalInput").ap()
        self.w_pg = dt("w_pg", [L, D, D], F32, kind="ExternalInput").ap()
        self.cst_d = dt("cst", [128, 32], F32, kind="ExternalInput").ap()
        self.ropeR_d = dt("ropeR", [128, 128], F32, kind="ExternalInput").ap()
        self.ropeT = dt("ropeT", [2, 128, S], F32, kind="Internal").ap()
        self.outT = dt("outT", [8, 128, S], F32, kind="ExternalOutput").ap()
        okind = lambda n: "ExternalOutput" if n in dbg else "Internal"
        self.hT = dt("hT", [8, 128, S], F32, kind=okind("hT")).ap()
        self.hnT = dt("hnT", [8, 128, S], BF16, kind=okind("hnT")).ap()
        self.yT = [dt("yT%d" % n, [4, 128, S], BF16, kind=okind("yT%d" % n)).ap() for n in range(3)]

    def uname(self, n):
        self._uid = getattr(self, "_uid", 0) + 1
        return "%s_u%d" % (n, self._uid)

    def pcol(self, name, j=0, rows=128):
        o, w = PCOLS[name]
        return self.prm_sb[0:rows, o + j:o + j + 1]

    def load_w(self, dst, key, src_fn, ncols, kt, scale=None, rows=128):
        fw = self.fw
        for k in range(kt):
            for c0 in range(0, ncols, 512):
                cn = min(512, ncols - c0)
                si = self.stg_i
                self.stg_i ^= 1
                stg = self.stg[si]
                fw.dma(stg[0:rows, 0:cn], src_fn(k)[:, c0:c0 + cn], writes=["stg%d" % si])
                self.cast_rr = getattr(self, "cast_rr", 0) + 1
                eng = ("vector", "scalar", "gpsimd")[self.cast_rr % 3]
                o_ap, i_ap = dst[0:rows, k, c0:c0 + cn], stg[0:rows, 0:cn]
                rk_ = ["stg%d" % si] + (["prm"] if scale is not None else [])
                if eng == "scalar":
                    fw.act(o_ap, i_ap, AF.Copy, scale=(scale(k) if scale is not None else 1.0), reads=rk_, writes=[(key, k)])
                elif scale is not None:
                    fw.v("tensor_scalar", o_ap, i_ap, scale(k), 0.0, ALU.mult, ALU.add, reads=rk_, writes=[(key, k)], eng=eng)
                else:
                    fw.v("tensor_copy", o_ap, i_ap, reads=rk_, writes=[(key, k)], eng=eng)

    def gcol(self, k):
        return self.pcol("norm_g", k)

    def build(self):
        nc = self.nc
        with ExitStack() as st:
            fw = self.fw = FW(nc, st)
            self.st = st
            self.stg = [fw.sb("stg%d" % i, [128, 512], F32) for i in range(2)]
            self.stg_i = 0
            self.prm_sb = fw.sb("prm_sb", [128, NPRM], F32)
            self.ones_f = fw.sb("ones_f", [128, 128], F32)
            fw.v("memset", self.ones_f[:], 1.0, writes=["ones_f"])
            self.PSALL = fw.ps("psall", [128, 8, 512], F32)
            self.PS = [self.PSALL[:, i, :] for i in range(8)]
            self.tiny_col = fw.sb("tiny_col", [128, 1], F32)
            self.gneps_col = fw.sb("gneps_col", [128, 1], F32)
            fw.v("memset", self.tiny_col[:], 1e-30, writes=["tiny"])
            fw.v("memset", self.gneps_col[:], GN_EPS, writes=["tiny"])
            self.eps6_col = fw.sb("eps6_col", [128, 1], F32)
            fw.v("memset", self.eps6_col[:], NORM_EPS, writes=["tiny"])
            self.cst_sb = fw.sb("cst_sb", [128, 32], F32)
            fw.dma(self.cst_sb[:], self.cst_d, writes=["cst"])
            self.make_consts()
            if "B" in self.phases:
                self.phase_R()
            with ExitStack() as zs:
                for n, ph_ in enumerate("ABC"):
                    if ph_ not in self.phases:
                        zt = zs.enter_context(self.nc.sbuf_tensor(self.uname("zt"), [128, 4, 512], BF16))
                        self.acquire(["zt%d" % n])
                        fw.v("memset", zt[:], 0.0, writes=["zt%d" % n])
                        for g in range(self.G):
                            fw.dma(self.yT[n][:, :, g * 512:(g + 1) * 512].rearrange("k p s -> p k s"), zt[:], reads=["zt%d" % n],
                                   writes=[("yT%d" % n, g, ct) for ct in range(4)])
                        self.release(["zt%d" % n])
            for l in range(self.L):
                fw.dma(self.prm_sb[:], self.prm[l], writes=["prm"])
                if l == 0 and "N" in self.phases:
                    self.phase_N0()
                if "A" in self.phases:
                    self.phase_A(l)
                if "C" in self.phases:
                    self.phase_C(l)
                if "B" in self.phases:
                    self.phase_B(l)
                if "M" in self.phases:
                    self.phase_M(l)
            fw.finish([("outT", g) for g in range(self.G)])
            fw.emit()
        return nc

    def norm_group(self, hbuf, hkey, g, tmp, tmpkey):
        fw, S = self.fw, self.S
        c0 = g * 512
        ps = self.PS[7]
        for k in range(8):
            fw.act(tmp[:, k % 2, :], hbuf[:, k, :], AF.Square, reads=[hkey], writes=[(tmpkey, k % 2)])
            fw.mm(ps[:], self.ones_f[:], tmp[:, k % 2, :], start=(k == 0), stop=(k == 7),
                  reads=["ones_f", (tmpkey, k % 2)], writes=["ps7"])
        rs = self.rs_sb
        fw.act(rs[:], ps[:], AF.Sqrt, bias=self.eps_col[:, 0:1], scale=1.0 / D, reads=["ps7", "eps"], writes=["rs"])
        fw.v("reciprocal", rs[:], rs[:], reads=["rs"], writes=["rs"])
        hn = self.hn_out
        fw.v("tensor_tensor", hn[:], hbuf[:], rs[:].unsqueeze(1).to_broadcast([128, 8, 512]), ALU.mult,
             reads=[hkey, "rs"], writes=[self.hn_out_key])
        fw.dma(self.hnT[:, :, c0:c0 + 512].rearrange("k p s -> p k s"), hn[:], reads=[self.hn_out_key],
               writes=[("hnT", g)], eng="gpsimd")

    def phase_N0(self):
        fw = self.fw
        with ExitStack() as ph:
            sb = lambda n, s, d: ph.enter_context(self.nc.sbuf_tensor(self.uname(n), list(s), d))
            hb = [sb("n0_h%d" % i, [128, 8, 512], F32) for i in range(2)]
            tmp = sb("n0_tmp", [128, 2, 512], F32)
            self.rs_sb = sb("n0_rs", [128, 512], F32)
            self.hn_out = sb("n0_hn", [128, 8, 512], BF16)
            self.hn_out_key = "hn_out"
            self.eps_col = sb("n0_eps", [128, 1], F32)
            self.acquire(["n0_h0", "n0_h1", ("n0_tmp", 0), ("n0_tmp", 1), "rs", "hn_out", "eps"])
            fw.v("memset", self.eps_col[:], NORM_EPS, writes=["eps"])
            for g in range(self.G):
                c0 = g * 512
                h = hb[g % 2]
                fw.dma(h[:], self.xT[:, :, c0:c0 + 512].rearrange("k p s -> p k s"), writes=["n0_h%d" % (g % 2)])
                self.norm_group(h, "n0_h%d" % (g % 2), g, tmp, "n0_tmp")
            self.release(["n0_h0", "n0_h1", ("n0_tmp", 0), ("n0_tmp", 1), "rs", "hn_out", "eps"])

    def release(self, keys):
        fw = self.fw
        ev = []
        for k in keys:
            if k in fw.lastw:
                ev.append(fw.lastw[k])
            ev.extend(fw.readers.get(k, ()))
        best = {}
        for e in getattr(fw, "pending_release", []) + ev:
            if e[0] not in best or best[e[0]][2] < e[2]:
                best[e[0]] = e
        fw.pending_release = list(best.values())

    def acquire(self, keys):
        fw = self.fw
        ev = getattr(fw, "pending_release", [])
        for k in keys:
            fw.readers.setdefault(k, []).extend(ev)


    def make_consts(self):
        fw = self.fw
        onesb = fw.sb("k_onesb", [128, 256], BF16)
        self.ident_b = fw.sb("k_ident", [128, 128], BF16)
        self.mask_ui = fw.sb("k_mask_ui", [128, 256], BF16)
        self.mask_sl = fw.sb("k_mask_sl", [128, 128], BF16)
        self.blk1 = fw.sb("k_blk1", [128, 128], F32)
        g = "gpsimd"
        fw.v("memset", onesb[:], 1.0, writes=["k_onesb"], eng=g)
        sel = lambda out, pat, cm, op, key: fw.op(g, lambda e: e.affine_select(out, onesb[:, 0:128], pat, op, 0.0, base=0,
                                                                                channel_multiplier=cm),
                                                  reads=["k_onesb"], writes=[key])
        sel(self.ident_b[:], [[-1, 128]], 1, ALU.is_equal, "k_ident")
        sel(self.mask_ui[:, 0:128], [[1, 128]], -1, ALU.is_gt, "k_mask_ui")
        sel(self.mask_ui[:, 128:256], [[1, 128]], -1, ALU.is_ge, "k_mask_ui")
        sel(self.mask_sl[:], [[-1, 128]], 1, ALU.is_gt, "k_mask_sl")
        fw.v("memset", self.blk1[:], 0.0, writes=["k_blk1"], eng=g)
        fw.v("memset", self.blk1[0:64, 0:64], 1.0, writes=["k_blk1"], eng=g)
        fw.v("memset", self.blk1[64:128, 64:128], 1.0, writes=["k_blk1"], eng=g)

    def phase_A(self, l):
        fw, S, G = self.fw, self.S, self.G
        PS = self.PS
        CDEC = 0.6065306597126334
        with ExitStack() as ph:
            allkeys = []

            def sb(n, s, d):
                allkeys.append(n)
                return ph.enter_context(self.nc.sbuf_tensor(self.uname(n), list(s), d))

            wA = sb("wA", [128, 8, 2176], BF16)
            w2b = sb("a_w2b", [64, 1, 512], BF16)
            a2b = sb("a_a2b", [64, 1, 512], BF16)
            hn = [sb("a_hn0", [128, 8, 512], BF16)] * 2
            omu = sb("a_omu", [128, NPRM], F32)
            prevc = sb("a_prevc", [128, 18], F32)
            ubP = [[sb("a_ub%d_%d" % (p, q), [128, 513], F32) for q in range(4)] for p in range(2)]
            usP = [[sb("a_us%d_%d" % (p, q), [128, 512], F32) for q in range(4)] for p in range(2)]
            ulo = [sb("a_ulo%d" % q, [64, 513], F32) for q in range(2)]
            twl = sb("a_twl", [64, 512], BF16)
            alb = sb("a_alb", [64, 512], BF16)
            tP = [[sb("a_t%d_%d" % (p, i), [128, 512], F32) for i in range(8)] for p in range(2)]
            t_ = tP[0]
            art = [sb("a_art%d" % ct, [128, 4, 2, 128], BF16) for ct in range(4)]
            bk = [sb("a_bk%d" % ct, [128, 2, 512], BF16) for ct in range(4)]
            vb = [sb("a_vb%d" % ct, [128, 512], BF16) for ct in range(4)]
            tok = [sb("a_tok%d" % ct, [128, 4, 3, 128], BF16) for ct in range(4)]
            bonus = [sb("a_bonus%d" % ct, [128, 512], BF16) for ct in range(4)]
            sgt = [sb("a_sg%d" % ct, [128, 512], BF16) for ct in range(4)]
            PC = sb("a_PC", [128, 4, 4], F32)
            T = sb("a_T", [128, 4, 64], F32)
            Tb = sb("a_Tb", [128, 4, 64], BF16)
            LAb = [sb("a_LAb%d" % i, [128, 4, 256], BF16) for i in range(2)]
            KAb = [sb("a_KAb%d" % i, [128, 4, 256], BF16) for i in range(2)]
            Lb = [sb("a_Lb%d" % i, [128, 4, 128], BF16) for i in range(2)]
            PPb = [[sb("a_PPb%d_%d" % (i, j), [128, 4, 256], BF16) for j in range(2)] for i in range(2)]
            XT = [[sb("a_XT%d_%d" % (i, j), [128, 4, 128], BF16) for j in range(2)] for i in range(2)]
            Wb = [sb("a_Wb%d" % i, [128, 4, 64], BF16) for i in range(2)]
            Ub = [sb("a_Ub%d" % i, [128, 4, 64], BF16) for i in range(2)]
            xc = sb("a_xc", [128, 8, 64], F32)
            sq = sb("a_sq", [128, 8, 64], F32)
            st8 = sb("a_st8", [128, 4, 8], F32)
            onb = sb("a_onb", [128, 512], BF16)
            yv = [sb("a_yv%d" % i, [128, 128], F32) for i in range(2)]
            yout = sb("a_yout", [128, 4, 512], BF16)
            self.rstm = sb("k_rstm", [128, 512], F32)
            keys = allkeys + [("wA", k) for k in range(8)] + [("a_w2b", 0), ("a_a2b", 0)]
            self.acquire(keys + ["stg0", "stg1"])
            fw.v("memset", self.rstm[:], 1.0, writes=["k_rstm"], eng="gpsimd")
            for c in range(4):
                fw.v("memset", self.rstm[:, c * 128:c * 128 + 1], 0.0, writes=["k_rstm"], eng="gpsimd")

            self.load_w(wA, "wA", lambda k: self.w_in[l, k * 128:(k + 1) * 128, OFF_A:OFF_A + 2176], 2176, 8, self.gcol)
            self.load_w(w2b, "a_w2b", lambda k: self.w2[l], 512, 1, rows=64)
            self.load_w(a2b, "a_a2b", lambda k: self.a2[l], 512, 1, rows=64)
            fw.v("tensor_scalar", omu[:], self.prm_sb[:], -1.0, 1.0, ALU.mult, ALU.add, reads=["prm"], writes=["a_omu"])
            fw.v("memset", prevc[:], 0.0, writes=["a_prevc"])
            fw.v("memset", T[:], 0.0, writes=["a_T"])
            fw.v("memset", Tb[:], 0.0, writes=["a_Tb"])
            oc = lambda name, j=0, rows=128: omu[0:rows, PCOLS[name][0] + j:PCOLS[name][0] + j + 1]
            psb0 = PS[0][:].bitcast(BF16)
            psb1 = PS[1][:].bitcast(BF16)

            def shift(ps, pskey, ubt, ubkey, pcol, out, okey, mu_ap, omu_ap, rows=128):
                fw.v("tensor_copy", ubt[0:rows, 0:1], prevc[0:rows, pcol:pcol + 1], reads=["a_prevc"], writes=[ubkey], eng="gpsimd")
                fw.act(ubt[0:rows, 1:513], ps, AF.Copy, reads=[pskey], writes=[ubkey])
                fw.v("tensor_copy", prevc[0:rows, pcol:pcol + 1], ubt[0:rows, 512:513], reads=[ubkey], writes=["a_prevc"], eng="gpsimd")
                fw.v("tensor_scalar", out, ubt[0:rows, 0:512], mu_ap, None, ALU.mult, reads=[ubkey, "prm"], writes=[okey])
                fw.v("scalar_tensor_tensor", out, ubt[0:rows, 1:513], omu_ap, out, ALU.mult, ALU.add,
                     reads=[ubkey, "a_omu", okey], writes=[okey])

            for g in range(G):
                c0 = g * 512
                hk = "a_hn0"
                hg = hn[0]
                fw.dma(hg[:], self.hnT[:, :, c0:c0 + 512].rearrange("k p s -> p k s"), reads=[("hnT", g)], writes=[hk])
                for q, (coff, nm) in enumerate([(1536, "mu_wl"), (1600, "mu_al")]):
                    for k in range(8):
                        fw.mm(PS[q][0:64, :], wA[:, k, coff:coff + 64], hg[:, k, :], start=(k == 0), stop=(k == 7),
                              reads=[("wA", k), hk], writes=["ps%d" % q])
                    shift(PS[q][0:64, :], "ps%d" % q, ulo[q], "a_ulo%d" % q, 16 + q, t_[q][0:64, :], "a_t0_%d" % q,
                          self.pcol(nm, 0, 64), oc(nm, 0, 64), rows=64)
                fw.act(twl[:], t_[0][0:64, :], AF.Tanh, reads=["a_t0_0"], writes=["a_twl"])
                fw.v("tensor_copy", alb[:], t_[1][0:64, :], reads=["a_t0_1"], writes=["a_alb"])
                def abody(ct, g=g, hk=hk, hg=hg):
                    p_ = ct % 2
                    PSp = PS[4 * p_:4 * p_ + 4]
                    pk = lambda q: "ps%d" % (4 * p_ + q)
                    ub, us, t_ = ubP[p_], usP[p_], tP[p_]
                    psb0 = PSp[0][:].bitcast(BF16)
                    psb1 = PSp[1][:].bitcast(BF16)
                    cs = slice(ct * 128, (ct + 1) * 128)
                    for q, (coff, nm) in enumerate([(0, "mu_r"), (512, "mu_k"), (1024, "mu_v"), (1664, "mu_g")]):
                        for k in range(8):
                            fw.mm(PSp[q][:], wA[:, k, coff + ct * 128:coff + (ct + 1) * 128], hg[:, k, :], start=(k == 0), stop=(k == 7),
                                  reads=[("wA", k), hk], writes=[pk(q)])
                            yield
                        shift(PSp[q][:], pk(q), ub[q], "a_ub%d_%d" % (p_, q), ct * 4 + q, us[q][:], "a_us%d_%d" % (p_, q),
                              self.pcol(nm, ct), oc(nm, ct))
                        yield
                    r_s, k_s, v_s, g_s = us
                    K = lambda i: "a_t%d_%d" % (p_, i)
                    fw.mm(PSp[0][:], w2b[:, 0, cs], twl[:], reads=[("a_w2b", 0), "a_twl"], writes=[pk(0)])
                    yield
                    fw.act(t_[0][:], PSp[0][:], AF.Sigmoid, bias=self.pcol("w0", ct), reads=[pk(0), "prm"], writes=[K(0)])
                    yield
                    fw.v("tensor_scalar", t_[0][:], t_[0][:], -CDEC, 0.0, ALU.mult, ALU.add, reads=[K(0)], writes=[K(0)], eng="gpsimd")
                    yield
                    fw.mm(PSp[1][:], a2b[:, 0, cs], alb[:], reads=[("a_a2b", 0), "a_alb"], writes=[pk(1)])
                    yield
                    fw.act(t_[1][:], PSp[1][:], AF.Sigmoid, bias=self.pcol("a0", ct), reads=[pk(1), "prm"], writes=[K(1)])
                    yield
                    fw.v("tensor_scalar", t_[2][:], k_s[:], self.pcol("k_k", ct), None, ALU.mult, reads=["a_us%d_1" % p_, "prm"], writes=[K(2)])
                    yield
                    fw.v("tensor_tensor", t_[3][:], t_[2][:], t_[2][:], ALU.mult, reads=[K(2)], writes=[K(3)], eng="gpsimd")
                    yield
                    fw.mm(PSp[2][:], self.blk1[:], t_[3][:], reads=["k_blk1", K(3)], writes=[pk(2)])
                    yield
                    fw.act(t_[3][:], PSp[2][:], AF.Sqrt, bias=self.tiny_col[:, 0:1], reads=[pk(2), "tiny"], writes=[K(3)])
                    yield
                    fw.v("reciprocal", t_[3][:], t_[3][:], reads=[K(3)], writes=[K(3)])
                    yield
                    fw.v("tensor_tensor", t_[2][:], t_[2][:], t_[3][:], ALU.mult, reads=[K(2), K(3)], writes=[K(2)])
                    yield
                    fw.v("tensor_scalar", t_[3][:], t_[1][:], self.pcol("k_a", ct), oc("k_a", ct), ALU.mult, ALU.add,
                         reads=[K(1), "prm", "a_omu"], writes=[K(3)])
                    yield
                    fw.v("tensor_tensor", t_[3][:], t_[3][:], k_s[:], ALU.mult, reads=[K(3), "a_us%d_1" % p_], writes=[K(3)], eng="gpsimd")
                    yield
                    fw.v("tensor_tensor", t_[4][:], t_[2][:], t_[1][:], ALU.mult, reads=[K(2), K(1)], writes=[K(4)], eng="gpsimd")
                    yield
                    fw.v("tensor_tensor_scan", t_[5][:], self.rstm[:], t_[0][:], 0.0, ALU.mult, ALU.add,
                         reads=["k_rstm", K(0)], writes=[K(5)])
                    yield
                    fw.v("tensor_tensor", t_[6][:], t_[5][:], t_[0][:], ALU.subtract, reads=[K(5), K(0)], writes=[K(6)], eng="gpsimd")
                    yield
                    fw.act(t_[6][:], t_[6][:], AF.Exp, reads=[K(6)], writes=[K(6)])
                    yield
                    fw.act(t_[7][:], t_[5][:], AF.Exp, scale=-1.0, reads=[K(5)], writes=[K(7)])
                    yield
                    fw.act(t_[5][:], t_[5][:], AF.Exp, reads=[K(5)], writes=[K(5)])
                    yield
                    fw.v("tensor_copy", PC[:, ct, :], t_[5][:].rearrange("p (c t) -> p c t", t=128)[:, :, 127], reads=[K(5)],
                         writes=["a_PC"], eng="gpsimd")
                    yield
                    v3 = lambda ap: ap.rearrange("p (c t) -> p c t", t=128)
                    akey = "a_art%d" % ct
                    fw.v("scalar_tensor_tensor", art[ct][:, :, 0, :], v3(t_[2][:]), -1.0, v3(t_[6][:]), ALU.mult, ALU.mult,
                         reads=[K(2), K(6)], writes=[akey])
                    yield
                    fw.v("tensor_tensor", art[ct][:, :, 1, :], v3(r_s[:]), v3(t_[5][:]), ALU.mult, reads=["a_us%d_0" % p_, K(5)], writes=[akey])
                    yield
                    fw.v("tensor_tensor", bk[ct][:, 0, :], t_[4][:], t_[7][:], ALU.mult, reads=[K(4), K(7)], writes=["a_bk%d" % ct])
                    yield
                    fw.v("tensor_tensor", bk[ct][:, 1, :], t_[3][:], t_[7][:], ALU.mult, reads=[K(3), K(7)], writes=["a_bk%d" % ct], eng="gpsimd")
                    yield
                    fw.v("tensor_copy", vb[ct][:], v_s[:], reads=["a_us%d_2" % p_], writes=["a_vb%d" % ct], eng="gpsimd")
                    yield
                    fw.v("scalar_tensor_tensor", t_[4][:], r_s[:], self.pcol("r_k", ct), t_[3][:], ALU.mult, ALU.mult,
                         reads=["a_us%d_0" % p_, "prm", K(3), K(4)], writes=[K(4)])
                    yield
                    fw.mm(PSp[3][:], self.blk1[:], t_[4][:], reads=["k_blk1", K(4)], writes=[pk(3)])
                    yield
                    fw.v("tensor_tensor", bonus[ct][:], PSp[3][:], v_s[:], ALU.mult, reads=[pk(3), "a_us%d_2" % p_], writes=["a_bonus%d" % ct])
                    yield
                    fw.act(sgt[ct][:], g_s[:], AF.Silu, reads=["a_us%d_3" % p_], writes=["a_sg%d" % ct])
                    yield
                    for half in range(2):
                        psb, pkey = (psb0, pk(0)) if half == 0 else (psb1, pk(1))
                        for cc in range(2):
                            c = half * 2 + cc
                            for qi_, (src, skey) in enumerate([(bk[ct][:, 0, c * 128:(c + 1) * 128], "a_bk%d" % ct),
                                                               (bk[ct][:, 1, c * 128:(c + 1) * 128], "a_bk%d" % ct),
                                                               (vb[ct][:, c * 128:(c + 1) * 128], "a_vb%d" % ct)]):
                                o = (cc * 3 + qi_) * 128
                                fw.tr(psb[:, o:o + 128], src, self.ident_b[:], reads=[skey, "k_ident"], writes=[pkey])
                                yield
                        fw.act(tok[ct][:, half * 2:half * 2 + 2, :, :].rearrange("p a b c -> p (a b c)"), psb[:, 0:768], AF.Copy,
                               reads=[pkey], writes=["a_tok%d" % ct])
                        yield

                fw.lockstep([abody(0), abody(1)])
                fw.lockstep([abody(2), abody(3)])
                PSALL = self.PSALL
                idb4 = self.ident_b[:].unsqueeze(1).to_broadcast([128, 4, 128])
                mui2 = self.mask_ui[:].unsqueeze(1).to_broadcast([128, 2, 256])
                msl4 = self.mask_sl[:].unsqueeze(1).to_broadcast([128, 4, 128])
                for c in range(4):
                    ccols = slice(c * 128, (c + 1) * 128)

                    def hv(qd, hi):
                        h = 2 * hi + qd
                        ct, hp = hi, qd
                        pr_ = slice(hp * 64, hp * 64 + 64)
                        d = dict(h=h, ct=ct, hp=hp, pr=pr_, po=hp * 64,
                                 at=art[ct][pr_, c, 0, :], rt=art[ct][pr_, c, 1, :],
                                 ar=art[ct][pr_, c, :, :].rearrange("p a t -> p (a t)"),
                                 bt=bk[ct][pr_, 0, ccols], kt=bk[ct][pr_, 1, ccols],
                                 rk=["a_art%d" % ct, "a_bk%d" % ct], tkey="a_tok%d" % ct,
                                 vt=tok[ct][:, c, 2, hp * 64:hp * 64 + 64], btk=tok[ct][:, c, 0, hp * 64:hp * 64 + 64],
                                 ktk=tok[ct][:, c, 1, hp * 64:hp * 64 + 64], T0b=Tb[pr_, ct, :])
                        return d

                    XYk = lambda qd: ["ps%d" % (3 * qd), "ps%d" % (3 * qd + 1)]
                    Zk = lambda qd: ["ps%d" % (3 * qd + 2)]
                    XY = lambda qd: PSALL[:, 3 * qd:3 * qd + 2, :].rearrange("p b (h x) -> p (b h) x", x=256)
                    Zv = lambda qd: PSALL[:, 3 * qd + 2, :].rearrange("p (h x) -> p h x", x=128)
                    import os as _os
                    _stop = int(_os.environ.get("A_STOP", "99"))
                    if _stop <= 0:
                        continue
                    for qd in range(2):
                        for hi in range(4):
                            d = hv(qd, hi)
                            fw.mm(XY(qd)[:, hi, :], d["bt"], d["ar"], reads=d["rk"], writes=[XYk(qd)[hi // 2]])
                            fw.mm(Zv(qd)[:, hi, :], d["at"], d["bt"], reads=d["rk"], writes=Zk(qd))
                    for qd in range(2):
                        for b2 in range(2):
                            fw.v("tensor_tensor", LAb[qd][:, 2 * b2:2 * b2 + 2, :], XY(qd)[:, 2 * b2:2 * b2 + 2, :], mui2, ALU.mult,
                                 reads=[XYk(qd)[b2], "k_mask_ui"], writes=["a_LAb%d" % qd])
                        fw.v("tensor_tensor", Lb[qd][:], Zv(qd), msl4, ALU.mult, reads=Zk(qd) + ["k_mask_sl"], writes=["a_Lb%d" % qd])
                        fw.v("tensor_tensor", XT[qd][0][:], LAb[qd][:, :, 0:128], idb4, ALU.add,
                             reads=["a_LAb%d" % qd, "k_ident"], writes=["a_XT%d_0" % qd], eng="gpsimd")
                    if _stop <= 1:
                        continue
                    for qd in range(2):
                        for hi in range(4):
                            d = hv(qd, hi)
                            fw.mm(XY(qd)[:, hi, :], d["kt"], d["ar"], reads=d["rk"], writes=[XYk(qd)[hi // 2]])
                    for qd in range(2):
                        for b2 in range(2):
                            fw.v("tensor_tensor", KAb[qd][:, 2 * b2:2 * b2 + 2, :], XY(qd)[:, 2 * b2:2 * b2 + 2, :], mui2, ALU.mult,
                                 reads=[XYk(qd)[b2], "k_mask_ui"], writes=["a_KAb%d" % qd])
                    if _stop <= 2:
                        continue
                    for k in range(1, 8):
                        for qd in range(2):
                            if k == 1:
                                Pp, PTp, pkeys = (lambda hi: Lb[qd][:, hi, :]), (lambda hi: LAb[qd][:, hi, 0:128]), ["a_Lb%d" % qd, "a_LAb%d" % qd]
                            else:
                                pb_ = PPb[qd][(k - 1) % 2]
                                Pp, PTp, pkeys = (lambda hi, pb_=pb_: pb_[:, hi, 0:128]), (lambda hi, pb_=pb_: pb_[:, hi, 128:256]), ["a_PPb%d_%d" % (qd, (k - 1) % 2)]
                            for hi in range(4):
                                if k <= 6:
                                    fw.mm(XY(qd)[:, hi, 0:128], PTp(hi), Pp(hi), reads=pkeys, writes=[XYk(qd)[hi // 2]])
                                    fw.mm(XY(qd)[:, hi, 128:256], Pp(hi), PTp(hi), reads=pkeys, writes=[XYk(qd)[hi // 2]])
                                if k >= 2:
                                    xo = XT[qd][(k - 2) % 2]
                                    xok = "a_XT%d_%d" % (qd, (k - 2) % 2)
                                    fw.mm(Zv(qd)[:, hi, :], self.ident_b[:], xo[:, hi, :], start=True, stop=False, reads=["k_ident", xok], writes=Zk(qd))
                                    fw.mm(Zv(qd)[:, hi, :], Pp(hi), xo[:, hi, :], start=False, stop=True, reads=pkeys + [xok], writes=Zk(qd))
                        for qd in range(2):
                            if k <= 6:
                                for b2 in range(2):
                                    fw.act(PPb[qd][k % 2][:, 2 * b2:2 * b2 + 2, :], XY(qd)[:, 2 * b2:2 * b2 + 2, :], AF.Copy,
                                           reads=[XYk(qd)[b2]], writes=["a_PPb%d_%d" % (qd, k % 2)])
                            if k >= 2:
                                fw.v("tensor_copy", XT[qd][(k - 1) % 2][:], Zv(qd), reads=Zk(qd), writes=["a_XT%d_%d" % (qd, (k - 1) % 2)])
                    if _stop <= 3:
                        continue
                    XTf = [XT[qd][0] for qd in range(2)]
                    xfk = ["a_XT%d_0" % qd for qd in range(2)]
                    Wv = lambda qd: PSALL[:, 3 * qd + 2, 0:256].rearrange("p (h x) -> p h x", x=64)
                    Uv = lambda qd: PSALL[:, 3 * qd + 2, 256:512].rearrange("p (h x) -> p h x", x=64)
                    for qd in range(2):
                        for hi in range(4):
                            d = hv(qd, hi)
                            fw.mm(Wv(qd)[:, hi, :], d["at"], d["T0b"], start=True, stop=False, reads=["a_art%d" % d["ct"], "a_Tb"], writes=Zk(qd))
                            fw.mm(Wv(qd)[:, hi, :], KAb[qd][:, hi, 0:128], d["vt"], start=False, stop=True, reads=["a_KAb%d" % qd, d["tkey"]], writes=Zk(qd))
                        fw.v("tensor_copy", Wb[qd][:], Wv(qd), reads=Zk(qd), writes=["a_Wb%d" % qd])
                    for qd in range(2):
                        for hi in range(4):
                            fw.mm(Uv(qd)[:, hi, :], XTf[qd][:, hi, :], Wb[qd][:, hi, :], reads=[xfk[qd], "a_Wb%d" % qd], writes=Zk(qd))
                        fw.v("tensor_copy", Ub[qd][:], Uv(qd), reads=Zk(qd), writes=["a_Ub%d" % qd])
                    for qd in range(2):
                        for hi in range(4):
                            d = hv(qd, hi)
                            h, ct = d["h"], d["ct"]
                            ob, okey = (PS[6], "ps6") if qd == 0 else (PS[7], "ps7")
                            osl = slice(qd * 256 + hi * 64, qd * 256 + (hi + 1) * 64)
                            fw.mm(ob[:, osl], d["rt"], d["T0b"], start=True, stop=False, reads=["a_art%d" % ct, "a_Tb"], writes=[okey])
                            fw.mm(ob[:, osl], LAb[qd][:, hi, 128:256], Ub[qd][:, hi, :], start=False, stop=False,
                                  reads=["a_LAb%d" % qd, "a_Ub%d" % qd], writes=[okey])
                            fw.mm(ob[:, osl], KAb[qd][:, hi, 128:256], d["vt"], start=False, stop=True, reads=["a_KAb%d" % qd, d["tkey"]], writes=[okey])
                            zsl = slice(ct * 64, (ct + 1) * 64)
                            fw.mm(PS[7][d["pr"], zsl], d["btk"], Ub[qd][:, hi, :], start=True, stop=False, reads=[d["tkey"], "a_Ub%d" % qd], writes=["ps7"])
                            fw.mm(PS[7][d["pr"], zsl], d["ktk"], d["vt"], start=False, stop=True, reads=[d["tkey"]], writes=["ps7"])
                    if _stop <= 4:
                        continue
                    zall = PS[7][:, 0:256].rearrange("p (c i) -> p c i", i=64)
                    fw.v("tensor_tensor", T[:], T[:], zall, ALU.add, reads=["a_T", "ps7"], writes=["a_T"])
                    fw.v("tensor_tensor", T[:], T[:], PC[:, :, c:c + 1].to_broadcast([128, 4, 64]), ALU.mult, reads=["a_T", "a_PC"], writes=["a_T"])
                    fw.v("tensor_copy", Tb[:], T[:], reads=["a_T"], writes=["a_Tb"], eng="gpsimd")
                    ov = [PS[6][:, 0:256].rearrange("p (h i) -> p h i", i=64), PS[7][:, 256:512].rearrange("p (h i) -> p h i", i=64)]
                    okeys = ["ps6", "ps7"]
                    for qd in range(2):
                        fw.v("tensor_reduce", st8[:, 0, qd * 4:qd * 4 + 4], ov[qd], AX.X, ALU.add, reads=[okeys[qd]], writes=["a_st8"])
                    fw.v("tensor_scalar", st8[:, 0, :], st8[:, 0, :], 1.0 / 64, None, ALU.mult, reads=["a_st8"], writes=["a_st8"])
                    for qd in range(2):
                        fw.v("tensor_tensor", xc[:, qd * 4:qd * 4 + 4, :], ov[qd],
                             st8[:, 0, qd * 4:qd * 4 + 4].unsqueeze(2).to_broadcast([128, 4, 64]), ALU.subtract,
                             reads=[okeys[qd], "a_st8"], writes=["a_xc"])
                    fw.v("tensor_tensor", sq[:], xc[:], xc[:], ALU.mult, reads=["a_xc"], writes=["a_sq"], eng="gpsimd")
                    fw.v("tensor_reduce", st8[:, 1, :], sq[:], AX.X, ALU.add, reads=["a_sq"], writes=["a_st8"])
                    fw.act(st8[:, 1, :], st8[:, 1, :], AF.Sqrt, bias=self.gneps_col[:, 0:1], scale=1.0 / 64, reads=["a_st8", "tiny"], writes=["a_st8"])
                    fw.v("reciprocal", st8[:, 1, :], st8[:, 1, :], reads=["a_st8"], writes=["a_st8"])
                    fw.v("tensor_tensor", onb[:].rearrange("p (h i) -> p h i", i=64), xc[:],
                         st8[:, 1, :].unsqueeze(2).to_broadcast([128, 8, 64]), ALU.mult, reads=["a_xc", "a_st8"], writes=["a_onb"])
                    for ct in range(4):
                        for hp in range(2):
                            qo = (hp * 4 + ct) * 64
                            fw.tr(psb0[hp * 64:(hp + 1) * 64, ct * 128:(ct + 1) * 128], onb[:, qo:qo + 64], self.ident_b[:],
                                  reads=["a_onb", "k_ident"], writes=["ps0"])
                    for ct in range(4):
                        j = ct % 2
                        fw.v("tensor_scalar", yv[j][:], psb0[:, ct * 128:(ct + 1) * 128], self.pcol("gn_g", ct), self.pcol("gn_b", ct),
                             ALU.mult, ALU.add, reads=["ps0", "prm"], writes=["a_yv%d" % j])
                        fw.v("tensor_tensor", yv[j][:], yv[j][:], bonus[ct][:, ccols], ALU.add, reads=["a_yv%d" % j, "a_bonus%d" % ct],
                             writes=["a_yv%d" % j], eng="gpsimd")
                        fw.v("tensor_tensor", yout[:, ct, ccols], yv[j][:], sgt[ct][:, ccols], ALU.mult,
                             reads=["a_yv%d" % j, "a_sg%d" % ct], writes=["a_yout"], eng="gpsimd")
                fw.dma(self.yT[0][:, :, c0:c0 + 512].rearrange("k p s -> p k s"), yout[:], reads=["a_yout"],
                       writes=[("yT0", g, ct) for ct in range(4)], eng="gpsimd")
            self.release(keys)


    def phase_R(self):
        fw, S, G = self.fw, self.S, self.G
        TWO_PI = 6.283185307179586
        C1 = 6.28125
        C2 = 0.0019350051879882812
        C3 = TWO_PI - C1 - C2
        PI = 3.1415925
        with ExitStack() as ph:
            sb = lambda n, s, d: ph.enter_context(self.nc.sbuf_tensor(self.uname(n), list(s), d))
            posi = sb("r_posi", [128, 512], I32)
            a = sb("r_a", [128, 512], F32)
            k = sb("r_k", [128, 512], F32)
            r = sb("r_r", [128, 512], F32)
            r2 = sb("r_r2", [128, 512], F32)
            m = sb("r_m", [128, 512], F32)
            cs = sb("r_cs", [128, 2, 512], F32)
            keys = ["r_posi", "r_a", "r_k", "r_r", "r_r2", "r_m", "r_cs"]
            self.acquire(keys)
            for g in range(G):
                c0 = g * 512
                fw.dma(posi[:], self.pos[0:1, c0:c0 + 512].to_broadcast([128, 512]), writes=["r_posi"])
                fw.v("tensor_copy", a[:], posi[:], reads=["r_posi"], writes=["r_a"])
                fw.v("tensor_scalar", a[:], a[:], self.cst_sb[:, 0:1], None, ALU.mult, reads=["r_a", "cst"], writes=["r_a"])
                fw.v("tensor_scalar", k[:], a[:], 1.0 / TWO_PI, None, ALU.mult, reads=["r_a"], writes=["r_k"])
                fw.v("tensor_scalar", k[:], k[:], 12582912.0, None, ALU.add, reads=["r_k"], writes=["r_k"])
                fw.v("tensor_scalar", k[:], k[:], 12582912.0, None, ALU.subtract, reads=["r_k"], writes=["r_k"])
                fw.v("scalar_tensor_tensor", r[:], k[:], -C1, a[:], ALU.mult, ALU.add, reads=["r_k", "r_a"], writes=["r_r"])
                fw.v("scalar_tensor_tensor", r[:], k[:], -C2, r[:], ALU.mult, ALU.add, reads=["r_k", "r_r"], writes=["r_r"])
                fw.v("scalar_tensor_tensor", r[:], k[:], -C3, r[:], ALU.mult, ALU.add, reads=["r_k", "r_r"], writes=["r_r"])
                fw.v("tensor_scalar", r[:], r[:], PI, -PI, ALU.min, ALU.max, reads=["r_r"], writes=["r_r"])
                fw.v("tensor_scalar", r2[:], r[:], TWO_PI / 4, None, ALU.add, reads=["r_r"], writes=["r_r2"])
                fw.v("tensor_scalar", m[:], r2[:], PI, -TWO_PI, ALU.is_gt, ALU.mult, reads=["r_r2"], writes=["r_m"])
                fw.v("tensor_tensor", r2[:], r2[:], m[:], ALU.add, reads=["r_r2", "r_m"], writes=["r_r2"])
                fw.v("tensor_scalar", r2[:], r2[:], PI, -PI, ALU.min, ALU.max, reads=["r_r2"], writes=["r_r2"])
                fw.act(cs[:, 0, :], r2[:], AF.Sin, reads=["r_r2"], writes=["r_cs"])
                fw.act(cs[:, 1, :], r[:], AF.Sin, reads=["r_r"], writes=["r_cs"])
                fw.dma(self.ropeT[:, :, c0:c0 + 512].rearrange("k p s -> p k s"), cs[:], reads=["r_cs"], writes=[("ropeT", g)], eng="gpsimd")
            self.release(keys)

    def phase_B(self, l):
        fw, S, G = self.fw, self.S, self.G
        PS = self.PS
        NT = S // 128
        import os as _os
        NIT = int(_os.environ.get('B_NIT', '20'))
        NOATT = _os.environ.get('B_NOATT') == '1'
        with ExitStack() as ph:
            allkeys = []

            def sb(n, s, d):
                allkeys.append(n)
                return ph.enter_context(self.nc.sbuf_tensor(self.uname(n), list(s), d))

            wB = sb("wB", [128, 8, 2372], BF16)
            wkd = sb("b_wkd", [128, 8, 128], BF16)
            ropeR = sb("b_ropeR", [128, 1, 128], BF16)
            KT = [sb("b_KT%d" % ct, [128, S], BF16) for ct in range(4)]
            KI = sb("b_KI", [128, S], BF16)
            V = sb("b_V", [128, NT, 8, 65], BF16)
            hn = sb("b_hn", [128, 8, 512], BF16)
            QT = [[sb("b_QT%d_%d" % (ct, i), [128, 512], BF16) for ct in range(4)] for i in range(2)]
            QI = [[sb("b_QI%d_%d" % (j, i), [128, 512], BF16) for j in range(2)] for i in range(2)]
            SG = [[sb("b_SG%d_%d" % (ct, i), [128, 512], BF16) for ct in range(4)] for i in range(2)]
            WI = [sb("b_WI%d" % i, [128, 4, 4], F32) for i in range(2)]
            yout = [sb("b_yout0", [128, 4, 128], BF16)] * 2
            score = sb("b_score", [128, S], F32)
            alias = False
            if alias:
                xL = [score[:, 0:512], score[:, 1280:1792]]
                x2L = [score[:, 512:1024], score[:, 1792:2304]]
                xbL = [score[:, 1024:1280].bitcast(BF16), score[:, 2304:2560].bitcast(BF16)]
                cs = score[:, 2560:3584].rearrange("p (a b) -> p a b", b=512)
            else:
                cs = sb("b_cs", [128, 2, 512], F32)
                xL = [sb("b_x%d" % i, [128, 512], F32) for i in range(2)]
                x2L = [sb("b_x2%d" % i, [128, 512], F32) for i in range(2)]
                xbL = [sb("b_xb%d" % i, [128, 512], BF16) for i in range(2)]
            tkeys = ["b_cs"] + ["b_x%d" % i for i in range(2)] + ["b_x2%d" % i for i in range(2)] + ["b_xb%d" % i for i in range(2)]
            mm1 = [sb("b_mm1_0", [128, S], BF16)] * 2
            MT = [sb("b_MT0", [128, NT, 128], BF16)] * 2
            E = [sb("b_E%d" % i, [128, 512], BF16) for i in range(4)]
            rl = [sb("b_rl%d" % i, [128, 512], F32) for i in range(2)]
            bs = sb("b_bs", [128, 8], F32)
            steps = sb("b_steps", [128, NIT], F32)
            rec = sb("b_rec", [128, 8], F32)
            otok = sb("b_otok", [128, 8, 64], BF16)
            dmask = sb("b_dmask", [128, 128], F32)
            keys = allkeys + tkeys + [("wB", k) for k in range(8)] + [("b_wkd", k) for k in range(8)] + [("b_ropeR", 0)] + \
                [("b_KT", ct, g) for ct in range(4) for g in range(G)] + [("b_KI", g) for g in range(G)] + [("b_V", g) for g in range(G)]
            self.acquire(keys + ["stg0", "stg1"])
            self.load_w(wB, "wB", lambda k: self.w_in[l, k * 128:(k + 1) * 128, OFF_B:OFF_B + 2372], 2372, 8, self.gcol)
            self.load_w(wkd, "b_wkd", lambda k: self.w_kidup[l, k * 128:(k + 1) * 128, :], 128, 8, self.gcol)
            self.load_w(ropeR, "b_ropeR", lambda k: self.ropeR_d, 128, 1)
            fw.v("memset", V[:], 1.0, writes=[("b_V", g) for g in range(G)], eng="gpsimd")
            fw.v("memset", dmask[:], 0.0, writes=["b_dmask"], eng="gpsimd")
            fw.v("memset", dmask[0:64, 64:128], -1e30, writes=["b_dmask"], eng="gpsimd")

            def lane(p, chains):
                x_, x2, xb = xL[p], x2L[p], xbL[p]
                kx, kx2, kxb = "b_x%d" % p, "b_x2%d" % p, "b_xb%d" % p
                ps, pk = PS[p], "ps%d" % p

                def proj(w, wkey, c_lo, c_hi):
                    for k in range(8):
                        fw.mm(ps[:], w[:, k, c_lo:c_hi], hn[:, k, :], start=(k == 0), stop=(k == 7), reads=[(wkey, k), "b_hn"], writes=[pk])
                        yield

                def rope(dst, dkey):
                    fw.v("tensor_copy", xb[:], x_[:], reads=[kx], writes=[kxb], eng="gpsimd")
                    yield
                    fw.mm(ps[:], ropeR[:, 0, :], xb[:], reads=[("b_ropeR", 0), kxb], writes=[pk])
                    yield
                    fw.v("tensor_tensor", x2[:], x_[:], cs[:, 0, :], ALU.mult, reads=[kx, "b_cs"], writes=[kx2], eng="gpsimd")
                    yield
                    fw.v("tensor_tensor", x_[:], ps[:], cs[:, 1, :], ALU.mult, reads=[pk, "b_cs", kx], writes=[kx])
                    yield
                    fw.v("tensor_tensor", dst, x2[:], x_[:], ALU.add, reads=[kx2, kx], writes=[dkey], eng="gpsimd")
                    yield

                for ch in chains:
                    kind = ch[0]
                    if kind in ("q", "k"):
                        _, ct, g = ch
                        gp = g % 2
                        gc = slice(g * 512, g * 512 + 512)
                        coff, gname = (0, "q_g") if kind == "q" else (512, "k_g")
                        yield from proj(wB, "wB", coff + ct * 128, coff + (ct + 1) * 128)
                        fw.act(x_[:], ps[:], AF.Copy, reads=[pk], writes=[kx])
                        yield
                        fw.v("tensor_tensor", x2[:], x_[:], x_[:], ALU.mult, reads=[kx], writes=[kx2], eng="gpsimd")
                        yield
                        fw.mm(ps[:], self.blk1[:], x2[:], reads=["k_blk1", kx2], writes=[pk])
                        yield
                        fw.act(x2[:], ps[:], AF.Sqrt, bias=self.eps6_col[:, 0:1], scale=1.0 / 64, reads=[pk, "tiny"], writes=[kx2])
                        yield
                        fw.v("reciprocal", x2[:], x2[:], reads=[kx2], writes=[kx2])
                        yield
                        fw.v("scalar_tensor_tensor", x_[:], x_[:], self.pcol(gname, 0), x2[:], ALU.mult, ALU.mult,
                             reads=[kx, "prm", kx2], writes=[kx])
                        yield
                        if kind == "q":
                            yield from rope(QT[gp][ct][:], "b_QT%d_%d" % (ct, gp))
                        else:
                            yield from rope(KT[ct][:, gc], ("b_KT", ct, g))
                    elif kind == "qi":
                        _, j, g = ch
                        gp = g % 2
                        yield from proj(wB, "wB", 1536 + j * 128, 1536 + (j + 1) * 128)
                        fw.act(x_[:], ps[:], AF.Copy, reads=[pk], writes=[kx])
                        yield
                        yield from rope(QI[gp][j][:], "b_QI%d_%d" % (j, gp))
                    elif kind == "ki":
                        _, g = ch
                        gc = slice(g * 512, g * 512 + 512)
                        yield from proj(wkd, "b_wkd", 0, 128)
                        fw.act(x_[:], ps[:], AF.Copy, reads=[pk], writes=[kx])
                        yield
                        yield from rope(KI[:, gc], ("b_KI", g))
                    elif kind == "sg":
                        _, ct, g = ch
                        gp = g % 2
                        yield from proj(wB, "wB", 1860 + ct * 128, 1860 + (ct + 1) * 128)
                        fw.act(SG[gp][ct][:], ps[:], AF.Silu, reads=[pk], writes=["b_SG%d_%d" % (ct, gp)])
                        yield
                    elif kind == "v":
                        _, tt, g = ch
                        gp = g % 2
                        tcols = slice(tt * 128, (tt + 1) * 128)
                        for k in range(8):
                            fw.mm(ps[:], hn[:, k, tcols], wB[:, k, 1024:1536], start=(k == 0), stop=(k == 7),
                                  reads=[("wB", k), "b_hn"], writes=[pk])
                            yield
                        fw.act(V[:, g * 4 + tt, :, 0:64], ps[:].rearrange("p (h i) -> p h i", i=64), AF.Copy, reads=[pk], writes=[("b_V", g)])
                        yield
                        for k in range(8):
                            fw.mm(ps[:, 0:4], hn[:, k, tcols], wB[:, k, 1856:1860], start=(k == 0), stop=(k == 7),
                                  reads=[("wB", k), "b_hn"], writes=[pk])
                            yield
                        fw.v("tensor_scalar", WI[gp][:, tt, :], ps[:, 0:4], 1.0 / 16, None, ALU.mult, reads=[pk], writes=["b_WI%d" % gp])
                        yield

            def prep_begin(g):
                gc = slice(g * 512, g * 512 + 512)
                if alias:
                    self.release(["b_score"])
                    self.acquire(tkeys)
                fw.dma(hn[:], self.hnT[:, :, gc].rearrange("k p s -> p k s"), reads=[("hnT", g)], writes=["b_hn"])
                fw.dma(cs[:], self.ropeT[:, :, gc].rearrange("k p s -> p k s"), reads=[("ropeT", g)], writes=["b_cs"])

            def prep_lanes(g):
                chains = []
                for ct in range(4):
                    chains += [("q", ct, g), ("k", ct, g)]
                chains += [("qi", 0, g), ("qi", 1, g), ("ki", g)]
                chains += [("sg", ct, g) for ct in range(4)]
                chains += [("v", tt, g) for tt in range(4)]
                return [lane(0, chains[0::2]), lane(1, chains[1::2])]

            def prep_end(g):
                if alias:
                    self.release(tkeys)
                    self.acquire(["b_score"])

            SB_ = [2, 4, 3, 5]

            def scores(qt):
                g, tt = qt // 4, qt % 4
                gp = g % 2
                N = (qt + 1) * 128
                tq = slice(tt * 128, (tt + 1) * 128)
                for pc in range((N + 511) // 512):
                    p0 = pc * 512
                    pn = min(512, N - p0)
                    for ih in range(4):
                        po = (ih % 2) * 64
                        fw.mm(PS[SB_[ih]][:, 0:pn], QI[gp][ih // 2][po:po + 64, tq], KI[po:po + 64, p0:p0 + pn],
                              reads=["b_QI%d_%d" % (ih // 2, gp), ("b_KI", pc)], writes=["ps%d" % SB_[ih]])
                    for ih in range(4):
                        r_ = rl[ih % 2]
                        rkey = "b_rl%d" % (ih % 2)
                        fw.act(r_[:, 0:pn], PS[SB_[ih]][:, 0:pn], AF.Relu, reads=["ps%d" % SB_[ih]], writes=[rkey])
                        if ih == 0:
                            fw.v("tensor_scalar", score[:, p0:p0 + pn], r_[:, 0:pn], WI[gp][:, tt, 0:1], None, ALU.mult,
                                 reads=[rkey, "b_WI%d" % gp], writes=["b_score"])
                        else:
                            fw.v("scalar_tensor_tensor", score[:, p0:p0 + pn], r_[:, 0:pn], WI[gp][:, tt, ih:ih + 1], score[:, p0:p0 + pn],
                                 ALU.mult, ALU.add, reads=[rkey, "b_WI%d" % gp, "b_score"], writes=["b_score"])

            def bisect_mask(qt):
                NB = qt + 1
                N = NB * 128
                mk = mm1[0]
                mkey = "b_mm1_0"
                A, lo, mid, cnt, tmp = (bs[:, i:i + 1] for i in range(5))
                if NB >= 3:
                    fw.v("tensor_reduce", A, score[:, 0:N], AX.X, ALU.max, apply_absolute_value=True, reads=["b_score"], writes=["b_bs"])
                    fw.v("tensor_scalar", A, A, 1.0001, 1e-20, ALU.mult, ALU.add, reads=["b_bs"], writes=["b_bs"])
                fw.v("tensor_tensor", score[:, N - 128:N], score[:, N - 128:N], dmask[:], ALU.add, reads=["b_score", "b_dmask"],
                     writes=["b_score"])
                if NB >= 3:
                    fw.v("tensor_scalar", steps[:], self.cst_sb[:, 1:1 + NIT], A, None, ALU.mult, reads=["cst", "b_bs"], writes=["b_steps"])
                    fw.v("tensor_scalar", lo, A, -1.0, None, ALU.mult, reads=["b_bs"], writes=["b_bs"])
                    for it in range(NIT):
                        fw.v("tensor_tensor", mid, lo, steps[:, it:it + 1], ALU.add, reads=["b_bs", "b_steps"], writes=["b_bs"])
                        fw.v("tensor_scalar", mk[:, 0:N], score[:, 0:N], mid, None, ALU.is_ge, ALU.add, accum_out=cnt,
                             reads=["b_score", "b_bs", mkey], writes=[mkey, "b_bs"])
                        fw.v("tensor_scalar", tmp, cnt, 255.5, steps[:, it:it + 1], ALU.is_ge, ALU.mult, reads=["b_bs", "b_steps"], writes=["b_bs"])
                        fw.v("tensor_tensor", lo, lo, tmp, ALU.add, reads=["b_bs"], writes=["b_bs"])
                else:
                    fw.v("memset", lo, -1e29, writes=["b_bs"])
                fw.v("tensor_scalar", mk[:, 0:N], score[:, 0:N], lo, None, ALU.is_ge, reads=["b_score", "b_bs"], writes=[mkey])
                psb1 = PS[2][:].bitcast(BF16)
                mt, mtkey = MT[0], "b_MT0"
                for kb0 in range(0, NB, 8):
                    nk = min(8, NB - kb0)
                    for j in range(nk):
                        kb = kb0 + j
                        fw.tr(psb1[:, j * 128:(j + 1) * 128], mk[:, kb * 128:(kb + 1) * 128], self.ident_b[:],
                              reads=[mkey, "k_ident"], writes=["ps2"])
                    fw.act(mt[:, kb0:kb0 + nk, :].rearrange("p a b -> p (a b)"), psb1[:, 0:nk * 128], AF.Copy, reads=["ps2"], writes=[mtkey])

            def attention_gen(qt):
                if NOATT:
                    return
                g, tt = qt // 4, qt % 4
                gp = g % 2
                NB = qt + 1
                tq = slice(tt * 128, (tt + 1) * 128)
                mt, mtkey = MT[0], "b_MT0"
                for hpair in range(4):
                    ct = hpair
                    for gi, kb0 in enumerate(range(0, NB, 4)):
                        nk = min(4, NB - kb0)
                        bis = [2 * e + gi % 2 for e in range(2)]
                        for j in range(nk):
                            kb = kb0 + j
                            for e in range(2):
                                po = e * 64
                                pl = PS[2 + bis[e]]
                                fw.mm(pl[:, j * 128:(j + 1) * 128], KT[ct][po:po + 64, kb * 128:(kb + 1) * 128], QT[gp][ct][po:po + 64, tq],
                                      reads=[("b_KT", ct, kb // 4), "b_QT%d_%d" % (ct, gp)], writes=["ps%d" % (2 + bis[e])])
                                yield
                        for e in range(2):
                            bi = bis[e]
                            fw.act(E[bi][:, 0:nk * 128], PS[2 + bi][:, 0:nk * 128], AF.Exp, scale=0.125, reads=["ps%d" % (2 + bi)], writes=["b_E%d" % bi])
                            yield
                            fw.v("tensor_tensor", E[bi][:, 0:nk * 128], E[bi][:, 0:nk * 128],
                                 mt[:, kb0:kb0 + nk, :].rearrange("p a b -> p (a b)"), ALU.mult,
                                 reads=["b_E%d" % bi, mtkey], writes=["b_E%d" % bi], eng="gpsimd")
                            yield
                        for e in range(2):
                            h = 2 * hpair + e
                            bi = bis[e]
                            pob = PS[7] if e == 0 else PS[6]
                            pokey = "ps7" if e == 0 else "ps6"
                            osl = slice(hpair * 65, hpair * 65 + 65)
                            for j in range(nk):
                                kb = kb0 + j
                                fw.mm(pob[:, osl], E[bi][:, j * 128:(j + 1) * 128], V[:, kb, h, :], start=(kb == 0), stop=(kb == NB - 1),
                                      reads=["b_E%d" % bi, ("b_V", kb // 4)], writes=[pokey])
                                yield

            def final(qt):
                g, tt = qt // 4, qt % 4
                gp = g % 2
                tq = slice(tt * 128, (tt + 1) * 128)
                otok4 = otok[:].rearrange("p (a e) i -> p a e i", e=2)
                for hb_ in range(2):
                    pob = PS[7] if hb_ == 0 else PS[6]
                    pokey = "ps7" if hb_ == 0 else "ps6"
                    pv = pob[:, 0:260].rearrange("p (h i) -> p h i", i=65)
                    fw.v("reciprocal", rec[:, hb_ * 4:hb_ * 4 + 4], pv[:, :, 64], reads=[pokey], writes=["b_rec"])
                    fw.v("tensor_tensor", otok4[:, :, hb_, :], pv[:, :, 0:64],
                         rec[:, hb_ * 4:hb_ * 4 + 4].unsqueeze(2).to_broadcast([128, 4, 64]), ALU.mult,
                         reads=[pokey, "b_rec"], writes=["b_otok"])
                of = otok[:].rearrange("p h i -> p (h i)")
                for ct in range(4):
                    pb_ = PS[7 - ct // 2][:, 384:512].bitcast(BF16)
                    pkey = "ps%d" % (7 - ct // 2)
                    fw.tr(pb_[:, (ct % 2) * 128:(ct % 2 + 1) * 128], of[:, ct * 128:(ct + 1) * 128], self.ident_b[:],
                          reads=["b_otok", "k_ident"], writes=[pkey])
                for ct in range(4):
                    pb_ = PS[7 - ct // 2][:, 384:512].bitcast(BF16)
                    pkey = "ps%d" % (7 - ct // 2)
                    fw.v("tensor_tensor", yout[gp][:, ct, :], pb_[:, (ct % 2) * 128:(ct % 2 + 1) * 128], SG[gp][ct][:, tq], ALU.mult,
                         reads=[pkey, "b_SG%d_%d" % (ct, gp)], writes=["b_yout0"])
                qc = slice(qt * 128, qt * 128 + 128)
                fw.dma(self.yT[1][:, :, qc].rearrange("k p s -> p k s"), yout[gp][:], reads=["b_yout0"],
                       writes=[("yT1", g, 10 + tt)], eng="gpsimd")

            def weighted(main, sides, ratio):
                sides = list(sides)
                while True:
                    for _ in range(ratio):
                        try:
                            next(main)
                        except StopIteration:
                            return sides
                    for g_ in list(sides):
                        try:
                            next(g_)
                        except StopIteration:
                            sides.remove(g_)

            prep_begin(0)
            fw.lockstep(prep_lanes(0))
            prep_end(0)
            scores(0)
            bisect_mask(0)
            lanes = []
            for qt in range(NT):
                nxt = qt + 1
                g, tt = qt // 4, qt % 4
                if tt == 0 and g + 1 < G:
                    prep_begin(g + 1)
                    lanes = prep_lanes(g + 1)
                if tt == 3 and lanes:
                    fw.lockstep(lanes)
                    lanes = []
                    prep_end(g + 1)
                if nxt < NT:
                    scores(nxt)
                lanes = weighted(attention_gen(qt), lanes, 3)
                if nxt < NT:
                    bisect_mask(nxt)
                final(qt)
            self.release(keys)

    def phase_C(self, l):
        fw, S, G = self.fw, self.S, self.G
        PS = self.PS
        with ExitStack() as ph:
            sb = lambda n, s, d: ph.enter_context(self.nc.sbuf_tensor(self.uname(n), list(s), d))
            wC = sb("wC", [128, 8, 1024], BF16)
            wr = sb("c_wr", [128, 4, 128], BF16)
            wi = sb("c_wi", [128, 4, 128], BF16)
            hn = [sb("c_hn%d" % i, [128, 8, 512], BF16) for i in range(2)]
            xbuf = sb("c_xbuf", [128, 4, 515], F32)
            hprev = sb("c_hprev", [128, 4], F32)
            cl = sb("c_cl", [128, 4], F32)
            xc = [sb("c_xc%d" % i, [128, 512], F32) for i in range(2)]
            xcb = [sb("c_xcb%d" % i, [128, 512], BF16) for i in range(2)]
            r_ = [sb("c_r%d" % i, [128, 512], F32) for i in range(2)]
            i_ = [sb("c_i%d" % i, [128, 512], F32) for i in range(2)]
            a_ = [sb("c_a%d" % i, [128, 512], F32) for i in range(2)]
            b_ = [sb("c_b%d" % i, [128, 512], F32) for i in range(2)]
            sg = [sb("c_sg%d" % i, [128, 512], F32) for i in range(2)]
            yo = [sb("c_y%d" % i, [128, 512], BF16) for i in range(2)]
            names = ["wC", "c_wr", "c_wi", "c_hn0", "c_hn1", "c_xbuf", "c_hprev", "c_cl"] + \
                    [n + str(i) for n in ("c_xc", "c_xcb", "c_r", "c_i", "c_a", "c_b", "c_sg", "c_y") for i in range(2)]
            keys = names + [("wC", k) for k in range(8)] + [("c_wr", k) for k in range(4)] + [("c_wi", k) for k in range(4)] + \
                ["c_xbuf%d" % i for i in range(4)] + ["c_hprev%d" % i for i in range(4)]
            self.acquire(keys + ["stg0", "stg1"])
            self.load_w(wC, "wC", lambda k: self.w_in[l, k * 128:(k + 1) * 128, OFF_C:OFF_C + 1024], 1024, 8, self.gcol)
            self.load_w(wr, "c_wr", lambda k: self.wr_bd[l, k], 128, 4)
            self.load_w(wi, "c_wi", lambda k: self.wi_bd[l, k], 128, 4)
            fw.act(cl[:], self.prm_sb[:, PCOLS["lam"][0]:PCOLS["lam"][0] + 4], AF.Exp, scale=-1.0, reads=["prm"], writes=["c_cl"])
            fw.act(cl[:], cl[:], AF.Ln, bias=1.0, reads=["c_cl"], writes=["c_cl"])
            fw.v("tensor_scalar", cl[:], cl[:], -8.0, None, ALU.mult, reads=["c_cl"], writes=["c_cl"])
            fw.v("memset", xbuf[:], 0.0, writes=["c_xbuf%d" % i for i in range(4)])
            fw.v("memset", hprev[:], 0.0, writes=["c_hprev%d" % i for i in range(4)])
            for g in range(G):
                c0 = g * 512
                hk = "c_hn%d" % (g % 2)
                hg = hn[g % 2]
                fw.dma(hg[:], self.hnT[:, :, c0:c0 + 512].rearrange("k p s -> p k s"), reads=[("hnT", g)], writes=[hk])
                def cbody(ct, g=g, c0=c0, hk=hk, hg=hg):
                    j = ct % 2
                    pb = 4 * j
                    px, pg, pr, pi = PS[pb], PS[pb + 1], PS[pb + 2], PS[pb + 3]
                    kx, kg, kr, ki = ["ps%d" % (pb + t) for t in range(4)]
                    for k in range(8):
                        fw.mm(px[:], wC[:, k, ct * 128:(ct + 1) * 128], hg[:, k, :], start=(k == 0), stop=(k == 7),
                              reads=[("wC", k), hk], writes=[kx])
                        yield
                    for k in range(8):
                        fw.mm(pg[:], wC[:, k, 512 + ct * 128:512 + (ct + 1) * 128], hg[:, k, :], start=(k == 0), stop=(k == 7),
                              reads=[("wC", k), hk], writes=[kg])
                        yield
                    xb = xbuf[:, ct, :]
                    fw.act(xb[:, 3:515], px[:], AF.Copy, reads=[kx], writes=["c_xbuf%d" % ct])
                    yield
                    cw = lambda i: self.pcol("conv_w", i * 4 + ct)
                    fw.v("tensor_scalar", xc[j][:], xb[:, 3:515], cw(3), self.pcol("conv_b", ct), ALU.mult, ALU.add,
                         reads=["c_xbuf%d" % ct, "prm"], writes=["c_xc%d" % j])
                    yield
                    for i in range(3):
                        fw.v("scalar_tensor_tensor", xc[j][:], xb[:, i:i + 512], cw(i), xc[j][:], ALU.mult, ALU.add,
                             reads=["c_xbuf%d" % ct, "prm", "c_xc%d" % j], writes=["c_xc%d" % j])
                        yield
                    fw.v("tensor_copy", xb[:, 0:3], xb[:, 512:515], reads=["c_xbuf%d" % ct], writes=["c_xbuf%d" % ct], eng="gpsimd")
                    yield
                    fw.v("tensor_copy", xcb[j][:], xc[j][:], reads=["c_xc%d" % j], writes=["c_xcb%d" % j], eng="gpsimd")
                    yield
                    fw.mm(pr[:], wr[:, ct, :], xcb[j][:], reads=[("c_wr", ct), "c_xcb%d" % j], writes=[kr])
                    yield
                    fw.mm(pi[:], wi[:, ct, :], xcb[j][:], reads=[("c_wi", ct), "c_xcb%d" % j], writes=[ki])
                    yield
                    fw.act(r_[j][:], pr[:], AF.Sigmoid, bias=self.pcol("b_r", ct), reads=[kr, "prm"], writes=["c_r%d" % j])
                    yield
                    fw.act(i_[j][:], pi[:], AF.Sigmoid, bias=self.pcol("b_i", ct), reads=[ki, "prm"], writes=["c_i%d" % j])
                    yield
                    fw.act(sg[j][:], pg[:], AF.Silu, reads=[kg], writes=["c_sg%d" % j])
                    yield
                    fw.act(a_[j][:], r_[j][:], AF.Exp, scale=cl[:, ct:ct + 1], reads=["c_r%d" % j, "c_cl"], writes=["c_a%d" % j])
                    yield
                    fw.v("tensor_tensor", b_[j][:], a_[j][:], a_[j][:], ALU.mult, reads=["c_a%d" % j], writes=["c_b%d" % j])
                    yield
                    fw.v("tensor_scalar", b_[j][:], b_[j][:], -1.0, 1.0, ALU.mult, ALU.add, reads=["c_b%d" % j], writes=["c_b%d" % j])
                    yield
                    fw.act(b_[j][:], b_[j][:], AF.Sqrt, reads=["c_b%d" % j], writes=["c_b%d" % j])
                    yield
                    fw.v("tensor_tensor", i_[j][:], i_[j][:], xc[j][:], ALU.mult, reads=["c_i%d" % j, "c_xc%d" % j],
                         writes=["c_i%d" % j], eng="gpsimd")
                    yield
                    fw.v("tensor_tensor", b_[j][:], b_[j][:], i_[j][:], ALU.mult, reads=["c_b%d" % j, "c_i%d" % j], writes=["c_b%d" % j])
                    yield
                    fw.v("tensor_tensor_scan", r_[j][:], a_[j][:], b_[j][:], hprev[:, ct:ct + 1], ALU.mult, ALU.add,
                         reads=["c_a%d" % j, "c_b%d" % j, "c_hprev%d" % ct, "c_r%d" % j], writes=["c_r%d" % j])
                    yield
                    fw.v("tensor_copy", hprev[:, ct:ct + 1], r_[j][:, 511:512], reads=["c_r%d" % j], writes=["c_hprev%d" % ct])
                    yield
                    fw.v("tensor_tensor", yo[j][:], r_[j][:], sg[j][:], ALU.mult, reads=["c_r%d" % j, "c_sg%d" % j],
                         writes=["c_y%d" % j], eng="gpsimd")
                    yield
                    fw.dma(self.yT[2][ct, :, c0:c0 + 512], yo[j][:], reads=["c_y%d" % j], writes=[("yT2", g, ct)], eng="gpsimd")
                    yield
                fw.lockstep([cbody(0), cbody(1)])
                fw.lockstep([cbody(2), cbody(3)])
            self.release(keys)

    def phase_M(self, l):
        fw, S, G, L = self.fw, self.S, self.G, self.L
        PS = self.PS
        last = (l == L - 1)
        with ExitStack() as ph:
            sb = lambda n, s, d: ph.enter_context(self.nc.sbuf_tensor(self.uname(n), list(s), d))
            wG = sb("wG", [128, 8, 3072], BF16)
            wbr = sb("wbr", [128, 12, 1024], BF16)
            wo = sb("wo", [128, 8, 1024], BF16)
            wpg = sb("wpg", [128, 8, 1024], BF16)
            wple = sb("wple", [128, 2, 1024], BF16)
            hn = sb("m_hn", [128, 8, 512], BF16)
            ys = [sb("m_y%d" % n, [128, 4, 512], BF16) for n in range(3)]
            hb = sb("m_h", [128, 8, 512], F32)
            h1b = sb("m_h1b", [128, 8, 512], BF16)
            pf = sb("m_pf", [128, 2, 512], F32)
            pb_ = sb("m_pb", [128, 2, 512], BF16)
            mrg = sb("m_mrg", [128, 8, 512], BF16)
            sgs = [sb("m_sg%d" % n, [128, 512], F32) for n in range(3)]
            tmp = sb("m_tmp", [128, 2, 512], F32)
            self.rs_sb = sb("m_rs", [128, 512], F32)
            self.hn_out = h1b
            self.hn_out_key = "m_h1b"
            self.eps_col = sb("m_eps", [128, 1], F32)
            names = ["wG", "wbr", "wo", "wpg", "wple", "m_hn", "m_y0", "m_y1", "m_y2", "m_h", "m_h1b", "m_pf", "m_pb",
                     "m_mrg", "m_sg0", "m_sg1", "m_sg2", ("m_tmp", 0), ("m_tmp", 1), "rs", "hn_out", "eps"]
            keys = names + [("wG", k) for k in range(8)] + [("wbr", k) for k in range(12)] + \
                [("wo", k) for k in range(8)] + [("wpg", k) for k in range(8)] + [("wple", k) for k in range(2)]
            self.acquire(keys + ["stg0", "stg1"])
            fw.v("memset", self.eps_col[:], NORM_EPS, writes=["eps"])
            self.load_w(wG, "wG", lambda k: self.w_in[l, k * 128:(k + 1) * 128, OFF_G:OFF_G + 3072], 3072, 8, self.gcol)
            self.load_w(wbr, "wbr", lambda k: self.w_branch[l, k // 4, (k % 4) * 128:(k % 4 + 1) * 128, :], 1024, 12)
            self.load_w(wo, "wo", lambda k: self.w_out[l, k * 128:(k + 1) * 128, :], 1024, 8)
            self.load_w(wpg, "wpg", lambda k: self.w_pg[l, k * 128:(k + 1) * 128, :], 1024, 8)
            self.load_w(wple, "wple", lambda k: self.w_ple[l, k * 128:(k + 1) * 128, :], 1024, 2)
            hsrc = self.xT if l == 0 else self.hT
            hdst = self.outT if last else self.hT
            for g in range(G):
                c0 = g * 512
                fw.dma(hn[:], self.hnT[:, :, c0:c0 + 512].rearrange("k p s -> p k s"), reads=[("hnT", g)], writes=["m_hn"])
                for n in range(3):
                    fw.dma(ys[n][:], self.yT[n][:, :, c0:c0 + 512].rearrange("k p s -> p k s"),
                           reads=[("yT%d" % n, g, ct) for ct in range(4)] + [("yT%d" % n, g, 10 + ct) for ct in range(4)], writes=["m_y%d" % n])
                fw.dma(hb[:], hsrc[:, :, c0:c0 + 512].rearrange("k p s -> p k s"),
                       reads=([("hT", g)] if l > 0 else []), writes=["m_h"])
                fw.dma(pf[:], self.pT[l, :, :, c0:c0 + 512].rearrange("k p s -> p k s"), writes=["m_pf"])
                fw.v("tensor_copy", pb_[:], pf[:], reads=["m_pf"], writes=["m_pb"], eng="gpsimd")
                for dmt in range(8):
                    cs = slice(dmt * 128, (dmt + 1) * 128)
                    for n in range(3):
                        for k in range(8):
                            fw.mm(PS[n][:], wG[:, k, n * 1024 + dmt * 128:n * 1024 + (dmt + 1) * 128], hn[:, k, :],
                                  start=(k == 0), stop=(k == 7), reads=[("wG", k), "m_hn"], writes=["ps%d" % n])
                        for kc in range(4):
                            fw.mm(PS[3 + n][:], wbr[:, n * 4 + kc, cs], ys[n][:, kc, :], start=(kc == 0), stop=(kc == 3),
                                  reads=[("wbr", n * 4 + kc), "m_y%d" % n], writes=["ps%d" % (3 + n)])
                    for n in range(3):
                        fw.act(sgs[n][:], PS[n][:], AF.Sigmoid, reads=["ps%d" % n], writes=["m_sg%d" % n])
                        fw.v("tensor_tensor", sgs[n][:], PS[3 + n][:], sgs[n][:], ALU.mult,
                             reads=["ps%d" % (3 + n), "m_sg%d" % n], writes=["m_sg%d" % n])
                    fw.v("tensor_tensor", sgs[0][:], sgs[0][:], sgs[1][:], ALU.add, reads=["m_sg0", "m_sg1"], writes=["m_sg0"], eng="gpsimd")
                    fw.v("tensor_tensor", mrg[:, dmt, :], sgs[0][:], sgs[2][:], ALU.add, reads=["m_sg0", "m_sg2"], writes=["m_mrg"], eng="gpsimd")
                for d2 in range(8):
                    pk = 6 + d2 % 2
                    for k in range(8):
                        fw.mm(PS[pk][:], wo[:, k, d2 * 128:(d2 + 1) * 128], mrg[:, k, :], start=(k == 0), stop=(k == 7),
                              reads=[("wo", k), "m_mrg"], writes=["ps%d" % pk])
                    fw.v("tensor_tensor", hb[:, d2, :], hb[:, d2, :], PS[pk][:], ALU.add, reads=["m_h", "ps%d" % pk], writes=["m_h"])
                fw.act(h1b[:], hb[:], AF.Copy, reads=["m_h"], writes=["m_h1b"])
                for d2 in range(8):
                    pa, pp = (0, 1) if d2 % 2 == 0 else (2, 3)
                    for k in range(8):
                        fw.mm(PS[pa][:], wpg[:, k, d2 * 128:(d2 + 1) * 128], h1b[:, k, :], start=(k == 0), stop=(k == 7),
                              reads=[("wpg", k), "m_h1b"], writes=["ps%d" % pa])
                    for k in range(2):
                        fw.mm(PS[pp][:], wple[:, k, d2 * 128:(d2 + 1) * 128], pb_[:, k, :], start=(k == 0), stop=(k == 1),
                              reads=[("wple", k), "m_pb"], writes=["ps%d" % pp])
                    sgk = d2 % 2
                    fw.act(sgs[sgk][:], PS[pa][:], AF.Sigmoid, reads=["ps%d" % pa], writes=["m_sg%d" % sgk])
                    fw.v("tensor_tensor", sgs[sgk][:], PS[pp][:], sgs[sgk][:], ALU.mult, reads=["ps%d" % pp, "m_sg%d" % sgk],
                         writes=["m_sg%d" % sgk])
                    fw.v("tensor_tensor", hb[:, d2, :], hb[:, d2, :], sgs[sgk][:], ALU.add, reads=["m_h", "m_sg%d" % sgk],
                         writes=["m_h"], eng="gpsimd")
                fw.dma(hdst[:, :, c0:c0 + 512].rearrange("k p s -> p k s"), hb[:], reads=["m_h"],
                       writes=[("outT" if last else "hT", g)], eng="gpsimd")
                if not last:
                    self.norm_group(hb, "m_h", g, tmp, "m_tmp")
            self.release(keys)


_CACHE = {}


def make_in_maps(inp, S, L, ncores):
    maps = []
    w_in = np.ascontiguousarray(np.asarray(inp["w_in"], np.float32)[:L])
    ki0 = OFF_B + 1792
    w_kidup = np.ascontiguousarray(np.concatenate([w_in[:, :, ki0:ki0 + 64], w_in[:, :, ki0:ki0 + 64]], axis=2))
    prm = np.stack([pack_params(inp, l) for l in range(L)])
    cst = np.zeros((128, 32), np.float32)
    invf = (np.float32(500000.0) ** (-(np.arange(0, 16, 2, dtype=np.float32) / np.float32(16)))).astype(np.float32)
    for p_ in range(128):
        if p_ % 64 < 16:
            cst[p_, 0] = invf[p_ % 8]
    cst[:, 1:25] = (2.0 ** (-np.arange(24, dtype=np.float64)))[None, :].astype(np.float32)
    ropeR = np.zeros((128, 128), np.float32)
    for m_ in range(128):
        if m_ % 64 < 8:
            ropeR[m_ + 8, m_] = -1.0
        elif m_ % 64 < 16:
            ropeR[m_ - 8, m_] = 1.0
    shared = {
        "cst": cst, "ropeR": ropeR,
        "prm": prm, "w_in": w_in, "w_kidup": w_kidup,
        "w2": np.ascontiguousarray(np.asarray(inp["rwkv_w2"], np.float32)[:L]),
        "a2": np.ascontiguousarray(np.asarray(inp["rwkv_a2"], np.float32)[:L]),
        "wr_bd": np.stack([blockdiag(inp["lru_w_r"][l]) for l in range(L)]),
        "wi_bd": np.stack([blockdiag(inp["lru_w_i"][l]) for l in range(L)]),
        "w_branch": np.ascontiguousarray(np.asarray(inp["w_branch"], np.float32)[:L]),
        "w_out": np.ascontiguousarray(np.asarray(inp["w_out"], np.float32)[:L]),
        "w_ple": np.ascontiguousarray(np.asarray(inp["w_ple"], np.float32)[:L]),
        "w_pg": np.ascontiguousarray(np.asarray(inp["w_ple_gate"], np.float32)[:L]),
    }
    x = np.asarray(inp["x"], np.float32)
    p = np.asarray(inp["p"], np.float32)
    pos = np.asarray(inp["positions"], np.int32)
    nb = x.shape[0]
    for c in range(ncores):
        b = (c // 2) % nb
        m = dict(shared)
        m["xT"] = np.ascontiguousarray(x[b].T.reshape(8, 128, S))
        m["pT"] = np.ascontiguousarray(np.stack([p[l, b].T.reshape(2, 128, S) for l in range(L)]))
        m["pos"] = np.ascontiguousarray(pos[b].reshape(1, S))
        maps.append(m)
    return maps


def kernel(**inputs):
    x = np.asarray(inputs["x"])
    B, S, _ = x.shape
    L = np.asarray(inputs["w_in"]).shape[0]
    key = (S, L)
    if key not in _CACHE:
        _CACHE[key] = Prog(S, L).build()
    nc = _CACHE[key]
    maps = make_in_maps(inputs, S, L, 8)
    res = run_bass_kernel_spmd(nc, maps, core_ids=list(range(8)))
    out = np.zeros((B, S, D), np.float32)
    for b in range(B):
        out[b] = res.results[2 * b]["outT"].reshape(D, S).T
    return out
```

```python
from contextlib import ExitStack
import numpy as np
import concourse.bass as bass
import concourse.mybir as mybir
from concourse.bass_utils import run_bass_kernel_spmd

F32 = mybir.dt.float32
BF16 = mybir.dt.bfloat16
I32 = mybir.dt.int32
AF = mybir.ActivationFunctionType
ALU = mybir.AluOpType
AX = mybir.AxisListType

ENGS = ("tensor", "vector", "scalar", "gpsimd", "sync")
N_DMA_SEMS = 24

D = 1024
DIN = 8644
OFF_A, OFF_B, OFF_C, OFF_G = 0, 2176, 4548, 5572
NORM_EPS = 1e-6
GN_EPS = 64e-5


class FW:
    def __init__(self, nc, stack, same_engine_sync=True):
        self.nc = nc
        self.stack = stack
        self.q = {e: [] for e in ENGS}
        self.cnt = {e: 0 for e in ENGS}
        self.sem = {e: stack.enter_context(nc.semaphore("s_" + e)) for e in ENGS}
        self.dsem = [stack.enter_context(nc.semaphore("d%d" % i)) for i in range(N_DMA_SEMS)]
        self.dcnt = [0] * N_DMA_SEMS
        self.dnext = 0
        self.seen = {e: {} for e in ENGS}
        self.lastw = {}
        self.readers = {}
        self.same = same_engine_sync
        self.ninst = 0
        self.rr = 0

    def sb(self, name, shape, dt):
        return self.stack.enter_context(self.nc.sbuf_tensor(name, list(shape), dt))

    def ps(self, name, shape, dt=F32):
        return self.stack.enter_context(self.nc.psum_tensor(name, list(shape), dt))

    def _deps(self, eng, reads, writes):
        ev = []
        for k in reads:
            if k in self.lastw:
                ev.append(self.lastw[k])
        for k in writes:
            if k in self.lastw:
                ev.append(self.lastw[k])
            ev.extend(self.readers.get(k, ()))
        best = {}
        for (sname, sem, val, src) in ev:
            if src == eng and (eng == "tensor" or not self.same):
                continue
            if self.seen[eng].get(sname, 0) >= val:
                continue
            if sname not in best or best[sname][1] < val:
                best[sname] = (sem, val)
        waits = []
        for sname, (sem, val) in best.items():
            self.seen[eng][sname] = val
            waits.append((sem, val))
        return waits

    def _commit(self, event, reads, writes):
        for k in writes:
            self.lastw[k] = event
            self.readers[k] = []
        for k in reads:
            if k in writes:
                continue
            self.readers.setdefault(k, []).append(event)

    def op(self, eng, fn, reads=(), writes=()):
        waits = self._deps(eng, reads, writes)
        self.cnt[eng] += 1
        idx = self.cnt[eng]
        sem = self.sem[eng]
        self.q[eng].append((waits, fn, sem, 1))
        self._commit(("s_" + eng, sem, idx, eng), reads, writes)
        self.ninst += 1

    def dma(self, out, in_, reads=(), writes=(), eng="sync", **kw):
        lo, n = (0, 16) if eng == "sync" else (16, N_DMA_SEMS - 16)
        self.dnext_q = getattr(self, "dnext_q", {})
        i = self.dnext_q.get(eng, 0)
        self.dnext_q[eng] = (i + 1) % n
        slot = lo + i
        sem = self.dsem[slot]
        sname = "d%d" % slot
        waits = self._deps(eng, reads, writes)
        prev = self.dcnt[slot] * 16
        if prev and self.seen[eng].get(sname, 0) < prev:
            waits.append((sem, prev))
            self.seen[eng][sname] = prev
        self.dcnt[slot] += 1
        val = self.dcnt[slot] * 16
        self.q[eng].append((waits, lambda e: e.dma_start(out=out, in_=in_, **kw), sem, 16))
        self._commit((sname, sem, val, "dma"), reads, writes)
        self.ninst += 1

    def finish(self, keys, eng="sync"):
        waits = self._deps(eng, keys, ())
        self.q[eng].append((waits, None, None, 0))

    def emit(self):
        nc = self.nc
        with nc.Block() as block:
            for ename in ENGS:
                items = self.q[ename]
                if not items:
                    continue

                def body(e, items=items):
                    for waits, fn, sem, inc in items:
                        for (ws, wv) in waits:
                            e.wait_ge(ws, wv)
                        if fn is not None:
                            fn(e).then_inc(sem, inc)

                getattr(block, ename)(body)

    def mm(self, out, lhsT, rhs, start=True, stop=True, reads=(), writes=()):
        self.op("tensor", lambda e: e.matmul(out, lhsT, rhs, start=start, stop=stop), reads, writes)

    def tr(self, out, in_, ident, reads=(), writes=()):
        self.op("tensor", lambda e: e.transpose(out, in_, ident), reads, writes)

    def act(self, out, in_, func, bias=0.0, scale=1.0, reads=(), writes=(), accum_out=None):
        if accum_out is None:
            self.op("scalar", lambda e: e.activation(out, in_, func, bias=bias, scale=scale), reads, writes)
        else:
            self.op("scalar", lambda e: e.activation(out, in_, func, bias=bias, scale=scale,
                                                     accum_out=accum_out), reads, writes)

    def v(self, name, *args, reads=(), writes=(), eng="vector", **kw):
        self.op(eng, lambda e: getattr(e, name)(*args, **kw), reads, writes)

    @staticmethod
    def lockstep(gens):
        gens = list(gens)
        while gens:
            for g_ in list(gens):
                try:
                    next(g_)
                except StopIteration:
                    gens.remove(g_)

    def cast_eng(self):
        self.rr += 1
        return ("vector", "gpsimd")[self.rr % 2]


PCOLS = {}
_o = 0
for _n, _w in [("norm_g", 8), ("mu_r", 4), ("mu_k", 4), ("mu_v", 4), ("mu_g", 4), ("mu_wl", 1), ("mu_al", 1),
               ("w0", 4), ("a0", 4), ("k_k", 4), ("k_a", 4), ("gn_g", 4), ("gn_b", 4), ("r_k", 4),
               ("q_g", 1), ("k_g", 1),
               ("conv_w", 16), ("conv_b", 4), ("b_r", 4), ("b_i", 4), ("lam", 4)]:
    PCOLS[_n] = (_o, _w)
    _o += _w
NPRM = _o


def _col4(v):
    return np.ascontiguousarray(np.asarray(v, np.float32).reshape(4, 128).T)


def pack_params(inp, l):
    prm = np.zeros((128, NPRM), np.float32)

    def put(name, arr):
        o, w = PCOLS[name]
        prm[:arr.shape[0], o:o + w] = arr

    put("norm_g", np.asarray(inp["norm_g"][l], np.float32).reshape(8, 128).T)
    mu = np.asarray(inp["rwkv_mu"][l], np.float32)
    put("mu_r", _col4(mu[0:512])); put("mu_k", _col4(mu[512:1024])); put("mu_v", _col4(mu[1024:1536]))
    put("mu_wl", mu[1536:1600].reshape(64, 1)); put("mu_al", mu[1600:1664].reshape(64, 1))
    put("mu_g", _col4(mu[1664:2176]))
    put("w0", _col4(inp["rwkv_w0"][l])); put("a0", _col4(inp["rwkv_a0"][l]))
    put("k_k", _col4(inp["rwkv_k_k"][l])); put("k_a", _col4(inp["rwkv_k_a"][l]))
    put("gn_g", _col4(inp["rwkv_gn_g"][l])); put("gn_b", _col4(inp["rwkv_gn_b"][l]))
    put("r_k", _col4(np.asarray(inp["rwkv_r_k"][l]).reshape(512)))
    put("q_g", np.tile(np.asarray(inp["dsa_q_g"][l], np.float32), 2).reshape(128, 1))
    put("k_g", np.tile(np.asarray(inp["dsa_k_g"][l], np.float32), 2).reshape(128, 1))
    cw = np.asarray(inp["lru_conv_w"][l], np.float32)
    put("conv_w", np.concatenate([_col4(cw[i]) for i in range(4)], axis=1))
    put("conv_b", _col4(inp["lru_conv_b"][l])); put("b_r", _col4(inp["lru_b_r"][l]))
    put("b_i", _col4(inp["lru_b_i"][l])); put("lam", _col4(inp["lru_lambda"][l]))
    return prm


def blockdiag(w):
    w = np.asarray(w, np.float32)
    out = np.zeros((4, 128, 128), np.float32)
    for ct in range(4):
        out[ct, 0:64, 0:64] = w[2 * ct]
        out[ct, 64:128, 64:128] = w[2 * ct + 1]
    return out


class Prog:
    def __init__(self, S, L, phases="NACBM", dbg=()):
        self.S, self.L, self.phases, self.dbg = S, L, phases, dbg
        self.G = S // 512
        nc = self.nc = bass.Bass("TRN2", target_bir_lowering=False)
        dt = nc.dram_tensor
        self.xT = dt("xT", [8, 128, S], F32, kind="ExternalInput").ap()
        self.pT = dt("pT", [L, 2, 128, S], F32, kind="ExternalInput").ap()
        self.pos = dt("pos", [1, S], I32, kind="ExternalInput").ap()
        self.prm = dt("prm", [L, 128, NPRM], F32, kind="ExternalInput").ap()
        self.w_in = dt("w_in", [L, D, DIN], F32, kind="ExternalInput").ap()
        self.w_kidup = dt("w_kidup", [L, D, 128], F32, kind="ExternalInput").ap()
        self.w2 = dt("w2", [L, 64, 512], F32, kind="ExternalInput").ap()
        self.a2 = dt("a2", [L, 64, 512], F32, kind="ExternalInput").ap()
        self.wr_bd = dt("wr_bd", [L, 4, 128, 128], F32, kind="ExternalInput").ap()
        self.wi_bd = dt("wi_bd", [L, 4, 128, 128], F32, kind="ExternalInput").ap()
        self.w_branch = dt("w_branch", [L, 3, 512, D], F32, kind="ExternalInput").ap()
        self.w_out = dt("w_out", [L, D, D], F32, kind="ExternalInput").ap()
        self.w_ple = dt("w_ple", [L, 256, D], F32, kind="ExternalInput").ap()
        self.w_pg = dt("w_pg", [L, D, D], F32, kind="ExternalInput").ap()
        self.cst_d = dt("cst", [128, 32], F32, kind="ExternalInput").ap()
        self.ropeR_d = dt("ropeR", [128, 128], F32, kind="ExternalInput").ap()
        self.ropeT = dt("ropeT", [2, 128, S], F32, kind="Internal").ap()
        self.outT = dt("outT", [8, 128, S], F32, kind="ExternalOutput").ap()
        okind = lambda n: "ExternalOutput" if n in dbg else "Internal"
        self.hT = dt("hT", [8, 128, S], F32, kind=okind("hT")).ap()
        self.hnT = dt("hnT", [8, 128, S], BF16, kind=okind("hnT")).ap()
        self.yT = [dt("yT%d" % n, [4, 128, S], BF16, kind=okind("yT%d" % n)).ap() for n in range(3)]

    def uname(self, n):
        self._uid = getattr(self, "_uid", 0) + 1
        return "%s_u%d" % (n, self._uid)

    def pcol(self, name, j=0, rows=128):
        o, w = PCOLS[name]
        return self.prm_sb[0:rows, o + j:o + j + 1]

    def load_w(self, dst, key, src_fn, ncols, kt, scale=None, rows=128):
        fw = self.fw
        for k in range(kt):
            for c0 in range(0, ncols, 512):
                cn = min(512, ncols - c0)
                si = self.stg_i
                self.stg_i ^= 1
                stg = self.stg[si]
                fw.dma(stg[0:rows, 0:cn], src_fn(k)[:, c0:c0 + cn], writes=["stg%d" % si])
                self.cast_rr = getattr(self, "cast_rr", 0) + 1
                eng = ("vector", "scalar", "gpsimd")[self.cast_rr % 3]
                o_ap, i_ap = dst[0:rows, k, c0:c0 + cn], stg[0:rows, 0:cn]
                rk_ = ["stg%d" % si] + (["prm"] if scale is not None else [])
                if eng == "scalar":
                    fw.act(o_ap, i_ap, AF.Copy, scale=(scale(k) if scale is not None else 1.0), reads=rk_, writes=[(key, k)])
                elif scale is not None:
                    fw.v("tensor_scalar", o_ap, i_ap, scale(k), 0.0, ALU.mult, ALU.add, reads=rk_, writes=[(key, k)], eng=eng)
                else:
                    fw.v("tensor_copy", o_ap, i_ap, reads=rk_, writes=[(key, k)], eng=eng)

    def gcol(self, k):
        return self.pcol("norm_g", k)

    def build(self):
        nc = self.nc
        with ExitStack() as st:
            fw = self.fw = FW(nc, st)
            self.st = st
            self.stg = [fw.sb("stg%d" % i, [128, 512], F32) for i in range(2)]
            self.stg_i = 0
            self.prm_sb = fw.sb("prm_sb", [128, NPRM], F32)
            self.ones_f = fw.sb("ones_f", [128, 128], F32)
            fw.v("memset", self.ones_f[:], 1.0, writes=["ones_f"])
            self.PSALL = fw.ps("psall", [128, 8, 512], F32)
            self.PS = [self.PSALL[:, i, :] for i in range(8)]
            self.tiny_col = fw.sb("tiny_col", [128, 1], F32)
            self.gneps_col = fw.sb("gneps_col", [128, 1], F32)
            fw.v("memset", self.tiny_col[:], 1e-30, writes=["tiny"])
            fw.v("memset", self.gneps_col[:], GN_EPS, writes=["tiny"])
            self.eps6_col = fw.sb("eps6_col", [128, 1], F32)
            fw.v("memset", self.eps6_col[:], NORM_EPS, writes=["tiny"])
            self.cst_sb = fw.sb("cst_sb", [128, 32], F32)
            fw.dma(self.cst_sb[:], self.cst_d, writes=["cst"])
            self.make_consts()
            if "B" in self.phases:
                self.phase_R()
            for n, ph_ in enumerate("ABC"):
                if ph_ not in self.phases:
                    zt = fw.sb("zt%d" % n, [128, 4, 512], BF16)
                    fw.v("memset", zt[:], 0.0, writes=["zt"])
                    for g in range(self.G):
                        fw.dma(self.yT[n][:, :, g * 512:(g + 1) * 512].rearrange("k p s -> p k s"), zt[:], reads=["zt"],
                               writes=[("yT%d" % n, g, ct) for ct in range(4)])
            for l in range(self.L):
                fw.dma(self.prm_sb[:], self.prm[l], writes=["prm"])
                if l == 0 and "N" in self.phases:
                    self.phase_N0()
                if "A" in self.phases:
                    self.phase_A(l)
                if "C" in self.phases:
                    self.phase_C(l)
                if "B" in self.phases:
                    self.phase_B(l)
                if "M" in self.phases:
                    self.phase_M(l)
            fw.finish([("outT", g) for g in range(self.G)])
            fw.emit()
        return nc

    def norm_group(self, hbuf, hkey, g, tmp, tmpkey):
        fw, S = self.fw, self.S
        c0 = g * 512
        ps = self.PS[7]
        for k in range(8):
            fw.act(tmp[:, k % 2, :], hbuf[:, k, :], AF.Square, reads=[hkey], writes=[(tmpkey, k % 2)])
            fw.mm(ps[:], self.ones_f[:], tmp[:, k % 2, :], start=(k == 0), stop=(k == 7),
                  reads=["ones_f", (tmpkey, k % 2)], writes=["ps7"])
        rs = self.rs_sb
        fw.act(rs[:], ps[:], AF.Sqrt, bias=self.eps_col[:, 0:1], scale=1.0 / D, reads=["ps7", "eps"], writes=["rs"])
        fw.v("reciprocal", rs[:], rs[:], reads=["rs"], writes=["rs"])
        hn = self.hn_out
        fw.v("tensor_tensor", hn[:], hbuf[:], rs[:].unsqueeze(1).to_broadcast([128, 8, 512]), ALU.mult,
             reads=[hkey, "rs"], writes=[self.hn_out_key])
        fw.dma(self.hnT[:, :, c0:c0 + 512].rearrange("k p s -> p k s"), hn[:], reads=[self.hn_out_key],
               writes=[("hnT", g)], eng="gpsimd")

    def phase_N0(self):
        fw = self.fw
        with ExitStack() as ph:
            sb = lambda n, s, d: ph.enter_context(self.nc.sbuf_tensor(self.uname(n), list(s), d))
            hb = [sb("n0_h%d" % i, [128, 8, 512], F32) for i in range(2)]
            tmp = sb("n0_tmp", [128, 2, 512], F32)
            self.rs_sb = sb("n0_rs", [128, 512], F32)
            self.hn_out = sb("n0_hn", [128, 8, 512], BF16)
            self.hn_out_key = "hn_out"
            self.eps_col = sb("n0_eps", [128, 1], F32)
            self.acquire(["n0_h0", "n0_h1", ("n0_tmp", 0), ("n0_tmp", 1), "rs", "hn_out", "eps"])
            fw.v("memset", self.eps_col[:], NORM_EPS, writes=["eps"])
            for g in range(self.G):
                c0 = g * 512
                h = hb[g % 2]
                fw.dma(h[:], self.xT[:, :, c0:c0 + 512].rearrange("k p s -> p k s"), writes=["n0_h%d" % (g % 2)])
                self.norm_group(h, "n0_h%d" % (g % 2), g, tmp, "n0_tmp")
            self.release(["n0_h0", "n0_h1", ("n0_tmp", 0), ("n0_tmp", 1), "rs", "hn_out", "eps"])

    def release(self, keys):
        fw = self.fw
        ev = []
        for k in keys:
            if k in fw.lastw:
                ev.append(fw.lastw[k])
            ev.extend(fw.readers.get(k, ()))
        best = {}
        for e in getattr(fw, "pending_release", []) + ev:
            if e[0] not in best or best[e[0]][2] < e[2]:
                best[e[0]] = e
        fw.pending_release = list(best.values())

    def acquire(self, keys):
        fw = self.fw
        ev = getattr(fw, "pending_release", [])
        for k in keys:
            fw.readers.setdefault(k, []).extend(ev)


    def make_consts(self):
        fw = self.fw
        onesb = fw.sb("k_onesb", [128, 256], BF16)
        self.ident_b = fw.sb("k_ident", [128, 128], BF16)
        self.mask_ui = fw.sb("k_mask_ui", [128, 256], BF16)
        self.mask_sl = fw.sb("k_mask_sl", [128, 128], BF16)
        self.blk1 = fw.sb("k_blk1", [128, 128], F32)
        g = "gpsimd"
        fw.v("memset", onesb[:], 1.0, writes=["k_onesb"], eng=g)
        sel = lambda out, pat, cm, op, key: fw.op(g, lambda e: e.affine_select(out, onesb[:, 0:128], pat, op, 0.0, base=0,
                                                                                channel_multiplier=cm),
                                                  reads=["k_onesb"], writes=[key])
        sel(self.ident_b[:], [[-1, 128]], 1, ALU.is_equal, "k_ident")
        sel(self.mask_ui[:, 0:128], [[1, 128]], -1, ALU.is_gt, "k_mask_ui")
        sel(self.mask_ui[:, 128:256], [[1, 128]], -1, ALU.is_ge, "k_mask_ui")
        sel(self.mask_sl[:], [[-1, 128]], 1, ALU.is_gt, "k_mask_sl")
        fw.v("memset", self.blk1[:], 0.0, writes=["k_blk1"], eng=g)
        fw.v("memset", self.blk1[0:64, 0:64], 1.0, writes=["k_blk1"], eng=g)
        fw.v("memset", self.blk1[64:128, 64:128], 1.0, writes=["k_blk1"], eng=g)

    def phase_A(self, l):
        fw, S, G = self.fw, self.S, self.G
        PS = self.PS
        CDEC = 0.6065306597126334
        with ExitStack() as ph:
            allkeys = []

            def sb(n, s, d):
                allkeys.append(n)
                return ph.enter_context(self.nc.sbuf_tensor(self.uname(n), list(s), d))

            wA = sb("wA", [128, 8, 2176], BF16)
            w2b = sb("a_w2b", [64, 1, 512], BF16)
            a2b = sb("a_a2b", [64, 1, 512], BF16)
            hn = [sb("a_hn0", [128, 8, 512], BF16)] * 2
            omu = sb("a_omu", [128, NPRM], F32)
            prevc = sb("a_prevc", [128, 18], F32)
            ubP = [[sb("a_ub%d_%d" % (p, q), [128, 513], F32) for q in range(4)] for p in range(2)]
            usP = [[sb("a_us%d_%d" % (p, q), [128, 512], F32) for q in range(4)] for p in range(2)]
            ulo = [sb("a_ulo%d" % q, [64, 513], F32) for q in range(2)]
            twl = sb("a_twl", [64, 512], BF16)
            alb = sb("a_alb", [64, 512], BF16)
            tP = [[sb("a_t%d_%d" % (p, i), [128, 512], F32) for i in range(8)] for p in range(2)]
            t_ = tP[0]
            art = [sb("a_art%d" % ct, [128, 4, 2, 128], BF16) for ct in range(4)]
            bk = [sb("a_bk%d" % ct, [128, 2, 512], BF16) for ct in range(4)]
            vb = [sb("a_vb%d" % ct, [128, 512], BF16) for ct in range(4)]
            tok = [sb("a_tok%d" % ct, [128, 4, 3, 128], BF16) for ct in range(4)]
            bonus = [sb("a_bonus%d" % ct, [128, 512], BF16) for ct in range(4)]
            sgt = [sb("a_sg%d" % ct, [128, 512], BF16) for ct in range(4)]
            PC = sb("a_PC", [128, 4, 4], F32)
            T = sb("a_T", [128, 4, 64], F32)
            Tb = sb("a_Tb", [128, 4, 64], BF16)
            LAb = [sb("a_LAb%d" % i, [128, 4, 256], BF16) for i in range(2)]
            KAb = [sb("a_KAb%d" % i, [128, 4, 256], BF16) for i in range(2)]
            Lb = [sb("a_Lb%d" % i, [128, 4, 128], BF16) for i in range(2)]
            PPb = [[sb("a_PPb%d_%d" % (i, j), [128, 4, 256], BF16) for j in range(2)] for i in range(2)]
            XT = [[sb("a_XT%d_%d" % (i, j), [128, 4, 128], BF16) for j in range(2)] for i in range(2)]
            Wb = [sb("a_Wb%d" % i, [128, 4, 64], BF16) for i in range(2)]
            Ub = [sb("a_Ub%d" % i, [128, 4, 64], BF16) for i in range(2)]
            xc = sb("a_xc", [128, 8, 64], F32)
            sq = sb("a_sq", [128, 8, 64], F32)
            st8 = sb("a_st8", [128, 4, 8], F32)
            onb = sb("a_onb", [128, 512], BF16)
            yv = [sb("a_yv%d" % i, [128, 128], F32) for i in range(2)]
            yout = sb("a_yout", [128, 4, 512], BF16)
            self.rstm = sb("k_rstm", [128, 512], F32)
            keys = allkeys + [("wA", k) for k in range(8)] + [("a_w2b", 0), ("a_a2b", 0)]
            self.acquire(keys + ["stg0", "stg1"])
            fw.v("memset", self.rstm[:], 1.0, writes=["k_rstm"], eng="gpsimd")
            for c in range(4):
                fw.v("memset", self.rstm[:, c * 128:c * 128 + 1], 0.0, writes=["k_rstm"], eng="gpsimd")

            self.load_w(wA, "wA", lambda k: self.w_in[l, k * 128:(k + 1) * 128, OFF_A:OFF_A + 2176], 2176, 8, self.gcol)
            self.load_w(w2b, "a_w2b", lambda k: self.w2[l], 512, 1, rows=64)
            self.load_w(a2b, "a_a2b", lambda k: self.a2[l], 512, 1, rows=64)
            fw.v("tensor_scalar", omu[:], self.prm_sb[:], -1.0, 1.0, ALU.mult, ALU.add, reads=["prm"], writes=["a_omu"])
            fw.v("memset", prevc[:], 0.0, writes=["a_prevc"])
            fw.v("memset", T[:], 0.0, writes=["a_T"])
            fw.v("memset", Tb[:], 0.0, writes=["a_Tb"])
            oc = lambda name, j=0, rows=128: omu[0:rows, PCOLS[name][0] + j:PCOLS[name][0] + j + 1]
            psb0 = PS[0][:].bitcast(BF16)
            psb1 = PS[1][:].bitcast(BF16)

            def shift(ps, pskey, ubt, ubkey, pcol, out, okey, mu_ap, omu_ap, rows=128):
                fw.v("tensor_copy", ubt[0:rows, 0:1], prevc[0:rows, pcol:pcol + 1], reads=["a_prevc"], writes=[ubkey], eng="gpsimd")
                fw.act(ubt[0:rows, 1:513], ps, AF.Copy, reads=[pskey], writes=[ubkey])
                fw.v("tensor_copy", prevc[0:rows, pcol:pcol + 1], ubt[0:rows, 512:513], reads=[ubkey], writes=["a_prevc"], eng="gpsimd")
                fw.v("tensor_scalar", out, ubt[0:rows, 0:512], mu_ap, None, ALU.mult, reads=[ubkey, "prm"], writes=[okey])
                fw.v("scalar_tensor_tensor", out, ubt[0:rows, 1:513], omu_ap, out, ALU.mult, ALU.add,
                     reads=[ubkey, "a_omu", okey], writes=[okey])

            for g in range(G):
                c0 = g * 512
                hk = "a_hn0"
                hg = hn[0]
                fw.dma(hg[:], self.hnT[:, :, c0:c0 + 512].rearrange("k p s -> p k s"), reads=[("hnT", g)], writes=[hk])
                for q, (coff, nm) in enumerate([(1536, "mu_wl"), (1600, "mu_al")]):
                    for k in range(8):
                        fw.mm(PS[q][0:64, :], wA[:, k, coff:coff + 64], hg[:, k, :], start=(k == 0), stop=(k == 7),
                              reads=[("wA", k), hk], writes=["ps%d" % q])
                    shift(PS[q][0:64, :], "ps%d" % q, ulo[q], "a_ulo%d" % q, 16 + q, t_[q][0:64, :], "a_t0_%d" % q,
                          self.pcol(nm, 0, 64), oc(nm, 0, 64), rows=64)
                fw.act(twl[:], t_[0][0:64, :], AF.Tanh, reads=["a_t0_0"], writes=["a_twl"])
                fw.v("tensor_copy", alb[:], t_[1][0:64, :], reads=["a_t0_1"], writes=["a_alb"])
                def abody(ct, g=g, hk=hk, hg=hg):
                    p_ = ct % 2
                    PSp = PS[4 * p_:4 * p_ + 4]
                    pk = lambda q: "ps%d" % (4 * p_ + q)
                    ub, us, t_ = ubP[p_], usP[p_], tP[p_]
                    psb0 = PSp[0][:].bitcast(BF16)
                    psb1 = PSp[1][:].bitcast(BF16)
                    cs = slice(ct * 128, (ct + 1) * 128)
                    for q, (coff, nm) in enumerate([(0, "mu_r"), (512, "mu_k"), (1024, "mu_v"), (1664, "mu_g")]):
                        for k in range(8):
                            fw.mm(PSp[q][:], wA[:, k, coff + ct * 128:coff + (ct + 1) * 128], hg[:, k, :], start=(k == 0), stop=(k == 7),
                                  reads=[("wA", k), hk], writes=[pk(q)])
                            yield
                        shift(PSp[q][:], pk(q), ub[q], "a_ub%d_%d" % (p_, q), ct * 4 + q, us[q][:], "a_us%d_%d" % (p_, q),
                              self.pcol(nm, ct), oc(nm, ct))
                        yield
                    r_s, k_s, v_s, g_s = us
                    K = lambda i: "a_t%d_%d" % (p_, i)
                    fw.mm(PSp[0][:], w2b[:, 0, cs], twl[:], reads=[("a_w2b", 0), "a_twl"], writes=[pk(0)])
                    yield
                    fw.act(t_[0][:], PSp[0][:], AF.Sigmoid, bias=self.pcol("w0", ct), reads=[pk(0), "prm"], writes=[K(0)])
                    yield
                    fw.v("tensor_scalar", t_[0][:], t_[0][:], -CDEC, 0.0, ALU.mult, ALU.add, reads=[K(0)], writes=[K(0)], eng="gpsimd")
                    yield
                    fw.mm(PSp[1][:], a2b[:, 0, cs], alb[:], reads=[("a_a2b", 0), "a_alb"], writes=[pk(1)])
                    yield
                    fw.act(t_[1][:], PSp[1][:], AF.Sigmoid, bias=self.pcol("a0", ct), reads=[pk(1), "prm"], writes=[K(1)])
                    yield
                    fw.v("tensor_scalar", t_[2][:], k_s[:], self.pcol("k_k", ct), None, ALU.mult, reads=["a_us%d_1" % p_, "prm"], writes=[K(2)])
                    yield
                    fw.v("tensor_tensor", t_[3][:], t_[2][:], t_[2][:], ALU.mult, reads=[K(2)], writes=[K(3)], eng="gpsimd")
                    yield
                    fw.mm(PSp[2][:], self.blk1[:], t_[3][:], reads=["k_blk1", K(3)], writes=[pk(2)])
                    yield
                    fw.act(t_[3][:], PSp[2][:], AF.Sqrt, bias=self.tiny_col[:, 0:1], reads=[pk(2), "tiny"], writes=[K(3)])
                    yield
                    fw.v("reciprocal", t_[3][:], t_[3][:], reads=[K(3)], writes=[K(3)])
                    yield
                    fw.v("tensor_tensor", t_[2][:], t_[2][:], t_[3][:], ALU.mult, reads=[K(2), K(3)], writes=[K(2)])
                    yield
                    fw.v("tensor_scalar", t_[3][:], t_[1][:], self.pcol("k_a", ct), oc("k_a", ct), ALU.mult, ALU.add,
                         reads=[K(1), "prm", "a_omu"], writes=[K(3)])
                    yield
                    fw.v("tensor_tensor", t_[3][:], t_[3][:], k_s[:], ALU.mult, reads=[K(3), "a_us%d_1" % p_], writes=[K(3)], eng="gpsimd")
                    yield
                    fw.v("tensor_tensor", t_[4][:], t_[2][:], t_[1][:], ALU.mult, reads=[K(2), K(1)], writes=[K(4)], eng="gpsimd")
                    yield
                    fw.v("tensor_tensor_scan", t_[5][:], self.rstm[:], t_[0][:], 0.0, ALU.mult, ALU.add,
                         reads=["k_rstm", K(0)], writes=[K(5)])
                    yield
                    fw.v("tensor_tensor", t_[6][:], t_[5][:], t_[0][:], ALU.subtract, reads=[K(5), K(0)], writes=[K(6)], eng="gpsimd")
                    yield
                    fw.act(t_[6][:], t_[6][:], AF.Exp, reads=[K(6)], writes=[K(6)])
                    yield
                    fw.act(t_[7][:], t_[5][:], AF.Exp, scale=-1.0, reads=[K(5)], writes=[K(7)])
                    yield
                    fw.act(t_[5][:], t_[5][:], AF.Exp, reads=[K(5)], writes=[K(5)])
                    yield
                    fw.v("tensor_copy", PC[:, ct, :], t_[5][:].rearrange("p (c t) -> p c t", t=128)[:, :, 127], reads=[K(5)],
                         writes=["a_PC"], eng="gpsimd")
                    yield
                    v3 = lambda ap: ap.rearrange("p (c t) -> p c t", t=128)
                    akey = "a_art%d" % ct
                    fw.v("scalar_tensor_tensor", art[ct][:, :, 0, :], v3(t_[2][:]), -1.0, v3(t_[6][:]), ALU.mult, ALU.mult,
                         reads=[K(2), K(6)], writes=[akey])
                    yield
                    fw.v("tensor_tensor", art[ct][:, :, 1, :], v3(r_s[:]), v3(t_[5][:]), ALU.mult, reads=["a_us%d_0" % p_, K(5)], writes=[akey])
                    yield
                    fw.v("tensor_tensor", bk[ct][:, 0, :], t_[4][:], t_[7][:], ALU.mult, reads=[K(4), K(7)], writes=["a_bk%d" % ct])
                    yield
                    fw.v("tensor_tensor", bk[ct][:, 1, :], t_[3][:], t_[7][:], ALU.mult, reads=[K(3), K(7)], writes=["a_bk%d" % ct], eng="gpsimd")
                    yield
                    fw.v("tensor_copy", vb[ct][:], v_s[:], reads=["a_us%d_2" % p_], writes=["a_vb%d" % ct], eng="gpsimd")
                    yield
                    fw.v("scalar_tensor_tensor", t_[4][:], r_s[:], self.pcol("r_k", ct), t_[3][:], ALU.mult, ALU.mult,
                         reads=["a_us%d_0" % p_, "prm", K(3), K(4)], writes=[K(4)])
                    yield
                    fw.mm(PSp[3][:], self.blk1[:], t_[4][:], reads=["k_blk1", K(4)], writes=[pk(3)])
                    yield
                    fw.v("tensor_tensor", bonus[ct][:], PSp[3][:], v_s[:], ALU.mult, reads=[pk(3), "a_us%d_2" % p_], writes=["a_bonus%d" % ct])
                    yield
                    fw.act(sgt[ct][:], g_s[:], AF.Silu, reads=["a_us%d_3" % p_], writes=["a_sg%d" % ct])
                    yield
                    for half in range(2):
                        psb, pkey = (psb0, pk(0)) if half == 0 else (psb1, pk(1))
                        for cc in range(2):
                            c = half * 2 + cc
                            for qi_, (src, skey) in enumerate([(bk[ct][:, 0, c * 128:(c + 1) * 128], "a_bk%d" % ct),
                                                               (bk[ct][:, 1, c * 128:(c + 1) * 128], "a_bk%d" % ct),
                                                               (vb[ct][:, c * 128:(c + 1) * 128], "a_vb%d" % ct)]):
                                o = (cc * 3 + qi_) * 128
                                fw.tr(psb[:, o:o + 128], src, self.ident_b[:], reads=[skey, "k_ident"], writes=[pkey])
                                yield
                        fw.act(tok[ct][:, half * 2:half * 2 + 2, :, :].rearrange("p a b c -> p (a b c)"), psb[:, 0:768], AF.Copy,
                               reads=[pkey], writes=["a_tok%d" % ct])
                        yield

                fw.lockstep([abody(0), abody(1)])
                fw.lockstep([abody(2), abody(3)])
                PSALL = self.PSALL
                idb4 = self.ident_b[:].unsqueeze(1).to_broadcast([128, 4, 128])
                mui2 = self.mask_ui[:].unsqueeze(1).to_broadcast([128, 2, 256])
                msl4 = self.mask_sl[:].unsqueeze(1).to_broadcast([128, 4, 128])
                for c in range(4):
                    ccols = slice(c * 128, (c + 1) * 128)

                    def hv(qd, hi):
                        h = 2 * hi + qd
                        ct, hp = hi, qd
                        pr_ = slice(hp * 64, hp * 64 + 64)
                        d = dict(h=h, ct=ct, hp=hp, pr=pr_, po=hp * 64,
                                 at=art[ct][pr_, c, 0, :], rt=art[ct][pr_, c, 1, :],
                                 ar=art[ct][pr_, c, :, :].rearrange("p a t -> p (a t)"),
                                 bt=bk[ct][pr_, 0, ccols], kt=bk[ct][pr_, 1, ccols],
                                 rk=["a_art%d" % ct, "a_bk%d" % ct], tkey="a_tok%d" % ct,
                                 vt=tok[ct][:, c, 2, hp * 64:hp * 64 + 64], btk=tok[ct][:, c, 0, hp * 64:hp * 64 + 64],
                                 ktk=tok[ct][:, c, 1, hp * 64:hp * 64 + 64], T0b=Tb[pr_, ct, :])
                        return d

                    XYk = lambda qd: ["ps%d" % (3 * qd), "ps%d" % (3 * qd + 1)]
                    Zk = lambda qd: ["ps%d" % (3 * qd + 2)]
                    XY = lambda qd: PSALL[:, 3 * qd:3 * qd + 2, :].rearrange("p b (h x) -> p (b h) x", x=256)
                    Zv = lambda qd: PSALL[:, 3 * qd + 2, :].rearrange("p (h x) -> p h x", x=128)
                    import os as _os
                    _stop = int(_os.environ.get("A_STOP", "99"))
                    if _stop <= 0:
                        continue
                    for qd in range(2):
                        for hi in range(4):
                            d = hv(qd, hi)
                            fw.mm(XY(qd)[:, hi, :], d["bt"], d["ar"], reads=d["rk"], writes=[XYk(qd)[hi // 2]])
                            fw.mm(Zv(qd)[:, hi, :], d["at"], d["bt"], reads=d["rk"], writes=Zk(qd))
                    for qd in range(2):
                        for b2 in range(2):
                            fw.v("tensor_tensor", LAb[qd][:, 2 * b2:2 * b2 + 2, :], XY(qd)[:, 2 * b2:2 * b2 + 2, :], mui2, ALU.mult,
                                 reads=[XYk(qd)[b2], "k_mask_ui"], writes=["a_LAb%d" % qd])
                        fw.v("tensor_tensor", Lb[qd][:], Zv(qd), msl4, ALU.mult, reads=Zk(qd) + ["k_mask_sl"], writes=["a_Lb%d" % qd])
                        fw.v("tensor_tensor", XT[qd][0][:], LAb[qd][:, :, 0:128], idb4, ALU.add,
                             reads=["a_LAb%d" % qd, "k_ident"], writes=["a_XT%d_0" % qd], eng="gpsimd")
                    if _stop <= 1:
                        continue
                    for qd in range(2):
                        for hi in range(4):
                            d = hv(qd, hi)
                            fw.mm(XY(qd)[:, hi, :], d["kt"], d["ar"], reads=d["rk"], writes=[XYk(qd)[hi // 2]])
                    for qd in range(2):
                        for b2 in range(2):
                            fw.v("tensor_tensor", KAb[qd][:, 2 * b2:2 * b2 + 2, :], XY(qd)[:, 2 * b2:2 * b2 + 2, :], mui2, ALU.mult,
                                 reads=[XYk(qd)[b2], "k_mask_ui"], writes=["a_KAb%d" % qd])
                    if _stop <= 2:
                        continue
                    for k in range(1, 8):
                        for qd in range(2):
                            if k == 1:
                                Pp, PTp, pkeys = (lambda hi: Lb[qd][:, hi, :]), (lambda hi: LAb[qd][:, hi, 0:128]), ["a_Lb%d" % qd, "a_LAb%d" % qd]
                            else:
                                pb_ = PPb[qd][(k - 1) % 2]
                                Pp, PTp, pkeys = (lambda hi, pb_=pb_: pb_[:, hi, 0:128]), (lambda hi, pb_=pb_: pb_[:, hi, 128:256]), ["a_PPb%d_%d" % (qd, (k - 1) % 2)]
                            for hi in range(4):
                                if k <= 6:
                                    fw.mm(XY(qd)[:, hi, 0:128], PTp(hi), Pp(hi), reads=pkeys, writes=[XYk(qd)[hi // 2]])
                                    fw.mm(XY(qd)[:, hi, 128:256], Pp(hi), PTp(hi), reads=pkeys, writes=[XYk(qd)[hi // 2]])
                                if k >= 2:
                                    xo = XT[qd][(k - 2) % 2]
                                    xok = "a_XT%d_%d" % (qd, (k - 2) % 2)
                                    fw.mm(Zv(qd)[:, hi, :], self.ident_b[:], xo[:, hi, :], start=True, stop=False, reads=["k_ident", xok], writes=Zk(qd))
                                    fw.mm(Zv(qd)[:, hi, :], Pp(hi), xo[:, hi, :], start=False, stop=True, reads=pkeys + [xok], writes=Zk(qd))
                        for qd in range(2):
                            if k <= 6:
                                for b2 in range(2):
                                    fw.act(PPb[qd][k % 2][:, 2 * b2:2 * b2 + 2, :], XY(qd)[:, 2 * b2:2 * b2 + 2, :], AF.Copy,
                                           reads=[XYk(qd)[b2]], writes=["a_PPb%d_%d" % (qd, k % 2)])
                            if k >= 2:
                                fw.v("tensor_copy", XT[qd][(k - 1) % 2][:], Zv(qd), reads=Zk(qd), writes=["a_XT%d_%d" % (qd, (k - 1) % 2)])
                    if _stop <= 3:
                        continue
                    XTf = [XT[qd][0] for qd in range(2)]
                    xfk = ["a_XT%d_0" % qd for qd in range(2)]
                    Wv = lambda qd: PSALL[:, 3 * qd + 2, 0:256].rearrange("p (h x) -> p h x", x=64)
                    Uv = lambda qd: PSALL[:, 3 * qd + 2, 256:512].rearrange("p (h x) -> p h x", x=64)
                    for qd in range(2):
                        for hi in range(4):
                            d = hv(qd, hi)
                            fw.mm(Wv(qd)[:, hi, :], d["at"], d["T0b"], start=True, stop=False, reads=["a_art%d" % d["ct"], "a_Tb"], writes=Zk(qd))
                            fw.mm(Wv(qd)[:, hi, :], KAb[qd][:, hi, 0:128], d["vt"], start=False, stop=True, reads=["a_KAb%d" % qd, d["tkey"]], writes=Zk(qd))
                        fw.v("tensor_copy", Wb[qd][:], Wv(qd), reads=Zk(qd), writes=["a_Wb%d" % qd])
                    for qd in range(2):
                        for hi in range(4):
                            fw.mm(Uv(qd)[:, hi, :], XTf[qd][:, hi, :], Wb[qd][:, hi, :], reads=[xfk[qd], "a_Wb%d" % qd], writes=Zk(qd))
                        fw.v("tensor_copy", Ub[qd][:], Uv(qd), reads=Zk(qd), writes=["a_Ub%d" % qd])
                    for qd in range(2):
                        for hi in range(4):
                            d = hv(qd, hi)
                            h, ct = d["h"], d["ct"]
                            ob, okey = (PS[6], "ps6") if qd == 0 else (PS[7], "ps7")
                            osl = slice(qd * 256 + hi * 64, qd * 256 + (hi + 1) * 64)
                            fw.mm(ob[:, osl], d["rt"], d["T0b"], start=True, stop=False, reads=["a_art%d" % ct, "a_Tb"], writes=[okey])
                            fw.mm(ob[:, osl], LAb[qd][:, hi, 128:256], Ub[qd][:, hi, :], start=False, stop=False,
                                  reads=["a_LAb%d" % qd, "a_Ub%d" % qd], writes=[okey])
                            fw.mm(ob[:, osl], KAb[qd][:, hi, 128:256], d["vt"], start=False, stop=True, reads=["a_KAb%d" % qd, d["tkey"]], writes=[okey])
                            zsl = slice(ct * 64, (ct + 1) * 64)
                            fw.mm(PS[7][d["pr"], zsl], d["btk"], Ub[qd][:, hi, :], start=True, stop=False, reads=[d["tkey"], "a_Ub%d" % qd], writes=["ps7"])
                            fw.mm(PS[7][d["pr"], zsl], d["ktk"], d["vt"], start=False, stop=True, reads=[d["tkey"]], writes=["ps7"])
                    if _stop <= 4:
                        continue
                    zall = PS[7][:, 0:256].rearrange("p (c i) -> p c i", i=64)
                    fw.v("tensor_tensor", T[:], T[:], zall, ALU.add, reads=["a_T", "ps7"], writes=["a_T"])
                    fw.v("tensor_tensor", T[:], T[:], PC[:, :, c:c + 1].to_broadcast([128, 4, 64]), ALU.mult, reads=["a_T", "a_PC"], writes=["a_T"])
                    fw.v("tensor_copy", Tb[:], T[:], reads=["a_T"], writes=["a_Tb"], eng="gpsimd")
                    ov = [PS[6][:, 0:256].rearrange("p (h i) -> p h i", i=64), PS[7][:, 256:512].rearrange("p (h i) -> p h i", i=64)]
                    okeys = ["ps6", "ps7"]
                    for qd in range(2):
                        fw.v("tensor_reduce", st8[:, 0, qd * 4:qd * 4 + 4], ov[qd], AX.X, ALU.add, reads=[okeys[qd]], writes=["a_st8"])
                    fw.v("tensor_scalar", st8[:, 0, :], st8[:, 0, :], 1.0 / 64, None, ALU.mult, reads=["a_st8"], writes=["a_st8"])
                    for qd in range(2):
                        fw.v("tensor_tensor", xc[:, qd * 4:qd * 4 + 4, :], ov[qd],
                             st8[:, 0, qd * 4:qd * 4 + 4].unsqueeze(2).to_broadcast([128, 4, 64]), ALU.subtract,
                             reads=[okeys[qd], "a_st8"], writes=["a_xc"])
                    fw.v("tensor_tensor", sq[:], xc[:], xc[:], ALU.mult, reads=["a_xc"], writes=["a_sq"], eng="gpsimd")
                    fw.v("tensor_reduce", st8[:, 1, :], sq[:], AX.X, ALU.add, reads=["a_sq"], writes=["a_st8"])
                    fw.act(st8[:, 1, :], st8[:, 1, :], AF.Sqrt, bias=self.gneps_col[:, 0:1], scale=1.0 / 64, reads=["a_st8", "tiny"], writes=["a_st8"])
                    fw.v("reciprocal", st8[:, 1, :], st8[:, 1, :], reads=["a_st8"], writes=["a_st8"])
                    fw.v("tensor_tensor", onb[:].rearrange("p (h i) -> p h i", i=64), xc[:],
                         st8[:, 1, :].unsqueeze(2).to_broadcast([128, 8, 64]), ALU.mult, reads=["a_xc", "a_st8"], writes=["a_onb"])
                    for ct in range(4):
                        for hp in range(2):
                            qo = (hp * 4 + ct) * 64
                            fw.tr(psb0[hp * 64:(hp + 1) * 64, ct * 128:(ct + 1) * 128], onb[:, qo:qo + 64], self.ident_b[:],
                                  reads=["a_onb", "k_ident"], writes=["ps0"])
                    for ct in range(4):
                        j = ct % 2
                        fw.v("tensor_scalar", yv[j][:], psb0[:, ct * 128:(ct + 1) * 128], self.pcol("gn_g", ct), self.pcol("gn_b", ct),
                             ALU.mult, ALU.add, reads=["ps0", "prm"], writes=["a_yv%d" % j])
                        fw.v("tensor_tensor", yv[j][:], yv[j][:], bonus[ct][:, ccols], ALU.add, reads=["a_yv%d" % j, "a_bonus%d" % ct],
                             writes=["a_yv%d" % j], eng="gpsimd")
                        fw.v("tensor_tensor", yout[:, ct, ccols], yv[j][:], sgt[ct][:, ccols], ALU.mult,
                             reads=["a_yv%d" % j, "a_sg%d" % ct], writes=["a_yout"], eng="gpsimd")
                fw.dma(self.yT[0][:, :, c0:c0 + 512].rearrange("k p s -> p k s"), yout[:], reads=["a_yout"],
                       writes=[("yT0", g, ct) for ct in range(4)], eng="gpsimd")
            self.release(keys)


    def phase_R(self):
        fw, S, G = self.fw, self.S, self.G
        TWO_PI = 6.283185307179586
        C1 = 6.28125
        C2 = 0.0019350051879882812
        C3 = TWO_PI - C1 - C2
        PI = 3.1415925
        with ExitStack() as ph:
            sb = lambda n, s, d: ph.enter_context(self.nc.sbuf_tensor(self.uname(n), list(s), d))
            posi = sb("r_posi", [128, 512], I32)
            a = sb("r_a", [128, 512], F32)
            k = sb("r_k", [128, 512], F32)
            r = sb("r_r", [128, 512], F32)
            r2 = sb("r_r2", [128, 512], F32)
            m = sb("r_m", [128, 512], F32)
            cs = sb("r_cs", [128, 2, 512], F32)
            keys = ["r_posi", "r_a", "r_k", "r_r", "r_r2", "r_m", "r_cs"]
            self.acquire(keys)
            for g in range(G):
                c0 = g * 512
                fw.dma(posi[:], self.pos[0:1, c0:c0 + 512].to_broadcast([128, 512]), writes=["r_posi"])
                fw.v("tensor_copy", a[:], posi[:], reads=["r_posi"], writes=["r_a"])
                fw.v("tensor_scalar", a[:], a[:], self.cst_sb[:, 0:1], None, ALU.mult, reads=["r_a", "cst"], writes=["r_a"])
                fw.v("tensor_scalar", k[:], a[:], 1.0 / TWO_PI, None, ALU.mult, reads=["r_a"], writes=["r_k"])
                fw.v("tensor_scalar", k[:], k[:], 12582912.0, None, ALU.add, reads=["r_k"], writes=["r_k"])
                fw.v("tensor_scalar", k[:], k[:], 12582912.0, None, ALU.subtract, reads=["r_k"], writes=["r_k"])
                fw.v("scalar_tensor_tensor", r[:], k[:], -C1, a[:], ALU.mult, ALU.add, reads=["r_k", "r_a"], writes=["r_r"])
                fw.v("scalar_tensor_tensor", r[:], k[:], -C2, r[:], ALU.mult, ALU.add, reads=["r_k", "r_r"], writes=["r_r"])
                fw.v("scalar_tensor_tensor", r[:], k[:], -C3, r[:], ALU.mult, ALU.add, reads=["r_k", "r_r"], writes=["r_r"])
                fw.v("tensor_scalar", r[:], r[:], PI, -PI, ALU.min, ALU.max, reads=["r_r"], writes=["r_r"])
                fw.v("tensor_scalar", r2[:], r[:], TWO_PI / 4, None, ALU.add, reads=["r_r"], writes=["r_r2"])
                fw.v("tensor_scalar", m[:], r2[:], PI, -TWO_PI, ALU.is_gt, ALU.mult, reads=["r_r2"], writes=["r_m"])
                fw.v("tensor_tensor", r2[:], r2[:], m[:], ALU.add, reads=["r_r2", "r_m"], writes=["r_r2"])
                fw.v("tensor_scalar", r2[:], r2[:], PI, -PI, ALU.min, ALU.max, reads=["r_r2"], writes=["r_r2"])
                fw.act(cs[:, 0, :], r2[:], AF.Sin, reads=["r_r2"], writes=["r_cs"])
                fw.act(cs[:, 1, :], r[:], AF.Sin, reads=["r_r"], writes=["r_cs"])
                fw.dma(self.ropeT[:, :, c0:c0 + 512].rearrange("k p s -> p k s"), cs[:], reads=["r_cs"], writes=[("ropeT", g)], eng="gpsimd")
            self.release(keys)

    def phase_B(self, l):
        fw, S, G = self.fw, self.S, self.G
        PS = self.PS
        NT = S // 128
        NIT = 20
        with ExitStack() as ph:
            allkeys = []

            def sb(n, s, d):
                allkeys.append(n)
                return ph.enter_context(self.nc.sbuf_tensor(self.uname(n), list(s), d))

            wB = sb("wB", [128, 8, 2372], BF16)
            wkd = sb("b_wkd", [128, 8, 128], BF16)
            ropeR = sb("b_ropeR", [128, 1, 128], BF16)
            KT = [sb("b_KT%d" % ct, [128, S], BF16) for ct in range(4)]
            KI = sb("b_KI", [128, S], BF16)
            V = sb("b_V", [128, NT, 8, 65], BF16)
            hn = sb("b_hn", [128, 8, 512], BF16)
            QT = [[sb("b_QT%d_%d" % (ct, i), [128, 512], BF16) for ct in range(4)] for i in range(2)]
            QI = [[sb("b_QI%d_%d" % (j, i), [128, 512], BF16) for j in range(2)] for i in range(2)]
            SG = [[sb("b_SG%d_%d" % (ct, i), [128, 512], BF16) for ct in range(4)] for i in range(2)]
            WI = [sb("b_WI%d" % i, [128, 4, 4], F32) for i in range(2)]
            yout = [sb("b_yout0", [128, 4, 512], BF16)] * 2
            score = sb("b_score", [128, S], F32)
            alias = S >= 4096
            if alias:
                x_ = score[:, 0:512]
                x2 = score[:, 512:1024]
                t1 = score[:, 1024:1536]
                t2 = score[:, 1536:2048]
                cs = score[:, 2048:3072].rearrange("p (a b) -> p a b", b=512)
                xb = score[:, 3072:3328].bitcast(BF16)
            else:
                cs = sb("b_cs", [128, 2, 512], F32)
                x_ = sb("b_x", [128, 512], F32)
                x2 = sb("b_x2", [128, 512], F32)
                xb = sb("b_xb", [128, 512], BF16)
                t1 = sb("b_t1", [128, 512], F32)
                t2 = sb("b_t2", [128, 512], F32)
            tkeys = ["b_cs", "b_x", "b_x2", "b_xb", "b_t1", "b_t2"]
            mm1 = [sb("b_mm1_0", [128, S], BF16)] * 2
            MT = [sb("b_MT%d" % i, [128, NT, 128], BF16) for i in range(2)]
            E = [sb("b_E%d" % i, [128, 512], BF16) for i in range(4)]
            rl = [sb("b_rl%d" % i, [128, 512], F32) for i in range(2)]
            PT = [sb("b_PT%d" % i, [128, 512], BF16) for i in range(4)]
            bs = sb("b_bs", [128, 8], F32)
            steps = sb("b_steps", [128, NIT + 1], F32)
            rec = sb("b_rec", [128, 8], F32)
            otok = sb("b_otok", [128, 8, 64], BF16)
            dmask = sb("b_dmask", [128, 128], F32)
            keys = allkeys + tkeys + [("wB", k) for k in range(8)] + [("b_wkd", k) for k in range(8)] + [("b_ropeR", 0)]
            self.acquire(keys + ["stg0", "stg1"])
            self.load_w(wB, "wB", lambda k: self.w_in[l, k * 128:(k + 1) * 128, OFF_B:OFF_B + 2372], 2372, 8, self.gcol)
            self.load_w(wkd, "b_wkd", lambda k: self.w_kidup[l, k * 128:(k + 1) * 128, :], 128, 8, self.gcol)
            self.load_w(ropeR, "b_ropeR", lambda k: self.ropeR_d, 128, 1)
            fw.v("memset", V[:], 1.0, writes=["b_V"], eng="gpsimd")
            fw.v("memset", dmask[:], 0.0, writes=["b_dmask"], eng="gpsimd")
            fw.v("memset", dmask[0:64, 64:128], -1e30, writes=["b_dmask"], eng="gpsimd")

            def rope(src_f32, skey, dst, dkey):
                fw.v("tensor_copy", xb[:], src_f32, reads=[skey], writes=["b_xb"], eng="gpsimd")
                fw.mm(PS[1][:], ropeR[:, 0, :], xb[:], reads=[("b_ropeR", 0), "b_xb"], writes=["ps1"])
                fw.v("tensor_tensor", t1[:], src_f32, cs[:, 0, :], ALU.mult, reads=[skey, "b_cs"], writes=["b_t1"], eng="gpsimd")
                fw.v("tensor_tensor", t2[:], PS[1][:], cs[:, 1, :], ALU.mult, reads=["ps1", "b_cs"], writes=["b_t2"])
                fw.v("tensor_tensor", dst, t1[:], t2[:], ALU.add, reads=["b_t1", "b_t2"], writes=[dkey], eng="gpsimd")

            def proj(ps, pskey, w, wkey, c_lo, c_hi):
                for k in range(8):
                    fw.mm(ps, w[:, k, c_lo:c_hi], hn[:, k, :], start=(k == 0), stop=(k == 7), reads=[(wkey, k), "b_hn"], writes=[pskey])

            def prep(g):
                gp = g % 2
                gc = slice(g * 512, g * 512 + 512)
                if alias:
                    self.release(["b_score"])
                    self.acquire(tkeys)
                fw.dma(hn[:], self.hnT[:, :, gc].rearrange("k p s -> p k s"), reads=[("hnT", g)], writes=["b_hn"])
                fw.dma(cs[:], self.ropeT[:, :, gc].rearrange("k p s -> p k s"), reads=[("ropeT", g)], writes=["b_cs"])
                for ct in range(4):
                    for which, coff, gname in (("q", 0, "q_g"), ("k", 512, "k_g")):
                        proj(PS[0][:], "ps0", wB, "wB", coff + ct * 128, coff + (ct + 1) * 128)
                        fw.act(x_[:], PS[0][:], AF.Copy, reads=["ps0"], writes=["b_x"])
                        fw.v("tensor_tensor", x2[:], x_[:], x_[:], ALU.mult, reads=["b_x"], writes=["b_x2"], eng="gpsimd")
                        fw.mm(PS[1][:], self.blk1[:], x2[:], reads=["k_blk1", "b_x2"], writes=["ps1"])
                        fw.act(x2[:], PS[1][:], AF.Sqrt, bias=self.eps6_col[:, 0:1], scale=1.0 / 64, reads=["ps1", "tiny"], writes=["b_x2"])
                        fw.v("reciprocal", x2[:], x2[:], reads=["b_x2"], writes=["b_x2"])
                        fw.v("scalar_tensor_tensor", x_[:], x_[:], self.pcol(gname, 0), x2[:], ALU.mult, ALU.mult,
                             reads=["b_x", "prm", "b_x2"], writes=["b_x"])
                        if which == "q":
                            rope(x_[:], "b_x", QT[gp][ct][:], "b_QT%d_%d" % (ct, gp))
                        else:
                            rope(x_[:], "b_x", KT[ct][:, gc], "b_KT%d" % ct)
                for j in range(2):
                    proj(PS[0][:], "ps0", wB, "wB", 1536 + j * 128, 1536 + (j + 1) * 128)
                    fw.act(x_[:], PS[0][:], AF.Copy, reads=["ps0"], writes=["b_x"])
                    rope(x_[:], "b_x", QI[gp][j][:], "b_QI%d_%d" % (j, gp))
                proj(PS[0][:], "ps0", wkd, "b_wkd", 0, 128)
                fw.act(x_[:], PS[0][:], AF.Copy, reads=["ps0"], writes=["b_x"])
                rope(x_[:], "b_x", KI[:, gc], "b_KI")
                for ct in range(4):
                    proj(PS[0][:], "ps0", wB, "wB", 1860 + ct * 128, 1860 + (ct + 1) * 128)
                    fw.act(SG[gp][ct][:], PS[0][:], AF.Silu, reads=["ps0"], writes=["b_SG%d_%d" % (ct, gp)])
                for tt in range(4):
                    tcols = slice(tt * 128, (tt + 1) * 128)
                    for k in range(8):
                        fw.mm(PS[0][:], hn[:, k, tcols], wB[:, k, 1024:1536], start=(k == 0), stop=(k == 7),
                              reads=[("wB", k), "b_hn"], writes=["ps0"])
                    fw.act(V[:, g * 4 + tt, :, 0:64], PS[0][:].rearrange("p (h i) -> p h i", i=64), AF.Copy, reads=["ps0"], writes=["b_V"])
                    for k in range(8):
                        fw.mm(PS[1][:, 0:4], hn[:, k, tcols], wB[:, k, 1856:1860], start=(k == 0), stop=(k == 7),
                              reads=[("wB", k), "b_hn"], writes=["ps1"])
                    fw.v("tensor_scalar", WI[gp][:, tt, :], PS[1][:, 0:4], 1.0 / 16, None, ALU.mult, reads=["ps1"], writes=["b_WI%d" % gp])
                if alias:
                    self.release(tkeys)
                    self.acquire(["b_score"])

            def scores(qt):
                g, tt = qt // 4, qt % 4
                gp = g % 2
                N = (qt + 1) * 128
                tq = slice(tt * 128, (tt + 1) * 128)
                for pc in range((N + 511) // 512):
                    p0 = pc * 512
                    pn = min(512, N - p0)
                    for ih in range(4):
                        po = (ih % 2) * 64
                        fw.mm(PS[ih][:, 0:pn], QI[gp][ih // 2][po:po + 64, tq], KI[po:po + 64, p0:p0 + pn],
                              reads=["b_QI%d_%d" % (ih // 2, gp), "b_KI"], writes=["ps%d" % ih])
                    for ih in range(4):
                        r_ = rl[ih % 2]
                        rkey = "b_rl%d" % (ih % 2)
                        fw.act(r_[:, 0:pn], PS[ih][:, 0:pn], AF.Relu, reads=["ps%d" % ih], writes=[rkey])
                        if ih == 0:
                            fw.v("tensor_scalar", score[:, p0:p0 + pn], r_[:, 0:pn], WI[gp][:, tt, 0:1], None, ALU.mult,
                                 reads=[rkey, "b_WI%d" % gp], writes=["b_score"])
                        else:
                            fw.v("scalar_tensor_tensor", score[:, p0:p0 + pn], r_[:, 0:pn], WI[gp][:, tt, ih:ih + 1], score[:, p0:p0 + pn],
                                 ALU.mult, ALU.add, reads=[rkey, "b_WI%d" % gp, "b_score"], writes=["b_score"])

            def bisect_mask(qt):
                NB = qt + 1
                N = NB * 128
                mk = mm1[0]
                mkey = "b_mm1_0"
                A, lo, mid, cnt, tmp = (bs[:, i:i + 1] for i in range(5))
                if NB >= 3:
                    fw.v("tensor_reduce", A, score[:, 0:N], AX.X, ALU.max, apply_absolute_value=True, reads=["b_score"], writes=["b_bs"])
                    fw.v("tensor_scalar", A, A, 1.0001, 1e-20, ALU.mult, ALU.add, reads=["b_bs"], writes=["b_bs"])
                fw.v("tensor_tensor", score[:, N - 128:N], score[:, N - 128:N], dmask[:], ALU.add, reads=["b_score", "b_dmask"],
                     writes=["b_score"])
                if NB >= 3:
                    fw.v("tensor_scalar", steps[:], self.cst_sb[:, 1:2 + NIT], A, None, ALU.mult, reads=["cst", "b_bs"], writes=["b_steps"])
                    fw.v("tensor_scalar", mid, A, -1.0, steps[:, 0:1], ALU.mult, ALU.add, reads=["b_bs", "b_steps"], writes=["b_bs"])
                    for it in range(NIT):
                        fw.v("tensor_scalar", mk[:, 0:N], score[:, 0:N], mid, None, ALU.is_ge, ALU.add, accum_out=cnt,
                             reads=["b_score", "b_bs", mkey], writes=[mkey, "b_bs"])
                        fw.v("tensor_scalar", tmp, cnt, 255.5, steps[:, it:it + 1], ALU.is_ge, ALU.mult, reads=["b_bs", "b_steps"], writes=["b_bs"])
                        fw.v("scalar_tensor_tensor", mid, tmp, steps[:, it + 1:it + 2], mid, ALU.subtract, ALU.add,
                             reads=["b_bs", "b_steps"], writes=["b_bs"])
                    fw.v("tensor_tensor", lo, mid, steps[:, NIT:NIT + 1], ALU.subtract, reads=["b_bs", "b_steps"], writes=["b_bs"])
                else:
                    fw.v("memset", lo, -1e29, writes=["b_bs"])
                fw.v("tensor_scalar", mk[:, 0:N], score[:, 0:N], lo, None, ALU.is_ge, reads=["b_score", "b_bs"], writes=[mkey])
                psb1 = PS[1][:].bitcast(BF16)
                mt, mtkey = MT[qt % 2], "b_MT%d" % (qt % 2)
                for kb0 in range(0, NB, 8):
                    nk = min(8, NB - kb0)
                    for j in range(nk):
                        kb = kb0 + j
                        fw.tr(psb1[:, j * 128:(j + 1) * 128], mk[:, kb * 128:(kb + 1) * 128], self.ident_b[:],
                              reads=[mkey, "k_ident"], writes=["ps1"])
                    fw.act(mt[:, kb0:kb0 + nk, :].rearrange("p a b -> p (a b)"), psb1[:, 0:nk * 128], AF.Copy, reads=["ps1"], writes=[mtkey])

            def attention(qt):
                g, tt = qt // 4, qt % 4
                gp = g % 2
                NB = qt + 1
                tq = slice(tt * 128, (tt + 1) * 128)
                mt, mtkey = MT[qt % 2], "b_MT%d" % (qt % 2)
                for hpair in range(4):
                    ct = hpair
                    for gi, kb0 in enumerate(range(0, NB, 4)):
                        nk = min(4, NB - kb0)
                        bis = [2 * e + gi % 2 for e in range(2)]
                        for j in range(nk):
                            kb = kb0 + j
                            for e in range(2):
                                po = e * 64
                                pl = PS[2 + bis[e]]
                                fw.mm(pl[:, j * 128:(j + 1) * 128], KT[ct][po:po + 64, kb * 128:(kb + 1) * 128], QT[gp][ct][po:po + 64, tq],
                                      reads=["b_KT%d" % ct, "b_QT%d_%d" % (ct, gp)], writes=["ps%d" % (2 + bis[e])])
                        for e in range(2):
                            bi = bis[e]
                            fw.act(E[bi][:, 0:nk * 128], PS[2 + bi][:, 0:nk * 128], AF.Exp, scale=0.125, reads=["ps%d" % (2 + bi)], writes=["b_E%d" % bi])
                            fw.v("tensor_tensor", PT[bi][:, 0:nk * 128], E[bi][:, 0:nk * 128],
                                 mt[:, kb0:kb0 + nk, :].rearrange("p a b -> p (a b)"), ALU.mult,
                                 reads=["b_E%d" % bi, mtkey], writes=["b_PT%d" % bi], eng="gpsimd")
                        for e in range(2):
                            h = 2 * hpair + e
                            bi = bis[e]
                            pob = PS[7] if e == 0 else PS[6]
                            pokey = "ps7" if e == 0 else "ps6"
                            osl = slice(hpair * 65, hpair * 65 + 65)
                            for j in range(nk):
                                kb = kb0 + j
                                fw.mm(pob[:, osl], PT[bi][:, j * 128:(j + 1) * 128], V[:, kb, h, :], start=(kb == 0), stop=(kb == NB - 1),
                                      reads=["b_PT%d" % bi, "b_V"], writes=[pokey])

            def final(qt):
                g, tt = qt // 4, qt % 4
                gp = g % 2
                tq = slice(tt * 128, (tt + 1) * 128)
                otok4 = otok[:].rearrange("p (a e) i -> p a e i", e=2)
                for hb_ in range(2):
                    pob = PS[7] if hb_ == 0 else PS[6]
                    pokey = "ps7" if hb_ == 0 else "ps6"
                    pv = pob[:, 0:260].rearrange("p (h i) -> p h i", i=65)
                    fw.v("reciprocal", rec[:, hb_ * 4:hb_ * 4 + 4], pv[:, :, 64], reads=[pokey], writes=["b_rec"])
                    fw.v("tensor_tensor", otok4[:, :, hb_, :], pv[:, :, 0:64],
                         rec[:, hb_ * 4:hb_ * 4 + 4].unsqueeze(2).to_broadcast([128, 4, 64]), ALU.mult,
                         reads=[pokey, "b_rec"], writes=["b_otok"])
                of = otok[:].rearrange("p h i -> p (h i)")
                for ct in range(4):
                    pb_ = PS[7 - ct // 2][:, 384:512].bitcast(BF16)
                    pkey = "ps%d" % (7 - ct // 2)
                    fw.tr(pb_[:, (ct % 2) * 128:(ct % 2 + 1) * 128], of[:, ct * 128:(ct + 1) * 128], self.ident_b[:],
                          reads=["b_otok", "k_ident"], writes=[pkey])
                for ct in range(4):
                    pb_ = PS[7 - ct // 2][:, 384:512].bitcast(BF16)
                    pkey = "ps%d" % (7 - ct // 2)
                    fw.v("tensor_tensor", yout[gp][:, ct, tq], pb_[:, (ct % 2) * 128:(ct % 2 + 1) * 128], SG[gp][ct][:, tq], ALU.mult,
                         reads=[pkey, "b_SG%d_%d" % (ct, gp)], writes=["b_yout0"])
                if tt == 3:
                    gc = slice(g * 512, g * 512 + 512)
                    fw.dma(self.yT[1][:, :, gc].rearrange("k p s -> p k s"), yout[gp][:], reads=["b_yout0"],
                           writes=[("yT1", g, ct) for ct in range(4)], eng="gpsimd")

            prep(0)
            scores(0)
            bisect_mask(0)
            for qt in range(NT):
                if qt + 1 < NT:
                    if (qt + 1) % 4 == 0:
                        prep((qt + 1) // 4)
                    scores(qt + 1)
                attention(qt)
                if qt + 1 < NT:
                    bisect_mask(qt + 1)
                final(qt)
            self.release(keys)

    def phase_C(self, l):
        fw, S, G = self.fw, self.S, self.G
        PS = self.PS
        with ExitStack() as ph:
            sb = lambda n, s, d: ph.enter_context(self.nc.sbuf_tensor(self.uname(n), list(s), d))
            wC = sb("wC", [128, 8, 1024], BF16)
            wr = sb("c_wr", [128, 4, 128], BF16)
            wi = sb("c_wi", [128, 4, 128], BF16)
            hn = [sb("c_hn%d" % i, [128, 8, 512], BF16) for i in range(2)]
            xbuf = sb("c_xbuf", [128, 4, 515], F32)
            hprev = sb("c_hprev", [128, 4], F32)
            cl = sb("c_cl", [128, 4], F32)
            xc = [sb("c_xc%d" % i, [128, 512], F32) for i in range(2)]
            xcb = [sb("c_xcb%d" % i, [128, 512], BF16) for i in range(2)]
            r_ = [sb("c_r%d" % i, [128, 512], F32) for i in range(2)]
            i_ = [sb("c_i%d" % i, [128, 512], F32) for i in range(2)]
            a_ = [sb("c_a%d" % i, [128, 512], F32) for i in range(2)]
            b_ = [sb("c_b%d" % i, [128, 512], F32) for i in range(2)]
            sg = [sb("c_sg%d" % i, [128, 512], F32) for i in range(2)]
            yo = [sb("c_y%d" % i, [128, 512], BF16) for i in range(2)]
            names = ["wC", "c_wr", "c_wi", "c_hn0", "c_hn1", "c_xbuf", "c_hprev", "c_cl"] + \
                    [n + str(i) for n in ("c_xc", "c_xcb", "c_r", "c_i", "c_a", "c_b", "c_sg", "c_y") for i in range(2)]
            keys = names + [("wC", k) for k in range(8)] + [("c_wr", k) for k in range(4)] + [("c_wi", k) for k in range(4)] + \
                ["c_xbuf%d" % i for i in range(4)] + ["c_hprev%d" % i for i in range(4)]
            self.acquire(keys + ["stg0", "stg1"])
            self.load_w(wC, "wC", lambda k: self.w_in[l, k * 128:(k + 1) * 128, OFF_C:OFF_C + 1024], 1024, 8, self.gcol)
            self.load_w(wr, "c_wr", lambda k: self.wr_bd[l, k], 128, 4)
            self.load_w(wi, "c_wi", lambda k: self.wi_bd[l, k], 128, 4)
            fw.act(cl[:], self.prm_sb[:, PCOLS["lam"][0]:PCOLS["lam"][0] + 4], AF.Exp, scale=-1.0, reads=["prm"], writes=["c_cl"])
            fw.act(cl[:], cl[:], AF.Ln, bias=1.0, reads=["c_cl"], writes=["c_cl"])
            fw.v("tensor_scalar", cl[:], cl[:], -8.0, None, ALU.mult, reads=["c_cl"], writes=["c_cl"])
            fw.v("memset", xbuf[:], 0.0, writes=["c_xbuf%d" % i for i in range(4)])
            fw.v("memset", hprev[:], 0.0, writes=["c_hprev%d" % i for i in range(4)])
            for g in range(G):
                c0 = g * 512
                hk = "c_hn%d" % (g % 2)
                hg = hn[g % 2]
                fw.dma(hg[:], self.hnT[:, :, c0:c0 + 512].rearrange("k p s -> p k s"), reads=[("hnT", g)], writes=[hk])
                def cbody(ct, g=g, c0=c0, hk=hk, hg=hg):
                    j = ct % 2
                    pb = 4 * j
                    px, pg, pr, pi = PS[pb], PS[pb + 1], PS[pb + 2], PS[pb + 3]
                    kx, kg, kr, ki = ["ps%d" % (pb + t) for t in range(4)]
                    for k in range(8):
                        fw.mm(px[:], wC[:, k, ct * 128:(ct + 1) * 128], hg[:, k, :], start=(k == 0), stop=(k == 7),
                              reads=[("wC", k), hk], writes=[kx])
                        yield
                    for k in range(8):
                        fw.mm(pg[:], wC[:, k, 512 + ct * 128:512 + (ct + 1) * 128], hg[:, k, :], start=(k == 0), stop=(k == 7),
                              reads=[("wC", k), hk], writes=[kg])
                        yield
                    xb = xbuf[:, ct, :]
                    fw.act(xb[:, 3:515], px[:], AF.Copy, reads=[kx], writes=["c_xbuf%d" % ct])
                    yield
                    cw = lambda i: self.pcol("conv_w", i * 4 + ct)
                    fw.v("tensor_scalar", xc[j][:], xb[:, 3:515], cw(3), self.pcol("conv_b", ct), ALU.mult, ALU.add,
                         reads=["c_xbuf%d" % ct, "prm"], writes=["c_xc%d" % j])
                    yield
                    for i in range(3):
                        fw.v("scalar_tensor_tensor", xc[j][:], xb[:, i:i + 512], cw(i), xc[j][:], ALU.mult, ALU.add,
                             reads=["c_xbuf%d" % ct, "prm", "c_xc%d" % j], writes=["c_xc%d" % j])
                        yield
                    fw.v("tensor_copy", xb[:, 0:3], xb[:, 512:515], reads=["c_xbuf%d" % ct], writes=["c_xbuf%d" % ct], eng="gpsimd")
                    yield
                    fw.v("tensor_copy", xcb[j][:], xc[j][:], reads=["c_xc%d" % j], writes=["c_xcb%d" % j], eng="gpsimd")
                    yield
                    fw.mm(pr[:], wr[:, ct, :], xcb[j][:], reads=[("c_wr", ct), "c_xcb%d" % j], writes=[kr])
                    yield
                    fw.mm(pi[:], wi[:, ct, :], xcb[j][:], reads=[("c_wi", ct), "c_xcb%d" % j], writes=[ki])
                    yield
                    fw.act(r_[j][:], pr[:], AF.Sigmoid, bias=self.pcol("b_r", ct), reads=[kr, "prm"], writes=["c_r%d" % j])
                    yield
                    fw.act(i_[j][:], pi[:], AF.Sigmoid, bias=self.pcol("b_i", ct), reads=[ki, "prm"], writes=["c_i%d" % j])
                    yield
                    fw.act(sg[j][:], pg[:], AF.Silu, reads=[kg], writes=["c_sg%d" % j])
                    yield
                    fw.act(a_[j][:], r_[j][:], AF.Exp, scale=cl[:, ct:ct + 1], reads=["c_r%d" % j, "c_cl"], writes=["c_a%d" % j])
                    yield
                    fw.v("tensor_tensor", b_[j][:], a_[j][:], a_[j][:], ALU.mult, reads=["c_a%d" % j], writes=["c_b%d" % j])
                    yield
                    fw.v("tensor_scalar", b_[j][:], b_[j][:], -1.0, 1.0, ALU.mult, ALU.add, reads=["c_b%d" % j], writes=["c_b%d" % j])
                    yield
                    fw.act(b_[j][:], b_[j][:], AF.Sqrt, reads=["c_b%d" % j], writes=["c_b%d" % j])
                    yield
                    fw.v("tensor_tensor", i_[j][:], i_[j][:], xc[j][:], ALU.mult, reads=["c_i%d" % j, "c_xc%d" % j],
                         writes=["c_i%d" % j], eng="gpsimd")
                    yield
                    fw.v("tensor_tensor", b_[j][:], b_[j][:], i_[j][:], ALU.mult, reads=["c_b%d" % j, "c_i%d" % j], writes=["c_b%d" % j])
                    yield
                    fw.v("tensor_tensor_scan", r_[j][:], a_[j][:], b_[j][:], hprev[:, ct:ct + 1], ALU.mult, ALU.add,
                         reads=["c_a%d" % j, "c_b%d" % j, "c_hprev%d" % ct, "c_r%d" % j], writes=["c_r%d" % j])
                    yield
                    fw.v("tensor_copy", hprev[:, ct:ct + 1], r_[j][:, 511:512], reads=["c_r%d" % j], writes=["c_hprev%d" % ct])
                    yield
                    fw.v("tensor_tensor", yo[j][:], r_[j][:], sg[j][:], ALU.mult, reads=["c_r%d" % j, "c_sg%d" % j],
                         writes=["c_y%d" % j], eng="gpsimd")
                    yield
                    fw.dma(self.yT[2][ct, :, c0:c0 + 512], yo[j][:], reads=["c_y%d" % j], writes=[("yT2", g, ct)], eng="gpsimd")
                    yield
                fw.lockstep([cbody(0), cbody(1)])
                fw.lockstep([cbody(2), cbody(3)])
            self.release(keys)

    def phase_M(self, l):
        fw, S, G, L = self.fw, self.S, self.G, self.L
        PS = self.PS
        last = (l == L - 1)
        with ExitStack() as ph:
            sb = lambda n, s, d: ph.enter_context(self.nc.sbuf_tensor(self.uname(n), list(s), d))
            wG = sb("wG", [128, 8, 3072], BF16)
            wbr = sb("wbr", [128, 12, 1024], BF16)
            wo = sb("wo", [128, 8, 1024], BF16)
            wpg = sb("wpg", [128, 8, 1024], BF16)
            wple = sb("wple", [128, 2, 1024], BF16)
            hn = sb("m_hn", [128, 8, 512], BF16)
            ys = [sb("m_y%d" % n, [128, 4, 512], BF16) for n in range(3)]
            hb = sb("m_h", [128, 8, 512], F32)
            h1b = sb("m_h1b", [128, 8, 512], BF16)
            pf = sb("m_pf", [128, 2, 512], F32)
            pb_ = sb("m_pb", [128, 2, 512], BF16)
            mrg = sb("m_mrg", [128, 8, 512], BF16)
            sgs = [sb("m_sg%d" % n, [128, 512], F32) for n in range(3)]
            tmp = sb("m_tmp", [128, 2, 512], F32)
            self.rs_sb = sb("m_rs", [128, 512], F32)
            self.hn_out = h1b
            self.hn_out_key = "m_h1b"
            self.eps_col = sb("m_eps", [128, 1], F32)
            names = ["wG", "wbr", "wo", "wpg", "wple", "m_hn", "m_y0", "m_y1", "m_y2", "m_h", "m_h1b", "m_pf", "m_pb",
                     "m_mrg", "m_sg0", "m_sg1", "m_sg2", ("m_tmp", 0), ("m_tmp", 1), "rs", "hn_out", "eps"]
            keys = names + [("wG", k) for k in range(8)] + [("wbr", k) for k in range(12)] + \
                [("wo", k) for k in range(8)] + [("wpg", k) for k in range(8)] + [("wple", k) for k in range(2)]
            self.acquire(keys + ["stg0", "stg1"])
            fw.v("memset", self.eps_col[:], NORM_EPS, writes=["eps"])
            self.load_w(wG, "wG", lambda k: self.w_in[l, k * 128:(k + 1) * 128, OFF_G:OFF_G + 3072], 3072, 8, self.gcol)
            self.load_w(wbr, "wbr", lambda k: self.w_branch[l, k // 4, (k % 4) * 128:(k % 4 + 1) * 128, :], 1024, 12)
            self.load_w(wo, "wo", lambda k: self.w_out[l, k * 128:(k + 1) * 128, :], 1024, 8)
            self.load_w(wpg, "wpg", lambda k: self.w_pg[l, k * 128:(k + 1) * 128, :], 1024, 8)
            self.load_w(wple, "wple", lambda k: self.w_ple[l, k * 128:(k + 1) * 128, :], 1024, 2)
            hsrc = self.xT if l == 0 else self.hT
            hdst = self.outT if last else self.hT
            for g in range(G):
                c0 = g * 512
                fw.dma(hn[:], self.hnT[:, :, c0:c0 + 512].rearrange("k p s -> p k s"), reads=[("hnT", g)], writes=["m_hn"])
                for n in range(3):
                    fw.dma(ys[n][:], self.yT[n][:, :, c0:c0 + 512].rearrange("k p s -> p k s"),
                           reads=[("yT%d" % n, g, ct) for ct in range(4)], writes=["m_y%d" % n])
                fw.dma(hb[:], hsrc[:, :, c0:c0 + 512].rearrange("k p s -> p k s"),
                       reads=([("hT", g)] if l > 0 else []), writes=["m_h"])
                fw.dma(pf[:], self.pT[l, :, :, c0:c0 + 512].rearrange("k p s -> p k s"), writes=["m_pf"])
                fw.v("tensor_copy", pb_[:], pf[:], reads=["m_pf"], writes=["m_pb"], eng="gpsimd")
                for dmt in range(8):
                    cs = slice(dmt * 128, (dmt + 1) * 128)
                    gb = 3 * (dmt % 2)
                    for n in range(3):
                        yb = 6 + (dmt * 3 + n) % 2
                        for k in range(8):
                            fw.mm(PS[gb + n][:], wG[:, k, n * 1024 + dmt * 128:n * 1024 + (dmt + 1) * 128], hn[:, k, :],
                                  start=(k == 0), stop=(k == 7), reads=[("wG", k), "m_hn"], writes=["ps%d" % (gb + n)])
                        for kc in range(4):
                            fw.mm(PS[yb][:], wbr[:, n * 4 + kc, cs], ys[n][:, kc, :], start=(kc == 0), stop=(kc == 3),
                                  reads=[("wbr", n * 4 + kc), "m_y%d" % n], writes=["ps%d" % yb])
                        fw.act(sgs[n][:], PS[gb + n][:], AF.Sigmoid, reads=["ps%d" % (gb + n)], writes=["m_sg%d" % n])
                        fw.v("tensor_tensor", sgs[n][:], PS[yb][:], sgs[n][:], ALU.mult,
                             reads=["ps%d" % yb, "m_sg%d" % n], writes=["m_sg%d" % n])
                    fw.v("tensor_tensor", sgs[0][:], sgs[0][:], sgs[1][:], ALU.add, reads=["m_sg0", "m_sg1"], writes=["m_sg0"], eng="gpsimd")
                    fw.v("tensor_tensor", mrg[:, dmt, :], sgs[0][:], sgs[2][:], ALU.add, reads=["m_sg0", "m_sg2"], writes=["m_mrg"], eng="gpsimd")
                for d2 in range(8):
                    pk = 6 + d2 % 2
                    for k in range(8):
                        fw.mm(PS[pk][:], wo[:, k, d2 * 128:(d2 + 1) * 128], mrg[:, k, :], start=(k == 0), stop=(k == 7),
                              reads=[("wo", k), "m_mrg"], writes=["ps%d" % pk])
                    fw.v("tensor_tensor", hb[:, d2, :], hb[:, d2, :], PS[pk][:], ALU.add, reads=["m_h", "ps%d" % pk], writes=["m_h"])
                fw.act(h1b[:], hb[:], AF.Copy, reads=["m_h"], writes=["m_h1b"])
                for d2 in range(8):
                    pa, pp = (0, 1) if d2 % 2 == 0 else (2, 3)
                    for k in range(8):
                        fw.mm(PS[pa][:], wpg[:, k, d2 * 128:(d2 + 1) * 128], h1b[:, k, :], start=(k == 0), stop=(k == 7),
                              reads=[("wpg", k), "m_h1b"], writes=["ps%d" % pa])
                    for k in range(2):
                        fw.mm(PS[pp][:], wple[:, k, d2 * 128:(d2 + 1) * 128], pb_[:, k, :], start=(k == 0), stop=(k == 1),
                              reads=[("wple", k), "m_pb"], writes=["ps%d" % pp])
                    sgk = d2 % 2
                    fw.act(sgs[sgk][:], PS[pa][:], AF.Sigmoid, reads=["ps%d" % pa], writes=["m_sg%d" % sgk])
                    fw.v("tensor_tensor", sgs[sgk][:], PS[pp][:], sgs[sgk][:], ALU.mult, reads=["ps%d" % pp, "m_sg%d" % sgk],
                         writes=["m_sg%d" % sgk])
                    fw.v("tensor_tensor", hb[:, d2, :], hb[:, d2, :], sgs[sgk][:], ALU.add, reads=["m_h", "m_sg%d" % sgk],
                         writes=["m_h"], eng="gpsimd")
                fw.dma(hdst[:, :, c0:c0 + 512].rearrange("k p s -> p k s"), hb[:], reads=["m_h"],
                       writes=[("outT" if last else "hT", g)], eng="gpsimd")
                if not last:
                    self.norm_group(hb, "m_h", g, tmp, "m_tmp")
            self.release(keys)


_CACHE = {}


def make_in_maps(inp, S, L, ncores):
    maps = []
    w_in = np.ascontiguousarray(np.asarray(inp["w_in"], np.float32)[:L])
    ki0 = OFF_B + 1792
    w_kidup = np.ascontiguousarray(np.concatenate([w_in[:, :, ki0:ki0 + 64], w_in[:, :, ki0:ki0 + 64]], axis=2))
    prm = np.stack([pack_params(inp, l) for l in range(L)])
    cst = np.zeros((128, 32), np.float32)
    invf = (np.float32(500000.0) ** (-(np.arange(0, 16, 2, dtype=np.float32) / np.float32(16)))).astype(np.float32)
    for p_ in range(128):
        if p_ % 64 < 16:
            cst[p_, 0] = invf[p_ % 8]
    cst[:, 1:25] = (2.0 ** (-np.arange(24, dtype=np.float64)))[None, :].astype(np.float32)
    ropeR = np.zeros((128, 128), np.float32)
    for m_ in range(128):
        if m_ % 64 < 8:
            ropeR[m_ + 8, m_] = -1.0
        elif m_ % 64 < 16:
            ropeR[m_ - 8, m_] = 1.0
    shared = {
        "cst": cst, "ropeR": ropeR,
        "prm": prm, "w_in": w_in, "w_kidup": w_kidup,
        "w2": np.ascontiguousarray(np.asarray(inp["rwkv_w2"], np.float32)[:L]),
        "a2": np.ascontiguousarray(np.asarray(inp["rwkv_a2"], np.float32)[:L]),
        "wr_bd": np.stack([blockdiag(inp["lru_w_r"][l]) for l in range(L)]),
        "wi_bd": np.stack([blockdiag(inp["lru_w_i"][l]) for l in range(L)]),
        "w_branch": np.ascontiguousarray(np.asarray(inp["w_branch"], np.float32)[:L]),
        "w_out": np.ascontiguousarray(np.asarray(inp["w_out"], np.float32)[:L]),
        "w_ple": np.ascontiguousarray(np.asarray(inp["w_ple"], np.float32)[:L]),
        "w_pg": np.ascontiguousarray(np.asarray(inp["w_ple_gate"], np.float32)[:L]),
    }
    x = np.asarray(inp["x"], np.float32)
    p = np.asarray(inp["p"], np.float32)
    pos = np.asarray(inp["positions"], np.int32)
    nb = x.shape[0]
    for c in range(ncores):
        b = (c // 2) % nb
        m = dict(shared)
        m["xT"] = np.ascontiguousarray(x[b].T.reshape(8, 128, S))
        m["pT"] = np.ascontiguousarray(np.stack([p[l, b].T.reshape(2, 128, S) for l in range(L)]))
        m["pos"] = np.ascontiguousarray(pos[b].reshape(1, S))
        maps.append(m)
    return maps


def kernel(**inputs):
    x = np.asarray(inputs["x"])
    B, S, _ = x.shape
    L = np.asarray(inputs["w_in"]).shape[0]
    key = (S, L)
    if key not in _CACHE:
        _CACHE[key] = Prog(S, L).build()
    nc = _CACHE[key]
    maps = make_in_maps(inputs, S, L, 8)
    res = run_bass_kernel_spmd(nc, maps, core_ids=list(range(8)))
    out = np.zeros((B, S, D), np.float32)
    for b in range(B):
        out[b] = res.results[2 * b]["outT"].reshape(D, S).T
    return out
```

```python
from contextlib import ExitStack
import numpy as np
import concourse.bass as bass
import concourse.mybir as mybir
from concourse.bass_utils import run_bass_kernel_spmd

F32 = mybir.dt.float32
BF16 = mybir.dt.bfloat16
I32 = mybir.dt.int32
AF = mybir.ActivationFunctionType
ALU = mybir.AluOpType
AX = mybir.AxisListType

ENGS = ("tensor", "vector", "scalar", "gpsimd", "sync")
N_DMA_SEMS = 24

D = 1024
DIN = 8644
OFF_A, OFF_B, OFF_C, OFF_G = 0, 2176, 4548, 5572
NORM_EPS = 1e-6
GN_EPS = 64e-5


class FW:
    def __init__(self, nc, stack, same_engine_sync=True):
        self.nc = nc
        self.stack = stack
        self.q = {e: [] for e in ENGS}
        self.cnt = {e: 0 for e in ENGS}
        self.sem = {e: stack.enter_context(nc.semaphore("s_" + e)) for e in ENGS}
        self.dsem = [stack.enter_context(nc.semaphore("d%d" % i)) for i in range(N_DMA_SEMS)]
        self.dcnt = [0] * N_DMA_SEMS
        self.dnext = 0
        self.seen = {e: {} for e in ENGS}
        self.lastw = {}
        self.readers = {}
        self.same = same_engine_sync
        self.ninst = 0
        self.rr = 0

    def sb(self, name, shape, dt):
        return self.stack.enter_context(self.nc.sbuf_tensor(name, list(shape), dt))

    def ps(self, name, shape, dt=F32):
        return self.stack.enter_context(self.nc.psum_tensor(name, list(shape), dt))

    def _deps(self, eng, reads, writes):
        ev = []
        for k in reads:
            if k in self.lastw:
                ev.append(self.lastw[k])
        for k in writes:
            if k in self.lastw:
                ev.append(self.lastw[k])
            ev.extend(self.readers.get(k, ()))
        best = {}
        for (sname, sem, val, src) in ev:
            if src == eng and (eng == "tensor" or not self.same):
                continue
            if self.seen[eng].get(sname, 0) >= val:
                continue
            if sname not in best or best[sname][1] < val:
                best[sname] = (sem, val)
        waits = []
        for sname, (sem, val) in best.items():
            self.seen[eng][sname] = val
            waits.append((sem, val))
        return waits

    def _commit(self, event, reads, writes):
        for k in writes:
            self.lastw[k] = event
            self.readers[k] = []
        for k in reads:
            if k in writes:
                continue
            self.readers.setdefault(k, []).append(event)

    def op(self, eng, fn, reads=(), writes=()):
        waits = self._deps(eng, reads, writes)
        self.cnt[eng] += 1
        idx = self.cnt[eng]
        sem = self.sem[eng]
        self.q[eng].append((waits, fn, sem, 1))
        self._commit(("s_" + eng, sem, idx, eng), reads, writes)
        self.ninst += 1

    def dma(self, out, in_, reads=(), writes=(), eng="sync", **kw):
        lo, n = (0, 16) if eng == "sync" else (16, N_DMA_SEMS - 16)
        self.dnext_q = getattr(self, "dnext_q", {})
        i = self.dnext_q.get(eng, 0)
        self.dnext_q[eng] = (i + 1) % n
        slot = lo + i
        sem = self.dsem[slot]
        sname = "d%d" % slot
        waits = self._deps(eng, reads, writes)
        prev = self.dcnt[slot] * 16
        if prev and self.seen[eng].get(sname, 0) < prev:
            waits.append((sem, prev))
            self.seen[eng][sname] = prev
        self.dcnt[slot] += 1
        val = self.dcnt[slot] * 16
        self.q[eng].append((waits, lambda e: e.dma_start(out=out, in_=in_, **kw), sem, 16))
        self._commit((sname, sem, val, "dma"), reads, writes)
        self.ninst += 1

    def finish(self, keys, eng="sync"):
        waits = self._deps(eng, keys, ())
        self.q[eng].append((waits, None, None, 0))

    def emit(self):
        nc = self.nc
        with nc.Block() as block:
            for ename in ENGS:
                items = self.q[ename]
                if not items:
                    continue

                def body(e, items=items):
                    for waits, fn, sem, inc in items:
                        for (ws, wv) in waits:
                            e.wait_ge(ws, wv)
                        if fn is not None:
                            fn(e).then_inc(sem, inc)

                getattr(block, ename)(body)

    def mm(self, out, lhsT, rhs, start=True, stop=True, reads=(), writes=()):
        self.op("tensor", lambda e: e.matmul(out, lhsT, rhs, start=start, stop=stop), reads, writes)

    def tr(self, out, in_, ident, reads=(), writes=()):
        self.op("tensor", lambda e: e.transpose(out, in_, ident), reads, writes)

    def act(self, out, in_, func, bias=0.0, scale=1.0, reads=(), writes=(), accum_out=None):
        if accum_out is None:
            self.op("scalar", lambda e: e.activation(out, in_, func, bias=bias, scale=scale), reads, writes)
        else:
            self.op("scalar", lambda e: e.activation(out, in_, func, bias=bias, scale=scale,
                                                     accum_out=accum_out), reads, writes)

    def v(self, name, *args, reads=(), writes=(), eng="vector", **kw):
        self.op(eng, lambda e: getattr(e, name)(*args, **kw), reads, writes)

    @staticmethod
    def lockstep(gens):
        gens = list(gens)
        while gens:
            for g_ in list(gens):
                try:
                    next(g_)
                except StopIteration:
                    gens.remove(g_)

    def cast_eng(self):
        self.rr += 1
        return ("vector", "gpsimd")[self.rr % 2]


PCOLS = {}
_o = 0
for _n, _w in [("norm_g", 8), ("mu_r", 4), ("mu_k", 4), ("mu_v", 4), ("mu_g", 4), ("mu_wl", 1), ("mu_al", 1),
               ("w0", 4), ("a0", 4), ("k_k", 4), ("k_a", 4), ("gn_g", 4), ("gn_b", 4), ("r_k", 4),
               ("q_g", 1), ("k_g", 1),
               ("conv_w", 16), ("conv_b", 4), ("b_r", 4), ("b_i", 4), ("lam", 4)]:
    PCOLS[_n] = (_o, _w)
    _o += _w
NPRM = _o


def _col4(v):
    return np.ascontiguousarray(np.asarray(v, np.float32).reshape(4, 128).T)


def pack_params(inp, l):
    prm = np.zeros((128, NPRM), np.float32)

    def put(name, arr):
        o, w = PCOLS[name]
        prm[:arr.shape[0], o:o + w] = arr

    put("norm_g", np.asarray(inp["norm_g"][l], np.float32).reshape(8, 128).T)
    mu = np.asarray(inp["rwkv_mu"][l], np.float32)
    put("mu_r", _col4(mu[0:512])); put("mu_k", _col4(mu[512:1024])); put("mu_v", _col4(mu[1024:1536]))
    put("mu_wl", mu[1536:1600].reshape(64, 1)); put("mu_al", mu[1600:1664].reshape(64, 1))
    put("mu_g", _col4(mu[1664:2176]))
    put("w0", _col4(inp["rwkv_w0"][l])); put("a0", _col4(inp["rwkv_a0"][l]))
    put("k_k", _col4(inp["rwkv_k_k"][l])); put("k_a", _col4(inp["rwkv_k_a"][l]))
    put("gn_g", _col4(inp["rwkv_gn_g"][l])); put("gn_b", _col4(inp["rwkv_gn_b"][l]))
    put("r_k", _col4(np.asarray(inp["rwkv_r_k"][l]).reshape(512)))
    put("q_g", np.tile(np.asarray(inp["dsa_q_g"][l], np.float32), 2).reshape(128, 1))
    put("k_g", np.tile(np.asarray(inp["dsa_k_g"][l], np.float32), 2).reshape(128, 1))
    cw = np.asarray(inp["lru_conv_w"][l], np.float32)
    put("conv_w", np.concatenate([_col4(cw[i]) for i in range(4)], axis=1))
    put("conv_b", _col4(inp["lru_conv_b"][l])); put("b_r", _col4(inp["lru_b_r"][l]))
    put("b_i", _col4(inp["lru_b_i"][l])); put("lam", _col4(inp["lru_lambda"][l]))
    return prm


def blockdiag(w):
    w = np.asarray(w, np.float32)
    out = np.zeros((4, 128, 128), np.float32)
    for ct in range(4):
        out[ct, 0:64, 0:64] = w[2 * ct]
        out[ct, 64:128, 64:128] = w[2 * ct + 1]
    return out


class Prog:
    def __init__(self, S, L, phases="NACBM", dbg=()):
        self.S, self.L, self.phases, self.dbg = S, L, phases, dbg
        self.G = S // 512
        nc = self.nc = bass.Bass("TRN2", target_bir_lowering=False)
        dt = nc.dram_tensor
        self.xT = dt("xT", [8, 128, S], F32, kind="ExternalInput").ap()
        self.pT = dt("pT", [L, 2, 128, S], F32, kind="ExternalInput").ap()
        self.pos = dt("pos", [1, S], I32, kind="ExternalInput").ap()
        self.prm = dt("prm", [L, 128, NPRM], F32, kind="ExternalInput").ap()
        self.w_in = dt("w_in", [L, D, DIN], F32, kind="ExternalInput").ap()
        self.w_kidup = dt("w_kidup", [L, D, 128], F32, kind="ExternalInput").ap()
        self.w2 = dt("w2", [L, 64, 512], F32, kind="ExternalInput").ap()
        self.a2 = dt("a2", [L, 64, 512], F32, kind="ExternalInput").ap()
        self.wr_bd = dt("wr_bd", [L, 4, 128, 128], F32, kind="ExternalInput").ap()
        self.wi_bd = dt("wi_bd", [L, 4, 128, 128], F32, kind="ExternalInput").ap()
        self.w_branch = dt("w_branch", [L, 3, 512, D], F32, kind="ExternalInput").ap()
        self.w_out = dt("w_out", [L, D, D], F32, kind="ExternalInput").ap()
        self.w_ple = dt("w_ple", [L, 256, D], F32, kind="ExternalInput").ap()
        self.w_pg = dt("w_pg", [L, D, D], F32, kind="ExternalInput").ap()
        self.cst_d = dt("cst", [128, 32], F32, kind="ExternalInput").ap()
        self.ropeR_d = dt("ropeR", [128, 128], F32, kind="ExternalInput").ap()
        self.ropeT = dt("ropeT", [2, 128, S], F32, kind="Internal").ap()
        self.outT = dt("outT", [8, 128, S], F32, kind="ExternalOutput").ap()
        okind = lambda n: "ExternalOutput" if n in dbg else "Internal"
        self.hT = dt("hT", [8, 128, S], F32, kind=okind("hT")).ap()
        self.hnT = dt("hnT", [8, 128, S], BF16, kind=okind("hnT")).ap()
        self.yT = [dt("yT%d" % n, [4, 128, S], BF16, kind=okind("yT%d" % n)).ap() for n in range(3)]

    def uname(self, n):
        self._uid = getattr(self, "_uid", 0) + 1
        return "%s_u%d" % (n, self._uid)

    def pcol(self, name, j=0, rows=128):
        o, w = PCOLS[name]
        return self.prm_sb[0:rows, o + j:o + j + 1]

    def load_w(self, dst, key, src_fn, ncols, kt, scale=None, rows=128):
        fw = self.fw
        for k in range(kt):
            for c0 in range(0, ncols, 512):
                cn = min(512, ncols - c0)
                si = self.stg_i
                self.stg_i ^= 1
                stg = self.stg[si]
                fw.dma(stg[0:rows, 0:cn], src_fn(k)[:, c0:c0 + cn], writes=["stg%d" % si])
                self.cast_rr = getattr(self, "cast_rr", 0) + 1
                eng = ("vector", "scalar", "gpsimd")[self.cast_rr % 3]
                o_ap, i_ap = dst[0:rows, k, c0:c0 + cn], stg[0:rows, 0:cn]
                rk_ = ["stg%d" % si] + (["prm"] if scale is not None else [])
                if eng == "scalar":
                    fw.act(o_ap, i_ap, AF.Copy, scale=(scale(k) if scale is not None else 1.0), reads=rk_, writes=[(key, k)])
                elif scale is not None:
                    fw.v("tensor_scalar", o_ap, i_ap, scale(k), 0.0, ALU.mult, ALU.add, reads=rk_, writes=[(key, k)], eng=eng)
                else:
                    fw.v("tensor_copy", o_ap, i_ap, reads=rk_, writes=[(key, k)], eng=eng)

    def gcol(self, k):
        return self.pcol("norm_g", k)

    def build(self):
        nc = self.nc
        with ExitStack() as st:
            fw = self.fw = FW(nc, st)
            self.st = st
            self.stg = [fw.sb("stg%d" % i, [128, 512], F32) for i in range(2)]
            self.stg_i = 0
            self.prm_sb = fw.sb("prm_sb", [128, NPRM], F32)
            self.ones_f = fw.sb("ones_f", [128, 128], F32)
            fw.v("memset", self.ones_f[:], 1.0, writes=["ones_f"])
            self.PSALL = fw.ps("psall", [128, 8, 512], F32)
            self.PS = [self.PSALL[:, i, :] for i in range(8)]
            self.tiny_col = fw.sb("tiny_col", [128, 1], F32)
            self.gneps_col = fw.sb("gneps_col", [128, 1], F32)
            fw.v("memset", self.tiny_col[:], 1e-30, writes=["tiny"])
            fw.v("memset", self.gneps_col[:], GN_EPS, writes=["tiny"])
            self.eps6_col = fw.sb("eps6_col", [128, 1], F32)
            fw.v("memset", self.eps6_col[:], NORM_EPS, writes=["tiny"])
            self.cst_sb = fw.sb("cst_sb", [128, 32], F32)
            fw.dma(self.cst_sb[:], self.cst_d, writes=["cst"])
            self.make_consts()
            if "B" in self.phases:
                self.phase_R()
            with ExitStack() as zs:
                for n, ph_ in enumerate("ABC"):
                    if ph_ not in self.phases:
                        zt = zs.enter_context(self.nc.sbuf_tensor(self.uname("zt"), [128, 4, 512], BF16))
                        self.acquire(["zt%d" % n])
                        fw.v("memset", zt[:], 0.0, writes=["zt%d" % n])
                        for g in range(self.G):
                            fw.dma(self.yT[n][:, :, g * 512:(g + 1) * 512].rearrange("k p s -> p k s"), zt[:], reads=["zt%d" % n],
                                   writes=[("yT%d" % n, g, ct) for ct in range(4)])
                        self.release(["zt%d" % n])
            for l in range(self.L):
                fw.dma(self.prm_sb[:], self.prm[l], writes=["prm"])
                if l == 0 and "N" in self.phases:
                    self.phase_N0()
                if "A" in self.phases:
                    self.phase_A(l)
                if "C" in self.phases:
                    self.phase_C(l)
                if "B" in self.phases:
                    self.phase_B(l)
                if "M" in self.phases:
                    self.phase_M(l)
            fw.finish([("outT", g) for g in range(self.G)])
            fw.emit()
        return nc

    def norm_group(self, hbuf, hkey, g, tmp, tmpkey):
        fw, S = self.fw, self.S
        c0 = g * 512
        ps = self.PS[7]
        for k in range(8):
            fw.act(tmp[:, k % 2, :], hbuf[:, k, :], AF.Square, reads=[hkey], writes=[(tmpkey, k % 2)])
            fw.mm(ps[:], self.ones_f[:], tmp[:, k % 2, :], start=(k == 0), stop=(k == 7),
                  reads=["ones_f", (tmpkey, k % 2)], writes=["ps7"])
        rs = self.rs_sb
        fw.act(rs[:], ps[:], AF.Sqrt, bias=self.eps_col[:, 0:1], scale=1.0 / D, reads=["ps7", "eps"], writes=["rs"])
        fw.v("reciprocal", rs[:], rs[:], reads=["rs"], writes=["rs"])
        hn = self.hn_out
        fw.v("tensor_tensor", hn[:], hbuf[:], rs[:].unsqueeze(1).to_broadcast([128, 8, 512]), ALU.mult,
             reads=[hkey, "rs"], writes=[self.hn_out_key])
        fw.dma(self.hnT[:, :, c0:c0 + 512].rearrange("k p s -> p k s"), hn[:], reads=[self.hn_out_key],
               writes=[("hnT", g)], eng="gpsimd")

    def phase_N0(self):
        fw = self.fw
        with ExitStack() as ph:
            sb = lambda n, s, d: ph.enter_context(self.nc.sbuf_tensor(self.uname(n), list(s), d))
            hb = [sb("n0_h%d" % i, [128, 8, 512], F32) for i in range(2)]
            tmp = sb("n0_tmp", [128, 2, 512], F32)
            self.rs_sb = sb("n0_rs", [128, 512], F32)
            self.hn_out = sb("n0_hn", [128, 8, 512], BF16)
            self.hn_out_key = "hn_out"
            self.eps_col = sb("n0_eps", [128, 1], F32)
            self.acquire(["n0_h0", "n0_h1", ("n0_tmp", 0), ("n0_tmp", 1), "rs", "hn_out", "eps"])
            fw.v("memset", self.eps_col[:], NORM_EPS, writes=["eps"])
            for g in range(self.G):
                c0 = g * 512
                h = hb[g % 2]
                fw.dma(h[:], self.xT[:, :, c0:c0 + 512].rearrange("k p s -> p k s"), writes=["n0_h%d" % (g % 2)])
                self.norm_group(h, "n0_h%d" % (g % 2), g, tmp, "n0_tmp")
            self.release(["n0_h0", "n0_h1", ("n0_tmp", 0), ("n0_tmp", 1), "rs", "hn_out", "eps"])

    def release(self, keys):
        fw = self.fw
        ev = []
        for k in keys:
            if k in fw.lastw:
                ev.append(fw.lastw[k])
            ev.extend(fw.readers.get(k, ()))
        best = {}
        for e in getattr(fw, "pending_release", []) + ev:
            if e[0] not in best or best[e[0]][2] < e[2]:
                best[e[0]] = e
        fw.pending_release = list(best.values())

    def acquire(self, keys):
        fw = self.fw
        ev = getattr(fw, "pending_release", [])
        for k in keys:
            fw.readers.setdefault(k, []).extend(ev)


    def make_consts(self):
        fw = self.fw
        onesb = fw.sb("k_onesb", [128, 256], BF16)
        self.ident_b = fw.sb("k_ident", [128, 128], BF16)
        self.mask_ui = fw.sb("k_mask_ui", [128, 256], BF16)
        self.mask_sl = fw.sb("k_mask_sl", [128, 128], BF16)
        self.blk1 = fw.sb("k_blk1", [128, 128], F32)
        g = "gpsimd"
        fw.v("memset", onesb[:], 1.0, writes=["k_onesb"], eng=g)
        sel = lambda out, pat, cm, op, key: fw.op(g, lambda e: e.affine_select(out, onesb[:, 0:128], pat, op, 0.0, base=0,
                                                                                channel_multiplier=cm),
                                                  reads=["k_onesb"], writes=[key])
        sel(self.ident_b[:], [[-1, 128]], 1, ALU.is_equal, "k_ident")
        sel(self.mask_ui[:, 0:128], [[1, 128]], -1, ALU.is_gt, "k_mask_ui")
        sel(self.mask_ui[:, 128:256], [[1, 128]], -1, ALU.is_ge, "k_mask_ui")
        sel(self.mask_sl[:], [[-1, 128]], 1, ALU.is_gt, "k_mask_sl")
        fw.v("memset", self.blk1[:], 0.0, writes=["k_blk1"], eng=g)
        fw.v("memset", self.blk1[0:64, 0:64], 1.0, writes=["k_blk1"], eng=g)
        fw.v("memset", self.blk1[64:128, 64:128], 1.0, writes=["k_blk1"], eng=g)

    def phase_A(self, l):
        fw, S, G = self.fw, self.S, self.G
        PS = self.PS
        CDEC = 0.6065306597126334
        with ExitStack() as ph:
            allkeys = []

            def sb(n, s, d):
                allkeys.append(n)
                return ph.enter_context(self.nc.sbuf_tensor(self.uname(n), list(s), d))

            wA = sb("wA", [128, 8, 2176], BF16)
            w2b = sb("a_w2b", [64, 1, 512], BF16)
            a2b = sb("a_a2b", [64, 1, 512], BF16)
            hn = [sb("a_hn0", [128, 8, 512], BF16)] * 2
            omu = sb("a_omu", [128, NPRM], F32)
            prevc = sb("a_prevc", [128, 18], F32)
            ubP = [[sb("a_ub%d_%d" % (p, q), [128, 513], F32) for q in range(4)] for p in range(2)]
            usP = [[sb("a_us%d_%d" % (p, q), [128, 512], F32) for q in range(4)] for p in range(2)]
            ulo = [sb("a_ulo%d" % q, [64, 513], F32) for q in range(2)]
            twl = sb("a_twl", [64, 512], BF16)
            alb = sb("a_alb", [64, 512], BF16)
            tP = [[sb("a_t%d_%d" % (p, i), [128, 512], F32) for i in range(8)] for p in range(2)]
            t_ = tP[0]
            art = [sb("a_art%d" % ct, [128, 4, 2, 128], BF16) for ct in range(4)]
            bk = [sb("a_bk%d" % ct, [128, 2, 512], BF16) for ct in range(4)]
            vb = [sb("a_vb%d" % ct, [128, 512], BF16) for ct in range(4)]
            tok = [sb("a_tok%d" % ct, [128, 4, 3, 128], BF16) for ct in range(4)]
            bonus = [sb("a_bonus%d" % ct, [128, 512], BF16) for ct in range(4)]
            sgt = [sb("a_sg%d" % ct, [128, 512], BF16) for ct in range(4)]
            PC = sb("a_PC", [128, 4, 4], F32)
            T = sb("a_T", [128, 4, 64], F32)
            Tb = sb("a_Tb", [128, 4, 64], BF16)
            LAb = [sb("a_LAb%d" % i, [128, 4, 256], BF16) for i in range(2)]
            KAb = [sb("a_KAb%d" % i, [128, 4, 256], BF16) for i in range(2)]
            Lb = [sb("a_Lb%d" % i, [128, 4, 128], BF16) for i in range(2)]
            PPb = [[sb("a_PPb%d_%d" % (i, j), [128, 4, 256], BF16) for j in range(2)] for i in range(2)]
            XT = [[sb("a_XT%d_%d" % (i, j), [128, 4, 128], BF16) for j in range(2)] for i in range(2)]
            Wb = [sb("a_Wb%d" % i, [128, 4, 64], BF16) for i in range(2)]
            Ub = [sb("a_Ub%d" % i, [128, 4, 64], BF16) for i in range(2)]
            xc = sb("a_xc", [128, 8, 64], F32)
            sq = sb("a_sq", [128, 8, 64], F32)
            st8 = sb("a_st8", [128, 4, 8], F32)
            onb = sb("a_onb", [128, 512], BF16)
            yv = [sb("a_yv%d" % i, [128, 128], F32) for i in range(2)]
            yout = sb("a_yout", [128, 4, 512], BF16)
            self.rstm = sb("k_rstm", [128, 512], F32)
            keys = allkeys + [("wA", k) for k in range(8)] + [("a_w2b", 0), ("a_a2b", 0)]
            self.acquire(keys + ["stg0", "stg1"])
            fw.v("memset", self.rstm[:], 1.0, writes=["k_rstm"], eng="gpsimd")
            for c in range(4):
                fw.v("memset", self.rstm[:, c * 128:c * 128 + 1], 0.0, writes=["k_rstm"], eng="gpsimd")

            self.load_w(wA, "wA", lambda k: self.w_in[l, k * 128:(k + 1) * 128, OFF_A:OFF_A + 2176], 2176, 8, self.gcol)
            self.load_w(w2b, "a_w2b", lambda k: self.w2[l], 512, 1, rows=64)
            self.load_w(a2b, "a_a2b", lambda k: self.a2[l], 512, 1, rows=64)
            fw.v("tensor_scalar", omu[:], self.prm_sb[:], -1.0, 1.0, ALU.mult, ALU.add, reads=["prm"], writes=["a_omu"])
            fw.v("memset", prevc[:], 0.0, writes=["a_prevc"])
            fw.v("memset", T[:], 0.0, writes=["a_T"])
            fw.v("memset", Tb[:], 0.0, writes=["a_Tb"])
            oc = lambda name, j=0, rows=128: omu[0:rows, PCOLS[name][0] + j:PCOLS[name][0] + j + 1]
            psb0 = PS[0][:].bitcast(BF16)
            psb1 = PS[1][:].bitcast(BF16)

            def shift(ps, pskey, ubt, ubkey, pcol, out, okey, mu_ap, omu_ap, rows=128):
                fw.v("tensor_copy", ubt[0:rows, 0:1], prevc[0:rows, pcol:pcol + 1], reads=["a_prevc"], writes=[ubkey], eng="gpsimd")
                fw.act(ubt[0:rows, 1:513], ps, AF.Copy, reads=[pskey], writes=[ubkey])
                fw.v("tensor_copy", prevc[0:rows, pcol:pcol + 1], ubt[0:rows, 512:513], reads=[ubkey], writes=["a_prevc"], eng="gpsimd")
                fw.v("tensor_scalar", out, ubt[0:rows, 0:512], mu_ap, None, ALU.mult, reads=[ubkey, "prm"], writes=[okey])
                fw.v("scalar_tensor_tensor", out, ubt[0:rows, 1:513], omu_ap, out, ALU.mult, ALU.add,
                     reads=[ubkey, "a_omu", okey], writes=[okey])

            for g in range(G):
                c0 = g * 512
                hk = "a_hn0"
                hg = hn[0]
                fw.dma(hg[:], self.hnT[:, :, c0:c0 + 512].rearrange("k p s -> p k s"), reads=[("hnT", g)], writes=[hk])
                for q, (coff, nm) in enumerate([(1536, "mu_wl"), (1600, "mu_al")]):
                    for k in range(8):
                        fw.mm(PS[q][0:64, :], wA[:, k, coff:coff + 64], hg[:, k, :], start=(k == 0), stop=(k == 7),
                              reads=[("wA", k), hk], writes=["ps%d" % q])
                    shift(PS[q][0:64, :], "ps%d" % q, ulo[q], "a_ulo%d" % q, 16 + q, t_[q][0:64, :], "a_t0_%d" % q,
                          self.pcol(nm, 0, 64), oc(nm, 0, 64), rows=64)
                fw.act(twl[:], t_[0][0:64, :], AF.Tanh, reads=["a_t0_0"], writes=["a_twl"])
                fw.v("tensor_copy", alb[:], t_[1][0:64, :], reads=["a_t0_1"], writes=["a_alb"])
                def abody(ct, g=g, hk=hk, hg=hg):
                    p_ = ct % 2
                    PSp = PS[4 * p_:4 * p_ + 4]
                    pk = lambda q: "ps%d" % (4 * p_ + q)
                    ub, us, t_ = ubP[p_], usP[p_], tP[p_]
                    psb0 = PSp[0][:].bitcast(BF16)
                    psb1 = PSp[1][:].bitcast(BF16)
                    cs = slice(ct * 128, (ct + 1) * 128)
                    for q, (coff, nm) in enumerate([(0, "mu_r"), (512, "mu_k"), (1024, "mu_v"), (1664, "mu_g")]):
                        for k in range(8):
                            fw.mm(PSp[q][:], wA[:, k, coff + ct * 128:coff + (ct + 1) * 128], hg[:, k, :], start=(k == 0), stop=(k == 7),
                                  reads=[("wA", k), hk], writes=[pk(q)])
                            yield
                        shift(PSp[q][:], pk(q), ub[q], "a_ub%d_%d" % (p_, q), ct * 4 + q, us[q][:], "a_us%d_%d" % (p_, q),
                              self.pcol(nm, ct), oc(nm, ct))
                        yield
                    r_s, k_s, v_s, g_s = us
                    K = lambda i: "a_t%d_%d" % (p_, i)
                    fw.mm(PSp[0][:], w2b[:, 0, cs], twl[:], reads=[("a_w2b", 0), "a_twl"], writes=[pk(0)])
                    yield
                    fw.act(t_[0][:], PSp[0][:], AF.Sigmoid, bias=self.pcol("w0", ct), reads=[pk(0), "prm"], writes=[K(0)])
                    yield
                    fw.v("tensor_scalar", t_[0][:], t_[0][:], -CDEC, 0.0, ALU.mult, ALU.add, reads=[K(0)], writes=[K(0)], eng="gpsimd")
                    yield
                    fw.mm(PSp[1][:], a2b[:, 0, cs], alb[:], reads=[("a_a2b", 0), "a_alb"], writes=[pk(1)])
                    yield
                    fw.act(t_[1][:], PSp[1][:], AF.Sigmoid, bias=self.pcol("a0", ct), reads=[pk(1), "prm"], writes=[K(1)])
                    yield
                    fw.v("tensor_scalar", t_[2][:], k_s[:], self.pcol("k_k", ct), None, ALU.mult, reads=["a_us%d_1" % p_, "prm"], writes=[K(2)])
                    yield
                    fw.v("tensor_tensor", t_[3][:], t_[2][:], t_[2][:], ALU.mult, reads=[K(2)], writes=[K(3)], eng="gpsimd")
                    yield
                    fw.mm(PSp[2][:], self.blk1[:], t_[3][:], reads=["k_blk1", K(3)], writes=[pk(2)])
                    yield
                    fw.act(t_[3][:], PSp[2][:], AF.Sqrt, bias=self.tiny_col[:, 0:1], reads=[pk(2), "tiny"], writes=[K(3)])
                    yield
                    fw.v("reciprocal", t_[3][:], t_[3][:], reads=[K(3)], writes=[K(3)])
                    yield
                    fw.v("tensor_tensor", t_[2][:], t_[2][:], t_[3][:], ALU.mult, reads=[K(2), K(3)], writes=[K(2)])
                    yield
                    fw.v("tensor_scalar", t_[3][:], t_[1][:], self.pcol("k_a", ct), oc("k_a", ct), ALU.mult, ALU.add,
                         reads=[K(1), "prm", "a_omu"], writes=[K(3)])
                    yield
                    fw.v("tensor_tensor", t_[3][:], t_[3][:], k_s[:], ALU.mult, reads=[K(3), "a_us%d_1" % p_], writes=[K(3)], eng="gpsimd")
                    yield
                    fw.v("tensor_tensor", t_[4][:], t_[2][:], t_[1][:], ALU.mult, reads=[K(2), K(1)], writes=[K(4)], eng="gpsimd")
                    yield
                    fw.v("tensor_tensor_scan", t_[5][:], self.rstm[:], t_[0][:], 0.0, ALU.mult, ALU.add,
                         reads=["k_rstm", K(0)], writes=[K(5)])
                    yield
                    fw.v("tensor_tensor", t_[6][:], t_[5][:], t_[0][:], ALU.subtract, reads=[K(5), K(0)], writes=[K(6)], eng="gpsimd")
                    yield
                    fw.act(t_[6][:], t_[6][:], AF.Exp, reads=[K(6)], writes=[K(6)])
                    yield
                    fw.act(t_[7][:], t_[5][:], AF.Exp, scale=-1.0, reads=[K(5)], writes=[K(7)])
                    yield
                    fw.act(t_[5][:], t_[5][:], AF.Exp, reads=[K(5)], writes=[K(5)])
                    yield
                    fw.v("tensor_copy", PC[:, ct, :], t_[5][:].rearrange("p (c t) -> p c t", t=128)[:, :, 127], reads=[K(5)],
                         writes=["a_PC"], eng="gpsimd")
                    yield
                    v3 = lambda ap: ap.rearrange("p (c t) -> p c t", t=128)
                    akey = "a_art%d" % ct
                    fw.v("scalar_tensor_tensor", art[ct][:, :, 0, :], v3(t_[2][:]), -1.0, v3(t_[6][:]), ALU.mult, ALU.mult,
                         reads=[K(2), K(6)], writes=[akey])
                    yield
                    fw.v("tensor_tensor", art[ct][:, :, 1, :], v3(r_s[:]), v3(t_[5][:]), ALU.mult, reads=["a_us%d_0" % p_, K(5)], writes=[akey])
                    yield
                    fw.v("tensor_tensor", bk[ct][:, 0, :], t_[4][:], t_[7][:], ALU.mult, reads=[K(4), K(7)], writes=["a_bk%d" % ct])
                    yield
                    fw.v("tensor_tensor", bk[ct][:, 1, :], t_[3][:], t_[7][:], ALU.mult, reads=[K(3), K(7)], writes=["a_bk%d" % ct], eng="gpsimd")
                    yield
                    fw.v("tensor_copy", vb[ct][:], v_s[:], reads=["a_us%d_2" % p_], writes=["a_vb%d" % ct], eng="gpsimd")
                    yield
                    fw.v("scalar_tensor_tensor", t_[4][:], r_s[:], self.pcol("r_k", ct), t_[3][:], ALU.mult, ALU.mult,
                         reads=["a_us%d_0" % p_, "prm", K(3), K(4)], writes=[K(4)])
                    yield
                    fw.mm(PSp[3][:], self.blk1[:], t_[4][:], reads=["k_blk1", K(4)], writes=[pk(3)])
                    yield
                    fw.v("tensor_tensor", bonus[ct][:], PSp[3][:], v_s[:], ALU.mult, reads=[pk(3), "a_us%d_2" % p_], writes=["a_bonus%d" % ct])
                    yield
                    fw.act(sgt[ct][:], g_s[:], AF.Silu, reads=["a_us%d_3" % p_], writes=["a_sg%d" % ct])
                    yield
                    for half in range(2):
                        psb, pkey = (psb0, pk(0)) if half == 0 else (psb1, pk(1))
                        for cc in range(2):
                            c = half * 2 + cc
                            for qi_, (src, skey) in enumerate([(bk[ct][:, 0, c * 128:(c + 1) * 128], "a_bk%d" % ct),
                                                               (bk[ct][:, 1, c * 128:(c + 1) * 128], "a_bk%d" % ct),
                                                               (vb[ct][:, c * 128:(c + 1) * 128], "a_vb%d" % ct)]):
                                o = (cc * 3 + qi_) * 128
                                fw.tr(psb[:, o:o + 128], src, self.ident_b[:], reads=[skey, "k_ident"], writes=[pkey])
                                yield
                        fw.act(tok[ct][:, half * 2:half * 2 + 2, :, :].rearrange("p a b c -> p (a b c)"), psb[:, 0:768], AF.Copy,
                               reads=[pkey], writes=["a_tok%d" % ct])
                        yield

                fw.lockstep([abody(0), abody(1)])
                fw.lockstep([abody(2), abody(3)])
                PSALL = self.PSALL
                idb4 = self.ident_b[:].unsqueeze(1).to_broadcast([128, 4, 128])
                mui2 = self.mask_ui[:].unsqueeze(1).to_broadcast([128, 2, 256])
                msl4 = self.mask_sl[:].unsqueeze(1).to_broadcast([128, 4, 128])
                for c in range(4):
                    ccols = slice(c * 128, (c + 1) * 128)

                    def hv(qd, hi):
                        h = 2 * hi + qd
                        ct, hp = hi, qd
                        pr_ = slice(hp * 64, hp * 64 + 64)
                        d = dict(h=h, ct=ct, hp=hp, pr=pr_, po=hp * 64,
                                 at=art[ct][pr_, c, 0, :], rt=art[ct][pr_, c, 1, :],
                                 ar=art[ct][pr_, c, :, :].rearrange("p a t -> p (a t)"),
                                 bt=bk[ct][pr_, 0, ccols], kt=bk[ct][pr_, 1, ccols],
                                 rk=["a_art%d" % ct, "a_bk%d" % ct], tkey="a_tok%d" % ct,
                                 vt=tok[ct][:, c, 2, hp * 64:hp * 64 + 64], btk=tok[ct][:, c, 0, hp * 64:hp * 64 + 64],
                                 ktk=tok[ct][:, c, 1, hp * 64:hp * 64 + 64], T0b=Tb[pr_, ct, :])
                        return d

                    XYk = lambda qd: ["ps%d" % (3 * qd), "ps%d" % (3 * qd + 1)]
                    Zk = lambda qd: ["ps%d" % (3 * qd + 2)]
                    XY = lambda qd: PSALL[:, 3 * qd:3 * qd + 2, :].rearrange("p b (h x) -> p (b h) x", x=256)
                    Zv = lambda qd: PSALL[:, 3 * qd + 2, :].rearrange("p (h x) -> p h x", x=128)
                    import os as _os
                    _stop = int(_os.environ.get("A_STOP", "99"))
                    if _stop <= 0:
                        continue
                    for qd in range(2):
                        for hi in range(4):
                            d = hv(qd, hi)
                            fw.mm(XY(qd)[:, hi, :], d["bt"], d["ar"], reads=d["rk"], writes=[XYk(qd)[hi // 2]])
                            fw.mm(Zv(qd)[:, hi, :], d["at"], d["bt"], reads=d["rk"], writes=Zk(qd))
                    for qd in range(2):
                        for b2 in range(2):
                            fw.v("tensor_tensor", LAb[qd][:, 2 * b2:2 * b2 + 2, :], XY(qd)[:, 2 * b2:2 * b2 + 2, :], mui2, ALU.mult,
                                 reads=[XYk(qd)[b2], "k_mask_ui"], writes=["a_LAb%d" % qd])
                        fw.v("tensor_tensor", Lb[qd][:], Zv(qd), msl4, ALU.mult, reads=Zk(qd) + ["k_mask_sl"], writes=["a_Lb%d" % qd])
                        fw.v("tensor_tensor", XT[qd][0][:], LAb[qd][:, :, 0:128], idb4, ALU.add,
                             reads=["a_LAb%d" % qd, "k_ident"], writes=["a_XT%d_0" % qd], eng="gpsimd")
                    if _stop <= 1:
                        continue
                    for qd in range(2):
                        for hi in range(4):
                            d = hv(qd, hi)
                            fw.mm(XY(qd)[:, hi, :], d["kt"], d["ar"], reads=d["rk"], writes=[XYk(qd)[hi // 2]])
                    for qd in range(2):
                        for b2 in range(2):
                            fw.v("tensor_tensor", KAb[qd][:, 2 * b2:2 * b2 + 2, :], XY(qd)[:, 2 * b2:2 * b2 + 2, :], mui2, ALU.mult,
                                 reads=[XYk(qd)[b2], "k_mask_ui"], writes=["a_KAb%d" % qd])
                    if _stop <= 2:
                        continue
                    for k in range(1, 8):
                        for qd in range(2):
                            if k == 1:
                                Pp, PTp, pkeys = (lambda hi: Lb[qd][:, hi, :]), (lambda hi: LAb[qd][:, hi, 0:128]), ["a_Lb%d" % qd, "a_LAb%d" % qd]
                            else:
                                pb_ = PPb[qd][(k - 1) % 2]
                                Pp, PTp, pkeys = (lambda hi, pb_=pb_: pb_[:, hi, 0:128]), (lambda hi, pb_=pb_: pb_[:, hi, 128:256]), ["a_PPb%d_%d" % (qd, (k - 1) % 2)]
                            for hi in range(4):
                                if k <= 6:
                                    fw.mm(XY(qd)[:, hi, 0:128], PTp(hi), Pp(hi), reads=pkeys, writes=[XYk(qd)[hi // 2]])
                                    fw.mm(XY(qd)[:, hi, 128:256], Pp(hi), PTp(hi), reads=pkeys, writes=[XYk(qd)[hi // 2]])
                                if k >= 2:
                                    xo = XT[qd][(k - 2) % 2]
                                    xok = "a_XT%d_%d" % (qd, (k - 2) % 2)
                                    fw.mm(Zv(qd)[:, hi, :], self.ident_b[:], xo[:, hi, :], start=True, stop=False, reads=["k_ident", xok], writes=Zk(qd))
                                    fw.mm(Zv(qd)[:, hi, :], Pp(hi), xo[:, hi, :], start=False, stop=True, reads=pkeys + [xok], writes=Zk(qd))
                        for qd in range(2):
                            if k <= 6:
                                for b2 in range(2):
                                    fw.act(PPb[qd][k % 2][:, 2 * b2:2 * b2 + 2, :], XY(qd)[:, 2 * b2:2 * b2 + 2, :], AF.Copy,
                                           reads=[XYk(qd)[b2]], writes=["a_PPb%d_%d" % (qd, k % 2)])
                            if k >= 2:
                                fw.v("tensor_copy", XT[qd][(k - 1) % 2][:], Zv(qd), reads=Zk(qd), writes=["a_XT%d_%d" % (qd, (k - 1) % 2)])
                    if _stop <= 3:
                        continue
                    XTf = [XT[qd][0] for qd in range(2)]
                    xfk = ["a_XT%d_0" % qd for qd in range(2)]
                    Wv = lambda qd: PSALL[:, 3 * qd + 2, 0:256].rearrange("p (h x) -> p h x", x=64)
                    Uv = lambda qd: PSALL[:, 3 * qd + 2, 256:512].rearrange("p (h x) -> p h x", x=64)
                    for qd in range(2):
                        for hi in range(4):
                            d = hv(qd, hi)
                            fw.mm(Wv(qd)[:, hi, :], d["at"], d["T0b"], start=True, stop=False, reads=["a_art%d" % d["ct"], "a_Tb"], writes=Zk(qd))
                            fw.mm(Wv(qd)[:, hi, :], KAb[qd][:, hi, 0:128], d["vt"], start=False, stop=True, reads=["a_KAb%d" % qd, d["tkey"]], writes=Zk(qd))
                        fw.v("tensor_copy", Wb[qd][:], Wv(qd), reads=Zk(qd), writes=["a_Wb%d" % qd])
                    for qd in range(2):
                        for hi in range(4):
                            fw.mm(Uv(qd)[:, hi, :], XTf[qd][:, hi, :], Wb[qd][:, hi, :], reads=[xfk[qd], "a_Wb%d" % qd], writes=Zk(qd))
                        fw.v("tensor_copy", Ub[qd][:], Uv(qd), reads=Zk(qd), writes=["a_Ub%d" % qd])
                    for qd in range(2):
                        for hi in range(4):
                            d = hv(qd, hi)
                            h, ct = d["h"], d["ct"]
                            ob, okey = (PS[6], "ps6") if qd == 0 else (PS[7], "ps7")
                            osl = slice(qd * 256 + hi * 64, qd * 256 + (hi + 1) * 64)
                            fw.mm(ob[:, osl], d["rt"], d["T0b"], start=True, stop=False, reads=["a_art%d" % ct, "a_Tb"], writes=[okey])
                            fw.mm(ob[:, osl], LAb[qd][:, hi, 128:256], Ub[qd][:, hi, :], start=False, stop=False,
                                  reads=["a_LAb%d" % qd, "a_Ub%d" % qd], writes=[okey])
                            fw.mm(ob[:, osl], KAb[qd][:, hi, 128:256], d["vt"], start=False, stop=True, reads=["a_KAb%d" % qd, d["tkey"]], writes=[okey])
                            zsl = slice(ct * 64, (ct + 1) * 64)
                            fw.mm(PS[7][d["pr"], zsl], d["btk"], Ub[qd][:, hi, :], start=True, stop=False, reads=[d["tkey"], "a_Ub%d" % qd], writes=["ps7"])
                            fw.mm(PS[7][d["pr"], zsl], d["ktk"], d["vt"], start=False, stop=True, reads=[d["tkey"]], writes=["ps7"])
                    if _stop <= 4:
                        continue
                    zall = PS[7][:, 0:256].rearrange("p (c i) -> p c i", i=64)
                    fw.v("tensor_tensor", T[:], T[:], zall, ALU.add, reads=["a_T", "ps7"], writes=["a_T"])
                    fw.v("tensor_tensor", T[:], T[:], PC[:, :, c:c + 1].to_broadcast([128, 4, 64]), ALU.mult, reads=["a_T", "a_PC"], writes=["a_T"])
                    fw.v("tensor_copy", Tb[:], T[:], reads=["a_T"], writes=["a_Tb"], eng="gpsimd")
                    ov = [PS[6][:, 0:256].rearrange("p (h i) -> p h i", i=64), PS[7][:, 256:512].rearrange("p (h i) -> p h i", i=64)]
                    okeys = ["ps6", "ps7"]
                    for qd in range(2):
                        fw.v("tensor_reduce", st8[:, 0, qd * 4:qd * 4 + 4], ov[qd], AX.X, ALU.add, reads=[okeys[qd]], writes=["a_st8"])
                    fw.v("tensor_scalar", st8[:, 0, :], st8[:, 0, :], 1.0 / 64, None, ALU.mult, reads=["a_st8"], writes=["a_st8"])
                    for qd in range(2):
                        fw.v("tensor_tensor", xc[:, qd * 4:qd * 4 + 4, :], ov[qd],
                             st8[:, 0, qd * 4:qd * 4 + 4].unsqueeze(2).to_broadcast([128, 4, 64]), ALU.subtract,
                             reads=[okeys[qd], "a_st8"], writes=["a_xc"])
                    fw.v("tensor_tensor", sq[:], xc[:], xc[:], ALU.mult, reads=["a_xc"], writes=["a_sq"], eng="gpsimd")
                    fw.v("tensor_reduce", st8[:, 1, :], sq[:], AX.X, ALU.add, reads=["a_sq"], writes=["a_st8"])
                    fw.act(st8[:, 1, :], st8[:, 1, :], AF.Sqrt, bias=self.gneps_col[:, 0:1], scale=1.0 / 64, reads=["a_st8", "tiny"], writes=["a_st8"])
                    fw.v("reciprocal", st8[:, 1, :], st8[:, 1, :], reads=["a_st8"], writes=["a_st8"])
                    fw.v("tensor_tensor", onb[:].rearrange("p (h i) -> p h i", i=64), xc[:],
                         st8[:, 1, :].unsqueeze(2).to_broadcast([128, 8, 64]), ALU.mult, reads=["a_xc", "a_st8"], writes=["a_onb"])
                    for ct in range(4):
                        for hp in range(2):
                            qo = (hp * 4 + ct) * 64
                            fw.tr(psb0[hp * 64:(hp + 1) * 64, ct * 128:(ct + 1) * 128], onb[:, qo:qo + 64], self.ident_b[:],
                                  reads=["a_onb", "k_ident"], writes=["ps0"])
                    for ct in range(4):
                        j = ct % 2
                        fw.v("tensor_scalar", yv[j][:], psb0[:, ct * 128:(ct + 1) * 128], self.pcol("gn_g", ct), self.pcol("gn_b", ct),
                             ALU.mult, ALU.add, reads=["ps0", "prm"], writes=["a_yv%d" % j])
                        fw.v("tensor_tensor", yv[j][:], yv[j][:], bonus[ct][:, ccols], ALU.add, reads=["a_yv%d" % j, "a_bonus%d" % ct],
                             writes=["a_yv%d" % j], eng="gpsimd")
                        fw.v("tensor_tensor", yout[:, ct, ccols], yv[j][:], sgt[ct][:, ccols], ALU.mult,
                             reads=["a_yv%d" % j, "a_sg%d" % ct], writes=["a_yout"], eng="gpsimd")
                fw.dma(self.yT[0][:, :, c0:c0 + 512].rearrange("k p s -> p k s"), yout[:], reads=["a_yout"],
                       writes=[("yT0", g, ct) for ct in range(4)], eng="gpsimd")
            self.release(keys)


    def phase_R(self):
        fw, S, G = self.fw, self.S, self.G
        TWO_PI = 6.283185307179586
        C1 = 6.28125
        C2 = 0.0019350051879882812
        C3 = TWO_PI - C1 - C2
        PI = 3.1415925
        with ExitStack() as ph:
            sb = lambda n, s, d: ph.enter_context(self.nc.sbuf_tensor(self.uname(n), list(s), d))
            posi = sb("r_posi", [128, 512], I32)
            a = sb("r_a", [128, 512], F32)
            k = sb("r_k", [128, 512], F32)
            r = sb("r_r", [128, 512], F32)
            r2 = sb("r_r2", [128, 512], F32)
            m = sb("r_m", [128, 512], F32)
            cs = sb("r_cs", [128, 2, 512], F32)
            keys = ["r_posi", "r_a", "r_k", "r_r", "r_r2", "r_m", "r_cs"]
            self.acquire(keys)
            for g in range(G):
                c0 = g * 512
                fw.dma(posi[:], self.pos[0:1, c0:c0 + 512].to_broadcast([128, 512]), writes=["r_posi"])
                fw.v("tensor_copy", a[:], posi[:], reads=["r_posi"], writes=["r_a"])
                fw.v("tensor_scalar", a[:], a[:], self.cst_sb[:, 0:1], None, ALU.mult, reads=["r_a", "cst"], writes=["r_a"])
                fw.v("tensor_scalar", k[:], a[:], 1.0 / TWO_PI, None, ALU.mult, reads=["r_a"], writes=["r_k"])
                fw.v("tensor_scalar", k[:], k[:], 12582912.0, None, ALU.add, reads=["r_k"], writes=["r_k"])
                fw.v("tensor_scalar", k[:], k[:], 12582912.0, None, ALU.subtract, reads=["r_k"], writes=["r_k"])
                fw.v("scalar_tensor_tensor", r[:], k[:], -C1, a[:], ALU.mult, ALU.add, reads=["r_k", "r_a"], writes=["r_r"])
                fw.v("scalar_tensor_tensor", r[:], k[:], -C2, r[:], ALU.mult, ALU.add, reads=["r_k", "r_r"], writes=["r_r"])
                fw.v("scalar_tensor_tensor", r[:], k[:], -C3, r[:], ALU.mult, ALU.add, reads=["r_k", "r_r"], writes=["r_r"])
                fw.v("tensor_scalar", r[:], r[:], PI, -PI, ALU.min, ALU.max, reads=["r_r"], writes=["r_r"])
                fw.v("tensor_scalar", r2[:], r[:], TWO_PI / 4, None, ALU.add, reads=["r_r"], writes=["r_r2"])
                fw.v("tensor_scalar", m[:], r2[:], PI, -TWO_PI, ALU.is_gt, ALU.mult, reads=["r_r2"], writes=["r_m"])
                fw.v("tensor_tensor", r2[:], r2[:], m[:], ALU.add, reads=["r_r2", "r_m"], writes=["r_r2"])
                fw.v("tensor_scalar", r2[:], r2[:], PI, -PI, ALU.min, ALU.max, reads=["r_r2"], writes=["r_r2"])
                fw.act(cs[:, 0, :], r2[:], AF.Sin, reads=["r_r2"], writes=["r_cs"])
                fw.act(cs[:, 1, :], r[:], AF.Sin, reads=["r_r"], writes=["r_cs"])
                fw.dma(self.ropeT[:, :, c0:c0 + 512].rearrange("k p s -> p k s"), cs[:], reads=["r_cs"], writes=[("ropeT", g)], eng="gpsimd")
            self.release(keys)

    def phase_B(self, l):
        fw, S, G = self.fw, self.S, self.G
        PS = self.PS
        NT = S // 128
        NIT = 20
        NOATT = False
        with ExitStack() as ph:
            allkeys = []

            def sb(n, s, d):
                allkeys.append(n)
                return ph.enter_context(self.nc.sbuf_tensor(self.uname(n), list(s), d))

            wB = sb("wB", [128, 8, 2372], BF16)
            wkd = sb("b_wkd", [128, 8, 128], BF16)
            ropeR = sb("b_ropeR", [128, 1, 128], BF16)
            KT = [sb("b_KT%d" % ct, [128, S], BF16) for ct in range(4)]
            KI = sb("b_KI", [128, S], BF16)
            V = sb("b_V", [128, NT, 8, 65], BF16)
            hn = sb("b_hn", [128, 8, 512], BF16)
            QT = [[sb("b_QT%d_%d" % (ct, i), [128, 512], BF16) for ct in range(4)] for i in range(2)]
            QI = [[sb("b_QI%d_%d" % (j, i), [128, 512], BF16) for j in range(2)] for i in range(2)]
            SG = [[sb("b_SG%d_%d" % (ct, i), [128, 512], BF16) for ct in range(4)] for i in range(2)]
            WI = [sb("b_WI%d" % i, [128, 4, 4], F32) for i in range(2)]
            yout = [sb("b_yout0", [128, 4, 512], BF16)] * 2
            score = sb("b_score", [128, S], F32)
            alias = S >= 4096
            if alias:
                xL = [score[:, 0:512], score[:, 1280:1792]]
                x2L = [score[:, 512:1024], score[:, 1792:2304]]
                xbL = [score[:, 1024:1280].bitcast(BF16), score[:, 2304:2560].bitcast(BF16)]
                cs = score[:, 2560:3584].rearrange("p (a b) -> p a b", b=512)
            else:
                cs = sb("b_cs", [128, 2, 512], F32)
                xL = [sb("b_x%d" % i, [128, 512], F32) for i in range(2)]
                x2L = [sb("b_x2%d" % i, [128, 512], F32) for i in range(2)]
                xbL = [sb("b_xb%d" % i, [128, 512], BF16) for i in range(2)]
            tkeys = ["b_cs"] + ["b_x%d" % i for i in range(2)] + ["b_x2%d" % i for i in range(2)] + ["b_xb%d" % i for i in range(2)]
            mm1 = [sb("b_mm1_0", [128, S], BF16)] * 2
            MT = [sb("b_MT%d" % i, [128, NT, 128], BF16) for i in range(2)]
            E = [sb("b_E%d" % i, [128, 512], BF16) for i in range(4)]
            rl = [sb("b_rl%d" % i, [128, 512], F32) for i in range(2)]
            PT = [sb("b_PT%d" % i, [128, 512], BF16) for i in range(4)]
            bs = sb("b_bs", [128, 8], F32)
            steps = sb("b_steps", [128, NIT + 1], F32)
            rec = sb("b_rec", [128, 8], F32)
            otok = sb("b_otok", [128, 8, 64], BF16)
            dmask = sb("b_dmask", [128, 128], F32)
            keys = allkeys + tkeys + [("wB", k) for k in range(8)] + [("b_wkd", k) for k in range(8)] + [("b_ropeR", 0)] + \
                [("b_KT", ct, g) for ct in range(4) for g in range(G)] + [("b_KI", g) for g in range(G)] + [("b_V", g) for g in range(G)]
            self.acquire(keys + ["stg0", "stg1"])
            self.load_w(wB, "wB", lambda k: self.w_in[l, k * 128:(k + 1) * 128, OFF_B:OFF_B + 2372], 2372, 8, self.gcol)
            self.load_w(wkd, "b_wkd", lambda k: self.w_kidup[l, k * 128:(k + 1) * 128, :], 128, 8, self.gcol)
            self.load_w(ropeR, "b_ropeR", lambda k: self.ropeR_d, 128, 1)
            fw.v("memset", V[:], 1.0, writes=[("b_V", g) for g in range(G)], eng="gpsimd")
            fw.v("memset", dmask[:], 0.0, writes=["b_dmask"], eng="gpsimd")
            fw.v("memset", dmask[0:64, 64:128], -1e30, writes=["b_dmask"], eng="gpsimd")

            def lane(p, chains):
                x_, x2, xb = xL[p], x2L[p], xbL[p]
                kx, kx2, kxb = "b_x%d" % p, "b_x2%d" % p, "b_xb%d" % p
                ps, pk = PS[p], "ps%d" % p

                def proj(w, wkey, c_lo, c_hi):
                    for k in range(8):
                        fw.mm(ps[:], w[:, k, c_lo:c_hi], hn[:, k, :], start=(k == 0), stop=(k == 7), reads=[(wkey, k), "b_hn"], writes=[pk])
                        yield

                def rope(dst, dkey):
                    fw.v("tensor_copy", xb[:], x_[:], reads=[kx], writes=[kxb], eng="gpsimd")
                    yield
                    fw.mm(ps[:], ropeR[:, 0, :], xb[:], reads=[("b_ropeR", 0), kxb], writes=[pk])
                    yield
                    fw.v("tensor_tensor", x2[:], x_[:], cs[:, 0, :], ALU.mult, reads=[kx, "b_cs"], writes=[kx2], eng="gpsimd")
                    yield
                    fw.v("tensor_tensor", x_[:], ps[:], cs[:, 1, :], ALU.mult, reads=[pk, "b_cs", kx], writes=[kx])
                    yield
                    fw.v("tensor_tensor", dst, x2[:], x_[:], ALU.add, reads=[kx2, kx], writes=[dkey], eng="gpsimd")
                    yield

                for ch in chains:
                    kind = ch[0]
                    if kind in ("q", "k"):
                        _, ct, g = ch
                        gp = g % 2
                        gc = slice(g * 512, g * 512 + 512)
                        coff, gname = (0, "q_g") if kind == "q" else (512, "k_g")
                        yield from proj(wB, "wB", coff + ct * 128, coff + (ct + 1) * 128)
                        fw.act(x_[:], ps[:], AF.Copy, reads=[pk], writes=[kx])
                        yield
                        fw.v("tensor_tensor", x2[:], x_[:], x_[:], ALU.mult, reads=[kx], writes=[kx2], eng="gpsimd")
                        yield
                        fw.mm(ps[:], self.blk1[:], x2[:], reads=["k_blk1", kx2], writes=[pk])
                        yield
                        fw.act(x2[:], ps[:], AF.Sqrt, bias=self.eps6_col[:, 0:1], scale=1.0 / 64, reads=[pk, "tiny"], writes=[kx2])
                        yield
                        fw.v("reciprocal", x2[:], x2[:], reads=[kx2], writes=[kx2])
                        yield
                        fw.v("scalar_tensor_tensor", x_[:], x_[:], self.pcol(gname, 0), x2[:], ALU.mult, ALU.mult,
                             reads=[kx, "prm", kx2], writes=[kx])
                        yield
                        if kind == "q":
                            yield from rope(QT[gp][ct][:], "b_QT%d_%d" % (ct, gp))
                        else:
                            yield from rope(KT[ct][:, gc], ("b_KT", ct, g))
                    elif kind == "qi":
                        _, j, g = ch
                        gp = g % 2
                        yield from proj(wB, "wB", 1536 + j * 128, 1536 + (j + 1) * 128)
                        fw.act(x_[:], ps[:], AF.Copy, reads=[pk], writes=[kx])
                        yield
                        yield from rope(QI[gp][j][:], "b_QI%d_%d" % (j, gp))
                    elif kind == "ki":
                        _, g = ch
                        gc = slice(g * 512, g * 512 + 512)
                        yield from proj(wkd, "b_wkd", 0, 128)
                        fw.act(x_[:], ps[:], AF.Copy, reads=[pk], writes=[kx])
                        yield
                        yield from rope(KI[:, gc], ("b_KI", g))
                    elif kind == "sg":
                        _, ct, g = ch
                        gp = g % 2
                        yield from proj(wB, "wB", 1860 + ct * 128, 1860 + (ct + 1) * 128)
                        fw.act(SG[gp][ct][:], ps[:], AF.Silu, reads=[pk], writes=["b_SG%d_%d" % (ct, gp)])
                        yield
                    elif kind == "v":
                        _, tt, g = ch
                        gp = g % 2
                        tcols = slice(tt * 128, (tt + 1) * 128)
                        for k in range(8):
                            fw.mm(ps[:], hn[:, k, tcols], wB[:, k, 1024:1536], start=(k == 0), stop=(k == 7),
                                  reads=[("wB", k), "b_hn"], writes=[pk])
                            yield
                        fw.act(V[:, g * 4 + tt, :, 0:64], ps[:].rearrange("p (h i) -> p h i", i=64), AF.Copy, reads=[pk], writes=[("b_V", g)])
                        yield
                        for k in range(8):
                            fw.mm(ps[:, 0:4], hn[:, k, tcols], wB[:, k, 1856:1860], start=(k == 0), stop=(k == 7),
                                  reads=[("wB", k), "b_hn"], writes=[pk])
                            yield
                        fw.v("tensor_scalar", WI[gp][:, tt, :], ps[:, 0:4], 1.0 / 16, None, ALU.mult, reads=[pk], writes=["b_WI%d" % gp])
                        yield

            def prep_begin(g):
                gc = slice(g * 512, g * 512 + 512)
                if alias:
                    self.release(["b_score"])
                    self.acquire(tkeys)
                fw.dma(hn[:], self.hnT[:, :, gc].rearrange("k p s -> p k s"), reads=[("hnT", g)], writes=["b_hn"])
                fw.dma(cs[:], self.ropeT[:, :, gc].rearrange("k p s -> p k s"), reads=[("ropeT", g)], writes=["b_cs"])

            def prep_lanes(g):
                chains = []
                for ct in range(4):
                    chains += [("q", ct, g), ("k", ct, g)]
                chains += [("qi", 0, g), ("qi", 1, g), ("ki", g)]
                chains += [("sg", ct, g) for ct in range(4)]
                chains += [("v", tt, g) for tt in range(4)]
                return [lane(0, chains[0::2]), lane(1, chains[1::2])]

            def prep_end(g):
                if alias:
                    self.release(tkeys)
                    self.acquire(["b_score"])

            def scores(qt):
                g, tt = qt // 4, qt % 4
                gp = g % 2
                N = (qt + 1) * 128
                tq = slice(tt * 128, (tt + 1) * 128)
                for pc in range((N + 511) // 512):
                    p0 = pc * 512
                    pn = min(512, N - p0)
                    for ih in range(4):
                        po = (ih % 2) * 64
                        fw.mm(PS[ih][:, 0:pn], QI[gp][ih // 2][po:po + 64, tq], KI[po:po + 64, p0:p0 + pn],
                              reads=["b_QI%d_%d" % (ih // 2, gp), ("b_KI", pc)], writes=["ps%d" % ih])
                    for ih in range(4):
                        r_ = rl[ih % 2]
                        rkey = "b_rl%d" % (ih % 2)
                        fw.act(r_[:, 0:pn], PS[ih][:, 0:pn], AF.Relu, reads=["ps%d" % ih], writes=[rkey])
                        if ih == 0:
                            fw.v("tensor_scalar", score[:, p0:p0 + pn], r_[:, 0:pn], WI[gp][:, tt, 0:1], None, ALU.mult,
                                 reads=[rkey, "b_WI%d" % gp], writes=["b_score"])
                        else:
                            fw.v("scalar_tensor_tensor", score[:, p0:p0 + pn], r_[:, 0:pn], WI[gp][:, tt, ih:ih + 1], score[:, p0:p0 + pn],
                                 ALU.mult, ALU.add, reads=[rkey, "b_WI%d" % gp, "b_score"], writes=["b_score"])

            def bisect_mask(qt):
                NB = qt + 1
                N = NB * 128
                mk = mm1[0]
                mkey = "b_mm1_0"
                A, lo, mid, cnt, tmp = (bs[:, i:i + 1] for i in range(5))
                if NB >= 3:
                    fw.v("tensor_reduce", A, score[:, 0:N], AX.X, ALU.max, apply_absolute_value=True, reads=["b_score"], writes=["b_bs"])
                    fw.v("tensor_scalar", A, A, 1.0001, 1e-20, ALU.mult, ALU.add, reads=["b_bs"], writes=["b_bs"])
                fw.v("tensor_tensor", score[:, N - 128:N], score[:, N - 128:N], dmask[:], ALU.add, reads=["b_score", "b_dmask"],
                     writes=["b_score"])
                if NB >= 3:
                    fw.v("tensor_scalar", steps[:], self.cst_sb[:, 1:2 + NIT], A, None, ALU.mult, reads=["cst", "b_bs"], writes=["b_steps"])
                    fw.v("tensor_scalar", mid, A, -1.0, steps[:, 0:1], ALU.mult, ALU.add, reads=["b_bs", "b_steps"], writes=["b_bs"])
                    for it in range(NIT):
                        fw.v("tensor_scalar", mk[:, 0:N], score[:, 0:N], mid, None, ALU.is_ge, ALU.add, accum_out=cnt,
                             reads=["b_score", "b_bs", mkey], writes=[mkey, "b_bs"])
                        fw.v("tensor_scalar", tmp, cnt, 255.5, steps[:, it:it + 1], ALU.is_ge, ALU.mult, reads=["b_bs", "b_steps"], writes=["b_bs"])
                        fw.v("scalar_tensor_tensor", mid, tmp, steps[:, it + 1:it + 2], mid, ALU.subtract, ALU.add,
                             reads=["b_bs", "b_steps"], writes=["b_bs"])
                    fw.v("tensor_tensor", lo, mid, steps[:, NIT:NIT + 1], ALU.subtract, reads=["b_bs", "b_steps"], writes=["b_bs"])
                else:
                    fw.v("memset", lo, -1e29, writes=["b_bs"])
                fw.v("tensor_scalar", mk[:, 0:N], score[:, 0:N], lo, None, ALU.is_ge, reads=["b_score", "b_bs"], writes=[mkey])
                psb1 = PS[1][:].bitcast(BF16)
                mt, mtkey = MT[qt % 2], "b_MT%d" % (qt % 2)
                for kb0 in range(0, NB, 8):
                    nk = min(8, NB - kb0)
                    for j in range(nk):
                        kb = kb0 + j
                        fw.tr(psb1[:, j * 128:(j + 1) * 128], mk[:, kb * 128:(kb + 1) * 128], self.ident_b[:],
                              reads=[mkey, "k_ident"], writes=["ps1"])
                    fw.act(mt[:, kb0:kb0 + nk, :].rearrange("p a b -> p (a b)"), psb1[:, 0:nk * 128], AF.Copy, reads=["ps1"], writes=[mtkey])

            def attention_gen(qt):
                if NOATT:
                    return
                g, tt = qt // 4, qt % 4
                gp = g % 2
                NB = qt + 1
                tq = slice(tt * 128, (tt + 1) * 128)
                mt, mtkey = MT[qt % 2], "b_MT%d" % (qt % 2)
                for hpair in range(4):
                    ct = hpair
                    for gi, kb0 in enumerate(range(0, NB, 4)):
                        nk = min(4, NB - kb0)
                        bis = [2 * e + gi % 2 for e in range(2)]
                        for j in range(nk):
                            kb = kb0 + j
                            for e in range(2):
                                po = e * 64
                                pl = PS[2 + bis[e]]
                                fw.mm(pl[:, j * 128:(j + 1) * 128], KT[ct][po:po + 64, kb * 128:(kb + 1) * 128], QT[gp][ct][po:po + 64, tq],
                                      reads=[("b_KT", ct, kb // 4), "b_QT%d_%d" % (ct, gp)], writes=["ps%d" % (2 + bis[e])])
                                yield
                        for e in range(2):
                            bi = bis[e]
                            fw.act(E[bi][:, 0:nk * 128], PS[2 + bi][:, 0:nk * 128], AF.Exp, scale=0.125, reads=["ps%d" % (2 + bi)], writes=["b_E%d" % bi])
                            yield
                            fw.v("tensor_tensor", PT[bi][:, 0:nk * 128], E[bi][:, 0:nk * 128],
                                 mt[:, kb0:kb0 + nk, :].rearrange("p a b -> p (a b)"), ALU.mult,
                                 reads=["b_E%d" % bi, mtkey], writes=["b_PT%d" % bi], eng="gpsimd")
                            yield
                        for e in range(2):
                            h = 2 * hpair + e
                            bi = bis[e]
                            pob = PS[7] if e == 0 else PS[6]
                            pokey = "ps7" if e == 0 else "ps6"
                            osl = slice(hpair * 65, hpair * 65 + 65)
                            for j in range(nk):
                                kb = kb0 + j
                                fw.mm(pob[:, osl], PT[bi][:, j * 128:(j + 1) * 128], V[:, kb, h, :], start=(kb == 0), stop=(kb == NB - 1),
                                      reads=["b_PT%d" % bi, ("b_V", kb // 4)], writes=[pokey])
                                yield

            def final(qt):
                g, tt = qt // 4, qt % 4
                gp = g % 2
                tq = slice(tt * 128, (tt + 1) * 128)
                otok4 = otok[:].rearrange("p (a e) i -> p a e i", e=2)
                for hb_ in range(2):
                    pob = PS[7] if hb_ == 0 else PS[6]
                    pokey = "ps7" if hb_ == 0 else "ps6"
                    pv = pob[:, 0:260].rearrange("p (h i) -> p h i", i=65)
                    fw.v("reciprocal", rec[:, hb_ * 4:hb_ * 4 + 4], pv[:, :, 64], reads=[pokey], writes=["b_rec"])
                    fw.v("tensor_tensor", otok4[:, :, hb_, :], pv[:, :, 0:64],
                         rec[:, hb_ * 4:hb_ * 4 + 4].unsqueeze(2).to_broadcast([128, 4, 64]), ALU.mult,
                         reads=[pokey, "b_rec"], writes=["b_otok"])
                of = otok[:].rearrange("p h i -> p (h i)")
                for ct in range(4):
                    pb_ = PS[7 - ct // 2][:, 384:512].bitcast(BF16)
                    pkey = "ps%d" % (7 - ct // 2)
                    fw.tr(pb_[:, (ct % 2) * 128:(ct % 2 + 1) * 128], of[:, ct * 128:(ct + 1) * 128], self.ident_b[:],
                          reads=["b_otok", "k_ident"], writes=[pkey])
                for ct in range(4):
                    pb_ = PS[7 - ct // 2][:, 384:512].bitcast(BF16)
                    pkey = "ps%d" % (7 - ct // 2)
                    fw.v("tensor_tensor", yout[gp][:, ct, tq], pb_[:, (ct % 2) * 128:(ct % 2 + 1) * 128], SG[gp][ct][:, tq], ALU.mult,
                         reads=[pkey, "b_SG%d_%d" % (ct, gp)], writes=["b_yout0"])
                if tt == 3:
                    gc = slice(g * 512, g * 512 + 512)
                    fw.dma(self.yT[1][:, :, gc].rearrange("k p s -> p k s"), yout[gp][:], reads=["b_yout0"],
                           writes=[("yT1", g, ct) for ct in range(4)], eng="gpsimd")

            prep_begin(0)
            fw.lockstep(prep_lanes(0))
            prep_end(0)
            scores(0)
            bisect_mask(0)
            for qt in range(NT):
                nxt = qt + 1
                if nxt < NT:
                    if nxt % 4 == 0:
                        gn = nxt // 4
                        prep_begin(gn)
                        fw.lockstep(prep_lanes(gn))
                        prep_end(gn)
                    scores(nxt)
                fw.lockstep([attention_gen(qt)])
                if nxt < NT:
                    bisect_mask(nxt)
                final(qt)
            self.release(keys)

    def phase_C(self, l):
        fw, S, G = self.fw, self.S, self.G
        PS = self.PS
        with ExitStack() as ph:
            sb = lambda n, s, d: ph.enter_context(self.nc.sbuf_tensor(self.uname(n), list(s), d))
            wC = sb("wC", [128, 8, 1024], BF16)
            wr = sb("c_wr", [128, 4, 128], BF16)
            wi = sb("c_wi", [128, 4, 128], BF16)
            hn = [sb("c_hn%d" % i, [128, 8, 512], BF16) for i in range(2)]
            xbuf = sb("c_xbuf", [128, 4, 515], F32)
            hprev = sb("c_hprev", [128, 4], F32)
            cl = sb("c_cl", [128, 4], F32)
            xc = [sb("c_xc%d" % i, [128, 512], F32) for i in range(2)]
            xcb = [sb("c_xcb%d" % i, [128, 512], BF16) for i in range(2)]
            r_ = [sb("c_r%d" % i, [128, 512], F32) for i in range(2)]
            i_ = [sb("c_i%d" % i, [128, 512], F32) for i in range(2)]
            a_ = [sb("c_a%d" % i, [128, 512], F32) for i in range(2)]
            b_ = [sb("c_b%d" % i, [128, 512], F32) for i in range(2)]
            sg = [sb("c_sg%d" % i, [128, 512], F32) for i in range(2)]
            yo = [sb("c_y%d" % i, [128, 512], BF16) for i in range(2)]
            names = ["wC", "c_wr", "c_wi", "c_hn0", "c_hn1", "c_xbuf", "c_hprev", "c_cl"] + \
                    [n + str(i) for n in ("c_xc", "c_xcb", "c_r", "c_i", "c_a", "c_b", "c_sg", "c_y") for i in range(2)]
            keys = names + [("wC", k) for k in range(8)] + [("c_wr", k) for k in range(4)] + [("c_wi", k) for k in range(4)] + \
                ["c_xbuf%d" % i for i in range(4)] + ["c_hprev%d" % i for i in range(4)]
            self.acquire(keys + ["stg0", "stg1"])
            self.load_w(wC, "wC", lambda k: self.w_in[l, k * 128:(k + 1) * 128, OFF_C:OFF_C + 1024], 1024, 8, self.gcol)
            self.load_w(wr, "c_wr", lambda k: self.wr_bd[l, k], 128, 4)
            self.load_w(wi, "c_wi", lambda k: self.wi_bd[l, k], 128, 4)
            fw.act(cl[:], self.prm_sb[:, PCOLS["lam"][0]:PCOLS["lam"][0] + 4], AF.Exp, scale=-1.0, reads=["prm"], writes=["c_cl"])
            fw.act(cl[:], cl[:], AF.Ln, bias=1.0, reads=["c_cl"], writes=["c_cl"])
            fw.v("tensor_scalar", cl[:], cl[:], -8.0, None, ALU.mult, reads=["c_cl"], writes=["c_cl"])
            fw.v("memset", xbuf[:], 0.0, writes=["c_xbuf%d" % i for i in range(4)])
            fw.v("memset", hprev[:], 0.0, writes=["c_hprev%d" % i for i in range(4)])
            for g in range(G):
                c0 = g * 512
                hk = "c_hn%d" % (g % 2)
                hg = hn[g % 2]
                fw.dma(hg[:], self.hnT[:, :, c0:c0 + 512].rearrange("k p s -> p k s"), reads=[("hnT", g)], writes=[hk])
                def cbody(ct, g=g, c0=c0, hk=hk, hg=hg):
                    j = ct % 2
                    pb = 4 * j
                    px, pg, pr, pi = PS[pb], PS[pb + 1], PS[pb + 2], PS[pb + 3]
                    kx, kg, kr, ki = ["ps%d" % (pb + t) for t in range(4)]
                    for k in range(8):
                        fw.mm(px[:], wC[:, k, ct * 128:(ct + 1) * 128], hg[:, k, :], start=(k == 0), stop=(k == 7),
                              reads=[("wC", k), hk], writes=[kx])
                        yield
                    for k in range(8):
                        fw.mm(pg[:], wC[:, k, 512 + ct * 128:512 + (ct + 1) * 128], hg[:, k, :], start=(k == 0), stop=(k == 7),
                              reads=[("wC", k), hk], writes=[kg])
                        yield
                    xb = xbuf[:, ct, :]
                    fw.act(xb[:, 3:515], px[:], AF.Copy, reads=[kx], writes=["c_xbuf%d" % ct])
                    yield
                    cw = lambda i: self.pcol("conv_w", i * 4 + ct)
                    fw.v("tensor_scalar", xc[j][:], xb[:, 3:515], cw(3), self.pcol("conv_b", ct), ALU.mult, ALU.add,
                         reads=["c_xbuf%d" % ct, "prm"], writes=["c_xc%d" % j])
                    yield
                    for i in range(3):
                        fw.v("scalar_tensor_tensor", xc[j][:], xb[:, i:i + 512], cw(i), xc[j][:], ALU.mult, ALU.add,
                             reads=["c_xbuf%d" % ct, "prm", "c_xc%d" % j], writes=["c_xc%d" % j])
                        yield
                    fw.v("tensor_copy", xb[:, 0:3], xb[:, 512:515], reads=["c_xbuf%d" % ct], writes=["c_xbuf%d" % ct], eng="gpsimd")
                    yield
                    fw.v("tensor_copy", xcb[j][:], xc[j][:], reads=["c_xc%d" % j], writes=["c_xcb%d" % j], eng="gpsimd")
                    yield
                    fw.mm(pr[:], wr[:, ct, :], xcb[j][:], reads=[("c_wr", ct), "c_xcb%d" % j], writes=[kr])
                    yield
                    fw.mm(pi[:], wi[:, ct, :], xcb[j][:], reads=[("c_wi", ct), "c_xcb%d" % j], writes=[ki])
                    yield
                    fw.act(r_[j][:], pr[:], AF.Sigmoid, bias=self.pcol("b_r", ct), reads=[kr, "prm"], writes=["c_r%d" % j])
                    yield
                    fw.act(i_[j][:], pi[:], AF.Sigmoid, bias=self.pcol("b_i", ct), reads=[ki, "prm"], writes=["c_i%d" % j])
                    yield
                    fw.act(sg[j][:], pg[:], AF.Silu, reads=[kg], writes=["c_sg%d" % j])
                    yield
                    fw.act(a_[j][:], r_[j][:], AF.Exp, scale=cl[:, ct:ct + 1], reads=["c_r%d" % j, "c_cl"], writes=["c_a%d" % j])
                    yield
                    fw.v("tensor_tensor", b_[j][:], a_[j][:], a_[j][:], ALU.mult, reads=["c_a%d" % j], writes=["c_b%d" % j])
                    yield
                    fw.v("tensor_scalar", b_[j][:], b_[j][:], -1.0, 1.0, ALU.mult, ALU.add, reads=["c_b%d" % j], writes=["c_b%d" % j])
                    yield
                    fw.act(b_[j][:], b_[j][:], AF.Sqrt, reads=["c_b%d" % j], writes=["c_b%d" % j])
                    yield
                    fw.v("tensor_tensor", i_[j][:], i_[j][:], xc[j][:], ALU.mult, reads=["c_i%d" % j, "c_xc%d" % j],
                         writes=["c_i%d" % j], eng="gpsimd")
                    yield
                    fw.v("tensor_tensor", b_[j][:], b_[j][:], i_[j][:], ALU.mult, reads=["c_b%d" % j, "c_i%d" % j], writes=["c_b%d" % j])
                    yield
                    fw.v("tensor_tensor_scan", r_[j][:], a_[j][:], b_[j][:], hprev[:, ct:ct + 1], ALU.mult, ALU.add,
                         reads=["c_a%d" % j, "c_b%d" % j, "c_hprev%d" % ct, "c_r%d" % j], writes=["c_r%d" % j])
                    yield
                    fw.v("tensor_copy", hprev[:, ct:ct + 1], r_[j][:, 511:512], reads=["c_r%d" % j], writes=["c_hprev%d" % ct])
                    yield
                    fw.v("tensor_tensor", yo[j][:], r_[j][:], sg[j][:], ALU.mult, reads=["c_r%d" % j, "c_sg%d" % j],
                         writes=["c_y%d" % j], eng="gpsimd")
                    yield
                    fw.dma(self.yT[2][ct, :, c0:c0 + 512], yo[j][:], reads=["c_y%d" % j], writes=[("yT2", g, ct)], eng="gpsimd")
                    yield
                fw.lockstep([cbody(0), cbody(1)])
                fw.lockstep([cbody(2), cbody(3)])
            self.release(keys)

    def phase_M(self, l):
        fw, S, G, L = self.fw, self.S, self.G, self.L
        PS = self.PS
        last = (l == L - 1)
        with ExitStack() as ph:
            sb = lambda n, s, d: ph.enter_context(self.nc.sbuf_tensor(self.uname(n), list(s), d))
            wG = sb("wG", [128, 8, 3072], BF16)
            wbr = sb("wbr", [128, 12, 1024], BF16)
            wo = sb("wo", [128, 8, 1024], BF16)
            wpg = sb("wpg", [128, 8, 1024], BF16)
            wple = sb("wple", [128, 2, 1024], BF16)
            hn = sb("m_hn", [128, 8, 512], BF16)
            ys = [sb("m_y%d" % n, [128, 4, 512], BF16) for n in range(3)]
            hb = sb("m_h", [128, 8, 512], F32)
            h1b = sb("m_h1b", [128, 8, 512], BF16)
            pf = sb("m_pf", [128, 2, 512], F32)
            pb_ = sb("m_pb", [128, 2, 512], BF16)
            mrg = sb("m_mrg", [128, 8, 512], BF16)
            sgs = [sb("m_sg%d" % n, [128, 512], F32) for n in range(3)]
            tmp = sb("m_tmp", [128, 2, 512], F32)
            self.rs_sb = sb("m_rs", [128, 512], F32)
            self.hn_out = h1b
            self.hn_out_key = "m_h1b"
            self.eps_col = sb("m_eps", [128, 1], F32)
            names = ["wG", "wbr", "wo", "wpg", "wple", "m_hn", "m_y0", "m_y1", "m_y2", "m_h", "m_h1b", "m_pf", "m_pb",
                     "m_mrg", "m_sg0", "m_sg1", "m_sg2", ("m_tmp", 0), ("m_tmp", 1), "rs", "hn_out", "eps"]
            keys = names + [("wG", k) for k in range(8)] + [("wbr", k) for k in range(12)] + \
                [("wo", k) for k in range(8)] + [("wpg", k) for k in range(8)] + [("wple", k) for k in range(2)]
            self.acquire(keys + ["stg0", "stg1"])
            fw.v("memset", self.eps_col[:], NORM_EPS, writes=["eps"])
            self.load_w(wG, "wG", lambda k: self.w_in[l, k * 128:(k + 1) * 128, OFF_G:OFF_G + 3072], 3072, 8, self.gcol)
            self.load_w(wbr, "wbr", lambda k: self.w_branch[l, k // 4, (k % 4) * 128:(k % 4 + 1) * 128, :], 1024, 12)
            self.load_w(wo, "wo", lambda k: self.w_out[l, k * 128:(k + 1) * 128, :], 1024, 8)
            self.load_w(wpg, "wpg", lambda k: self.w_pg[l, k * 128:(k + 1) * 128, :], 1024, 8)
            self.load_w(wple, "wple", lambda k: self.w_ple[l, k * 128:(k + 1) * 128, :], 1024, 2)
            hsrc = self.xT if l == 0 else self.hT
            hdst = self.outT if last else self.hT
            for g in range(G):
                c0 = g * 512
                fw.dma(hn[:], self.hnT[:, :, c0:c0 + 512].rearrange("k p s -> p k s"), reads=[("hnT", g)], writes=["m_hn"])
                for n in range(3):
                    fw.dma(ys[n][:], self.yT[n][:, :, c0:c0 + 512].rearrange("k p s -> p k s"),
                           reads=[("yT%d" % n, g, ct) for ct in range(4)], writes=["m_y%d" % n])
                fw.dma(hb[:], hsrc[:, :, c0:c0 + 512].rearrange("k p s -> p k s"),
                       reads=([("hT", g)] if l > 0 else []), writes=["m_h"])
                fw.dma(pf[:], self.pT[l, :, :, c0:c0 + 512].rearrange("k p s -> p k s"), writes=["m_pf"])
                fw.v("tensor_copy", pb_[:], pf[:], reads=["m_pf"], writes=["m_pb"], eng="gpsimd")
                for dmt in range(8):
                    cs = slice(dmt * 128, (dmt + 1) * 128)
                    gb = 3 * (dmt % 2)
                    for n in range(3):
                        yb = 6 + (dmt * 3 + n) % 2
                        for k in range(8):
                            fw.mm(PS[gb + n][:], wG[:, k, n * 1024 + dmt * 128:n * 1024 + (dmt + 1) * 128], hn[:, k, :],
                                  start=(k == 0), stop=(k == 7), reads=[("wG", k), "m_hn"], writes=["ps%d" % (gb + n)])
                        for kc in range(4):
                            fw.mm(PS[yb][:], wbr[:, n * 4 + kc, cs], ys[n][:, kc, :], start=(kc == 0), stop=(kc == 3),
                                  reads=[("wbr", n * 4 + kc), "m_y%d" % n], writes=["ps%d" % yb])
                        fw.act(sgs[n][:], PS[gb + n][:], AF.Sigmoid, reads=["ps%d" % (gb + n)], writes=["m_sg%d" % n])
                        fw.v("tensor_tensor", sgs[n][:], PS[yb][:], sgs[n][:], ALU.mult,
                             reads=["ps%d" % yb, "m_sg%d" % n], writes=["m_sg%d" % n])
                    fw.v("tensor_tensor", sgs[0][:], sgs[0][:], sgs[1][:], ALU.add, reads=["m_sg0", "m_sg1"], writes=["m_sg0"], eng="gpsimd")
                    fw.v("tensor_tensor", mrg[:, dmt, :], sgs[0][:], sgs[2][:], ALU.add, reads=["m_sg0", "m_sg2"], writes=["m_mrg"], eng="gpsimd")
                for d2 in range(8):
                    pk = 6 + d2 % 2
                    for k in range(8):
                        fw.mm(PS[pk][:], wo[:, k, d2 * 128:(d2 + 1) * 128], mrg[:, k, :], start=(k == 0), stop=(k == 7),
                              reads=[("wo", k), "m_mrg"], writes=["ps%d" % pk])
                    fw.v("tensor_tensor", hb[:, d2, :], hb[:, d2, :], PS[pk][:], ALU.add, reads=["m_h", "ps%d" % pk], writes=["m_h"])
                fw.act(h1b[:], hb[:], AF.Copy, reads=["m_h"], writes=["m_h1b"])
                for d2 in range(8):
                    pa, pp = (0, 1) if d2 % 2 == 0 else (2, 3)
                    for k in range(8):
                        fw.mm(PS[pa][:], wpg[:, k, d2 * 128:(d2 + 1) * 128], h1b[:, k, :], start=(k == 0), stop=(k == 7),
                              reads=[("wpg", k), "m_h1b"], writes=["ps%d" % pa])
                    for k in range(2):
                        fw.mm(PS[pp][:], wple[:, k, d2 * 128:(d2 + 1) * 128], pb_[:, k, :], start=(k == 0), stop=(k == 1),
                              reads=[("wple", k), "m_pb"], writes=["ps%d" % pp])
                    sgk = d2 % 2
                    fw.act(sgs[sgk][:], PS[pa][:], AF.Sigmoid, reads=["ps%d" % pa], writes=["m_sg%d" % sgk])
                    fw.v("tensor_tensor", sgs[sgk][:], PS[pp][:], sgs[sgk][:], ALU.mult, reads=["ps%d" % pp, "m_sg%d" % sgk],
                         writes=["m_sg%d" % sgk])
                    fw.v("tensor_tensor", hb[:, d2, :], hb[:, d2, :], sgs[sgk][:], ALU.add, reads=["m_h", "m_sg%d" % sgk],
                         writes=["m_h"], eng="gpsimd")
                fw.dma(hdst[:, :, c0:c0 + 512].rearrange("k p s -> p k s"), hb[:], reads=["m_h"],
                       writes=[("outT" if last else "hT", g)], eng="gpsimd")
                if not last:
                    self.norm_group(hb, "m_h", g, tmp, "m_tmp")
            self.release(keys)


_CACHE = {}


def make_in_maps(inp, S, L, ncores):
    maps = []
    w_in = np.ascontiguousarray(np.asarray(inp["w_in"], np.float32)[:L])
    ki0 = OFF_B + 1792
    w_kidup = np.ascontiguousarray(np.concatenate([w_in[:, :, ki0:ki0 + 64], w_in[:, :, ki0:ki0 + 64]], axis=2))
    prm = np.stack([pack_params(inp, l) for l in range(L)])
    cst = np.zeros((128, 32), np.float32)
    invf = (np.float32(500000.0) ** (-(np.arange(0, 16, 2, dtype=np.float32) / np.float32(16)))).astype(np.float32)
    for p_ in range(128):
        if p_ % 64 < 16:
            cst[p_, 0] = invf[p_ % 8]
    cst[:, 1:25] = (2.0 ** (-np.arange(24, dtype=np.float64)))[None, :].astype(np.float32)
    ropeR = np.zeros((128, 128), np.float32)
    for m_ in range(128):
        if m_ % 64 < 8:
            ropeR[m_ + 8, m_] = -1.0
        elif m_ % 64 < 16:
            ropeR[m_ - 8, m_] = 1.0
    shared = {
        "cst": cst, "ropeR": ropeR,
        "prm": prm, "w_in": w_in, "w_kidup": w_kidup,
        "w2": np.ascontiguousarray(np.asarray(inp["rwkv_w2"], np.float32)[:L]),
        "a2": np.ascontiguousarray(np.asarray(inp["rwkv_a2"], np.float32)[:L]),
        "wr_bd": np.stack([blockdiag(inp["lru_w_r"][l]) for l in range(L)]),
        "wi_bd": np.stack([blockdiag(inp["lru_w_i"][l]) for l in range(L)]),
        "w_branch": np.ascontiguousarray(np.asarray(inp["w_branch"], np.float32)[:L]),
        "w_out": np.ascontiguousarray(np.asarray(inp["w_out"], np.float32)[:L]),
        "w_ple": np.ascontiguousarray(np.asarray(inp["w_ple"], np.float32)[:L]),
        "w_pg": np.ascontiguousarray(np.asarray(inp["w_ple_gate"], np.float32)[:L]),
    }
    x = np.asarray(inp["x"], np.float32)
    p = np.asarray(inp["p"], np.float32)
    pos = np.asarray(inp["positions"], np.int32)
    nb = x.shape[0]
    for c in range(ncores):
        b = (c // 2) % nb
        m = dict(shared)
        m["xT"] = np.ascontiguousarray(x[b].T.reshape(8, 128, S))
        m["pT"] = np.ascontiguousarray(np.stack([p[l, b].T.reshape(2, 128, S) for l in range(L)]))
        m["pos"] = np.ascontiguousarray(pos[b].reshape(1, S))
        maps.append(m)
    return maps


def kernel(**inputs):
    x = np.asarray(inputs["x"])
    B, S, _ = x.shape
    L = np.asarray(inputs["w_in"]).shape[0]
    key = (S, L)
    if key not in _CACHE:
        _CACHE[key] = Prog(S, L).build()
    nc = _CACHE[key]
    maps = make_in_maps(inputs, S, L, 8)
    res = run_bass_kernel_spmd(nc, maps, core_ids=list(range(8)))
    out = np.zeros((B, S, D), np.float32)
    for b in range(B):
        out[b] = res.results[2 * b]["outT"].reshape(D, S).T
    return out
```

```python
from contextlib import ExitStack
import numpy as np
import concourse.bass as bass
import concourse.mybir as mybir
from concourse.bass_utils import run_bass_kernel_spmd

F32 = mybir.dt.float32
BF16 = mybir.dt.bfloat16
I32 = mybir.dt.int32
AF = mybir.ActivationFunctionType
ALU = mybir.AluOpType
AX = mybir.AxisListType

ENGS = ("tensor", "vector", "scalar", "gpsimd", "sync")
N_DMA_SEMS = 24

D = 1024
DIN = 8644
OFF_A, OFF_B, OFF_C, OFF_G = 0, 2176, 4548, 5572
NORM_EPS = 1e-6
GN_EPS = 64e-5


class FW:
    def __init__(self, nc, stack, same_engine_sync=True):
        self.nc = nc
        self.stack = stack
        self.q = {e: [] for e in ENGS}
        self.cnt = {e: 0 for e in ENGS}
        self.sem = {e: stack.enter_context(nc.semaphore("s_" + e)) for e in ENGS}
        self.dsem = [stack.enter_context(nc.semaphore("d%d" % i)) for i in range(N_DMA_SEMS)]
        self.dcnt = [0] * N_DMA_SEMS
        self.dnext = 0
        self.seen = {e: {} for e in ENGS}
        self.lastw = {}
        self.readers = {}
        self.same = same_engine_sync
        self.ninst = 0
        self.rr = 0

    def sb(self, name, shape, dt):
        return self.stack.enter_context(self.nc.sbuf_tensor(name, list(shape), dt))

    def ps(self, name, shape, dt=F32):
        return self.stack.enter_context(self.nc.psum_tensor(name, list(shape), dt))

    def _deps(self, eng, reads, writes):
        ev = []
        for k in reads:
            if k in self.lastw:
                ev.append(self.lastw[k])
        for k in writes:
            if k in self.lastw:
                ev.append(self.lastw[k])
            ev.extend(self.readers.get(k, ()))
        best = {}
        for (sname, sem, val, src) in ev:
            if src == eng and (eng == "tensor" or not self.same):
                continue
            if self.seen[eng].get(sname, 0) >= val:
                continue
            if sname not in best or best[sname][1] < val:
                best[sname] = (sem, val)
        waits = []
        for sname, (sem, val) in best.items():
            self.seen[eng][sname] = val
            waits.append((sem, val))
        return waits

    def _commit(self, event, reads, writes):
        for k in writes:
            self.lastw[k] = event
            self.readers[k] = []
        for k in reads:
            if k in writes:
                continue
            self.readers.setdefault(k, []).append(event)

    def op(self, eng, fn, reads=(), writes=()):
        waits = self._deps(eng, reads, writes)
        self.cnt[eng] += 1
        idx = self.cnt[eng]
        sem = self.sem[eng]
        self.q[eng].append((waits, fn, sem, 1))
        self._commit(("s_" + eng, sem, idx, eng), reads, writes)
        self.ninst += 1

    def dma(self, out, in_, reads=(), writes=(), eng="sync", **kw):
        lo, n = (0, 16) if eng == "sync" else (16, N_DMA_SEMS - 16)
        self.dnext_q = getattr(self, "dnext_q", {})
        i = self.dnext_q.get(eng, 0)
        self.dnext_q[eng] = (i + 1) % n
        slot = lo + i
        sem = self.dsem[slot]
        sname = "d%d" % slot
        waits = self._deps(eng, reads, writes)
        prev = self.dcnt[slot] * 16
        if prev and self.seen[eng].get(sname, 0) < prev:
            waits.append((sem, prev))
            self.seen[eng][sname] = prev
        self.dcnt[slot] += 1
        val = self.dcnt[slot] * 16
        self.q[eng].append((waits, lambda e: e.dma_start(out=out, in_=in_, **kw), sem, 16))
        self._commit((sname, sem, val, "dma"), reads, writes)
        self.ninst += 1

    def finish(self, keys, eng="sync"):
        waits = self._deps(eng, keys, ())
        self.q[eng].append((waits, None, None, 0))

    def emit(self):
        nc = self.nc
        with nc.Block() as block:
            for ename in ENGS:
                items = self.q[ename]
                if not items:
                    continue

                def body(e, items=items):
                    for waits, fn, sem, inc in items:
                        for (ws, wv) in waits:
                            e.wait_ge(ws, wv)
                        if fn is not None:
                            fn(e).then_inc(sem, inc)

                getattr(block, ename)(body)

    def mm(self, out, lhsT, rhs, start=True, stop=True, reads=(), writes=()):
        self.op("tensor", lambda e: e.matmul(out, lhsT, rhs, start=start, stop=stop), reads, writes)

    def tr(self, out, in_, ident, reads=(), writes=()):
        self.op("tensor", lambda e: e.transpose(out, in_, ident), reads, writes)

    def act(self, out, in_, func, bias=0.0, scale=1.0, reads=(), writes=(), accum_out=None):
        if accum_out is None:
            self.op("scalar", lambda e: e.activation(out, in_, func, bias=bias, scale=scale), reads, writes)
        else:
            self.op("scalar", lambda e: e.activation(out, in_, func, bias=bias, scale=scale,
                                                     accum_out=accum_out), reads, writes)

    def v(self, name, *args, reads=(), writes=(), eng="vector", **kw):
        self.op(eng, lambda e: getattr(e, name)(*args, **kw), reads, writes)

    @staticmethod
    def lockstep(gens):
        gens = list(gens)
        while gens:
            for g_ in list(gens):
                try:
                    next(g_)
                except StopIteration:
                    gens.remove(g_)

    def cast_eng(self):
        self.rr += 1
        return ("vector", "gpsimd")[self.rr % 2]


PCOLS = {}
_o = 0
for _n, _w in [("norm_g", 8), ("mu_r", 4), ("mu_k", 4), ("mu_v", 4), ("mu_g", 4), ("mu_wl", 1), ("mu_al", 1),
               ("w0", 4), ("a0", 4), ("k_k", 4), ("k_a", 4), ("gn_g", 4), ("gn_b", 4), ("r_k", 4),
               ("q_g", 1), ("k_g", 1),
               ("conv_w", 16), ("conv_b", 4), ("b_r", 4), ("b_i", 4), ("lam", 4)]:
    PCOLS[_n] = (_o, _w)
    _o += _w
NPRM = _o


def _col4(v):
    return np.ascontiguousarray(np.asarray(v, np.float32).reshape(4, 128).T)


def pack_params(inp, l):
    prm = np.zeros((128, NPRM), np.float32)

    def put(name, arr):
        o, w = PCOLS[name]
        prm[:arr.shape[0], o:o + w] = arr

    put("norm_g", np.asarray(inp["norm_g"][l], np.float32).reshape(8, 128).T)
    mu = np.asarray(inp["rwkv_mu"][l], np.float32)
    put("mu_r", _col4(mu[0:512])); put("mu_k", _col4(mu[512:1024])); put("mu_v", _col4(mu[1024:1536]))
    put("mu_wl", mu[1536:1600].reshape(64, 1)); put("mu_al", mu[1600:1664].reshape(64, 1))
    put("mu_g", _col4(mu[1664:2176]))
    put("w0", _col4(inp["rwkv_w0"][l])); put("a0", _col4(inp["rwkv_a0"][l]))
    put("k_k", _col4(inp["rwkv_k_k"][l])); put("k_a", _col4(inp["rwkv_k_a"][l]))
    put("gn_g", _col4(inp["rwkv_gn_g"][l])); put("gn_b", _col4(inp["rwkv_gn_b"][l]))
    put("r_k", _col4(np.asarray(inp["rwkv_r_k"][l]).reshape(512)))
    put("q_g", np.tile(np.asarray(inp["dsa_q_g"][l], np.float32), 2).reshape(128, 1))
    put("k_g", np.tile(np.asarray(inp["dsa_k_g"][l], np.float32), 2).reshape(128, 1))
    cw = np.asarray(inp["lru_conv_w"][l], np.float32)
    put("conv_w", np.concatenate([_col4(cw[i]) for i in range(4)], axis=1))
    put("conv_b", _col4(inp["lru_conv_b"][l])); put("b_r", _col4(inp["lru_b_r"][l]))
    put("b_i", _col4(inp["lru_b_i"][l])); put("lam", _col4(inp["lru_lambda"][l]))
    return prm


def blockdiag(w):
    w = np.asarray(w, np.float32)
    out = np.zeros((4, 128, 128), np.float32)
    for ct in range(4):
        out[ct, 0:64, 0:64] = w[2 * ct]
        out[ct, 64:128, 64:128] = w[2 * ct + 1]
    return out


class Prog:
    def __init__(self, S, L, phases="NACBM", dbg=()):
        self.S, self.L, self.phases, self.dbg = S, L, phases, dbg
        self.G = S // 512
        nc = self.nc = bass.Bass("TRN2", target_bir_lowering=False)
        dt = nc.dram_tensor
        self.xT = dt("xT", [8, 128, S], F32, kind="ExternalInput").ap()
        self.pT = dt("pT", [L, 2, 128, S], F32, kind="ExternalInput").ap()
        self.pos = dt("pos", [1, S], I32, kind="ExternalInput").ap()
        self.prm = dt("prm", [L, 128, NPRM], F32, kind="ExternalInput").ap()
        self.w_in = dt("w_in", [L, D, DIN], F32, kind="ExternalInput").ap()
        self.w_kidup = dt("w_kidup", [L, D, 128], F32, kind="ExternalInput").ap()
        self.w2 = dt("w2", [L, 64, 512], F32, kind="ExternalInput").ap()
        self.a2 = dt("a2", [L, 64, 512], F32, kind="ExternalInput").ap()
        self.wr_bd = dt("wr_bd", [L, 4, 128, 128], F32, kind="ExternalInput").ap()
        self.wi_bd = dt("wi_bd", [L, 4, 128, 128], F32, kind="ExternalInput").ap()
        self.w_branch = dt("w_branch", [L, 3, 512, D], F32, kind="ExternalInput").ap()
        self.w_out = dt("w_out", [L, D, D], F32, kind="ExternalInput").ap()
        self.w_ple = dt("w_ple", [L, 256, D], F32, kind="ExternalInput").ap()
        self.w_pg = dt("w_pg", [L, D, D], F32, kind="ExternalInput").ap()
        self.cst_d = dt("cst", [128, 32], F32, kind="ExternalInput").ap()
        self.ropeR_d = dt("ropeR", [128, 128], F32, kind="ExternalInput").ap()
        self.ropeT = dt("ropeT", [2, 128, S], F32, kind="Internal").ap()
        self.outT = dt("outT", [8, 128, S], F32, kind="ExternalOutput").ap()
        okind = lambda n: "ExternalOutput" if n in dbg else "Internal"
        self.hT = dt("hT", [8, 128, S], F32, kind=okind("hT")).ap()
        self.hnT = dt("hnT", [8, 128, S], BF16, kind=okind("hnT")).ap()
        self.yT = [dt("yT%d" % n, [4, 128, S], BF16, kind=okind("yT%d" % n)).ap() for n in range(3)]

    def uname(self, n):
        self._uid = getattr(self, "_uid", 0) + 1
        return "%s_u%d" % (n, self._uid)

    def pcol(self, name, j=0, rows=128):
        o, w = PCOLS[name]
        return self.prm_sb[0:rows, o + j:o + j + 1]

    def load_w(self, dst, key, src_fn, ncols, kt, scale=None, rows=128):
        fw = self.fw
        for k in range(kt):
            for c0 in range(0, ncols, 512):
                cn = min(512, ncols - c0)
                si = self.stg_i
                self.stg_i ^= 1
                stg = self.stg[si]
                fw.dma(stg[0:rows, 0:cn], src_fn(k)[:, c0:c0 + cn], writes=["stg%d" % si])
                self.cast_rr = getattr(self, "cast_rr", 0) + 1
                eng = ("vector", "scalar", "gpsimd")[self.cast_rr % 3]
                o_ap, i_ap = dst[0:rows, k, c0:c0 + cn], stg[0:rows, 0:cn]
                rk_ = ["stg%d" % si] + (["prm"] if scale is not None else [])
                if eng == "scalar":
                    fw.act(o_ap, i_ap, AF.Copy, scale=(scale(k) if scale is not None else 1.0), reads=rk_, writes=[(key, k)])
                elif scale is not None:
                    fw.v("tensor_scalar", o_ap, i_ap, scale(k), 0.0, ALU.mult, ALU.add, reads=rk_, writes=[(key, k)], eng=eng)
                else:
                    fw.v("tensor_copy", o_ap, i_ap, reads=rk_, writes=[(key, k)], eng=eng)

    def gcol(self, k):
        return self.pcol("norm_g", k)

    def build(self):
        nc = self.nc
        with ExitStack() as st:
            fw = self.fw = FW(nc, st)
            self.st = st
            self.stg = [fw.sb("stg%d" % i, [128, 512], F32) for i in range(2)]
            self.stg_i = 0
            self.prm_sb = fw.sb("prm_sb", [128, NPRM], F32)
            self.ones_f = fw.sb("ones_f", [128, 128], F32)
            fw.v("memset", self.ones_f[:], 1.0, writes=["ones_f"])
            self.PSALL = fw.ps("psall", [128, 8, 512], F32)
            self.PS = [self.PSALL[:, i, :] for i in range(8)]
            self.tiny_col = fw.sb("tiny_col", [128, 1], F32)
            self.gneps_col = fw.sb("gneps_col", [128, 1], F32)
            fw.v("memset", self.tiny_col[:], 1e-30, writes=["tiny"])
            fw.v("memset", self.gneps_col[:], GN_EPS, writes=["tiny"])
            self.eps6_col = fw.sb("eps6_col", [128, 1], F32)
            fw.v("memset", self.eps6_col[:], NORM_EPS, writes=["tiny"])
            self.cst_sb = fw.sb("cst_sb", [128, 32], F32)
            fw.dma(self.cst_sb[:], self.cst_d, writes=["cst"])
            self.make_consts()
            if "B" in self.phases:
                self.phase_R()
            with ExitStack() as zs:
                for n, ph_ in enumerate("ABC"):
                    if ph_ not in self.phases:
                        zt = zs.enter_context(self.nc.sbuf_tensor(self.uname("zt"), [128, 4, 512], BF16))
                        self.acquire(["zt%d" % n])
                        fw.v("memset", zt[:], 0.0, writes=["zt%d" % n])
                        for g in range(self.G):
                            fw.dma(self.yT[n][:, :, g * 512:(g + 1) * 512].rearrange("k p s -> p k s"), zt[:], reads=["zt%d" % n],
                                   writes=[("yT%d" % n, g, ct) for ct in range(4)])
                        self.release(["zt%d" % n])
            for l in range(self.L):
                fw.dma(self.prm_sb[:], self.prm[l], writes=["prm"])
                if l == 0 and "N" in self.phases:
                    self.phase_N0()
                if "A" in self.phases:
                    self.phase_A(l)
                if "C" in self.phases:
                    self.phase_C(l)
                if "B" in self.phases:
                    self.phase_B(l)
                if "M" in self.phases:
                    self.phase_M(l)
            fw.finish([("outT", g) for g in range(self.G)])
            fw.emit()
        return nc

    def norm_group(self, hbuf, hkey, g, tmp, tmpkey):
        fw, S = self.fw, self.S
        c0 = g * 512
        ps = self.PS[7]
        for k in range(8):
            fw.act(tmp[:, k % 2, :], hbuf[:, k, :], AF.Square, reads=[hkey], writes=[(tmpkey, k % 2)])
            fw.mm(ps[:], self.ones_f[:], tmp[:, k % 2, :], start=(k == 0), stop=(k == 7),
                  reads=["ones_f", (tmpkey, k % 2)], writes=["ps7"])
        rs = self.rs_sb
        fw.act(rs[:], ps[:], AF.Sqrt, bias=self.eps_col[:, 0:1], scale=1.0 / D, reads=["ps7", "eps"], writes=["rs"])
        fw.v("reciprocal", rs[:], rs[:], reads=["rs"], writes=["rs"])
        hn = self.hn_out
        fw.v("tensor_tensor", hn[:], hbuf[:], rs[:].unsqueeze(1).to_broadcast([128, 8, 512]), ALU.mult,
             reads=[hkey, "rs"], writes=[self.hn_out_key])
        fw.dma(self.hnT[:, :, c0:c0 + 512].rearrange("k p s -> p k s"), hn[:], reads=[self.hn_out_key],
               writes=[("hnT", g)], eng="gpsimd")

    def phase_N0(self):
        fw = self.fw
        with ExitStack() as ph:
            sb = lambda n, s, d: ph.enter_context(self.nc.sbuf_tensor(self.uname(n), list(s), d))
            hb = [sb("n0_h%d" % i, [128, 8, 512], F32) for i in range(2)]
            tmp = sb("n0_tmp", [128, 2, 512], F32)
            self.rs_sb = sb("n0_rs", [128, 512], F32)
            self.hn_out = sb("n0_hn", [128, 8, 512], BF16)
            self.hn_out_key = "hn_out"
            self.eps_col = sb("n0_eps", [128, 1], F32)
            self.acquire(["n0_h0", "n0_h1", ("n0_tmp", 0), ("n0_tmp", 1), "rs", "hn_out", "eps"])
            fw.v("memset", self.eps_col[:], NORM_EPS, writes=["eps"])
            for g in range(self.G):
                c0 = g * 512
                h = hb[g % 2]
                fw.dma(h[:], self.xT[:, :, c0:c0 + 512].rearrange("k p s -> p k s"), writes=["n0_h%d" % (g % 2)])
                self.norm_group(h, "n0_h%d" % (g % 2), g, tmp, "n0_tmp")
            self.release(["n0_h0", "n0_h1", ("n0_tmp", 0), ("n0_tmp", 1), "rs", "hn_out", "eps"])

    def release(self, keys):
        fw = self.fw
        ev = []
        for k in keys:
            if k in fw.lastw:
                ev.append(fw.lastw[k])
            ev.extend(fw.readers.get(k, ()))
        best = {}
        for e in getattr(fw, "pending_release", []) + ev:
            if e[0] not in best or best[e[0]][2] < e[2]:
                best[e[0]] = e
        fw.pending_release = list(best.values())

    def acquire(self, keys):
        fw = self.fw
        ev = getattr(fw, "pending_release", [])
        for k in keys:
            fw.readers.setdefault(k, []).extend(ev)


    def make_consts(self):
        fw = self.fw
        onesb = fw.sb("k_onesb", [128, 256], BF16)
        self.ident_b = fw.sb("k_ident", [128, 128], BF16)
        self.mask_ui = fw.sb("k_mask_ui", [128, 256], BF16)
        self.mask_sl = fw.sb("k_mask_sl", [128, 128], BF16)
        self.blk1 = fw.sb("k_blk1", [128, 128], F32)
        g = "gpsimd"
        fw.v("memset", onesb[:], 1.0, writes=["k_onesb"], eng=g)
        sel = lambda out, pat, cm, op, key: fw.op(g, lambda e: e.affine_select(out, onesb[:, 0:128], pat, op, 0.0, base=0,
                                                                                channel_multiplier=cm),
                                                  reads=["k_onesb"], writes=[key])
        sel(self.ident_b[:], [[-1, 128]], 1, ALU.is_equal, "k_ident")
        sel(self.mask_ui[:, 0:128], [[1, 128]], -1, ALU.is_gt, "k_mask_ui")
        sel(self.mask_ui[:, 128:256], [[1, 128]], -1, ALU.is_ge, "k_mask_ui")
        sel(self.mask_sl[:], [[-1, 128]], 1, ALU.is_gt, "k_mask_sl")
        fw.v("memset", self.blk1[:], 0.0, writes=["k_blk1"], eng=g)
        fw.v("memset", self.blk1[0:64, 0:64], 1.0, writes=["k_blk1"], eng=g)
        fw.v("memset", self.blk1[64:128, 64:128], 1.0, writes=["k_blk1"], eng=g)

    def phase_A(self, l):
        fw, S, G = self.fw, self.S, self.G
        PS = self.PS
        CDEC = 0.6065306597126334
        with ExitStack() as ph:
            allkeys = []

            def sb(n, s, d):
                allkeys.append(n)
                return ph.enter_context(self.nc.sbuf_tensor(self.uname(n), list(s), d))

            wA = sb("wA", [128, 8, 2176], BF16)
            w2b = sb("a_w2b", [64, 1, 512], BF16)
            a2b = sb("a_a2b", [64, 1, 512], BF16)
            hn = [sb("a_hn0", [128, 8, 512], BF16)] * 2
            omu = sb("a_omu", [128, NPRM], F32)
            prevc = sb("a_prevc", [128, 18], F32)
            ubP = [[sb("a_ub%d_%d" % (p, q), [128, 513], F32) for q in range(4)] for p in range(2)]
            usP = [[sb("a_us%d_%d" % (p, q), [128, 512], F32) for q in range(4)] for p in range(2)]
            ulo = [sb("a_ulo%d" % q, [64, 513], F32) for q in range(2)]
            twl = sb("a_twl", [64, 512], BF16)
            alb = sb("a_alb", [64, 512], BF16)
            tP = [[sb("a_t%d_%d" % (p, i), [128, 512], F32) for i in range(8)] for p in range(2)]
            t_ = tP[0]
            art = [sb("a_art%d" % ct, [128, 4, 2, 128], BF16) for ct in range(4)]
            bk = [sb("a_bk%d" % ct, [128, 2, 512], BF16) for ct in range(4)]
            vb = [sb("a_vb%d" % ct, [128, 512], BF16) for ct in range(4)]
            tok = [sb("a_tok%d" % ct, [128, 4, 3, 128], BF16) for ct in range(4)]
            bonus = [sb("a_bonus%d" % ct, [128, 512], BF16) for ct in range(4)]
            sgt = [sb("a_sg%d" % ct, [128, 512], BF16) for ct in range(4)]
            PC = sb("a_PC", [128, 4, 4], F32)
            T = sb("a_T", [128, 4, 64], F32)
            Tb = sb("a_Tb", [128, 4, 64], BF16)
            LAb = [sb("a_LAb%d" % i, [128, 4, 256], BF16) for i in range(2)]
            KAb = [sb("a_KAb%d" % i, [128, 4, 256], BF16) for i in range(2)]
            Lb = [sb("a_Lb%d" % i, [128, 4, 128], BF16) for i in range(2)]
            PPb = [[sb("a_PPb%d_%d" % (i, j), [128, 4, 256], BF16) for j in range(2)] for i in range(2)]
            XT = [[sb("a_XT%d_%d" % (i, j), [128, 4, 128], BF16) for j in range(2)] for i in range(2)]
            Wb = [sb("a_Wb%d" % i, [128, 4, 64], BF16) for i in range(2)]
            Ub = [sb("a_Ub%d" % i, [128, 4, 64], BF16) for i in range(2)]
            xc = sb("a_xc", [128, 8, 64], F32)
            sq = sb("a_sq", [128, 8, 64], F32)
            st8 = sb("a_st8", [128, 4, 8], F32)
            onb = sb("a_onb", [128, 512], BF16)
            yv = [sb("a_yv%d" % i, [128, 128], F32) for i in range(2)]
            yout = sb("a_yout", [128, 4, 512], BF16)
            self.rstm = sb("k_rstm", [128, 512], F32)
            keys = allkeys + [("wA", k) for k in range(8)] + [("a_w2b", 0), ("a_a2b", 0)]
            self.acquire(keys + ["stg0", "stg1"])
            fw.v("memset", self.rstm[:], 1.0, writes=["k_rstm"], eng="gpsimd")
            for c in range(4):
                fw.v("memset", self.rstm[:, c * 128:c * 128 + 1], 0.0, writes=["k_rstm"], eng="gpsimd")

            self.load_w(wA, "wA", lambda k: self.w_in[l, k * 128:(k + 1) * 128, OFF_A:OFF_A + 2176], 2176, 8, self.gcol)
            self.load_w(w2b, "a_w2b", lambda k: self.w2[l], 512, 1, rows=64)
            self.load_w(a2b, "a_a2b", lambda k: self.a2[l], 512, 1, rows=64)
            fw.v("tensor_scalar", omu[:], self.prm_sb[:], -1.0, 1.0, ALU.mult, ALU.add, reads=["prm"], writes=["a_omu"])
            fw.v("memset", prevc[:], 0.0, writes=["a_prevc"])
            fw.v("memset", T[:], 0.0, writes=["a_T"])
            fw.v("memset", Tb[:], 0.0, writes=["a_Tb"])
            oc = lambda name, j=0, rows=128: omu[0:rows, PCOLS[name][0] + j:PCOLS[name][0] + j + 1]
            psb0 = PS[0][:].bitcast(BF16)
            psb1 = PS[1][:].bitcast(BF16)

            def shift(ps, pskey, ubt, ubkey, pcol, out, okey, mu_ap, omu_ap, rows=128):
                fw.v("tensor_copy", ubt[0:rows, 0:1], prevc[0:rows, pcol:pcol + 1], reads=["a_prevc"], writes=[ubkey], eng="gpsimd")
                fw.act(ubt[0:rows, 1:513], ps, AF.Copy, reads=[pskey], writes=[ubkey])
                fw.v("tensor_copy", prevc[0:rows, pcol:pcol + 1], ubt[0:rows, 512:513], reads=[ubkey], writes=["a_prevc"], eng="gpsimd")
                fw.v("tensor_scalar", out, ubt[0:rows, 0:512], mu_ap, None, ALU.mult, reads=[ubkey, "prm"], writes=[okey])
                fw.v("scalar_tensor_tensor", out, ubt[0:rows, 1:513], omu_ap, out, ALU.mult, ALU.add,
                     reads=[ubkey, "a_omu", okey], writes=[okey])

            for g in range(G):
                c0 = g * 512
                hk = "a_hn0"
                hg = hn[0]
                fw.dma(hg[:], self.hnT[:, :, c0:c0 + 512].rearrange("k p s -> p k s"), reads=[("hnT", g)], writes=[hk])
                for q, (coff, nm) in enumerate([(1536, "mu_wl"), (1600, "mu_al")]):
                    for k in range(8):
                        fw.mm(PS[q][0:64, :], wA[:, k, coff:coff + 64], hg[:, k, :], start=(k == 0), stop=(k == 7),
                              reads=[("wA", k), hk], writes=["ps%d" % q])
                    shift(PS[q][0:64, :], "ps%d" % q, ulo[q], "a_ulo%d" % q, 16 + q, t_[q][0:64, :], "a_t0_%d" % q,
                          self.pcol(nm, 0, 64), oc(nm, 0, 64), rows=64)
                fw.act(twl[:], t_[0][0:64, :], AF.Tanh, reads=["a_t0_0"], writes=["a_twl"])
                fw.v("tensor_copy", alb[:], t_[1][0:64, :], reads=["a_t0_1"], writes=["a_alb"])
                def abody(ct, g=g, hk=hk, hg=hg):
                    p_ = ct % 2
                    PSp = PS[4 * p_:4 * p_ + 4]
                    pk = lambda q: "ps%d" % (4 * p_ + q)
                    ub, us, t_ = ubP[p_], usP[p_], tP[p_]
                    psb0 = PSp[0][:].bitcast(BF16)
                    psb1 = PSp[1][:].bitcast(BF16)
                    cs = slice(ct * 128, (ct + 1) * 128)
                    for q, (coff, nm) in enumerate([(0, "mu_r"), (512, "mu_k"), (1024, "mu_v"), (1664, "mu_g")]):
                        for k in range(8):
                            fw.mm(PSp[q][:], wA[:, k, coff + ct * 128:coff + (ct + 1) * 128], hg[:, k, :], start=(k == 0), stop=(k == 7),
                                  reads=[("wA", k), hk], writes=[pk(q)])
                            yield
                        shift(PSp[q][:], pk(q), ub[q], "a_ub%d_%d" % (p_, q), ct * 4 + q, us[q][:], "a_us%d_%d" % (p_, q),
                              self.pcol(nm, ct), oc(nm, ct))
                        yield
                    r_s, k_s, v_s, g_s = us
                    K = lambda i: "a_t%d_%d" % (p_, i)
                    fw.mm(PSp[0][:], w2b[:, 0, cs], twl[:], reads=[("a_w2b", 0), "a_twl"], writes=[pk(0)])
                    yield
                    fw.act(t_[0][:], PSp[0][:], AF.Sigmoid, bias=self.pcol("w0", ct), reads=[pk(0), "prm"], writes=[K(0)])
                    yield
                    fw.v("tensor_scalar", t_[0][:], t_[0][:], -CDEC, 0.0, ALU.mult, ALU.add, reads=[K(0)], writes=[K(0)], eng="gpsimd")
                    yield
                    fw.mm(PSp[1][:], a2b[:, 0, cs], alb[:], reads=[("a_a2b", 0), "a_alb"], writes=[pk(1)])
                    yield
                    fw.act(t_[1][:], PSp[1][:], AF.Sigmoid, bias=self.pcol("a0", ct), reads=[pk(1), "prm"], writes=[K(1)])
                    yield
                    fw.v("tensor_scalar", t_[2][:], k_s[:], self.pcol("k_k", ct), None, ALU.mult, reads=["a_us%d_1" % p_, "prm"], writes=[K(2)])
                    yield
                    fw.v("tensor_tensor", t_[3][:], t_[2][:], t_[2][:], ALU.mult, reads=[K(2)], writes=[K(3)], eng="gpsimd")
                    yield
                    fw.mm(PSp[2][:], self.blk1[:], t_[3][:], reads=["k_blk1", K(3)], writes=[pk(2)])
                    yield
                    fw.act(t_[3][:], PSp[2][:], AF.Sqrt, bias=self.tiny_col[:, 0:1], reads=[pk(2), "tiny"], writes=[K(3)])
                    yield
                    fw.v("reciprocal", t_[3][:], t_[3][:], reads=[K(3)], writes=[K(3)])
                    yield
                    fw.v("tensor_tensor", t_[2][:], t_[2][:], t_[3][:], ALU.mult, reads=[K(2), K(3)], writes=[K(2)])
                    yield
                    fw.v("tensor_scalar", t_[3][:], t_[1][:], self.pcol("k_a", ct), oc("k_a", ct), ALU.mult, ALU.add,
                         reads=[K(1), "prm", "a_omu"], writes=[K(3)])
                    yield
                    fw.v("tensor_tensor", t_[3][:], t_[3][:], k_s[:], ALU.mult, reads=[K(3), "a_us%d_1" % p_], writes=[K(3)], eng="gpsimd")
                    yield
                    fw.v("tensor_tensor", t_[4][:], t_[2][:], t_[1][:], ALU.mult, reads=[K(2), K(1)], writes=[K(4)], eng="gpsimd")
                    yield
                    fw.v("tensor_tensor_scan", t_[5][:], self.rstm[:], t_[0][:], 0.0, ALU.mult, ALU.add,
                         reads=["k_rstm", K(0)], writes=[K(5)])
                    yield
                    fw.v("tensor_tensor", t_[6][:], t_[5][:], t_[0][:], ALU.subtract, reads=[K(5), K(0)], writes=[K(6)], eng="gpsimd")
                    yield
                    fw.act(t_[6][:], t_[6][:], AF.Exp, reads=[K(6)], writes=[K(6)])
                    yield
                    fw.act(t_[7][:], t_[5][:], AF.Exp, scale=-1.0, reads=[K(5)], writes=[K(7)])
                    yield
                    fw.act(t_[5][:], t_[5][:], AF.Exp, reads=[K(5)], writes=[K(5)])
                    yield
                    fw.v("tensor_copy", PC[:, ct, :], t_[5][:].rearrange("p (c t) -> p c t", t=128)[:, :, 127], reads=[K(5)],
                         writes=["a_PC"], eng="gpsimd")
                    yield
                    v3 = lambda ap: ap.rearrange("p (c t) -> p c t", t=128)
                    akey = "a_art%d" % ct
                    fw.v("scalar_tensor_tensor", art[ct][:, :, 0, :], v3(t_[2][:]), -1.0, v3(t_[6][:]), ALU.mult, ALU.mult,
                         reads=[K(2), K(6)], writes=[akey])
                    yield
                    fw.v("tensor_tensor", art[ct][:, :, 1, :], v3(r_s[:]), v3(t_[5][:]), ALU.mult, reads=["a_us%d_0" % p_, K(5)], writes=[akey])
                    yield
                    fw.v("tensor_tensor", bk[ct][:, 0, :], t_[4][:], t_[7][:], ALU.mult, reads=[K(4), K(7)], writes=["a_bk%d" % ct])
                    yield
                    fw.v("tensor_tensor", bk[ct][:, 1, :], t_[3][:], t_[7][:], ALU.mult, reads=[K(3), K(7)], writes=["a_bk%d" % ct], eng="gpsimd")
                    yield
                    fw.v("tensor_copy", vb[ct][:], v_s[:], reads=["a_us%d_2" % p_], writes=["a_vb%d" % ct], eng="gpsimd")
                    yield
                    fw.v("scalar_tensor_tensor", t_[4][:], r_s[:], self.pcol("r_k", ct), t_[3][:], ALU.mult, ALU.mult,
                         reads=["a_us%d_0" % p_, "prm", K(3), K(4)], writes=[K(4)])
                    yield
                    fw.mm(PSp[3][:], self.blk1[:], t_[4][:], reads=["k_blk1", K(4)], writes=[pk(3)])
                    yield
                    fw.v("tensor_tensor", bonus[ct][:], PSp[3][:], v_s[:], ALU.mult, reads=[pk(3), "a_us%d_2" % p_], writes=["a_bonus%d" % ct])
                    yield
                    fw.act(sgt[ct][:], g_s[:], AF.Silu, reads=["a_us%d_3" % p_], writes=["a_sg%d" % ct])
                    yield
                    for half in range(2):
                        psb, pkey = (psb0, pk(0)) if half == 0 else (psb1, pk(1))
                        for cc in range(2):
                            c = half * 2 + cc
                            for qi_, (src, skey) in enumerate([(bk[ct][:, 0, c * 128:(c + 1) * 128], "a_bk%d" % ct),
                                                               (bk[ct][:, 1, c * 128:(c + 1) * 128], "a_bk%d" % ct),
                                                               (vb[ct][:, c * 128:(c + 1) * 128], "a_vb%d" % ct)]):
                                o = (cc * 3 + qi_) * 128
                                fw.tr(psb[:, o:o + 128], src, self.ident_b[:], reads=[skey, "k_ident"], writes=[pkey])
                                yield
                        fw.act(tok[ct][:, half * 2:half * 2 + 2, :, :].rearrange("p a b c -> p (a b c)"), psb[:, 0:768], AF.Copy,
                               reads=[pkey], writes=["a_tok%d" % ct])
                        yield

                fw.lockstep([abody(0), abody(1)])
                fw.lockstep([abody(2), abody(3)])
                PSALL = self.PSALL
                idb4 = self.ident_b[:].unsqueeze(1).to_broadcast([128, 4, 128])
                mui2 = self.mask_ui[:].unsqueeze(1).to_broadcast([128, 2, 256])
                msl4 = self.mask_sl[:].unsqueeze(1).to_broadcast([128, 4, 128])
                for c in range(4):
                    ccols = slice(c * 128, (c + 1) * 128)

                    def hv(qd, hi):
                        h = 2 * hi + qd
                        ct, hp = hi, qd
                        pr_ = slice(hp * 64, hp * 64 + 64)
                        d = dict(h=h, ct=ct, hp=hp, pr=pr_, po=hp * 64,
                                 at=art[ct][pr_, c, 0, :], rt=art[ct][pr_, c, 1, :],
                                 ar=art[ct][pr_, c, :, :].rearrange("p a t -> p (a t)"),
                                 bt=bk[ct][pr_, 0, ccols], kt=bk[ct][pr_, 1, ccols],
                                 rk=["a_art%d" % ct, "a_bk%d" % ct], tkey="a_tok%d" % ct,
                                 vt=tok[ct][:, c, 2, hp * 64:hp * 64 + 64], btk=tok[ct][:, c, 0, hp * 64:hp * 64 + 64],
                                 ktk=tok[ct][:, c, 1, hp * 64:hp * 64 + 64], T0b=Tb[pr_, ct, :])
                        return d

                    XYk = lambda qd: ["ps%d" % (3 * qd), "ps%d" % (3 * qd + 1)]
                    Zk = lambda qd: ["ps%d" % (3 * qd + 2)]
                    XY = lambda qd: PSALL[:, 3 * qd:3 * qd + 2, :].rearrange("p b (h x) -> p (b h) x", x=256)
                    Zv = lambda qd: PSALL[:, 3 * qd + 2, :].rearrange("p (h x) -> p h x", x=128)
                    for qd in range(2):
                        for hi in range(4):
                            d = hv(qd, hi)
                            fw.mm(XY(qd)[:, hi, :], d["bt"], d["ar"], reads=d["rk"], writes=[XYk(qd)[hi // 2]])
                            fw.mm(Zv(qd)[:, hi, :], d["at"], d["bt"], reads=d["rk"], writes=Zk(qd))
                    for qd in range(2):
                        for b2 in range(2):
                            fw.v("tensor_tensor", LAb[qd][:, 2 * b2:2 * b2 + 2, :], XY(qd)[:, 2 * b2:2 * b2 + 2, :], mui2, ALU.mult,
                                 reads=[XYk(qd)[b2], "k_mask_ui"], writes=["a_LAb%d" % qd])
                        fw.v("tensor_tensor", Lb[qd][:], Zv(qd), msl4, ALU.mult, reads=Zk(qd) + ["k_mask_sl"], writes=["a_Lb%d" % qd])
                        fw.v("tensor_tensor", XT[qd][0][:], LAb[qd][:, :, 0:128], idb4, ALU.add,
                             reads=["a_LAb%d" % qd, "k_ident"], writes=["a_XT%d_0" % qd], eng="gpsimd")
                    for k in range(1, 8):
                        for qd in range(2):
                            if k == 1:
                                Pp, PTp, pkeys = (lambda hi: Lb[qd][:, hi, :]), (lambda hi: LAb[qd][:, hi, 0:128]), ["a_Lb%d" % qd, "a_LAb%d" % qd]
                            else:
                                pb_ = PPb[qd][(k - 1) % 2]
                                Pp, PTp, pkeys = (lambda hi, pb_=pb_: pb_[:, hi, 0:128]), (lambda hi, pb_=pb_: pb_[:, hi, 128:256]), ["a_PPb%d_%d" % (qd, (k - 1) % 2)]
                            for hi in range(4):
                                if k <= 6:
                                    fw.mm(XY(qd)[:, hi, 0:128], PTp(hi), Pp(hi), reads=pkeys, writes=[XYk(qd)[hi // 2]])
                                    fw.mm(XY(qd)[:, hi, 128:256], Pp(hi), PTp(hi), reads=pkeys, writes=[XYk(qd)[hi // 2]])
                                if k == 7:
                                    d7 = hv(qd, hi)
                                    fw.mm(XY(qd)[:, hi, :], d7["kt"], d7["ar"], reads=d7["rk"], writes=[XYk(qd)[hi // 2]])
                                if k >= 2:
                                    xo = XT[qd][(k - 2) % 2]
                                    xok = "a_XT%d_%d" % (qd, (k - 2) % 2)
                                    fw.mm(Zv(qd)[:, hi, :], self.ident_b[:], xo[:, hi, :], start=True, stop=False, reads=["k_ident", xok], writes=Zk(qd))
                                    fw.mm(Zv(qd)[:, hi, :], Pp(hi), xo[:, hi, :], start=False, stop=True, reads=pkeys + [xok], writes=Zk(qd))
                        for qd in range(2):
                            if k <= 6:
                                for b2 in range(2):
                                    fw.act(PPb[qd][k % 2][:, 2 * b2:2 * b2 + 2, :], XY(qd)[:, 2 * b2:2 * b2 + 2, :], AF.Copy,
                                           reads=[XYk(qd)[b2]], writes=["a_PPb%d_%d" % (qd, k % 2)])
                            if k >= 2:
                                fw.v("tensor_copy", XT[qd][(k - 1) % 2][:], Zv(qd), reads=Zk(qd), writes=["a_XT%d_%d" % (qd, (k - 1) % 2)])
                            if k == 7:
                                for b2 in range(2):
                                    fw.v("tensor_tensor", KAb[qd][:, 2 * b2:2 * b2 + 2, :], XY(qd)[:, 2 * b2:2 * b2 + 2, :], mui2, ALU.mult,
                                         reads=[XYk(qd)[b2], "k_mask_ui"], writes=["a_KAb%d" % qd])
                    XTf = [XT[qd][0] for qd in range(2)]
                    xfk = ["a_XT%d_0" % qd for qd in range(2)]
                    Wv = lambda qd: PSALL[:, 3 * qd + 2, 0:256].rearrange("p (h x) -> p h x", x=64)
                    Uv = lambda qd: PSALL[:, 3 * qd + 2, 256:512].rearrange("p (h x) -> p h x", x=64)
                    for qd in range(2):
                        for hi in range(4):
                            d = hv(qd, hi)
                            fw.mm(Wv(qd)[:, hi, :], d["at"], d["T0b"], start=True, stop=False, reads=["a_art%d" % d["ct"], "a_Tb"], writes=Zk(qd))
                            fw.mm(Wv(qd)[:, hi, :], KAb[qd][:, hi, 0:128], d["vt"], start=False, stop=True, reads=["a_KAb%d" % qd, d["tkey"]], writes=Zk(qd))
                        fw.v("tensor_copy", Wb[qd][:], Wv(qd), reads=Zk(qd), writes=["a_Wb%d" % qd])
                    for qd in range(2):
                        for hi in range(4):
                            fw.mm(Uv(qd)[:, hi, :], XTf[qd][:, hi, :], Wb[qd][:, hi, :], reads=[xfk[qd], "a_Wb%d" % qd], writes=Zk(qd))
                        fw.v("tensor_copy", Ub[qd][:], Uv(qd), reads=Zk(qd), writes=["a_Ub%d" % qd])
                    for qd in range(2):
                        for hi in range(4):
                            d = hv(qd, hi)
                            h, ct = d["h"], d["ct"]
                            ob, okey = (PS[6], "ps6") if qd == 0 else (PS[7], "ps7")
                            osl = slice(qd * 256 + hi * 64, qd * 256 + (hi + 1) * 64)
                            fw.mm(ob[:, osl], d["rt"], d["T0b"], start=True, stop=False, reads=["a_art%d" % ct, "a_Tb"], writes=[okey])
                            fw.mm(ob[:, osl], LAb[qd][:, hi, 128:256], Ub[qd][:, hi, :], start=False, stop=False,
                                  reads=["a_LAb%d" % qd, "a_Ub%d" % qd], writes=[okey])
                            fw.mm(ob[:, osl], KAb[qd][:, hi, 128:256], d["vt"], start=False, stop=True, reads=["a_KAb%d" % qd, d["tkey"]], writes=[okey])
                            zsl = slice(ct * 64, (ct + 1) * 64)
                            fw.mm(PS[7][d["pr"], zsl], d["btk"], Ub[qd][:, hi, :], start=True, stop=False, reads=[d["tkey"], "a_Ub%d" % qd], writes=["ps7"])
                            fw.mm(PS[7][d["pr"], zsl], d["ktk"], d["vt"], start=False, stop=True, reads=[d["tkey"]], writes=["ps7"])
                    zall = PS[7][:, 0:256].rearrange("p (c i) -> p c i", i=64)
                    fw.v("tensor_tensor", T[:], T[:], zall, ALU.add, reads=["a_T", "ps7"], writes=["a_T"])
                    fw.v("tensor_tensor", T[:], T[:], PC[:, :, c:c + 1].to_broadcast([128, 4, 64]), ALU.mult, reads=["a_T", "a_PC"], writes=["a_T"])
                    fw.v("tensor_copy", Tb[:], T[:], reads=["a_T"], writes=["a_Tb"], eng="gpsimd")
                    ov = [PS[6][:, 0:256].rearrange("p (h i) -> p h i", i=64), PS[7][:, 256:512].rearrange("p (h i) -> p h i", i=64)]
                    okeys = ["ps6", "ps7"]
                    for qd in range(2):
                        fw.v("tensor_reduce", st8[:, 0, qd * 4:qd * 4 + 4], ov[qd], AX.X, ALU.add, reads=[okeys[qd]], writes=["a_st8"])
                    fw.v("tensor_scalar", st8[:, 0, :], st8[:, 0, :], 1.0 / 64, None, ALU.mult, reads=["a_st8"], writes=["a_st8"])
                    for qd in range(2):
                        fw.v("tensor_tensor", xc[:, qd * 4:qd * 4 + 4, :], ov[qd],
                             st8[:, 0, qd * 4:qd * 4 + 4].unsqueeze(2).to_broadcast([128, 4, 64]), ALU.subtract,
                             reads=[okeys[qd], "a_st8"], writes=["a_xc"])
                    fw.v("tensor_tensor", sq[:], xc[:], xc[:], ALU.mult, reads=["a_xc"], writes=["a_sq"], eng="gpsimd")
                    fw.v("tensor_reduce", st8[:, 1, :], sq[:], AX.X, ALU.add, reads=["a_sq"], writes=["a_st8"])
                    fw.act(st8[:, 1, :], st8[:, 1, :], AF.Sqrt, bias=self.gneps_col[:, 0:1], scale=1.0 / 64, reads=["a_st8", "tiny"], writes=["a_st8"])
                    fw.v("reciprocal", st8[:, 1, :], st8[:, 1, :], reads=["a_st8"], writes=["a_st8"])
                    fw.v("tensor_tensor", onb[:].rearrange("p (h i) -> p h i", i=64), xc[:],
                         st8[:, 1, :].unsqueeze(2).to_broadcast([128, 8, 64]), ALU.mult, reads=["a_xc", "a_st8"], writes=["a_onb"])
                    for ct in range(4):
                        for hp in range(2):
                            qo = (hp * 4 + ct) * 64
                            fw.tr(psb0[hp * 64:(hp + 1) * 64, ct * 128:(ct + 1) * 128], onb[:, qo:qo + 64], self.ident_b[:],
                                  reads=["a_onb", "k_ident"], writes=["ps0"])
                    for ct in range(4):
                        j = ct % 2
                        fw.v("tensor_scalar", yv[j][:], psb0[:, ct * 128:(ct + 1) * 128], self.pcol("gn_g", ct), self.pcol("gn_b", ct),
                             ALU.mult, ALU.add, reads=["ps0", "prm"], writes=["a_yv%d" % j])
                        fw.v("tensor_tensor", yv[j][:], yv[j][:], bonus[ct][:, ccols], ALU.add, reads=["a_yv%d" % j, "a_bonus%d" % ct],
                             writes=["a_yv%d" % j], eng="gpsimd")
                        fw.v("tensor_tensor", yout[:, ct, ccols], yv[j][:], sgt[ct][:, ccols], ALU.mult,
                             reads=["a_yv%d" % j, "a_sg%d" % ct], writes=["a_yout"], eng="gpsimd")
                fw.dma(self.yT[0][:, :, c0:c0 + 512].rearrange("k p s -> p k s"), yout[:], reads=["a_yout"],
                       writes=[("yT0", g, ct) for ct in range(4)], eng="gpsimd")
            self.release(keys)


    def phase_R(self):
        fw, S, G = self.fw, self.S, self.G
        TWO_PI = 6.283185307179586
        C1 = 6.28125
        C2 = 0.0019350051879882812
        C3 = TWO_PI - C1 - C2
        PI = 3.1415925
        with ExitStack() as ph:
            sb = lambda n, s, d: ph.enter_context(self.nc.sbuf_tensor(self.uname(n), list(s), d))
            posi = sb("r_posi", [128, 512], I32)
            a = sb("r_a", [128, 512], F32)
            k = sb("r_k", [128, 512], F32)
            r = sb("r_r", [128, 512], F32)
            r2 = sb("r_r2", [128, 512], F32)
            m = sb("r_m", [128, 512], F32)
            cs = sb("r_cs", [128, 2, 512], F32)
            keys = ["r_posi", "r_a", "r_k", "r_r", "r_r2", "r_m", "r_cs"]
            self.acquire(keys)
            for g in range(G):
                c0 = g * 512
                fw.dma(posi[:], self.pos[0:1, c0:c0 + 512].to_broadcast([128, 512]), writes=["r_posi"])
                fw.v("tensor_copy", a[:], posi[:], reads=["r_posi"], writes=["r_a"])
                fw.v("tensor_scalar", a[:], a[:], self.cst_sb[:, 0:1], None, ALU.mult, reads=["r_a", "cst"], writes=["r_a"])
                fw.v("tensor_scalar", k[:], a[:], 1.0 / TWO_PI, None, ALU.mult, reads=["r_a"], writes=["r_k"])
                fw.v("tensor_scalar", k[:], k[:], 12582912.0, None, ALU.add, reads=["r_k"], writes=["r_k"])
                fw.v("tensor_scalar", k[:], k[:], 12582912.0, None, ALU.subtract, reads=["r_k"], writes=["r_k"])
                fw.v("scalar_tensor_tensor", r[:], k[:], -C1, a[:], ALU.mult, ALU.add, reads=["r_k", "r_a"], writes=["r_r"])
                fw.v("scalar_tensor_tensor", r[:], k[:], -C2, r[:], ALU.mult, ALU.add, reads=["r_k", "r_r"], writes=["r_r"])
                fw.v("scalar_tensor_tensor", r[:], k[:], -C3, r[:], ALU.mult, ALU.add, reads=["r_k", "r_r"], writes=["r_r"])
                fw.v("tensor_scalar", r[:], r[:], PI, -PI, ALU.min, ALU.max, reads=["r_r"], writes=["r_r"])
                fw.v("tensor_scalar", r2[:], r[:], TWO_PI / 4, None, ALU.add, reads=["r_r"], writes=["r_r2"])
                fw.v("tensor_scalar", m[:], r2[:], PI, -TWO_PI, ALU.is_gt, ALU.mult, reads=["r_r2"], writes=["r_m"])
                fw.v("tensor_tensor", r2[:], r2[:], m[:], ALU.add, reads=["r_r2", "r_m"], writes=["r_r2"])
                fw.v("tensor_scalar", r2[:], r2[:], PI, -PI, ALU.min, ALU.max, reads=["r_r2"], writes=["r_r2"])
                fw.act(cs[:, 0, :], r2[:], AF.Sin, reads=["r_r2"], writes=["r_cs"])
                fw.act(cs[:, 1, :], r[:], AF.Sin, reads=["r_r"], writes=["r_cs"])
                fw.dma(self.ropeT[:, :, c0:c0 + 512].rearrange("k p s -> p k s"), cs[:], reads=["r_cs"], writes=[("ropeT", g)], eng="gpsimd")
            self.release(keys)

    def phase_B(self, l):
        fw, S, G = self.fw, self.S, self.G
        PS = self.PS
        NT = S // 128
        NIT = 20
        NOATT = False
        with ExitStack() as ph:
            allkeys = []

            def sb(n, s, d):
                allkeys.append(n)
                return ph.enter_context(self.nc.sbuf_tensor(self.uname(n), list(s), d))

            wB = sb("wB", [128, 8, 2372], BF16)
            wkd = sb("b_wkd", [128, 8, 128], BF16)
            ropeR = sb("b_ropeR", [128, 1, 128], BF16)
            KT = [sb("b_KT%d" % ct, [128, S], BF16) for ct in range(4)]
            KI = sb("b_KI", [128, S], BF16)
            V = sb("b_V", [128, NT, 8, 65], BF16)
            hn = sb("b_hn", [128, 8, 512], BF16)
            QT = [[sb("b_QT%d_%d" % (ct, i), [128, 512], BF16) for ct in range(4)] for i in range(2)]
            QI = [[sb("b_QI%d_%d" % (j, i), [128, 512], BF16) for j in range(2)] for i in range(2)]
            SG = [[sb("b_SG%d_%d" % (ct, i), [128, 512], BF16) for ct in range(4)] for i in range(2)]
            WI = [sb("b_WI%d" % i, [128, 4, 4], F32) for i in range(2)]
            yout = [sb("b_yout0", [128, 4, 512], BF16)] * 2
            score = sb("b_score", [128, S], F32)
            alias = S >= 4096
            if alias:
                xL = [score[:, 0:512], score[:, 1280:1792]]
                x2L = [score[:, 512:1024], score[:, 1792:2304]]
                xbL = [score[:, 1024:1280].bitcast(BF16), score[:, 2304:2560].bitcast(BF16)]
                cs = score[:, 2560:3584].rearrange("p (a b) -> p a b", b=512)
            else:
                cs = sb("b_cs", [128, 2, 512], F32)
                xL = [sb("b_x%d" % i, [128, 512], F32) for i in range(2)]
                x2L = [sb("b_x2%d" % i, [128, 512], F32) for i in range(2)]
                xbL = [sb("b_xb%d" % i, [128, 512], BF16) for i in range(2)]
            tkeys = ["b_cs"] + ["b_x%d" % i for i in range(2)] + ["b_x2%d" % i for i in range(2)] + ["b_xb%d" % i for i in range(2)]
            mm1 = [sb("b_mm1_0", [128, S], BF16)] * 2
            MT = [sb("b_MT%d" % i, [128, NT, 128], BF16) for i in range(2)]
            E = [sb("b_E%d" % i, [128, 512], BF16) for i in range(4)]
            rl = [sb("b_rl%d" % i, [128, 512], F32) for i in range(2)]
            PT = [sb("b_PT%d" % i, [128, 512], BF16) for i in range(4)]
            bs = sb("b_bs", [128, 8], F32)
            steps = sb("b_steps", [128, NIT + 1], F32)
            rec = sb("b_rec", [128, 8], F32)
            otok = sb("b_otok", [128, 8, 64], BF16)
            dmask = sb("b_dmask", [128, 128], F32)
            keys = allkeys + tkeys + [("wB", k) for k in range(8)] + [("b_wkd", k) for k in range(8)] + [("b_ropeR", 0)] + \
                [("b_KT", ct, g) for ct in range(4) for g in range(G)] + [("b_KI", g) for g in range(G)] + [("b_V", g) for g in range(G)]
            self.acquire(keys + ["stg0", "stg1"])
            self.load_w(wB, "wB", lambda k: self.w_in[l, k * 128:(k + 1) * 128, OFF_B:OFF_B + 2372], 2372, 8, self.gcol)
            self.load_w(wkd, "b_wkd", lambda k: self.w_kidup[l, k * 128:(k + 1) * 128, :], 128, 8, self.gcol)
            self.load_w(ropeR, "b_ropeR", lambda k: self.ropeR_d, 128, 1)
            fw.v("memset", V[:], 1.0, writes=[("b_V", g) for g in range(G)], eng="gpsimd")
            fw.v("memset", dmask[:], 0.0, writes=["b_dmask"], eng="gpsimd")
            fw.v("memset", dmask[0:64, 64:128], -1e30, writes=["b_dmask"], eng="gpsimd")

            def lane(p, chains):
                x_, x2, xb = xL[p], x2L[p], xbL[p]
                kx, kx2, kxb = "b_x%d" % p, "b_x2%d" % p, "b_xb%d" % p
                ps, pk = PS[p], "ps%d" % p

                def proj(w, wkey, c_lo, c_hi):
                    for k in range(8):
                        fw.mm(ps[:], w[:, k, c_lo:c_hi], hn[:, k, :], start=(k == 0), stop=(k == 7), reads=[(wkey, k), "b_hn"], writes=[pk])
                        yield

                def rope(dst, dkey):
                    fw.v("tensor_copy", xb[:], x_[:], reads=[kx], writes=[kxb], eng="gpsimd")
                    yield
                    fw.mm(ps[:], ropeR[:, 0, :], xb[:], reads=[("b_ropeR", 0), kxb], writes=[pk])
                    yield
                    fw.v("tensor_tensor", x2[:], x_[:], cs[:, 0, :], ALU.mult, reads=[kx, "b_cs"], writes=[kx2], eng="gpsimd")
                    yield
                    fw.v("tensor_tensor", x_[:], ps[:], cs[:, 1, :], ALU.mult, reads=[pk, "b_cs", kx], writes=[kx])
                    yield
                    fw.v("tensor_tensor", dst, x2[:], x_[:], ALU.add, reads=[kx2, kx], writes=[dkey], eng="gpsimd")
                    yield

                for ch in chains:
                    kind = ch[0]
                    if kind in ("q", "k"):
                        _, ct, g = ch
                        gp = g % 2
                        gc = slice(g * 512, g * 512 + 512)
                        coff, gname = (0, "q_g") if kind == "q" else (512, "k_g")
                        yield from proj(wB, "wB", coff + ct * 128, coff + (ct + 1) * 128)
                        fw.act(x_[:], ps[:], AF.Copy, reads=[pk], writes=[kx])
                        yield
                        fw.v("tensor_tensor", x2[:], x_[:], x_[:], ALU.mult, reads=[kx], writes=[kx2], eng="gpsimd")
                        yield
                        fw.mm(ps[:], self.blk1[:], x2[:], reads=["k_blk1", kx2], writes=[pk])
                        yield
                        fw.act(x2[:], ps[:], AF.Sqrt, bias=self.eps6_col[:, 0:1], scale=1.0 / 64, reads=[pk, "tiny"], writes=[kx2])
                        yield
                        fw.v("reciprocal", x2[:], x2[:], reads=[kx2], writes=[kx2])
                        yield
                        fw.v("scalar_tensor_tensor", x_[:], x_[:], self.pcol(gname, 0), x2[:], ALU.mult, ALU.mult,
                             reads=[kx, "prm", kx2], writes=[kx])
                        yield
                        if kind == "q":
                            yield from rope(QT[gp][ct][:], "b_QT%d_%d" % (ct, gp))
                        else:
                            yield from rope(KT[ct][:, gc], ("b_KT", ct, g))
                    elif kind == "qi":
                        _, j, g = ch
                        gp = g % 2
                        yield from proj(wB, "wB", 1536 + j * 128, 1536 + (j + 1) * 128)
                        fw.act(x_[:], ps[:], AF.Copy, reads=[pk], writes=[kx])
                        yield
                        yield from rope(QI[gp][j][:], "b_QI%d_%d" % (j, gp))
                    elif kind == "ki":
                        _, g = ch
                        gc = slice(g * 512, g * 512 + 512)
                        yield from proj(wkd, "b_wkd", 0, 128)
                        fw.act(x_[:], ps[:], AF.Copy, reads=[pk], writes=[kx])
                        yield
                        yield from rope(KI[:, gc], ("b_KI", g))
                    elif kind == "sg":
                        _, ct, g = ch
                        gp = g % 2
                        yield from proj(wB, "wB", 1860 + ct * 128, 1860 + (ct + 1) * 128)
                        fw.act(SG[gp][ct][:], ps[:], AF.Silu, reads=[pk], writes=["b_SG%d_%d" % (ct, gp)])
                        yield
                    elif kind == "v":
                        _, tt, g = ch
                        gp = g % 2
                        tcols = slice(tt * 128, (tt + 1) * 128)
                        for k in range(8):
                            fw.mm(ps[:], hn[:, k, tcols], wB[:, k, 1024:1536], start=(k == 0), stop=(k == 7),
                                  reads=[("wB", k), "b_hn"], writes=[pk])
                            yield
                        fw.act(V[:, g * 4 + tt, :, 0:64], ps[:].rearrange("p (h i) -> p h i", i=64), AF.Copy, reads=[pk], writes=[("b_V", g)])
                        yield
                        for k in range(8):
                            fw.mm(ps[:, 0:4], hn[:, k, tcols], wB[:, k, 1856:1860], start=(k == 0), stop=(k == 7),
                                  reads=[("wB", k), "b_hn"], writes=[pk])
                            yield
                        fw.v("tensor_scalar", WI[gp][:, tt, :], ps[:, 0:4], 1.0 / 16, None, ALU.mult, reads=[pk], writes=["b_WI%d" % gp])
                        yield

            def prep_begin(g):
                gc = slice(g * 512, g * 512 + 512)
                if alias:
                    self.release(["b_score"])
                    self.acquire(tkeys)
                fw.dma(hn[:], self.hnT[:, :, gc].rearrange("k p s -> p k s"), reads=[("hnT", g)], writes=["b_hn"])
                fw.dma(cs[:], self.ropeT[:, :, gc].rearrange("k p s -> p k s"), reads=[("ropeT", g)], writes=["b_cs"])

            def prep_lanes(g):
                chains = []
                for ct in range(4):
                    chains += [("q", ct, g), ("k", ct, g)]
                chains += [("qi", 0, g), ("qi", 1, g), ("ki", g)]
                chains += [("sg", ct, g) for ct in range(4)]
                chains += [("v", tt, g) for tt in range(4)]
                return [lane(0, chains[0::2]), lane(1, chains[1::2])]

            def prep_end(g):
                if alias:
                    self.release(tkeys)
                    self.acquire(["b_score"])

            def scores(qt):
                g, tt = qt // 4, qt % 4
                gp = g % 2
                N = (qt + 1) * 128
                tq = slice(tt * 128, (tt + 1) * 128)
                for pc in range((N + 511) // 512):
                    p0 = pc * 512
                    pn = min(512, N - p0)
                    for ih in range(4):
                        po = (ih % 2) * 64
                        fw.mm(PS[ih][:, 0:pn], QI[gp][ih // 2][po:po + 64, tq], KI[po:po + 64, p0:p0 + pn],
                              reads=["b_QI%d_%d" % (ih // 2, gp), ("b_KI", pc)], writes=["ps%d" % ih])
                    for ih in range(4):
                        r_ = rl[ih % 2]
                        rkey = "b_rl%d" % (ih % 2)
                        fw.act(r_[:, 0:pn], PS[ih][:, 0:pn], AF.Relu, reads=["ps%d" % ih], writes=[rkey])
                        if ih == 0:
                            fw.v("tensor_scalar", score[:, p0:p0 + pn], r_[:, 0:pn], WI[gp][:, tt, 0:1], None, ALU.mult,
                                 reads=[rkey, "b_WI%d" % gp], writes=["b_score"])
                        else:
                            fw.v("scalar_tensor_tensor", score[:, p0:p0 + pn], r_[:, 0:pn], WI[gp][:, tt, ih:ih + 1], score[:, p0:p0 + pn],
                                 ALU.mult, ALU.add, reads=[rkey, "b_WI%d" % gp, "b_score"], writes=["b_score"])

            def bisect_mask(qt):
                NB = qt + 1
                N = NB * 128
                mk = mm1[0]
                mkey = "b_mm1_0"
                A, lo, mid, cnt, tmp = (bs[:, i:i + 1] for i in range(5))
                if NB >= 3:
                    fw.v("tensor_reduce", A, score[:, 0:N], AX.X, ALU.max, apply_absolute_value=True, reads=["b_score"], writes=["b_bs"])
                    fw.v("tensor_scalar", A, A, 1.0001, 1e-20, ALU.mult, ALU.add, reads=["b_bs"], writes=["b_bs"])
                fw.v("tensor_tensor", score[:, N - 128:N], score[:, N - 128:N], dmask[:], ALU.add, reads=["b_score", "b_dmask"],
                     writes=["b_score"])
                if NB >= 3:
                    fw.v("tensor_scalar", steps[:], self.cst_sb[:, 1:2 + NIT], A, None, ALU.mult, reads=["cst", "b_bs"], writes=["b_steps"])
                    fw.v("tensor_scalar", mid, A, -1.0, steps[:, 0:1], ALU.mult, ALU.add, reads=["b_bs", "b_steps"], writes=["b_bs"])
                    for it in range(NIT):
                        fw.v("tensor_scalar", mk[:, 0:N], score[:, 0:N], mid, None, ALU.is_ge, ALU.add, accum_out=cnt,
                             reads=["b_score", "b_bs", mkey], writes=[mkey, "b_bs"])
                        fw.v("tensor_scalar", tmp, cnt, 255.5, steps[:, it:it + 1], ALU.is_ge, ALU.mult, reads=["b_bs", "b_steps"], writes=["b_bs"])
                        fw.v("scalar_tensor_tensor", mid, tmp, steps[:, it + 1:it + 2], mid, ALU.subtract, ALU.add,
                             reads=["b_bs", "b_steps"], writes=["b_bs"])
                    fw.v("tensor_tensor", lo, mid, steps[:, NIT:NIT + 1], ALU.subtract, reads=["b_bs", "b_steps"], writes=["b_bs"])
                else:
                    fw.v("memset", lo, -1e29, writes=["b_bs"])
                fw.v("tensor_scalar", mk[:, 0:N], score[:, 0:N], lo, None, ALU.is_ge, reads=["b_score", "b_bs"], writes=[mkey])
                psb1 = PS[1][:].bitcast(BF16)
                mt, mtkey = MT[qt % 2], "b_MT%d" % (qt % 2)
                for kb0 in range(0, NB, 8):
                    nk = min(8, NB - kb0)
                    for j in range(nk):
                        kb = kb0 + j
                        fw.tr(psb1[:, j * 128:(j + 1) * 128], mk[:, kb * 128:(kb + 1) * 128], self.ident_b[:],
                              reads=[mkey, "k_ident"], writes=["ps1"])
                    fw.act(mt[:, kb0:kb0 + nk, :].rearrange("p a b -> p (a b)"), psb1[:, 0:nk * 128], AF.Copy, reads=["ps1"], writes=[mtkey])

            def attention_gen(qt):
                if NOATT:
                    return
                g, tt = qt // 4, qt % 4
                gp = g % 2
                NB = qt + 1
                tq = slice(tt * 128, (tt + 1) * 128)
                mt, mtkey = MT[qt % 2], "b_MT%d" % (qt % 2)
                for hpair in range(4):
                    ct = hpair
                    for gi, kb0 in enumerate(range(0, NB, 4)):
                        nk = min(4, NB - kb0)
                        bis = [2 * e + gi % 2 for e in range(2)]
                        for j in range(nk):
                            kb = kb0 + j
                            for e in range(2):
                                po = e * 64
                                pl = PS[2 + bis[e]]
                                fw.mm(pl[:, j * 128:(j + 1) * 128], KT[ct][po:po + 64, kb * 128:(kb + 1) * 128], QT[gp][ct][po:po + 64, tq],
                                      reads=[("b_KT", ct, kb // 4), "b_QT%d_%d" % (ct, gp)], writes=["ps%d" % (2 + bis[e])])
                                yield
                        for e in range(2):
                            bi = bis[e]
                            fw.act(E[bi][:, 0:nk * 128], PS[2 + bi][:, 0:nk * 128], AF.Exp, scale=0.125, reads=["ps%d" % (2 + bi)], writes=["b_E%d" % bi])
                            yield
                            fw.v("tensor_tensor", PT[bi][:, 0:nk * 128], E[bi][:, 0:nk * 128],
                                 mt[:, kb0:kb0 + nk, :].rearrange("p a b -> p (a b)"), ALU.mult,
                                 reads=["b_E%d" % bi, mtkey], writes=["b_PT%d" % bi], eng="gpsimd")
                            yield
                        for e in range(2):
                            h = 2 * hpair + e
                            bi = bis[e]
                            pob = PS[7] if e == 0 else PS[6]
                            pokey = "ps7" if e == 0 else "ps6"
                            osl = slice(hpair * 65, hpair * 65 + 65)
                            for j in range(nk):
                                kb = kb0 + j
                                fw.mm(pob[:, osl], PT[bi][:, j * 128:(j + 1) * 128], V[:, kb, h, :], start=(kb == 0), stop=(kb == NB - 1),
                                      reads=["b_PT%d" % bi, ("b_V", kb // 4)], writes=[pokey])
                                yield

            def final(qt):
                g, tt = qt // 4, qt % 4
                gp = g % 2
                tq = slice(tt * 128, (tt + 1) * 128)
                otok4 = otok[:].rearrange("p (a e) i -> p a e i", e=2)
                for hb_ in range(2):
                    pob = PS[7] if hb_ == 0 else PS[6]
                    pokey = "ps7" if hb_ == 0 else "ps6"
                    pv = pob[:, 0:260].rearrange("p (h i) -> p h i", i=65)
                    fw.v("reciprocal", rec[:, hb_ * 4:hb_ * 4 + 4], pv[:, :, 64], reads=[pokey], writes=["b_rec"])
                    fw.v("tensor_tensor", otok4[:, :, hb_, :], pv[:, :, 0:64],
                         rec[:, hb_ * 4:hb_ * 4 + 4].unsqueeze(2).to_broadcast([128, 4, 64]), ALU.mult,
                         reads=[pokey, "b_rec"], writes=["b_otok"])
                of = otok[:].rearrange("p h i -> p (h i)")
                for ct in range(4):
                    pb_ = PS[7 - ct // 2][:, 384:512].bitcast(BF16)
                    pkey = "ps%d" % (7 - ct // 2)
                    fw.tr(pb_[:, (ct % 2) * 128:(ct % 2 + 1) * 128], of[:, ct * 128:(ct + 1) * 128], self.ident_b[:],
                          reads=["b_otok", "k_ident"], writes=[pkey])
                for ct in range(4):
                    pb_ = PS[7 - ct // 2][:, 384:512].bitcast(BF16)
                    pkey = "ps%d" % (7 - ct // 2)
                    fw.v("tensor_tensor", yout[gp][:, ct, tq], pb_[:, (ct % 2) * 128:(ct % 2 + 1) * 128], SG[gp][ct][:, tq], ALU.mult,
                         reads=[pkey, "b_SG%d_%d" % (ct, gp)], writes=["b_yout0"])
                if tt == 3:
                    gc = slice(g * 512, g * 512 + 512)
                    fw.dma(self.yT[1][:, :, gc].rearrange("k p s -> p k s"), yout[gp][:], reads=["b_yout0"],
                           writes=[("yT1", g, ct) for ct in range(4)], eng="gpsimd")

            prep_begin(0)
            fw.lockstep(prep_lanes(0))
            prep_end(0)
            scores(0)
            bisect_mask(0)
            for qt in range(NT):
                nxt = qt + 1
                if nxt < NT:
                    if nxt % 4 == 0:
                        gn = nxt // 4
                        prep_begin(gn)
                        fw.lockstep(prep_lanes(gn))
                        prep_end(gn)
                    scores(nxt)
                fw.lockstep([attention_gen(qt)])
                if nxt < NT:
                    bisect_mask(nxt)
                final(qt)
            self.release(keys)

    def phase_C(self, l):
        fw, S, G = self.fw, self.S, self.G
        PS = self.PS
        with ExitStack() as ph:
            sb = lambda n, s, d: ph.enter_context(self.nc.sbuf_tensor(self.uname(n), list(s), d))
            wC = sb("wC", [128, 8, 1024], BF16)
            wr = sb("c_wr", [128, 4, 128], BF16)
            wi = sb("c_wi", [128, 4, 128], BF16)
            hn = [sb("c_hn%d" % i, [128, 8, 512], BF16) for i in range(2)]
            xbuf = sb("c_xbuf", [128, 4, 515], F32)
            hprev = sb("c_hprev", [128, 4], F32)
            cl = sb("c_cl", [128, 4], F32)
            xc = [sb("c_xc%d" % i, [128, 512], F32) for i in range(2)]
            xcb = [sb("c_xcb%d" % i, [128, 512], BF16) for i in range(2)]
            r_ = [sb("c_r%d" % i, [128, 512], F32) for i in range(2)]
            i_ = [sb("c_i%d" % i, [128, 512], F32) for i in range(2)]
            a_ = [sb("c_a%d" % i, [128, 512], F32) for i in range(2)]
            b_ = [sb("c_b%d" % i, [128, 512], F32) for i in range(2)]
            sg = [sb("c_sg%d" % i, [128, 512], F32) for i in range(2)]
            yo = [sb("c_y%d" % i, [128, 512], BF16) for i in range(2)]
            names = ["wC", "c_wr", "c_wi", "c_hn0", "c_hn1", "c_xbuf", "c_hprev", "c_cl"] + \
                    [n + str(i) for n in ("c_xc", "c_xcb", "c_r", "c_i", "c_a", "c_b", "c_sg", "c_y") for i in range(2)]
            keys = names + [("wC", k) for k in range(8)] + [("c_wr", k) for k in range(4)] + [("c_wi", k) for k in range(4)] + \
                ["c_xbuf%d" % i for i in range(4)] + ["c_hprev%d" % i for i in range(4)]
            self.acquire(keys + ["stg0", "stg1"])
            self.load_w(wC, "wC", lambda k: self.w_in[l, k * 128:(k + 1) * 128, OFF_C:OFF_C + 1024], 1024, 8, self.gcol)
            self.load_w(wr, "c_wr", lambda k: self.wr_bd[l, k], 128, 4)
            self.load_w(wi, "c_wi", lambda k: self.wi_bd[l, k], 128, 4)
            fw.act(cl[:], self.prm_sb[:, PCOLS["lam"][0]:PCOLS["lam"][0] + 4], AF.Exp, scale=-1.0, reads=["prm"], writes=["c_cl"])
            fw.act(cl[:], cl[:], AF.Ln, bias=1.0, reads=["c_cl"], writes=["c_cl"])
            fw.v("tensor_scalar", cl[:], cl[:], -8.0, None, ALU.mult, reads=["c_cl"], writes=["c_cl"])
            fw.v("memset", xbuf[:], 0.0, writes=["c_xbuf%d" % i for i in range(4)])
            fw.v("memset", hprev[:], 0.0, writes=["c_hprev%d" % i for i in range(4)])
            for g in range(G):
                c0 = g * 512
                hk = "c_hn%d" % (g % 2)
                hg = hn[g % 2]
                fw.dma(hg[:], self.hnT[:, :, c0:c0 + 512].rearrange("k p s -> p k s"), reads=[("hnT", g)], writes=[hk])
                def cbody(ct, g=g, c0=c0, hk=hk, hg=hg):
                    j = ct % 2
                    pb = 4 * j
                    px, pg, pr, pi = PS[pb], PS[pb + 1], PS[pb + 2], PS[pb + 3]
                    kx, kg, kr, ki = ["ps%d" % (pb + t) for t in range(4)]
                    for k in range(8):
                        fw.mm(px[:], wC[:, k, ct * 128:(ct + 1) * 128], hg[:, k, :], start=(k == 0), stop=(k == 7),
                              reads=[("wC", k), hk], writes=[kx])
                        yield
                    for k in range(8):
                        fw.mm(pg[:], wC[:, k, 512 + ct * 128:512 + (ct + 1) * 128], hg[:, k, :], start=(k == 0), stop=(k == 7),
                              reads=[("wC", k), hk], writes=[kg])
                        yield
                    xb = xbuf[:, ct, :]
                    fw.act(xb[:, 3:515], px[:], AF.Copy, reads=[kx], writes=["c_xbuf%d" % ct])
                    yield
                    cw = lambda i: self.pcol("conv_w", i * 4 + ct)
                    fw.v("tensor_scalar", xc[j][:], xb[:, 3:515], cw(3), self.pcol("conv_b", ct), ALU.mult, ALU.add,
                         reads=["c_xbuf%d" % ct, "prm"], writes=["c_xc%d" % j])
                    yield
                    for i in range(3):
                        fw.v("scalar_tensor_tensor", xc[j][:], xb[:, i:i + 512], cw(i), xc[j][:], ALU.mult, ALU.add,
                             reads=["c_xbuf%d" % ct, "prm", "c_xc%d" % j], writes=["c_xc%d" % j])
                        yield
                    fw.v("tensor_copy", xb[:, 0:3], xb[:, 512:515], reads=["c_xbuf%d" % ct], writes=["c_xbuf%d" % ct], eng="gpsimd")
                    yield
                    fw.v("tensor_copy", xcb[j][:], xc[j][:], reads=["c_xc%d" % j], writes=["c_xcb%d" % j], eng="gpsimd")
                    yield
                    fw.mm(pr[:], wr[:, ct, :], xcb[j][:], reads=[("c_wr", ct), "c_xcb%d" % j], writes=[kr])
                    yield
                    fw.mm(pi[:], wi[:, ct, :], xcb[j][:], reads=[("c_wi", ct), "c_xcb%d" % j], writes=[ki])
                    yield
                    fw.act(r_[j][:], pr[:], AF.Sigmoid, bias=self.pcol("b_r", ct), reads=[kr, "prm"], writes=["c_r%d" % j])
                    yield
                    fw.act(i_[j][:], pi[:], AF.Sigmoid, bias=self.pcol("b_i", ct), reads=[ki, "prm"], writes=["c_i%d" % j])
                    yield
                    fw.act(sg[j][:], pg[:], AF.Silu, reads=[kg], writes=["c_sg%d" % j])
                    yield
                    fw.act(a_[j][:], r_[j][:], AF.Exp, scale=cl[:, ct:ct + 1], reads=["c_r%d" % j, "c_cl"], writes=["c_a%d" % j])
                    yield
                    fw.v("tensor_tensor", b_[j][:], a_[j][:], a_[j][:], ALU.mult, reads=["c_a%d" % j], writes=["c_b%d" % j])
                    yield
                    fw.v("tensor_scalar", b_[j][:], b_[j][:], -1.0, 1.0, ALU.mult, ALU.add, reads=["c_b%d" % j], writes=["c_b%d" % j])
                    yield
                    fw.act(b_[j][:], b_[j][:], AF.Sqrt, reads=["c_b%d" % j], writes=["c_b%d" % j])
                    yield
                    fw.v("tensor_tensor", i_[j][:], i_[j][:], xc[j][:], ALU.mult, reads=["c_i%d" % j, "c_xc%d" % j],
                         writes=["c_i%d" % j], eng="gpsimd")
                    yield
                    fw.v("tensor_tensor", b_[j][:], b_[j][:], i_[j][:], ALU.mult, reads=["c_b%d" % j, "c_i%d" % j], writes=["c_b%d" % j])
                    yield
                    fw.v("tensor_tensor_scan", r_[j][:], a_[j][:], b_[j][:], hprev[:, ct:ct + 1], ALU.mult, ALU.add,
                         reads=["c_a%d" % j, "c_b%d" % j, "c_hprev%d" % ct, "c_r%d" % j], writes=["c_r%d" % j])
                    yield
                    fw.v("tensor_copy", hprev[:, ct:ct + 1], r_[j][:, 511:512], reads=["c_r%d" % j], writes=["c_hprev%d" % ct])
                    yield
                    fw.v("tensor_tensor", yo[j][:], r_[j][:], sg[j][:], ALU.mult, reads=["c_r%d" % j, "c_sg%d" % j],
                         writes=["c_y%d" % j], eng="gpsimd")
                    yield
                    fw.dma(self.yT[2][ct, :, c0:c0 + 512], yo[j][:], reads=["c_y%d" % j], writes=[("yT2", g, ct)], eng="gpsimd")
                    yield
                fw.lockstep([cbody(0), cbody(1)])
                fw.lockstep([cbody(2), cbody(3)])
            self.release(keys)

    def phase_M(self, l):
        fw, S, G, L = self.fw, self.S, self.G, self.L
        PS = self.PS
        last = (l == L - 1)
        with ExitStack() as ph:
            sb = lambda n, s, d: ph.enter_context(self.nc.sbuf_tensor(self.uname(n), list(s), d))
            wG = sb("wG", [128, 8, 3072], BF16)
            wbr = sb("wbr", [128, 12, 1024], BF16)
            wo = sb("wo", [128, 8, 1024], BF16)
            wpg = sb("wpg", [128, 8, 1024], BF16)
            wple = sb("wple", [128, 2, 1024], BF16)
            hn = sb("m_hn", [128, 8, 512], BF16)
            ys = [sb("m_y%d" % n, [128, 4, 512], BF16) for n in range(3)]
            hb = sb("m_h", [128, 8, 512], F32)
            h1b = sb("m_h1b", [128, 8, 512], BF16)
            pf = sb("m_pf", [128, 2, 512], F32)
            pb_ = sb("m_pb", [128, 2, 512], BF16)
            mrg = sb("m_mrg", [128, 8, 512], BF16)
            sgs = [sb("m_sg%d" % n, [128, 512], F32) for n in range(3)]
            tmp = sb("m_tmp", [128, 2, 512], F32)
            self.rs_sb = sb("m_rs", [128, 512], F32)
            self.hn_out = h1b
            self.hn_out_key = "m_h1b"
            self.eps_col = sb("m_eps", [128, 1], F32)
            names = ["wG", "wbr", "wo", "wpg", "wple", "m_hn", "m_y0", "m_y1", "m_y2", "m_h", "m_h1b", "m_pf", "m_pb",
                     "m_mrg", "m_sg0", "m_sg1", "m_sg2", ("m_tmp", 0), ("m_tmp", 1), "rs", "hn_out", "eps"]
            keys = names + [("wG", k) for k in range(8)] + [("wbr", k) for k in range(12)] + \
                [("wo", k) for k in range(8)] + [("wpg", k) for k in range(8)] + [("wple", k) for k in range(2)]
            self.acquire(keys + ["stg0", "stg1"])
            fw.v("memset", self.eps_col[:], NORM_EPS, writes=["eps"])
            self.load_w(wG, "wG", lambda k: self.w_in[l, k * 128:(k + 1) * 128, OFF_G:OFF_G + 3072], 3072, 8, self.gcol)
            self.load_w(wbr, "wbr", lambda k: self.w_branch[l, k // 4, (k % 4) * 128:(k % 4 + 1) * 128, :], 1024, 12)
            self.load_w(wo, "wo", lambda k: self.w_out[l, k * 128:(k + 1) * 128, :], 1024, 8)
            self.load_w(wpg, "wpg", lambda k: self.w_pg[l, k * 128:(k + 1) * 128, :], 1024, 8)
            self.load_w(wple, "wple", lambda k: self.w_ple[l, k * 128:(k + 1) * 128, :], 1024, 2)
            hsrc = self.xT if l == 0 else self.hT
            hdst = self.outT if last else self.hT
            for g in range(G):
                c0 = g * 512
                fw.dma(hn[:], self.hnT[:, :, c0:c0 + 512].rearrange("k p s -> p k s"), reads=[("hnT", g)], writes=["m_hn"])
                for n in range(3):
                    fw.dma(ys[n][:], self.yT[n][:, :, c0:c0 + 512].rearrange("k p s -> p k s"),
                           reads=[("yT%d" % n, g, ct) for ct in range(4)], writes=["m_y%d" % n])
                fw.dma(hb[:], hsrc[:, :, c0:c0 + 512].rearrange("k p s -> p k s"),
                       reads=([("hT", g)] if l > 0 else []), writes=["m_h"])
                fw.dma(pf[:], self.pT[l, :, :, c0:c0 + 512].rearrange("k p s -> p k s"), writes=["m_pf"])
                fw.v("tensor_copy", pb_[:], pf[:], reads=["m_pf"], writes=["m_pb"], eng="gpsimd")
                for dmt in range(8):
                    cs = slice(dmt * 128, (dmt + 1) * 128)
                    gb = 3 * (dmt % 2)
                    for n in range(3):
                        yb = 6 + (dmt * 3 + n) % 2
                        for k in range(8):
                            fw.mm(PS[gb + n][:], wG[:, k, n * 1024 + dmt * 128:n * 1024 + (dmt + 1) * 128], hn[:, k, :],
                                  start=(k == 0), stop=(k == 7), reads=[("wG", k), "m_hn"], writes=["ps%d" % (gb + n)])
                        for kc in range(4):
                            fw.mm(PS[yb][:], wbr[:, n * 4 + kc, cs], ys[n][:, kc, :], start=(kc == 0), stop=(kc == 3),
                                  reads=[("wbr", n * 4 + kc), "m_y%d" % n], writes=["ps%d" % yb])
                        fw.act(sgs[n][:], PS[gb + n][:], AF.Sigmoid, reads=["ps%d" % (gb + n)], writes=["m_sg%d" % n])
                        fw.v("tensor_tensor", sgs[n][:], PS[yb][:], sgs[n][:], ALU.mult,
                             reads=["ps%d" % yb, "m_sg%d" % n], writes=["m_sg%d" % n])
                    fw.v("tensor_tensor", sgs[0][:], sgs[0][:], sgs[1][:], ALU.add, reads=["m_sg0", "m_sg1"], writes=["m_sg0"], eng="gpsimd")
                    fw.v("tensor_tensor", mrg[:, dmt, :], sgs[0][:], sgs[2][:], ALU.add, reads=["m_sg0", "m_sg2"], writes=["m_mrg"], eng="gpsimd")
                for d2 in range(8):
                    pk = 6 + d2 % 2
                    for k in range(8):
                        fw.mm(PS[pk][:], wo[:, k, d2 * 128:(d2 + 1) * 128], mrg[:, k, :], start=(k == 0), stop=(k == 7),
                              reads=[("wo", k), "m_mrg"], writes=["ps%d" % pk])
                    fw.v("tensor_tensor", hb[:, d2, :], hb[:, d2, :], PS[pk][:], ALU.add, reads=["m_h", "ps%d" % pk], writes=["m_h"])
                fw.act(h1b[:], hb[:], AF.Copy, reads=["m_h"], writes=["m_h1b"])
                for d2 in range(8):
                    pa, pp = (0, 1) if d2 % 2 == 0 else (2, 3)
                    for k in range(8):
                        fw.mm(PS[pa][:], wpg[:, k, d2 * 128:(d2 + 1) * 128], h1b[:, k, :], start=(k == 0), stop=(k == 7),
                              reads=[("wpg", k), "m_h1b"], writes=["ps%d" % pa])
                    for k in range(2):
                        fw.mm(PS[pp][:], wple[:, k, d2 * 128:(d2 + 1) * 128], pb_[:, k, :], start=(k == 0), stop=(k == 1),
                              reads=[("wple", k), "m_pb"], writes=["ps%d" % pp])
                    sgk = d2 % 2
                    fw.act(sgs[sgk][:], PS[pa][:], AF.Sigmoid, reads=["ps%d" % pa], writes=["m_sg%d" % sgk])
                    fw.v("tensor_tensor", sgs[sgk][:], PS[pp][:], sgs[sgk][:], ALU.mult, reads=["ps%d" % pp, "m_sg%d" % sgk],
                         writes=["m_sg%d" % sgk])
                    fw.v("tensor_tensor", hb[:, d2, :], hb[:, d2, :], sgs[sgk][:], ALU.add, reads=["m_h", "m_sg%d" % sgk],
                         writes=["m_h"], eng="gpsimd")
                fw.dma(hdst[:, :, c0:c0 + 512].rearrange("k p s -> p k s"), hb[:], reads=["m_h"],
                       writes=[("outT" if last else "hT", g)], eng="gpsimd")
                if not last:
                    self.norm_group(hb, "m_h", g, tmp, "m_tmp")
            self.release(keys)


_CACHE = {}


def make_in_maps(inp, S, L, ncores):
    maps = []
    w_in = np.ascontiguousarray(np.asarray(inp["w_in"], np.float32)[:L])
    ki0 = OFF_B + 1792
    w_kidup = np.ascontiguousarray(np.concatenate([w_in[:, :, ki0:ki0 + 64], w_in[:, :, ki0:ki0 + 64]], axis=2))
    prm = np.stack([pack_params(inp, l) for l in range(L)])
    cst = np.zeros((128, 32), np.float32)
    invf = (np.float32(500000.0) ** (-(np.arange(0, 16, 2, dtype=np.float32) / np.float32(16)))).astype(np.float32)
    for p_ in range(128):
        if p_ % 64 < 16:
            cst[p_, 0] = invf[p_ % 8]
    cst[:, 1:25] = (2.0 ** (-np.arange(24, dtype=np.float64)))[None, :].astype(np.float32)
    ropeR = np.zeros((128, 128), np.float32)
    for m_ in range(128):
        if m_ % 64 < 8:
            ropeR[m_ + 8, m_] = -1.0
        elif m_ % 64 < 16:
            ropeR[m_ - 8, m_] = 1.0
    shared = {
        "cst": cst, "ropeR": ropeR,
        "prm": prm, "w_in": w_in, "w_kidup": w_kidup,
        "w2": np.ascontiguousarray(np.asarray(inp["rwkv_w2"], np.float32)[:L]),
        "a2": np.ascontiguousarray(np.asarray(inp["rwkv_a2"], np.float32)[:L]),
        "wr_bd": np.stack([blockdiag(inp["lru_w_r"][l]) for l in range(L)]),
        "wi_bd": np.stack([blockdiag(inp["lru_w_i"][l]) for l in range(L)]),
        "w_branch": np.ascontiguousarray(np.asarray(inp["w_branch"], np.float32)[:L]),
        "w_out": np.ascontiguousarray(np.asarray(inp["w_out"], np.float32)[:L]),
        "w_ple": np.ascontiguousarray(np.asarray(inp["w_ple"], np.float32)[:L]),
        "w_pg": np.ascontiguousarray(np.asarray(inp["w_ple_gate"], np.float32)[:L]),
    }
    x = np.asarray(inp["x"], np.float32)
    p = np.asarray(inp["p"], np.float32)
    pos = np.asarray(inp["positions"], np.int32)
    nb = x.shape[0]
    for c in range(ncores):
        b = (c // 2) % nb
        m = dict(shared)
        m["xT"] = np.ascontiguousarray(x[b].T.reshape(8, 128, S))
        m["pT"] = np.ascontiguousarray(np.stack([p[l, b].T.reshape(2, 128, S) for l in range(L)]))
        m["pos"] = np.ascontiguousarray(pos[b].reshape(1, S))
        maps.append(m)
    return maps


def kernel(**inputs):
    x = np.asarray(inputs["x"])
    B, S, _ = x.shape
    L = np.asarray(inputs["w_in"]).shape[0]
    key = (S, L)
    if key not in _CACHE:
        _CACHE[key] = Prog(S, L).build()
    nc = _CACHE[key]
    maps = make_in_maps(inputs, S, L, 8)
    res = run_bass_kernel_spmd(nc, maps, core_ids=list(range(8)))
    out = np.zeros((B, S, D), np.float32)
    for b in range(B):
        out[b] = res.results[2 * b]["outT"].reshape(D, S).T
    return out
```

```python
from contextlib import ExitStack
import numpy as np
import concourse.bass as bass
import concourse.mybir as mybir
from concourse.bass_utils import run_bass_kernel_spmd

F32 = mybir.dt.float32
BF16 = mybir.dt.bfloat16
I32 = mybir.dt.int32
AF = mybir.ActivationFunctionType
ALU = mybir.AluOpType
AX = mybir.AxisListType

ENGS = ("tensor", "vector", "scalar", "gpsimd", "sync")
N_DMA_SEMS = 24

D = 1024
DIN = 8644
OFF_A, OFF_B, OFF_C, OFF_G = 0, 2176, 4548, 5572
NORM_EPS = 1e-6
GN_EPS = 64e-5


class FW:
    def __init__(self, nc, stack, same_engine_sync=True):
        self.nc = nc
        self.stack = stack
        self.q = {e: [] for e in ENGS}
        self.cnt = {e: 0 for e in ENGS}
        self.sem = {e: stack.enter_context(nc.semaphore("s_" + e)) for e in ENGS}
        self.dsem = [stack.enter_context(nc.semaphore("d%d" % i)) for i in range(N_DMA_SEMS)]
        self.dcnt = [0] * N_DMA_SEMS
        self.dnext = 0
        self.seen = {e: {} for e in ENGS}
        self.lastw = {}
        self.readers = {}
        self.same = same_engine_sync
        self.ninst = 0
        self.rr = 0

    def sb(self, name, shape, dt):
        return self.stack.enter_context(self.nc.sbuf_tensor(name, list(shape), dt))

    def ps(self, name, shape, dt=F32):
        return self.stack.enter_context(self.nc.psum_tensor(name, list(shape), dt))

    def _deps(self, eng, reads, writes):
        ev = []
        for k in reads:
            if k in self.lastw:
                ev.append(self.lastw[k])
        for k in writes:
            if k in self.lastw:
                ev.append(self.lastw[k])
            ev.extend(self.readers.get(k, ()))
        best = {}
        for (sname, sem, val, src) in ev:
            if src == eng and (eng == "tensor" or not self.same):
                continue
            if self.seen[eng].get(sname, 0) >= val:
                continue
            if sname not in best or best[sname][1] < val:
                best[sname] = (sem, val)
        waits = []
        for sname, (sem, val) in best.items():
            self.seen[eng][sname] = val
            waits.append((sem, val))
        return waits

    def _commit(self, event, reads, writes):
        for k in writes:
            self.lastw[k] = event
            self.readers[k] = []
        for k in reads:
            if k in writes:
                continue
            self.readers.setdefault(k, []).append(event)

    def op(self, eng, fn, reads=(), writes=()):
        waits = self._deps(eng, reads, writes)
        self.cnt[eng] += 1
        idx = self.cnt[eng]
        sem = self.sem[eng]
        self.q[eng].append((waits, fn, sem, 1))
        self._commit(("s_" + eng, sem, idx, eng), reads, writes)
        self.ninst += 1

    def dma(self, out, in_, reads=(), writes=(), eng="sync", **kw):
        lo, n = (0, 16) if eng == "sync" else (16, N_DMA_SEMS - 16)
        self.dnext_q = getattr(self, "dnext_q", {})
        i = self.dnext_q.get(eng, 0)
        self.dnext_q[eng] = (i + 1) % n
        slot = lo + i
        sem = self.dsem[slot]
        sname = "d%d" % slot
        waits = self._deps(eng, reads, writes)
        prev = self.dcnt[slot] * 16
        if prev and self.seen[eng].get(sname, 0) < prev:
            waits.append((sem, prev))
            self.seen[eng][sname] = prev
        self.dcnt[slot] += 1
        val = self.dcnt[slot] * 16
        self.q[eng].append((waits, lambda e: e.dma_start(out=out, in_=in_, **kw), sem, 16))
        self._commit((sname, sem, val, "dma"), reads, writes)
        self.ninst += 1

    def finish(self, keys, eng="sync"):
        waits = self._deps(eng, keys, ())
        self.q[eng].append((waits, None, None, 0))

    def emit(self):
        nc = self.nc
        with nc.Block() as block:
            for ename in ENGS:
                items = self.q[ename]
                if not items:
                    continue

                def body(e, items=items):
                    for waits, fn, sem, inc in items:
                        for (ws, wv) in waits:
                            e.wait_ge(ws, wv)
                        if fn is not None:
                            fn(e).then_inc(sem, inc)

                getattr(block, ename)(body)

    def mm(self, out, lhsT, rhs, start=True, stop=True, reads=(), writes=()):
        self.op("tensor", lambda e: e.matmul(out, lhsT, rhs, start=start, stop=stop), reads, writes)

    def tr(self, out, in_, ident, reads=(), writes=()):
        self.op("tensor", lambda e: e.transpose(out, in_, ident), reads, writes)

    def act(self, out, in_, func, bias=0.0, scale=1.0, reads=(), writes=(), accum_out=None):
        if accum_out is None:
            self.op("scalar", lambda e: e.activation(out, in_, func, bias=bias, scale=scale), reads, writes)
        else:
            self.op("scalar", lambda e: e.activation(out, in_, func, bias=bias, scale=scale,
                                                     accum_out=accum_out), reads, writes)

    def v(self, name, *args, reads=(), writes=(), eng="vector", **kw):
        self.op(eng, lambda e: getattr(e, name)(*args, **kw), reads, writes)

    @staticmethod
    def lockstep(gens):
        gens = list(gens)
        while gens:
            for g_ in list(gens):
                try:
                    next(g_)
                except StopIteration:
                    gens.remove(g_)

    def cast_eng(self):
        self.rr += 1
        return ("vector", "gpsimd")[self.rr % 2]


PCOLS = {}
_o = 0
for _n, _w in [("norm_g", 8), ("mu_r", 4), ("mu_k", 4), ("mu_v", 4), ("mu_g", 4), ("mu_wl", 1), ("mu_al", 1),
               ("w0", 4), ("a0", 4), ("k_k", 4), ("k_a", 4), ("gn_g", 4), ("gn_b", 4), ("r_k", 4),
               ("q_g", 1), ("k_g", 1),
               ("conv_w", 16), ("conv_b", 4), ("b_r", 4), ("b_i", 4), ("lam", 4)]:
    PCOLS[_n] = (_o, _w)
    _o += _w
NPRM = _o


def _col4(v):
    return np.ascontiguousarray(np.asarray(v, np.float32).reshape(4, 128).T)


def pack_params(inp, l):
    prm = np.zeros((128, NPRM), np.float32)

    def put(name, arr):
        o, w = PCOLS[name]
        prm[:arr.shape[0], o:o + w] = arr

    put("norm_g", np.asarray(inp["norm_g"][l], np.float32).reshape(8, 128).T)
    mu = np.asarray(inp["rwkv_mu"][l], np.float32)
    put("mu_r", _col4(mu[0:512])); put("mu_k", _col4(mu[512:1024])); put("mu_v", _col4(mu[1024:1536]))
    put("mu_wl", mu[1536:1600].reshape(64, 1)); put("mu_al", mu[1600:1664].reshape(64, 1))
    put("mu_g", _col4(mu[1664:2176]))
    put("w0", _col4(inp["rwkv_w0"][l])); put("a0", _col4(inp["rwkv_a0"][l]))
    put("k_k", _col4(inp["rwkv_k_k"][l])); put("k_a", _col4(inp["rwkv_k_a"][l]))
    put("gn_g", _col4(inp["rwkv_gn_g"][l])); put("gn_b", _col4(inp["rwkv_gn_b"][l]))
    put("r_k", _col4(np.asarray(inp["rwkv_r_k"][l]).reshape(512)))
    put("q_g", np.tile(np.asarray(inp["dsa_q_g"][l], np.float32), 2).reshape(128, 1))
    put("k_g", np.tile(np.asarray(inp["dsa_k_g"][l], np.float32), 2).reshape(128, 1))
    cw = np.asarray(inp["lru_conv_w"][l], np.float32)
    put("conv_w", np.concatenate([_col4(cw[i]) for i in range(4)], axis=1))
    put("conv_b", _col4(inp["lru_conv_b"][l])); put("b_r", _col4(inp["lru_b_r"][l]))
    put("b_i", _col4(inp["lru_b_i"][l])); put("lam", _col4(inp["lru_lambda"][l]))
    return prm


def blockdiag(w):
    w = np.asarray(w, np.float32)
    out = np.zeros((4, 128, 128), np.float32)
    for ct in range(4):
        out[ct, 0:64, 0:64] = w[2 * ct]
        out[ct, 64:128, 64:128] = w[2 * ct + 1]
    return out


class Prog:
    def __init__(self, S, L, phases="NACBM", dbg=()):
        self.S, self.L, self.phases, self.dbg = S, L, phases, dbg
        self.G = S // 512
        nc = self.nc = bass.Bass("TRN2", target_bir_lowering=False)
        dt = nc.dram_tensor
        self.xT = dt("xT", [8, 128, S], F32, kind="ExternalInput").ap()
        self.pT = dt("pT", [L, 2, 128, S], F32, kind="ExternalInput").ap()
        self.pos = dt("pos", [1, S], I32, kind="ExternalInput").ap()
        self.prm = dt("prm", [L, 128, NPRM], F32, kind="ExternalInput").ap()
        self.w_in = dt("w_in", [L, D, DIN], F32, kind="ExternalInput").ap()
        self.w_kidup = dt("w_kidup", [L, D, 128], F32, kind="ExternalInput").ap()
        self.w2 = dt("w2", [L, 64, 512], F32, kind="ExternalInput").ap()
        self.a2 = dt("a2", [L, 64, 512], F32, kind="ExternalInput").ap()
        self.wr_bd = dt("wr_bd", [L, 4, 128, 128], F32, kind="ExternalInput").ap()
        self.wi_bd = dt("wi_bd", [L, 4, 128, 128], F32, kind="ExternalInput").ap()
        self.w_branch = dt("w_branch", [L, 3, 512, D], F32, kind="ExternalInput").ap()
        self.w_out = dt("w_out", [L, D, D], F32, kind="ExternalInput").ap()
        self.w_ple = dt("w_ple", [L, 256, D], F32, kind="ExternalInput").ap()
        self.w_pg = dt("w_pg", [L, D, D], F32, kind="ExternalInput").ap()
        self.cst_d = dt("cst", [128, 32], F32, kind="ExternalInput").ap()
        self.ropeR_d = dt("ropeR", [128, 128], F32, kind="ExternalInput").ap()
        self.ropeT = dt("ropeT", [2, 128, S], F32, kind="Internal").ap()
        self.outT = dt("outT", [8, 128, S], F32, kind="ExternalOutput").ap()
        okind = lambda n: "ExternalOutput" if n in dbg else "Internal"
        self.hT = dt("hT", [8, 128, S], F32, kind=okind("hT")).ap()
        self.hnT = dt("hnT", [8, 128, S], BF16, kind=okind("hnT")).ap()
        self.yT = [dt("yT%d" % n, [4, 128, S], BF16, kind=okind("yT%d" % n)).ap() for n in range(3)]

    def uname(self, n):
        self._uid = getattr(self, "_uid", 0) + 1
        return "%s_u%d" % (n, self._uid)

    def pcol(self, name, j=0, rows=128):
        o, w = PCOLS[name]
        return self.prm_sb[0:rows, o + j:o + j + 1]

    def load_w(self, dst, key, src_fn, ncols, kt, scale=None, rows=128):
        fw = self.fw
        for k in range(kt):
            for c0 in range(0, ncols, 512):
                cn = min(512, ncols - c0)
                si = self.stg_i
                self.stg_i ^= 1
                stg = self.stg[si]
                fw.dma(stg[0:rows, 0:cn], src_fn(k)[:, c0:c0 + cn], writes=["stg%d" % si])
                self.cast_rr = getattr(self, "cast_rr", 0) + 1
                eng = ("vector", "scalar", "gpsimd")[self.cast_rr % 3]
                o_ap, i_ap = dst[0:rows, k, c0:c0 + cn], stg[0:rows, 0:cn]
                rk_ = ["stg%d" % si] + (["prm"] if scale is not None else [])
                if eng == "scalar":
                    fw.act(o_ap, i_ap, AF.Copy, scale=(scale(k) if scale is not None else 1.0), reads=rk_, writes=[(key, k)])
                elif scale is not None:
                    fw.v("tensor_scalar", o_ap, i_ap, scale(k), 0.0, ALU.mult, ALU.add, reads=rk_, writes=[(key, k)], eng=eng)
                else:
                    fw.v("tensor_copy", o_ap, i_ap, reads=rk_, writes=[(key, k)], eng=eng)

    def gcol(self, k):
        return self.pcol("norm_g", k)

    def build(self):
        nc = self.nc
        with ExitStack() as st:
            fw = self.fw = FW(nc, st)
            self.st = st
            self.stg = [fw.sb("stg%d" % i, [128, 512], F32) for i in range(2)]
            self.stg_i = 0
            self.prm_sb = fw.sb("prm_sb", [128, NPRM], F32)
            self.ones_f = fw.sb("ones_f", [128, 128], F32)
            fw.v("memset", self.ones_f[:], 1.0, writes=["ones_f"])
            self.PSALL = fw.ps("psall", [128, 8, 512], F32)
            self.PS = [self.PSALL[:, i, :] for i in range(8)]
            self.tiny_col = fw.sb("tiny_col", [128, 1], F32)
            self.gneps_col = fw.sb("gneps_col", [128, 1], F32)
            fw.v("memset", self.tiny_col[:], 1e-30, writes=["tiny"])
            fw.v("memset", self.gneps_col[:], GN_EPS, writes=["tiny"])
            self.eps6_col = fw.sb("eps6_col", [128, 1], F32)
            fw.v("memset", self.eps6_col[:], NORM_EPS, writes=["tiny"])
            self.cst_sb = fw.sb("cst_sb", [128, 32], F32)
            fw.dma(self.cst_sb[:], self.cst_d, writes=["cst"])
            self.make_consts()
            if "B" in self.phases:
                self.phase_R()
            with ExitStack() as zs:
                for n, ph_ in enumerate("ABC"):
                    if ph_ not in self.phases:
                        zt = zs.enter_context(self.nc.sbuf_tensor(self.uname("zt"), [128, 4, 512], BF16))
                        self.acquire(["zt%d" % n])
                        fw.v("memset", zt[:], 0.0, writes=["zt%d" % n])
                        for g in range(self.G):
                            fw.dma(self.yT[n][:, :, g * 512:(g + 1) * 512].rearrange("k p s -> p k s"), zt[:], reads=["zt%d" % n],
                                   writes=[("yT%d" % n, g, ct) for ct in range(4)])
                        self.release(["zt%d" % n])
            for l in range(self.L):
                fw.dma(self.prm_sb[:], self.prm[l], writes=["prm"])
                if l == 0 and "N" in self.phases:
                    self.phase_N0()
                if "A" in self.phases:
                    self.phase_A(l)
                if "C" in self.phases:
                    self.phase_C(l)
                if "B" in self.phases:
                    self.phase_B(l)
                if "M" in self.phases:
                    self.phase_M(l)
            fw.finish([("outT", g) for g in range(self.G)])
            fw.emit()
        return nc

    def norm_group(self, hbuf, hkey, g, tmp, tmpkey):
        fw, S = self.fw, self.S
        c0 = g * 512
        ps = self.PS[7]
        for k in range(8):
            fw.act(tmp[:, k % 2, :], hbuf[:, k, :], AF.Square, reads=[hkey], writes=[(tmpkey, k % 2)])
            fw.mm(ps[:], self.ones_f[:], tmp[:, k % 2, :], start=(k == 0), stop=(k == 7),
                  reads=["ones_f", (tmpkey, k % 2)], writes=["ps7"])
        rs = self.rs_sb
        fw.act(rs[:], ps[:], AF.Sqrt, bias=self.eps_col[:, 0:1], scale=1.0 / D, reads=["ps7", "eps"], writes=["rs"])
        fw.v("reciprocal", rs[:], rs[:], reads=["rs"], writes=["rs"])
        hn = self.hn_out
        fw.v("tensor_tensor", hn[:], hbuf[:], rs[:].unsqueeze(1).to_broadcast([128, 8, 512]), ALU.mult,
             reads=[hkey, "rs"], writes=[self.hn_out_key])
        fw.dma(self.hnT[:, :, c0:c0 + 512].rearrange("k p s -> p k s"), hn[:], reads=[self.hn_out_key],
               writes=[("hnT", g)], eng="gpsimd")

    def phase_N0(self):
        fw = self.fw
        with ExitStack() as ph:
            sb = lambda n, s, d: ph.enter_context(self.nc.sbuf_tensor(self.uname(n), list(s), d))
            hb = [sb("n0_h%d" % i, [128, 8, 512], F32) for i in range(2)]
            tmp = sb("n0_tmp", [128, 2, 512], F32)
            self.rs_sb = sb("n0_rs", [128, 512], F32)
            self.hn_out = sb("n0_hn", [128, 8, 512], BF16)
            self.hn_out_key = "hn_out"
            self.eps_col = sb("n0_eps", [128, 1], F32)
            self.acquire(["n0_h0", "n0_h1", ("n0_tmp", 0), ("n0_tmp", 1), "rs", "hn_out", "eps"])
            fw.v("memset", self.eps_col[:], NORM_EPS, writes=["eps"])
            for g in range(self.G):
                c0 = g * 512
                h = hb[g % 2]
                fw.dma(h[:], self.xT[:, :, c0:c0 + 512].rearrange("k p s -> p k s"), writes=["n0_h%d" % (g % 2)])
                self.norm_group(h, "n0_h%d" % (g % 2), g, tmp, "n0_tmp")
            self.release(["n0_h0", "n0_h1", ("n0_tmp", 0), ("n0_tmp", 1), "rs", "hn_out", "eps"])

    def release(self, keys):
        fw = self.fw
        ev = []
        for k in keys:
            if k in fw.lastw:
                ev.append(fw.lastw[k])
            ev.extend(fw.readers.get(k, ()))
        best = {}
        for e in getattr(fw, "pending_release", []) + ev:
            if e[0] not in best or best[e[0]][2] < e[2]:
                best[e[0]] = e
        fw.pending_release = list(best.values())

    def acquire(self, keys):
        fw = self.fw
        ev = getattr(fw, "pending_release", [])
        for k in keys:
            fw.readers.setdefault(k, []).extend(ev)


    def make_consts(self):
        fw = self.fw
        onesb = fw.sb("k_onesb", [128, 256], BF16)
        self.ident_b = fw.sb("k_ident", [128, 128], BF16)
        self.mask_ui = fw.sb("k_mask_ui", [128, 256], BF16)
        self.mask_sl = fw.sb("k_mask_sl", [128, 128], BF16)
        self.blk1 = fw.sb("k_blk1", [128, 128], F32)
        g = "gpsimd"
        fw.v("memset", onesb[:], 1.0, writes=["k_onesb"], eng=g)
        sel = lambda out, pat, cm, op, key: fw.op(g, lambda e: e.affine_select(out, onesb[:, 0:128], pat, op, 0.0, base=0,
                                                                                channel_multiplier=cm),
                                                  reads=["k_onesb"], writes=[key])
        sel(self.ident_b[:], [[-1, 128]], 1, ALU.is_equal, "k_ident")
        sel(self.mask_ui[:, 0:128], [[1, 128]], -1, ALU.is_gt, "k_mask_ui")
        sel(self.mask_ui[:, 128:256], [[1, 128]], -1, ALU.is_ge, "k_mask_ui")
        sel(self.mask_sl[:], [[-1, 128]], 1, ALU.is_gt, "k_mask_sl")
        fw.v("memset", self.blk1[:], 0.0, writes=["k_blk1"], eng=g)
        fw.v("memset", self.blk1[0:64, 0:64], 1.0, writes=["k_blk1"], eng=g)
        fw.v("memset", self.blk1[64:128, 64:128], 1.0, writes=["k_blk1"], eng=g)

    def phase_A(self, l):
        fw, S, G = self.fw, self.S, self.G
        PS = self.PS
        CDEC = 0.6065306597126334
        with ExitStack() as ph:
            allkeys = []

            def sb(n, s, d):
                allkeys.append(n)
                return ph.enter_context(self.nc.sbuf_tensor(self.uname(n), list(s), d))

            wA = sb("wA", [128, 8, 2176], BF16)
            w2b = sb("a_w2b", [64, 1, 512], BF16)
            a2b = sb("a_a2b", [64, 1, 512], BF16)
            hn = [sb("a_hn0", [128, 8, 512], BF16)] * 2
            omu = sb("a_omu", [128, NPRM], F32)
            prevc = sb("a_prevc", [128, 18], F32)
            ubP = [[sb("a_ub%d_%d" % (p, q), [128, 513], F32) for q in range(4)] for p in range(2)]
            usP = [[sb("a_us%d_%d" % (p, q), [128, 512], F32) for q in range(4)] for p in range(2)]
            ulo = [sb("a_ulo%d" % q, [64, 513], F32) for q in range(2)]
            twl = sb("a_twl", [64, 512], BF16)
            alb = sb("a_alb", [64, 512], BF16)
            tP = [[sb("a_t%d_%d" % (p, i), [128, 512], F32) for i in range(8)] for p in range(2)]
            t_ = tP[0]
            art = [sb("a_art%d" % ct, [128, 4, 2, 128], BF16) for ct in range(4)]
            bk = [sb("a_bk%d" % ct, [128, 2, 512], BF16) for ct in range(4)]
            vb = [sb("a_vb%d" % ct, [128, 512], BF16) for ct in range(4)]
            tok = [sb("a_tok%d" % ct, [128, 4, 3, 128], BF16) for ct in range(4)]
            bonus = [sb("a_bonus%d" % ct, [128, 512], BF16) for ct in range(4)]
            sgt = [sb("a_sg%d" % ct, [128, 512], BF16) for ct in range(4)]
            PC = sb("a_PC", [128, 4, 4], F32)
            T = sb("a_T", [128, 4, 64], F32)
            Tb = sb("a_Tb", [128, 4, 64], BF16)
            LAb = [sb("a_LAb%d" % i, [128, 4, 256], BF16) for i in range(2)]
            KAb = [sb("a_KAb%d" % i, [128, 4, 256], BF16) for i in range(2)]
            Lb = [sb("a_Lb%d" % i, [128, 4, 128], BF16) for i in range(2)]
            PPb = [[sb("a_PPb%d_%d" % (i, j), [128, 4, 256], BF16) for j in range(2)] for i in range(2)]
            XT = [[sb("a_XT%d_%d" % (i, j), [128, 4, 128], BF16) for j in range(2)] for i in range(2)]
            Wb = [sb("a_Wb%d" % i, [128, 4, 64], BF16) for i in range(2)]
            Ub = [sb("a_Ub%d" % i, [128, 4, 64], BF16) for i in range(2)]
            xc = sb("a_xc", [128, 8, 64], F32)
            sq = sb("a_sq", [128, 8, 64], F32)
            st8 = sb("a_st8", [128, 4, 8], F32)
            onb = sb("a_onb", [128, 512], BF16)
            yv = [sb("a_yv%d" % i, [128, 128], F32) for i in range(2)]
            yout = sb("a_yout", [128, 4, 512], BF16)
            self.rstm = sb("k_rstm", [128, 512], F32)
            keys = allkeys + [("wA", k) for k in range(8)] + [("a_w2b", 0), ("a_a2b", 0)]
            self.acquire(keys + ["stg0", "stg1"])
            fw.v("memset", self.rstm[:], 1.0, writes=["k_rstm"], eng="gpsimd")
            for c in range(4):
                fw.v("memset", self.rstm[:, c * 128:c * 128 + 1], 0.0, writes=["k_rstm"], eng="gpsimd")

            self.load_w(wA, "wA", lambda k: self.w_in[l, k * 128:(k + 1) * 128, OFF_A:OFF_A + 2176], 2176, 8, self.gcol)
            self.load_w(w2b, "a_w2b", lambda k: self.w2[l], 512, 1, rows=64)
            self.load_w(a2b, "a_a2b", lambda k: self.a2[l], 512, 1, rows=64)
            fw.v("tensor_scalar", omu[:], self.prm_sb[:], -1.0, 1.0, ALU.mult, ALU.add, reads=["prm"], writes=["a_omu"])
            fw.v("memset", prevc[:], 0.0, writes=["a_prevc"])
            fw.v("memset", T[:], 0.0, writes=["a_T"])
            fw.v("memset", Tb[:], 0.0, writes=["a_Tb"])
            oc = lambda name, j=0, rows=128: omu[0:rows, PCOLS[name][0] + j:PCOLS[name][0] + j + 1]
            psb0 = PS[0][:].bitcast(BF16)
            psb1 = PS[1][:].bitcast(BF16)

            def shift(ps, pskey, ubt, ubkey, pcol, out, okey, mu_ap, omu_ap, rows=128):
                fw.v("tensor_copy", ubt[0:rows, 0:1], prevc[0:rows, pcol:pcol + 1], reads=["a_prevc"], writes=[ubkey], eng="gpsimd")
                fw.act(ubt[0:rows, 1:513], ps, AF.Copy, reads=[pskey], writes=[ubkey])
                fw.v("tensor_copy", prevc[0:rows, pcol:pcol + 1], ubt[0:rows, 512:513], reads=[ubkey], writes=["a_prevc"], eng="gpsimd")
                fw.v("tensor_scalar", out, ubt[0:rows, 0:512], mu_ap, None, ALU.mult, reads=[ubkey, "prm"], writes=[okey])
                fw.v("scalar_tensor_tensor", out, ubt[0:rows, 1:513], omu_ap, out, ALU.mult, ALU.add,
                     reads=[ubkey, "a_omu", okey], writes=[okey])

            for g in range(G):
                c0 = g * 512
                hk = "a_hn0"
                hg = hn[0]
                fw.dma(hg[:], self.hnT[:, :, c0:c0 + 512].rearrange("k p s -> p k s"), reads=[("hnT", g)], writes=[hk])
                for q, (coff, nm) in enumerate([(1536, "mu_wl"), (1600, "mu_al")]):
                    for k in range(8):
                        fw.mm(PS[q][0:64, :], wA[:, k, coff:coff + 64], hg[:, k, :], start=(k == 0), stop=(k == 7),
                              reads=[("wA", k), hk], writes=["ps%d" % q])
                    shift(PS[q][0:64, :], "ps%d" % q, ulo[q], "a_ulo%d" % q, 16 + q, t_[q][0:64, :], "a_t0_%d" % q,
                          self.pcol(nm, 0, 64), oc(nm, 0, 64), rows=64)
                fw.act(twl[:], t_[0][0:64, :], AF.Tanh, reads=["a_t0_0"], writes=["a_twl"])
                fw.v("tensor_copy", alb[:], t_[1][0:64, :], reads=["a_t0_1"], writes=["a_alb"])
                def abody(ct, g=g, hk=hk, hg=hg):
                    p_ = ct % 2
                    PSp = PS[4 * p_:4 * p_ + 4]
                    pk = lambda q: "ps%d" % (4 * p_ + q)
                    ub, us, t_ = ubP[p_], usP[p_], tP[p_]
                    psb0 = PSp[0][:].bitcast(BF16)
                    psb1 = PSp[1][:].bitcast(BF16)
                    cs = slice(ct * 128, (ct + 1) * 128)
                    for q, (coff, nm) in enumerate([(0, "mu_r"), (512, "mu_k"), (1024, "mu_v"), (1664, "mu_g")]):
                        for k in range(8):
                            fw.mm(PSp[q][:], wA[:, k, coff + ct * 128:coff + (ct + 1) * 128], hg[:, k, :], start=(k == 0), stop=(k == 7),
                                  reads=[("wA", k), hk], writes=[pk(q)])
                            yield
                        shift(PSp[q][:], pk(q), ub[q], "a_ub%d_%d" % (p_, q), ct * 4 + q, us[q][:], "a_us%d_%d" % (p_, q),
                              self.pcol(nm, ct), oc(nm, ct))
                        yield
                    r_s, k_s, v_s, g_s = us
                    K = lambda i: "a_t%d_%d" % (p_, i)
                    fw.mm(PSp[0][:], w2b[:, 0, cs], twl[:], reads=[("a_w2b", 0), "a_twl"], writes=[pk(0)])
                    yield
                    fw.act(t_[0][:], PSp[0][:], AF.Sigmoid, bias=self.pcol("w0", ct), reads=[pk(0), "prm"], writes=[K(0)])
                    yield
                    fw.v("tensor_scalar", t_[0][:], t_[0][:], -CDEC, 0.0, ALU.mult, ALU.add, reads=[K(0)], writes=[K(0)], eng="gpsimd")
                    yield
                    fw.mm(PSp[1][:], a2b[:, 0, cs], alb[:], reads=[("a_a2b", 0), "a_alb"], writes=[pk(1)])
                    yield
                    fw.act(t_[1][:], PSp[1][:], AF.Sigmoid, bias=self.pcol("a0", ct), reads=[pk(1), "prm"], writes=[K(1)])
                    yield
                    fw.v("tensor_scalar", t_[2][:], k_s[:], self.pcol("k_k", ct), None, ALU.mult, reads=["a_us%d_1" % p_, "prm"], writes=[K(2)])
                    yield
                    fw.v("tensor_tensor", t_[3][:], t_[2][:], t_[2][:], ALU.mult, reads=[K(2)], writes=[K(3)], eng="gpsimd")
                    yield
                    fw.mm(PSp[2][:], self.blk1[:], t_[3][:], reads=["k_blk1", K(3)], writes=[pk(2)])
                    yield
                    fw.act(t_[3][:], PSp[2][:], AF.Sqrt, bias=self.tiny_col[:, 0:1], reads=[pk(2), "tiny"], writes=[K(3)])
                    yield
                    fw.v("reciprocal", t_[3][:], t_[3][:], reads=[K(3)], writes=[K(3)])
                    yield
                    fw.v("tensor_tensor", t_[2][:], t_[2][:], t_[3][:], ALU.mult, reads=[K(2), K(3)], writes=[K(2)])
                    yield
                    fw.v("tensor_scalar", t_[3][:], t_[1][:], self.pcol("k_a", ct), oc("k_a", ct), ALU.mult, ALU.add,
                         reads=[K(1), "prm", "a_omu"], writes=[K(3)])
                    yield
                    fw.v("tensor_tensor", t_[3][:], t_[3][:], k_s[:], ALU.mult, reads=[K(3), "a_us%d_1" % p_], writes=[K(3)], eng="gpsimd")
                    yield
                    fw.v("tensor_tensor", t_[4][:], t_[2][:], t_[1][:], ALU.mult, reads=[K(2), K(1)], writes=[K(4)], eng="gpsimd")
                    yield
                    fw.v("tensor_tensor_scan", t_[5][:], self.rstm[:], t_[0][:], 0.0, ALU.mult, ALU.add,
                         reads=["k_rstm", K(0)], writes=[K(5)])
                    yield
                    fw.v("tensor_tensor", t_[6][:], t_[5][:], t_[0][:], ALU.subtract, reads=[K(5), K(0)], writes=[K(6)], eng="gpsimd")
                    yield
                    fw.act(t_[6][:], t_[6][:], AF.Exp, reads=[K(6)], writes=[K(6)])
                    yield
                    fw.act(t_[7][:], t_[5][:], AF.Exp, scale=-1.0, reads=[K(5)], writes=[K(7)])
                    yield
                    fw.act(t_[5][:], t_[5][:], AF.Exp, reads=[K(5)], writes=[K(5)])
                    yield
                    fw.v("tensor_copy", PC[:, ct, :], t_[5][:].rearrange("p (c t) -> p c t", t=128)[:, :, 127], reads=[K(5)],
                         writes=["a_PC"], eng="gpsimd")
                    yield
                    v3 = lambda ap: ap.rearrange("p (c t) -> p c t", t=128)
                    akey = "a_art%d" % ct
                    fw.v("scalar_tensor_tensor", art[ct][:, :, 0, :], v3(t_[2][:]), -1.0, v3(t_[6][:]), ALU.mult, ALU.mult,
                         reads=[K(2), K(6)], writes=[akey])
                    yield
                    fw.v("tensor_tensor", art[ct][:, :, 1, :], v3(r_s[:]), v3(t_[5][:]), ALU.mult, reads=["a_us%d_0" % p_, K(5)], writes=[akey])
                    yield
                    fw.v("tensor_tensor", bk[ct][:, 0, :], t_[4][:], t_[7][:], ALU.mult, reads=[K(4), K(7)], writes=["a_bk%d" % ct])
                    yield
                    fw.v("tensor_tensor", bk[ct][:, 1, :], t_[3][:], t_[7][:], ALU.mult, reads=[K(3), K(7)], writes=["a_bk%d" % ct], eng="gpsimd")
                    yield
                    fw.v("tensor_copy", vb[ct][:], v_s[:], reads=["a_us%d_2" % p_], writes=["a_vb%d" % ct], eng="gpsimd")
                    yield
                    fw.v("scalar_tensor_tensor", t_[4][:], r_s[:], self.pcol("r_k", ct), t_[3][:], ALU.mult, ALU.mult,
                         reads=["a_us%d_0" % p_, "prm", K(3), K(4)], writes=[K(4)])
                    yield
                    fw.mm(PSp[3][:], self.blk1[:], t_[4][:], reads=["k_blk1", K(4)], writes=[pk(3)])
                    yield
                    fw.v("tensor_tensor", bonus[ct][:], PSp[3][:], v_s[:], ALU.mult, reads=[pk(3), "a_us%d_2" % p_], writes=["a_bonus%d" % ct])
                    yield
                    fw.act(sgt[ct][:], g_s[:], AF.Silu, reads=["a_us%d_3" % p_], writes=["a_sg%d" % ct])
                    yield
                    for half in range(2):
                        psb, pkey = (psb0, pk(0)) if half == 0 else (psb1, pk(1))
                        for cc in range(2):
                            c = half * 2 + cc
                            for qi_, (src, skey) in enumerate([(bk[ct][:, 0, c * 128:(c + 1) * 128], "a_bk%d" % ct),
                                                               (bk[ct][:, 1, c * 128:(c + 1) * 128], "a_bk%d" % ct),
                                                               (vb[ct][:, c * 128:(c + 1) * 128], "a_vb%d" % ct)]):
                                o = (cc * 3 + qi_) * 128
                                fw.tr(psb[:, o:o + 128], src, self.ident_b[:], reads=[skey, "k_ident"], writes=[pkey])
                                yield
                        fw.act(tok[ct][:, half * 2:half * 2 + 2, :, :].rearrange("p a b c -> p (a b c)"), psb[:, 0:768], AF.Copy,
                               reads=[pkey], writes=["a_tok%d" % ct])
                        yield

                fw.lockstep([abody(0), abody(1)])
                fw.lockstep([abody(2), abody(3)])
                PSALL = self.PSALL
                idb4 = self.ident_b[:].unsqueeze(1).to_broadcast([128, 4, 128])
                mui2 = self.mask_ui[:].unsqueeze(1).to_broadcast([128, 2, 256])
                msl4 = self.mask_sl[:].unsqueeze(1).to_broadcast([128, 4, 128])
                for c in range(4):
                    ccols = slice(c * 128, (c + 1) * 128)

                    def hv(qd, hi):
                        h = 2 * hi + qd
                        ct, hp = hi, qd
                        pr_ = slice(hp * 64, hp * 64 + 64)
                        d = dict(h=h, ct=ct, hp=hp, pr=pr_, po=hp * 64,
                                 at=art[ct][pr_, c, 0, :], rt=art[ct][pr_, c, 1, :],
                                 ar=art[ct][pr_, c, :, :].rearrange("p a t -> p (a t)"),
                                 bt=bk[ct][pr_, 0, ccols], kt=bk[ct][pr_, 1, ccols],
                                 rk=["a_art%d" % ct, "a_bk%d" % ct], tkey="a_tok%d" % ct,
                                 vt=tok[ct][:, c, 2, hp * 64:hp * 64 + 64], btk=tok[ct][:, c, 0, hp * 64:hp * 64 + 64],
                                 ktk=tok[ct][:, c, 1, hp * 64:hp * 64 + 64], T0b=Tb[pr_, ct, :])
                        return d

                    XYk = lambda qd: ["ps%d" % (3 * qd), "ps%d" % (3 * qd + 1)]
                    Zk = lambda qd: ["ps%d" % (3 * qd + 2)]
                    XY = lambda qd: PSALL[:, 3 * qd:3 * qd + 2, :].rearrange("p b (h x) -> p (b h) x", x=256)
                    Zv = lambda qd: PSALL[:, 3 * qd + 2, :].rearrange("p (h x) -> p h x", x=128)
                    for qd in range(2):
                        for hi in range(4):
                            d = hv(qd, hi)
                            fw.mm(XY(qd)[:, hi, :], d["bt"], d["ar"], reads=d["rk"], writes=[XYk(qd)[hi // 2]])
                            fw.mm(Zv(qd)[:, hi, :], d["at"], d["bt"], reads=d["rk"], writes=Zk(qd))
                    for qd in range(2):
                        for b2 in range(2):
                            fw.v("tensor_tensor", LAb[qd][:, 2 * b2:2 * b2 + 2, :], XY(qd)[:, 2 * b2:2 * b2 + 2, :], mui2, ALU.mult,
                                 reads=[XYk(qd)[b2], "k_mask_ui"], writes=["a_LAb%d" % qd])
                        fw.v("tensor_tensor", Lb[qd][:], Zv(qd), msl4, ALU.mult, reads=Zk(qd) + ["k_mask_sl"], writes=["a_Lb%d" % qd])
                        fw.v("tensor_tensor", XT[qd][0][:], LAb[qd][:, :, 0:128], idb4, ALU.add,
                             reads=["a_LAb%d" % qd, "k_ident"], writes=["a_XT%d_0" % qd], eng="gpsimd")
                    for k in range(1, 8):
                        for qd in range(2):
                            if k == 1:
                                Pp, PTp, pkeys = (lambda hi: Lb[qd][:, hi, :]), (lambda hi: LAb[qd][:, hi, 0:128]), ["a_Lb%d" % qd, "a_LAb%d" % qd]
                            else:
                                pb_ = PPb[qd][(k - 1) % 2]
                                Pp, PTp, pkeys = (lambda hi, pb_=pb_: pb_[:, hi, 0:128]), (lambda hi, pb_=pb_: pb_[:, hi, 128:256]), ["a_PPb%d_%d" % (qd, (k - 1) % 2)]
                            for hi in range(4):
                                if k <= 6:
                                    fw.mm(XY(qd)[:, hi, 0:128], PTp(hi), Pp(hi), reads=pkeys, writes=[XYk(qd)[hi // 2]])
                                    fw.mm(XY(qd)[:, hi, 128:256], Pp(hi), PTp(hi), reads=pkeys, writes=[XYk(qd)[hi // 2]])
                                if k == 7:
                                    d7 = hv(qd, hi)
                                    fw.mm(XY(qd)[:, hi, :], d7["kt"], d7["ar"], reads=d7["rk"], writes=[XYk(qd)[hi // 2]])
                                if k >= 2:
                                    xo = XT[qd][(k - 2) % 2]
                                    xok = "a_XT%d_%d" % (qd, (k - 2) % 2)
                                    fw.mm(Zv(qd)[:, hi, :], self.ident_b[:], xo[:, hi, :], start=True, stop=False, reads=["k_ident", xok], writes=Zk(qd))
                                    fw.mm(Zv(qd)[:, hi, :], Pp(hi), xo[:, hi, :], start=False, stop=True, reads=pkeys + [xok], writes=Zk(qd))
                        for qd in range(2):
                            if k <= 6:
                                for b2 in range(2):
                                    fw.act(PPb[qd][k % 2][:, 2 * b2:2 * b2 + 2, :], XY(qd)[:, 2 * b2:2 * b2 + 2, :], AF.Copy,
                                           reads=[XYk(qd)[b2]], writes=["a_PPb%d_%d" % (qd, k % 2)])
                            if k >= 2:
                                fw.v("tensor_copy", XT[qd][(k - 1) % 2][:], Zv(qd), reads=Zk(qd), writes=["a_XT%d_%d" % (qd, (k - 1) % 2)])
                            if k == 7:
                                for b2 in range(2):
                                    fw.v("tensor_tensor", KAb[qd][:, 2 * b2:2 * b2 + 2, :], XY(qd)[:, 2 * b2:2 * b2 + 2, :], mui2, ALU.mult,
                                         reads=[XYk(qd)[b2], "k_mask_ui"], writes=["a_KAb%d" % qd])
                    XTf = [XT[qd][0] for qd in range(2)]
                    xfk = ["a_XT%d_0" % qd for qd in range(2)]
                    Wv = lambda qd: PSALL[:, 3 * qd + 2, 0:256].rearrange("p (h x) -> p h x", x=64)
                    Uv = lambda qd: PSALL[:, 3 * qd + 2, 256:512].rearrange("p (h x) -> p h x", x=64)
                    for qd in range(2):
                        for hi in range(4):
                            d = hv(qd, hi)
                            fw.mm(Wv(qd)[:, hi, :], d["at"], d["T0b"], start=True, stop=False, reads=["a_art%d" % d["ct"], "a_Tb"], writes=Zk(qd))
                            fw.mm(Wv(qd)[:, hi, :], KAb[qd][:, hi, 0:128], d["vt"], start=False, stop=True, reads=["a_KAb%d" % qd, d["tkey"]], writes=Zk(qd))
                        fw.v("tensor_copy", Wb[qd][:], Wv(qd), reads=Zk(qd), writes=["a_Wb%d" % qd])
                    for qd in range(2):
                        for hi in range(4):
                            fw.mm(Uv(qd)[:, hi, :], XTf[qd][:, hi, :], Wb[qd][:, hi, :], reads=[xfk[qd], "a_Wb%d" % qd], writes=Zk(qd))
                        fw.v("tensor_copy", Ub[qd][:], Uv(qd), reads=Zk(qd), writes=["a_Ub%d" % qd])
                    for qd in range(2):
                        for hi in range(4):
                            d = hv(qd, hi)
                            h, ct = d["h"], d["ct"]
                            ob, okey = (PS[6], "ps6") if qd == 0 else (PS[7], "ps7")
                            osl = slice(qd * 256 + hi * 64, qd * 256 + (hi + 1) * 64)
                            fw.mm(ob[:, osl], d["rt"], d["T0b"], start=True, stop=False, reads=["a_art%d" % ct, "a_Tb"], writes=[okey])
                            fw.mm(ob[:, osl], LAb[qd][:, hi, 128:256], Ub[qd][:, hi, :], start=False, stop=False,
                                  reads=["a_LAb%d" % qd, "a_Ub%d" % qd], writes=[okey])
                            fw.mm(ob[:, osl], KAb[qd][:, hi, 128:256], d["vt"], start=False, stop=True, reads=["a_KAb%d" % qd, d["tkey"]], writes=[okey])
                            zsl = slice(ct * 64, (ct + 1) * 64)
                            fw.mm(PS[7][d["pr"], zsl], d["btk"], Ub[qd][:, hi, :], start=True, stop=False, reads=[d["tkey"], "a_Ub%d" % qd], writes=["ps7"])
                            fw.mm(PS[7][d["pr"], zsl], d["ktk"], d["vt"], start=False, stop=True, reads=[d["tkey"]], writes=["ps7"])
                    zall = PS[7][:, 0:256].rearrange("p (c i) -> p c i", i=64)
                    fw.v("tensor_tensor", T[:], T[:], zall, ALU.add, reads=["a_T", "ps7"], writes=["a_T"])
                    fw.v("tensor_tensor", T[:], T[:], PC[:, :, c:c + 1].to_broadcast([128, 4, 64]), ALU.mult, reads=["a_T", "a_PC"], writes=["a_T"])
                    fw.v("tensor_copy", Tb[:], T[:], reads=["a_T"], writes=["a_Tb"], eng="gpsimd")
                    ov = [PS[6][:, 0:256].rearrange("p (h i) -> p h i", i=64), PS[7][:, 256:512].rearrange("p (h i) -> p h i", i=64)]
                    okeys = ["ps6", "ps7"]
                    for qd in range(2):
                        fw.v("tensor_reduce", st8[:, 0, qd * 4:qd * 4 + 4], ov[qd], AX.X, ALU.add, reads=[okeys[qd]], writes=["a_st8"])
                    fw.v("tensor_scalar", st8[:, 0, :], st8[:, 0, :], 1.0 / 64, None, ALU.mult, reads=["a_st8"], writes=["a_st8"])
                    for qd in range(2):
                        fw.v("tensor_tensor", xc[:, qd * 4:qd * 4 + 4, :], ov[qd],
                             st8[:, 0, qd * 4:qd * 4 + 4].unsqueeze(2).to_broadcast([128, 4, 64]), ALU.subtract,
                             reads=[okeys[qd], "a_st8"], writes=["a_xc"])
                    fw.v("tensor_tensor", sq[:], xc[:], xc[:], ALU.mult, reads=["a_xc"], writes=["a_sq"], eng="gpsimd")
                    fw.v("tensor_reduce", st8[:, 1, :], sq[:], AX.X, ALU.add, reads=["a_sq"], writes=["a_st8"])
                    fw.act(st8[:, 1, :], st8[:, 1, :], AF.Sqrt, bias=self.gneps_col[:, 0:1], scale=1.0 / 64, reads=["a_st8", "tiny"], writes=["a_st8"])
                    fw.v("reciprocal", st8[:, 1, :], st8[:, 1, :], reads=["a_st8"], writes=["a_st8"])
                    fw.v("tensor_tensor", onb[:].rearrange("p (h i) -> p h i", i=64), xc[:],
                         st8[:, 1, :].unsqueeze(2).to_broadcast([128, 8, 64]), ALU.mult, reads=["a_xc", "a_st8"], writes=["a_onb"])
                    for ct in range(4):
                        for hp in range(2):
                            qo = (hp * 4 + ct) * 64
                            fw.tr(psb0[hp * 64:(hp + 1) * 64, ct * 128:(ct + 1) * 128], onb[:, qo:qo + 64], self.ident_b[:],
                                  reads=["a_onb", "k_ident"], writes=["ps0"])
                    for ct in range(4):
                        j = ct % 2
                        fw.v("tensor_scalar", yv[j][:], psb0[:, ct * 128:(ct + 1) * 128], self.pcol("gn_g", ct), self.pcol("gn_b", ct),
                             ALU.mult, ALU.add, reads=["ps0", "prm"], writes=["a_yv%d" % j])
                        fw.v("tensor_tensor", yv[j][:], yv[j][:], bonus[ct][:, ccols], ALU.add, reads=["a_yv%d" % j, "a_bonus%d" % ct],
                             writes=["a_yv%d" % j], eng="gpsimd")
                        fw.v("tensor_tensor", yout[:, ct, ccols], yv[j][:], sgt[ct][:, ccols], ALU.mult,
                             reads=["a_yv%d" % j, "a_sg%d" % ct], writes=["a_yout"], eng="gpsimd")
                fw.dma(self.yT[0][:, :, c0:c0 + 512].rearrange("k p s -> p k s"), yout[:], reads=["a_yout"],
                       writes=[("yT0", g, ct) for ct in range(4)], eng="gpsimd")
            self.release(keys)


    def phase_R(self):
        fw, S, G = self.fw, self.S, self.G
        TWO_PI = 6.283185307179586
        C1 = 6.28125
        C2 = 0.0019350051879882812
        C3 = TWO_PI - C1 - C2
        PI = 3.1415925
        with ExitStack() as ph:
            sb = lambda n, s, d: ph.enter_context(self.nc.sbuf_tensor(self.uname(n), list(s), d))
            posi = sb("r_posi", [128, 512], I32)
            a = sb("r_a", [128, 512], F32)
            k = sb("r_k", [128, 512], F32)
            r = sb("r_r", [128, 512], F32)
            r2 = sb("r_r2", [128, 512], F32)
            m = sb("r_m", [128, 512], F32)
            cs = sb("r_cs", [128, 2, 512], F32)
            keys = ["r_posi", "r_a", "r_k", "r_r", "r_r2", "r_m", "r_cs"]
            self.acquire(keys)
            for g in range(G):
                c0 = g * 512
                fw.dma(posi[:], self.pos[0:1, c0:c0 + 512].to_broadcast([128, 512]), writes=["r_posi"])
                fw.v("tensor_copy", a[:], posi[:], reads=["r_posi"], writes=["r_a"])
                fw.v("tensor_scalar", a[:], a[:], self.cst_sb[:, 0:1], None, ALU.mult, reads=["r_a", "cst"], writes=["r_a"])
                fw.v("tensor_scalar", k[:], a[:], 1.0 / TWO_PI, None, ALU.mult, reads=["r_a"], writes=["r_k"])
                fw.v("tensor_scalar", k[:], k[:], 12582912.0, None, ALU.add, reads=["r_k"], writes=["r_k"])
                fw.v("tensor_scalar", k[:], k[:], 12582912.0, None, ALU.subtract, reads=["r_k"], writes=["r_k"])
                fw.v("scalar_tensor_tensor", r[:], k[:], -C1, a[:], ALU.mult, ALU.add, reads=["r_k", "r_a"], writes=["r_r"])
                fw.v("scalar_tensor_tensor", r[:], k[:], -C2, r[:], ALU.mult, ALU.add, reads=["r_k", "r_r"], writes=["r_r"])
                fw.v("scalar_tensor_tensor", r[:], k[:], -C3, r[:], ALU.mult, ALU.add, reads=["r_k", "r_r"], writes=["r_r"])
                fw.v("tensor_scalar", r[:], r[:], PI, -PI, ALU.min, ALU.max, reads=["r_r"], writes=["r_r"])
                fw.v("tensor_scalar", r2[:], r[:], TWO_PI / 4, None, ALU.add, reads=["r_r"], writes=["r_r2"])
                fw.v("tensor_scalar", m[:], r2[:], PI, -TWO_PI, ALU.is_gt, ALU.mult, reads=["r_r2"], writes=["r_m"])
                fw.v("tensor_tensor", r2[:], r2[:], m[:], ALU.add, reads=["r_r2", "r_m"], writes=["r_r2"])
                fw.v("tensor_scalar", r2[:], r2[:], PI, -PI, ALU.min, ALU.max, reads=["r_r2"], writes=["r_r2"])
                fw.act(cs[:, 0, :], r2[:], AF.Sin, reads=["r_r2"], writes=["r_cs"])
                fw.act(cs[:, 1, :], r[:], AF.Sin, reads=["r_r"], writes=["r_cs"])
                fw.dma(self.ropeT[:, :, c0:c0 + 512].rearrange("k p s -> p k s"), cs[:], reads=["r_cs"], writes=[("ropeT", g)], eng="gpsimd")
            self.release(keys)

    def phase_B(self, l):
        fw, S, G = self.fw, self.S, self.G
        PS = self.PS
        NT = S // 128
        NIT = 20
        NOATT = False
        with ExitStack() as ph:
            allkeys = []

            def sb(n, s, d):
                allkeys.append(n)
                return ph.enter_context(self.nc.sbuf_tensor(self.uname(n), list(s), d))

            wB = sb("wB", [128, 8, 2372], BF16)
            wkd = sb("b_wkd", [128, 8, 128], BF16)
            ropeR = sb("b_ropeR", [128, 1, 128], BF16)
            KT = [sb("b_KT%d" % ct, [128, S], BF16) for ct in range(4)]
            KI = sb("b_KI", [128, S], BF16)
            V = sb("b_V", [128, NT, 8, 65], BF16)
            hn = sb("b_hn", [128, 8, 512], BF16)
            QT = [[sb("b_QT%d_%d" % (ct, i), [128, 512], BF16) for ct in range(4)] for i in range(2)]
            QI = [[sb("b_QI%d_%d" % (j, i), [128, 512], BF16) for j in range(2)] for i in range(2)]
            SG = [[sb("b_SG%d_%d" % (ct, i), [128, 512], BF16) for ct in range(4)] for i in range(2)]
            WI = [sb("b_WI%d" % i, [128, 4, 4], F32) for i in range(2)]
            yout = [sb("b_yout0", [128, 4, 512], BF16)] * 2
            score = sb("b_score", [128, S], F32)
            alias = S >= 4096
            if alias:
                xL = [score[:, 0:512], score[:, 1280:1792]]
                x2L = [score[:, 512:1024], score[:, 1792:2304]]
                xbL = [score[:, 1024:1280].bitcast(BF16), score[:, 2304:2560].bitcast(BF16)]
                cs = score[:, 2560:3584].rearrange("p (a b) -> p a b", b=512)
            else:
                cs = sb("b_cs", [128, 2, 512], F32)
                xL = [sb("b_x%d" % i, [128, 512], F32) for i in range(2)]
                x2L = [sb("b_x2%d" % i, [128, 512], F32) for i in range(2)]
                xbL = [sb("b_xb%d" % i, [128, 512], BF16) for i in range(2)]
            tkeys = ["b_cs"] + ["b_x%d" % i for i in range(2)] + ["b_x2%d" % i for i in range(2)] + ["b_xb%d" % i for i in range(2)]
            mm1 = [sb("b_mm1_0", [128, S], BF16)] * 2
            MT = [sb("b_MT%d" % i, [128, NT, 128], BF16) for i in range(2)]
            E = [sb("b_E%d" % i, [128, 512], BF16) for i in range(4)]
            rl = [sb("b_rl%d" % i, [128, 512], F32) for i in range(2)]
            PT = [sb("b_PT%d" % i, [128, 512], BF16) for i in range(4)]
            bs = sb("b_bs", [128, 8], F32)
            steps = sb("b_steps", [128, NIT + 1], F32)
            rec = sb("b_rec", [128, 8], F32)
            otok = sb("b_otok", [128, 8, 64], BF16)
            dmask = sb("b_dmask", [128, 128], F32)
            keys = allkeys + tkeys + [("wB", k) for k in range(8)] + [("b_wkd", k) for k in range(8)] + [("b_ropeR", 0)] + \
                [("b_KT", ct, g) for ct in range(4) for g in range(G)] + [("b_KI", g) for g in range(G)] + [("b_V", g) for g in range(G)]
            self.acquire(keys + ["stg0", "stg1"])
            self.load_w(wB, "wB", lambda k: self.w_in[l, k * 128:(k + 1) * 128, OFF_B:OFF_B + 2372], 2372, 8, self.gcol)
            self.load_w(wkd, "b_wkd", lambda k: self.w_kidup[l, k * 128:(k + 1) * 128, :], 128, 8, self.gcol)
            self.load_w(ropeR, "b_ropeR", lambda k: self.ropeR_d, 128, 1)
            fw.v("memset", V[:], 1.0, writes=[("b_V", g) for g in range(G)], eng="gpsimd")
            fw.v("memset", dmask[:], 0.0, writes=["b_dmask"], eng="gpsimd")
            fw.v("memset", dmask[0:64, 64:128], -1e30, writes=["b_dmask"], eng="gpsimd")

            def lane(p, chains):
                x_, x2, xb = xL[p], x2L[p], xbL[p]
                kx, kx2, kxb = "b_x%d" % p, "b_x2%d" % p, "b_xb%d" % p
                ps, pk = PS[p], "ps%d" % p

                def proj(w, wkey, c_lo, c_hi):
                    for k in range(8):
                        fw.mm(ps[:], w[:, k, c_lo:c_hi], hn[:, k, :], start=(k == 0), stop=(k == 7), reads=[(wkey, k), "b_hn"], writes=[pk])
                        yield

                def rope(dst, dkey):
                    fw.v("tensor_copy", xb[:], x_[:], reads=[kx], writes=[kxb], eng="gpsimd")
                    yield
                    fw.mm(ps[:], ropeR[:, 0, :], xb[:], reads=[("b_ropeR", 0), kxb], writes=[pk])
                    yield
                    fw.v("tensor_tensor", x2[:], x_[:], cs[:, 0, :], ALU.mult, reads=[kx, "b_cs"], writes=[kx2], eng="gpsimd")
                    yield
                    fw.v("tensor_tensor", x_[:], ps[:], cs[:, 1, :], ALU.mult, reads=[pk, "b_cs", kx], writes=[kx])
                    yield
                    fw.v("tensor_tensor", dst, x2[:], x_[:], ALU.add, reads=[kx2, kx], writes=[dkey], eng="gpsimd")
                    yield

                for ch in chains:
                    kind = ch[0]
                    if kind in ("q", "k"):
                        _, ct, g = ch
                        gp = g % 2
                        gc = slice(g * 512, g * 512 + 512)
                        coff, gname = (0, "q_g") if kind == "q" else (512, "k_g")
                        yield from proj(wB, "wB", coff + ct * 128, coff + (ct + 1) * 128)
                        fw.act(x_[:], ps[:], AF.Copy, reads=[pk], writes=[kx])
                        yield
                        fw.v("tensor_tensor", x2[:], x_[:], x_[:], ALU.mult, reads=[kx], writes=[kx2], eng="gpsimd")
                        yield
                        fw.mm(ps[:], self.blk1[:], x2[:], reads=["k_blk1", kx2], writes=[pk])
                        yield
                        fw.act(x2[:], ps[:], AF.Sqrt, bias=self.eps6_col[:, 0:1], scale=1.0 / 64, reads=[pk, "tiny"], writes=[kx2])
                        yield
                        fw.v("reciprocal", x2[:], x2[:], reads=[kx2], writes=[kx2])
                        yield
                        fw.v("scalar_tensor_tensor", x_[:], x_[:], self.pcol(gname, 0), x2[:], ALU.mult, ALU.mult,
                             reads=[kx, "prm", kx2], writes=[kx])
                        yield
                        if kind == "q":
                            yield from rope(QT[gp][ct][:], "b_QT%d_%d" % (ct, gp))
                        else:
                            yield from rope(KT[ct][:, gc], ("b_KT", ct, g))
                    elif kind == "qi":
                        _, j, g = ch
                        gp = g % 2
                        yield from proj(wB, "wB", 1536 + j * 128, 1536 + (j + 1) * 128)
                        fw.act(x_[:], ps[:], AF.Copy, reads=[pk], writes=[kx])
                        yield
                        yield from rope(QI[gp][j][:], "b_QI%d_%d" % (j, gp))
                    elif kind == "ki":
                        _, g = ch
                        gc = slice(g * 512, g * 512 + 512)
                        yield from proj(wkd, "b_wkd", 0, 128)
                        fw.act(x_[:], ps[:], AF.Copy, reads=[pk], writes=[kx])
                        yield
                        yield from rope(KI[:, gc], ("b_KI", g))
                    elif kind == "sg":
                        _, ct, g = ch
                        gp = g % 2
                        yield from proj(wB, "wB", 1860 + ct * 128, 1860 + (ct + 1) * 128)
                        fw.act(SG[gp][ct][:], ps[:], AF.Silu, reads=[pk], writes=["b_SG%d_%d" % (ct, gp)])
                        yield
                    elif kind == "v":
                        _, tt, g = ch
                        gp = g % 2
                        tcols = slice(tt * 128, (tt + 1) * 128)
                        for k in range(8):
                            fw.mm(ps[:], hn[:, k, tcols], wB[:, k, 1024:1536], start=(k == 0), stop=(k == 7),
                                  reads=[("wB", k), "b_hn"], writes=[pk])
                            yield
                        fw.act(V[:, g * 4 + tt, :, 0:64], ps[:].rearrange("p (h i) -> p h i", i=64), AF.Copy, reads=[pk], writes=[("b_V", g)])
                        yield
                        for k in range(8):
                            fw.mm(ps[:, 0:4], hn[:, k, tcols], wB[:, k, 1856:1860], start=(k == 0), stop=(k == 7),
                                  reads=[("wB", k), "b_hn"], writes=[pk])
                            yield
                        fw.v("tensor_scalar", WI[gp][:, tt, :], ps[:, 0:4], 1.0 / 16, None, ALU.mult, reads=[pk], writes=["b_WI%d" % gp])
                        yield

            def prep_begin(g):
                gc = slice(g * 512, g * 512 + 512)
                if alias:
                    self.release(["b_score"])
                    self.acquire(tkeys)
                fw.dma(hn[:], self.hnT[:, :, gc].rearrange("k p s -> p k s"), reads=[("hnT", g)], writes=["b_hn"])
                fw.dma(cs[:], self.ropeT[:, :, gc].rearrange("k p s -> p k s"), reads=[("ropeT", g)], writes=["b_cs"])

            def prep_lanes(g):
                chains = []
                for ct in range(4):
                    chains += [("q", ct, g), ("k", ct, g)]
                chains += [("qi", 0, g), ("qi", 1, g), ("ki", g)]
                chains += [("sg", ct, g) for ct in range(4)]
                chains += [("v", tt, g) for tt in range(4)]
                return [lane(0, chains[0::2]), lane(1, chains[1::2])]

            def prep_end(g):
                if alias:
                    self.release(tkeys)
                    self.acquire(["b_score"])

            def scores(qt):
                g, tt = qt // 4, qt % 4
                gp = g % 2
                N = (qt + 1) * 128
                tq = slice(tt * 128, (tt + 1) * 128)
                for pc in range((N + 511) // 512):
                    p0 = pc * 512
                    pn = min(512, N - p0)
                    for ih in range(4):
                        po = (ih % 2) * 64
                        fw.mm(PS[ih][:, 0:pn], QI[gp][ih // 2][po:po + 64, tq], KI[po:po + 64, p0:p0 + pn],
                              reads=["b_QI%d_%d" % (ih // 2, gp), ("b_KI", pc)], writes=["ps%d" % ih])
                    for ih in range(4):
                        r_ = rl[ih % 2]
                        rkey = "b_rl%d" % (ih % 2)
                        fw.act(r_[:, 0:pn], PS[ih][:, 0:pn], AF.Relu, reads=["ps%d" % ih], writes=[rkey])
                        if ih == 0:
                            fw.v("tensor_scalar", score[:, p0:p0 + pn], r_[:, 0:pn], WI[gp][:, tt, 0:1], None, ALU.mult,
                                 reads=[rkey, "b_WI%d" % gp], writes=["b_score"])
                        else:
                            fw.v("scalar_tensor_tensor", score[:, p0:p0 + pn], r_[:, 0:pn], WI[gp][:, tt, ih:ih + 1], score[:, p0:p0 + pn],
                                 ALU.mult, ALU.add, reads=[rkey, "b_WI%d" % gp, "b_score"], writes=["b_score"])

            def bisect_mask(qt):
                NB = qt + 1
                N = NB * 128
                mk = mm1[0]
                mkey = "b_mm1_0"
                A, lo, mid, cnt, tmp = (bs[:, i:i + 1] for i in range(5))
                if NB >= 3:
                    fw.v("tensor_reduce", A, score[:, 0:N], AX.X, ALU.max, apply_absolute_value=True, reads=["b_score"], writes=["b_bs"])
                    fw.v("tensor_scalar", A, A, 1.0001, 1e-20, ALU.mult, ALU.add, reads=["b_bs"], writes=["b_bs"])
                fw.v("tensor_tensor", score[:, N - 128:N], score[:, N - 128:N], dmask[:], ALU.add, reads=["b_score", "b_dmask"],
                     writes=["b_score"])
                if NB >= 3:
                    fw.v("tensor_scalar", steps[:], self.cst_sb[:, 1:2 + NIT], A, None, ALU.mult, reads=["cst", "b_bs"], writes=["b_steps"])
                    fw.v("tensor_scalar", mid, A, -1.0, steps[:, 0:1], ALU.mult, ALU.add, reads=["b_bs", "b_steps"], writes=["b_bs"])
                    for it in range(NIT):
                        fw.v("tensor_scalar", mk[:, 0:N], score[:, 0:N], mid, None, ALU.is_ge, ALU.add, accum_out=cnt,
                             reads=["b_score", "b_bs", mkey], writes=[mkey, "b_bs"])
                        fw.v("tensor_scalar", tmp, cnt, 255.5, steps[:, it:it + 1], ALU.is_ge, ALU.mult, reads=["b_bs", "b_steps"], writes=["b_bs"])
                        fw.v("scalar_tensor_tensor", mid, tmp, steps[:, it + 1:it + 2], mid, ALU.subtract, ALU.add,
                             reads=["b_bs", "b_steps"], writes=["b_bs"])
                    fw.v("tensor_tensor", lo, mid, steps[:, NIT:NIT + 1], ALU.subtract, reads=["b_bs", "b_steps"], writes=["b_bs"])
                else:
                    fw.v("memset", lo, -1e29, writes=["b_bs"])
                fw.v("tensor_scalar", mk[:, 0:N], score[:, 0:N], lo, None, ALU.is_ge, reads=["b_score", "b_bs"], writes=[mkey])
                psb1 = PS[1][:].bitcast(BF16)
                mt, mtkey = MT[qt % 2], "b_MT%d" % (qt % 2)
                for kb0 in range(0, NB, 8):
                    nk = min(8, NB - kb0)
                    for j in range(nk):
                        kb = kb0 + j
                        fw.tr(psb1[:, j * 128:(j + 1) * 128], mk[:, kb * 128:(kb + 1) * 128], self.ident_b[:],
                              reads=[mkey, "k_ident"], writes=["ps1"])
                    fw.act(mt[:, kb0:kb0 + nk, :].rearrange("p a b -> p (a b)"), psb1[:, 0:nk * 128], AF.Copy, reads=["ps1"], writes=[mtkey])

            def attention_gen(qt):
                if NOATT:
                    return
                g, tt = qt // 4, qt % 4
                gp = g % 2
                NB = qt + 1
                tq = slice(tt * 128, (tt + 1) * 128)
                mt, mtkey = MT[qt % 2], "b_MT%d" % (qt % 2)
                for hpair in range(4):
                    ct = hpair
                    for gi, kb0 in enumerate(range(0, NB, 4)):
                        nk = min(4, NB - kb0)
                        bis = [2 * e + gi % 2 for e in range(2)]
                        for j in range(nk):
                            kb = kb0 + j
                            for e in range(2):
                                po = e * 64
                                pl = PS[2 + bis[e]]
                                fw.mm(pl[:, j * 128:(j + 1) * 128], KT[ct][po:po + 64, kb * 128:(kb + 1) * 128], QT[gp][ct][po:po + 64, tq],
                                      reads=[("b_KT", ct, kb // 4), "b_QT%d_%d" % (ct, gp)], writes=["ps%d" % (2 + bis[e])])
                                yield
                        for e in range(2):
                            bi = bis[e]
                            fw.act(E[bi][:, 0:nk * 128], PS[2 + bi][:, 0:nk * 128], AF.Exp, scale=0.125, reads=["ps%d" % (2 + bi)], writes=["b_E%d" % bi])
                            yield
                            fw.v("tensor_tensor", PT[bi][:, 0:nk * 128], E[bi][:, 0:nk * 128],
                                 mt[:, kb0:kb0 + nk, :].rearrange("p a b -> p (a b)"), ALU.mult,
                                 reads=["b_E%d" % bi, mtkey], writes=["b_PT%d" % bi], eng="gpsimd")
                            yield
                        for e in range(2):
                            h = 2 * hpair + e
                            bi = bis[e]
                            pob = PS[7] if e == 0 else PS[6]
                            pokey = "ps7" if e == 0 else "ps6"
                            osl = slice(hpair * 65, hpair * 65 + 65)
                            for j in range(nk):
                                kb = kb0 + j
                                fw.mm(pob[:, osl], PT[bi][:, j * 128:(j + 1) * 128], V[:, kb, h, :], start=(kb == 0), stop=(kb == NB - 1),
                                      reads=["b_PT%d" % bi, ("b_V", kb // 4)], writes=[pokey])
                                yield

            def final(qt):
                g, tt = qt // 4, qt % 4
                gp = g % 2
                tq = slice(tt * 128, (tt + 1) * 128)
                otok4 = otok[:].rearrange("p (a e) i -> p a e i", e=2)
                for hb_ in range(2):
                    pob = PS[7] if hb_ == 0 else PS[6]
                    pokey = "ps7" if hb_ == 0 else "ps6"
                    pv = pob[:, 0:260].rearrange("p (h i) -> p h i", i=65)
                    fw.v("reciprocal", rec[:, hb_ * 4:hb_ * 4 + 4], pv[:, :, 64], reads=[pokey], writes=["b_rec"])
                    fw.v("tensor_tensor", otok4[:, :, hb_, :], pv[:, :, 0:64],
                         rec[:, hb_ * 4:hb_ * 4 + 4].unsqueeze(2).to_broadcast([128, 4, 64]), ALU.mult,
                         reads=[pokey, "b_rec"], writes=["b_otok"])
                of = otok[:].rearrange("p h i -> p (h i)")
                for ct in range(4):
                    pb_ = PS[7 - ct // 2][:, 384:512].bitcast(BF16)
                    pkey = "ps%d" % (7 - ct // 2)
                    fw.tr(pb_[:, (ct % 2) * 128:(ct % 2 + 1) * 128], of[:, ct * 128:(ct + 1) * 128], self.ident_b[:],
                          reads=["b_otok", "k_ident"], writes=[pkey])
                for ct in range(4):
                    pb_ = PS[7 - ct // 2][:, 384:512].bitcast(BF16)
                    pkey = "ps%d" % (7 - ct // 2)
                    fw.v("tensor_tensor", yout[gp][:, ct, tq], pb_[:, (ct % 2) * 128:(ct % 2 + 1) * 128], SG[gp][ct][:, tq], ALU.mult,
                         reads=[pkey, "b_SG%d_%d" % (ct, gp)], writes=["b_yout0"])
                if tt == 3:
                    gc = slice(g * 512, g * 512 + 512)
                    fw.dma(self.yT[1][:, :, gc].rearrange("k p s -> p k s"), yout[gp][:], reads=["b_yout0"],
                           writes=[("yT1", g, ct) for ct in range(4)], eng="gpsimd")

            prep_begin(0)
            fw.lockstep(prep_lanes(0))
            prep_end(0)
            scores(0)
            bisect_mask(0)
            for qt in range(NT):
                nxt = qt + 1
                if nxt < NT:
                    if nxt % 4 == 0:
                        gn = nxt // 4
                        prep_begin(gn)
                        fw.lockstep(prep_lanes(gn))
                        prep_end(gn)
                    scores(nxt)
                fw.lockstep([attention_gen(qt)])
                if nxt < NT:
                    bisect_mask(nxt)
                final(qt)
            self.release(keys)

    def phase_C(self, l):
        fw, S, G = self.fw, self.S, self.G
        PS = self.PS
        with ExitStack() as ph:
            sb = lambda n, s, d: ph.enter_context(self.nc.sbuf_tensor(self.uname(n), list(s), d))
            wC = sb("wC", [128, 8, 1024], BF16)
            wr = sb("c_wr", [128, 4, 128], BF16)
            wi = sb("c_wi", [128, 4, 128], BF16)
            hn = [sb("c_hn%d" % i, [128, 8, 512], BF16) for i in range(2)]
            xbuf = sb("c_xbuf", [128, 4, 515], F32)
            hprev = sb("c_hprev", [128, 4], F32)
            cl = sb("c_cl", [128, 4], F32)
            xc = [sb("c_xc%d" % i, [128, 512], F32) for i in range(2)]
            xcb = [sb("c_xcb%d" % i, [128, 512], BF16) for i in range(2)]
            r_ = [sb("c_r%d" % i, [128, 512], F32) for i in range(2)]
            i_ = [sb("c_i%d" % i, [128, 512], F32) for i in range(2)]
            a_ = [sb("c_a%d" % i, [128, 512], F32) for i in range(2)]
            b_ = [sb("c_b%d" % i, [128, 512], F32) for i in range(2)]
            sg = [sb("c_sg%d" % i, [128, 512], F32) for i in range(2)]
            yo = [sb("c_y%d" % i, [128, 512], BF16) for i in range(2)]
            names = ["wC", "c_wr", "c_wi", "c_hn0", "c_hn1", "c_xbuf", "c_hprev", "c_cl"] + \
                    [n + str(i) for n in ("c_xc", "c_xcb", "c_r", "c_i", "c_a", "c_b", "c_sg", "c_y") for i in range(2)]
            keys = names + [("wC", k) for k in range(8)] + [("c_wr", k) for k in range(4)] + [("c_wi", k) for k in range(4)] + \
                ["c_xbuf%d" % i for i in range(4)] + ["c_hprev%d" % i for i in range(4)]
            self.acquire(keys + ["stg0", "stg1"])
            self.load_w(wC, "wC", lambda k: self.w_in[l, k * 128:(k + 1) * 128, OFF_C:OFF_C + 1024], 1024, 8, self.gcol)
            self.load_w(wr, "c_wr", lambda k: self.wr_bd[l, k], 128, 4)
            self.load_w(wi, "c_wi", lambda k: self.wi_bd[l, k], 128, 4)
            fw.act(cl[:], self.prm_sb[:, PCOLS["lam"][0]:PCOLS["lam"][0] + 4], AF.Exp, scale=-1.0, reads=["prm"], writes=["c_cl"])
            fw.act(cl[:], cl[:], AF.Ln, bias=1.0, reads=["c_cl"], writes=["c_cl"])
            fw.v("tensor_scalar", cl[:], cl[:], -8.0, None, ALU.mult, reads=["c_cl"], writes=["c_cl"])
            fw.v("memset", xbuf[:], 0.0, writes=["c_xbuf%d" % i for i in range(4)])
            fw.v("memset", hprev[:], 0.0, writes=["c_hprev%d" % i for i in range(4)])
            for g in range(G):
                c0 = g * 512
                hk = "c_hn%d" % (g % 2)
                hg = hn[g % 2]
                fw.dma(hg[:], self.hnT[:, :, c0:c0 + 512].rearrange("k p s -> p k s"), reads=[("hnT", g)], writes=[hk])
                def cbody(ct, g=g, c0=c0, hk=hk, hg=hg):
                    j = ct % 2
                    pb = 4 * j
                    px, pg, pr, pi = PS[pb], PS[pb + 1], PS[pb + 2], PS[pb + 3]
                    kx, kg, kr, ki = ["ps%d" % (pb + t) for t in range(4)]
                    for k in range(8):
                        fw.mm(px[:], wC[:, k, ct * 128:(ct + 1) * 128], hg[:, k, :], start=(k == 0), stop=(k == 7),
                              reads=[("wC", k), hk], writes=[kx])
                        yield
                    for k in range(8):
                        fw.mm(pg[:], wC[:, k, 512 + ct * 128:512 + (ct + 1) * 128], hg[:, k, :], start=(k == 0), stop=(k == 7),
                              reads=[("wC", k), hk], writes=[kg])
                        yield
                    xb = xbuf[:, ct, :]
                    fw.act(xb[:, 3:515], px[:], AF.Copy, reads=[kx], writes=["c_xbuf%d" % ct])
                    yield
                    cw = lambda i: self.pcol("conv_w", i * 4 + ct)
                    fw.v("tensor_scalar", xc[j][:], xb[:, 3:515], cw(3), self.pcol("conv_b", ct), ALU.mult, ALU.add,
                         reads=["c_xbuf%d" % ct, "prm"], writes=["c_xc%d" % j])
                    yield
                    for i in range(3):
                        fw.v("scalar_tensor_tensor", xc[j][:], xb[:, i:i + 512], cw(i), xc[j][:], ALU.mult, ALU.add,
                             reads=["c_xbuf%d" % ct, "prm", "c_xc%d" % j], writes=["c_xc%d" % j])
                        yield
                    fw.v("tensor_copy", xb[:, 0:3], xb[:, 512:515], reads=["c_xbuf%d" % ct], writes=["c_xbuf%d" % ct], eng="gpsimd")
                    yield
                    fw.v("tensor_copy", xcb[j][:], xc[j][:], reads=["c_xc%d" % j], writes=["c_xcb%d" % j], eng="gpsimd")
                    yield
                    fw.mm(pr[:], wr[:, ct, :], xcb[j][:], reads=[("c_wr", ct), "c_xcb%d" % j], writes=[kr])
                    yield
                    fw.mm(pi[:], wi[:, ct, :], xcb[j][:], reads=[("c_wi", ct), "c_xcb%d" % j], writes=[ki])
                    yield
                    fw.act(r_[j][:], pr[:], AF.Sigmoid, bias=self.pcol("b_r", ct), reads=[kr, "prm"], writes=["c_r%d" % j])
                    yield
                    fw.act(i_[j][:], pi[:], AF.Sigmoid, bias=self.pcol("b_i", ct), reads=[ki, "prm"], writes=["c_i%d" % j])
                    yield
                    fw.act(sg[j][:], pg[:], AF.Silu, reads=[kg], writes=["c_sg%d" % j])
                    yield
                    fw.act(a_[j][:], r_[j][:], AF.Exp, scale=cl[:, ct:ct + 1], reads=["c_r%d" % j, "c_cl"], writes=["c_a%d" % j])
                    yield
                    fw.v("tensor_tensor", b_[j][:], a_[j][:], a_[j][:], ALU.mult, reads=["c_a%d" % j], writes=["c_b%d" % j])
                    yield
                    fw.v("tensor_scalar", b_[j][:], b_[j][:], -1.0, 1.0, ALU.mult, ALU.add, reads=["c_b%d" % j], writes=["c_b%d" % j])
                    yield
                    fw.act(b_[j][:], b_[j][:], AF.Sqrt, reads=["c_b%d" % j], writes=["c_b%d" % j])
                    yield
                    fw.v("tensor_tensor", i_[j][:], i_[j][:], xc[j][:], ALU.mult, reads=["c_i%d" % j, "c_xc%d" % j],
                         writes=["c_i%d" % j], eng="gpsimd")
                    yield
                    fw.v("tensor_tensor", b_[j][:], b_[j][:], i_[j][:], ALU.mult, reads=["c_b%d" % j, "c_i%d" % j], writes=["c_b%d" % j])
                    yield
                    fw.v("tensor_tensor_scan", r_[j][:], a_[j][:], b_[j][:], hprev[:, ct:ct + 1], ALU.mult, ALU.add,
                         reads=["c_a%d" % j, "c_b%d" % j, "c_hprev%d" % ct, "c_r%d" % j], writes=["c_r%d" % j])
                    yield
                    fw.v("tensor_copy", hprev[:, ct:ct + 1], r_[j][:, 511:512], reads=["c_r%d" % j], writes=["c_hprev%d" % ct])
                    yield
                    fw.v("tensor_tensor", yo[j][:], r_[j][:], sg[j][:], ALU.mult, reads=["c_r%d" % j, "c_sg%d" % j],
                         writes=["c_y%d" % j], eng="gpsimd")
                    yield
                    fw.dma(self.yT[2][ct, :, c0:c0 + 512], yo[j][:], reads=["c_y%d" % j], writes=[("yT2", g, ct)], eng="gpsimd")
                    yield
                fw.lockstep([cbody(0), cbody(1)])
                fw.lockstep([cbody(2), cbody(3)])
            self.release(keys)

    def phase_M(self, l):
        fw, S, G, L = self.fw, self.S, self.G, self.L
        PS = self.PS
        last = (l == L - 1)
        with ExitStack() as ph:
            sb = lambda n, s, d: ph.enter_context(self.nc.sbuf_tensor(self.uname(n), list(s), d))
            wG = sb("wG", [128, 8, 3072], BF16)
            wbr = sb("wbr", [128, 12, 1024], BF16)
            wo = sb("wo", [128, 8, 1024], BF16)
            wpg = sb("wpg", [128, 8, 1024], BF16)
            wple = sb("wple", [128, 2, 1024], BF16)
            hn = sb("m_hn", [128, 8, 512], BF16)
            ys = [sb("m_y%d" % n, [128, 4, 512], BF16) for n in range(3)]
            hb = sb("m_h", [128, 8, 512], F32)
            h1b = sb("m_h1b", [128, 8, 512], BF16)
            pf = sb("m_pf", [128, 2, 512], F32)
            pb_ = sb("m_pb", [128, 2, 512], BF16)
            mrg = sb("m_mrg", [128, 8, 512], BF16)
            sgsP = [[sb("m_sg%d_%d" % (p, n), [128, 512], F32) for n in range(3)] for p in range(2)]
            sgs = sgsP[0]
            tmp = sb("m_tmp", [128, 2, 512], F32)
            self.rs_sb = sb("m_rs", [128, 512], F32)
            self.hn_out = h1b
            self.hn_out_key = "m_h1b"
            self.eps_col = sb("m_eps", [128, 1], F32)
            names = ["wG", "wbr", "wo", "wpg", "wple", "m_hn", "m_y0", "m_y1", "m_y2", "m_h", "m_h1b", "m_pf", "m_pb",
                     "m_mrg", "m_sg0_0", "m_sg0_1", "m_sg0_2", "m_sg1_0", "m_sg1_1", "m_sg1_2", ("m_tmp", 0), ("m_tmp", 1), "rs", "hn_out", "eps"]
            keys = names + [("wG", k) for k in range(8)] + [("wbr", k) for k in range(12)] + \
                [("wo", k) for k in range(8)] + [("wpg", k) for k in range(8)] + [("wple", k) for k in range(2)]
            self.acquire(keys + ["stg0", "stg1"])
            fw.v("memset", self.eps_col[:], NORM_EPS, writes=["eps"])
            self.load_w(wG, "wG", lambda k: self.w_in[l, k * 128:(k + 1) * 128, OFF_G:OFF_G + 3072], 3072, 8, self.gcol)
            self.load_w(wbr, "wbr", lambda k: self.w_branch[l, k // 4, (k % 4) * 128:(k % 4 + 1) * 128, :], 1024, 12)
            self.load_w(wo, "wo", lambda k: self.w_out[l, k * 128:(k + 1) * 128, :], 1024, 8)
            self.load_w(wpg, "wpg", lambda k: self.w_pg[l, k * 128:(k + 1) * 128, :], 1024, 8)
            self.load_w(wple, "wple", lambda k: self.w_ple[l, k * 128:(k + 1) * 128, :], 1024, 2)
            hsrc = self.xT if l == 0 else self.hT
            hdst = self.outT if last else self.hT
            for g in range(G):
                c0 = g * 512
                fw.dma(hn[:], self.hnT[:, :, c0:c0 + 512].rearrange("k p s -> p k s"), reads=[("hnT", g)], writes=["m_hn"])
                for n in range(3):
                    fw.dma(ys[n][:], self.yT[n][:, :, c0:c0 + 512].rearrange("k p s -> p k s"),
                           reads=[("yT%d" % n, g, ct) for ct in range(4)], writes=["m_y%d" % n])
                fw.dma(hb[:], hsrc[:, :, c0:c0 + 512].rearrange("k p s -> p k s"),
                       reads=([("hT", g)] if l > 0 else []), writes=["m_h"])
                fw.dma(pf[:], self.pT[l, :, :, c0:c0 + 512].rearrange("k p s -> p k s"), writes=["m_pf"])
                fw.v("tensor_copy", pb_[:], pf[:], reads=["m_pf"], writes=["m_pb"], eng="gpsimd")
                for dmt in range(8):
                    cs = slice(dmt * 128, (dmt + 1) * 128)
                    dp = dmt % 2
                    sg_ = sgsP[dp]
                    gb = 3 * (dmt % 2)
                    for n in range(3):
                        yb = 6 + (dmt * 3 + n) % 2
                        for k in range(8):
                            fw.mm(PS[gb + n][:], wG[:, k, n * 1024 + dmt * 128:n * 1024 + (dmt + 1) * 128], hn[:, k, :],
                                  start=(k == 0), stop=(k == 7), reads=[("wG", k), "m_hn"], writes=["ps%d" % (gb + n)])
                        for kc in range(4):
                            fw.mm(PS[yb][:], wbr[:, n * 4 + kc, cs], ys[n][:, kc, :], start=(kc == 0), stop=(kc == 3),
                                  reads=[("wbr", n * 4 + kc), "m_y%d" % n], writes=["ps%d" % yb])
                        fw.act(sg_[n][:], PS[gb + n][:], AF.Sigmoid, reads=["ps%d" % (gb + n)], writes=["m_sg%d_%d" % (dp, n)])
                        fw.v("tensor_tensor", sg_[n][:], PS[yb][:], sg_[n][:], ALU.mult,
                             reads=["ps%d" % yb, "m_sg%d_%d" % (dp, n)], writes=["m_sg%d_%d" % (dp, n)])
                    fw.v("tensor_tensor", sg_[0][:], sg_[0][:], sg_[1][:], ALU.add, reads=["m_sg%d_0" % dp, "m_sg%d_1" % dp], writes=["m_sg%d_0" % dp], eng="gpsimd")
                    fw.v("tensor_tensor", mrg[:, dmt, :], sg_[0][:], sg_[2][:], ALU.add, reads=["m_sg%d_0" % dp, "m_sg%d_2" % dp], writes=["m_mrg"], eng="gpsimd")
                for d2 in range(8):
                    pk = 6 + d2 % 2
                    for k in range(8):
                        fw.mm(PS[pk][:], wo[:, k, d2 * 128:(d2 + 1) * 128], mrg[:, k, :], start=(k == 0), stop=(k == 7),
                              reads=[("wo", k), "m_mrg"], writes=["ps%d" % pk])
                    fw.v("tensor_tensor", hb[:, d2, :], hb[:, d2, :], PS[pk][:], ALU.add, reads=["m_h", "ps%d" % pk], writes=["m_h"])
                fw.act(h1b[:], hb[:], AF.Copy, reads=["m_h"], writes=["m_h1b"])
                for d2 in range(8):
                    pa, pp = (0, 1) if d2 % 2 == 0 else (2, 3)
                    for k in range(8):
                        fw.mm(PS[pa][:], wpg[:, k, d2 * 128:(d2 + 1) * 128], h1b[:, k, :], start=(k == 0), stop=(k == 7),
                              reads=[("wpg", k), "m_h1b"], writes=["ps%d" % pa])
                    for k in range(2):
                        fw.mm(PS[pp][:], wple[:, k, d2 * 128:(d2 + 1) * 128], pb_[:, k, :], start=(k == 0), stop=(k == 1),
                              reads=[("wple", k), "m_pb"], writes=["ps%d" % pp])
                    sgk = d2 % 2
                    fw.act(sgs[sgk][:], PS[pa][:], AF.Sigmoid, reads=["ps%d" % pa], writes=["m_sg0_%d" % sgk])
                    fw.v("tensor_tensor", sgs[sgk][:], PS[pp][:], sgs[sgk][:], ALU.mult, reads=["ps%d" % pp, "m_sg0_%d" % sgk],
                         writes=["m_sg0_%d" % sgk])
                    fw.v("tensor_tensor", hb[:, d2, :], hb[:, d2, :], sgs[sgk][:], ALU.add, reads=["m_h", "m_sg0_%d" % sgk],
                         writes=["m_h"], eng="gpsimd")
                fw.dma(hdst[:, :, c0:c0 + 512].rearrange("k p s -> p k s"), hb[:], reads=["m_h"],
                       writes=[("outT" if last else "hT", g)], eng="gpsimd")
                if not last:
                    self.norm_group(hb, "m_h", g, tmp, "m_tmp")
            self.release(keys)


_CACHE = {}


def make_in_maps(inp, S, L, ncores):
    maps = []
    w_in = np.ascontiguousarray(np.asarray(inp["w_in"], np.float32)[:L])
    ki0 = OFF_B + 1792
    w_kidup = np.ascontiguousarray(np.concatenate([w_in[:, :, ki0:ki0 + 64], w_in[:, :, ki0:ki0 + 64]], axis=2))
    prm = np.stack([pack_params(inp, l) for l in range(L)])
    cst = np.zeros((128, 32), np.float32)
    invf = (np.float32(500000.0) ** (-(np.arange(0, 16, 2, dtype=np.float32) / np.float32(16)))).astype(np.float32)
    for p_ in range(128):
        if p_ % 64 < 16:
            cst[p_, 0] = invf[p_ % 8]
    cst[:, 1:25] = (2.0 ** (-np.arange(24, dtype=np.float64)))[None, :].astype(np.float32)
    ropeR = np.zeros((128, 128), np.float32)
    for m_ in range(128):
        if m_ % 64 < 8:
            ropeR[m_ + 8, m_] = -1.0
        elif m_ % 64 < 16:
            ropeR[m_ - 8, m_] = 1.0
    shared = {
        "cst": cst, "ropeR": ropeR,
        "prm": prm, "w_in": w_in, "w_kidup": w_kidup,
        "w2": np.ascontiguousarray(np.asarray(inp["rwkv_w2"], np.float32)[:L]),
        "a2": np.ascontiguousarray(np.asarray(inp["rwkv_a2"], np.float32)[:L]),
        "wr_bd": np.stack([blockdiag(inp["lru_w_r"][l]) for l in range(L)]),
        "wi_bd": np.stack([blockdiag(inp["lru_w_i"][l]) for l in range(L)]),
        "w_branch": np.ascontiguousarray(np.asarray(inp["w_branch"], np.float32)[:L]),
        "w_out": np.ascontiguousarray(np.asarray(inp["w_out"], np.float32)[:L]),
        "w_ple": np.ascontiguousarray(np.asarray(inp["w_ple"], np.float32)[:L]),
        "w_pg": np.ascontiguousarray(np.asarray(inp["w_ple_gate"], np.float32)[:L]),
    }
    x = np.asarray(inp["x"], np.float32)
    p = np.asarray(inp["p"], np.float32)
    pos = np.asarray(inp["positions"], np.int32)
    nb = x.shape[0]
    for c in range(ncores):
        b = (c // 2) % nb
        m = dict(shared)
        m["xT"] = np.ascontiguousarray(x[b].T.reshape(8, 128, S))
        m["pT"] = np.ascontiguousarray(np.stack([p[l, b].T.reshape(2, 128, S) for l in range(L)]))
        m["pos"] = np.ascontiguousarray(pos[b].reshape(1, S))
        maps.append(m)
    return maps


def kernel(**inputs):
    x = np.asarray(inputs["x"])
    B, S, _ = x.shape
    L = np.asarray(inputs["w_in"]).shape[0]
    key = (S, L)
    if key not in _CACHE:
        _CACHE[key] = Prog(S, L).build()
    nc = _CACHE[key]
    maps = make_in_maps(inputs, S, L, 8)
    res = run_bass_kernel_spmd(nc, maps, core_ids=list(range(8)))
    out = np.zeros((B, S, D), np.float32)
    for b in range(B):
        out[b] = res.results[2 * b]["outT"].reshape(D, S).T
    return out
```

```python
from contextlib import ExitStack
import numpy as np
import concourse.bass as bass
import concourse.mybir as mybir
from concourse.bass_utils import run_bass_kernel_spmd

F32 = mybir.dt.float32
BF16 = mybir.dt.bfloat16
I32 = mybir.dt.int32
AF = mybir.ActivationFunctionType
ALU = mybir.AluOpType
AX = mybir.AxisListType

ENGS = ("tensor", "vector", "scalar", "gpsimd", "sync")
N_DMA_SEMS = 24

D = 1024
DIN = 8644
OFF_A, OFF_B, OFF_C, OFF_G = 0, 2176, 4548, 5572
NORM_EPS = 1e-6
GN_EPS = 64e-5


class FW:
    def __init__(self, nc, stack, same_engine_sync=True):
        self.nc = nc
        self.stack = stack
        self.q = {e: [] for e in ENGS}
        self.cnt = {e: 0 for e in ENGS}
        self.sem = {e: stack.enter_context(nc.semaphore("s_" + e)) for e in ENGS}
        self.dsem = [stack.enter_context(nc.semaphore("d%d" % i)) for i in range(N_DMA_SEMS)]
        self.dcnt = [0] * N_DMA_SEMS
        self.dnext = 0
        self.seen = {e: {} for e in ENGS}
        self.lastw = {}
        self.readers = {}
        self.same = same_engine_sync
        self.ninst = 0
        self.rr = 0

    def sb(self, name, shape, dt):
        return self.stack.enter_context(self.nc.sbuf_tensor(name, list(shape), dt))

    def ps(self, name, shape, dt=F32):
        return self.stack.enter_context(self.nc.psum_tensor(name, list(shape), dt))

    def _deps(self, eng, reads, writes):
        ev = []
        for k in reads:
            if k in self.lastw:
                ev.append((self.lastw[k], True))
        for k in writes:
            if k in self.lastw:
                ev.append((self.lastw[k], False))
            ev.extend((e, False) for e in self.readers.get(k, ()))
        best = {}
        for (sname, sem, val, src), raw in ev:
            if src == eng and (eng == "tensor" or not self.same or not raw):
                continue
            if self.seen[eng].get(sname, 0) >= val:
                continue
            if sname not in best or best[sname][1] < val:
                best[sname] = (sem, val)
        waits = []
        for sname, (sem, val) in best.items():
            self.seen[eng][sname] = val
            waits.append((sem, val))
        return waits

    def _commit(self, event, reads, writes):
        for k in writes:
            self.lastw[k] = event
            self.readers[k] = []
        for k in reads:
            if k in writes:
                continue
            self.readers.setdefault(k, []).append(event)

    def op(self, eng, fn, reads=(), writes=()):
        waits = self._deps(eng, reads, writes)
        self.cnt[eng] += 1
        idx = self.cnt[eng]
        sem = self.sem[eng]
        self.q[eng].append((waits, fn, sem, 1))
        self._commit(("s_" + eng, sem, idx, eng), reads, writes)
        self.ninst += 1

    def dma(self, out, in_, reads=(), writes=(), eng="sync", **kw):
        lo, n = (0, 16) if eng == "sync" else (16, N_DMA_SEMS - 16)
        self.dnext_q = getattr(self, "dnext_q", {})
        i = self.dnext_q.get(eng, 0)
        self.dnext_q[eng] = (i + 1) % n
        slot = lo + i
        sem = self.dsem[slot]
        sname = "d%d" % slot
        waits = self._deps(eng, reads, writes)
        prev = self.dcnt[slot] * 16
        if prev and self.seen[eng].get(sname, 0) < prev:
            waits.append((sem, prev))
            self.seen[eng][sname] = prev
        self.dcnt[slot] += 1
        val = self.dcnt[slot] * 16
        self.q[eng].append((waits, lambda e: e.dma_start(out=out, in_=in_, **kw), sem, 16))
        self._commit((sname, sem, val, "dma"), reads, writes)
        self.ninst += 1

    def finish(self, keys, eng="sync"):
        waits = self._deps(eng, keys, ())
        self.q[eng].append((waits, None, None, 0))

    def emit(self):
        nc = self.nc
        with nc.Block() as block:
            for ename in ENGS:
                items = self.q[ename]
                if not items:
                    continue

                def body(e, items=items):
                    for waits, fn, sem, inc in items:
                        for (ws, wv) in waits:
                            e.wait_ge(ws, wv)
                        if fn is not None:
                            fn(e).then_inc(sem, inc)

                getattr(block, ename)(body)

    def mm(self, out, lhsT, rhs, start=True, stop=True, reads=(), writes=()):
        self.op("tensor", lambda e: e.matmul(out, lhsT, rhs, start=start, stop=stop), reads, writes)

    def tr(self, out, in_, ident, reads=(), writes=()):
        self.op("tensor", lambda e: e.transpose(out, in_, ident), reads, writes)

    def act(self, out, in_, func, bias=0.0, scale=1.0, reads=(), writes=(), accum_out=None):
        if accum_out is None:
            self.op("scalar", lambda e: e.activation(out, in_, func, bias=bias, scale=scale), reads, writes)
        else:
            self.op("scalar", lambda e: e.activation(out, in_, func, bias=bias, scale=scale,
                                                     accum_out=accum_out), reads, writes)

    def v(self, name, *args, reads=(), writes=(), eng="vector", **kw):
        self.op(eng, lambda e: getattr(e, name)(*args, **kw), reads, writes)

    @staticmethod
    def lockstep(gens):
        gens = list(gens)
        while gens:
            for g_ in list(gens):
                try:
                    next(g_)
                except StopIteration:
                    gens.remove(g_)

    def cast_eng(self):
        self.rr += 1
        return ("vector", "gpsimd")[self.rr % 2]


PCOLS = {}
_o = 0
for _n, _w in [("norm_g", 8), ("mu_r", 4), ("mu_k", 4), ("mu_v", 4), ("mu_g", 4), ("mu_wl", 1), ("mu_al", 1),
               ("w0", 4), ("a0", 4), ("k_k", 4), ("k_a", 4), ("gn_g", 4), ("gn_b", 4), ("r_k", 4),
               ("q_g", 1), ("k_g", 1),
               ("conv_w", 16), ("conv_b", 4), ("b_r", 4), ("b_i", 4), ("lam", 4)]:
    PCOLS[_n] = (_o, _w)
    _o += _w
NPRM = _o


def _col4(v):
    return np.ascontiguousarray(np.asarray(v, np.float32).reshape(4, 128).T)


def pack_params(inp, l):
    prm = np.zeros((128, NPRM), np.float32)

    def put(name, arr):
        o, w = PCOLS[name]
        prm[:arr.shape[0], o:o + w] = arr

    put("norm_g", np.asarray(inp["norm_g"][l], np.float32).reshape(8, 128).T)
    mu = np.asarray(inp["rwkv_mu"][l], np.float32)
    put("mu_r", _col4(mu[0:512])); put("mu_k", _col4(mu[512:1024])); put("mu_v", _col4(mu[1024:1536]))
    put("mu_wl", mu[1536:1600].reshape(64, 1)); put("mu_al", mu[1600:1664].reshape(64, 1))
    put("mu_g", _col4(mu[1664:2176]))
    put("w0", _col4(inp["rwkv_w0"][l])); put("a0", _col4(inp["rwkv_a0"][l]))
    put("k_k", _col4(inp["rwkv_k_k"][l])); put("k_a", _col4(inp["rwkv_k_a"][l]))
    put("gn_g", _col4(inp["rwkv_gn_g"][l])); put("gn_b", _col4(inp["rwkv_gn_b"][l]))
    put("r_k", _col4(np.asarray(inp["rwkv_r_k"][l]).reshape(512)))
    put("q_g", np.tile(np.asarray(inp["dsa_q_g"][l], np.float32), 2).reshape(128, 1))
    put("k_g", np.tile(np.asarray(inp["dsa_k_g"][l], np.float32), 2).reshape(128, 1))
    cw = np.asarray(inp["lru_conv_w"][l], np.float32)
    put("conv_w", np.concatenate([_col4(cw[i]) for i in range(4)], axis=1))
    put("conv_b", _col4(inp["lru_conv_b"][l])); put("b_r", _col4(inp["lru_b_r"][l]))
    put("b_i", _col4(inp["lru_b_i"][l])); put("lam", _col4(inp["lru_lambda"][l]))
    return prm


def blockdiag(w):
    w = np.asarray(w, np.float32)
    out = np.zeros((4, 128, 128), np.float32)
    for ct in range(4):
        out[ct, 0:64, 0:64] = w[2 * ct]
        out[ct, 64:128, 64:128] = w[2 * ct + 1]
    return out


class Prog:
    def __init__(self, S, L, phases="NACBM", dbg=()):
        self.S, self.L, self.phases, self.dbg = S, L, phases, dbg
        self.G = S // 512
        nc = self.nc = bass.Bass("TRN2", target_bir_lowering=False)
        dt = nc.dram_tensor
        self.xT = dt("xT", [8, 128, S], F32, kind="ExternalInput").ap()
        self.pT = dt("pT", [L, 2, 128, S], F32, kind="ExternalInput").ap()
        self.pos = dt("pos", [1, S], I32, kind="ExternalInput").ap()
        self.prm = dt("prm", [L, 128, NPRM], F32, kind="ExternalInput").ap()
        self.w_in = dt("w_in", [L, D, DIN], F32, kind="ExternalInput").ap()
        self.w_kidup = dt("w_kidup", [L, D, 128], F32, kind="ExternalInput").ap()
        self.w2 = dt("w2", [L, 64, 512], F32, kind="ExternalInput").ap()
        self.a2 = dt("a2", [L, 64, 512], F32, kind="ExternalInput").ap()
        self.wr_bd = dt("wr_bd", [L, 4, 128, 128], F32, kind="ExternalInput").ap()
        self.wi_bd = dt("wi_bd", [L, 4, 128, 128], F32, kind="ExternalInput").ap()
        self.w_branch = dt("w_branch", [L, 3, 512, D], F32, kind="ExternalInput").ap()
        self.w_out = dt("w_out", [L, D, D], F32, kind="ExternalInput").ap()
        self.w_ple = dt("w_ple", [L, 256, D], F32, kind="ExternalInput").ap()
        self.w_pg = dt("w_pg", [L, D, D], F32, kind="ExternalInput").ap()
        self.cst_d = dt("cst", [128, 32], F32, kind="ExternalInput").ap()
        self.ropeR_d = dt("ropeR", [128, 128], F32, kind="ExternalInput").ap()
        self.ropeT = dt("ropeT", [2, 128, S], F32, kind="Internal").ap()
        self.outT = dt("outT", [8, 128, S], F32, kind="ExternalOutput").ap()
        okind = lambda n: "ExternalOutput" if n in dbg else "Internal"
        self.hT = dt("hT", [8, 128, S], F32, kind=okind("hT")).ap()
        self.hnT = dt("hnT", [8, 128, S], BF16, kind=okind("hnT")).ap()
        self.yT = [dt("yT%d" % n, [4, 128, S], BF16, kind=okind("yT%d" % n)).ap() for n in range(3)]

    def uname(self, n):
        self._uid = getattr(self, "_uid", 0) + 1
        return "%s_u%d" % (n, self._uid)

    def pcol(self, name, j=0, rows=128):
        o, w = PCOLS[name]
        return self.prm_sb[0:rows, o + j:o + j + 1]

    def load_w(self, dst, key, src_fn, ncols, kt, scale=None, rows=128):
        fw = self.fw
        for k in range(kt):
            for c0 in range(0, ncols, 512):
                cn = min(512, ncols - c0)
                si = self.stg_i
                self.stg_i ^= 1
                stg = self.stg[si]
                fw.dma(stg[0:rows, 0:cn], src_fn(k)[:, c0:c0 + cn], writes=["stg%d" % si])
                self.cast_rr = getattr(self, "cast_rr", 0) + 1
                eng = ("vector", "scalar", "gpsimd")[self.cast_rr % 3]
                o_ap, i_ap = dst[0:rows, k, c0:c0 + cn], stg[0:rows, 0:cn]
                rk_ = ["stg%d" % si] + (["prm"] if scale is not None else [])
                if eng == "scalar":
                    fw.act(o_ap, i_ap, AF.Copy, scale=(scale(k) if scale is not None else 1.0), reads=rk_, writes=[(key, k)])
                elif scale is not None:
                    fw.v("tensor_scalar", o_ap, i_ap, scale(k), 0.0, ALU.mult, ALU.add, reads=rk_, writes=[(key, k)], eng=eng)
                else:
                    fw.v("tensor_copy", o_ap, i_ap, reads=rk_, writes=[(key, k)], eng=eng)

    def gcol(self, k):
        return self.pcol("norm_g", k)

    def build(self):
        nc = self.nc
        with ExitStack() as st:
            fw = self.fw = FW(nc, st)
            self.st = st
            self.stg = [fw.sb("stg%d" % i, [128, 512], F32) for i in range(2)]
            self.stg_i = 0
            self.prm_sb = fw.sb("prm_sb", [128, NPRM], F32)
            self.ones_f = fw.sb("ones_f", [128, 128], F32)
            fw.v("memset", self.ones_f[:], 1.0, writes=["ones_f"])
            self.PSALL = fw.ps("psall", [128, 8, 512], F32)
            self.PS = [self.PSALL[:, i, :] for i in range(8)]
            self.tiny_col = fw.sb("tiny_col", [128, 1], F32)
            self.gneps_col = fw.sb("gneps_col", [128, 1], F32)
            fw.v("memset", self.tiny_col[:], 1e-30, writes=["tiny"])
            fw.v("memset", self.gneps_col[:], GN_EPS, writes=["tiny"])
            self.eps6_col = fw.sb("eps6_col", [128, 1], F32)
            fw.v("memset", self.eps6_col[:], NORM_EPS, writes=["tiny"])
            self.cst_sb = fw.sb("cst_sb", [128, 32], F32)
            fw.dma(self.cst_sb[:], self.cst_d, writes=["cst"])
            self.make_consts()
            if "B" in self.phases:
                self.phase_R()
            with ExitStack() as zs:
                for n, ph_ in enumerate("ABC"):
                    if ph_ not in self.phases:
                        zt = zs.enter_context(self.nc.sbuf_tensor(self.uname("zt"), [128, 4, 512], BF16))
                        self.acquire(["zt%d" % n])
                        fw.v("memset", zt[:], 0.0, writes=["zt%d" % n])
                        for g in range(self.G):
                            fw.dma(self.yT[n][:, :, g * 512:(g + 1) * 512].rearrange("k p s -> p k s"), zt[:], reads=["zt%d" % n],
                                   writes=[("yT%d" % n, g, ct) for ct in range(4)])
                        self.release(["zt%d" % n])
            for l in range(self.L):
                fw.dma(self.prm_sb[:], self.prm[l], writes=["prm"])
                if l == 0 and "N" in self.phases:
                    self.phase_N0()
                if "A" in self.phases:
                    self.phase_A(l)
                if "C" in self.phases:
                    self.phase_C(l)
                if "B" in self.phases:
                    self.phase_B(l)
                if "M" in self.phases:
                    self.phase_M(l)
            fw.finish([("outT", g) for g in range(self.G)])
            fw.emit()
        return nc

    def norm_group(self, hbuf, hkey, g, tmp, tmpkey):
        fw, S = self.fw, self.S
        c0 = g * 512
        ps = self.PS[7]
        for k in range(8):
            fw.act(tmp[:, k % 2, :], hbuf[:, k, :], AF.Square, reads=[hkey], writes=[(tmpkey, k % 2)])
            fw.mm(ps[:], self.ones_f[:], tmp[:, k % 2, :], start=(k == 0), stop=(k == 7),
                  reads=["ones_f", (tmpkey, k % 2)], writes=["ps7"])
        rs = self.rs_sb
        fw.act(rs[:], ps[:], AF.Sqrt, bias=self.eps_col[:, 0:1], scale=1.0 / D, reads=["ps7", "eps"], writes=["rs"])
        fw.v("reciprocal", rs[:], rs[:], reads=["rs"], writes=["rs"])
        hn = self.hn_out
        fw.v("tensor_tensor", hn[:], hbuf[:], rs[:].unsqueeze(1).to_broadcast([128, 8, 512]), ALU.mult,
             reads=[hkey, "rs"], writes=[self.hn_out_key])
        fw.dma(self.hnT[:, :, c0:c0 + 512].rearrange("k p s -> p k s"), hn[:], reads=[self.hn_out_key],
               writes=[("hnT", g)], eng="gpsimd")

    def phase_N0(self):
        fw = self.fw
        with ExitStack() as ph:
            sb = lambda n, s, d: ph.enter_context(self.nc.sbuf_tensor(self.uname(n), list(s), d))
            hb = [sb("n0_h%d" % i, [128, 8, 512], F32) for i in range(2)]
            tmp = sb("n0_tmp", [128, 2, 512], F32)
            self.rs_sb = sb("n0_rs", [128, 512], F32)
            self.hn_out = sb("n0_hn", [128, 8, 512], BF16)
            self.hn_out_key = "hn_out"
            self.eps_col = sb("n0_eps", [128, 1], F32)
            self.acquire(["n0_h0", "n0_h1", ("n0_tmp", 0), ("n0_tmp", 1), "rs", "hn_out", "eps"])
            fw.v("memset", self.eps_col[:], NORM_EPS, writes=["eps"])
            for g in range(self.G):
                c0 = g * 512
                h = hb[g % 2]
                fw.dma(h[:], self.xT[:, :, c0:c0 + 512].rearrange("k p s -> p k s"), writes=["n0_h%d" % (g % 2)])
                self.norm_group(h, "n0_h%d" % (g % 2), g, tmp, "n0_tmp")
            self.release(["n0_h0", "n0_h1", ("n0_tmp", 0), ("n0_tmp", 1), "rs", "hn_out", "eps"])

    def release(self, keys):
        fw = self.fw
        ev = []
        for k in keys:
            if k in fw.lastw:
                ev.append(fw.lastw[k])
            ev.extend(fw.readers.get(k, ()))
        best = {}
        for e in getattr(fw, "pending_release", []) + ev:
            if e[0] not in best or best[e[0]][2] < e[2]:
                best[e[0]] = e
        fw.pending_release = list(best.values())

    def acquire(self, keys):
        fw = self.fw
        ev = getattr(fw, "pending_release", [])
        for k in keys:
            fw.readers.setdefault(k, []).extend(ev)


    def make_consts(self):
        fw = self.fw
        onesb = fw.sb("k_onesb", [128, 256], BF16)
        self.ident_b = fw.sb("k_ident", [128, 128], BF16)
        self.mask_ui = fw.sb("k_mask_ui", [128, 256], BF16)
        self.mask_sl = fw.sb("k_mask_sl", [128, 128], BF16)
        self.blk1 = fw.sb("k_blk1", [128, 128], F32)
        g = "gpsimd"
        fw.v("memset", onesb[:], 1.0, writes=["k_onesb"], eng=g)
        sel = lambda out, pat, cm, op, key: fw.op(g, lambda e: e.affine_select(out, onesb[:, 0:128], pat, op, 0.0, base=0,
                                                                                channel_multiplier=cm),
                                                  reads=["k_onesb"], writes=[key])
        sel(self.ident_b[:], [[-1, 128]], 1, ALU.is_equal, "k_ident")
        sel(self.mask_ui[:, 0:128], [[1, 128]], -1, ALU.is_gt, "k_mask_ui")
        sel(self.mask_ui[:, 128:256], [[1, 128]], -1, ALU.is_ge, "k_mask_ui")
        sel(self.mask_sl[:], [[-1, 128]], 1, ALU.is_gt, "k_mask_sl")
        fw.v("memset", self.blk1[:], 0.0, writes=["k_blk1"], eng=g)
        fw.v("memset", self.blk1[0:64, 0:64], 1.0, writes=["k_blk1"], eng=g)
        fw.v("memset", self.blk1[64:128, 64:128], 1.0, writes=["k_blk1"], eng=g)

    def phase_A(self, l):
        fw, S, G = self.fw, self.S, self.G
        PS = self.PS
        CDEC = 0.6065306597126334
        with ExitStack() as ph:
            allkeys = []

            def sb(n, s, d):
                allkeys.append(n)
                return ph.enter_context(self.nc.sbuf_tensor(self.uname(n), list(s), d))

            wA = sb("wA", [128, 8, 2176], BF16)
            w2b = sb("a_w2b", [64, 1, 512], BF16)
            a2b = sb("a_a2b", [64, 1, 512], BF16)
            hn = [sb("a_hn0", [128, 8, 512], BF16)] * 2
            omu = sb("a_omu", [128, NPRM], F32)
            prevc = sb("a_prevc", [128, 18], F32)
            ubP = [[sb("a_ub%d_%d" % (p, q), [128, 513], F32) for q in range(4)] for p in range(2)]
            usP = [[sb("a_us%d_%d" % (p, q), [128, 512], F32) for q in range(4)] for p in range(2)]
            ulo = [sb("a_ulo%d" % q, [64, 513], F32) for q in range(2)]
            twl = sb("a_twl", [64, 512], BF16)
            alb = sb("a_alb", [64, 512], BF16)
            tP = [[sb("a_t%d_%d" % (p, i), [128, 512], F32) for i in range(8)] for p in range(2)]
            t_ = tP[0]
            art = [sb("a_art%d" % ct, [128, 4, 2, 128], BF16) for ct in range(4)]
            bk = [sb("a_bk%d" % ct, [128, 2, 512], BF16) for ct in range(4)]
            vb = [sb("a_vb%d" % ct, [128, 512], BF16) for ct in range(4)]
            tok = [sb("a_tok%d" % ct, [128, 4, 3, 128], BF16) for ct in range(4)]
            bonus = [sb("a_bonus%d" % ct, [128, 512], BF16) for ct in range(4)]
            sgt = [sb("a_sg%d" % ct, [128, 512], BF16) for ct in range(4)]
            PC = sb("a_PC", [128, 4, 4], F32)
            T = sb("a_T", [128, 4, 64], F32)
            Tb = sb("a_Tb", [128, 4, 64], BF16)
            LAb = [sb("a_LAb%d" % i, [128, 4, 256], BF16) for i in range(2)]
            KAb = [sb("a_KAb%d" % i, [128, 4, 256], BF16) for i in range(2)]
            Lb = [sb("a_Lb%d" % i, [128, 4, 128], BF16) for i in range(2)]
            PPb = [[sb("a_PPb%d_%d" % (i, j), [128, 4, 256], BF16) for j in range(2)] for i in range(2)]
            XT = [[sb("a_XT%d_%d" % (i, j), [128, 4, 128], BF16) for j in range(2)] for i in range(2)]
            Wb = [sb("a_Wb%d" % i, [128, 4, 64], BF16) for i in range(2)]
            Ub = [sb("a_Ub%d" % i, [128, 4, 64], BF16) for i in range(2)]
            xc = sb("a_xc", [128, 8, 64], F32)
            sq = sb("a_sq", [128, 8, 64], F32)
            st8 = sb("a_st8", [128, 4, 8], F32)
            onb = sb("a_onb", [128, 512], BF16)
            yv = [sb("a_yv%d" % i, [128, 128], F32) for i in range(2)]
            yout = sb("a_yout", [128, 4, 512], BF16)
            self.rstm = sb("k_rstm", [128, 512], F32)
            keys = allkeys + [("wA", k) for k in range(8)] + [("a_w2b", 0), ("a_a2b", 0)]
            self.acquire(keys + ["stg0", "stg1"])
            fw.v("memset", self.rstm[:], 1.0, writes=["k_rstm"], eng="gpsimd")
            for c in range(4):
                fw.v("memset", self.rstm[:, c * 128:c * 128 + 1], 0.0, writes=["k_rstm"], eng="gpsimd")

            self.load_w(wA, "wA", lambda k: self.w_in[l, k * 128:(k + 1) * 128, OFF_A:OFF_A + 2176], 2176, 8, self.gcol)
            self.load_w(w2b, "a_w2b", lambda k: self.w2[l], 512, 1, rows=64)
            self.load_w(a2b, "a_a2b", lambda k: self.a2[l], 512, 1, rows=64)
            fw.v("tensor_scalar", omu[:], self.prm_sb[:], -1.0, 1.0, ALU.mult, ALU.add, reads=["prm"], writes=["a_omu"])
            fw.v("memset", prevc[:], 0.0, writes=["a_prevc"])
            fw.v("memset", T[:], 0.0, writes=["a_T"])
            fw.v("memset", Tb[:], 0.0, writes=["a_Tb"])
            oc = lambda name, j=0, rows=128: omu[0:rows, PCOLS[name][0] + j:PCOLS[name][0] + j + 1]
            psb0 = PS[0][:].bitcast(BF16)
            psb1 = PS[1][:].bitcast(BF16)

            def shift(ps, pskey, ubt, ubkey, pcol, out, okey, mu_ap, omu_ap, rows=128):
                fw.v("tensor_copy", ubt[0:rows, 0:1], prevc[0:rows, pcol:pcol + 1], reads=["a_prevc"], writes=[ubkey], eng="gpsimd")
                fw.act(ubt[0:rows, 1:513], ps, AF.Copy, reads=[pskey], writes=[ubkey])
                fw.v("tensor_copy", prevc[0:rows, pcol:pcol + 1], ubt[0:rows, 512:513], reads=[ubkey], writes=["a_prevc"], eng="gpsimd")
                fw.v("tensor_scalar", out, ubt[0:rows, 0:512], mu_ap, None, ALU.mult, reads=[ubkey, "prm"], writes=[okey])
                fw.v("scalar_tensor_tensor", out, ubt[0:rows, 1:513], omu_ap, out, ALU.mult, ALU.add,
                     reads=[ubkey, "a_omu", okey], writes=[okey])

            for g in range(G):
                c0 = g * 512
                hk = "a_hn0"
                hg = hn[0]
                fw.dma(hg[:], self.hnT[:, :, c0:c0 + 512].rearrange("k p s -> p k s"), reads=[("hnT", g)], writes=[hk])
                for q, (coff, nm) in enumerate([(1536, "mu_wl"), (1600, "mu_al")]):
                    for k in range(8):
                        fw.mm(PS[q][0:64, :], wA[:, k, coff:coff + 64], hg[:, k, :], start=(k == 0), stop=(k == 7),
                              reads=[("wA", k), hk], writes=["ps%d" % q])
                    shift(PS[q][0:64, :], "ps%d" % q, ulo[q], "a_ulo%d" % q, 16 + q, t_[q][0:64, :], "a_t0_%d" % q,
                          self.pcol(nm, 0, 64), oc(nm, 0, 64), rows=64)
                fw.act(twl[:], t_[0][0:64, :], AF.Tanh, reads=["a_t0_0"], writes=["a_twl"])
                fw.v("tensor_copy", alb[:], t_[1][0:64, :], reads=["a_t0_1"], writes=["a_alb"])
                def abody(ct, g=g, hk=hk, hg=hg):
                    p_ = ct % 2
                    PSp = PS[4 * p_:4 * p_ + 4]
                    pk = lambda q: "ps%d" % (4 * p_ + q)
                    ub, us, t_ = ubP[p_], usP[p_], tP[p_]
                    psb0 = PSp[0][:].bitcast(BF16)
                    psb1 = PSp[1][:].bitcast(BF16)
                    cs = slice(ct * 128, (ct + 1) * 128)
                    for q, (coff, nm) in enumerate([(0, "mu_r"), (512, "mu_k"), (1024, "mu_v"), (1664, "mu_g")]):
                        for k in range(8):
                            fw.mm(PSp[q][:], wA[:, k, coff + ct * 128:coff + (ct + 1) * 128], hg[:, k, :], start=(k == 0), stop=(k == 7),
                                  reads=[("wA", k), hk], writes=[pk(q)])
                            yield
                        shift(PSp[q][:], pk(q), ub[q], "a_ub%d_%d" % (p_, q), ct * 4 + q, us[q][:], "a_us%d_%d" % (p_, q),
                              self.pcol(nm, ct), oc(nm, ct))
                        yield
                    r_s, k_s, v_s, g_s = us
                    K = lambda i: "a_t%d_%d" % (p_, i)
                    fw.mm(PSp[0][:], w2b[:, 0, cs], twl[:], reads=[("a_w2b", 0), "a_twl"], writes=[pk(0)])
                    yield
                    fw.act(t_[0][:], PSp[0][:], AF.Sigmoid, bias=self.pcol("w0", ct), reads=[pk(0), "prm"], writes=[K(0)])
                    yield
                    fw.v("tensor_scalar", t_[0][:], t_[0][:], -CDEC, 0.0, ALU.mult, ALU.add, reads=[K(0)], writes=[K(0)], eng="gpsimd")
                    yield
                    fw.mm(PSp[1][:], a2b[:, 0, cs], alb[:], reads=[("a_a2b", 0), "a_alb"], writes=[pk(1)])
                    yield
                    fw.act(t_[1][:], PSp[1][:], AF.Sigmoid, bias=self.pcol("a0", ct), reads=[pk(1), "prm"], writes=[K(1)])
                    yield
                    fw.v("tensor_scalar", t_[2][:], k_s[:], self.pcol("k_k", ct), None, ALU.mult, reads=["a_us%d_1" % p_, "prm"], writes=[K(2)])
                    yield
                    fw.v("tensor_tensor", t_[3][:], t_[2][:], t_[2][:], ALU.mult, reads=[K(2)], writes=[K(3)], eng="gpsimd")
                    yield
                    fw.mm(PSp[2][:], self.blk1[:], t_[3][:], reads=["k_blk1", K(3)], writes=[pk(2)])
                    yield
                    fw.act(t_[3][:], PSp[2][:], AF.Sqrt, bias=self.tiny_col[:, 0:1], reads=[pk(2), "tiny"], writes=[K(3)])
                    yield
                    fw.v("reciprocal", t_[3][:], t_[3][:], reads=[K(3)], writes=[K(3)])
                    yield
                    fw.v("tensor_tensor", t_[2][:], t_[2][:], t_[3][:], ALU.mult, reads=[K(2), K(3)], writes=[K(2)])
                    yield
                    fw.v("tensor_scalar", t_[3][:], t_[1][:], self.pcol("k_a", ct), oc("k_a", ct), ALU.mult, ALU.add,
                         reads=[K(1), "prm", "a_omu"], writes=[K(3)])
                    yield
                    fw.v("tensor_tensor", t_[3][:], t_[3][:], k_s[:], ALU.mult, reads=[K(3), "a_us%d_1" % p_], writes=[K(3)], eng="gpsimd")
                    yield
                    fw.v("tensor_tensor", t_[4][:], t_[2][:], t_[1][:], ALU.mult, reads=[K(2), K(1)], writes=[K(4)], eng="gpsimd")
                    yield
                    fw.v("tensor_tensor_scan", t_[5][:], self.rstm[:], t_[0][:], 0.0, ALU.mult, ALU.add,
                         reads=["k_rstm", K(0)], writes=[K(5)])
                    yield
                    fw.v("tensor_tensor", t_[6][:], t_[5][:], t_[0][:], ALU.subtract, reads=[K(5), K(0)], writes=[K(6)], eng="gpsimd")
                    yield
                    fw.act(t_[6][:], t_[6][:], AF.Exp, reads=[K(6)], writes=[K(6)])
                    yield
                    fw.act(t_[7][:], t_[5][:], AF.Exp, scale=-1.0, reads=[K(5)], writes=[K(7)])
                    yield
                    fw.act(t_[5][:], t_[5][:], AF.Exp, reads=[K(5)], writes=[K(5)])
                    yield
                    fw.v("tensor_copy", PC[:, ct, :], t_[5][:].rearrange("p (c t) -> p c t", t=128)[:, :, 127], reads=[K(5)],
                         writes=["a_PC"], eng="gpsimd")
                    yield
                    v3 = lambda ap: ap.rearrange("p (c t) -> p c t", t=128)
                    akey = "a_art%d" % ct
                    fw.v("scalar_tensor_tensor", art[ct][:, :, 0, :], v3(t_[2][:]), -1.0, v3(t_[6][:]), ALU.mult, ALU.mult,
                         reads=[K(2), K(6)], writes=[akey])
                    yield
                    fw.v("tensor_tensor", art[ct][:, :, 1, :], v3(r_s[:]), v3(t_[5][:]), ALU.mult, reads=["a_us%d_0" % p_, K(5)], writes=[akey])
                    yield
                    fw.v("tensor_tensor", bk[ct][:, 0, :], t_[4][:], t_[7][:], ALU.mult, reads=[K(4), K(7)], writes=["a_bk%d" % ct])
                    yield
                    fw.v("tensor_tensor", bk[ct][:, 1, :], t_[3][:], t_[7][:], ALU.mult, reads=[K(3), K(7)], writes=["a_bk%d" % ct], eng="gpsimd")
                    yield
                    fw.v("tensor_copy", vb[ct][:], v_s[:], reads=["a_us%d_2" % p_], writes=["a_vb%d" % ct], eng="gpsimd")
                    yield
                    fw.v("scalar_tensor_tensor", t_[4][:], r_s[:], self.pcol("r_k", ct), t_[3][:], ALU.mult, ALU.mult,
                         reads=["a_us%d_0" % p_, "prm", K(3), K(4)], writes=[K(4)])
                    yield
                    fw.mm(PSp[3][:], self.blk1[:], t_[4][:], reads=["k_blk1", K(4)], writes=[pk(3)])
                    yield
                    fw.v("tensor_tensor", bonus[ct][:], PSp[3][:], v_s[:], ALU.mult, reads=[pk(3), "a_us%d_2" % p_], writes=["a_bonus%d" % ct])
                    yield
                    fw.act(sgt[ct][:], g_s[:], AF.Silu, reads=["a_us%d_3" % p_], writes=["a_sg%d" % ct])
                    yield
                    for half in range(2):
                        psb, pkey = (psb0, pk(0)) if half == 0 else (psb1, pk(1))
                        for cc in range(2):
                            c = half * 2 + cc
                            for qi_, (src, skey) in enumerate([(bk[ct][:, 0, c * 128:(c + 1) * 128], "a_bk%d" % ct),
                                                               (bk[ct][:, 1, c * 128:(c + 1) * 128], "a_bk%d" % ct),
                                                               (vb[ct][:, c * 128:(c + 1) * 128], "a_vb%d" % ct)]):
                                o = (cc * 3 + qi_) * 128
                                fw.tr(psb[:, o:o + 128], src, self.ident_b[:], reads=[skey, "k_ident"], writes=[pkey])
                                yield
                        fw.act(tok[ct][:, half * 2:half * 2 + 2, :, :].rearrange("p a b c -> p (a b c)"), psb[:, 0:768], AF.Copy,
                               reads=[pkey], writes=["a_tok%d" % ct])
                        yield

                fw.lockstep([abody(0), abody(1)])
                fw.lockstep([abody(2), abody(3)])
                PSALL = self.PSALL
                idb4 = self.ident_b[:].unsqueeze(1).to_broadcast([128, 4, 128])
                mui2 = self.mask_ui[:].unsqueeze(1).to_broadcast([128, 2, 256])
                msl4 = self.mask_sl[:].unsqueeze(1).to_broadcast([128, 4, 128])
                for c in range(4):
                    ccols = slice(c * 128, (c + 1) * 128)

                    def hv(qd, hi):
                        h = 2 * hi + qd
                        ct, hp = hi, qd
                        pr_ = slice(hp * 64, hp * 64 + 64)
                        d = dict(h=h, ct=ct, hp=hp, pr=pr_, po=hp * 64,
                                 at=art[ct][pr_, c, 0, :], rt=art[ct][pr_, c, 1, :],
                                 ar=art[ct][pr_, c, :, :].rearrange("p a t -> p (a t)"),
                                 bt=bk[ct][pr_, 0, ccols], kt=bk[ct][pr_, 1, ccols],
                                 rk=["a_art%d" % ct, "a_bk%d" % ct], tkey="a_tok%d" % ct,
                                 vt=tok[ct][:, c, 2, hp * 64:hp * 64 + 64], btk=tok[ct][:, c, 0, hp * 64:hp * 64 + 64],
                                 ktk=tok[ct][:, c, 1, hp * 64:hp * 64 + 64], T0b=Tb[pr_, ct, :])
                        return d

                    XYk = lambda qd: ["ps%d" % (3 * qd), "ps%d" % (3 * qd + 1)]
                    Zk = lambda qd: ["ps%d" % (3 * qd + 2)]
                    XY = lambda qd: PSALL[:, 3 * qd:3 * qd + 2, :].rearrange("p b (h x) -> p (b h) x", x=256)
                    Zv = lambda qd: PSALL[:, 3 * qd + 2, :].rearrange("p (h x) -> p h x", x=128)
                    for qd in range(2):
                        for hi in range(4):
                            d = hv(qd, hi)
                            fw.mm(XY(qd)[:, hi, :], d["bt"], d["ar"], reads=d["rk"], writes=[XYk(qd)[hi // 2]])
                            fw.mm(Zv(qd)[:, hi, :], d["at"], d["bt"], reads=d["rk"], writes=Zk(qd))
                    for qd in range(2):
                        for b2 in range(2):
                            fw.v("tensor_tensor", LAb[qd][:, 2 * b2:2 * b2 + 2, :], XY(qd)[:, 2 * b2:2 * b2 + 2, :], mui2, ALU.mult,
                                 reads=[XYk(qd)[b2], "k_mask_ui"], writes=["a_LAb%d" % qd])
                        fw.v("tensor_tensor", Lb[qd][:], Zv(qd), msl4, ALU.mult, reads=Zk(qd) + ["k_mask_sl"], writes=["a_Lb%d" % qd])
                        fw.v("tensor_tensor", XT[qd][0][:], LAb[qd][:, :, 0:128], idb4, ALU.add,
                             reads=["a_LAb%d" % qd, "k_ident"], writes=["a_XT%d_0" % qd], eng="gpsimd")
                    for k in range(1, 8):
                        for qd in range(2):
                            if k == 1:
                                Pp, PTp, pkeys = (lambda hi: Lb[qd][:, hi, :]), (lambda hi: LAb[qd][:, hi, 0:128]), ["a_Lb%d" % qd, "a_LAb%d" % qd]
                            else:
                                pb_ = PPb[qd][(k - 1) % 2]
                                Pp, PTp, pkeys = (lambda hi, pb_=pb_: pb_[:, hi, 0:128]), (lambda hi, pb_=pb_: pb_[:, hi, 128:256]), ["a_PPb%d_%d" % (qd, (k - 1) % 2)]
                            for hi in range(4):
                                if k <= 6:
                                    fw.mm(XY(qd)[:, hi, 0:128], PTp(hi), Pp(hi), reads=pkeys, writes=[XYk(qd)[hi // 2]])
                                    fw.mm(XY(qd)[:, hi, 128:256], Pp(hi), PTp(hi), reads=pkeys, writes=[XYk(qd)[hi // 2]])
                                if k == 7:
                                    d7 = hv(qd, hi)
                                    fw.mm(XY(qd)[:, hi, :], d7["kt"], d7["ar"], reads=d7["rk"], writes=[XYk(qd)[hi // 2]])
                                if k >= 2:
                                    xo = XT[qd][(k - 2) % 2]
                                    xok = "a_XT%d_%d" % (qd, (k - 2) % 2)
                                    fw.mm(Zv(qd)[:, hi, :], self.ident_b[:], xo[:, hi, :], start=True, stop=False, reads=["k_ident", xok], writes=Zk(qd))
                                    fw.mm(Zv(qd)[:, hi, :], Pp(hi), xo[:, hi, :], start=False, stop=True, reads=pkeys + [xok], writes=Zk(qd))
                        for qd in range(2):
                            if k <= 6:
                                for b2 in range(2):
                                    fw.act(PPb[qd][k % 2][:, 2 * b2:2 * b2 + 2, :], XY(qd)[:, 2 * b2:2 * b2 + 2, :], AF.Copy,
                                           reads=[XYk(qd)[b2]], writes=["a_PPb%d_%d" % (qd, k % 2)])
                            if k >= 2:
                                fw.v("tensor_copy", XT[qd][(k - 1) % 2][:], Zv(qd), reads=Zk(qd), writes=["a_XT%d_%d" % (qd, (k - 1) % 2)])
                            if k == 7:
                                for b2 in range(2):
                                    fw.v("tensor_tensor", KAb[qd][:, 2 * b2:2 * b2 + 2, :], XY(qd)[:, 2 * b2:2 * b2 + 2, :], mui2, ALU.mult,
                                         reads=[XYk(qd)[b2], "k_mask_ui"], writes=["a_KAb%d" % qd])
                    XTf = [XT[qd][0] for qd in range(2)]
                    xfk = ["a_XT%d_0" % qd for qd in range(2)]
                    Wv = lambda qd: PSALL[:, 3 * qd + 2, 0:256].rearrange("p (h x) -> p h x", x=64)
                    Uv = lambda qd: PSALL[:, 3 * qd + 2, 256:512].rearrange("p (h x) -> p h x", x=64)
                    for qd in range(2):
                        for hi in range(4):
                            d = hv(qd, hi)
                            fw.mm(Wv(qd)[:, hi, :], d["at"], d["T0b"], start=True, stop=False, reads=["a_art%d" % d["ct"], "a_Tb"], writes=Zk(qd))
                            fw.mm(Wv(qd)[:, hi, :], KAb[qd][:, hi, 0:128], d["vt"], start=False, stop=True, reads=["a_KAb%d" % qd, d["tkey"]], writes=Zk(qd))
                        fw.v("tensor_copy", Wb[qd][:], Wv(qd), reads=Zk(qd), writes=["a_Wb%d" % qd])
                    for qd in range(2):
                        for hi in range(4):
                            fw.mm(Uv(qd)[:, hi, :], XTf[qd][:, hi, :], Wb[qd][:, hi, :], reads=[xfk[qd], "a_Wb%d" % qd], writes=Zk(qd))
                        fw.v("tensor_copy", Ub[qd][:], Uv(qd), reads=Zk(qd), writes=["a_Ub%d" % qd])
                    for qd in range(2):
                        for hi in range(4):
                            d = hv(qd, hi)
                            h, ct = d["h"], d["ct"]
                            ob, okey = (PS[6], "ps6") if qd == 0 else (PS[7], "ps7")
                            osl = slice(qd * 256 + hi * 64, qd * 256 + (hi + 1) * 64)
                            fw.mm(ob[:, osl], d["rt"], d["T0b"], start=True, stop=False, reads=["a_art%d" % ct, "a_Tb"], writes=[okey])
                            fw.mm(ob[:, osl], LAb[qd][:, hi, 128:256], Ub[qd][:, hi, :], start=False, stop=False,
                                  reads=["a_LAb%d" % qd, "a_Ub%d" % qd], writes=[okey])
                            fw.mm(ob[:, osl], KAb[qd][:, hi, 128:256], d["vt"], start=False, stop=True, reads=["a_KAb%d" % qd, d["tkey"]], writes=[okey])
                            zsl = slice(ct * 64, (ct + 1) * 64)
                            fw.mm(PS[7][d["pr"], zsl], d["btk"], Ub[qd][:, hi, :], start=True, stop=False, reads=[d["tkey"], "a_Ub%d" % qd], writes=["ps7"])
                            fw.mm(PS[7][d["pr"], zsl], d["ktk"], d["vt"], start=False, stop=True, reads=[d["tkey"]], writes=["ps7"])
                    zall = PS[7][:, 0:256].rearrange("p (c i) -> p c i", i=64)
                    fw.v("tensor_tensor", T[:], T[:], zall, ALU.add, reads=["a_T", "ps7"], writes=["a_T"])
                    fw.v("tensor_tensor", T[:], T[:], PC[:, :, c:c + 1].to_broadcast([128, 4, 64]), ALU.mult, reads=["a_T", "a_PC"], writes=["a_T"])
                    fw.v("tensor_copy", Tb[:], T[:], reads=["a_T"], writes=["a_Tb"], eng="gpsimd")
                    ov = [PS[6][:, 0:256].rearrange("p (h i) -> p h i", i=64), PS[7][:, 256:512].rearrange("p (h i) -> p h i", i=64)]
                    okeys = ["ps6", "ps7"]
                    for qd in range(2):
                        fw.v("tensor_reduce", st8[:, 0, qd * 4:qd * 4 + 4], ov[qd], AX.X, ALU.add, reads=[okeys[qd]], writes=["a_st8"])
                    fw.v("tensor_scalar", st8[:, 0, :], st8[:, 0, :], 1.0 / 64, None, ALU.mult, reads=["a_st8"], writes=["a_st8"])
                    for qd in range(2):
                        fw.v("tensor_tensor", xc[:, qd * 4:qd * 4 + 4, :], ov[qd],
                             st8[:, 0, qd * 4:qd * 4 + 4].unsqueeze(2).to_broadcast([128, 4, 64]), ALU.subtract,
                             reads=[okeys[qd], "a_st8"], writes=["a_xc"])
                    fw.v("tensor_tensor", sq[:], xc[:], xc[:], ALU.mult, reads=["a_xc"], writes=["a_sq"], eng="gpsimd")
                    fw.v("tensor_reduce", st8[:, 1, :], sq[:], AX.X, ALU.add, reads=["a_sq"], writes=["a_st8"])
                    fw.act(st8[:, 1, :], st8[:, 1, :], AF.Sqrt, bias=self.gneps_col[:, 0:1], scale=1.0 / 64, reads=["a_st8", "tiny"], writes=["a_st8"])
                    fw.v("reciprocal", st8[:, 1, :], st8[:, 1, :], reads=["a_st8"], writes=["a_st8"])
                    fw.v("tensor_tensor", onb[:].rearrange("p (h i) -> p h i", i=64), xc[:],
                         st8[:, 1, :].unsqueeze(2).to_broadcast([128, 8, 64]), ALU.mult, reads=["a_xc", "a_st8"], writes=["a_onb"])
                    for ct in range(4):
                        for hp in range(2):
                            qo = (hp * 4 + ct) * 64
                            fw.tr(psb0[hp * 64:(hp + 1) * 64, ct * 128:(ct + 1) * 128], onb[:, qo:qo + 64], self.ident_b[:],
                                  reads=["a_onb", "k_ident"], writes=["ps0"])
                    for ct in range(4):
                        j = ct % 2
                        fw.v("tensor_scalar", yv[j][:], psb0[:, ct * 128:(ct + 1) * 128], self.pcol("gn_g", ct), self.pcol("gn_b", ct),
                             ALU.mult, ALU.add, reads=["ps0", "prm"], writes=["a_yv%d" % j])
                        fw.v("tensor_tensor", yv[j][:], yv[j][:], bonus[ct][:, ccols], ALU.add, reads=["a_yv%d" % j, "a_bonus%d" % ct],
                             writes=["a_yv%d" % j], eng="gpsimd")
                        fw.v("tensor_tensor", yout[:, ct, ccols], yv[j][:], sgt[ct][:, ccols], ALU.mult,
                             reads=["a_yv%d" % j, "a_sg%d" % ct], writes=["a_yout"], eng="gpsimd")
                fw.dma(self.yT[0][:, :, c0:c0 + 512].rearrange("k p s -> p k s"), yout[:], reads=["a_yout"],
                       writes=[("yT0", g, ct) for ct in range(4)], eng="gpsimd")
            self.release(keys)


    def phase_R(self):
        fw, S, G = self.fw, self.S, self.G
        TWO_PI = 6.283185307179586
        C1 = 6.28125
        C2 = 0.0019350051879882812
        C3 = TWO_PI - C1 - C2
        PI = 3.1415925
        with ExitStack() as ph:
            sb = lambda n, s, d: ph.enter_context(self.nc.sbuf_tensor(self.uname(n), list(s), d))
            posi = sb("r_posi", [128, 512], I32)
            a = sb("r_a", [128, 512], F32)
            k = sb("r_k", [128, 512], F32)
            r = sb("r_r", [128, 512], F32)
            r2 = sb("r_r2", [128, 512], F32)
            m = sb("r_m", [128, 512], F32)
            cs = sb("r_cs", [128, 2, 512], F32)
            keys = ["r_posi", "r_a", "r_k", "r_r", "r_r2", "r_m", "r_cs"]
            self.acquire(keys)
            for g in range(G):
                c0 = g * 512
                fw.dma(posi[:], self.pos[0:1, c0:c0 + 512].to_broadcast([128, 512]), writes=["r_posi"])
                fw.v("tensor_copy", a[:], posi[:], reads=["r_posi"], writes=["r_a"])
                fw.v("tensor_scalar", a[:], a[:], self.cst_sb[:, 0:1], None, ALU.mult, reads=["r_a", "cst"], writes=["r_a"])
                fw.v("tensor_scalar", k[:], a[:], 1.0 / TWO_PI, None, ALU.mult, reads=["r_a"], writes=["r_k"])
                fw.v("tensor_scalar", k[:], k[:], 12582912.0, None, ALU.add, reads=["r_k"], writes=["r_k"])
                fw.v("tensor_scalar", k[:], k[:], 12582912.0, None, ALU.subtract, reads=["r_k"], writes=["r_k"])
                fw.v("scalar_tensor_tensor", r[:], k[:], -C1, a[:], ALU.mult, ALU.add, reads=["r_k", "r_a"], writes=["r_r"])
                fw.v("scalar_tensor_tensor", r[:], k[:], -C2, r[:], ALU.mult, ALU.add, reads=["r_k", "r_r"], writes=["r_r"])
                fw.v("scalar_tensor_tensor", r[:], k[:], -C3, r[:], ALU.mult, ALU.add, reads=["r_k", "r_r"], writes=["r_r"])
                fw.v("tensor_scalar", r[:], r[:], PI, -PI, ALU.min, ALU.max, reads=["r_r"], writes=["r_r"])
                fw.v("tensor_scalar", r2[:], r[:], TWO_PI / 4, None, ALU.add, reads=["r_r"], writes=["r_r2"])
                fw.v("tensor_scalar", m[:], r2[:], PI, -TWO_PI, ALU.is_gt, ALU.mult, reads=["r_r2"], writes=["r_m"])
                fw.v("tensor_tensor", r2[:], r2[:], m[:], ALU.add, reads=["r_r2", "r_m"], writes=["r_r2"])
                fw.v("tensor_scalar", r2[:], r2[:], PI, -PI, ALU.min, ALU.max, reads=["r_r2"], writes=["r_r2"])
                fw.act(cs[:, 0, :], r2[:], AF.Sin, reads=["r_r2"], writes=["r_cs"])
                fw.act(cs[:, 1, :], r[:], AF.Sin, reads=["r_r"], writes=["r_cs"])
                fw.dma(self.ropeT[:, :, c0:c0 + 512].rearrange("k p s -> p k s"), cs[:], reads=["r_cs"], writes=[("ropeT", g)], eng="gpsimd")
            self.release(keys)

    def phase_B(self, l):
        fw, S, G = self.fw, self.S, self.G
        PS = self.PS
        NT = S // 128
        NIT = 20
        NOATT = False
        with ExitStack() as ph:
            allkeys = []

            def sb(n, s, d):
                allkeys.append(n)
                return ph.enter_context(self.nc.sbuf_tensor(self.uname(n), list(s), d))

            wB = sb("wB", [128, 8, 2372], BF16)
            wkd = sb("b_wkd", [128, 8, 128], BF16)
            ropeR = sb("b_ropeR", [128, 1, 128], BF16)
            KT = [sb("b_KT%d" % ct, [128, S], BF16) for ct in range(4)]
            KI = sb("b_KI", [128, S], BF16)
            V = sb("b_V", [128, NT, 8, 65], BF16)
            hn = sb("b_hn", [128, 8, 512], BF16)
            QT = [[sb("b_QT%d_%d" % (ct, i), [128, 512], BF16) for ct in range(4)] for i in range(2)]
            QI = [[sb("b_QI%d_%d" % (j, i), [128, 512], BF16) for j in range(2)] for i in range(2)]
            SG = [[sb("b_SG%d_%d" % (ct, i), [128, 512], BF16) for ct in range(4)] for i in range(2)]
            WI = [sb("b_WI%d" % i, [128, 4, 4], F32) for i in range(2)]
            yout = [sb("b_yout0", [128, 4, 512], BF16)] * 2
            score = sb("b_score", [128, S], F32)
            alias = S >= 4096
            if alias:
                xL = [score[:, 0:512], score[:, 1280:1792]]
                x2L = [score[:, 512:1024], score[:, 1792:2304]]
                xbL = [score[:, 1024:1280].bitcast(BF16), score[:, 2304:2560].bitcast(BF16)]
                cs = score[:, 2560:3584].rearrange("p (a b) -> p a b", b=512)
            else:
                cs = sb("b_cs", [128, 2, 512], F32)
                xL = [sb("b_x%d" % i, [128, 512], F32) for i in range(2)]
                x2L = [sb("b_x2%d" % i, [128, 512], F32) for i in range(2)]
                xbL = [sb("b_xb%d" % i, [128, 512], BF16) for i in range(2)]
            tkeys = ["b_cs"] + ["b_x%d" % i for i in range(2)] + ["b_x2%d" % i for i in range(2)] + ["b_xb%d" % i for i in range(2)]
            mm1 = [sb("b_mm1_0", [128, S], BF16)] * 2
            MT = [sb("b_MT%d" % i, [128, NT, 128], BF16) for i in range(2)]
            E = [sb("b_E%d" % i, [128, 512], BF16) for i in range(4)]
            rl = [sb("b_rl%d" % i, [128, 512], F32) for i in range(2)]
            PT = [sb("b_PT%d" % i, [128, 512], BF16) for i in range(4)]
            bs = sb("b_bs", [128, 8], F32)
            steps = sb("b_steps", [128, NIT + 1], F32)
            rec = sb("b_rec", [128, 8], F32)
            otok = sb("b_otok", [128, 8, 64], BF16)
            dmask = sb("b_dmask", [128, 128], F32)
            keys = allkeys + tkeys + [("wB", k) for k in range(8)] + [("b_wkd", k) for k in range(8)] + [("b_ropeR", 0)] + \
                [("b_KT", ct, g) for ct in range(4) for g in range(G)] + [("b_KI", g) for g in range(G)] + [("b_V", g) for g in range(G)]
            self.acquire(keys + ["stg0", "stg1"])
            self.load_w(wB, "wB", lambda k: self.w_in[l, k * 128:(k + 1) * 128, OFF_B:OFF_B + 2372], 2372, 8, self.gcol)
            self.load_w(wkd, "b_wkd", lambda k: self.w_kidup[l, k * 128:(k + 1) * 128, :], 128, 8, self.gcol)
            self.load_w(ropeR, "b_ropeR", lambda k: self.ropeR_d, 128, 1)
            fw.v("memset", V[:], 1.0, writes=[("b_V", g) for g in range(G)], eng="gpsimd")
            fw.v("memset", dmask[:], 0.0, writes=["b_dmask"], eng="gpsimd")
            fw.v("memset", dmask[0:64, 64:128], -1e30, writes=["b_dmask"], eng="gpsimd")

            def lane(p, chains):
                x_, x2, xb = xL[p], x2L[p], xbL[p]
                kx, kx2, kxb = "b_x%d" % p, "b_x2%d" % p, "b_xb%d" % p
                ps, pk = PS[p], "ps%d" % p

                def proj(w, wkey, c_lo, c_hi):
                    for k in range(8):
                        fw.mm(ps[:], w[:, k, c_lo:c_hi], hn[:, k, :], start=(k == 0), stop=(k == 7), reads=[(wkey, k), "b_hn"], writes=[pk])
                        yield

                def rope(dst, dkey):
                    fw.v("tensor_copy", xb[:], x_[:], reads=[kx], writes=[kxb], eng="gpsimd")
                    yield
                    fw.mm(ps[:], ropeR[:, 0, :], xb[:], reads=[("b_ropeR", 0), kxb], writes=[pk])
                    yield
                    fw.v("tensor_tensor", x2[:], x_[:], cs[:, 0, :], ALU.mult, reads=[kx, "b_cs"], writes=[kx2], eng="gpsimd")
                    yield
                    fw.v("tensor_tensor", x_[:], ps[:], cs[:, 1, :], ALU.mult, reads=[pk, "b_cs", kx], writes=[kx])
                    yield
                    fw.v("tensor_tensor", dst, x2[:], x_[:], ALU.add, reads=[kx2, kx], writes=[dkey], eng="gpsimd")
                    yield

                for ch in chains:
                    kind = ch[0]
                    if kind in ("q", "k"):
                        _, ct, g = ch
                        gp = g % 2
                        gc = slice(g * 512, g * 512 + 512)
                        coff, gname = (0, "q_g") if kind == "q" else (512, "k_g")
                        yield from proj(wB, "wB", coff + ct * 128, coff + (ct + 1) * 128)
                        fw.act(x_[:], ps[:], AF.Copy, reads=[pk], writes=[kx])
                        yield
                        fw.v("tensor_tensor", x2[:], x_[:], x_[:], ALU.mult, reads=[kx], writes=[kx2], eng="gpsimd")
                        yield
                        fw.mm(ps[:], self.blk1[:], x2[:], reads=["k_blk1", kx2], writes=[pk])
                        yield
                        fw.act(x2[:], ps[:], AF.Sqrt, bias=self.eps6_col[:, 0:1], scale=1.0 / 64, reads=[pk, "tiny"], writes=[kx2])
                        yield
                        fw.v("reciprocal", x2[:], x2[:], reads=[kx2], writes=[kx2])
                        yield
                        fw.v("scalar_tensor_tensor", x_[:], x_[:], self.pcol(gname, 0), x2[:], ALU.mult, ALU.mult,
                             reads=[kx, "prm", kx2], writes=[kx])
                        yield
                        if kind == "q":
                            yield from rope(QT[gp][ct][:], "b_QT%d_%d" % (ct, gp))
                        else:
                            yield from rope(KT[ct][:, gc], ("b_KT", ct, g))
                    elif kind == "qi":
                        _, j, g = ch
                        gp = g % 2
                        yield from proj(wB, "wB", 1536 + j * 128, 1536 + (j + 1) * 128)
                        fw.act(x_[:], ps[:], AF.Copy, reads=[pk], writes=[kx])
                        yield
                        yield from rope(QI[gp][j][:], "b_QI%d_%d" % (j, gp))
                    elif kind == "ki":
                        _, g = ch
                        gc = slice(g * 512, g * 512 + 512)
                        yield from proj(wkd, "b_wkd", 0, 128)
                        fw.act(x_[:], ps[:], AF.Copy, reads=[pk], writes=[kx])
                        yield
                        yield from rope(KI[:, gc], ("b_KI", g))
                    elif kind == "sg":
                        _, ct, g = ch
                        gp = g % 2
                        yield from proj(wB, "wB", 1860 + ct * 128, 1860 + (ct + 1) * 128)
                        fw.act(SG[gp][ct][:], ps[:], AF.Silu, reads=[pk], writes=["b_SG%d_%d" % (ct, gp)])
                        yield
                    elif kind == "v":
                        _, tt, g = ch
                        gp = g % 2
                        tcols = slice(tt * 128, (tt + 1) * 128)
                        for k in range(8):
                            fw.mm(ps[:], hn[:, k, tcols], wB[:, k, 1024:1536], start=(k == 0), stop=(k == 7),
                                  reads=[("wB", k), "b_hn"], writes=[pk])
                            yield
                        fw.act(V[:, g * 4 + tt, :, 0:64], ps[:].rearrange("p (h i) -> p h i", i=64), AF.Copy, reads=[pk], writes=[("b_V", g)])
                        yield
                        for k in range(8):
                            fw.mm(ps[:, 0:4], hn[:, k, tcols], wB[:, k, 1856:1860], start=(k == 0), stop=(k == 7),
                                  reads=[("wB", k), "b_hn"], writes=[pk])
                            yield
                        fw.v("tensor_scalar", WI[gp][:, tt, :], ps[:, 0:4], 1.0 / 16, None, ALU.mult, reads=[pk], writes=["b_WI%d" % gp])
                        yield

            def prep_begin(g):
                gc = slice(g * 512, g * 512 + 512)
                if alias:
                    self.release(["b_score"])
                    self.acquire(tkeys)
                fw.dma(hn[:], self.hnT[:, :, gc].rearrange("k p s -> p k s"), reads=[("hnT", g)], writes=["b_hn"])
                fw.dma(cs[:], self.ropeT[:, :, gc].rearrange("k p s -> p k s"), reads=[("ropeT", g)], writes=["b_cs"])

            def prep_lanes(g):
                chains = []
                for ct in range(4):
                    chains += [("q", ct, g), ("k", ct, g)]
                chains += [("qi", 0, g), ("qi", 1, g), ("ki", g)]
                chains += [("sg", ct, g) for ct in range(4)]
                chains += [("v", tt, g) for tt in range(4)]
                return [lane(0, chains[0::2]), lane(1, chains[1::2])]

            def prep_end(g):
                if alias:
                    self.release(tkeys)
                    self.acquire(["b_score"])

            def scores(qt):
                g, tt = qt // 4, qt % 4
                gp = g % 2
                N = (qt + 1) * 128
                tq = slice(tt * 128, (tt + 1) * 128)
                for pc in range((N + 511) // 512):
                    p0 = pc * 512
                    pn = min(512, N - p0)
                    for ih in range(4):
                        po = (ih % 2) * 64
                        fw.mm(PS[ih][:, 0:pn], QI[gp][ih // 2][po:po + 64, tq], KI[po:po + 64, p0:p0 + pn],
                              reads=["b_QI%d_%d" % (ih // 2, gp), ("b_KI", pc)], writes=["ps%d" % ih])
                    for ih in range(4):
                        r_ = rl[ih % 2]
                        rkey = "b_rl%d" % (ih % 2)
                        fw.act(r_[:, 0:pn], PS[ih][:, 0:pn], AF.Relu, reads=["ps%d" % ih], writes=[rkey])
                        if ih == 0:
                            fw.v("tensor_scalar", score[:, p0:p0 + pn], r_[:, 0:pn], WI[gp][:, tt, 0:1], None, ALU.mult,
                                 reads=[rkey, "b_WI%d" % gp], writes=["b_score"])
                        else:
                            fw.v("scalar_tensor_tensor", score[:, p0:p0 + pn], r_[:, 0:pn], WI[gp][:, tt, ih:ih + 1], score[:, p0:p0 + pn],
                                 ALU.mult, ALU.add, reads=[rkey, "b_WI%d" % gp, "b_score"], writes=["b_score"])

            def bisect_mask(qt):
                NB = qt + 1
                N = NB * 128
                mk = mm1[0]
                mkey = "b_mm1_0"
                A, lo, mid, cnt, tmp = (bs[:, i:i + 1] for i in range(5))
                if NB >= 3:
                    fw.v("tensor_reduce", A, score[:, 0:N], AX.X, ALU.max, apply_absolute_value=True, reads=["b_score"], writes=["b_bs"])
                    fw.v("tensor_scalar", A, A, 1.0001, 1e-20, ALU.mult, ALU.add, reads=["b_bs"], writes=["b_bs"])
                fw.v("tensor_tensor", score[:, N - 128:N], score[:, N - 128:N], dmask[:], ALU.add, reads=["b_score", "b_dmask"],
                     writes=["b_score"])
                if NB >= 3:
                    fw.v("tensor_scalar", steps[:], self.cst_sb[:, 1:2 + NIT], A, None, ALU.mult, reads=["cst", "b_bs"], writes=["b_steps"])
                    fw.v("tensor_scalar", mid, A, -1.0, steps[:, 0:1], ALU.mult, ALU.add, reads=["b_bs", "b_steps"], writes=["b_bs"])
                    for it in range(NIT):
                        fw.v("tensor_scalar", mk[:, 0:N], score[:, 0:N], mid, None, ALU.is_ge, ALU.add, accum_out=cnt,
                             reads=["b_score", "b_bs", mkey], writes=[mkey, "b_bs"])
                        fw.v("tensor_scalar", tmp, cnt, 255.5, steps[:, it:it + 1], ALU.is_ge, ALU.mult, reads=["b_bs", "b_steps"], writes=["b_bs"])
                        fw.v("scalar_tensor_tensor", mid, tmp, steps[:, it + 1:it + 2], mid, ALU.subtract, ALU.add,
                             reads=["b_bs", "b_steps"], writes=["b_bs"])
                    fw.v("tensor_tensor", lo, mid, steps[:, NIT:NIT + 1], ALU.subtract, reads=["b_bs", "b_steps"], writes=["b_bs"])
                else:
                    fw.v("memset", lo, -1e29, writes=["b_bs"])
                fw.v("tensor_scalar", mk[:, 0:N], score[:, 0:N], lo, None, ALU.is_ge, reads=["b_score", "b_bs"], writes=[mkey])
                psb1 = PS[1][:].bitcast(BF16)
                mt, mtkey = MT[qt % 2], "b_MT%d" % (qt % 2)
                for kb0 in range(0, NB, 8):
                    nk = min(8, NB - kb0)
                    for j in range(nk):
                        kb = kb0 + j
                        fw.tr(psb1[:, j * 128:(j + 1) * 128], mk[:, kb * 128:(kb + 1) * 128], self.ident_b[:],
                              reads=[mkey, "k_ident"], writes=["ps1"])
                    fw.act(mt[:, kb0:kb0 + nk, :].rearrange("p a b -> p (a b)"), psb1[:, 0:nk * 128], AF.Copy, reads=["ps1"], writes=[mtkey])

            def attention_gen(qt):
                if NOATT:
                    return
                g, tt = qt // 4, qt % 4
                gp = g % 2
                NB = qt + 1
                tq = slice(tt * 128, (tt + 1) * 128)
                mt, mtkey = MT[qt % 2], "b_MT%d" % (qt % 2)
                for hpair in range(4):
                    ct = hpair
                    for gi, kb0 in enumerate(range(0, NB, 4)):
                        nk = min(4, NB - kb0)
                        bis = [2 * e + gi % 2 for e in range(2)]
                        for j in range(nk):
                            kb = kb0 + j
                            for e in range(2):
                                po = e * 64
                                pl = PS[2 + bis[e]]
                                fw.mm(pl[:, j * 128:(j + 1) * 128], KT[ct][po:po + 64, kb * 128:(kb + 1) * 128], QT[gp][ct][po:po + 64, tq],
                                      reads=[("b_KT", ct, kb // 4), "b_QT%d_%d" % (ct, gp)], writes=["ps%d" % (2 + bis[e])])
                                yield
                        for e in range(2):
                            bi = bis[e]
                            fw.act(E[bi][:, 0:nk * 128], PS[2 + bi][:, 0:nk * 128], AF.Exp, scale=0.125, reads=["ps%d" % (2 + bi)], writes=["b_E%d" % bi])
                            yield
                            fw.v("tensor_tensor", PT[bi][:, 0:nk * 128], E[bi][:, 0:nk * 128],
                                 mt[:, kb0:kb0 + nk, :].rearrange("p a b -> p (a b)"), ALU.mult,
                                 reads=["b_E%d" % bi, mtkey], writes=["b_PT%d" % bi], eng="gpsimd")
                            yield
                        for e in range(2):
                            h = 2 * hpair + e
                            bi = bis[e]
                            pob = PS[7] if e == 0 else PS[6]
                            pokey = "ps7" if e == 0 else "ps6"
                            osl = slice(hpair * 65, hpair * 65 + 65)
                            for j in range(nk):
                                kb = kb0 + j
                                fw.mm(pob[:, osl], PT[bi][:, j * 128:(j + 1) * 128], V[:, kb, h, :], start=(kb == 0), stop=(kb == NB - 1),
                                      reads=["b_PT%d" % bi, ("b_V", kb // 4)], writes=[pokey])
                                yield

            def final(qt):
                g, tt = qt // 4, qt % 4
                gp = g % 2
                tq = slice(tt * 128, (tt + 1) * 128)
                otok4 = otok[:].rearrange("p (a e) i -> p a e i", e=2)
                for hb_ in range(2):
                    pob = PS[7] if hb_ == 0 else PS[6]
                    pokey = "ps7" if hb_ == 0 else "ps6"
                    pv = pob[:, 0:260].rearrange("p (h i) -> p h i", i=65)
                    fw.v("reciprocal", rec[:, hb_ * 4:hb_ * 4 + 4], pv[:, :, 64], reads=[pokey], writes=["b_rec"])
                    fw.v("tensor_tensor", otok4[:, :, hb_, :], pv[:, :, 0:64],
                         rec[:, hb_ * 4:hb_ * 4 + 4].unsqueeze(2).to_broadcast([128, 4, 64]), ALU.mult,
                         reads=[pokey, "b_rec"], writes=["b_otok"])
                of = otok[:].rearrange("p h i -> p (h i)")
                for ct in range(4):
                    pb_ = PS[7 - ct // 2][:, 384:512].bitcast(BF16)
                    pkey = "ps%d" % (7 - ct // 2)
                    fw.tr(pb_[:, (ct % 2) * 128:(ct % 2 + 1) * 128], of[:, ct * 128:(ct + 1) * 128], self.ident_b[:],
                          reads=["b_otok", "k_ident"], writes=[pkey])
                for ct in range(4):
                    pb_ = PS[7 - ct // 2][:, 384:512].bitcast(BF16)
                    pkey = "ps%d" % (7 - ct // 2)
                    fw.v("tensor_tensor", yout[gp][:, ct, tq], pb_[:, (ct % 2) * 128:(ct % 2 + 1) * 128], SG[gp][ct][:, tq], ALU.mult,
                         reads=[pkey, "b_SG%d_%d" % (ct, gp)], writes=["b_yout0"])
                if tt == 3:
                    gc = slice(g * 512, g * 512 + 512)
                    fw.dma(self.yT[1][:, :, gc].rearrange("k p s -> p k s"), yout[gp][:], reads=["b_yout0"],
                           writes=[("yT1", g, ct) for ct in range(4)], eng="gpsimd")

            prep_begin(0)
            fw.lockstep(prep_lanes(0))
            prep_end(0)
            scores(0)
            bisect_mask(0)
            for qt in range(NT):
                nxt = qt + 1
                if nxt < NT:
                    if nxt % 4 == 0:
                        gn = nxt // 4
                        prep_begin(gn)
                        fw.lockstep(prep_lanes(gn))
                        prep_end(gn)
                    scores(nxt)
                fw.lockstep([attention_gen(qt)])
                if nxt < NT:
                    bisect_mask(nxt)
                final(qt)
            self.release(keys)

    def phase_C(self, l):
        fw, S, G = self.fw, self.S, self.G
        PS = self.PS
        with ExitStack() as ph:
            sb = lambda n, s, d: ph.enter_context(self.nc.sbuf_tensor(self.uname(n), list(s), d))
            wC = sb("wC", [128, 8, 1024], BF16)
            wr = sb("c_wr", [128, 4, 128], BF16)
            wi = sb("c_wi", [128, 4, 128], BF16)
            hn = [sb("c_hn%d" % i, [128, 8, 512], BF16) for i in range(2)]
            xbuf = sb("c_xbuf", [128, 4, 515], F32)
            hprev = sb("c_hprev", [128, 4], F32)
            cl = sb("c_cl", [128, 4], F32)
            xc = [sb("c_xc%d" % i, [128, 512], F32) for i in range(2)]
            xcb = [sb("c_xcb%d" % i, [128, 512], BF16) for i in range(2)]
            r_ = [sb("c_r%d" % i, [128, 512], F32) for i in range(2)]
            i_ = [sb("c_i%d" % i, [128, 512], F32) for i in range(2)]
            a_ = [sb("c_a%d" % i, [128, 512], F32) for i in range(2)]
            b_ = [sb("c_b%d" % i, [128, 512], F32) for i in range(2)]
            sg = [sb("c_sg%d" % i, [128, 512], F32) for i in range(2)]
            yo = [sb("c_y%d" % i, [128, 512], BF16) for i in range(2)]
            names = ["wC", "c_wr", "c_wi", "c_hn0", "c_hn1", "c_xbuf", "c_hprev", "c_cl"] + \
                    [n + str(i) for n in ("c_xc", "c_xcb", "c_r", "c_i", "c_a", "c_b", "c_sg", "c_y") for i in range(2)]
            keys = names + [("wC", k) for k in range(8)] + [("c_wr", k) for k in range(4)] + [("c_wi", k) for k in range(4)] + \
                ["c_xbuf%d" % i for i in range(4)] + ["c_hprev%d" % i for i in range(4)]
            self.acquire(keys + ["stg0", "stg1"])
            self.load_w(wC, "wC", lambda k: self.w_in[l, k * 128:(k + 1) * 128, OFF_C:OFF_C + 1024], 1024, 8, self.gcol)
            self.load_w(wr, "c_wr", lambda k: self.wr_bd[l, k], 128, 4)
            self.load_w(wi, "c_wi", lambda k: self.wi_bd[l, k], 128, 4)
            fw.act(cl[:], self.prm_sb[:, PCOLS["lam"][0]:PCOLS["lam"][0] + 4], AF.Exp, scale=-1.0, reads=["prm"], writes=["c_cl"])
            fw.act(cl[:], cl[:], AF.Ln, bias=1.0, reads=["c_cl"], writes=["c_cl"])
            fw.v("tensor_scalar", cl[:], cl[:], -8.0, None, ALU.mult, reads=["c_cl"], writes=["c_cl"])
            fw.v("memset", xbuf[:], 0.0, writes=["c_xbuf%d" % i for i in range(4)])
            fw.v("memset", hprev[:], 0.0, writes=["c_hprev%d" % i for i in range(4)])
            for g in range(G):
                c0 = g * 512
                hk = "c_hn%d" % (g % 2)
                hg = hn[g % 2]
                fw.dma(hg[:], self.hnT[:, :, c0:c0 + 512].rearrange("k p s -> p k s"), reads=[("hnT", g)], writes=[hk])
                def cbody(ct, g=g, c0=c0, hk=hk, hg=hg):
                    j = ct % 2
                    pb = 4 * j
                    px, pg, pr, pi = PS[pb], PS[pb + 1], PS[pb + 2], PS[pb + 3]
                    kx, kg, kr, ki = ["ps%d" % (pb + t) for t in range(4)]
                    for k in range(8):
                        fw.mm(px[:], wC[:, k, ct * 128:(ct + 1) * 128], hg[:, k, :], start=(k == 0), stop=(k == 7),
                              reads=[("wC", k), hk], writes=[kx])
                        yield
                    for k in range(8):
                        fw.mm(pg[:], wC[:, k, 512 + ct * 128:512 + (ct + 1) * 128], hg[:, k, :], start=(k == 0), stop=(k == 7),
                              reads=[("wC", k), hk], writes=[kg])
                        yield
                    xb = xbuf[:, ct, :]
                    fw.act(xb[:, 3:515], px[:], AF.Copy, reads=[kx], writes=["c_xbuf%d" % ct])
                    yield
                    cw = lambda i: self.pcol("conv_w", i * 4 + ct)
                    fw.v("tensor_scalar", xc[j][:], xb[:, 3:515], cw(3), self.pcol("conv_b", ct), ALU.mult, ALU.add,
                         reads=["c_xbuf%d" % ct, "prm"], writes=["c_xc%d" % j])
                    yield
                    for i in range(3):
                        fw.v("scalar_tensor_tensor", xc[j][:], xb[:, i:i + 512], cw(i), xc[j][:], ALU.mult, ALU.add,
                             reads=["c_xbuf%d" % ct, "prm", "c_xc%d" % j], writes=["c_xc%d" % j])
                        yield
                    fw.v("tensor_copy", xb[:, 0:3], xb[:, 512:515], reads=["c_xbuf%d" % ct], writes=["c_xbuf%d" % ct], eng="gpsimd")
                    yield
                    fw.v("tensor_copy", xcb[j][:], xc[j][:], reads=["c_xc%d" % j], writes=["c_xcb%d" % j], eng="gpsimd")
                    yield
                    fw.mm(pr[:], wr[:, ct, :], xcb[j][:], reads=[("c_wr", ct), "c_xcb%d" % j], writes=[kr])
                    yield
                    fw.mm(pi[:], wi[:, ct, :], xcb[j][:], reads=[("c_wi", ct), "c_xcb%d" % j], writes=[ki])
                    yield
                    fw.act(r_[j][:], pr[:], AF.Sigmoid, bias=self.pcol("b_r", ct), reads=[kr, "prm"], writes=["c_r%d" % j])
                    yield
                    fw.act(i_[j][:], pi[:], AF.Sigmoid, bias=self.pcol("b_i", ct), reads=[ki, "prm"], writes=["c_i%d" % j])
                    yield
                    fw.act(sg[j][:], pg[:], AF.Silu, reads=[kg], writes=["c_sg%d" % j])
                    yield
                    fw.act(a_[j][:], r_[j][:], AF.Exp, scale=cl[:, ct:ct + 1], reads=["c_r%d" % j, "c_cl"], writes=["c_a%d" % j])
                    yield
                    fw.v("tensor_tensor", b_[j][:], a_[j][:], a_[j][:], ALU.mult, reads=["c_a%d" % j], writes=["c_b%d" % j])
                    yield
                    fw.v("tensor_scalar", b_[j][:], b_[j][:], -1.0, 1.0, ALU.mult, ALU.add, reads=["c_b%d" % j], writes=["c_b%d" % j])
                    yield
                    fw.act(b_[j][:], b_[j][:], AF.Sqrt, reads=["c_b%d" % j], writes=["c_b%d" % j])
                    yield
                    fw.v("tensor_tensor", i_[j][:], i_[j][:], xc[j][:], ALU.mult, reads=["c_i%d" % j, "c_xc%d" % j],
                         writes=["c_i%d" % j], eng="gpsimd")
                    yield
                    fw.v("tensor_tensor", b_[j][:], b_[j][:], i_[j][:], ALU.mult, reads=["c_b%d" % j, "c_i%d" % j], writes=["c_b%d" % j])
                    yield
                    fw.v("tensor_tensor_scan", r_[j][:], a_[j][:], b_[j][:], hprev[:, ct:ct + 1], ALU.mult, ALU.add,
                         reads=["c_a%d" % j, "c_b%d" % j, "c_hprev%d" % ct, "c_r%d" % j], writes=["c_r%d" % j])
                    yield
                    fw.v("tensor_copy", hprev[:, ct:ct + 1], r_[j][:, 511:512], reads=["c_r%d" % j], writes=["c_hprev%d" % ct])
                    yield
                    fw.v("tensor_tensor", yo[j][:], r_[j][:], sg[j][:], ALU.mult, reads=["c_r%d" % j, "c_sg%d" % j],
                         writes=["c_y%d" % j], eng="gpsimd")
                    yield
                    fw.dma(self.yT[2][ct, :, c0:c0 + 512], yo[j][:], reads=["c_y%d" % j], writes=[("yT2", g, ct)], eng="gpsimd")
                    yield
                fw.lockstep([cbody(0), cbody(1)])
                fw.lockstep([cbody(2), cbody(3)])
            self.release(keys)

    def phase_M(self, l):
        fw, S, G, L = self.fw, self.S, self.G, self.L
        PS = self.PS
        last = (l == L - 1)
        with ExitStack() as ph:
            sb = lambda n, s, d: ph.enter_context(self.nc.sbuf_tensor(self.uname(n), list(s), d))
            wG = sb("wG", [128, 8, 3072], BF16)
            wbr = sb("wbr", [128, 12, 1024], BF16)
            wo = sb("wo", [128, 8, 1024], BF16)
            wpg = sb("wpg", [128, 8, 1024], BF16)
            wple = sb("wple", [128, 2, 1024], BF16)
            hn = sb("m_hn", [128, 8, 512], BF16)
            ys = [sb("m_y%d" % n, [128, 4, 512], BF16) for n in range(3)]
            hb = sb("m_h", [128, 8, 512], F32)
            h1b = sb("m_h1b", [128, 8, 512], BF16)
            pf = sb("m_pf", [128, 2, 512], F32)
            pb_ = sb("m_pb", [128, 2, 512], BF16)
            mrg = sb("m_mrg", [128, 8, 512], BF16)
            sgsP = [[sb("m_sg%d_%d" % (p, n), [128, 512], F32) for n in range(3)] for p in range(2)]
            sgs = sgsP[0]
            tmp = sb("m_tmp", [128, 2, 512], F32)
            self.rs_sb = sb("m_rs", [128, 512], F32)
            self.hn_out = h1b
            self.hn_out_key = "m_h1b"
            self.eps_col = sb("m_eps", [128, 1], F32)
            names = ["wG", "wbr", "wo", "wpg", "wple", "m_hn", "m_y0", "m_y1", "m_y2", "m_h", "m_h1b", "m_pf", "m_pb",
                     "m_mrg", "m_sg0_0", "m_sg0_1", "m_sg0_2", "m_sg1_0", "m_sg1_1", "m_sg1_2", ("m_tmp", 0), ("m_tmp", 1), "rs", "hn_out", "eps"]
            keys = names + [("wG", k) for k in range(8)] + [("wbr", k) for k in range(12)] + \
                [("wo", k) for k in range(8)] + [("wpg", k) for k in range(8)] + [("wple", k) for k in range(2)]
            self.acquire(keys + ["stg0", "stg1"])
            fw.v("memset", self.eps_col[:], NORM_EPS, writes=["eps"])
            self.load_w(wG, "wG", lambda k: self.w_in[l, k * 128:(k + 1) * 128, OFF_G:OFF_G + 3072], 3072, 8, self.gcol)
            self.load_w(wbr, "wbr", lambda k: self.w_branch[l, k // 4, (k % 4) * 128:(k % 4 + 1) * 128, :], 1024, 12)
            self.load_w(wo, "wo", lambda k: self.w_out[l, k * 128:(k + 1) * 128, :], 1024, 8)
            self.load_w(wpg, "wpg", lambda k: self.w_pg[l, k * 128:(k + 1) * 128, :], 1024, 8)
            self.load_w(wple, "wple", lambda k: self.w_ple[l, k * 128:(k + 1) * 128, :], 1024, 2)
            hsrc = self.xT if l == 0 else self.hT
            hdst = self.outT if last else self.hT
            for g in range(G):
                c0 = g * 512
                fw.dma(hn[:], self.hnT[:, :, c0:c0 + 512].rearrange("k p s -> p k s"), reads=[("hnT", g)], writes=["m_hn"])
                for n in range(3):
                    fw.dma(ys[n][:], self.yT[n][:, :, c0:c0 + 512].rearrange("k p s -> p k s"),
                           reads=[("yT%d" % n, g, ct) for ct in range(4)], writes=["m_y%d" % n])
                fw.dma(hb[:], hsrc[:, :, c0:c0 + 512].rearrange("k p s -> p k s"),
                       reads=([("hT", g)] if l > 0 else []), writes=["m_h"])
                fw.dma(pf[:], self.pT[l, :, :, c0:c0 + 512].rearrange("k p s -> p k s"), writes=["m_pf"])
                fw.v("tensor_copy", pb_[:], pf[:], reads=["m_pf"], writes=["m_pb"], eng="gpsimd")
                for dmt in range(8):
                    cs = slice(dmt * 128, (dmt + 1) * 128)
                    dp = dmt % 2
                    sg_ = sgsP[dp]
                    gb = 3 * (dmt % 2)
                    for n in range(3):
                        yb = 6 + (dmt * 3 + n) % 2
                        for k in range(8):
                            fw.mm(PS[gb + n][:], wG[:, k, n * 1024 + dmt * 128:n * 1024 + (dmt + 1) * 128], hn[:, k, :],
                                  start=(k == 0), stop=(k == 7), reads=[("wG", k), "m_hn"], writes=["ps%d" % (gb + n)])
                        for kc in range(4):
                            fw.mm(PS[yb][:], wbr[:, n * 4 + kc, cs], ys[n][:, kc, :], start=(kc == 0), stop=(kc == 3),
                                  reads=[("wbr", n * 4 + kc), "m_y%d" % n], writes=["ps%d" % yb])
                        fw.act(sg_[n][:], PS[gb + n][:], AF.Sigmoid, reads=["ps%d" % (gb + n)], writes=["m_sg%d_%d" % (dp, n)])
                        fw.v("tensor_tensor", sg_[n][:], PS[yb][:], sg_[n][:], ALU.mult,
                             reads=["ps%d" % yb, "m_sg%d_%d" % (dp, n)], writes=["m_sg%d_%d" % (dp, n)])
                    fw.v("tensor_tensor", sg_[0][:], sg_[0][:], sg_[1][:], ALU.add, reads=["m_sg%d_0" % dp, "m_sg%d_1" % dp], writes=["m_sg%d_0" % dp], eng="gpsimd")
                    fw.v("tensor_tensor", mrg[:, dmt, :], sg_[0][:], sg_[2][:], ALU.add, reads=["m_sg%d_0" % dp, "m_sg%d_2" % dp], writes=["m_mrg"], eng="gpsimd")
                for d2 in range(8):
                    pk = 6 + d2 % 2
                    for k in range(8):
                        fw.mm(PS[pk][:], wo[:, k, d2 * 128:(d2 + 1) * 128], mrg[:, k, :], start=(k == 0), stop=(k == 7),
                              reads=[("wo", k), "m_mrg"], writes=["ps%d" % pk])
                    fw.v("tensor_tensor", hb[:, d2, :], hb[:, d2, :], PS[pk][:], ALU.add, reads=["m_h", "ps%d" % pk], writes=["m_h"])
                fw.act(h1b[:], hb[:], AF.Copy, reads=["m_h"], writes=["m_h1b"])
                for d2 in range(8):
                    pa, pp = (0, 1) if d2 % 2 == 0 else (2, 3)
                    for k in range(8):
                        fw.mm(PS[pa][:], wpg[:, k, d2 * 128:(d2 + 1) * 128], h1b[:, k, :], start=(k == 0), stop=(k == 7),
                              reads=[("wpg", k), "m_h1b"], writes=["ps%d" % pa])
                    for k in range(2):
                        fw.mm(PS[pp][:], wple[:, k, d2 * 128:(d2 + 1) * 128], pb_[:, k, :], start=(k == 0), stop=(k == 1),
                              reads=[("wple", k), "m_pb"], writes=["ps%d" % pp])
                    sgk = d2 % 2
                    fw.act(sgs[sgk][:], PS[pa][:], AF.Sigmoid, reads=["ps%d" % pa], writes=["m_sg0_%d" % sgk])
                    fw.v("tensor_tensor", sgs[sgk][:], PS[pp][:], sgs[sgk][:], ALU.mult, reads=["ps%d" % pp, "m_sg0_%d" % sgk],
                         writes=["m_sg0_%d" % sgk])
                    fw.v("tensor_tensor", hb[:, d2, :], hb[:, d2, :], sgs[sgk][:], ALU.add, reads=["m_h", "m_sg0_%d" % sgk],
                         writes=["m_h"], eng="gpsimd")
                fw.dma(hdst[:, :, c0:c0 + 512].rearrange("k p s -> p k s"), hb[:], reads=["m_h"],
                       writes=[("outT" if last else "hT", g)], eng="gpsimd")
                if not last:
                    self.norm_group(hb, "m_h", g, tmp, "m_tmp")
            self.release(keys)


_CACHE = {}


def make_in_maps(inp, S, L, ncores):
    maps = []
    w_in = np.ascontiguousarray(np.asarray(inp["w_in"], np.float32)[:L])
    ki0 = OFF_B + 1792
    w_kidup = np.ascontiguousarray(np.concatenate([w_in[:, :, ki0:ki0 + 64], w_in[:, :, ki0:ki0 + 64]], axis=2))
    prm = np.stack([pack_params(inp, l) for l in range(L)])
    cst = np.zeros((128, 32), np.float32)
    invf = (np.float32(500000.0) ** (-(np.arange(0, 16, 2, dtype=np.float32) / np.float32(16)))).astype(np.float32)
    for p_ in range(128):
        if p_ % 64 < 16:
            cst[p_, 0] = invf[p_ % 8]
    cst[:, 1:25] = (2.0 ** (-np.arange(24, dtype=np.float64)))[None, :].astype(np.float32)
    ropeR = np.zeros((128, 128), np.float32)
    for m_ in range(128):
        if m_ % 64 < 8:
            ropeR[m_ + 8, m_] = -1.0
        elif m_ % 64 < 16:
            ropeR[m_ - 8, m_] = 1.0
    shared = {
        "cst": cst, "ropeR": ropeR,
        "prm": prm, "w_in": w_in, "w_kidup": w_kidup,
        "w2": np.ascontiguousarray(np.asarray(inp["rwkv_w2"], np.float32)[:L]),
        "a2": np.ascontiguousarray(np.asarray(inp["rwkv_a2"], np.float32)[:L]),
        "wr_bd": np.stack([blockdiag(inp["lru_w_r"][l]) for l in range(L)]),
        "wi_bd": np.stack([blockdiag(inp["lru_w_i"][l]) for l in range(L)]),
        "w_branch": np.ascontiguousarray(np.asarray(inp["w_branch"], np.float32)[:L]),
        "w_out": np.ascontiguousarray(np.asarray(inp["w_out"], np.float32)[:L]),
        "w_ple": np.ascontiguousarray(np.asarray(inp["w_ple"], np.float32)[:L]),
        "w_pg": np.ascontiguousarray(np.asarray(inp["w_ple_gate"], np.float32)[:L]),
    }
    x = np.asarray(inp["x"], np.float32)
    p = np.asarray(inp["p"], np.float32)
    pos = np.asarray(inp["positions"], np.int32)
    nb = x.shape[0]
    for c in range(ncores):
        b = (c // 2) % nb
        m = dict(shared)
        m["xT"] = np.ascontiguousarray(x[b].T.reshape(8, 128, S))
        m["pT"] = np.ascontiguousarray(np.stack([p[l, b].T.reshape(2, 128, S) for l in range(L)]))
        m["pos"] = np.ascontiguousarray(pos[b].reshape(1, S))
        maps.append(m)
    return maps


def kernel(**inputs):
    x = np.asarray(inputs["x"])
    B, S, _ = x.shape
    L = np.asarray(inputs["w_in"]).shape[0]
    key = (S, L)
    if key not in _CACHE:
        _CACHE[key] = Prog(S, L).build()
    nc = _CACHE[key]
    maps = make_in_maps(inputs, S, L, 8)
    res = run_bass_kernel_spmd(nc, maps, core_ids=list(range(8)))
    out = np.zeros((B, S, D), np.float32)
    for b in range(B):
        out[b] = res.results[2 * b]["outT"].reshape(D, S).T
    return out
```

```python
from contextlib import ExitStack
import numpy as np
import concourse.bass as bass
import concourse.mybir as mybir
from concourse.bass_utils import run_bass_kernel_spmd

F32 = mybir.dt.float32
BF16 = mybir.dt.bfloat16
I32 = mybir.dt.int32
AF = mybir.ActivationFunctionType
ALU = mybir.AluOpType
AX = mybir.AxisListType

ENGS = ("tensor", "vector", "scalar", "gpsimd", "sync")
N_DMA_SEMS = 24

D = 1024
DIN = 8644
OFF_A, OFF_B, OFF_C, OFF_G = 0, 2176, 4548, 5572
NORM_EPS = 1e-6
GN_EPS = 64e-5


class FW:
    def __init__(self, nc, stack, same_engine_sync=True):
        self.nc = nc
        self.stack = stack
        self.q = {e: [] for e in ENGS}
        self.cnt = {e: 0 for e in ENGS}
        self.sem = {e: stack.enter_context(nc.semaphore("s_" + e)) for e in ENGS}
        self.dsem = [stack.enter_context(nc.semaphore("d%d" % i)) for i in range(N_DMA_SEMS)]
        self.dcnt = [0] * N_DMA_SEMS
        self.dnext = 0
        self.seen = {e: {} for e in ENGS}
        self.lastw = {}
        self.readers = {}
        self.same = same_engine_sync
        self.ninst = 0
        self.rr = 0

    def sb(self, name, shape, dt):
        return self.stack.enter_context(self.nc.sbuf_tensor(name, list(shape), dt))

    def ps(self, name, shape, dt=F32):
        return self.stack.enter_context(self.nc.psum_tensor(name, list(shape), dt))

    def _deps(self, eng, reads, writes):
        ev = []
        for k in reads:
            if k in self.lastw:
                ev.append((self.lastw[k], True))
        for k in writes:
            if k in self.lastw:
                ev.append((self.lastw[k], False))
            ev.extend((e, False) for e in self.readers.get(k, ()))
        best = {}
        for (sname, sem, val, src), raw in ev:
            if src == eng and (eng == "tensor" or not self.same or not raw):
                continue
            if self.seen[eng].get(sname, 0) >= val:
                continue
            if sname not in best or best[sname][1] < val:
                best[sname] = (sem, val)
        waits = []
        for sname, (sem, val) in best.items():
            self.seen[eng][sname] = val
            waits.append((sem, val))
        return waits

    def _commit(self, event, reads, writes):
        for k in writes:
            self.lastw[k] = event
            self.readers[k] = []
        for k in reads:
            if k in writes:
                continue
            self.readers.setdefault(k, []).append(event)

    def op(self, eng, fn, reads=(), writes=()):
        waits = self._deps(eng, reads, writes)
        self.cnt[eng] += 1
        idx = self.cnt[eng]
        sem = self.sem[eng]
        self.q[eng].append((waits, fn, sem, 1))
        self._commit(("s_" + eng, sem, idx, eng), reads, writes)
        self.ninst += 1

    def dma(self, out, in_, reads=(), writes=(), eng="sync", **kw):
        lo, n = (0, 16) if eng == "sync" else (16, N_DMA_SEMS - 16)
        self.dnext_q = getattr(self, "dnext_q", {})
        i = self.dnext_q.get(eng, 0)
        self.dnext_q[eng] = (i + 1) % n
        slot = lo + i
        sem = self.dsem[slot]
        sname = "d%d" % slot
        waits = self._deps(eng, reads, writes)
        prev = self.dcnt[slot] * 16
        if prev and self.seen[eng].get(sname, 0) < prev:
            waits.append((sem, prev))
            self.seen[eng][sname] = prev
        self.dcnt[slot] += 1
        val = self.dcnt[slot] * 16
        self.q[eng].append((waits, lambda e: e.dma_start(out=out, in_=in_, **kw), sem, 16))
        self._commit((sname, sem, val, "dma"), reads, writes)
        self.ninst += 1

    def finish(self, keys, eng="sync"):
        waits = self._deps(eng, keys, ())
        self.q[eng].append((waits, None, None, 0))

    def emit(self):
        nc = self.nc
        with nc.Block() as block:
            for ename in ENGS:
                items = self.q[ename]
                if not items:
                    continue

                def body(e, items=items):
                    for waits, fn, sem, inc in items:
                        for (ws, wv) in waits:
                            e.wait_ge(ws, wv)
                        if fn is not None:
                            fn(e).then_inc(sem, inc)

                getattr(block, ename)(body)

    def mm(self, out, lhsT, rhs, start=True, stop=True, reads=(), writes=()):
        self.op("tensor", lambda e: e.matmul(out, lhsT, rhs, start=start, stop=stop), reads, writes)

    def tr(self, out, in_, ident, reads=(), writes=()):
        self.op("tensor", lambda e: e.transpose(out, in_, ident), reads, writes)

    def act(self, out, in_, func, bias=0.0, scale=1.0, reads=(), writes=(), accum_out=None):
        if accum_out is None:
            self.op("scalar", lambda e: e.activation(out, in_, func, bias=bias, scale=scale), reads, writes)
        else:
            self.op("scalar", lambda e: e.activation(out, in_, func, bias=bias, scale=scale,
                                                     accum_out=accum_out), reads, writes)

    def v(self, name, *args, reads=(), writes=(), eng="vector", **kw):
        self.op(eng, lambda e: getattr(e, name)(*args, **kw), reads, writes)

    @staticmethod
    def lockstep(gens):
        gens = list(gens)
        while gens:
            for g_ in list(gens):
                try:
                    next(g_)
                except StopIteration:
                    gens.remove(g_)

    def cast_eng(self):
        self.rr += 1
        return ("vector", "gpsimd")[self.rr % 2]


PCOLS = {}
_o = 0
for _n, _w in [("norm_g", 8), ("mu_r", 4), ("mu_k", 4), ("mu_v", 4), ("mu_g", 4), ("mu_wl", 1), ("mu_al", 1),
               ("w0", 4), ("a0", 4), ("k_k", 4), ("k_a", 4), ("gn_g", 4), ("gn_b", 4), ("r_k", 4),
               ("q_g", 1), ("k_g", 1),
               ("conv_w", 16), ("conv_b", 4), ("b_r", 4), ("b_i", 4), ("lam", 4)]:
    PCOLS[_n] = (_o, _w)
    _o += _w
NPRM = _o


def _col4(v):
    return np.ascontiguousarray(np.asarray(v, np.float32).reshape(4, 128).T)


def pack_params(inp, l):
    prm = np.zeros((128, NPRM), np.float32)

    def put(name, arr):
        o, w = PCOLS[name]
        prm[:arr.shape[0], o:o + w] = arr

    put("norm_g", np.asarray(inp["norm_g"][l], np.float32).reshape(8, 128).T)
    mu = np.asarray(inp["rwkv_mu"][l], np.float32)
    put("mu_r", _col4(mu[0:512])); put("mu_k", _col4(mu[512:1024])); put("mu_v", _col4(mu[1024:1536]))
    put("mu_wl", mu[1536:1600].reshape(64, 1)); put("mu_al", mu[1600:1664].reshape(64, 1))
    put("mu_g", _col4(mu[1664:2176]))
    put("w0", _col4(inp["rwkv_w0"][l])); put("a0", _col4(inp["rwkv_a0"][l]))
    put("k_k", _col4(inp["rwkv_k_k"][l])); put("k_a", _col4(inp["rwkv_k_a"][l]))
    put("gn_g", _col4(inp["rwkv_gn_g"][l])); put("gn_b", _col4(inp["rwkv_gn_b"][l]))
    put("r_k", _col4(np.asarray(inp["rwkv_r_k"][l]).reshape(512)))
    put("q_g", np.tile(np.asarray(inp["dsa_q_g"][l], np.float32), 2).reshape(128, 1))
    put("k_g", np.tile(np.asarray(inp["dsa_k_g"][l], np.float32), 2).reshape(128, 1))
    cw = np.asarray(inp["lru_conv_w"][l], np.float32)
    put("conv_w", np.concatenate([_col4(cw[i]) for i in range(4)], axis=1))
    put("conv_b", _col4(inp["lru_conv_b"][l])); put("b_r", _col4(inp["lru_b_r"][l]))
    put("b_i", _col4(inp["lru_b_i"][l])); put("lam", _col4(inp["lru_lambda"][l]))
    return prm


def blockdiag(w):
    w = np.asarray(w, np.float32)
    out = np.zeros((4, 128, 128), np.float32)
    for ct in range(4):
        out[ct, 0:64, 0:64] = w[2 * ct]
        out[ct, 64:128, 64:128] = w[2 * ct + 1]
    return out


class Prog:
    def __init__(self, S, L, phases="NACBM", dbg=()):
        self.S, self.L, self.phases, self.dbg = S, L, phases, dbg
        self.G = S // 512
        nc = self.nc = bass.Bass("TRN2", target_bir_lowering=False)
        dt = nc.dram_tensor
        self.xT = dt("xT", [8, 128, S], F32, kind="ExternalInput").ap()
        self.pT = dt("pT", [L, 2, 128, S], F32, kind="ExternalInput").ap()
        self.pos = dt("pos", [1, S], I32, kind="ExternalInput").ap()
        self.prm = dt("prm", [L, 128, NPRM], F32, kind="ExternalInput").ap()
        self.w_in = dt("w_in", [L, D, DIN], F32, kind="ExternalInput").ap()
        self.w_kidup = dt("w_kidup", [L, D, 128], F32, kind="ExternalInput").ap()
        self.w2 = dt("w2", [L, 64, 512], F32, kind="ExternalInput").ap()
        self.a2 = dt("a2", [L, 64, 512], F32, kind="ExternalInput").ap()
        self.wr_bd = dt("wr_bd", [L, 4, 128, 128], F32, kind="ExternalInput").ap()
        self.wi_bd = dt("wi_bd", [L, 4, 128, 128], F32, kind="ExternalInput").ap()
        self.w_branch = dt("w_branch", [L, 3, 512, D], F32, kind="ExternalInput").ap()
        self.w_out = dt("w_out", [L, D, D], F32, kind="ExternalInput").ap()
        self.w_ple = dt("w_ple", [L, 256, D], F32, kind="ExternalInput").ap()
        self.w_pg = dt("w_pg", [L, D, D], F32, kind="ExternalInput").ap()
        self.cst_d = dt("cst", [128, 32], F32, kind="ExternalInput").ap()
        self.ropeR_d = dt("ropeR", [128, 128], F32, kind="ExternalInput").ap()
        self.ropeT = dt("ropeT", [2, 128, S], F32, kind="Internal").ap()
        self.outT = dt("outT", [8, 128, S], F32, kind="ExternalOutput").ap()
        okind = lambda n: "ExternalOutput" if n in dbg else "Internal"
        self.hT = dt("hT", [8, 128, S], F32, kind=okind("hT")).ap()
        self.hnT = dt("hnT", [8, 128, S], BF16, kind=okind("hnT")).ap()
        self.yT = [dt("yT%d" % n, [4, 128, S], BF16, kind=okind("yT%d" % n)).ap() for n in range(3)]

    def uname(self, n):
        self._uid = getattr(self, "_uid", 0) + 1
        return "%s_u%d" % (n, self._uid)

    def pcol(self, name, j=0, rows=128):
        o, w = PCOLS[name]
        return self.prm_sb[0:rows, o + j:o + j + 1]

    def load_w(self, dst, key, src_fn, ncols, kt, scale=None, rows=128):
        fw = self.fw
        for k in range(kt):
            for c0 in range(0, ncols, 512):
                cn = min(512, ncols - c0)
                si = self.stg_i
                self.stg_i = (self.stg_i + 1) % 3
                stg = self.stg[si]
                fw.dma(stg[0:rows, 0:cn], src_fn(k)[:, c0:c0 + cn], writes=["stg%d" % si])
                self.cast_rr = getattr(self, "cast_rr", 0) + 1
                eng = ("vector", "scalar", "gpsimd")[self.cast_rr % 3]
                o_ap, i_ap = dst[0:rows, k, c0:c0 + cn], stg[0:rows, 0:cn]
                rk_ = ["stg%d" % si] + (["prm"] if scale is not None else [])
                if eng == "scalar":
                    fw.act(o_ap, i_ap, AF.Copy, scale=(scale(k) if scale is not None else 1.0), reads=rk_, writes=[(key, k)])
                elif scale is not None:
                    fw.v("tensor_scalar", o_ap, i_ap, scale(k), 0.0, ALU.mult, ALU.add, reads=rk_, writes=[(key, k)], eng=eng)
                else:
                    fw.v("tensor_copy", o_ap, i_ap, reads=rk_, writes=[(key, k)], eng=eng)

    def gcol(self, k):
        return self.pcol("norm_g", k)

    def build(self):
        nc = self.nc
        with ExitStack() as st:
            fw = self.fw = FW(nc, st)
            self.st = st
            self.stg = [fw.sb("stg%d" % i, [128, 512], F32) for i in range(3)]
            self.stg_i = 0
            self.prm_sb = fw.sb("prm_sb", [128, NPRM], F32)
            self.ones_f = fw.sb("ones_f", [128, 128], F32)
            fw.v("memset", self.ones_f[:], 1.0, writes=["ones_f"])
            self.PSALL = fw.ps("psall", [128, 8, 512], F32)
            self.PS = [self.PSALL[:, i, :] for i in range(8)]
            self.tiny_col = fw.sb("tiny_col", [128, 1], F32)
            self.gneps_col = fw.sb("gneps_col", [128, 1], F32)
            fw.v("memset", self.tiny_col[:], 1e-30, writes=["tiny"])
            fw.v("memset", self.gneps_col[:], GN_EPS, writes=["tiny"])
            self.eps6_col = fw.sb("eps6_col", [128, 1], F32)
            fw.v("memset", self.eps6_col[:], NORM_EPS, writes=["tiny"])
            self.cst_sb = fw.sb("cst_sb", [128, 32], F32)
            fw.dma(self.cst_sb[:], self.cst_d, writes=["cst"])
            self.make_consts()
            if "B" in self.phases:
                self.phase_R()
            with ExitStack() as zs:
                for n, ph_ in enumerate("ABC"):
                    if ph_ not in self.phases:
                        zt = zs.enter_context(self.nc.sbuf_tensor(self.uname("zt"), [128, 4, 512], BF16))
                        self.acquire(["zt%d" % n])
                        fw.v("memset", zt[:], 0.0, writes=["zt%d" % n])
                        for g in range(self.G):
                            fw.dma(self.yT[n][:, :, g * 512:(g + 1) * 512].rearrange("k p s -> p k s"), zt[:], reads=["zt%d" % n],
                                   writes=[("yT%d" % n, g, ct) for ct in range(4)])
                        self.release(["zt%d" % n])
            for l in range(self.L):
                fw.dma(self.prm_sb[:], self.prm[l], writes=["prm"])
                if l == 0 and "N" in self.phases:
                    self.phase_N0()
                if "A" in self.phases:
                    self.phase_A(l)
                if "C" in self.phases:
                    self.phase_C(l)
                if "B" in self.phases:
                    self.phase_B(l)
                if "M" in self.phases:
                    self.phase_M(l)
            fw.finish([("outT", g) for g in range(self.G)])
            fw.emit()
        return nc

    def norm_group(self, hbuf, hkey, g, tmp, tmpkey):
        fw, S = self.fw, self.S
        c0 = g * 512
        ps = self.PS[7]
        for k in range(8):
            fw.act(tmp[:, k % 2, :], hbuf[:, k, :], AF.Square, reads=[hkey], writes=[(tmpkey, k % 2)])
            fw.mm(ps[:], self.ones_f[:], tmp[:, k % 2, :], start=(k == 0), stop=(k == 7),
                  reads=["ones_f", (tmpkey, k % 2)], writes=["ps7"])
        rs = self.rs_sb
        fw.act(rs[:], ps[:], AF.Sqrt, bias=self.eps_col[:, 0:1], scale=1.0 / D, reads=["ps7", "eps"], writes=["rs"])
        fw.v("reciprocal", rs[:], rs[:], reads=["rs"], writes=["rs"])
        hn = self.hn_out
        fw.v("tensor_tensor", hn[:], hbuf[:], rs[:].unsqueeze(1).to_broadcast([128, 8, 512]), ALU.mult,
             reads=[hkey, "rs"], writes=[self.hn_out_key])
        fw.dma(self.hnT[:, :, c0:c0 + 512].rearrange("k p s -> p k s"), hn[:], reads=[self.hn_out_key],
               writes=[("hnT", g)], eng="gpsimd")

    def phase_N0(self):
        fw = self.fw
        with ExitStack() as ph:
            sb = lambda n, s, d: ph.enter_context(self.nc.sbuf_tensor(self.uname(n), list(s), d))
            hb = [sb("n0_h%d" % i, [128, 8, 512], F32) for i in range(2)]
            tmp = sb("n0_tmp", [128, 2, 512], F32)
            self.rs_sb = sb("n0_rs", [128, 512], F32)
            self.hn_out = sb("n0_hn", [128, 8, 512], BF16)
            self.hn_out_key = "hn_out"
            self.eps_col = sb("n0_eps", [128, 1], F32)
            self.acquire(["n0_h0", "n0_h1", ("n0_tmp", 0), ("n0_tmp", 1), "rs", "hn_out", "eps"])
            fw.v("memset", self.eps_col[:], NORM_EPS, writes=["eps"])
            for g in range(self.G):
                c0 = g * 512
                h = hb[g % 2]
                fw.dma(h[:], self.xT[:, :, c0:c0 + 512].rearrange("k p s -> p k s"), writes=["n0_h%d" % (g % 2)])
                self.norm_group(h, "n0_h%d" % (g % 2), g, tmp, "n0_tmp")
            self.release(["n0_h0", "n0_h1", ("n0_tmp", 0), ("n0_tmp", 1), "rs", "hn_out", "eps"])

    def release(self, keys):
        fw = self.fw
        ev = []
        for k in keys:
            if k in fw.lastw:
                ev.append(fw.lastw[k])
            ev.extend(fw.readers.get(k, ()))
        best = {}
        for e in getattr(fw, "pending_release", []) + ev:
            if e[0] not in best or best[e[0]][2] < e[2]:
                best[e[0]] = e
        fw.pending_release = list(best.values())

    def acquire(self, keys):
        fw = self.fw
        ev = getattr(fw, "pending_release", [])
        for k in keys:
            fw.readers.setdefault(k, []).extend(ev)


    def make_consts(self):
        fw = self.fw
        onesb = fw.sb("k_onesb", [128, 256], BF16)
        self.ident_b = fw.sb("k_ident", [128, 128], BF16)
        self.mask_ui = fw.sb("k_mask_ui", [128, 256], BF16)
        self.mask_sl = fw.sb("k_mask_sl", [128, 128], BF16)
        self.blk1 = fw.sb("k_blk1", [128, 128], F32)
        g = "gpsimd"
        fw.v("memset", onesb[:], 1.0, writes=["k_onesb"], eng=g)
        sel = lambda out, pat, cm, op, key: fw.op(g, lambda e: e.affine_select(out, onesb[:, 0:128], pat, op, 0.0, base=0,
                                                                                channel_multiplier=cm),
                                                  reads=["k_onesb"], writes=[key])
        sel(self.ident_b[:], [[-1, 128]], 1, ALU.is_equal, "k_ident")
        sel(self.mask_ui[:, 0:128], [[1, 128]], -1, ALU.is_gt, "k_mask_ui")
        sel(self.mask_ui[:, 128:256], [[1, 128]], -1, ALU.is_ge, "k_mask_ui")
        sel(self.mask_sl[:], [[-1, 128]], 1, ALU.is_gt, "k_mask_sl")
        fw.v("memset", self.blk1[:], 0.0, writes=["k_blk1"], eng=g)
        fw.v("memset", self.blk1[0:64, 0:64], 1.0, writes=["k_blk1"], eng=g)
        fw.v("memset", self.blk1[64:128, 64:128], 1.0, writes=["k_blk1"], eng=g)

    def phase_A(self, l):
        fw, S, G = self.fw, self.S, self.G
        PS = self.PS
        CDEC = 0.6065306597126334
        with ExitStack() as ph:
            allkeys = []

            def sb(n, s, d):
                allkeys.append(n)
                return ph.enter_context(self.nc.sbuf_tensor(self.uname(n), list(s), d))

            wA = sb("wA", [128, 8, 2176], BF16)
            w2b = sb("a_w2b", [64, 1, 512], BF16)
            a2b = sb("a_a2b", [64, 1, 512], BF16)
            hn = [sb("a_hn0", [128, 8, 512], BF16)] * 2
            omu = sb("a_omu", [128, NPRM], F32)
            prevc = sb("a_prevc", [128, 18], F32)
            ubP = [[sb("a_ub%d_%d" % (p, q), [128, 513], F32) for q in range(4)] for p in range(2)]
            usP = [[sb("a_us%d_%d" % (p, q), [128, 512], F32) for q in range(4)] for p in range(2)]
            ulo = [sb("a_ulo%d" % q, [64, 513], F32) for q in range(2)]
            twl = sb("a_twl", [64, 512], BF16)
            alb = sb("a_alb", [64, 512], BF16)
            tP = [[sb("a_t%d_%d" % (p, i), [128, 512], F32) for i in range(8)] for p in range(2)]
            t_ = tP[0]
            art = [sb("a_art%d" % ct, [128, 4, 2, 128], BF16) for ct in range(4)]
            bk = [sb("a_bk%d" % ct, [128, 2, 512], BF16) for ct in range(4)]
            vb = [sb("a_vb%d" % ct, [128, 512], BF16) for ct in range(4)]
            tok = [sb("a_tok%d" % ct, [128, 4, 3, 128], BF16) for ct in range(4)]
            bonus = [sb("a_bonus%d" % ct, [128, 512], BF16) for ct in range(4)]
            sgt = [sb("a_sg%d" % ct, [128, 512], BF16) for ct in range(4)]
            PC = sb("a_PC", [128, 4, 4], F32)
            T = sb("a_T", [128, 4, 64], F32)
            Tb = sb("a_Tb", [128, 4, 64], BF16)
            LAb = [sb("a_LAb%d" % i, [128, 4, 256], BF16) for i in range(2)]
            KAb = [sb("a_KAb%d" % i, [128, 4, 256], BF16) for i in range(2)]
            Lb = [sb("a_Lb%d" % i, [128, 4, 128], BF16) for i in range(2)]
            PPb = [[sb("a_PPb%d_%d" % (i, j), [128, 4, 256], BF16) for j in range(2)] for i in range(2)]
            XT = [[sb("a_XT%d_%d" % (i, j), [128, 4, 128], BF16) for j in range(2)] for i in range(2)]
            Wb = [sb("a_Wb%d" % i, [128, 4, 64], BF16) for i in range(2)]
            Ub = [sb("a_Ub%d" % i, [128, 4, 64], BF16) for i in range(2)]
            xc = sb("a_xc", [128, 8, 64], F32)
            sq = sb("a_sq", [128, 8, 64], F32)
            st8 = sb("a_st8", [128, 4, 8], F32)
            onb = sb("a_onb", [128, 512], BF16)
            yv = [sb("a_yv%d" % i, [128, 128], F32) for i in range(2)]
            yout = sb("a_yout", [128, 4, 512], BF16)
            self.rstm = sb("k_rstm", [128, 512], F32)
            keys = allkeys + [("wA", k) for k in range(8)] + [("a_w2b", 0), ("a_a2b", 0)]
            self.acquire(keys + ["stg0", "stg1"])
            fw.v("memset", self.rstm[:], 1.0, writes=["k_rstm"], eng="gpsimd")
            for c in range(4):
                fw.v("memset", self.rstm[:, c * 128:c * 128 + 1], 0.0, writes=["k_rstm"], eng="gpsimd")

            self.load_w(wA, "wA", lambda k: self.w_in[l, k * 128:(k + 1) * 128, OFF_A:OFF_A + 2176], 2176, 8, self.gcol)
            self.load_w(w2b, "a_w2b", lambda k: self.w2[l], 512, 1, rows=64)
            self.load_w(a2b, "a_a2b", lambda k: self.a2[l], 512, 1, rows=64)
            fw.v("tensor_scalar", omu[:], self.prm_sb[:], -1.0, 1.0, ALU.mult, ALU.add, reads=["prm"], writes=["a_omu"])
            fw.v("memset", prevc[:], 0.0, writes=["a_prevc"])
            fw.v("memset", T[:], 0.0, writes=["a_T"])
            fw.v("memset", Tb[:], 0.0, writes=["a_Tb"])
            oc = lambda name, j=0, rows=128: omu[0:rows, PCOLS[name][0] + j:PCOLS[name][0] + j + 1]
            psb0 = PS[0][:].bitcast(BF16)
            psb1 = PS[1][:].bitcast(BF16)

            def shift(ps, pskey, ubt, ubkey, pcol, out, okey, mu_ap, omu_ap, rows=128):
                fw.v("tensor_copy", ubt[0:rows, 0:1], prevc[0:rows, pcol:pcol + 1], reads=["a_prevc"], writes=[ubkey], eng="gpsimd")
                fw.act(ubt[0:rows, 1:513], ps, AF.Copy, reads=[pskey], writes=[ubkey])
                fw.v("tensor_copy", prevc[0:rows, pcol:pcol + 1], ubt[0:rows, 512:513], reads=[ubkey], writes=["a_prevc"], eng="gpsimd")
                fw.v("tensor_scalar", out, ubt[0:rows, 0:512], mu_ap, None, ALU.mult, reads=[ubkey, "prm"], writes=[okey])
                fw.v("scalar_tensor_tensor", out, ubt[0:rows, 1:513], omu_ap, out, ALU.mult, ALU.add,
                     reads=[ubkey, "a_omu", okey], writes=[okey])

            for g in range(G):
                c0 = g * 512
                hk = "a_hn0"
                hg = hn[0]
                fw.dma(hg[:], self.hnT[:, :, c0:c0 + 512].rearrange("k p s -> p k s"), reads=[("hnT", g)], writes=[hk])
                for q, (coff, nm) in enumerate([(1536, "mu_wl"), (1600, "mu_al")]):
                    for k in range(8):
                        fw.mm(PS[q][0:64, :], wA[:, k, coff:coff + 64], hg[:, k, :], start=(k == 0), stop=(k == 7),
                              reads=[("wA", k), hk], writes=["ps%d" % q])
                    shift(PS[q][0:64, :], "ps%d" % q, ulo[q], "a_ulo%d" % q, 16 + q, t_[q][0:64, :], "a_t0_%d" % q,
                          self.pcol(nm, 0, 64), oc(nm, 0, 64), rows=64)
                fw.act(twl[:], t_[0][0:64, :], AF.Tanh, reads=["a_t0_0"], writes=["a_twl"])
                fw.v("tensor_copy", alb[:], t_[1][0:64, :], reads=["a_t0_1"], writes=["a_alb"])
                def abody(ct, g=g, hk=hk, hg=hg):
                    p_ = ct % 2
                    PSp = PS[4 * p_:4 * p_ + 4]
                    pk = lambda q: "ps%d" % (4 * p_ + q)
                    ub, us, t_ = ubP[p_], usP[p_], tP[p_]
                    psb0 = PSp[0][:].bitcast(BF16)
                    psb1 = PSp[1][:].bitcast(BF16)
                    cs = slice(ct * 128, (ct + 1) * 128)
                    for q, (coff, nm) in enumerate([(0, "mu_r"), (512, "mu_k"), (1024, "mu_v"), (1664, "mu_g")]):
                        for k in range(8):
                            fw.mm(PSp[q][:], wA[:, k, coff + ct * 128:coff + (ct + 1) * 128], hg[:, k, :], start=(k == 0), stop=(k == 7),
                                  reads=[("wA", k), hk], writes=[pk(q)])
                            yield
                        shift(PSp[q][:], pk(q), ub[q], "a_ub%d_%d" % (p_, q), ct * 4 + q, us[q][:], "a_us%d_%d" % (p_, q),
                              self.pcol(nm, ct), oc(nm, ct))
                        yield
                    r_s, k_s, v_s, g_s = us
                    K = lambda i: "a_t%d_%d" % (p_, i)
                    fw.mm(PSp[0][:], w2b[:, 0, cs], twl[:], reads=[("a_w2b", 0), "a_twl"], writes=[pk(0)])
                    yield
                    fw.act(t_[0][:], PSp[0][:], AF.Sigmoid, bias=self.pcol("w0", ct), reads=[pk(0), "prm"], writes=[K(0)])
                    yield
                    fw.v("tensor_scalar", t_[0][:], t_[0][:], -CDEC, 0.0, ALU.mult, ALU.add, reads=[K(0)], writes=[K(0)], eng="gpsimd")
                    yield
                    fw.mm(PSp[1][:], a2b[:, 0, cs], alb[:], reads=[("a_a2b", 0), "a_alb"], writes=[pk(1)])
                    yield
                    fw.act(t_[1][:], PSp[1][:], AF.Sigmoid, bias=self.pcol("a0", ct), reads=[pk(1), "prm"], writes=[K(1)])
                    yield
                    fw.v("tensor_scalar", t_[2][:], k_s[:], self.pcol("k_k", ct), None, ALU.mult, reads=["a_us%d_1" % p_, "prm"], writes=[K(2)])
                    yield
                    fw.v("tensor_tensor", t_[3][:], t_[2][:], t_[2][:], ALU.mult, reads=[K(2)], writes=[K(3)], eng="gpsimd")
                    yield
                    fw.mm(PSp[2][:], self.blk1[:], t_[3][:], reads=["k_blk1", K(3)], writes=[pk(2)])
                    yield
                    fw.act(t_[3][:], PSp[2][:], AF.Sqrt, bias=self.tiny_col[:, 0:1], reads=[pk(2), "tiny"], writes=[K(3)])
                    yield
                    fw.v("reciprocal", t_[3][:], t_[3][:], reads=[K(3)], writes=[K(3)])
                    yield
                    fw.v("tensor_tensor", t_[2][:], t_[2][:], t_[3][:], ALU.mult, reads=[K(2), K(3)], writes=[K(2)])
                    yield
                    fw.v("tensor_scalar", t_[3][:], t_[1][:], self.pcol("k_a", ct), oc("k_a", ct), ALU.mult, ALU.add,
                         reads=[K(1), "prm", "a_omu"], writes=[K(3)])
                    yield
                    fw.v("tensor_tensor", t_[3][:], t_[3][:], k_s[:], ALU.mult, reads=[K(3), "a_us%d_1" % p_], writes=[K(3)], eng="gpsimd")
                    yield
                    fw.v("tensor_tensor", t_[4][:], t_[2][:], t_[1][:], ALU.mult, reads=[K(2), K(1)], writes=[K(4)], eng="gpsimd")
                    yield
                    fw.v("tensor_tensor_scan", t_[5][:], self.rstm[:], t_[0][:], 0.0, ALU.mult, ALU.add,
                         reads=["k_rstm", K(0)], writes=[K(5)])
                    yield
                    fw.v("tensor_tensor", t_[6][:], t_[5][:], t_[0][:], ALU.subtract, reads=[K(5), K(0)], writes=[K(6)], eng="gpsimd")
                    yield
                    fw.act(t_[6][:], t_[6][:], AF.Exp, reads=[K(6)], writes=[K(6)])
                    yield
                    fw.act(t_[7][:], t_[5][:], AF.Exp, scale=-1.0, reads=[K(5)], writes=[K(7)])
                    yield
                    fw.act(t_[5][:], t_[5][:], AF.Exp, reads=[K(5)], writes=[K(5)])
                    yield
                    fw.v("tensor_copy", PC[:, ct, :], t_[5][:].rearrange("p (c t) -> p c t", t=128)[:, :, 127], reads=[K(5)],
                         writes=["a_PC"], eng="gpsimd")
                    yield
                    v3 = lambda ap: ap.rearrange("p (c t) -> p c t", t=128)
                    akey = "a_art%d" % ct
                    fw.v("scalar_tensor_tensor", art[ct][:, :, 0, :], v3(t_[2][:]), -1.0, v3(t_[6][:]), ALU.mult, ALU.mult,
                         reads=[K(2), K(6)], writes=[akey])
                    yield
                    fw.v("tensor_tensor", art[ct][:, :, 1, :], v3(r_s[:]), v3(t_[5][:]), ALU.mult, reads=["a_us%d_0" % p_, K(5)], writes=[akey])
                    yield
                    fw.v("tensor_tensor", bk[ct][:, 0, :], t_[4][:], t_[7][:], ALU.mult, reads=[K(4), K(7)], writes=["a_bk%d" % ct])
                    yield
                    fw.v("tensor_tensor", bk[ct][:, 1, :], t_[3][:], t_[7][:], ALU.mult, reads=[K(3), K(7)], writes=["a_bk%d" % ct], eng="gpsimd")
                    yield
                    fw.v("tensor_copy", vb[ct][:], v_s[:], reads=["a_us%d_2" % p_], writes=["a_vb%d" % ct], eng="gpsimd")
                    yield
                    fw.v("scalar_tensor_tensor", t_[4][:], r_s[:], self.pcol("r_k", ct), t_[3][:], ALU.mult, ALU.mult,
                         reads=["a_us%d_0" % p_, "prm", K(3), K(4)], writes=[K(4)])
                    yield
                    fw.mm(PSp[3][:], self.blk1[:], t_[4][:], reads=["k_blk1", K(4)], writes=[pk(3)])
                    yield
                    fw.v("tensor_tensor", bonus[ct][:], PSp[3][:], v_s[:], ALU.mult, reads=[pk(3), "a_us%d_2" % p_], writes=["a_bonus%d" % ct])
                    yield
                    fw.act(sgt[ct][:], g_s[:], AF.Silu, reads=["a_us%d_3" % p_], writes=["a_sg%d" % ct])
                    yield
                    for half in range(2):
                        psb, pkey = (psb0, pk(0)) if half == 0 else (psb1, pk(1))
                        for cc in range(2):
                            c = half * 2 + cc
                            for qi_, (src, skey) in enumerate([(bk[ct][:, 0, c * 128:(c + 1) * 128], "a_bk%d" % ct),
                                                               (bk[ct][:, 1, c * 128:(c + 1) * 128], "a_bk%d" % ct),
                                                               (vb[ct][:, c * 128:(c + 1) * 128], "a_vb%d" % ct)]):
                                o = (cc * 3 + qi_) * 128
                                fw.tr(psb[:, o:o + 128], src, self.ident_b[:], reads=[skey, "k_ident"], writes=[pkey])
                                yield
                        fw.act(tok[ct][:, half * 2:half * 2 + 2, :, :].rearrange("p a b c -> p (a b c)"), psb[:, 0:768], AF.Copy,
                               reads=[pkey], writes=["a_tok%d" % ct])
                        yield

                fw.lockstep([abody(0), abody(1)])
                fw.lockstep([abody(2), abody(3)])
                PSALL = self.PSALL
                idb4 = self.ident_b[:].unsqueeze(1).to_broadcast([128, 4, 128])
                mui2 = self.mask_ui[:].unsqueeze(1).to_broadcast([128, 2, 256])
                msl4 = self.mask_sl[:].unsqueeze(1).to_broadcast([128, 4, 128])
                for c in range(4):
                    ccols = slice(c * 128, (c + 1) * 128)

                    def hv(qd, hi):
                        h = 2 * hi + qd
                        ct, hp = hi, qd
                        pr_ = slice(hp * 64, hp * 64 + 64)
                        d = dict(h=h, ct=ct, hp=hp, pr=pr_, po=hp * 64,
                                 at=art[ct][pr_, c, 0, :], rt=art[ct][pr_, c, 1, :],
                                 ar=art[ct][pr_, c, :, :].rearrange("p a t -> p (a t)"),
                                 bt=bk[ct][pr_, 0, ccols], kt=bk[ct][pr_, 1, ccols],
                                 rk=["a_art%d" % ct, "a_bk%d" % ct], tkey="a_tok%d" % ct,
                                 vt=tok[ct][:, c, 2, hp * 64:hp * 64 + 64], btk=tok[ct][:, c, 0, hp * 64:hp * 64 + 64],
                                 ktk=tok[ct][:, c, 1, hp * 64:hp * 64 + 64], T0b=Tb[pr_, ct, :])
                        return d

                    XYk = lambda qd: ["ps%d" % (3 * qd), "ps%d" % (3 * qd + 1)]
                    Zk = lambda qd: ["ps%d" % (3 * qd + 2)]
                    XY = lambda qd: PSALL[:, 3 * qd:3 * qd + 2, :].rearrange("p b (h x) -> p (b h) x", x=256)
                    Zv = lambda qd: PSALL[:, 3 * qd + 2, :].rearrange("p (h x) -> p h x", x=128)
                    for qd in range(2):
                        for hi in range(4):
                            d = hv(qd, hi)
                            fw.mm(XY(qd)[:, hi, :], d["bt"], d["ar"], reads=d["rk"], writes=[XYk(qd)[hi // 2]])
                            fw.mm(Zv(qd)[:, hi, :], d["at"], d["bt"], reads=d["rk"], writes=Zk(qd))
                    for qd in range(2):
                        for b2 in range(2):
                            fw.v("tensor_tensor", LAb[qd][:, 2 * b2:2 * b2 + 2, :], XY(qd)[:, 2 * b2:2 * b2 + 2, :], mui2, ALU.mult,
                                 reads=[XYk(qd)[b2], "k_mask_ui"], writes=["a_LAb%d" % qd])
                        fw.v("tensor_tensor", Lb[qd][:], Zv(qd), msl4, ALU.mult, reads=Zk(qd) + ["k_mask_sl"], writes=["a_Lb%d" % qd])
                        fw.v("tensor_tensor", XT[qd][0][:], LAb[qd][:, :, 0:128], idb4, ALU.add,
                             reads=["a_LAb%d" % qd, "k_ident"], writes=["a_XT%d_0" % qd], eng="gpsimd")
                    for k in range(1, 8):
                        for qd in range(2):
                            if k == 1:
                                Pp, PTp, pkeys = (lambda hi: Lb[qd][:, hi, :]), (lambda hi: LAb[qd][:, hi, 0:128]), ["a_Lb%d" % qd, "a_LAb%d" % qd]
                            else:
                                pb_ = PPb[qd][(k - 1) % 2]
                                Pp, PTp, pkeys = (lambda hi, pb_=pb_: pb_[:, hi, 0:128]), (lambda hi, pb_=pb_: pb_[:, hi, 128:256]), ["a_PPb%d_%d" % (qd, (k - 1) % 2)]
                            for hi in range(4):
                                if k <= 6:
                                    fw.mm(XY(qd)[:, hi, 0:128], PTp(hi), Pp(hi), reads=pkeys, writes=[XYk(qd)[hi // 2]])
                                    fw.mm(XY(qd)[:, hi, 128:256], Pp(hi), PTp(hi), reads=pkeys, writes=[XYk(qd)[hi // 2]])
                                if k == 7:
                                    d7 = hv(qd, hi)
                                    fw.mm(XY(qd)[:, hi, :], d7["kt"], d7["ar"], reads=d7["rk"], writes=[XYk(qd)[hi // 2]])
                                if k >= 2:
                                    xo = XT[qd][(k - 2) % 2]
                                    xok = "a_XT%d_%d" % (qd, (k - 2) % 2)
                                    fw.mm(Zv(qd)[:, hi, :], self.ident_b[:], xo[:, hi, :], start=True, stop=False, reads=["k_ident", xok], writes=Zk(qd))
                                    fw.mm(Zv(qd)[:, hi, :], Pp(hi), xo[:, hi, :], start=False, stop=True, reads=pkeys + [xok], writes=Zk(qd))
                        for qd in range(2):
                            if k <= 6:
                                for b2 in range(2):
                                    fw.act(PPb[qd][k % 2][:, 2 * b2:2 * b2 + 2, :], XY(qd)[:, 2 * b2:2 * b2 + 2, :], AF.Copy,
                                           reads=[XYk(qd)[b2]], writes=["a_PPb%d_%d" % (qd, k % 2)])
                            if k >= 2:
                                fw.v("tensor_copy", XT[qd][(k - 1) % 2][:], Zv(qd), reads=Zk(qd), writes=["a_XT%d_%d" % (qd, (k - 1) % 2)])
                            if k == 7:
                                for b2 in range(2):
                                    fw.v("tensor_tensor", KAb[qd][:, 2 * b2:2 * b2 + 2, :], XY(qd)[:, 2 * b2:2 * b2 + 2, :], mui2, ALU.mult,
                                         reads=[XYk(qd)[b2], "k_mask_ui"], writes=["a_KAb%d" % qd])
                    XTf = [XT[qd][0] for qd in range(2)]
                    xfk = ["a_XT%d_0" % qd for qd in range(2)]
                    Wv = lambda qd: PSALL[:, 3 * qd + 2, 0:256].rearrange("p (h x) -> p h x", x=64)
                    Uv = lambda qd: PSALL[:, 3 * qd + 2, 256:512].rearrange("p (h x) -> p h x", x=64)
                    for qd in range(2):
                        for hi in range(4):
                            d = hv(qd, hi)
                            fw.mm(Wv(qd)[:, hi, :], d["at"], d["T0b"], start=True, stop=False, reads=["a_art%d" % d["ct"], "a_Tb"], writes=Zk(qd))
                            fw.mm(Wv(qd)[:, hi, :], KAb[qd][:, hi, 0:128], d["vt"], start=False, stop=True, reads=["a_KAb%d" % qd, d["tkey"]], writes=Zk(qd))
                        fw.v("tensor_copy", Wb[qd][:], Wv(qd), reads=Zk(qd), writes=["a_Wb%d" % qd])
                    for qd in range(2):
                        for hi in range(4):
                            fw.mm(Uv(qd)[:, hi, :], XTf[qd][:, hi, :], Wb[qd][:, hi, :], reads=[xfk[qd], "a_Wb%d" % qd], writes=Zk(qd))
                        fw.v("tensor_copy", Ub[qd][:], Uv(qd), reads=Zk(qd), writes=["a_Ub%d" % qd])
                    for qd in range(2):
                        for hi in range(4):
                            d = hv(qd, hi)
                            h, ct = d["h"], d["ct"]
                            ob, okey = (PS[6], "ps6") if qd == 0 else (PS[7], "ps7")
                            osl = slice(qd * 256 + hi * 64, qd * 256 + (hi + 1) * 64)
                            fw.mm(ob[:, osl], d["rt"], d["T0b"], start=True, stop=False, reads=["a_art%d" % ct, "a_Tb"], writes=[okey])
                            fw.mm(ob[:, osl], LAb[qd][:, hi, 128:256], Ub[qd][:, hi, :], start=False, stop=False,
                                  reads=["a_LAb%d" % qd, "a_Ub%d" % qd], writes=[okey])
                            fw.mm(ob[:, osl], KAb[qd][:, hi, 128:256], d["vt"], start=False, stop=True, reads=["a_KAb%d" % qd, d["tkey"]], writes=[okey])
                            zsl = slice(ct * 64, (ct + 1) * 64)
                            fw.mm(PS[7][d["pr"], zsl], d["btk"], Ub[qd][:, hi, :], start=True, stop=False, reads=[d["tkey"], "a_Ub%d" % qd], writes=["ps7"])
                            fw.mm(PS[7][d["pr"], zsl], d["ktk"], d["vt"], start=False, stop=True, reads=[d["tkey"]], writes=["ps7"])
                    zall = PS[7][:, 0:256].rearrange("p (c i) -> p c i", i=64)
                    fw.v("tensor_tensor", T[:], T[:], zall, ALU.add, reads=["a_T", "ps7"], writes=["a_T"])
                    fw.v("tensor_tensor", T[:], T[:], PC[:, :, c:c + 1].to_broadcast([128, 4, 64]), ALU.mult, reads=["a_T", "a_PC"], writes=["a_T"])
                    fw.v("tensor_copy", Tb[:], T[:], reads=["a_T"], writes=["a_Tb"], eng="gpsimd")
                    ov = [PS[6][:, 0:256].rearrange("p (h i) -> p h i", i=64), PS[7][:, 256:512].rearrange("p (h i) -> p h i", i=64)]
                    okeys = ["ps6", "ps7"]
                    for qd in range(2):
                        fw.v("tensor_reduce", st8[:, 0, qd * 4:qd * 4 + 4], ov[qd], AX.X, ALU.add, reads=[okeys[qd]], writes=["a_st8"])
                    fw.v("tensor_scalar", st8[:, 0, :], st8[:, 0, :], 1.0 / 64, None, ALU.mult, reads=["a_st8"], writes=["a_st8"])
                    for qd in range(2):
                        fw.v("tensor_tensor", xc[:, qd * 4:qd * 4 + 4, :], ov[qd],
                             st8[:, 0, qd * 4:qd * 4 + 4].unsqueeze(2).to_broadcast([128, 4, 64]), ALU.subtract,
                             reads=[okeys[qd], "a_st8"], writes=["a_xc"])
                    fw.v("tensor_tensor", sq[:], xc[:], xc[:], ALU.mult, reads=["a_xc"], writes=["a_sq"], eng="gpsimd")
                    fw.v("tensor_reduce", st8[:, 1, :], sq[:], AX.X, ALU.add, reads=["a_sq"], writes=["a_st8"])
                    fw.act(st8[:, 1, :], st8[:, 1, :], AF.Sqrt, bias=self.gneps_col[:, 0:1], scale=1.0 / 64, reads=["a_st8", "tiny"], writes=["a_st8"])
                    fw.v("reciprocal", st8[:, 1, :], st8[:, 1, :], reads=["a_st8"], writes=["a_st8"])
                    fw.v("tensor_tensor", onb[:].rearrange("p (h i) -> p h i", i=64), xc[:],
                         st8[:, 1, :].unsqueeze(2).to_broadcast([128, 8, 64]), ALU.mult, reads=["a_xc", "a_st8"], writes=["a_onb"])
                    for ct in range(4):
                        for hp in range(2):
                            qo = (hp * 4 + ct) * 64
                            fw.tr(psb0[hp * 64:(hp + 1) * 64, ct * 128:(ct + 1) * 128], onb[:, qo:qo + 64], self.ident_b[:],
                                  reads=["a_onb", "k_ident"], writes=["ps0"])
                    for ct in range(4):
                        j = ct % 2
                        fw.v("tensor_scalar", yv[j][:], psb0[:, ct * 128:(ct + 1) * 128], self.pcol("gn_g", ct), self.pcol("gn_b", ct),
                             ALU.mult, ALU.add, reads=["ps0", "prm"], writes=["a_yv%d" % j])
                        fw.v("tensor_tensor", yv[j][:], yv[j][:], bonus[ct][:, ccols], ALU.add, reads=["a_yv%d" % j, "a_bonus%d" % ct],
                             writes=["a_yv%d" % j], eng="gpsimd")
                        fw.v("tensor_tensor", yout[:, ct, ccols], yv[j][:], sgt[ct][:, ccols], ALU.mult,
                             reads=["a_yv%d" % j, "a_sg%d" % ct], writes=["a_yout"], eng="gpsimd")
                fw.dma(self.yT[0][:, :, c0:c0 + 512].rearrange("k p s -> p k s"), yout[:], reads=["a_yout"],
                       writes=[("yT0", g, ct) for ct in range(4)], eng="gpsimd")
            self.release(keys)


    def phase_R(self):
        fw, S, G = self.fw, self.S, self.G
        TWO_PI = 6.283185307179586
        C1 = 6.28125
        C2 = 0.0019350051879882812
        C3 = TWO_PI - C1 - C2
        PI = 3.1415925
        with ExitStack() as ph:
            sb = lambda n, s, d: ph.enter_context(self.nc.sbuf_tensor(self.uname(n), list(s), d))
            posi = sb("r_posi", [128, 512], I32)
            a = sb("r_a", [128, 512], F32)
            k = sb("r_k", [128, 512], F32)
            r = sb("r_r", [128, 512], F32)
            r2 = sb("r_r2", [128, 512], F32)
            m = sb("r_m", [128, 512], F32)
            cs = sb("r_cs", [128, 2, 512], F32)
            keys = ["r_posi", "r_a", "r_k", "r_r", "r_r2", "r_m", "r_cs"]
            self.acquire(keys)
            for g in range(G):
                c0 = g * 512
                fw.dma(posi[:], self.pos[0:1, c0:c0 + 512].to_broadcast([128, 512]), writes=["r_posi"])
                fw.v("tensor_copy", a[:], posi[:], reads=["r_posi"], writes=["r_a"])
                fw.v("tensor_scalar", a[:], a[:], self.cst_sb[:, 0:1], None, ALU.mult, reads=["r_a", "cst"], writes=["r_a"])
                fw.v("tensor_scalar", k[:], a[:], 1.0 / TWO_PI, None, ALU.mult, reads=["r_a"], writes=["r_k"])
                fw.v("tensor_scalar", k[:], k[:], 12582912.0, None, ALU.add, reads=["r_k"], writes=["r_k"])
                fw.v("tensor_scalar", k[:], k[:], 12582912.0, None, ALU.subtract, reads=["r_k"], writes=["r_k"])
                fw.v("scalar_tensor_tensor", r[:], k[:], -C1, a[:], ALU.mult, ALU.add, reads=["r_k", "r_a"], writes=["r_r"])
                fw.v("scalar_tensor_tensor", r[:], k[:], -C2, r[:], ALU.mult, ALU.add, reads=["r_k", "r_r"], writes=["r_r"])
                fw.v("scalar_tensor_tensor", r[:], k[:], -C3, r[:], ALU.mult, ALU.add, reads=["r_k", "r_r"], writes=["r_r"])
                fw.v("tensor_scalar", r[:], r[:], PI, -PI, ALU.min, ALU.max, reads=["r_r"], writes=["r_r"])
                fw.v("tensor_scalar", r2[:], r[:], TWO_PI / 4, None, ALU.add, reads=["r_r"], writes=["r_r2"])
                fw.v("tensor_scalar", m[:], r2[:], PI, -TWO_PI, ALU.is_gt, ALU.mult, reads=["r_r2"], writes=["r_m"])
                fw.v("tensor_tensor", r2[:], r2[:], m[:], ALU.add, reads=["r_r2", "r_m"], writes=["r_r2"])
                fw.v("tensor_scalar", r2[:], r2[:], PI, -PI, ALU.min, ALU.max, reads=["r_r2"], writes=["r_r2"])
                fw.act(cs[:, 0, :], r2[:], AF.Sin, reads=["r_r2"], writes=["r_cs"])
                fw.act(cs[:, 1, :], r[:], AF.Sin, reads=["r_r"], writes=["r_cs"])
                fw.dma(self.ropeT[:, :, c0:c0 + 512].rearrange("k p s -> p k s"), cs[:], reads=["r_cs"], writes=[("ropeT", g)], eng="gpsimd")
            self.release(keys)

    def phase_B(self, l):
        fw, S, G = self.fw, self.S, self.G
        PS = self.PS
        NT = S // 128
        NIT = 20
        NOATT = False
        with ExitStack() as ph:
            allkeys = []

            def sb(n, s, d):
                allkeys.append(n)
                return ph.enter_context(self.nc.sbuf_tensor(self.uname(n), list(s), d))

            wB = sb("wB", [128, 8, 2372], BF16)
            wkd = sb("b_wkd", [128, 8, 128], BF16)
            ropeR = sb("b_ropeR", [128, 1, 128], BF16)
            KT = [sb("b_KT%d" % ct, [128, S], BF16) for ct in range(4)]
            KI = sb("b_KI", [128, S], BF16)
            V = sb("b_V", [128, NT, 8, 65], BF16)
            hn = sb("b_hn", [128, 8, 512], BF16)
            QT = [[sb("b_QT%d_%d" % (ct, i), [128, 512], BF16) for ct in range(4)] for i in range(2)]
            QI = [[sb("b_QI%d_%d" % (j, i), [128, 512], BF16) for j in range(2)] for i in range(2)]
            SG = [[sb("b_SG%d_%d" % (ct, i), [128, 512], BF16) for ct in range(4)] for i in range(2)]
            WI = [sb("b_WI%d" % i, [128, 4, 4], F32) for i in range(2)]
            yout = [sb("b_yout0", [128, 4, 512], BF16)] * 2
            score = sb("b_score", [128, S], F32)
            alias = S >= 4096
            if alias:
                xL = [score[:, 0:512], score[:, 1280:1792]]
                x2L = [score[:, 512:1024], score[:, 1792:2304]]
                xbL = [score[:, 1024:1280].bitcast(BF16), score[:, 2304:2560].bitcast(BF16)]
                cs = score[:, 2560:3584].rearrange("p (a b) -> p a b", b=512)
            else:
                cs = sb("b_cs", [128, 2, 512], F32)
                xL = [sb("b_x%d" % i, [128, 512], F32) for i in range(2)]
                x2L = [sb("b_x2%d" % i, [128, 512], F32) for i in range(2)]
                xbL = [sb("b_xb%d" % i, [128, 512], BF16) for i in range(2)]
            tkeys = ["b_cs"] + ["b_x%d" % i for i in range(2)] + ["b_x2%d" % i for i in range(2)] + ["b_xb%d" % i for i in range(2)]
            mm1 = [sb("b_mm1_0", [128, S], BF16)] * 2
            MT = [sb("b_MT%d" % i, [128, NT, 128], BF16) for i in range(2)]
            E = [sb("b_E%d" % i, [128, 512], BF16) for i in range(4)]
            rl = [sb("b_rl%d" % i, [128, 512], F32) for i in range(2)]
            PT = [sb("b_PT%d" % i, [128, 512], BF16) for i in range(4)]
            bs = sb("b_bs", [128, 8], F32)
            steps = sb("b_steps", [128, NIT + 1], F32)
            rec = sb("b_rec", [128, 8], F32)
            otok = sb("b_otok", [128, 8, 64], BF16)
            dmask = sb("b_dmask", [128, 128], F32)
            keys = allkeys + tkeys + [("wB", k) for k in range(8)] + [("b_wkd", k) for k in range(8)] + [("b_ropeR", 0)] + \
                [("b_KT", ct, g) for ct in range(4) for g in range(G)] + [("b_KI", g) for g in range(G)] + [("b_V", g) for g in range(G)]
            self.acquire(keys + ["stg0", "stg1"])
            self.load_w(wB, "wB", lambda k: self.w_in[l, k * 128:(k + 1) * 128, OFF_B:OFF_B + 2372], 2372, 8, self.gcol)
            self.load_w(wkd, "b_wkd", lambda k: self.w_kidup[l, k * 128:(k + 1) * 128, :], 128, 8, self.gcol)
            self.load_w(ropeR, "b_ropeR", lambda k: self.ropeR_d, 128, 1)
            fw.v("memset", V[:], 1.0, writes=[("b_V", g) for g in range(G)], eng="gpsimd")
            fw.v("memset", dmask[:], 0.0, writes=["b_dmask"], eng="gpsimd")
            fw.v("memset", dmask[0:64, 64:128], -1e30, writes=["b_dmask"], eng="gpsimd")

            def lane(p, chains):
                x_, x2, xb = xL[p], x2L[p], xbL[p]
                kx, kx2, kxb = "b_x%d" % p, "b_x2%d" % p, "b_xb%d" % p
                ps, pk = PS[p], "ps%d" % p

                def proj(w, wkey, c_lo, c_hi):
                    for k in range(8):
                        fw.mm(ps[:], w[:, k, c_lo:c_hi], hn[:, k, :], start=(k == 0), stop=(k == 7), reads=[(wkey, k), "b_hn"], writes=[pk])
                        yield

                def rope(dst, dkey):
                    fw.v("tensor_copy", xb[:], x_[:], reads=[kx], writes=[kxb], eng="gpsimd")
                    yield
                    fw.mm(ps[:], ropeR[:, 0, :], xb[:], reads=[("b_ropeR", 0), kxb], writes=[pk])
                    yield
                    fw.v("tensor_tensor", x2[:], x_[:], cs[:, 0, :], ALU.mult, reads=[kx, "b_cs"], writes=[kx2], eng="gpsimd")
                    yield
                    fw.v("tensor_tensor", x_[:], ps[:], cs[:, 1, :], ALU.mult, reads=[pk, "b_cs", kx], writes=[kx])
                    yield
                    fw.v("tensor_tensor", dst, x2[:], x_[:], ALU.add, reads=[kx2, kx], writes=[dkey], eng="gpsimd")
                    yield

                for ch in chains:
                    kind = ch[0]
                    if kind in ("q", "k"):
                        _, ct, g = ch
                        gp = g % 2
                        gc = slice(g * 512, g * 512 + 512)
                        coff, gname = (0, "q_g") if kind == "q" else (512, "k_g")
                        yield from proj(wB, "wB", coff + ct * 128, coff + (ct + 1) * 128)
                        fw.act(x_[:], ps[:], AF.Copy, reads=[pk], writes=[kx])
                        yield
                        fw.v("tensor_tensor", x2[:], x_[:], x_[:], ALU.mult, reads=[kx], writes=[kx2], eng="gpsimd")
                        yield
                        fw.mm(ps[:], self.blk1[:], x2[:], reads=["k_blk1", kx2], writes=[pk])
                        yield
                        fw.act(x2[:], ps[:], AF.Sqrt, bias=self.eps6_col[:, 0:1], scale=1.0 / 64, reads=[pk, "tiny"], writes=[kx2])
                        yield
                        fw.v("reciprocal", x2[:], x2[:], reads=[kx2], writes=[kx2])
                        yield
                        fw.v("scalar_tensor_tensor", x_[:], x_[:], self.pcol(gname, 0), x2[:], ALU.mult, ALU.mult,
                             reads=[kx, "prm", kx2], writes=[kx])
                        yield
                        if kind == "q":
                            yield from rope(QT[gp][ct][:], "b_QT%d_%d" % (ct, gp))
                        else:
                            yield from rope(KT[ct][:, gc], ("b_KT", ct, g))
                    elif kind == "qi":
                        _, j, g = ch
                        gp = g % 2
                        yield from proj(wB, "wB", 1536 + j * 128, 1536 + (j + 1) * 128)
                        fw.act(x_[:], ps[:], AF.Copy, reads=[pk], writes=[kx])
                        yield
                        yield from rope(QI[gp][j][:], "b_QI%d_%d" % (j, gp))
                    elif kind == "ki":
                        _, g = ch
                        gc = slice(g * 512, g * 512 + 512)
                        yield from proj(wkd, "b_wkd", 0, 128)
                        fw.act(x_[:], ps[:], AF.Copy, reads=[pk], writes=[kx])
                        yield
                        yield from rope(KI[:, gc], ("b_KI", g))
                    elif kind == "sg":
                        _, ct, g = ch
                        gp = g % 2
                        yield from proj(wB, "wB", 1860 + ct * 128, 1860 + (ct + 1) * 128)
                        fw.act(SG[gp][ct][:], ps[:], AF.Silu, reads=[pk], writes=["b_SG%d_%d" % (ct, gp)])
                        yield
                    elif kind == "v":
                        _, tt, g = ch
                        gp = g % 2
                        tcols = slice(tt * 128, (tt + 1) * 128)
                        for k in range(8):
                            fw.mm(ps[:], hn[:, k, tcols], wB[:, k, 1024:1536], start=(k == 0), stop=(k == 7),
                                  reads=[("wB", k), "b_hn"], writes=[pk])
                            yield
                        fw.act(V[:, g * 4 + tt, :, 0:64], ps[:].rearrange("p (h i) -> p h i", i=64), AF.Copy, reads=[pk], writes=[("b_V", g)])
                        yield
                        for k in range(8):
                            fw.mm(ps[:, 0:4], hn[:, k, tcols], wB[:, k, 1856:1860], start=(k == 0), stop=(k == 7),
                                  reads=[("wB", k), "b_hn"], writes=[pk])
                            yield
                        fw.v("tensor_scalar", WI[gp][:, tt, :], ps[:, 0:4], 1.0 / 16, None, ALU.mult, reads=[pk], writes=["b_WI%d" % gp])
                        yield

            def prep_begin(g):
                gc = slice(g * 512, g * 512 + 512)
                if alias:
                    self.release(["b_score"])
                    self.acquire(tkeys)
                fw.dma(hn[:], self.hnT[:, :, gc].rearrange("k p s -> p k s"), reads=[("hnT", g)], writes=["b_hn"])
                fw.dma(cs[:], self.ropeT[:, :, gc].rearrange("k p s -> p k s"), reads=[("ropeT", g)], writes=["b_cs"])

            def prep_lanes(g):
                chains = []
                for ct in range(4):
                    chains += [("q", ct, g), ("k", ct, g)]
                chains += [("qi", 0, g), ("qi", 1, g), ("ki", g)]
                chains += [("sg", ct, g) for ct in range(4)]
                chains += [("v", tt, g) for tt in range(4)]
                return [lane(0, chains[0::2]), lane(1, chains[1::2])]

            def prep_end(g):
                if alias:
                    self.release(tkeys)
                    self.acquire(["b_score"])

            def scores(qt):
                g, tt = qt // 4, qt % 4
                gp = g % 2
                N = (qt + 1) * 128
                tq = slice(tt * 128, (tt + 1) * 128)
                for pc in range((N + 511) // 512):
                    p0 = pc * 512
                    pn = min(512, N - p0)
                    for ih in range(4):
                        po = (ih % 2) * 64
                        fw.mm(PS[ih][:, 0:pn], QI[gp][ih // 2][po:po + 64, tq], KI[po:po + 64, p0:p0 + pn],
                              reads=["b_QI%d_%d" % (ih // 2, gp), ("b_KI", pc)], writes=["ps%d" % ih])
                    for ih in range(4):
                        r_ = rl[ih % 2]
                        rkey = "b_rl%d" % (ih % 2)
                        fw.act(r_[:, 0:pn], PS[ih][:, 0:pn], AF.Relu, reads=["ps%d" % ih], writes=[rkey])
                        if ih == 0:
                            fw.v("tensor_scalar", score[:, p0:p0 + pn], r_[:, 0:pn], WI[gp][:, tt, 0:1], None, ALU.mult,
                                 reads=[rkey, "b_WI%d" % gp], writes=["b_score"])
                        else:
                            fw.v("scalar_tensor_tensor", score[:, p0:p0 + pn], r_[:, 0:pn], WI[gp][:, tt, ih:ih + 1], score[:, p0:p0 + pn],
                                 ALU.mult, ALU.add, reads=[rkey, "b_WI%d" % gp, "b_score"], writes=["b_score"])

            def bisect_mask(qt):
                NB = qt + 1
                N = NB * 128
                mk = mm1[0]
                mkey = "b_mm1_0"
                A, lo, mid, cnt, tmp = (bs[:, i:i + 1] for i in range(5))
                if NB >= 3:
                    fw.v("tensor_reduce", A, score[:, 0:N], AX.X, ALU.max, apply_absolute_value=True, reads=["b_score"], writes=["b_bs"])
                    fw.v("tensor_scalar", A, A, 1.0001, 1e-20, ALU.mult, ALU.add, reads=["b_bs"], writes=["b_bs"])
                fw.v("tensor_tensor", score[:, N - 128:N], score[:, N - 128:N], dmask[:], ALU.add, reads=["b_score", "b_dmask"],
                     writes=["b_score"])
                if NB >= 3:
                    fw.v("tensor_scalar", steps[:], self.cst_sb[:, 1:2 + NIT], A, None, ALU.mult, reads=["cst", "b_bs"], writes=["b_steps"])
                    fw.v("tensor_scalar", mid, A, -1.0, steps[:, 0:1], ALU.mult, ALU.add, reads=["b_bs", "b_steps"], writes=["b_bs"])
                    for it in range(NIT):
                        fw.v("tensor_scalar", mk[:, 0:N], score[:, 0:N], mid, None, ALU.is_ge, ALU.add, accum_out=cnt,
                             reads=["b_score", "b_bs", mkey], writes=[mkey, "b_bs"])
                        fw.v("tensor_scalar", tmp, cnt, 255.5, steps[:, it:it + 1], ALU.is_ge, ALU.mult, reads=["b_bs", "b_steps"], writes=["b_bs"])
                        fw.v("scalar_tensor_tensor", mid, tmp, steps[:, it + 1:it + 2], mid, ALU.subtract, ALU.add,
                             reads=["b_bs", "b_steps"], writes=["b_bs"])
                    fw.v("tensor_tensor", lo, mid, steps[:, NIT:NIT + 1], ALU.subtract, reads=["b_bs", "b_steps"], writes=["b_bs"])
                else:
                    fw.v("memset", lo, -1e29, writes=["b_bs"])
                fw.v("tensor_scalar", mk[:, 0:N], score[:, 0:N], lo, None, ALU.is_ge, reads=["b_score", "b_bs"], writes=[mkey])
                psb1 = PS[1][:].bitcast(BF16)
                mt, mtkey = MT[qt % 2], "b_MT%d" % (qt % 2)
                for kb0 in range(0, NB, 8):
                    nk = min(8, NB - kb0)
                    for j in range(nk):
                        kb = kb0 + j
                        fw.tr(psb1[:, j * 128:(j + 1) * 128], mk[:, kb * 128:(kb + 1) * 128], self.ident_b[:],
                              reads=[mkey, "k_ident"], writes=["ps1"])
                    fw.act(mt[:, kb0:kb0 + nk, :].rearrange("p a b -> p (a b)"), psb1[:, 0:nk * 128], AF.Copy, reads=["ps1"], writes=[mtkey])

            def attention_gen(qt):
                if NOATT:
                    return
                g, tt = qt // 4, qt % 4
                gp = g % 2
                NB = qt + 1
                tq = slice(tt * 128, (tt + 1) * 128)
                mt, mtkey = MT[qt % 2], "b_MT%d" % (qt % 2)
                for hpair in range(4):
                    ct = hpair
                    for gi, kb0 in enumerate(range(0, NB, 4)):
                        nk = min(4, NB - kb0)
                        bis = [2 * e + gi % 2 for e in range(2)]
                        for j in range(nk):
                            kb = kb0 + j
                            for e in range(2):
                                po = e * 64
                                pl = PS[2 + bis[e]]
                                fw.mm(pl[:, j * 128:(j + 1) * 128], KT[ct][po:po + 64, kb * 128:(kb + 1) * 128], QT[gp][ct][po:po + 64, tq],
                                      reads=[("b_KT", ct, kb // 4), "b_QT%d_%d" % (ct, gp)], writes=["ps%d" % (2 + bis[e])])
                                yield
                        for e in range(2):
                            bi = bis[e]
                            fw.act(E[bi][:, 0:nk * 128], PS[2 + bi][:, 0:nk * 128], AF.Exp, scale=0.125, reads=["ps%d" % (2 + bi)], writes=["b_E%d" % bi])
                            yield
                            fw.v("tensor_tensor", PT[bi][:, 0:nk * 128], E[bi][:, 0:nk * 128],
                                 mt[:, kb0:kb0 + nk, :].rearrange("p a b -> p (a b)"), ALU.mult,
                                 reads=["b_E%d" % bi, mtkey], writes=["b_PT%d" % bi], eng="gpsimd")
                            yield
                        for e in range(2):
                            h = 2 * hpair + e
                            bi = bis[e]
                            pob = PS[7] if e == 0 else PS[6]
                            pokey = "ps7" if e == 0 else "ps6"
                            osl = slice(hpair * 65, hpair * 65 + 65)
                            for j in range(nk):
                                kb = kb0 + j
                                fw.mm(pob[:, osl], PT[bi][:, j * 128:(j + 1) * 128], V[:, kb, h, :], start=(kb == 0), stop=(kb == NB - 1),
                                      reads=["b_PT%d" % bi, ("b_V", kb // 4)], writes=[pokey])
                                yield

            def final(qt):
                g, tt = qt // 4, qt % 4
                gp = g % 2
                tq = slice(tt * 128, (tt + 1) * 128)
                otok4 = otok[:].rearrange("p (a e) i -> p a e i", e=2)
                for hb_ in range(2):
                    pob = PS[7] if hb_ == 0 else PS[6]
                    pokey = "ps7" if hb_ == 0 else "ps6"
                    pv = pob[:, 0:260].rearrange("p (h i) -> p h i", i=65)
                    fw.v("reciprocal", rec[:, hb_ * 4:hb_ * 4 + 4], pv[:, :, 64], reads=[pokey], writes=["b_rec"])
                    fw.v("tensor_tensor", otok4[:, :, hb_, :], pv[:, :, 0:64],
                         rec[:, hb_ * 4:hb_ * 4 + 4].unsqueeze(2).to_broadcast([128, 4, 64]), ALU.mult,
                         reads=[pokey, "b_rec"], writes=["b_otok"])
                of = otok[:].rearrange("p h i -> p (h i)")
                for ct in range(4):
                    pb_ = PS[7 - ct // 2][:, 384:512].bitcast(BF16)
                    pkey = "ps%d" % (7 - ct // 2)
                    fw.tr(pb_[:, (ct % 2) * 128:(ct % 2 + 1) * 128], of[:, ct * 128:(ct + 1) * 128], self.ident_b[:],
                          reads=["b_otok", "k_ident"], writes=[pkey])
                for ct in range(4):
                    pb_ = PS[7 - ct // 2][:, 384:512].bitcast(BF16)
                    pkey = "ps%d" % (7 - ct // 2)
                    fw.v("tensor_tensor", yout[gp][:, ct, tq], pb_[:, (ct % 2) * 128:(ct % 2 + 1) * 128], SG[gp][ct][:, tq], ALU.mult,
                         reads=[pkey, "b_SG%d_%d" % (ct, gp)], writes=["b_yout0"])
                if tt == 3:
                    gc = slice(g * 512, g * 512 + 512)
                    fw.dma(self.yT[1][:, :, gc].rearrange("k p s -> p k s"), yout[gp][:], reads=["b_yout0"],
                           writes=[("yT1", g, ct) for ct in range(4)], eng="gpsimd")

            prep_begin(0)
            fw.lockstep(prep_lanes(0))
            prep_end(0)
            scores(0)
            bisect_mask(0)
            for qt in range(NT):
                nxt = qt + 1
                if nxt < NT:
                    if nxt % 4 == 0:
                        gn = nxt // 4
                        prep_begin(gn)
                        fw.lockstep(prep_lanes(gn))
                        prep_end(gn)
                    scores(nxt)
                fw.lockstep([attention_gen(qt)])
                if nxt < NT:
                    bisect_mask(nxt)
                final(qt)
            self.release(keys)

    def phase_C(self, l):
        fw, S, G = self.fw, self.S, self.G
        PS = self.PS
        with ExitStack() as ph:
            sb = lambda n, s, d: ph.enter_context(self.nc.sbuf_tensor(self.uname(n), list(s), d))
            wC = sb("wC", [128, 8, 1024], BF16)
            wr = sb("c_wr", [128, 4, 128], BF16)
            wi = sb("c_wi", [128, 4, 128], BF16)
            hn = [sb("c_hn%d" % i, [128, 8, 512], BF16) for i in range(2)]
            xbuf = sb("c_xbuf", [128, 4, 515], F32)
            hprev = sb("c_hprev", [128, 4], F32)
            cl = sb("c_cl", [128, 4], F32)
            xc = [sb("c_xc%d" % i, [128, 512], F32) for i in range(2)]
            xcb = [sb("c_xcb%d" % i, [128, 512], BF16) for i in range(2)]
            r_ = [sb("c_r%d" % i, [128, 512], F32) for i in range(2)]
            i_ = [sb("c_i%d" % i, [128, 512], F32) for i in range(2)]
            a_ = [sb("c_a%d" % i, [128, 512], F32) for i in range(2)]
            b_ = [sb("c_b%d" % i, [128, 512], F32) for i in range(2)]
            sg = [sb("c_sg%d" % i, [128, 512], F32) for i in range(2)]
            yo = [sb("c_y%d" % i, [128, 512], BF16) for i in range(2)]
            names = ["wC", "c_wr", "c_wi", "c_hn0", "c_hn1", "c_xbuf", "c_hprev", "c_cl"] + \
                    [n + str(i) for n in ("c_xc", "c_xcb", "c_r", "c_i", "c_a", "c_b", "c_sg", "c_y") for i in range(2)]
            keys = names + [("wC", k) for k in range(8)] + [("c_wr", k) for k in range(4)] + [("c_wi", k) for k in range(4)] + \
                ["c_xbuf%d" % i for i in range(4)] + ["c_hprev%d" % i for i in range(4)]
            self.acquire(keys + ["stg0", "stg1"])
            self.load_w(wC, "wC", lambda k: self.w_in[l, k * 128:(k + 1) * 128, OFF_C:OFF_C + 1024], 1024, 8, self.gcol)
            self.load_w(wr, "c_wr", lambda k: self.wr_bd[l, k], 128, 4)
            self.load_w(wi, "c_wi", lambda k: self.wi_bd[l, k], 128, 4)
            fw.act(cl[:], self.prm_sb[:, PCOLS["lam"][0]:PCOLS["lam"][0] + 4], AF.Exp, scale=-1.0, reads=["prm"], writes=["c_cl"])
            fw.act(cl[:], cl[:], AF.Ln, bias=1.0, reads=["c_cl"], writes=["c_cl"])
            fw.v("tensor_scalar", cl[:], cl[:], -8.0, None, ALU.mult, reads=["c_cl"], writes=["c_cl"])
            fw.v("memset", xbuf[:], 0.0, writes=["c_xbuf%d" % i for i in range(4)])
            fw.v("memset", hprev[:], 0.0, writes=["c_hprev%d" % i for i in range(4)])
            for g in range(G):
                c0 = g * 512
                hk = "c_hn%d" % (g % 2)
                hg = hn[g % 2]
                fw.dma(hg[:], self.hnT[:, :, c0:c0 + 512].rearrange("k p s -> p k s"), reads=[("hnT", g)], writes=[hk])
                def cbody(ct, g=g, c0=c0, hk=hk, hg=hg):
                    j = ct % 2
                    pb = 4 * j
                    px, pg, pr, pi = PS[pb], PS[pb + 1], PS[pb + 2], PS[pb + 3]
                    kx, kg, kr, ki = ["ps%d" % (pb + t) for t in range(4)]
                    for k in range(8):
                        fw.mm(px[:], wC[:, k, ct * 128:(ct + 1) * 128], hg[:, k, :], start=(k == 0), stop=(k == 7),
                              reads=[("wC", k), hk], writes=[kx])
                        yield
                    for k in range(8):
                        fw.mm(pg[:], wC[:, k, 512 + ct * 128:512 + (ct + 1) * 128], hg[:, k, :], start=(k == 0), stop=(k == 7),
                              reads=[("wC", k), hk], writes=[kg])
                        yield
                    xb = xbuf[:, ct, :]
                    fw.act(xb[:, 3:515], px[:], AF.Copy, reads=[kx], writes=["c_xbuf%d" % ct])
                    yield
                    cw = lambda i: self.pcol("conv_w", i * 4 + ct)
                    fw.v("tensor_scalar", xc[j][:], xb[:, 3:515], cw(3), self.pcol("conv_b", ct), ALU.mult, ALU.add,
                         reads=["c_xbuf%d" % ct, "prm"], writes=["c_xc%d" % j])
                    yield
                    for i in range(3):
                        fw.v("scalar_tensor_tensor", xc[j][:], xb[:, i:i + 512], cw(i), xc[j][:], ALU.mult, ALU.add,
                             reads=["c_xbuf%d" % ct, "prm", "c_xc%d" % j], writes=["c_xc%d" % j])
                        yield
                    fw.v("tensor_copy", xb[:, 0:3], xb[:, 512:515], reads=["c_xbuf%d" % ct], writes=["c_xbuf%d" % ct], eng="gpsimd")
                    yield
                    fw.v("tensor_copy", xcb[j][:], xc[j][:], reads=["c_xc%d" % j], writes=["c_xcb%d" % j], eng="gpsimd")
                    yield
                    fw.mm(pr[:], wr[:, ct, :], xcb[j][:], reads=[("c_wr", ct), "c_xcb%d" % j], writes=[kr])
                    yield
                    fw.mm(pi[:], wi[:, ct, :], xcb[j][:], reads=[("c_wi", ct), "c_xcb%d" % j], writes=[ki])
                    yield
                    fw.act(r_[j][:], pr[:], AF.Sigmoid, bias=self.pcol("b_r", ct), reads=[kr, "prm"], writes=["c_r%d" % j])
                    yield
                    fw.act(i_[j][:], pi[:], AF.Sigmoid, bias=self.pcol("b_i", ct), reads=[ki, "prm"], writes=["c_i%d" % j])
                    yield
                    fw.act(sg[j][:], pg[:], AF.Silu, reads=[kg], writes=["c_sg%d" % j])
                    yield
                    fw.act(a_[j][:], r_[j][:], AF.Exp, scale=cl[:, ct:ct + 1], reads=["c_r%d" % j, "c_cl"], writes=["c_a%d" % j])
                    yield
                    fw.v("tensor_tensor", b_[j][:], a_[j][:], a_[j][:], ALU.mult, reads=["c_a%d" % j], writes=["c_b%d" % j])
                    yield
                    fw.v("tensor_scalar", b_[j][:], b_[j][:], -1.0, 1.0, ALU.mult, ALU.add, reads=["c_b%d" % j], writes=["c_b%d" % j])
                    yield
                    fw.act(b_[j][:], b_[j][:], AF.Sqrt, reads=["c_b%d" % j], writes=["c_b%d" % j])
                    yield
                    fw.v("tensor_tensor", i_[j][:], i_[j][:], xc[j][:], ALU.mult, reads=["c_i%d" % j, "c_xc%d" % j],
                         writes=["c_i%d" % j], eng="gpsimd")
                    yield
                    fw.v("tensor_tensor", b_[j][:], b_[j][:], i_[j][:], ALU.mult, reads=["c_b%d" % j, "c_i%d" % j], writes=["c_b%d" % j])
                    yield
                    fw.v("tensor_tensor_scan", r_[j][:], a_[j][:], b_[j][:], hprev[:, ct:ct + 1], ALU.mult, ALU.add,
                         reads=["c_a%d" % j, "c_b%d" % j, "c_hprev%d" % ct, "c_r%d" % j], writes=["c_r%d" % j])
                    yield
                    fw.v("tensor_copy", hprev[:, ct:ct + 1], r_[j][:, 511:512], reads=["c_r%d" % j], writes=["c_hprev%d" % ct])
                    yield
                    fw.v("tensor_tensor", yo[j][:], r_[j][:], sg[j][:], ALU.mult, reads=["c_r%d" % j, "c_sg%d" % j],
                         writes=["c_y%d" % j], eng="gpsimd")
                    yield
                    fw.dma(self.yT[2][ct, :, c0:c0 + 512], yo[j][:], reads=["c_y%d" % j], writes=[("yT2", g, ct)], eng="gpsimd")
                    yield
                fw.lockstep([cbody(0), cbody(1)])
                fw.lockstep([cbody(2), cbody(3)])
            self.release(keys)

    def phase_M(self, l):
        fw, S, G, L = self.fw, self.S, self.G, self.L
        PS = self.PS
        last = (l == L - 1)
        with ExitStack() as ph:
            sb = lambda n, s, d: ph.enter_context(self.nc.sbuf_tensor(self.uname(n), list(s), d))
            wG = sb("wG", [128, 8, 3072], BF16)
            wbr = sb("wbr", [128, 12, 1024], BF16)
            wo = sb("wo", [128, 8, 1024], BF16)
            wpg = sb("wpg", [128, 8, 1024], BF16)
            wple = sb("wple", [128, 2, 1024], BF16)
            hn = sb("m_hn", [128, 8, 512], BF16)
            ys = [sb("m_y%d" % n, [128, 4, 512], BF16) for n in range(3)]
            hb = sb("m_h", [128, 8, 512], F32)
            h1b = sb("m_h1b", [128, 8, 512], BF16)
            pf = sb("m_pf", [128, 2, 512], F32)
            pb_ = sb("m_pb", [128, 2, 512], BF16)
            mrg = sb("m_mrg", [128, 8, 512], BF16)
            sgsP = [[sb("m_sg%d_%d" % (p, n), [128, 512], F32) for n in range(3)] for p in range(2)]
            sgs = sgsP[0]
            tmp = sb("m_tmp", [128, 2, 512], F32)
            self.rs_sb = sb("m_rs", [128, 512], F32)
            self.hn_out = h1b
            self.hn_out_key = "m_h1b"
            self.eps_col = sb("m_eps", [128, 1], F32)
            names = ["wG", "wbr", "wo", "wpg", "wple", "m_hn", "m_y0", "m_y1", "m_y2", "m_h", "m_h1b", "m_pf", "m_pb",
                     "m_mrg", "m_sg0_0", "m_sg0_1", "m_sg0_2", "m_sg1_0", "m_sg1_1", "m_sg1_2", ("m_tmp", 0), ("m_tmp", 1), "rs", "hn_out", "eps"]
            keys = names + [("wG", k) for k in range(8)] + [("wbr", k) for k in range(12)] + \
                [("wo", k) for k in range(8)] + [("wpg", k) for k in range(8)] + [("wple", k) for k in range(2)]
            self.acquire(keys + ["stg0", "stg1"])
            fw.v("memset", self.eps_col[:], NORM_EPS, writes=["eps"])
            self.load_w(wG, "wG", lambda k: self.w_in[l, k * 128:(k + 1) * 128, OFF_G:OFF_G + 3072], 3072, 8, self.gcol)
            self.load_w(wbr, "wbr", lambda k: self.w_branch[l, k // 4, (k % 4) * 128:(k % 4 + 1) * 128, :], 1024, 12)
            self.load_w(wo, "wo", lambda k: self.w_out[l, k * 128:(k + 1) * 128, :], 1024, 8)
            self.load_w(wpg, "wpg", lambda k: self.w_pg[l, k * 128:(k + 1) * 128, :], 1024, 8)
            self.load_w(wple, "wple", lambda k: self.w_ple[l, k * 128:(k + 1) * 128, :], 1024, 2)
            hsrc = self.xT if l == 0 else self.hT
            hdst = self.outT if last else self.hT
            for g in range(G):
                c0 = g * 512
                fw.dma(hn[:], self.hnT[:, :, c0:c0 + 512].rearrange("k p s -> p k s"), reads=[("hnT", g)], writes=["m_hn"])
                for n in range(3):
                    fw.dma(ys[n][:], self.yT[n][:, :, c0:c0 + 512].rearrange("k p s -> p k s"),
                           reads=[("yT%d" % n, g, ct) for ct in range(4)], writes=["m_y%d" % n])
                fw.dma(hb[:], hsrc[:, :, c0:c0 + 512].rearrange("k p s -> p k s"),
                       reads=([("hT", g)] if l > 0 else []), writes=["m_h"])
                fw.dma(pf[:], self.pT[l, :, :, c0:c0 + 512].rearrange("k p s -> p k s"), writes=["m_pf"])
                fw.v("tensor_copy", pb_[:], pf[:], reads=["m_pf"], writes=["m_pb"], eng="gpsimd")
                for dmt in range(8):
                    cs = slice(dmt * 128, (dmt + 1) * 128)
                    dp = dmt % 2
                    sg_ = sgsP[dp]
                    gb = 3 * (dmt % 2)
                    for n in range(3):
                        yb = 6 + (dmt * 3 + n) % 2
                        for k in range(8):
                            fw.mm(PS[gb + n][:], wG[:, k, n * 1024 + dmt * 128:n * 1024 + (dmt + 1) * 128], hn[:, k, :],
                                  start=(k == 0), stop=(k == 7), reads=[("wG", k), "m_hn"], writes=["ps%d" % (gb + n)])
                        for kc in range(4):
                            fw.mm(PS[yb][:], wbr[:, n * 4 + kc, cs], ys[n][:, kc, :], start=(kc == 0), stop=(kc == 3),
                                  reads=[("wbr", n * 4 + kc), "m_y%d" % n], writes=["ps%d" % yb])
                        fw.act(sg_[n][:], PS[gb + n][:], AF.Sigmoid, reads=["ps%d" % (gb + n)], writes=["m_sg%d_%d" % (dp, n)])
                        fw.v("tensor_tensor", sg_[n][:], PS[yb][:], sg_[n][:], ALU.mult,
                             reads=["ps%d" % yb, "m_sg%d_%d" % (dp, n)], writes=["m_sg%d_%d" % (dp, n)])
                    fw.v("tensor_tensor", sg_[0][:], sg_[0][:], sg_[1][:], ALU.add, reads=["m_sg%d_0" % dp, "m_sg%d_1" % dp], writes=["m_sg%d_0" % dp], eng="gpsimd")
                    fw.v("tensor_tensor", mrg[:, dmt, :], sg_[0][:], sg_[2][:], ALU.add, reads=["m_sg%d_0" % dp, "m_sg%d_2" % dp], writes=["m_mrg"], eng="gpsimd")
                for d2 in range(8):
                    pk = 6 + d2 % 2
                    for k in range(8):
                        fw.mm(PS[pk][:], wo[:, k, d2 * 128:(d2 + 1) * 128], mrg[:, k, :], start=(k == 0), stop=(k == 7),
                              reads=[("wo", k), "m_mrg"], writes=["ps%d" % pk])
                    fw.v("tensor_tensor", hb[:, d2, :], hb[:, d2, :], PS[pk][:], ALU.add, reads=["m_h", "ps%d" % pk], writes=["m_h"])
                fw.act(h1b[:], hb[:], AF.Copy, reads=["m_h"], writes=["m_h1b"])
                for d2 in range(8):
                    pa, pp = (0, 1) if d2 % 2 == 0 else (2, 3)
                    for k in range(8):
                        fw.mm(PS[pa][:], wpg[:, k, d2 * 128:(d2 + 1) * 128], h1b[:, k, :], start=(k == 0), stop=(k == 7),
                              reads=[("wpg", k), "m_h1b"], writes=["ps%d" % pa])
                    for k in range(2):
                        fw.mm(PS[pp][:], wple[:, k, d2 * 128:(d2 + 1) * 128], pb_[:, k, :], start=(k == 0), stop=(k == 1),
                              reads=[("wple", k), "m_pb"], writes=["ps%d" % pp])
                    sgk = d2 % 2
                    fw.act(sgs[sgk][:], PS[pa][:], AF.Sigmoid, reads=["ps%d" % pa], writes=["m_sg0_%d" % sgk])
                    fw.v("tensor_tensor", sgs[sgk][:], PS[pp][:], sgs[sgk][:], ALU.mult, reads=["ps%d" % pp, "m_sg0_%d" % sgk],
                         writes=["m_sg0_%d" % sgk])
                    fw.v("tensor_tensor", hb[:, d2, :], hb[:, d2, :], sgs[sgk][:], ALU.add, reads=["m_h", "m_sg0_%d" % sgk],
                         writes=["m_h"], eng="gpsimd")
                fw.dma(hdst[:, :, c0:c0 + 512].rearrange("k p s -> p k s"), hb[:], reads=["m_h"],
                       writes=[("outT" if last else "hT", g)], eng="gpsimd")
                if not last:
                    self.norm_group(hb, "m_h", g, tmp, "m_tmp")
            self.release(keys)


_CACHE = {}


def make_in_maps(inp, S, L, ncores):
    maps = []
    w_in = np.ascontiguousarray(np.asarray(inp["w_in"], np.float32)[:L])
    ki0 = OFF_B + 1792
    w_kidup = np.ascontiguousarray(np.concatenate([w_in[:, :, ki0:ki0 + 64], w_in[:, :, ki0:ki0 + 64]], axis=2))
    prm = np.stack([pack_params(inp, l) for l in range(L)])
    cst = np.zeros((128, 32), np.float32)
    invf = (np.float32(500000.0) ** (-(np.arange(0, 16, 2, dtype=np.float32) / np.float32(16)))).astype(np.float32)
    for p_ in range(128):
        if p_ % 64 < 16:
            cst[p_, 0] = invf[p_ % 8]
    cst[:, 1:25] = (2.0 ** (-np.arange(24, dtype=np.float64)))[None, :].astype(np.float32)
    ropeR = np.zeros((128, 128), np.float32)
    for m_ in range(128):
        if m_ % 64 < 8:
            ropeR[m_ + 8, m_] = -1.0
        elif m_ % 64 < 16:
            ropeR[m_ - 8, m_] = 1.0
    shared = {
        "cst": cst, "ropeR": ropeR,
        "prm": prm, "w_in": w_in, "w_kidup": w_kidup,
        "w2": np.ascontiguousarray(np.asarray(inp["rwkv_w2"], np.float32)[:L]),
        "a2": np.ascontiguousarray(np.asarray(inp["rwkv_a2"], np.float32)[:L]),
        "wr_bd": np.stack([blockdiag(inp["lru_w_r"][l]) for l in range(L)]),
        "wi_bd": np.stack([blockdiag(inp["lru_w_i"][l]) for l in range(L)]),
        "w_branch": np.ascontiguousarray(np.asarray(inp["w_branch"], np.float32)[:L]),
        "w_out": np.ascontiguousarray(np.asarray(inp["w_out"], np.float32)[:L]),
        "w_ple": np.ascontiguousarray(np.asarray(inp["w_ple"], np.float32)[:L]),
        "w_pg": np.ascontiguousarray(np.asarray(inp["w_ple_gate"], np.float32)[:L]),
    }
    x = np.asarray(inp["x"], np.float32)
    p = np.asarray(inp["p"], np.float32)
    pos = np.asarray(inp["positions"], np.int32)
    nb = x.shape[0]
    for c in range(ncores):
        b = (c // 2) % nb
        m = dict(shared)
        m["xT"] = np.ascontiguousarray(x[b].T.reshape(8, 128, S))
        m["pT"] = np.ascontiguousarray(np.stack([p[l, b].T.reshape(2, 128, S) for l in range(L)]))
        m["pos"] = np.ascontiguousarray(pos[b].reshape(1, S))
        maps.append(m)
    return maps


def kernel(**inputs):
    x = np.asarray(inputs["x"])
    B, S, _ = x.shape
    L = np.asarray(inputs["w_in"]).shape[0]
    key = (S, L)
    if key not in _CACHE:
        _CACHE[key] = Prog(S, L).build()
    nc = _CACHE[key]
    maps = make_in_maps(inputs, S, L, 8)
    res = run_bass_kernel_spmd(nc, maps, core_ids=list(range(8)))
    out = np.zeros((B, S, D), np.float32)
    for b in range(B):
        out[b] = res.results[2 * b]["outT"].reshape(D, S).T
    return out
```

```python
from contextlib import ExitStack
import numpy as np
import concourse.bass as bass
import concourse.mybir as mybir
from concourse.bass_utils import run_bass_kernel_spmd

F32 = mybir.dt.float32
BF16 = mybir.dt.bfloat16
I32 = mybir.dt.int32
AF = mybir.ActivationFunctionType
ALU = mybir.AluOpType
AX = mybir.AxisListType

ENGS = ("tensor", "vector", "scalar", "gpsimd", "sync")
N_DMA_SEMS = 24

D = 1024
DIN = 8644
OFF_A, OFF_B, OFF_C, OFF_G = 0, 2176, 4548, 5572
NORM_EPS = 1e-6
GN_EPS = 64e-5


class FW:
    def __init__(self, nc, stack, same_engine_sync=True):
        self.nc = nc
        self.stack = stack
        self.q = {e: [] for e in ENGS}
        self.cnt = {e: 0 for e in ENGS}
        self.sem = {e: stack.enter_context(nc.semaphore("s_" + e)) for e in ENGS}
        self.dsem = [stack.enter_context(nc.semaphore("d%d" % i)) for i in range(N_DMA_SEMS)]
        self.dcnt = [0] * N_DMA_SEMS
        self.dnext = 0
        self.seen = {e: {} for e in ENGS}
        self.lastw = {}
        self.readers = {}
        self.same = same_engine_sync
        self.ninst = 0
        self.rr = 0

    def sb(self, name, shape, dt):
        return self.stack.enter_context(self.nc.sbuf_tensor(name, list(shape), dt))

    def ps(self, name, shape, dt=F32):
        return self.stack.enter_context(self.nc.psum_tensor(name, list(shape), dt))

    def _deps(self, eng, reads, writes):
        ev = []
        for k in reads:
            if k in self.lastw:
                ev.append((self.lastw[k], True))
        for k in writes:
            if k in self.lastw:
                ev.append((self.lastw[k], False))
            ev.extend((e, False) for e in self.readers.get(k, ()))
        best = {}
        for (sname, sem, val, src), raw in ev:
            if src == eng and (eng == "tensor" or not self.same or not raw):
                continue
            if self.seen[eng].get(sname, 0) >= val:
                continue
            if sname not in best or best[sname][1] < val:
                best[sname] = (sem, val)
        waits = []
        for sname, (sem, val) in best.items():
            self.seen[eng][sname] = val
            waits.append((sem, val))
        return waits

    def _commit(self, event, reads, writes):
        for k in writes:
            self.lastw[k] = event
            self.readers[k] = []
        for k in reads:
            if k in writes:
                continue
            self.readers.setdefault(k, []).append(event)

    def op(self, eng, fn, reads=(), writes=()):
        waits = self._deps(eng, reads, writes)
        self.cnt[eng] += 1
        idx = self.cnt[eng]
        sem = self.sem[eng]
        self.q[eng].append((waits, fn, sem, 1))
        self._commit(("s_" + eng, sem, idx, eng), reads, writes)
        self.ninst += 1

    def dma(self, out, in_, reads=(), writes=(), eng="sync", **kw):
        lo, n = (0, 16) if eng == "sync" else (16, N_DMA_SEMS - 16)
        self.dnext_q = getattr(self, "dnext_q", {})
        i = self.dnext_q.get(eng, 0)
        self.dnext_q[eng] = (i + 1) % n
        slot = lo + i
        sem = self.dsem[slot]
        sname = "d%d" % slot
        waits = self._deps(eng, reads, writes)
        prev = self.dcnt[slot] * 16
        if prev and self.seen[eng].get(sname, 0) < prev:
            waits.append((sem, prev))
            self.seen[eng][sname] = prev
        self.dcnt[slot] += 1
        val = self.dcnt[slot] * 16
        self.q[eng].append((waits, lambda e: e.dma_start(out=out, in_=in_, **kw), sem, 16))
        self._commit((sname, sem, val, "dma"), reads, writes)
        self.ninst += 1

    def finish(self, keys, eng="sync"):
        waits = self._deps(eng, keys, ())
        self.q[eng].append((waits, None, None, 0))

    def emit(self):
        nc = self.nc
        with nc.Block() as block:
            for ename in ENGS:
                items = self.q[ename]
                if not items:
                    continue

                def body(e, items=items):
                    for waits, fn, sem, inc in items:
                        for (ws, wv) in waits:
                            e.wait_ge(ws, wv)
                        if fn is not None:
                            fn(e).then_inc(sem, inc)

                getattr(block, ename)(body)

    def mm(self, out, lhsT, rhs, start=True, stop=True, reads=(), writes=()):
        self.op("tensor", lambda e: e.matmul(out, lhsT, rhs, start=start, stop=stop), reads, writes)

    def tr(self, out, in_, ident, reads=(), writes=()):
        self.op("tensor", lambda e: e.transpose(out, in_, ident), reads, writes)

    def act(self, out, in_, func, bias=0.0, scale=1.0, reads=(), writes=(), accum_out=None):
        if accum_out is None:
            self.op("scalar", lambda e: e.activation(out, in_, func, bias=bias, scale=scale), reads, writes)
        else:
            self.op("scalar", lambda e: e.activation(out, in_, func, bias=bias, scale=scale,
                                                     accum_out=accum_out), reads, writes)

    def v(self, name, *args, reads=(), writes=(), eng="vector", **kw):
        self.op(eng, lambda e: getattr(e, name)(*args, **kw), reads, writes)

    @staticmethod
    def lockstep(gens):
        gens = list(gens)
        while gens:
            for g_ in list(gens):
                try:
                    next(g_)
                except StopIteration:
                    gens.remove(g_)

    def cast_eng(self):
        self.rr += 1
        return ("vector", "gpsimd")[self.rr % 2]


PCOLS = {}
_o = 0
for _n, _w in [("norm_g", 8), ("mu_r", 4), ("mu_k", 4), ("mu_v", 4), ("mu_g", 4), ("mu_wl", 1), ("mu_al", 1),
               ("w0", 4), ("a0", 4), ("k_k", 4), ("k_a", 4), ("gn_g", 4), ("gn_b", 4), ("r_k", 4),
               ("q_g", 1), ("k_g", 1),
               ("conv_w", 16), ("conv_b", 4), ("b_r", 4), ("b_i", 4), ("lam", 4)]:
    PCOLS[_n] = (_o, _w)
    _o += _w
NPRM = _o


def _col4(v):
    return np.ascontiguousarray(np.asarray(v, np.float32).reshape(4, 128).T)


def pack_params(inp, l):
    prm = np.zeros((128, NPRM), np.float32)

    def put(name, arr):
        o, w = PCOLS[name]
        prm[:arr.shape[0], o:o + w] = arr

    put("norm_g", np.asarray(inp["norm_g"][l], np.float32).reshape(8, 128).T)
    mu = np.asarray(inp["rwkv_mu"][l], np.float32)
    put("mu_r", _col4(mu[0:512])); put("mu_k", _col4(mu[512:1024])); put("mu_v", _col4(mu[1024:1536]))
    put("mu_wl", mu[1536:1600].reshape(64, 1)); put("mu_al", mu[1600:1664].reshape(64, 1))
    put("mu_g", _col4(mu[1664:2176]))
    put("w0", _col4(inp["rwkv_w0"][l])); put("a0", _col4(inp["rwkv_a0"][l]))
    put("k_k", _col4(inp["rwkv_k_k"][l])); put("k_a", _col4(inp["rwkv_k_a"][l]))
    put("gn_g", _col4(inp["rwkv_gn_g"][l])); put("gn_b", _col4(inp["rwkv_gn_b"][l]))
    put("r_k", _col4(np.asarray(inp["rwkv_r_k"][l]).reshape(512)))
    put("q_g", np.tile(np.asarray(inp["dsa_q_g"][l], np.float32), 2).reshape(128, 1))
    put("k_g", np.tile(np.asarray(inp["dsa_k_g"][l], np.float32), 2).reshape(128, 1))
    cw = np.asarray(inp["lru_conv_w"][l], np.float32)
    put("conv_w", np.concatenate([_col4(cw[i]) for i in range(4)], axis=1))
    put("conv_b", _col4(inp["lru_conv_b"][l])); put("b_r", _col4(inp["lru_b_r"][l]))
    put("b_i", _col4(inp["lru_b_i"][l])); put("lam", _col4(inp["lru_lambda"][l]))
    return prm


def blockdiag(w):
    w = np.asarray(w, np.float32)
    out = np.zeros((4, 128, 128), np.float32)
    for ct in range(4):
        out[ct, 0:64, 0:64] = w[2 * ct]
        out[ct, 64:128, 64:128] = w[2 * ct + 1]
    return out


class Prog:
    def __init__(self, S, L, phases="NACBM", dbg=()):
        self.S, self.L, self.phases, self.dbg = S, L, phases, dbg
        self.G = S // 512
        nc = self.nc = bass.Bass("TRN2", target_bir_lowering=False)
        dt = nc.dram_tensor
        self.xT = dt("xT", [8, 128, S], F32, kind="ExternalInput").ap()
        self.pT = dt("pT", [L, 2, 128, S], F32, kind="ExternalInput").ap()
        self.pos = dt("pos", [1, S], I32, kind="ExternalInput").ap()
        self.prm = dt("prm", [L, 128, NPRM], F32, kind="ExternalInput").ap()
        self.w_in = dt("w_in", [L, D, DIN], F32, kind="ExternalInput").ap()
        self.w_kidup = dt("w_kidup", [L, D, 128], F32, kind="ExternalInput").ap()
        self.w2 = dt("w2", [L, 64, 512], F32, kind="ExternalInput").ap()
        self.a2 = dt("a2", [L, 64, 512], F32, kind="ExternalInput").ap()
        self.wr_bd = dt("wr_bd", [L, 4, 128, 128], F32, kind="ExternalInput").ap()
        self.wi_bd = dt("wi_bd", [L, 4, 128, 128], F32, kind="ExternalInput").ap()
        self.w_branch = dt("w_branch", [L, 3, 512, D], F32, kind="ExternalInput").ap()
        self.w_out = dt("w_out", [L, D, D], F32, kind="ExternalInput").ap()
        self.w_ple = dt("w_ple", [L, 256, D], F32, kind="ExternalInput").ap()
        self.w_pg = dt("w_pg", [L, D, D], F32, kind="ExternalInput").ap()
        self.cst_d = dt("cst", [128, 32], F32, kind="ExternalInput").ap()
        self.ropeR_d = dt("ropeR", [128, 128], F32, kind="ExternalInput").ap()
        self.ropeT = dt("ropeT", [2, 128, S], F32, kind="Internal").ap()
        self.outT = dt("outT", [8, 128, S], F32, kind="ExternalOutput").ap()
        okind = lambda n: "ExternalOutput" if n in dbg else "Internal"
        self.hT = dt("hT", [8, 128, S], F32, kind=okind("hT")).ap()
        self.hnT = dt("hnT", [8, 128, S], BF16, kind=okind("hnT")).ap()
        self.yT = [dt("yT%d" % n, [4, 128, S], BF16, kind=okind("yT%d" % n)).ap() for n in range(3)]

    def uname(self, n):
        self._uid = getattr(self, "_uid", 0) + 1
        return "%s_u%d" % (n, self._uid)

    def pcol(self, name, j=0, rows=128):
        o, w = PCOLS[name]
        return self.prm_sb[0:rows, o + j:o + j + 1]

    def load_w(self, dst, key, src_fn, ncols, kt, scale=None, rows=128):
        fw = self.fw
        for k in range(kt):
            for c0 in range(0, ncols, 256):
                cn = min(256, ncols - c0)
                si = self.stg_i
                self.stg_i = (self.stg_i + 1) % 6
                stg = self.stg[si]
                fw.dma(stg[0:rows, 0:cn], src_fn(k)[:, c0:c0 + cn], writes=["stg%d" % si])
                self.cast_rr = getattr(self, "cast_rr", 0) + 1
                eng = ("vector", "scalar", "gpsimd")[self.cast_rr % 3]
                o_ap, i_ap = dst[0:rows, k, c0:c0 + cn], stg[0:rows, 0:cn]
                rk_ = ["stg%d" % si] + (["prm"] if scale is not None else [])
                if eng == "scalar":
                    fw.act(o_ap, i_ap, AF.Copy, scale=(scale(k) if scale is not None else 1.0), reads=rk_, writes=[(key, k)])
                elif scale is not None:
                    fw.v("tensor_scalar", o_ap, i_ap, scale(k), 0.0, ALU.mult, ALU.add, reads=rk_, writes=[(key, k)], eng=eng)
                else:
                    fw.v("tensor_copy", o_ap, i_ap, reads=rk_, writes=[(key, k)], eng=eng)

    def gcol(self, k):
        return self.pcol("norm_g", k)

    def build(self):
        nc = self.nc
        with ExitStack() as st:
            fw = self.fw = FW(nc, st)
            self.st = st
            self.stg = [fw.sb("stg%d" % i, [128, 256], F32) for i in range(6)]
            self.stg_i = 0
            self.prm_sb = fw.sb("prm_sb", [128, NPRM], F32)
            self.ones_f = fw.sb("ones_f", [128, 128], F32)
            fw.v("memset", self.ones_f[:], 1.0, writes=["ones_f"])
            self.PSALL = fw.ps("psall", [128, 8, 512], F32)
            self.PS = [self.PSALL[:, i, :] for i in range(8)]
            self.tiny_col = fw.sb("tiny_col", [128, 1], F32)
            self.gneps_col = fw.sb("gneps_col", [128, 1], F32)
            fw.v("memset", self.tiny_col[:], 1e-30, writes=["tiny"])
            fw.v("memset", self.gneps_col[:], GN_EPS, writes=["tiny"])
            self.eps6_col = fw.sb("eps6_col", [128, 1], F32)
            fw.v("memset", self.eps6_col[:], NORM_EPS, writes=["tiny"])
            self.cst_sb = fw.sb("cst_sb", [128, 32], F32)
            fw.dma(self.cst_sb[:], self.cst_d, writes=["cst"])
            self.make_consts()
            if "B" in self.phases:
                self.phase_R()
            with ExitStack() as zs:
                for n, ph_ in enumerate("ABC"):
                    if ph_ not in self.phases:
                        zt = zs.enter_context(self.nc.sbuf_tensor(self.uname("zt"), [128, 4, 512], BF16))
                        self.acquire(["zt%d" % n])
                        fw.v("memset", zt[:], 0.0, writes=["zt%d" % n])
                        for g in range(self.G):
                            fw.dma(self.yT[n][:, :, g * 512:(g + 1) * 512].rearrange("k p s -> p k s"), zt[:], reads=["zt%d" % n],
                                   writes=[("yT%d" % n, g, ct) for ct in range(4)])
                        self.release(["zt%d" % n])
            for l in range(self.L):
                fw.dma(self.prm_sb[:], self.prm[l], writes=["prm"])
                if l == 0 and "N" in self.phases:
                    self.phase_N0()
                if "A" in self.phases:
                    self.phase_A(l)
                if "C" in self.phases:
                    self.phase_C(l)
                if "B" in self.phases:
                    self.phase_B(l)
                if "M" in self.phases:
                    self.phase_M(l)
            fw.finish([("outT", g) for g in range(self.G)])
            fw.emit()
        return nc

    def norm_group(self, hbuf, hkey, g, tmp, tmpkey):
        fw, S = self.fw, self.S
        c0 = g * 512
        ps = self.PS[7]
        for k in range(8):
            fw.act(tmp[:, k % 2, :], hbuf[:, k, :], AF.Square, reads=[hkey], writes=[(tmpkey, k % 2)])
            fw.mm(ps[:], self.ones_f[:], tmp[:, k % 2, :], start=(k == 0), stop=(k == 7),
                  reads=["ones_f", (tmpkey, k % 2)], writes=["ps7"])
        rs = self.rs_sb
        fw.act(rs[:], ps[:], AF.Sqrt, bias=self.eps_col[:, 0:1], scale=1.0 / D, reads=["ps7", "eps"], writes=["rs"])
        fw.v("reciprocal", rs[:], rs[:], reads=["rs"], writes=["rs"])
        hn = self.hn_out
        fw.v("tensor_tensor", hn[:], hbuf[:], rs[:].unsqueeze(1).to_broadcast([128, 8, 512]), ALU.mult,
             reads=[hkey, "rs"], writes=[self.hn_out_key])
        fw.dma(self.hnT[:, :, c0:c0 + 512].rearrange("k p s -> p k s"), hn[:], reads=[self.hn_out_key],
               writes=[("hnT", g)], eng="gpsimd")

    def phase_N0(self):
        fw = self.fw
        with ExitStack() as ph:
            sb = lambda n, s, d: ph.enter_context(self.nc.sbuf_tensor(self.uname(n), list(s), d))
            hb = [sb("n0_h%d" % i, [128, 8, 512], F32) for i in range(2)]
            tmp = sb("n0_tmp", [128, 2, 512], F32)
            self.rs_sb = sb("n0_rs", [128, 512], F32)
            self.hn_out = sb("n0_hn", [128, 8, 512], BF16)
            self.hn_out_key = "hn_out"
            self.eps_col = sb("n0_eps", [128, 1], F32)
            self.acquire(["n0_h0", "n0_h1", ("n0_tmp", 0), ("n0_tmp", 1), "rs", "hn_out", "eps"])
            fw.v("memset", self.eps_col[:], NORM_EPS, writes=["eps"])
            for g in range(self.G):
                c0 = g * 512
                h = hb[g % 2]
                fw.dma(h[:], self.xT[:, :, c0:c0 + 512].rearrange("k p s -> p k s"), writes=["n0_h%d" % (g % 2)])
                self.norm_group(h, "n0_h%d" % (g % 2), g, tmp, "n0_tmp")
            self.release(["n0_h0", "n0_h1", ("n0_tmp", 0), ("n0_tmp", 1), "rs", "hn_out", "eps"])

    def release(self, keys):
        fw = self.fw
        ev = []
        for k in keys:
            if k in fw.lastw:
                ev.append(fw.lastw[k])
            ev.extend(fw.readers.get(k, ()))
        best = {}
        for e in getattr(fw, "pending_release", []) + ev:
            if e[0] not in best or best[e[0]][2] < e[2]:
                best[e[0]] = e
        fw.pending_release = list(best.values())

    def acquire(self, keys):
        fw = self.fw
        ev = getattr(fw, "pending_release", [])
        for k in keys:
            fw.readers.setdefault(k, []).extend(ev)


    def make_consts(self):
        fw = self.fw
        onesb = fw.sb("k_onesb", [128, 256], BF16)
        self.ident_b = fw.sb("k_ident", [128, 128], BF16)
        self.mask_ui = fw.sb("k_mask_ui", [128, 256], BF16)
        self.mask_sl = fw.sb("k_mask_sl", [128, 128], BF16)
        self.blk1 = fw.sb("k_blk1", [128, 128], F32)
        g = "gpsimd"
        fw.v("memset", onesb[:], 1.0, writes=["k_onesb"], eng=g)
        sel = lambda out, pat, cm, op, key: fw.op(g, lambda e: e.affine_select(out, onesb[:, 0:128], pat, op, 0.0, base=0,
                                                                                channel_multiplier=cm),
                                                  reads=["k_onesb"], writes=[key])
        sel(self.ident_b[:], [[-1, 128]], 1, ALU.is_equal, "k_ident")
        sel(self.mask_ui[:, 0:128], [[1, 128]], -1, ALU.is_gt, "k_mask_ui")
        sel(self.mask_ui[:, 128:256], [[1, 128]], -1, ALU.is_ge, "k_mask_ui")
        sel(self.mask_sl[:], [[-1, 128]], 1, ALU.is_gt, "k_mask_sl")
        fw.v("memset", self.blk1[:], 0.0, writes=["k_blk1"], eng=g)
        fw.v("memset", self.blk1[0:64, 0:64], 1.0, writes=["k_blk1"], eng=g)
        fw.v("memset", self.blk1[64:128, 64:128], 1.0, writes=["k_blk1"], eng=g)

    def phase_A(self, l):
        fw, S, G = self.fw, self.S, self.G
        PS = self.PS
        CDEC = 0.6065306597126334
        with ExitStack() as ph:
            allkeys = []

            def sb(n, s, d):
                allkeys.append(n)
                return ph.enter_context(self.nc.sbuf_tensor(self.uname(n), list(s), d))

            wA = sb("wA", [128, 8, 2176], BF16)
            w2b = sb("a_w2b", [64, 1, 512], BF16)
            a2b = sb("a_a2b", [64, 1, 512], BF16)
            hn = [sb("a_hn0", [128, 8, 512], BF16)] * 2
            omu = sb("a_omu", [128, NPRM], F32)
            prevc = sb("a_prevc", [128, 18], F32)
            ubP = [[sb("a_ub%d_%d" % (p, q), [128, 513], F32) for q in range(4)] for p in range(2)]
            usP = [[sb("a_us%d_%d" % (p, q), [128, 512], F32) for q in range(4)] for p in range(2)]
            ulo = [sb("a_ulo%d" % q, [64, 513], F32) for q in range(2)]
            twl = sb("a_twl", [64, 512], BF16)
            alb = sb("a_alb", [64, 512], BF16)
            tP = [[sb("a_t%d_%d" % (p, i), [128, 512], F32) for i in range(8)] for p in range(2)]
            t_ = tP[0]
            art = [sb("a_art%d" % ct, [128, 4, 2, 128], BF16) for ct in range(4)]
            bk = [sb("a_bk%d" % ct, [128, 2, 512], BF16) for ct in range(4)]
            vb = [sb("a_vb%d" % ct, [128, 512], BF16) for ct in range(4)]
            tok = [sb("a_tok%d" % ct, [128, 4, 3, 128], BF16) for ct in range(4)]
            bonus = [sb("a_bonus%d" % ct, [128, 512], BF16) for ct in range(4)]
            sgt = [sb("a_sg%d" % ct, [128, 512], BF16) for ct in range(4)]
            PC = sb("a_PC", [128, 4, 4], F32)
            T = sb("a_T", [128, 4, 64], F32)
            Tb = sb("a_Tb", [128, 4, 64], BF16)
            LAb = [sb("a_LAb%d" % i, [128, 4, 256], BF16) for i in range(2)]
            KAb = [sb("a_KAb%d" % i, [128, 4, 256], BF16) for i in range(2)]
            Lb = [sb("a_Lb%d" % i, [128, 4, 128], BF16) for i in range(2)]
            PPb = [[sb("a_PPb%d_%d" % (i, j), [128, 4, 256], BF16) for j in range(2)] for i in range(2)]
            XT = [[sb("a_XT%d_%d" % (i, j), [128, 4, 128], BF16) for j in range(2)] for i in range(2)]
            Wb = [sb("a_Wb%d" % i, [128, 4, 64], BF16) for i in range(2)]
            Ub = [sb("a_Ub%d" % i, [128, 4, 64], BF16) for i in range(2)]
            xc = sb("a_xc", [128, 8, 64], F32)
            sq = sb("a_sq", [128, 8, 64], F32)
            st8 = sb("a_st8", [128, 4, 8], F32)
            onb = sb("a_onb", [128, 512], BF16)
            yv = [sb("a_yv%d" % i, [128, 128], F32) for i in range(2)]
            yout = sb("a_yout", [128, 4, 512], BF16)
            self.rstm = sb("k_rstm", [128, 512], F32)
            keys = allkeys + [("wA", k) for k in range(8)] + [("a_w2b", 0), ("a_a2b", 0)]
            self.acquire(keys + ["stg0", "stg1"])
            fw.v("memset", self.rstm[:], 1.0, writes=["k_rstm"], eng="gpsimd")
            for c in range(4):
                fw.v("memset", self.rstm[:, c * 128:c * 128 + 1], 0.0, writes=["k_rstm"], eng="gpsimd")

            self.load_w(wA, "wA", lambda k: self.w_in[l, k * 128:(k + 1) * 128, OFF_A:OFF_A + 2176], 2176, 8, self.gcol)
            self.load_w(w2b, "a_w2b", lambda k: self.w2[l], 512, 1, rows=64)
            self.load_w(a2b, "a_a2b", lambda k: self.a2[l], 512, 1, rows=64)
            fw.v("tensor_scalar", omu[:], self.prm_sb[:], -1.0, 1.0, ALU.mult, ALU.add, reads=["prm"], writes=["a_omu"])
            fw.v("memset", prevc[:], 0.0, writes=["a_prevc"])
            fw.v("memset", T[:], 0.0, writes=["a_T"])
            fw.v("memset", Tb[:], 0.0, writes=["a_Tb"])
            oc = lambda name, j=0, rows=128: omu[0:rows, PCOLS[name][0] + j:PCOLS[name][0] + j + 1]
            psb0 = PS[0][:].bitcast(BF16)
            psb1 = PS[1][:].bitcast(BF16)

            def shift(ps, pskey, ubt, ubkey, pcol, out, okey, mu_ap, omu_ap, rows=128):
                fw.v("tensor_copy", ubt[0:rows, 0:1], prevc[0:rows, pcol:pcol + 1], reads=["a_prevc"], writes=[ubkey], eng="gpsimd")
                fw.act(ubt[0:rows, 1:513], ps, AF.Copy, reads=[pskey], writes=[ubkey])
                fw.v("tensor_copy", prevc[0:rows, pcol:pcol + 1], ubt[0:rows, 512:513], reads=[ubkey], writes=["a_prevc"], eng="gpsimd")
                fw.v("tensor_scalar", out, ubt[0:rows, 0:512], mu_ap, None, ALU.mult, reads=[ubkey, "prm"], writes=[okey])
                fw.v("scalar_tensor_tensor", out, ubt[0:rows, 1:513], omu_ap, out, ALU.mult, ALU.add,
                     reads=[ubkey, "a_omu", okey], writes=[okey])

            for g in range(G):
                c0 = g * 512
                hk = "a_hn0"
                hg = hn[0]
                fw.dma(hg[:], self.hnT[:, :, c0:c0 + 512].rearrange("k p s -> p k s"), reads=[("hnT", g)], writes=[hk])
                for q, (coff, nm) in enumerate([(1536, "mu_wl"), (1600, "mu_al")]):
                    for k in range(8):
                        fw.mm(PS[q][0:64, :], wA[:, k, coff:coff + 64], hg[:, k, :], start=(k == 0), stop=(k == 7),
                              reads=[("wA", k), hk], writes=["ps%d" % q])
                    shift(PS[q][0:64, :], "ps%d" % q, ulo[q], "a_ulo%d" % q, 16 + q, t_[q][0:64, :], "a_t0_%d" % q,
                          self.pcol(nm, 0, 64), oc(nm, 0, 64), rows=64)
                fw.act(twl[:], t_[0][0:64, :], AF.Tanh, reads=["a_t0_0"], writes=["a_twl"])
                fw.v("tensor_copy", alb[:], t_[1][0:64, :], reads=["a_t0_1"], writes=["a_alb"])
                def abody(ct, g=g, hk=hk, hg=hg):
                    p_ = ct % 2
                    PSp = PS[4 * p_:4 * p_ + 4]
                    pk = lambda q: "ps%d" % (4 * p_ + q)
                    ub, us, t_ = ubP[p_], usP[p_], tP[p_]
                    psb0 = PSp[0][:].bitcast(BF16)
                    psb1 = PSp[1][:].bitcast(BF16)
                    cs = slice(ct * 128, (ct + 1) * 128)
                    for q, (coff, nm) in enumerate([(0, "mu_r"), (512, "mu_k"), (1024, "mu_v"), (1664, "mu_g")]):
                        for k in range(8):
                            fw.mm(PSp[q][:], wA[:, k, coff + ct * 128:coff + (ct + 1) * 128], hg[:, k, :], start=(k == 0), stop=(k == 7),
                                  reads=[("wA", k), hk], writes=[pk(q)])
                            yield
                        shift(PSp[q][:], pk(q), ub[q], "a_ub%d_%d" % (p_, q), ct * 4 + q, us[q][:], "a_us%d_%d" % (p_, q),
                              self.pcol(nm, ct), oc(nm, ct))
                        yield
                    r_s, k_s, v_s, g_s = us
                    K = lambda i: "a_t%d_%d" % (p_, i)
                    fw.mm(PSp[0][:], w2b[:, 0, cs], twl[:], reads=[("a_w2b", 0), "a_twl"], writes=[pk(0)])
                    yield
                    fw.act(t_[0][:], PSp[0][:], AF.Sigmoid, bias=self.pcol("w0", ct), reads=[pk(0), "prm"], writes=[K(0)])
                    yield
                    fw.v("tensor_scalar", t_[0][:], t_[0][:], -CDEC, 0.0, ALU.mult, ALU.add, reads=[K(0)], writes=[K(0)], eng="gpsimd")
                    yield
                    fw.mm(PSp[1][:], a2b[:, 0, cs], alb[:], reads=[("a_a2b", 0), "a_alb"], writes=[pk(1)])
                    yield
                    fw.act(t_[1][:], PSp[1][:], AF.Sigmoid, bias=self.pcol("a0", ct), reads=[pk(1), "prm"], writes=[K(1)])
                    yield
                    fw.v("tensor_scalar", t_[2][:], k_s[:], self.pcol("k_k", ct), None, ALU.mult, reads=["a_us%d_1" % p_, "prm"], writes=[K(2)])
                    yield
                    fw.v("tensor_tensor", t_[3][:], t_[2][:], t_[2][:], ALU.mult, reads=[K(2)], writes=[K(3)], eng="gpsimd")
                    yield
                    fw.mm(PSp[2][:], self.blk1[:], t_[3][:], reads=["k_blk1", K(3)], writes=[pk(2)])
                    yield
                    fw.act(t_[3][:], PSp[2][:], AF.Sqrt, bias=self.tiny_col[:, 0:1], reads=[pk(2), "tiny"], writes=[K(3)])
                    yield
                    fw.v("reciprocal", t_[3][:], t_[3][:], reads=[K(3)], writes=[K(3)])
                    yield
                    fw.v("tensor_tensor", t_[2][:], t_[2][:], t_[3][:], ALU.mult, reads=[K(2), K(3)], writes=[K(2)])
                    yield
                    fw.v("tensor_scalar", t_[3][:], t_[1][:], self.pcol("k_a", ct), oc("k_a", ct), ALU.mult, ALU.add,
                         reads=[K(1), "prm", "a_omu"], writes=[K(3)])
                    yield
                    fw.v("tensor_tensor", t_[3][:], t_[3][:], k_s[:], ALU.mult, reads=[K(3), "a_us%d_1" % p_], writes=[K(3)], eng="gpsimd")
                    yield
                    fw.v("tensor_tensor", t_[4][:], t_[2][:], t_[1][:], ALU.mult, reads=[K(2), K(1)], writes=[K(4)], eng="gpsimd")
                    yield
                    fw.v("tensor_tensor_scan", t_[5][:], self.rstm[:], t_[0][:], 0.0, ALU.mult, ALU.add,
                         reads=["k_rstm", K(0)], writes=[K(5)])
                    yield
                    fw.v("tensor_tensor", t_[6][:], t_[5][:], t_[0][:], ALU.subtract, reads=[K(5), K(0)], writes=[K(6)], eng="gpsimd")
                    yield
                    fw.act(t_[6][:], t_[6][:], AF.Exp, reads=[K(6)], writes=[K(6)])
                    yield
                    fw.act(t_[7][:], t_[5][:], AF.Exp, scale=-1.0, reads=[K(5)], writes=[K(7)])
                    yield
                    fw.act(t_[5][:], t_[5][:], AF.Exp, reads=[K(5)], writes=[K(5)])
                    yield
                    fw.v("tensor_copy", PC[:, ct, :], t_[5][:].rearrange("p (c t) -> p c t", t=128)[:, :, 127], reads=[K(5)],
                         writes=["a_PC"], eng="gpsimd")
                    yield
                    v3 = lambda ap: ap.rearrange("p (c t) -> p c t", t=128)
                    akey = "a_art%d" % ct
                    fw.v("scalar_tensor_tensor", art[ct][:, :, 0, :], v3(t_[2][:]), -1.0, v3(t_[6][:]), ALU.mult, ALU.mult,
                         reads=[K(2), K(6)], writes=[akey])
                    yield
                    fw.v("tensor_tensor", art[ct][:, :, 1, :], v3(r_s[:]), v3(t_[5][:]), ALU.mult, reads=["a_us%d_0" % p_, K(5)], writes=[akey])
                    yield
                    fw.v("tensor_tensor", bk[ct][:, 0, :], t_[4][:], t_[7][:], ALU.mult, reads=[K(4), K(7)], writes=["a_bk%d" % ct])
                    yield
                    fw.v("tensor_tensor", bk[ct][:, 1, :], t_[3][:], t_[7][:], ALU.mult, reads=[K(3), K(7)], writes=["a_bk%d" % ct], eng="gpsimd")
                    yield
                    fw.v("tensor_copy", vb[ct][:], v_s[:], reads=["a_us%d_2" % p_], writes=["a_vb%d" % ct], eng="gpsimd")
                    yield
                    fw.v("scalar_tensor_tensor", t_[4][:], r_s[:], self.pcol("r_k", ct), t_[3][:], ALU.mult, ALU.mult,
                         reads=["a_us%d_0" % p_, "prm", K(3), K(4)], writes=[K(4)])
                    yield
                    fw.mm(PSp[3][:], self.blk1[:], t_[4][:], reads=["k_blk1", K(4)], writes=[pk(3)])
                    yield
                    fw.v("tensor_tensor", bonus[ct][:], PSp[3][:], v_s[:], ALU.mult, reads=[pk(3), "a_us%d_2" % p_], writes=["a_bonus%d" % ct])
                    yield
                    fw.act(sgt[ct][:], g_s[:], AF.Silu, reads=["a_us%d_3" % p_], writes=["a_sg%d" % ct])
                    yield
                    for half in range(2):
                        psb, pkey = (psb0, pk(0)) if half == 0 else (psb1, pk(1))
                        for cc in range(2):
                            c = half * 2 + cc
                            for qi_, (src, skey) in enumerate([(bk[ct][:, 0, c * 128:(c + 1) * 128], "a_bk%d" % ct),
                                                               (bk[ct][:, 1, c * 128:(c + 1) * 128], "a_bk%d" % ct),
                                                               (vb[ct][:, c * 128:(c + 1) * 128], "a_vb%d" % ct)]):
                                o = (cc * 3 + qi_) * 128
                                fw.tr(psb[:, o:o + 128], src, self.ident_b[:], reads=[skey, "k_ident"], writes=[pkey])
                                yield
                        fw.act(tok[ct][:, half * 2:half * 2 + 2, :, :].rearrange("p a b c -> p (a b c)"), psb[:, 0:768], AF.Copy,
                               reads=[pkey], writes=["a_tok%d" % ct])
                        yield

                fw.lockstep([abody(0), abody(1)])
                fw.lockstep([abody(2), abody(3)])
                PSALL = self.PSALL
                idb4 = self.ident_b[:].unsqueeze(1).to_broadcast([128, 4, 128])
                mui2 = self.mask_ui[:].unsqueeze(1).to_broadcast([128, 2, 256])
                msl4 = self.mask_sl[:].unsqueeze(1).to_broadcast([128, 4, 128])
                for c in range(4):
                    ccols = slice(c * 128, (c + 1) * 128)

                    def hv(qd, hi):
                        h = 2 * hi + qd
                        ct, hp = hi, qd
                        pr_ = slice(hp * 64, hp * 64 + 64)
                        d = dict(h=h, ct=ct, hp=hp, pr=pr_, po=hp * 64,
                                 at=art[ct][pr_, c, 0, :], rt=art[ct][pr_, c, 1, :],
                                 ar=art[ct][pr_, c, :, :].rearrange("p a t -> p (a t)"),
                                 bt=bk[ct][pr_, 0, ccols], kt=bk[ct][pr_, 1, ccols],
                                 rk=["a_art%d" % ct, "a_bk%d" % ct], tkey="a_tok%d" % ct,
                                 vt=tok[ct][:, c, 2, hp * 64:hp * 64 + 64], btk=tok[ct][:, c, 0, hp * 64:hp * 64 + 64],
                                 ktk=tok[ct][:, c, 1, hp * 64:hp * 64 + 64], T0b=Tb[pr_, ct, :])
                        return d

                    XYk = lambda qd: ["ps%d" % (3 * qd), "ps%d" % (3 * qd + 1)]
                    Zk = lambda qd: ["ps%d" % (3 * qd + 2)]
                    XY = lambda qd: PSALL[:, 3 * qd:3 * qd + 2, :].rearrange("p b (h x) -> p (b h) x", x=256)
                    Zv = lambda qd: PSALL[:, 3 * qd + 2, :].rearrange("p (h x) -> p h x", x=128)
                    for qd in range(2):
                        for hi in range(4):
                            d = hv(qd, hi)
                            fw.mm(XY(qd)[:, hi, :], d["bt"], d["ar"], reads=d["rk"], writes=[XYk(qd)[hi // 2]])
                            fw.mm(Zv(qd)[:, hi, :], d["at"], d["bt"], reads=d["rk"], writes=Zk(qd))
                    for qd in range(2):
                        for b2 in range(2):
                            fw.v("tensor_tensor", LAb[qd][:, 2 * b2:2 * b2 + 2, :], XY(qd)[:, 2 * b2:2 * b2 + 2, :], mui2, ALU.mult,
                                 reads=[XYk(qd)[b2], "k_mask_ui"], writes=["a_LAb%d" % qd])
                        fw.v("tensor_tensor", Lb[qd][:], Zv(qd), msl4, ALU.mult, reads=Zk(qd) + ["k_mask_sl"], writes=["a_Lb%d" % qd])
                        fw.v("tensor_tensor", XT[qd][0][:], LAb[qd][:, :, 0:128], idb4, ALU.add,
                             reads=["a_LAb%d" % qd, "k_ident"], writes=["a_XT%d_0" % qd], eng="gpsimd")
                    for k in range(1, 8):
                        for qd in range(2):
                            if k == 1:
                                Pp, PTp, pkeys = (lambda hi: Lb[qd][:, hi, :]), (lambda hi: LAb[qd][:, hi, 0:128]), ["a_Lb%d" % qd, "a_LAb%d" % qd]
                            else:
                                pb_ = PPb[qd][(k - 1) % 2]
                                Pp, PTp, pkeys = (lambda hi, pb_=pb_: pb_[:, hi, 0:128]), (lambda hi, pb_=pb_: pb_[:, hi, 128:256]), ["a_PPb%d_%d" % (qd, (k - 1) % 2)]
                            for hi in range(4):
                                if k <= 6:
                                    fw.mm(XY(qd)[:, hi, 0:128], PTp(hi), Pp(hi), reads=pkeys, writes=[XYk(qd)[hi // 2]])
                                    fw.mm(XY(qd)[:, hi, 128:256], Pp(hi), PTp(hi), reads=pkeys, writes=[XYk(qd)[hi // 2]])
                                if k == 7:
                                    d7 = hv(qd, hi)
                                    fw.mm(XY(qd)[:, hi, :], d7["kt"], d7["ar"], reads=d7["rk"], writes=[XYk(qd)[hi // 2]])
                                if k >= 2:
                                    xo = XT[qd][(k - 2) % 2]
                                    xok = "a_XT%d_%d" % (qd, (k - 2) % 2)
                                    fw.mm(Zv(qd)[:, hi, :], self.ident_b[:], xo[:, hi, :], start=True, stop=False, reads=["k_ident", xok], writes=Zk(qd))
                                    fw.mm(Zv(qd)[:, hi, :], Pp(hi), xo[:, hi, :], start=False, stop=True, reads=pkeys + [xok], writes=Zk(qd))
                        for qd in range(2):
                            if k <= 6:
                                for b2 in range(2):
                                    fw.act(PPb[qd][k % 2][:, 2 * b2:2 * b2 + 2, :], XY(qd)[:, 2 * b2:2 * b2 + 2, :], AF.Copy,
                                           reads=[XYk(qd)[b2]], writes=["a_PPb%d_%d" % (qd, k % 2)])
                            if k >= 2:
                                fw.v("tensor_copy", XT[qd][(k - 1) % 2][:], Zv(qd), reads=Zk(qd), writes=["a_XT%d_%d" % (qd, (k - 1) % 2)])
                            if k == 7:
                                for b2 in range(2):
                                    fw.v("tensor_tensor", KAb[qd][:, 2 * b2:2 * b2 + 2, :], XY(qd)[:, 2 * b2:2 * b2 + 2, :], mui2, ALU.mult,
                                         reads=[XYk(qd)[b2], "k_mask_ui"], writes=["a_KAb%d" % qd])
                    XTf = [XT[qd][0] for qd in range(2)]
                    xfk = ["a_XT%d_0" % qd for qd in range(2)]
                    Wv = lambda qd: PSALL[:, 3 * qd + 2, 0:256].rearrange("p (h x) -> p h x", x=64)
                    Uv = lambda qd: PSALL[:, 3 * qd + 2, 256:512].rearrange("p (h x) -> p h x", x=64)
                    for qd in range(2):
                        for hi in range(4):
                            d = hv(qd, hi)
                            fw.mm(Wv(qd)[:, hi, :], d["at"], d["T0b"], start=True, stop=False, reads=["a_art%d" % d["ct"], "a_Tb"], writes=Zk(qd))
                            fw.mm(Wv(qd)[:, hi, :], KAb[qd][:, hi, 0:128], d["vt"], start=False, stop=True, reads=["a_KAb%d" % qd, d["tkey"]], writes=Zk(qd))
                        fw.v("tensor_copy", Wb[qd][:], Wv(qd), reads=Zk(qd), writes=["a_Wb%d" % qd])
                    for qd in range(2):
                        for hi in range(4):
                            fw.mm(Uv(qd)[:, hi, :], XTf[qd][:, hi, :], Wb[qd][:, hi, :], reads=[xfk[qd], "a_Wb%d" % qd], writes=Zk(qd))
                        fw.v("tensor_copy", Ub[qd][:], Uv(qd), reads=Zk(qd), writes=["a_Ub%d" % qd])
                    for qd in range(2):
                        for hi in range(4):
                            d = hv(qd, hi)
                            h, ct = d["h"], d["ct"]
                            ob, okey = (PS[6], "ps6") if qd == 0 else (PS[7], "ps7")
                            osl = slice(qd * 256 + hi * 64, qd * 256 + (hi + 1) * 64)
                            fw.mm(ob[:, osl], d["rt"], d["T0b"], start=True, stop=False, reads=["a_art%d" % ct, "a_Tb"], writes=[okey])
                            fw.mm(ob[:, osl], LAb[qd][:, hi, 128:256], Ub[qd][:, hi, :], start=False, stop=False,
                                  reads=["a_LAb%d" % qd, "a_Ub%d" % qd], writes=[okey])
                            fw.mm(ob[:, osl], KAb[qd][:, hi, 128:256], d["vt"], start=False, stop=True, reads=["a_KAb%d" % qd, d["tkey"]], writes=[okey])
                            zsl = slice(ct * 64, (ct + 1) * 64)
                            fw.mm(PS[7][d["pr"], zsl], d["btk"], Ub[qd][:, hi, :], start=True, stop=False, reads=[d["tkey"], "a_Ub%d" % qd], writes=["ps7"])
                            fw.mm(PS[7][d["pr"], zsl], d["ktk"], d["vt"], start=False, stop=True, reads=[d["tkey"]], writes=["ps7"])
                    zall = PS[7][:, 0:256].rearrange("p (c i) -> p c i", i=64)
                    fw.v("tensor_tensor", T[:], T[:], zall, ALU.add, reads=["a_T", "ps7"], writes=["a_T"])
                    fw.v("tensor_tensor", T[:], T[:], PC[:, :, c:c + 1].to_broadcast([128, 4, 64]), ALU.mult, reads=["a_T", "a_PC"], writes=["a_T"])
                    fw.v("tensor_copy", Tb[:], T[:], reads=["a_T"], writes=["a_Tb"], eng="gpsimd")
                    ov = [PS[6][:, 0:256].rearrange("p (h i) -> p h i", i=64), PS[7][:, 256:512].rearrange("p (h i) -> p h i", i=64)]
                    okeys = ["ps6", "ps7"]
                    for qd in range(2):
                        fw.v("tensor_reduce", st8[:, 0, qd * 4:qd * 4 + 4], ov[qd], AX.X, ALU.add, reads=[okeys[qd]], writes=["a_st8"])
                    fw.v("tensor_scalar", st8[:, 0, :], st8[:, 0, :], 1.0 / 64, None, ALU.mult, reads=["a_st8"], writes=["a_st8"])
                    for qd in range(2):
                        fw.v("tensor_tensor", xc[:, qd * 4:qd * 4 + 4, :], ov[qd],
                             st8[:, 0, qd * 4:qd * 4 + 4].unsqueeze(2).to_broadcast([128, 4, 64]), ALU.subtract,
                             reads=[okeys[qd], "a_st8"], writes=["a_xc"])
                    fw.v("tensor_tensor", sq[:], xc[:], xc[:], ALU.mult, reads=["a_xc"], writes=["a_sq"], eng="gpsimd")
                    fw.v("tensor_reduce", st8[:, 1, :], sq[:], AX.X, ALU.add, reads=["a_sq"], writes=["a_st8"])
                    fw.act(st8[:, 1, :], st8[:, 1, :], AF.Sqrt, bias=self.gneps_col[:, 0:1], scale=1.0 / 64, reads=["a_st8", "tiny"], writes=["a_st8"])
                    fw.v("reciprocal", st8[:, 1, :], st8[:, 1, :], reads=["a_st8"], writes=["a_st8"])
                    fw.v("tensor_tensor", onb[:].rearrange("p (h i) -> p h i", i=64), xc[:],
                         st8[:, 1, :].unsqueeze(2).to_broadcast([128, 8, 64]), ALU.mult, reads=["a_xc", "a_st8"], writes=["a_onb"])
                    for ct in range(4):
                        for hp in range(2):
                            qo = (hp * 4 + ct) * 64
                            fw.tr(psb0[hp * 64:(hp + 1) * 64, ct * 128:(ct + 1) * 128], onb[:, qo:qo + 64], self.ident_b[:],
                                  reads=["a_onb", "k_ident"], writes=["ps0"])
                    for ct in range(4):
                        j = ct % 2
                        fw.v("tensor_scalar", yv[j][:], psb0[:, ct * 128:(ct + 1) * 128], self.pcol("gn_g", ct), self.pcol("gn_b", ct),
                             ALU.mult, ALU.add, reads=["ps0", "prm"], writes=["a_yv%d" % j])
                        fw.v("tensor_tensor", yv[j][:], yv[j][:], bonus[ct][:, ccols], ALU.add, reads=["a_yv%d" % j, "a_bonus%d" % ct],
                             writes=["a_yv%d" % j], eng="gpsimd")
                        fw.v("tensor_tensor", yout[:, ct, ccols], yv[j][:], sgt[ct][:, ccols], ALU.mult,
                             reads=["a_yv%d" % j, "a_sg%d" % ct], writes=["a_yout"], eng="gpsimd")
                fw.dma(self.yT[0][:, :, c0:c0 + 512].rearrange("k p s -> p k s"), yout[:], reads=["a_yout"],
                       writes=[("yT0", g, ct) for ct in range(4)], eng="gpsimd")
            self.release(keys)


    def phase_R(self):
        fw, S, G = self.fw, self.S, self.G
        TWO_PI = 6.283185307179586
        C1 = 6.28125
        C2 = 0.0019350051879882812
        C3 = TWO_PI - C1 - C2
        PI = 3.1415925
        with ExitStack() as ph:
            sb = lambda n, s, d: ph.enter_context(self.nc.sbuf_tensor(self.uname(n), list(s), d))
            posi = sb("r_posi", [128, 512], I32)
            a = sb("r_a", [128, 512], F32)
            k = sb("r_k", [128, 512], F32)
            r = sb("r_r", [128, 512], F32)
            r2 = sb("r_r2", [128, 512], F32)
            m = sb("r_m", [128, 512], F32)
            cs = sb("r_cs", [128, 2, 512], F32)
            keys = ["r_posi", "r_a", "r_k", "r_r", "r_r2", "r_m", "r_cs"]
            self.acquire(keys)
            for g in range(G):
                c0 = g * 512
                fw.dma(posi[:], self.pos[0:1, c0:c0 + 512].to_broadcast([128, 512]), writes=["r_posi"])
                fw.v("tensor_copy", a[:], posi[:], reads=["r_posi"], writes=["r_a"])
                fw.v("tensor_scalar", a[:], a[:], self.cst_sb[:, 0:1], None, ALU.mult, reads=["r_a", "cst"], writes=["r_a"])
                fw.v("tensor_scalar", k[:], a[:], 1.0 / TWO_PI, None, ALU.mult, reads=["r_a"], writes=["r_k"])
                fw.v("tensor_scalar", k[:], k[:], 12582912.0, None, ALU.add, reads=["r_k"], writes=["r_k"])
                fw.v("tensor_scalar", k[:], k[:], 12582912.0, None, ALU.subtract, reads=["r_k"], writes=["r_k"])
                fw.v("scalar_tensor_tensor", r[:], k[:], -C1, a[:], ALU.mult, ALU.add, reads=["r_k", "r_a"], writes=["r_r"])
                fw.v("scalar_tensor_tensor", r[:], k[:], -C2, r[:], ALU.mult, ALU.add, reads=["r_k", "r_r"], writes=["r_r"])
                fw.v("scalar_tensor_tensor", r[:], k[:], -C3, r[:], ALU.mult, ALU.add, reads=["r_k", "r_r"], writes=["r_r"])
                fw.v("tensor_scalar", r[:], r[:], PI, -PI, ALU.min, ALU.max, reads=["r_r"], writes=["r_r"])
                fw.v("tensor_scalar", r2[:], r[:], TWO_PI / 4, None, ALU.add, reads=["r_r"], writes=["r_r2"])
                fw.v("tensor_scalar", m[:], r2[:], PI, -TWO_PI, ALU.is_gt, ALU.mult, reads=["r_r2"], writes=["r_m"])
                fw.v("tensor_tensor", r2[:], r2[:], m[:], ALU.add, reads=["r_r2", "r_m"], writes=["r_r2"])
                fw.v("tensor_scalar", r2[:], r2[:], PI, -PI, ALU.min, ALU.max, reads=["r_r2"], writes=["r_r2"])
                fw.act(cs[:, 0, :], r2[:], AF.Sin, reads=["r_r2"], writes=["r_cs"])
                fw.act(cs[:, 1, :], r[:], AF.Sin, reads=["r_r"], writes=["r_cs"])
                fw.dma(self.ropeT[:, :, c0:c0 + 512].rearrange("k p s -> p k s"), cs[:], reads=["r_cs"], writes=[("ropeT", g)], eng="gpsimd")
            self.release(keys)

    def phase_B(self, l):
        fw, S, G = self.fw, self.S, self.G
        PS = self.PS
        NT = S // 128
        NIT = 20
        NOATT = False
        with ExitStack() as ph:
            allkeys = []

            def sb(n, s, d):
                allkeys.append(n)
                return ph.enter_context(self.nc.sbuf_tensor(self.uname(n), list(s), d))

            wB = sb("wB", [128, 8, 2372], BF16)
            wkd = sb("b_wkd", [128, 8, 128], BF16)
            ropeR = sb("b_ropeR", [128, 1, 128], BF16)
            KT = [sb("b_KT%d" % ct, [128, S], BF16) for ct in range(4)]
            KI = sb("b_KI", [128, S], BF16)
            V = sb("b_V", [128, NT, 8, 65], BF16)
            hn = sb("b_hn", [128, 8, 512], BF16)
            QT = [[sb("b_QT%d_%d" % (ct, i), [128, 512], BF16) for ct in range(4)] for i in range(2)]
            QI = [[sb("b_QI%d_%d" % (j, i), [128, 512], BF16) for j in range(2)] for i in range(2)]
            SG = [[sb("b_SG%d_%d" % (ct, i), [128, 512], BF16) for ct in range(4)] for i in range(2)]
            WI = [sb("b_WI%d" % i, [128, 4, 4], F32) for i in range(2)]
            yout = [sb("b_yout0", [128, 4, 512], BF16)] * 2
            score = sb("b_score", [128, S], F32)
            alias = S >= 4096
            if alias:
                xL = [score[:, 0:512], score[:, 1280:1792]]
                x2L = [score[:, 512:1024], score[:, 1792:2304]]
                xbL = [score[:, 1024:1280].bitcast(BF16), score[:, 2304:2560].bitcast(BF16)]
                cs = score[:, 2560:3584].rearrange("p (a b) -> p a b", b=512)
            else:
                cs = sb("b_cs", [128, 2, 512], F32)
                xL = [sb("b_x%d" % i, [128, 512], F32) for i in range(2)]
                x2L = [sb("b_x2%d" % i, [128, 512], F32) for i in range(2)]
                xbL = [sb("b_xb%d" % i, [128, 512], BF16) for i in range(2)]
            tkeys = ["b_cs"] + ["b_x%d" % i for i in range(2)] + ["b_x2%d" % i for i in range(2)] + ["b_xb%d" % i for i in range(2)]
            mm1 = [sb("b_mm1_0", [128, S], BF16)] * 2
            MT = [sb("b_MT%d" % i, [128, NT, 128], BF16) for i in range(2)]
            E = [sb("b_E%d" % i, [128, 512], BF16) for i in range(4)]
            rl = [sb("b_rl%d" % i, [128, 512], F32) for i in range(2)]
            PT = [sb("b_PT%d" % i, [128, 512], BF16) for i in range(4)]
            bs = sb("b_bs", [128, 8], F32)
            steps = sb("b_steps", [128, NIT + 1], F32)
            rec = sb("b_rec", [128, 8], F32)
            otok = sb("b_otok", [128, 8, 64], BF16)
            dmask = sb("b_dmask", [128, 128], F32)
            keys = allkeys + tkeys + [("wB", k) for k in range(8)] + [("b_wkd", k) for k in range(8)] + [("b_ropeR", 0)] + \
                [("b_KT", ct, g) for ct in range(4) for g in range(G)] + [("b_KI", g) for g in range(G)] + [("b_V", g) for g in range(G)]
            self.acquire(keys + ["stg0", "stg1"])
            self.load_w(wB, "wB", lambda k: self.w_in[l, k * 128:(k + 1) * 128, OFF_B:OFF_B + 2372], 2372, 8, self.gcol)
            self.load_w(wkd, "b_wkd", lambda k: self.w_kidup[l, k * 128:(k + 1) * 128, :], 128, 8, self.gcol)
            self.load_w(ropeR, "b_ropeR", lambda k: self.ropeR_d, 128, 1)
            fw.v("memset", V[:], 1.0, writes=[("b_V", g) for g in range(G)], eng="gpsimd")
            fw.v("memset", dmask[:], 0.0, writes=["b_dmask"], eng="gpsimd")
            fw.v("memset", dmask[0:64, 64:128], -1e30, writes=["b_dmask"], eng="gpsimd")

            def lane(p, chains):
                x_, x2, xb = xL[p], x2L[p], xbL[p]
                kx, kx2, kxb = "b_x%d" % p, "b_x2%d" % p, "b_xb%d" % p
                ps, pk = PS[p], "ps%d" % p

                def proj(w, wkey, c_lo, c_hi):
                    for k in range(8):
                        fw.mm(ps[:], w[:, k, c_lo:c_hi], hn[:, k, :], start=(k == 0), stop=(k == 7), reads=[(wkey, k), "b_hn"], writes=[pk])
                        yield

                def rope(dst, dkey):
                    fw.v("tensor_copy", xb[:], x_[:], reads=[kx], writes=[kxb], eng="gpsimd")
                    yield
                    fw.mm(ps[:], ropeR[:, 0, :], xb[:], reads=[("b_ropeR", 0), kxb], writes=[pk])
                    yield
                    fw.v("tensor_tensor", x2[:], x_[:], cs[:, 0, :], ALU.mult, reads=[kx, "b_cs"], writes=[kx2], eng="gpsimd")
                    yield
                    fw.v("tensor_tensor", x_[:], ps[:], cs[:, 1, :], ALU.mult, reads=[pk, "b_cs", kx], writes=[kx])
                    yield
                    fw.v("tensor_tensor", dst, x2[:], x_[:], ALU.add, reads=[kx2, kx], writes=[dkey], eng="gpsimd")
                    yield

                for ch in chains:
                    kind = ch[0]
                    if kind in ("q", "k"):
                        _, ct, g = ch
                        gp = g % 2
                        gc = slice(g * 512, g * 512 + 512)
                        coff, gname = (0, "q_g") if kind == "q" else (512, "k_g")
                        yield from proj(wB, "wB", coff + ct * 128, coff + (ct + 1) * 128)
                        fw.act(x_[:], ps[:], AF.Copy, reads=[pk], writes=[kx])
                        yield
                        fw.v("tensor_tensor", x2[:], x_[:], x_[:], ALU.mult, reads=[kx], writes=[kx2], eng="gpsimd")
                        yield
                        fw.mm(ps[:], self.blk1[:], x2[:], reads=["k_blk1", kx2], writes=[pk])
                        yield
                        fw.act(x2[:], ps[:], AF.Sqrt, bias=self.eps6_col[:, 0:1], scale=1.0 / 64, reads=[pk, "tiny"], writes=[kx2])
                        yield
                        fw.v("reciprocal", x2[:], x2[:], reads=[kx2], writes=[kx2])
                        yield
                        fw.v("scalar_tensor_tensor", x_[:], x_[:], self.pcol(gname, 0), x2[:], ALU.mult, ALU.mult,
                             reads=[kx, "prm", kx2], writes=[kx])
                        yield
                        if kind == "q":
                            yield from rope(QT[gp][ct][:], "b_QT%d_%d" % (ct, gp))
                        else:
                            yield from rope(KT[ct][:, gc], ("b_KT", ct, g))
                    elif kind == "qi":
                        _, j, g = ch
                        gp = g % 2
                        yield from proj(wB, "wB", 1536 + j * 128, 1536 + (j + 1) * 128)
                        fw.act(x_[:], ps[:], AF.Copy, reads=[pk], writes=[kx])
                        yield
                        yield from rope(QI[gp][j][:], "b_QI%d_%d" % (j, gp))
                    elif kind == "ki":
                        _, g = ch
                        gc = slice(g * 512, g * 512 + 512)
                        yield from proj(wkd, "b_wkd", 0, 128)
                        fw.act(x_[:], ps[:], AF.Copy, reads=[pk], writes=[kx])
                        yield
                        yield from rope(KI[:, gc], ("b_KI", g))
                    elif kind == "sg":
                        _, ct, g = ch
                        gp = g % 2
                        yield from proj(wB, "wB", 1860 + ct * 128, 1860 + (ct + 1) * 128)
                        fw.act(SG[gp][ct][:], ps[:], AF.Silu, reads=[pk], writes=["b_SG%d_%d" % (ct, gp)])
                        yield
                    elif kind == "v":
                        _, tt, g = ch
                        gp = g % 2
                        tcols = slice(tt * 128, (tt + 1) * 128)
                        for k in range(8):
                            fw.mm(ps[:], hn[:, k, tcols], wB[:, k, 1024:1536], start=(k == 0), stop=(k == 7),
                                  reads=[("wB", k), "b_hn"], writes=[pk])
                            yield
                        fw.act(V[:, g * 4 + tt, :, 0:64], ps[:].rearrange("p (h i) -> p h i", i=64), AF.Copy, reads=[pk], writes=[("b_V", g)])
                        yield
                        for k in range(8):
                            fw.mm(ps[:, 0:4], hn[:, k, tcols], wB[:, k, 1856:1860], start=(k == 0), stop=(k == 7),
                                  reads=[("wB", k), "b_hn"], writes=[pk])
                            yield
                        fw.v("tensor_scalar", WI[gp][:, tt, :], ps[:, 0:4], 1.0 / 16, None, ALU.mult, reads=[pk], writes=["b_WI%d" % gp])
                        yield

            def prep_begin(g):
                gc = slice(g * 512, g * 512 + 512)
                if alias:
                    self.release(["b_score"])
                    self.acquire(tkeys)
                fw.dma(hn[:], self.hnT[:, :, gc].rearrange("k p s -> p k s"), reads=[("hnT", g)], writes=["b_hn"])
                fw.dma(cs[:], self.ropeT[:, :, gc].rearrange("k p s -> p k s"), reads=[("ropeT", g)], writes=["b_cs"])

            def prep_lanes(g):
                chains = []
                for ct in range(4):
                    chains += [("q", ct, g), ("k", ct, g)]
                chains += [("qi", 0, g), ("qi", 1, g), ("ki", g)]
                chains += [("sg", ct, g) for ct in range(4)]
                chains += [("v", tt, g) for tt in range(4)]
                return [lane(0, chains[0::2]), lane(1, chains[1::2])]

            def prep_end(g):
                if alias:
                    self.release(tkeys)
                    self.acquire(["b_score"])

            def scores(qt):
                g, tt = qt // 4, qt % 4
                gp = g % 2
                N = (qt + 1) * 128
                tq = slice(tt * 128, (tt + 1) * 128)
                for pc in range((N + 511) // 512):
                    p0 = pc * 512
                    pn = min(512, N - p0)
                    for ih in range(4):
                        po = (ih % 2) * 64
                        fw.mm(PS[ih][:, 0:pn], QI[gp][ih // 2][po:po + 64, tq], KI[po:po + 64, p0:p0 + pn],
                              reads=["b_QI%d_%d" % (ih // 2, gp), ("b_KI", pc)], writes=["ps%d" % ih])
                    for ih in range(4):
                        r_ = rl[ih % 2]
                        rkey = "b_rl%d" % (ih % 2)
                        fw.act(r_[:, 0:pn], PS[ih][:, 0:pn], AF.Relu, reads=["ps%d" % ih], writes=[rkey])
                        if ih == 0:
                            fw.v("tensor_scalar", score[:, p0:p0 + pn], r_[:, 0:pn], WI[gp][:, tt, 0:1], None, ALU.mult,
                                 reads=[rkey, "b_WI%d" % gp], writes=["b_score"])
                        else:
                            fw.v("scalar_tensor_tensor", score[:, p0:p0 + pn], r_[:, 0:pn], WI[gp][:, tt, ih:ih + 1], score[:, p0:p0 + pn],
                                 ALU.mult, ALU.add, reads=[rkey, "b_WI%d" % gp, "b_score"], writes=["b_score"])

            def bisect_mask(qt):
                NB = qt + 1
                N = NB * 128
                mk = mm1[0]
                mkey = "b_mm1_0"
                A, lo, mid, cnt, tmp = (bs[:, i:i + 1] for i in range(5))
                if NB >= 3:
                    fw.v("tensor_reduce", A, score[:, 0:N], AX.X, ALU.max, apply_absolute_value=True, reads=["b_score"], writes=["b_bs"])
                    fw.v("tensor_scalar", A, A, 1.0001, 1e-20, ALU.mult, ALU.add, reads=["b_bs"], writes=["b_bs"])
                fw.v("tensor_tensor", score[:, N - 128:N], score[:, N - 128:N], dmask[:], ALU.add, reads=["b_score", "b_dmask"],
                     writes=["b_score"])
                if NB >= 3:
                    fw.v("tensor_scalar", steps[:], self.cst_sb[:, 1:2 + NIT], A, None, ALU.mult, reads=["cst", "b_bs"], writes=["b_steps"])
                    fw.v("tensor_scalar", mid, A, -1.0, steps[:, 0:1], ALU.mult, ALU.add, reads=["b_bs", "b_steps"], writes=["b_bs"])
                    for it in range(NIT):
                        fw.v("tensor_scalar", mk[:, 0:N], score[:, 0:N], mid, None, ALU.is_ge, ALU.add, accum_out=cnt,
                             reads=["b_score", "b_bs", mkey], writes=[mkey, "b_bs"])
                        fw.v("tensor_scalar", tmp, cnt, 255.5, steps[:, it:it + 1], ALU.is_ge, ALU.mult, reads=["b_bs", "b_steps"], writes=["b_bs"])
                        fw.v("scalar_tensor_tensor", mid, tmp, steps[:, it + 1:it + 2], mid, ALU.subtract, ALU.add,
                             reads=["b_bs", "b_steps"], writes=["b_bs"])
                    fw.v("tensor_tensor", lo, mid, steps[:, NIT:NIT + 1], ALU.subtract, reads=["b_bs", "b_steps"], writes=["b_bs"])
                else:
                    fw.v("memset", lo, -1e29, writes=["b_bs"])
                fw.v("tensor_scalar", mk[:, 0:N], score[:, 0:N], lo, None, ALU.is_ge, reads=["b_score", "b_bs"], writes=[mkey])
                psb1 = PS[1][:].bitcast(BF16)
                mt, mtkey = MT[qt % 2], "b_MT%d" % (qt % 2)
                for kb0 in range(0, NB, 8):
                    nk = min(8, NB - kb0)
                    for j in range(nk):
                        kb = kb0 + j
                        fw.tr(psb1[:, j * 128:(j + 1) * 128], mk[:, kb * 128:(kb + 1) * 128], self.ident_b[:],
                              reads=[mkey, "k_ident"], writes=["ps1"])
                    fw.act(mt[:, kb0:kb0 + nk, :].rearrange("p a b -> p (a b)"), psb1[:, 0:nk * 128], AF.Copy, reads=["ps1"], writes=[mtkey])

            def attention_gen(qt):
                if NOATT:
                    return
                g, tt = qt // 4, qt % 4
                gp = g % 2
                NB = qt + 1
                tq = slice(tt * 128, (tt + 1) * 128)
                mt, mtkey = MT[qt % 2], "b_MT%d" % (qt % 2)
                for hpair in range(4):
                    ct = hpair
                    for gi, kb0 in enumerate(range(0, NB, 4)):
                        nk = min(4, NB - kb0)
                        bis = [2 * e + gi % 2 for e in range(2)]
                        for j in range(nk):
                            kb = kb0 + j
                            for e in range(2):
                                po = e * 64
                                pl = PS[2 + bis[e]]
                                fw.mm(pl[:, j * 128:(j + 1) * 128], KT[ct][po:po + 64, kb * 128:(kb + 1) * 128], QT[gp][ct][po:po + 64, tq],
                                      reads=[("b_KT", ct, kb // 4), "b_QT%d_%d" % (ct, gp)], writes=["ps%d" % (2 + bis[e])])
                                yield
                        for e in range(2):
                            bi = bis[e]
                            fw.act(E[bi][:, 0:nk * 128], PS[2 + bi][:, 0:nk * 128], AF.Exp, scale=0.125, reads=["ps%d" % (2 + bi)], writes=["b_E%d" % bi])
                            yield
                            fw.v("tensor_tensor", PT[bi][:, 0:nk * 128], E[bi][:, 0:nk * 128],
                                 mt[:, kb0:kb0 + nk, :].rearrange("p a b -> p (a b)"), ALU.mult,
                                 reads=["b_E%d" % bi, mtkey], writes=["b_PT%d" % bi], eng="gpsimd")
                            yield
                        for e in range(2):
                            h = 2 * hpair + e
                            bi = bis[e]
                            pob = PS[7] if e == 0 else PS[6]
                            pokey = "ps7" if e == 0 else "ps6"
                            osl = slice(hpair * 65, hpair * 65 + 65)
                            for j in range(nk):
                                kb = kb0 + j
                                fw.mm(pob[:, osl], PT[bi][:, j * 128:(j + 1) * 128], V[:, kb, h, :], start=(kb == 0), stop=(kb == NB - 1),
                                      reads=["b_PT%d" % bi, ("b_V", kb // 4)], writes=[pokey])
                                yield

            def final(qt):
                g, tt = qt // 4, qt % 4
                gp = g % 2
                tq = slice(tt * 128, (tt + 1) * 128)
                otok4 = otok[:].rearrange("p (a e) i -> p a e i", e=2)
                for hb_ in range(2):
                    pob = PS[7] if hb_ == 0 else PS[6]
                    pokey = "ps7" if hb_ == 0 else "ps6"
                    pv = pob[:, 0:260].rearrange("p (h i) -> p h i", i=65)
                    fw.v("reciprocal", rec[:, hb_ * 4:hb_ * 4 + 4], pv[:, :, 64], reads=[pokey], writes=["b_rec"])
                    fw.v("tensor_tensor", otok4[:, :, hb_, :], pv[:, :, 0:64],
                         rec[:, hb_ * 4:hb_ * 4 + 4].unsqueeze(2).to_broadcast([128, 4, 64]), ALU.mult,
                         reads=[pokey, "b_rec"], writes=["b_otok"])
                of = otok[:].rearrange("p h i -> p (h i)")
                for ct in range(4):
                    pb_ = PS[7 - ct // 2][:, 384:512].bitcast(BF16)
                    pkey = "ps%d" % (7 - ct // 2)
                    fw.tr(pb_[:, (ct % 2) * 128:(ct % 2 + 1) * 128], of[:, ct * 128:(ct + 1) * 128], self.ident_b[:],
                          reads=["b_otok", "k_ident"], writes=[pkey])
                for ct in range(4):
                    pb_ = PS[7 - ct // 2][:, 384:512].bitcast(BF16)
                    pkey = "ps%d" % (7 - ct // 2)
                    fw.v("tensor_tensor", yout[gp][:, ct, tq], pb_[:, (ct % 2) * 128:(ct % 2 + 1) * 128], SG[gp][ct][:, tq], ALU.mult,
                         reads=[pkey, "b_SG%d_%d" % (ct, gp)], writes=["b_yout0"])
                if tt == 3:
                    gc = slice(g * 512, g * 512 + 512)
                    fw.dma(self.yT[1][:, :, gc].rearrange("k p s -> p k s"), yout[gp][:], reads=["b_yout0"],
                           writes=[("yT1", g, ct) for ct in range(4)], eng="gpsimd")

            prep_begin(0)
            fw.lockstep(prep_lanes(0))
            prep_end(0)
            scores(0)
            bisect_mask(0)
            for qt in range(NT):
                nxt = qt + 1
                if nxt < NT:
                    if nxt % 4 == 0:
                        gn = nxt // 4
                        prep_begin(gn)
                        fw.lockstep(prep_lanes(gn))
                        prep_end(gn)
                    scores(nxt)
                fw.lockstep([attention_gen(qt)])
                if nxt < NT:
                    bisect_mask(nxt)
                final(qt)
            self.release(keys)

    def phase_C(self, l):
        fw, S, G = self.fw, self.S, self.G
        PS = self.PS
        with ExitStack() as ph:
            sb = lambda n, s, d: ph.enter_context(self.nc.sbuf_tensor(self.uname(n), list(s), d))
            wC = sb("wC", [128, 8, 1024], BF16)
            wr = sb("c_wr", [128, 4, 128], BF16)
            wi = sb("c_wi", [128, 4, 128], BF16)
            hn = [sb("c_hn%d" % i, [128, 8, 512], BF16) for i in range(2)]
            xbuf = sb("c_xbuf", [128, 4, 515], F32)
            hprev = sb("c_hprev", [128, 4], F32)
            cl = sb("c_cl", [128, 4], F32)
            xc = [sb("c_xc%d" % i, [128, 512], F32) for i in range(2)]
            xcb = [sb("c_xcb%d" % i, [128, 512], BF16) for i in range(2)]
            r_ = [sb("c_r%d" % i, [128, 512], F32) for i in range(2)]
            i_ = [sb("c_i%d" % i, [128, 512], F32) for i in range(2)]
            a_ = [sb("c_a%d" % i, [128, 512], F32) for i in range(2)]
            b_ = [sb("c_b%d" % i, [128, 512], F32) for i in range(2)]
            sg = [sb("c_sg%d" % i, [128, 512], F32) for i in range(2)]
            yo = [sb("c_y%d" % i, [128, 512], BF16) for i in range(2)]
            names = ["wC", "c_wr", "c_wi", "c_hn0", "c_hn1", "c_xbuf", "c_hprev", "c_cl"] + \
                    [n + str(i) for n in ("c_xc", "c_xcb", "c_r", "c_i", "c_a", "c_b", "c_sg", "c_y") for i in range(2)]
            keys = names + [("wC", k) for k in range(8)] + [("c_wr", k) for k in range(4)] + [("c_wi", k) for k in range(4)] + \
                ["c_xbuf%d" % i for i in range(4)] + ["c_hprev%d" % i for i in range(4)]
            self.acquire(keys + ["stg0", "stg1"])
            self.load_w(wC, "wC", lambda k: self.w_in[l, k * 128:(k + 1) * 128, OFF_C:OFF_C + 1024], 1024, 8, self.gcol)
            self.load_w(wr, "c_wr", lambda k: self.wr_bd[l, k], 128, 4)
            self.load_w(wi, "c_wi", lambda k: self.wi_bd[l, k], 128, 4)
            fw.act(cl[:], self.prm_sb[:, PCOLS["lam"][0]:PCOLS["lam"][0] + 4], AF.Exp, scale=-1.0, reads=["prm"], writes=["c_cl"])
            fw.act(cl[:], cl[:], AF.Ln, bias=1.0, reads=["c_cl"], writes=["c_cl"])
            fw.v("tensor_scalar", cl[:], cl[:], -8.0, None, ALU.mult, reads=["c_cl"], writes=["c_cl"])
            fw.v("memset", xbuf[:], 0.0, writes=["c_xbuf%d" % i for i in range(4)])
            fw.v("memset", hprev[:], 0.0, writes=["c_hprev%d" % i for i in range(4)])
            for g in range(G):
                c0 = g * 512
                hk = "c_hn%d" % (g % 2)
                hg = hn[g % 2]
                fw.dma(hg[:], self.hnT[:, :, c0:c0 + 512].rearrange("k p s -> p k s"), reads=[("hnT", g)], writes=[hk])
                def cbody(ct, g=g, c0=c0, hk=hk, hg=hg):
                    j = ct % 2
                    pb = 4 * j
                    px, pg, pr, pi = PS[pb], PS[pb + 1], PS[pb + 2], PS[pb + 3]
                    kx, kg, kr, ki = ["ps%d" % (pb + t) for t in range(4)]
                    for k in range(8):
                        fw.mm(px[:], wC[:, k, ct * 128:(ct + 1) * 128], hg[:, k, :], start=(k == 0), stop=(k == 7),
                              reads=[("wC", k), hk], writes=[kx])
                        yield
                    for k in range(8):
                        fw.mm(pg[:], wC[:, k, 512 + ct * 128:512 + (ct + 1) * 128], hg[:, k, :], start=(k == 0), stop=(k == 7),
                              reads=[("wC", k), hk], writes=[kg])
                        yield
                    xb = xbuf[:, ct, :]
                    fw.act(xb[:, 3:515], px[:], AF.Copy, reads=[kx], writes=["c_xbuf%d" % ct])
                    yield
                    cw = lambda i: self.pcol("conv_w", i * 4 + ct)
                    fw.v("tensor_scalar", xc[j][:], xb[:, 3:515], cw(3), self.pcol("conv_b", ct), ALU.mult, ALU.add,
                         reads=["c_xbuf%d" % ct, "prm"], writes=["c_xc%d" % j])
                    yield
                    for i in range(3):
                        fw.v("scalar_tensor_tensor", xc[j][:], xb[:, i:i + 512], cw(i), xc[j][:], ALU.mult, ALU.add,
                             reads=["c_xbuf%d" % ct, "prm", "c_xc%d" % j], writes=["c_xc%d" % j])
                        yield
                    fw.v("tensor_copy", xb[:, 0:3], xb[:, 512:515], reads=["c_xbuf%d" % ct], writes=["c_xbuf%d" % ct], eng="gpsimd")
                    yield
                    fw.v("tensor_copy", xcb[j][:], xc[j][:], reads=["c_xc%d" % j], writes=["c_xcb%d" % j], eng="gpsimd")
                    yield
                    fw.mm(pr[:], wr[:, ct, :], xcb[j][:], reads=[("c_wr", ct), "c_xcb%d" % j], writes=[kr])
                    yield
                    fw.mm(pi[:], wi[:, ct, :], xcb[j][:], reads=[("c_wi", ct), "c_xcb%d" % j], writes=[ki])
                    yield
                    fw.act(r_[j][:], pr[:], AF.Sigmoid, bias=self.pcol("b_r", ct), reads=[kr, "prm"], writes=["c_r%d" % j])
                    yield
                    fw.act(i_[j][:], pi[:], AF.Sigmoid, bias=self.pcol("b_i", ct), reads=[ki, "prm"], writes=["c_i%d" % j])
                    yield
                    fw.act(sg[j][:], pg[:], AF.Silu, reads=[kg], writes=["c_sg%d" % j])
                    yield
                    fw.act(a_[j][:], r_[j][:], AF.Exp, scale=cl[:, ct:ct + 1], reads=["c_r%d" % j, "c_cl"], writes=["c_a%d" % j])
                    yield
                    fw.v("tensor_tensor", b_[j][:], a_[j][:], a_[j][:], ALU.mult, reads=["c_a%d" % j], writes=["c_b%d" % j])
                    yield
                    fw.v("tensor_scalar", b_[j][:], b_[j][:], -1.0, 1.0, ALU.mult, ALU.add, reads=["c_b%d" % j], writes=["c_b%d" % j])
                    yield
                    fw.act(b_[j][:], b_[j][:], AF.Sqrt, reads=["c_b%d" % j], writes=["c_b%d" % j])
                    yield
                    fw.v("tensor_tensor", i_[j][:], i_[j][:], xc[j][:], ALU.mult, reads=["c_i%d" % j, "c_xc%d" % j],
                         writes=["c_i%d" % j], eng="gpsimd")
                    yield
                    fw.v("tensor_tensor", b_[j][:], b_[j][:], i_[j][:], ALU.mult, reads=["c_b%d" % j, "c_i%d" % j], writes=["c_b%d" % j])
                    yield
                    fw.v("tensor_tensor_scan", r_[j][:], a_[j][:], b_[j][:], hprev[:, ct:ct + 1], ALU.mult, ALU.add,
                         reads=["c_a%d" % j, "c_b%d" % j, "c_hprev%d" % ct, "c_r%d" % j], writes=["c_r%d" % j])
                    yield
                    fw.v("tensor_copy", hprev[:, ct:ct + 1], r_[j][:, 511:512], reads=["c_r%d" % j], writes=["c_hprev%d" % ct])
                    yield
                    fw.v("tensor_tensor", yo[j][:], r_[j][:], sg[j][:], ALU.mult, reads=["c_r%d" % j, "c_sg%d" % j],
                         writes=["c_y%d" % j], eng="gpsimd")
                    yield
                    fw.dma(self.yT[2][ct, :, c0:c0 + 512], yo[j][:], reads=["c_y%d" % j], writes=[("yT2", g, ct)], eng="gpsimd")
                    yield
                fw.lockstep([cbody(0), cbody(1)])
                fw.lockstep([cbody(2), cbody(3)])
            self.release(keys)

    def phase_M(self, l):
        fw, S, G, L = self.fw, self.S, self.G, self.L
        PS = self.PS
        last = (l == L - 1)
        with ExitStack() as ph:
            sb = lambda n, s, d: ph.enter_context(self.nc.sbuf_tensor(self.uname(n), list(s), d))
            wG = sb("wG", [128, 8, 3072], BF16)
            wbr = sb("wbr", [128, 12, 1024], BF16)
            wo = sb("wo", [128, 8, 1024], BF16)
            wpg = sb("wpg", [128, 8, 1024], BF16)
            wple = sb("wple", [128, 2, 1024], BF16)
            hn = sb("m_hn", [128, 8, 512], BF16)
            ys = [sb("m_y%d" % n, [128, 4, 512], BF16) for n in range(3)]
            hb = sb("m_h", [128, 8, 512], F32)
            h1b = sb("m_h1b", [128, 8, 512], BF16)
            pf = sb("m_pf", [128, 2, 512], F32)
            pb_ = sb("m_pb", [128, 2, 512], BF16)
            mrg = sb("m_mrg", [128, 8, 512], BF16)
            sgsP = [[sb("m_sg%d_%d" % (p, n), [128, 512], F32) for n in range(3)] for p in range(2)]
            sgs = sgsP[0]
            tmp = sb("m_tmp", [128, 2, 512], F32)
            self.rs_sb = sb("m_rs", [128, 512], F32)
            self.hn_out = h1b
            self.hn_out_key = "m_h1b"
            self.eps_col = sb("m_eps", [128, 1], F32)
            names = ["wG", "wbr", "wo", "wpg", "wple", "m_hn", "m_y0", "m_y1", "m_y2", "m_h", "m_h1b", "m_pf", "m_pb",
                     "m_mrg", "m_sg0_0", "m_sg0_1", "m_sg0_2", "m_sg1_0", "m_sg1_1", "m_sg1_2", ("m_tmp", 0), ("m_tmp", 1), "rs", "hn_out", "eps"]
            keys = names + [("wG", k) for k in range(8)] + [("wbr", k) for k in range(12)] + \
                [("wo", k) for k in range(8)] + [("wpg", k) for k in range(8)] + [("wple", k) for k in range(2)]
            self.acquire(keys + ["stg0", "stg1"])
            fw.v("memset", self.eps_col[:], NORM_EPS, writes=["eps"])
            self.load_w(wG, "wG", lambda k: self.w_in[l, k * 128:(k + 1) * 128, OFF_G:OFF_G + 3072], 3072, 8, self.gcol)
            self.load_w(wbr, "wbr", lambda k: self.w_branch[l, k // 4, (k % 4) * 128:(k % 4 + 1) * 128, :], 1024, 12)
            self.load_w(wo, "wo", lambda k: self.w_out[l, k * 128:(k + 1) * 128, :], 1024, 8)
            self.load_w(wpg, "wpg", lambda k: self.w_pg[l, k * 128:(k + 1) * 128, :], 1024, 8)
            self.load_w(wple, "wple", lambda k: self.w_ple[l, k * 128:(k + 1) * 128, :], 1024, 2)
            hsrc = self.xT if l == 0 else self.hT
            hdst = self.outT if last else self.hT
            for g in range(G):
                c0 = g * 512
                fw.dma(hn[:], self.hnT[:, :, c0:c0 + 512].rearrange("k p s -> p k s"), reads=[("hnT", g)], writes=["m_hn"])
                for n in range(3):
                    fw.dma(ys[n][:], self.yT[n][:, :, c0:c0 + 512].rearrange("k p s -> p k s"),
                           reads=[("yT%d" % n, g, ct) for ct in range(4)], writes=["m_y%d" % n])
                fw.dma(hb[:], hsrc[:, :, c0:c0 + 512].rearrange("k p s -> p k s"),
                       reads=([("hT", g)] if l > 0 else []), writes=["m_h"])
                fw.dma(pf[:], self.pT[l, :, :, c0:c0 + 512].rearrange("k p s -> p k s"), writes=["m_pf"])
                fw.v("tensor_copy", pb_[:], pf[:], reads=["m_pf"], writes=["m_pb"], eng="gpsimd")
                for dmt in range(8):
                    cs = slice(dmt * 128, (dmt + 1) * 128)
                    dp = dmt % 2
                    sg_ = sgsP[dp]
                    gb = 3 * (dmt % 2)
                    for n in range(3):
                        yb = 6 + (dmt * 3 + n) % 2
                        for k in range(8):
                            fw.mm(PS[gb + n][:], wG[:, k, n * 1024 + dmt * 128:n * 1024 + (dmt + 1) * 128], hn[:, k, :],
                                  start=(k == 0), stop=(k == 7), reads=[("wG", k), "m_hn"], writes=["ps%d" % (gb + n)])
                        for kc in range(4):
                            fw.mm(PS[yb][:], wbr[:, n * 4 + kc, cs], ys[n][:, kc, :], start=(kc == 0), stop=(kc == 3),
                                  reads=[("wbr", n * 4 + kc), "m_y%d" % n], writes=["ps%d" % yb])
                        fw.act(sg_[n][:], PS[gb + n][:], AF.Sigmoid, reads=["ps%d" % (gb + n)], writes=["m_sg%d_%d" % (dp, n)])
                        fw.v("tensor_tensor", sg_[n][:], PS[yb][:], sg_[n][:], ALU.mult,
                             reads=["ps%d" % yb, "m_sg%d_%d" % (dp, n)], writes=["m_sg%d_%d" % (dp, n)])
                    fw.v("tensor_tensor", sg_[0][:], sg_[0][:], sg_[1][:], ALU.add, reads=["m_sg%d_0" % dp, "m_sg%d_1" % dp], writes=["m_sg%d_0" % dp], eng="gpsimd")
                    fw.v("tensor_tensor", mrg[:, dmt, :], sg_[0][:], sg_[2][:], ALU.add, reads=["m_sg%d_0" % dp, "m_sg%d_2" % dp], writes=["m_mrg"], eng="gpsimd")
                for d2 in range(8):
                    pk = 6 + d2 % 2
                    for k in range(8):
                        fw.mm(PS[pk][:], wo[:, k, d2 * 128:(d2 + 1) * 128], mrg[:, k, :], start=(k == 0), stop=(k == 7),
                              reads=[("wo", k), "m_mrg"], writes=["ps%d" % pk])
                    fw.v("tensor_tensor", hb[:, d2, :], hb[:, d2, :], PS[pk][:], ALU.add, reads=["m_h", "ps%d" % pk], writes=["m_h"])
                fw.act(h1b[:], hb[:], AF.Copy, reads=["m_h"], writes=["m_h1b"])
                for d2 in range(8):
                    pa, pp = (0, 1) if d2 % 2 == 0 else (2, 3)
                    for k in range(8):
                        fw.mm(PS[pa][:], wpg[:, k, d2 * 128:(d2 + 1) * 128], h1b[:, k, :], start=(k == 0), stop=(k == 7),
                              reads=[("wpg", k), "m_h1b"], writes=["ps%d" % pa])
                    for k in range(2):
                        fw.mm(PS[pp][:], wple[:, k, d2 * 128:(d2 + 1) * 128], pb_[:, k, :], start=(k == 0), stop=(k == 1),
                              reads=[("wple", k), "m_pb"], writes=["ps%d" % pp])
                    sgk = d2 % 2
                    fw.act(sgs[sgk][:], PS[pa][:], AF.Sigmoid, reads=["ps%d" % pa], writes=["m_sg0_%d" % sgk])
                    fw.v("tensor_tensor", sgs[sgk][:], PS[pp][:], sgs[sgk][:], ALU.mult, reads=["ps%d" % pp, "m_sg0_%d" % sgk],
                         writes=["m_sg0_%d" % sgk])
                    fw.v("tensor_tensor", hb[:, d2, :], hb[:, d2, :], sgs[sgk][:], ALU.add, reads=["m_h", "m_sg0_%d" % sgk],
                         writes=["m_h"], eng="gpsimd")
                fw.dma(hdst[:, :, c0:c0 + 512].rearrange("k p s -> p k s"), hb[:], reads=["m_h"],
                       writes=[("outT" if last else "hT", g)], eng="gpsimd")
                if not last:
                    self.norm_group(hb, "m_h", g, tmp, "m_tmp")
            self.release(keys)


_CACHE = {}


def make_in_maps(inp, S, L, ncores):
    maps = []
    w_in = np.ascontiguousarray(np.asarray(inp["w_in"], np.float32)[:L])
    ki0 = OFF_B + 1792
    w_kidup = np.ascontiguousarray(np.concatenate([w_in[:, :, ki0:ki0 + 64], w_in[:, :, ki0:ki0 + 64]], axis=2))
    prm = np.stack([pack_params(inp, l) for l in range(L)])
    cst = np.zeros((128, 32), np.float32)
    invf = (np.float32(500000.0) ** (-(np.arange(0, 16, 2, dtype=np.float32) / np.float32(16)))).astype(np.float32)
    for p_ in range(128):
        if p_ % 64 < 16:
            cst[p_, 0] = invf[p_ % 8]
    cst[:, 1:25] = (2.0 ** (-np.arange(24, dtype=np.float64)))[None, :].astype(np.float32)
    ropeR = np.zeros((128, 128), np.float32)
    for m_ in range(128):
        if m_ % 64 < 8:
            ropeR[m_ + 8, m_] = -1.0
        elif m_ % 64 < 16:
            ropeR[m_ - 8, m_] = 1.0
    shared = {
        "cst": cst, "ropeR": ropeR,
        "prm": prm, "w_in": w_in, "w_kidup": w_kidup,
        "w2": np.ascontiguousarray(np.asarray(inp["rwkv_w2"], np.float32)[:L]),
        "a2": np.ascontiguousarray(np.asarray(inp["rwkv_a2"], np.float32)[:L]),
        "wr_bd": np.stack([blockdiag(inp["lru_w_r"][l]) for l in range(L)]),
        "wi_bd": np.stack([blockdiag(inp["lru_w_i"][l]) for l in range(L)]),
        "w_branch": np.ascontiguousarray(np.asarray(inp["w_branch"], np.float32)[:L]),
        "w_out": np.ascontiguousarray(np.asarray(inp["w_out"], np.float32)[:L]),
        "w_ple": np.ascontiguousarray(np.asarray(inp["w_ple"], np.float32)[:L]),
        "w_pg": np.ascontiguousarray(np.asarray(inp["w_ple_gate"], np.float32)[:L]),
    }
    x = np.asarray(inp["x"], np.float32)
    p = np.asarray(inp["p"], np.float32)
    pos = np.asarray(inp["positions"], np.int32)
    nb = x.shape[0]
    for c in range(ncores):
        b = (c // 2) % nb
        m = dict(shared)
        m["xT"] = np.ascontiguousarray(x[b].T.reshape(8, 128, S))
        m["pT"] = np.ascontiguousarray(np.stack([p[l, b].T.reshape(2, 128, S) for l in range(L)]))
        m["pos"] = np.ascontiguousarray(pos[b].reshape(1, S))
        maps.append(m)
    return maps


def kernel(**inputs):
    x = np.asarray(inputs["x"])
    B, S, _ = x.shape
    L = np.asarray(inputs["w_in"]).shape[0]
    key = (S, L)
    if key not in _CACHE:
        _CACHE[key] = Prog(S, L).build()
    nc = _CACHE[key]
    maps = make_in_maps(inputs, S, L, 8)
    res = run_bass_kernel_spmd(nc, maps, core_ids=list(range(8)))
    out = np.zeros((B, S, D), np.float32)
    for b in range(B):
        out[b] = res.results[2 * b]["outT"].reshape(D, S).T
    return out
```
